# Optimizing a Trainium2 kernel written in Bass

```python
import math
import jax
import jax.numpy as jnp
from jax import lax
import numpy as np

D_MODEL = 1024
BATCH = 8
SEQ = 4096
DEPTH = 2

A_HEADS = 4
A_DK = 64
A_DV = 64
B_HEADS = 8
B_GROUPS = 2
B_HPG = B_HEADS // B_GROUPS
B_DH = 64
C_HEADS = 4
C_DH = 64
MIX_WIDTH = A_HEADS * A_DV + B_HEADS * B_DH + C_HEADS * C_DH
D_FF = 4 * D_MODEL
ROPE_THETA = 500000.0
ROT_DIM = B_DH // 4
EPS = 1e-6
NEG_BIG = -1e30
POS_BIG = 1e30
TINY = 1e-30
HGRN_CHUNK = 64
CMP_BLOCK = 32
CMP_STRIDE = 16
CMP_HIDDEN = 128
SEL_BLOCK = 64
SEL_TOPK = 16
WINDOW = 512
NSA_QBLOCK = 64
FOX_QBLOCK = 128

IN_SPLIT = (
    A_HEADS * A_DK, A_HEADS * A_DK, A_HEADS * A_DV, A_HEADS * A_DV,
    B_HEADS * B_DH,
    B_GROUPS * B_DH, B_GROUPS * B_DH,
    B_GROUPS * B_DH, B_GROUPS * B_DH,
    B_GROUPS * B_DH, B_GROUPS * B_DH,
    3 * B_HEADS,
    C_HEADS * C_DH, C_HEADS * C_DH, C_HEADS * C_DH, C_HEADS,
)
IN_WIDTH = sum(IN_SPLIT)

kernel_name = "hybrid_hgrn2_nsa_fox_block"


def _rmsnorm(x, g):
    xf = x.astype(jnp.float32)
    var = jnp.mean(xf * xf, axis=-1, keepdims=True)
    return xf * lax.rsqrt(var + EPS) * g.astype(jnp.float32)


def _rope_partial(x, pos):
    half = ROT_DIM // 2
    inv_freq = jnp.power(jnp.float32(ROPE_THETA), -jnp.arange(0, ROT_DIM, 2, dtype=jnp.float32) / ROT_DIM)
    ang = pos.astype(jnp.float32)[:, None] * inv_freq[None, :]
    ang = ang.reshape((1, pos.shape[0]) + (1,) * (x.ndim - 3) + (half,))
    cos, sin = jnp.cos(ang), jnp.sin(ang)
    x1 = x[..., :half]
    x2 = x[..., half:ROT_DIM]
    return jnp.concatenate([x1 * cos - x2 * sin, x2 * cos + x1 * sin, x[..., ROT_DIM:]], axis=-1)


def _masked_softmax(logits, mask):
    logits = jnp.where(mask, logits.astype(jnp.float32), NEG_BIG)
    m = jnp.max(logits, axis=-1, keepdims=True)
    p = jnp.where(mask, jnp.exp(logits - m), 0.0)
    return p / jnp.maximum(jnp.sum(p, axis=-1, keepdims=True), TINY)


def _hgrn2(q, f_logit, i_val, g_out, lb, onorm_g):
    bsz, seq = q.shape[0], q.shape[1]
    lb = lb.reshape(A_HEADS, A_DK)
    z = f_logit.astype(jnp.float32).reshape(bsz, seq, A_HEADS, A_DK)
    f = lb + (1.0 - lb) * jax.nn.sigmoid(z)
    log_f = jnp.log(jnp.maximum(f, TINY))
    k = (1.0 - lb) * jax.nn.sigmoid(-z)
    qf = q.astype(jnp.float32).reshape(bsz, seq, A_HEADS, A_DK) * (A_DK ** -0.5)
    v = i_val.astype(jnp.float32).reshape(bsz, seq, A_HEADS, A_DV)
    n_chunks = seq // HGRN_CHUNK

    def to_chunks(t):
        return t.reshape(bsz, n_chunks, HGRN_CHUNK, A_HEADS, -1).transpose(1, 0, 3, 2, 4)

    causal = jnp.tril(jnp.ones((HGRN_CHUNK, HGRN_CHUNK), dtype=bool))

    def step(state, inp):
        qc, kc, vc, gc = inp
        G = jnp.cumsum(gc, axis=2)
        o_inter = jnp.einsum('bhtk,bhkv->bhtv', qc * jnp.exp(G), state)
        diff = G[:, :, :, None, :] - G[:, :, None, :, :]
        decay = jnp.exp(jnp.where(causal[:, :, None], diff, NEG_BIG))
        scores = jnp.einsum('bhtk,bhsk,bhtsk->bhts', qc, kc, decay)
        o_intra = jnp.einsum('bhts,bhsv->bhtv', scores, vc)
        g_last = G[:, :, -1:, :]
        k_dec = kc * jnp.exp(g_last - G)
        state = state * jnp.exp(g_last[:, :, 0, :])[..., None] + jnp.einsum('bhsk,bhsv->bhkv', k_dec, vc)
        return state, o_inter + o_intra

    state0 = jnp.zeros((bsz, A_HEADS, A_DK, A_DV), jnp.float32)
    _, o = lax.scan(step, state0, (to_chunks(qf), to_chunks(k), to_chunks(v), to_chunks(log_f)))
    o = o.transpose(1, 0, 3, 2, 4).reshape(bsz, seq, A_HEADS, A_DV)
    gate = jax.nn.silu(g_out.astype(jnp.float32)).reshape(bsz, seq, A_HEADS, A_DV)
    o = _rmsnorm(o, onorm_g) * gate
    return o.reshape(bsz, seq, A_HEADS * A_DV)


def _nsa(q, k_cmp, v_cmp, k_slc, v_slc, k_win, v_win, gates, qn_g, kn_g, cmp_pos, cmp_w1, cmp_w2):
    bsz, seq = q.shape[0], q.shape[1]
    pos = jnp.arange(seq)
    qh = _rope_partial(_rmsnorm(q.reshape(bsz, seq, B_GROUPS, B_HPG, B_DH), qn_g), pos) * (B_DH ** -0.5)
    ks = _rope_partial(_rmsnorm(k_slc.reshape(bsz, seq, B_GROUPS, B_DH), kn_g), pos)
    kw = _rope_partial(_rmsnorm(k_win.reshape(bsz, seq, B_GROUPS, B_DH), kn_g), pos)
    vs = v_slc.astype(jnp.float32).reshape(bsz, seq, B_GROUPS, B_DH)
    vw = v_win.astype(jnp.float32).reshape(bsz, seq, B_GROUPS, B_DH)

    n_sub = CMP_BLOCK // CMP_STRIDE
    n_cmp = seq // CMP_STRIDE - n_sub + 1

    def compress(t, which):
        sub = t.astype(jnp.float32).reshape(bsz, seq // CMP_STRIDE, CMP_STRIDE, B_GROUPS, B_DH)
        blocks = jnp.concatenate([sub[:, m:m + n_cmp] for m in range(n_sub)], axis=2)
        blocks = blocks + cmp_pos[which][None, None, :, None, :]
        flat = blocks.transpose(0, 1, 3, 2, 4).reshape(bsz, n_cmp, B_GROUPS, CMP_BLOCK * B_DH)
        hidden = jax.nn.gelu(flat @ cmp_w1[which])
        return hidden @ cmp_w2[which]

    cmp_end = jnp.arange(n_cmp) * CMP_STRIDE + CMP_BLOCK - 1
    kc = _rope_partial(_rmsnorm(compress(k_cmp.reshape(bsz, seq, B_GROUPS, B_DH), 0), kn_g), cmp_end)
    vc = compress(v_cmp.reshape(bsz, seq, B_GROUPS, B_DH), 1)

    n_sel = seq // SEL_BLOCK
    ci = np.arange(n_cmp)[:, None]
    sj = np.arange(n_sel)[None, :]
    c_start = ci * CMP_STRIDE
    overlap = jnp.asarray(((c_start <= sj * SEL_BLOCK + SEL_BLOCK - 1)
                           & (c_start + CMP_BLOCK - 1 >= sj * SEL_BLOCK)).astype(np.float32))
    topk = min(SEL_TOPK, n_sel)
    kb = ks.reshape(bsz, n_sel, SEL_BLOCK, B_GROUPS, B_DH).transpose(0, 3, 1, 2, 4)
    vb = vs.reshape(bsz, n_sel, SEL_BLOCK, B_GROUPS, B_DH).transpose(0, 3, 1, 2, 4)
    kw_pad = jnp.pad(kw, ((0, 0), (WINDOW, 0), (0, 0), (0, 0)))
    vw_pad = jnp.pad(vw, ((0, 0), (WINDOW, 0), (0, 0), (0, 0)))
    gate = jax.nn.sigmoid(gates.astype(jnp.float32)).reshape(bsz, seq, 3, B_GROUPS, B_HPG)
    b_idx = jnp.arange(bsz)[:, None, None, None]
    g_idx = jnp.arange(B_GROUPS)[None, :, None, None]
    sel_off = jnp.arange(SEL_BLOCK)
    win_off = jnp.arange(WINDOW + NSA_QBLOCK)
    blk_ids = jnp.arange(n_sel)

    def block_fn(blk):
        q0 = blk * NSA_QBLOCK
        t = q0 + jnp.arange(NSA_QBLOCK)
        qb = lax.dynamic_slice_in_dim(qh, q0, NSA_QBLOCK, axis=1)
        s_c = jnp.einsum('bqghd,bngd->bghqn', qb, kc)
        p_c = _masked_softmax(s_c, cmp_end[None, :] <= t[:, None])
        o_c = jnp.einsum('bghqn,bngd->bqghd', p_c, vc)
        imp = jnp.einsum('bghqn,nj->bgqj', p_c, overlap)
        cur = (t // SEL_BLOCK)[:, None]
        forced = (blk_ids[None, :] == 0) | (blk_ids[None, :] == cur) | (blk_ids[None, :] == cur - 1)
        imp = jnp.where(forced, POS_BIG, imp)
        imp = jnp.where(blk_ids[None, :] * SEL_BLOCK > t[:, None], NEG_BIG, imp)
        vals, idx = lax.top_k(imp, topk)
        k_sel = kb[b_idx, g_idx, idx]
        v_sel = vb[b_idx, g_idx, idx]
        s_s = jnp.einsum('bqghd,bgqkld->bghqkl', qb, k_sel).reshape(bsz, B_GROUPS, B_HPG, NSA_QBLOCK, topk * SEL_BLOCK)
        key_pos = idx[..., None] * SEL_BLOCK + sel_off
        m_s = (key_pos <= t[None, None, :, None, None]) & (vals > NEG_BIG * 0.5)[..., None]
        m_s = m_s.reshape(bsz, B_GROUPS, 1, NSA_QBLOCK, topk * SEL_BLOCK)
        p_s = _masked_softmax(s_s, m_s).reshape(bsz, B_GROUPS, B_HPG, NSA_QBLOCK, topk, SEL_BLOCK)
        o_s = jnp.einsum('bghqkl,bgqkld->bqghd', p_s, v_sel)
        kwb = lax.dynamic_slice_in_dim(kw_pad, q0, WINDOW + NSA_QBLOCK, axis=1)
        vwb = lax.dynamic_slice_in_dim(vw_pad, q0, WINDOW + NSA_QBLOCK, axis=1)
        kpos = q0 - WINDOW + win_off
        m_w = (kpos[None, :] >= 0) & (kpos[None, :] <= t[:, None]) & (kpos[None, :] > t[:, None] - WINDOW)
        s_w = jnp.einsum('bqghd,bkgd->bghqk', qb, kwb)
        p_w = _masked_softmax(s_w, m_w)
        o_w = jnp.einsum('bghqk,bkgd->bqghd', p_w, vwb)
        gb = lax.dynamic_slice_in_dim(gate, q0, NSA_QBLOCK, axis=1)
        return gb[:, :, 0, :, :, None] * o_c + gb[:, :, 1, :, :, None] * o_s + gb[:, :, 2, :, :, None] * o_w

    out = lax.map(block_fn, jnp.arange(seq // NSA_QBLOCK))
    return out.transpose(1, 0, 2, 3, 4, 5).reshape(bsz, seq, B_HEADS * B_DH)


def _fox(q, k, v, f_logit, qn_g, kn_g, fb):
    bsz, seq = q.shape[0], q.shape[1]
    qh = _rmsnorm(q.reshape(bsz, seq, C_HEADS, C_DH), qn_g) * (C_DH ** -0.5)
    kh = _rmsnorm(k.reshape(bsz, seq, C_HEADS, C_DH), kn_g)
    vh = v.astype(jnp.float32).reshape(bsz, seq, C_HEADS, C_DH)
    log_f = jax.nn.log_sigmoid(f_logit.astype(jnp.float32) + fb.astype(jnp.float32))
    c = jnp.cumsum(log_f, axis=1).transpose(0, 2, 1)
    kpos = jnp.arange(seq)

    def block_fn(blk):
        q0 = blk * FOX_QBLOCK
        t = q0 + jnp.arange(FOX_QBLOCK)
        qb = lax.dynamic_slice_in_dim(qh, q0, FOX_QBLOCK, axis=1)
        cq = lax.dynamic_slice_in_dim(c, q0, FOX_QBLOCK, axis=2)
        s = jnp.einsum('bqhd,bkhd->bhqk', qb, kh) + cq[..., None] - c[:, :, None, :]
        p = _masked_softmax(s, kpos[None, :] <= t[:, None])
        return jnp.einsum('bhqk,bkhd->bqhd', p, vh)

    out = lax.map(block_fn, jnp.arange(seq // FOX_QBLOCK))
    return out.transpose(1, 0, 2, 3, 4).reshape(bsz, seq, C_HEADS * C_DH)


def setup_inputs(seed: int = 0) -> dict:
    key = jax.random.key(seed)
    ks = jax.random.split(key, 17)

    def nrm(k, shape, scale):
        return jax.random.normal(k, shape, jnp.float32) * scale

    return {
        'x': nrm(ks[0], (BATCH, SEQ, D_MODEL), 1.0),
        'norm1_g': 1.0 + nrm(ks[1], (DEPTH, D_MODEL), 0.02),
        'w_in': nrm(ks[2], (DEPTH, D_MODEL, IN_WIDTH), D_MODEL ** -0.5),
        'hgrn_lb_logits': nrm(ks[3], (DEPTH, A_HEADS * A_DK), 0.1),
        'hgrn_onorm_g': 1.0 + nrm(ks[4], (DEPTH, A_DV), 0.02),
        'nsa_qn_g': 1.0 + nrm(ks[5], (DEPTH, B_DH), 0.02),
        'nsa_kn_g': 1.0 + nrm(ks[6], (DEPTH, B_DH), 0.02),
        'nsa_cmp_pos': nrm(ks[7], (DEPTH, 2, CMP_BLOCK, B_DH), 0.02),
        'nsa_cmp_w1': nrm(ks[8], (DEPTH, 2, CMP_BLOCK * B_DH, CMP_HIDDEN), (CMP_BLOCK * B_DH) ** -0.5),
        'nsa_cmp_w2': nrm(ks[9], (DEPTH, 2, CMP_HIDDEN, B_DH), CMP_HIDDEN ** -0.5),
        'fox_qn_g': 1.0 + nrm(ks[10], (DEPTH, C_DH), 0.02),
        'fox_kn_g': 1.0 + nrm(ks[11], (DEPTH, C_DH), 0.02),
        'fox_fb': 2.0 + nrm(ks[12], (DEPTH, C_HEADS), 0.1),
        'w_o': nrm(ks[13], (DEPTH, MIX_WIDTH, D_MODEL), MIX_WIDTH ** -0.5),
        'norm2_g': 1.0 + nrm(ks[14], (DEPTH, D_MODEL), 0.02),
        'w_up': nrm(ks[15], (DEPTH, D_MODEL, D_FF), D_MODEL ** -0.5),
        'w_down': nrm(ks[16], (DEPTH, D_FF, D_MODEL), D_FF ** -0.5),
    }


def reference(x, norm1_g, w_in, hgrn_lb_logits, hgrn_onorm_g, nsa_qn_g, nsa_kn_g, nsa_cmp_pos,
              nsa_cmp_w1, nsa_cmp_w2, fox_qn_g, fox_kn_g, fox_fb, w_o, norm2_g, w_up, w_down):
    lb_p = jax.nn.softmax(hgrn_lb_logits.astype(jnp.float32), axis=0)
    lb_all = jnp.cumsum(lb_p, axis=0) - lb_p[0:1]
    split_at = [int(v) for v in np.cumsum(IN_SPLIT)[:-1]]
    for layer in range(DEPTH):
        h = _rmsnorm(x, norm1_g[layer]).astype(x.dtype)
        proj = h @ w_in[layer]
        (aq, af, ai, ag, bq, bkc, bvc, bks, bvs, bkw, bvw, bg,
         cq, ck, cv, cf) = jnp.split(proj, split_at, axis=-1)
        o_a = _hgrn2(aq, af, ai, ag, lb_all[layer], hgrn_onorm_g[layer])
        o_b = _nsa(bq, bkc, bvc, bks, bvs, bkw, bvw, bg, nsa_qn_g[layer], nsa_kn_g[layer],
                   nsa_cmp_pos[layer], nsa_cmp_w1[layer], nsa_cmp_w2[layer])
        o_c = _fox(cq, ck, cv, cf, fox_qn_g[layer], fox_kn_g[layer], fox_fb[layer])
        mix = jnp.concatenate([o_a, o_b, o_c], axis=-1).astype(x.dtype)
        x = x + (mix @ w_o[layer]).astype(x.dtype)
        h2 = _rmsnorm(x, norm2_g[layer]).astype(x.dtype)
        x = x + (jnp.square(jax.nn.relu(h2 @ w_up[layer])) @ w_down[layer]).astype(x.dtype)
    return x
```

```python
import numpy as np, sys, time, os, math
import numpy as np
from contextlib import ExitStack
import concourse.bass as bass
import concourse.mybir as mybir

F32 = mybir.dt.float32
BF16 = mybir.dt.bfloat16
AF = mybir.ActivationFunctionType
ALU = mybir.AluOpType
AX = mybir.AxisListType


def _box(ap):
    t = ap.tensor
    dims = ap.ap
    off = int(ap.offset)
    shp = tuple(t.shape)
    rowsize = 1
    for s in shp[1:]:
        rowsize *= int(s)
    r0 = off // rowsize
    f0 = off % rowsize
    rows = 0
    free = 0
    for (st, cnt) in dims:
        st = int(st); cnt = int(cnt)
        if cnt <= 1 or st == 0:
            continue
        if st % rowsize == 0:
            rows += (st // rowsize) * (cnt - 1)
        else:
            free += st * (cnt - 1)
    return t.name, (r0, r0 + rows, f0, f0 + free)


def _ov(a, b):
    return a[0] <= b[1] and b[0] <= a[1] and a[2] <= b[3] and b[2] <= a[3]


def _cont(a, b):
    return a[0] <= b[0] and b[1] <= a[1] and a[2] <= b[2] and b[3] <= a[3]


class Prog:
    def __init__(self, nc, plan=None):
        self.nc = nc
        self.plan = plan
        self.rec = plan is None
        self.eng = dict(pe=nc.tensor, dve=nc.vector, act=nc.scalar, pool=nc.gpsimd, sp=nc.sync)
        self.n = 0
        self.ins = []
        self.track = {}
        self.lane_cnt = {}
        self.freed = {}
        self.uid = 0
        self.stack = ExitStack()
        self.psum_rr = 0
        self.psum_banks = []
        if not self.rec:
            self.sem = {}
            for e in ['pe', 'dve', 'act', 'pool']:
                self.sem[e] = self.stack.enter_context(nc.semaphore("sem_" + e))
            self.lane_sem = {}
            for ln in plan['lanes']:
                self.lane_sem[ln] = self.stack.enter_context(nc.semaphore("ln_" + ln))

    def sb(self, st, name, shape, dtype):
        self.uid += 1
        name = "%s_%d" % (name, self.uid)
        t = st.enter_context(self.nc.sbuf_tensor("s_" + name, list(shape), dtype))
        st.callback(self._free, "s_" + name)
        return t

    def ps(self, st, name, shape, dtype=F32):
        self.uid += 1
        name = "%s_%d" % (name, self.uid)
        t = st.enter_context(self.nc.psum_tensor("p_" + name, list(shape), dtype))
        st.callback(self._free, "p_" + name)
        return t

    def _free(self, name):
        if not self.rec:
            return
        recs = self.track.pop(name, [])
        for (b, i, w) in recs:
            r = self.ins[i]
            key = ('l', r['lane'], i) if r['dma'] else ('e', r['eng'])
            if r['dma']:
                self.freed[key] = i
            else:
                self.freed[key] = max(self.freed.get(key, -1), i)

    def _access(self, idx, eng, dma, ap, write, deps):
        name, box = _box(ap)
        if name not in self.track:
            big = (0, 10 ** 9, 0, 10 ** 9)
            kind = ap.space
            self.track[name] = [] if str(kind) == 'DRAM' else [(big, i, True) for i in sorted(set(self.freed.values()))]
        recs = self.track[name]
        for (b, i, w) in recs:
            if (write or w) and _ov(b, box):
                deps.append((i, (w and not write)))
        if write:
            recs[:] = [r for r in recs if not _cont(box, r[0])]
        elif not dma:
            recs[:] = [r for r in recs if not ((not r[2]) and r[1] < len(self.ins) and self.ins[r[1]]['eng'] == eng
                                               and not self.ins[r[1]]['dma'] and _cont(box, r[0]))]
        recs.append((box, idx, write))

    def op(self, eng, fn, reads=(), writes=(), dma=False, lane=None):
        idx = self.n
        self.n += 1
        if self.rec:
            deps = []
            for ap in reads:
                self._access(idx, eng, dma, ap, False, deps)
            for ap in writes:
                self._access(idx, eng, dma, ap, True, deps)
            lanewaits = {}
            d2 = {}
            for (j, raw) in deps:
                if j == idx:
                    continue
                pj = self.ins[j]
                if pj['dma']:
                    ln = pj['lane']
                    lanewaits[ln] = max(lanewaits.get(ln, 0), pj['lane_val_at'])
                    lanewaits[ln] = max(lanewaits[ln], self.lane_cnt[ln])
                    continue
                if pj['eng'] == eng and not dma:
                    if eng == 'pe':
                        continue
                    if not raw:
                        continue
                d2[j] = True
            rec = dict(eng=eng, deps=list(d2.keys()), lanewaits=lanewaits, dma=dma, lane=lane)
            if dma:
                self.lane_cnt[lane] = self.lane_cnt.get(lane, 0) + 16
                rec['lane_val_at'] = self.lane_cnt[lane]
            self.ins.append(rec)
            return None
        else:
            info = self.plan['ins'][idx]
            e = self.eng[eng]
            for (sname, val) in info['waits']:
                s = self.sem[sname[1]] if sname[0] == 'e' else self.lane_sem[sname[1]]
                e.wait_ge(s, val)
            inst = fn(e)
            if dma:
                inst.then_inc(self.lane_sem[lane], 16)
            elif info['signal']:
                inst.then_inc(self.sem[eng], 1)
            return inst

    def make_plan(self):
        ins = self.ins
        signal = [False] * len(ins)
        for r in ins:
            for j in r['deps']:
                signal[j] = True
        cnt = dict(pe=0, dve=0, act=0, pool=0, sp=0)
        sigval = [0] * len(ins)
        for i, r in enumerate(ins):
            if signal[i] and not r['dma']:
                cnt[r['eng']] += 1
                sigval[i] = cnt[r['eng']]
        seen = {e: {} for e in cnt}
        out = []
        for i, r in enumerate(ins):
            need = {}
            for j in r['deps']:
                k = ('e', ins[j]['eng'])
                need[k] = max(need.get(k, 0), sigval[j])
            for ln, v in r['lanewaits'].items():
                k = ('l', ln)
                need[k] = max(need.get(k, 0), v)
            waits = []
            sd = seen[r['eng']]
            for k, v in need.items():
                if sd.get(k, 0) >= v:
                    continue
                sd[k] = v
                waits.append((k, v))
            out.append(dict(waits=waits, signal=signal[i]))
        return dict(ins=out, lanes=sorted(self.lane_cnt.keys()), lane_final=dict(self.lane_cnt))

    def finish(self):
        if self.rec:
            return
        for ln, v in self.plan['lane_final'].items():
            self.nc.sync.wait_ge(self.lane_sem[ln], v)

    def dma(self, out, in_, lane, q='sp', **kw):
        return self.op(q, lambda e: e.dma_start(out=out, in_=in_, **kw), reads=[in_], writes=[out],
                       dma=True, lane=lane)

    def mm(self, out, lhsT, rhs, start=True, stop=True, **kw):
        return self.op('pe', lambda e: e.matmul(out, lhsT, rhs, start=start, stop=stop, **kw),
                       reads=[lhsT, rhs], writes=[out])

    def transpose(self, out, in_, ident):
        return self.op('pe', lambda e: e.transpose(out, in_, ident), reads=[in_, ident], writes=[out])

    def act(self, out, in_, func, bias=None, scale=None, accum_out=None, eng='act'):
        reads = [in_]
        kw = {}
        if bias is not None:
            kw['bias'] = bias
            if not isinstance(bias, (int, float)):
                reads.append(bias)
        if scale is not None:
            kw['scale'] = scale
            if not isinstance(scale, (int, float)):
                reads.append(scale)
        writes = [out]
        if accum_out is not None:
            kw['accum_out'] = accum_out
            writes.append(accum_out)
        return self.op(eng, lambda e: e.activation(out=out, in_=in_, func=func, **kw), reads=reads, writes=writes)

    def tt(self, eng, out, in0, in1, op):
        return self.op(eng, lambda e: e.tensor_tensor(out=out, in0=in0, in1=in1, op=op), reads=[in0, in1], writes=[out])

    def ts(self, eng, out, in0, s1, s2, op0, op1=None, accum_out=None):
        reads = [in0]
        if not isinstance(s1, (int, float)):
            reads.append(s1)
        if s2 is not None and not isinstance(s2, (int, float)):
            reads.append(s2)
        kw = {}
        writes = [out]
        if op1 is not None:
            kw['op1'] = op1
        if accum_out is not None:
            kw['accum_out'] = accum_out
            writes.append(accum_out)
        return self.op(eng, lambda e: e.tensor_scalar(out=out, in0=in0, scalar1=s1, scalar2=s2, op0=op0, **kw),
                       reads=reads, writes=writes)

    def stt(self, eng, out, in0, scalar, in1, op0, op1):
        reads = [in0, in1]
        if not isinstance(scalar, (int, float)):
            reads.append(scalar)
        return self.op(eng, lambda e: e.scalar_tensor_tensor(out=out, in0=in0, scalar=scalar, in1=in1, op0=op0, op1=op1),
                       reads=reads, writes=[out])

    def copy(self, eng, out, in_):
        if eng == 'act':
            return self.op(eng, lambda e: e.copy(out=out, in_=in_), reads=[in_], writes=[out])
        return self.op(eng, lambda e: e.tensor_copy(out=out, in_=in_), reads=[in_], writes=[out])

    def memset(self, eng, ap, val):
        return self.op(eng, lambda e: e.memset(ap, val), reads=[], writes=[ap])

    def scan(self, out, d0, d1, initial, op0, op1):
        reads = [d0, d1]
        if not isinstance(initial, (int, float)):
            reads.append(initial)
        return self.op('dve', lambda e: e.tensor_tensor_scan(out=out, data0=d0, data1=d1, initial=initial, op0=op0, op1=op1),
                       reads=reads, writes=[out])

    def generic(self, eng, fn, reads, writes):
        return self.op(eng, fn, reads=reads, writes=writes)


def build_two_pass(make_nc, body):
    nc1 = make_nc()
    p1 = Prog(nc1, None)
    body(p1)
    p1.stack.close()
    plan = p1.make_plan()
    nc2 = make_nc()
    p2 = Prog(nc2, plan)
    body(p2)
    p2.finish()
    p2.stack.close()
    return nc2, plan


T = 4096
NT = 32
D = 1024
KC = 8
TOKC = 2048
TC = 1152
WCOLS = TOKC + TC
EPS = 1e-6


def phase1(P, st, A, layer):
    nc = P.nc
    s = ExitStack()
    W = P.sb(s, "w_in", [128, KC, WCOLS], BF16)
    hT = P.sb(s, "hT", [128, KC, T], BF16)
    ident = P.sb(s, "ident", [128, 128], BF16)
    g1 = P.sb(s, "g1", [128, KC], F32)
    G = P.sb(s, "Gq", [128, 1280], F32)
    cos = P.sb(s, "cos", [128, NT, 8], F32)
    sin = P.sb(s, "sin", [128, NT, 8], F32)
    P.dma(ident[:], A['ident'], 'c0')
    P.dma(g1[:], A['g1'][layer], 'c0')
    P.dma(G[:], A['gq'][layer].partition_broadcast(128), 'c0')
    P.dma(cos[:], A['cos'], 'c0')
    P.dma(sin[:], A['sin'], 'c0')
    P.ts('dve', G[:, 0:512], G[:, 0:512], 0.125, None, ALU.mult)
    P.ts('dve', G[:, 768:1024], G[:, 768:1024], 0.125, None, ALU.mult)

    with ExitStack() as s2:
        wst = [P.sb(s2, "wst%d" % i, [128, WCOLS], F32) for i in range(2)]
        for kc in range(KC):
            b = wst[kc % 2]
            P.dma(b[:], A['win'][layer, kc * 128:(kc + 1) * 128, :], 'wst%d' % (kc % 2), q='sp' if kc % 2 == 0 else 'act')
            half = WCOLS // 2
            P.ts('dve', W[:, kc, 0:half], b[:, 0:half], g1[:, kc:kc + 1], None, ALU.mult)
            P.act(W[:, kc, half:WCOLS], b[:, half:WCOLS], AF.Copy, scale=g1[:, kc:kc + 1])

    with ExitStack() as s2:
        xt = [P.sb(s2, "xt%d" % i, [128, D], F32) for i in range(2)]
        sq = P.sb(s2, "sqj", [128, D], F32)
        hb = [P.sb(s2, "hb%d" % i, [128, D], BF16) for i in range(2)]
        ss = [P.sb(s2, "ss%d" % i, [128, 2], F32) for i in range(2)]
        ptr = [P.ps(s2, "ptr%d" % i, [128, KC, 128], BF16) for i in range(2)]
        for t in range(NT):
            b = t % 2
            P.dma(xt[b][:], A['x'][t * 128:(t + 1) * 128, :], 'xt%d' % b)
            P.act(sq[:], xt[b][:], AF.Square, accum_out=ss[b][:, 0:1])
            P.act(ss[b][:, 1:2], ss[b][:, 0:1], AF.Sqrt, bias=EPS_AP(P), scale=1.0 / D)
            P.op('dve', lambda e, o=ss[b][:, 1:2]: e.reciprocal(out=o, in_=o), reads=[ss[b][:, 1:2]], writes=[ss[b][:, 1:2]])
            P.ts('dve', hb[b][:], xt[b][:], ss[b][:, 1:2], None, ALU.mult)
            for kc in range(KC):
                P.transpose(ptr[b][:, kc, :], hb[b][:, kc * 128:(kc + 1) * 128], ident[:])
            P.copy('act' if t % 2 else 'dve', hT[:, :, t * 128:(t + 1) * 128], ptr[b][:])

    with ExitStack() as s2:
        pp = [P.ps(s2, "ppT%d" % i, [128, 512], F32) for i in range(3)]
        so = [P.sb(s2, "soT%d" % i, [128, 512], F32) for i in range(3)]
        k = 0
        for c in range(TC // 128):
            for j in range(T // 512):
                b = k % 3
                for kc in range(KC):
                    P.mm(pp[b][:], W[:, kc, TOKC + c * 128:TOKC + (c + 1) * 128], hT[:, kc, j * 512:(j + 1) * 512],
                         start=(kc == 0), stop=(kc == KC - 1))
                P.copy('act' if k % 2 else 'dve', so[b][:], pp[b][:])
                P.dma(A['pT'][c * 128:(c + 1) * 128, j * 512:(j + 1) * 512], so[b][:], 'soT%d' % b, q='pool')
                k += 1

    with ExitStack() as s2:
        pg = [P.ps(s2, "pg%d" % i, [128, 512], F32) for i in range(3)]
        ptq = [P.ps(s2, "ptq%d" % i, [128, 4, 128], BF16) for i in range(2)]
        sqh = P.sb(s2, "sqh", [128, 512], F32)
        ssh = [P.sb(s2, "ssh%d" % i, [128, 8], F32) for i in range(3)]
        xn = [P.sb(s2, "xn%d" % i, [128, 512], F32) for i in range(2)]
        qb = [P.sb(s2, "qb%d" % i, [128, 512], BF16) for i in range(2)]
        rt = P.sb(s2, "rt", [128, 4, 8, 8], F32)
        qst = [P.sb(s2, "qst%d" % i, [128, 10, 512], BF16) for i in range(2)]
        vst = [P.sb(s2, "vst%d" % i, [128, 768], BF16) for i in range(2)]
        k = 0
        kq = 0
        for t in range(NT):
            sb_ = (t // 4) % 2
            for gi in range(4):
                b = k % 3
                k += 1
                for kc in range(KC):
                    P.mm(pg[b][:], hT[:, kc, t * 128:(t + 1) * 128], W[:, kc, gi * 512:(gi + 1) * 512],
                         start=(kc == 0), stop=(kc == KC - 1))
                nh = [8, 8, 4, 0][gi]
                nr = [8, 4, 0, 0][gi]
                if nh:
                    w = nh * 64
                    q = kq % 2
                    kq += 1
                    P.act(sqh[:, 0:w], pg[b][:, 0:w], AF.Square)
                    P.op('dve', lambda e, o=ssh[b][:, 0:nh], i=sqh[:, 0:w].rearrange("p (h d) -> p h d", d=64):
                         e.tensor_reduce(out=o, in_=i, axis=AX.X, op=ALU.add),
                         reads=[sqh[:, 0:w]], writes=[ssh[b][:, 0:nh]])
                    P.act(ssh[b][:, 0:nh], ssh[b][:, 0:nh], AF.Sqrt, bias=EPS_AP(P), scale=1.0 / 64)
                    P.op('dve', lambda e, o=ssh[b][:, 0:nh]: e.reciprocal(out=o, in_=o), reads=[ssh[b][:, 0:nh]], writes=[ssh[b][:, 0:nh]])
                    P.tt('dve', xn[q][:, 0:w].rearrange("p (h d) -> p h d", d=64),
                         pg[b][:, 0:w].rearrange("p (h d) -> p h d", d=64),
                         ssh[b][:, 0:nh].unsqueeze(2).broadcast_to([128, nh, 64]), ALU.mult)
                    goff = [0, 512, 1024][gi]
                    if nr:
                        P.tt('pool', xn[q][:, 0:w], xn[q][:, 0:w], G[:, goff:goff + w], ALU.mult)
                        P.copy('act', qb[q][:, 0:w], xn[q][:, 0:w])
                        xv = xn[q][:, 0:nr * 64].rearrange("p (h d) -> p h d", d=64)
                        qv = qb[q][:, 0:nr * 64].rearrange("p (h d) -> p h d", d=64)
                        cb = cos[:, t, :].unsqueeze(1).broadcast_to([128, nr, 8])
                        sb2 = sin[:, t, :].unsqueeze(1).broadcast_to([128, nr, 8])
                        P.tt('dve', rt[:, 0, 0:nr, :], xv[:, :, 0:8], cb, ALU.mult)
                        P.tt('dve', rt[:, 1, 0:nr, :], xv[:, :, 8:16], sb2, ALU.mult)
                        P.tt('pool', rt[:, 2, 0:nr, :], xv[:, :, 8:16], cb, ALU.mult)
                        P.tt('pool', rt[:, 3, 0:nr, :], xv[:, :, 0:8], sb2, ALU.mult)
                        P.tt('dve', qv[:, :, 0:8], rt[:, 0, 0:nr, :], rt[:, 1, 0:nr, :], ALU.subtract)
                        P.tt('pool', qv[:, :, 8:16], rt[:, 2, 0:nr, :], rt[:, 3, 0:nr, :], ALU.add)
                    else:
                        P.tt('pool', qb[q][:, 0:w], xn[q][:, 0:w], G[:, goff:goff + w], ALU.mult)
                    npair = nh // 2
                    pbase = [0, 4, 8][gi]
                    for pr in range(npair):
                        P.transpose(ptq[q][:, pr, :], qb[q][:, pr * 128:(pr + 1) * 128], ident[:])
                    P.copy('act' if gi % 2 else 'dve', qst[sb_][:, pbase:pbase + npair, (t % 4) * 128:(t % 4 + 1) * 128], ptq[q][:, 0:npair, :])
                vb = t % 2
                if gi == 2:
                    P.copy('act', vst[vb][:, 0:256], pg[b][:, 256:512])
                if gi == 3:
                    P.copy('act', vst[vb][:, 256:768], pg[b][:, 0:512])
                    P.dma(A['vtok'][t * 128:(t + 1) * 128, :], vst[vb][:], 'vst%d' % vb, q='pool')
            if t % 4 == 3:
                j = t // 4
                P.dma(A['qkT'][:, j * 512:(j + 1) * 512].rearrange("(a p) n -> p a n", p=128), qst[sb_][:], 'qst%d' % sb_, q='pool')
    s.close()


_eps_cache = {}


def EPS_AP(P):
    return EPS


T = 4096
NEG = -30000.0


class AttnCtx:
    def __init__(self, P, st, consts):
        self.P = P
        self.psS = [P.ps(st, "aS%d" % i, [128, 512], F32) for i in range(2)]
        self.psO = [P.ps(st, "aO%d" % i, [128, 512], F32) for i in range(2)]
        self.psB = [P.ps(st, "aB%d" % i, [128, 512], F32) for i in range(2)]
        self.pT = [P.sb(st, "apT%d" % i, [128, 512], BF16) for i in range(3)]
        self.lr = [P.sb(st, "alr%d" % i, [65, 512], F32) for i in range(2)]
        self.F = [P.sb(st, "aF%d" % i, [64, 512], F32) for i in range(2)]
        self.G2 = [P.sb(st, "aG%d" % i, [64, 512], F32) for i in range(2)]
        self.kS = 0
        self.kO = 0
        self.kF = 0
        self.c = consts


def attn_chunk(cx, Kaug, kr, Qaug, j, Vaug, entries, lngT=None, gate_c=None, extra=None, nofactor=False):
    P = cx.P
    c = cx.c
    po = cx.psO[cx.kO % 2]
    cx.kO += 1
    q0 = j * 512
    P.mm(po[0:65, :], c['zeros'][:, 0:65], cx.c['ident_w'][:, 0:512], start=True, stop=False)
    n = len(entries)
    for ei, (kt, lo, hi, masks) in enumerate(entries):
        ps = cx.psS[cx.kS % 2]
        pt = cx.pT[cx.kS % 3]
        cx.kS += 1
        P.mm(ps[:, lo:hi], Kaug[0:kr, kt * 128:(kt + 1) * 128], Qaug[0:kr, q0 + lo:q0 + hi], start=True, stop=(len(masks) == 0 and extra is None))
        if extra is not None:
            P.mm(ps[:, lo:hi], extra[0][0:64, kt * 128:(kt + 1) * 128], extra[1][0:64, q0 + lo:q0 + hi], start=False, stop=(len(masks) == 0))
        for mi, (mk, m) in enumerate(masks):
            P.mm(ps[:, m * 128:(m + 1) * 128], c['ident'][:], mk[:], start=False, stop=(mi == len(masks) - 1))
        P.act(pt[:, lo:hi], ps[:, lo:hi], AF.Exp)
        P.mm(po[0:65, lo:hi], Vaug[:, kt, :], pt[:, lo:hi], start=False, stop=(ei == n - 1))
    if nofactor:
        return po
    lr = cx.lr[cx.kF % 2]
    F = cx.F[cx.kF % 2]
    cx.kF += 1
    pb = cx.psB[0]
    P.ts('dve', lr[64:65, :], po[64:65, :], 1e-18, None, ALU.max)
    P.act(lr[64:65, :], lr[64:65, :], AF.Ln)
    P.mm(pb[0:64, :], c['negones'][64:65, :], lr[64:65, :], start=True, stop=(lngT is None))
    if lngT is not None:
        P.mm(pb[0:64, :], c['selneg'][0:24, gate_c * 64:(gate_c + 1) * 64], lngT[0:24, q0:q0 + 512], start=False, stop=True)
    P.act(F[:], pb[0:64, :], AF.Exp)
    return po, F


def causal_entries(j, mc):
    ent = []
    for kt in range(4 * j + 4):
        if kt < 4 * j:
            ent.append((kt, 0, 512, []))
        else:
            m = kt - 4 * j
            ent.append((kt, 128 * m, 512, [(mc, m)]))
    return ent


def window_entries(j, mc, mu):
    ent = []
    for cc in range(-4, 4):
        kt = 4 * j + cc
        if kt < 0:
            continue
        lo = 128 * max(cc, 0)
        hi = 128 * (min(cc + 4, 3) + 1)
        masks = []
        if 0 <= cc <= 3:
            masks.append((mc, cc))
        if 0 <= cc + 4 <= 3:
            masks.append((mu, cc + 4))
        ent.append((kt, lo, hi, masks))
    return ent


def load_consts(P, st, A):
    c = {}
    c['ident'] = P.sb(st, "c_ident", [128, 128], BF16)
    c['mc'] = P.sb(st, "c_mc", [128, 128], BF16)
    c['mu'] = P.sb(st, "c_mu", [128, 128], BF16)
    c['zeros'] = P.sb(st, "c_zeros", [128, 128], BF16)
    c['ident_w'] = P.sb(st, "c_identw", [128, 512], BF16)
    c['negones'] = P.sb(st, "c_negones", [65, 64], F32)
    c['selneg'] = P.sb(st, "c_selneg", [24, 24 * 64], F32)
    P.dma(c['ident'][:], A['ident'], 'c0')
    P.dma(c['mc'][:], A['mc'], 'c0')
    P.dma(c['mu'][:], A['mu'], 'c0')
    P.dma(c['selneg'][:], A['selneg'], 'c0')
    P.memset('dve', c['zeros'][:], 0.0)
    P.memset('dve', c['ident_w'][:], 0.0)
    P.memset('dve', c['negones'][:], -1.0)
    return c


def phase_fox(P, A, layer, consts):
    with ExitStack() as st:
        cx = AttnCtx(P, st, consts)
        cf = P.sb(st, "f_cf", [4, T], F32)
        tmp = P.sb(st, "f_tmp", [4, T], F32)
        ones = P.sb(st, "f_ones", [4, T], F32)
        fb = P.sb(st, "f_fb", [4, 2], F32)
        cs = P.sb(st, "f_cs", [4, 3, T], BF16)
        ncs = P.sb(st, "f_ncs", [4, 3, T], BF16)
        P.dma(cf[:], A['pT'][1048:1052, :], 'fx0')
        P.dma(fb[:, 0:1], A['fb'][layer], 'fx0')
        P.ts('dve', fb[:, 1:2], fb[:, 0:1], -1.0, None, ALU.mult)
        P.memset('pool', ones[:], 1.0)
        P.act(tmp[:], cf[:], AF.Exp, bias=fb[:, 1:2], scale=-1.0)
        P.act(tmp[:], tmp[:], AF.Ln, bias=1.0)
        P.scan(cf[:], ones[:], tmp[:], 0.0, ALU.mult, ALU.subtract)
        P.copy('dve', cs[:, 0, :], cf[:])
        P.tt('dve', tmp[:], cf[:], cs[:, 0, :], ALU.subtract)
        P.copy('dve', cs[:, 1, :], tmp[:])
        P.tt('dve', tmp[:], tmp[:], cs[:, 1, :], ALU.subtract)
        P.copy('dve', cs[:, 2, :], tmp[:])
        P.ts('dve', ncs[:].rearrange("p a t -> p (a t)"), cs[:].rearrange("p a t -> p (a t)"), -1.0, None, ALU.mult)
        for h in range(4):
            with ExitStack() as s2:
                Q = P.sb(s2, "f_Q", [128, T], BF16)
                K = P.sb(s2, "f_K", [128, T], BF16)
                V = P.sb(s2, "f_V", [128, 32, 65], BF16)
                ob = [P.sb(s2, "f_ob%d" % i, [64, 512], BF16) for i in range(2)]
                P.dma(Q[0:64, :], A['qkT'][768 + 64 * h:768 + 64 * (h + 1), :], 'fxq')
                P.dma(K[0:64, :], A['qkT'][1024 + 64 * h:1024 + 64 * (h + 1), :], 'fxk')
                P.memset('pool', Q[64:70, :], 1.0)
                P.memset('pool', K[64:70, :], 1.0)
                for i in range(3):
                    P.dma(Q[64 + i:65 + i, :], cs[h:h + 1, i, :], 'fxq')
                    P.dma(K[67 + i:68 + i, :], ncs[h:h + 1, i, :], 'fxk')
                P.memset('pool', V[:, :, 64:65], 1.0)
                P.dma(V[:, :, 0:64], A['vtok'][:, 512 + 64 * h:512 + 64 * (h + 1)].rearrange("(n p) d -> p n d", p=128), 'fxv')
                for j in range(8):
                    po, F = attn_chunk(cx, K, 70, Q, j, V, causal_entries(j, consts['mc']))
                    o = ob[j % 2]
                    P.tt('dve', o[:], po[0:64, :], F[:], ALU.mult)
                    P.dma(A['mixT'][768 + 64 * h:768 + 64 * (h + 1), j * 512:(j + 1) * 512], o[:], 'fxo%d' % (j % 2), q='pool')

import math, os
STAGE = int(os.environ.get('STAGE', '99'))

T = 4096
LN8 = math.log(0.125)


def phase_hgrn(P, A, layer, consts):
    for ct in range(2):
        with ExitStack() as st:
            B = [P.sb(st, "hB%d" % i, [128, T], F32) for i in range(5)]
            qt = P.sb(st, "h_qt", [128, T], BF16)
            kt = P.sb(st, "h_kt", [128, 2, T], BF16)
            qg = P.sb(st, "h_qg", [128, T], BF16)
            kd = P.sb(st, "h_kd", [128, T], BF16)
            kdt = P.sb(st, "h_kdt", [128, 32, 2, 128], BF16)
            Vt = P.sb(st, "h_Vt", [128, 32, 128], BF16)
            Vz = P.sb(st, "h_Vz", [128, 32, 2, 128], BF16)
            Sbd = P.sb(st, "h_Sbd", [128, 64, 128], BF16)
            rst = P.sb(st, "h_rst", [128, T], BF16)
            sm = P.sb(st, "h_sm", [128, 8], F32)
            dl = P.sb(st, "h_dl", [128, 64], F32)
            mh = P.sb(st, "h_mh", [128, 128], BF16)
            bones = P.sb(st, "h_bones", [128, 128], BF16)
            ident = consts['ident']
            P.dma(mh[:], A['mh'], 'hg0')
            P.dma(bones[:], A['bones'], 'hg0')
            P.dma(sm[:, 0:2], A['lbl'][ct], 'hg0')
            P.dma(sm[:, 4:5], A['og'][layer], 'hg0')
            P.dma(B[0][:], A['pT'][256 + ct * 128:256 + (ct + 1) * 128, :], 'hgz')
            P.dma(B[3][:], A['pT'][ct * 128:(ct + 1) * 128, :], 'hgq', q='act')
            P.dma(Vt[:], A['vtok'][:, ct * 128:(ct + 1) * 128].rearrange("(n p) d -> p n d", p=128), 'hgv', q='pool')
            P.memset('pool', Vz[:], 0.0)
            for hh in range(2):
                P.dma(Vz[:, :, hh, hh * 64:(hh + 1) * 64],
                      A['vtok'][:, ct * 128 + hh * 64:ct * 128 + (hh + 1) * 64].rearrange("(n p) d -> p n d", p=128), 'hgv', q='pool')
            P.memset('pool', Sbd[:], 0.0)
            P.memset('pool', kt[:], 0.0)
            P.memset('pool', kdt[:], 0.0)
            P.memset('pool', rst[:], 1.0)
            P.memset('pool', rst[:].rearrange("p (c s) -> p c s", s=64)[:, :, 0:1], 0.0)
            lb = sm[:, 2:3]; oml = sm[:, 3:4]; noml = sm[:, 5:6]
            if layer == 0:
                P.memset('dve', lb, 0.0)
            else:
                P.act(sm[:, 0:2], sm[:, 0:2], AF.Exp)
                P.tt('dve', sm[:, 6:7], sm[:, 0:1], sm[:, 1:2], ALU.add)
                P.op('dve', lambda e, o=sm[:, 6:7]: e.reciprocal(out=o, in_=o), reads=[sm[:, 6:7]], writes=[sm[:, 6:7]])
                P.tt('dve', lb, sm[:, 1:2], sm[:, 6:7], ALU.mult)
            P.ts('dve', oml, lb, -1.0, 1.0, ALU.mult, ALU.add)
            P.ts('dve', noml, oml, -1.0, None, ALU.mult)
            P.act(B[0][:], B[0][:], AF.Sigmoid)
            P.ts('dve', B[1][:], B[0][:], oml, lb, ALU.mult, ALU.add)
            P.act(B[1][:], B[1][:], AF.Ln)
            P.scan(B[2][:], rst[:], B[1][:], 0.0, ALU.mult, ALU.add)
            P.ts('dve', B[1][:], B[0][:], noml, oml, ALU.mult, ALU.add)
            G3 = B[2][:].rearrange("p (c s) -> p c s", s=64)
            D3 = B[0][:].rearrange("p (c s) -> p c s", s=64)
            P.tt('dve', D3, G3, G3[:, :, 31:32].broadcast_to([128, 64, 64]), ALU.subtract)
            P.act(B[4][:], B[0][:], AF.Exp, bias=LN8)
            P.tt('dve', qt[:], B[3][:], B[4][:], ALU.mult)
            P.act(B[4][:], B[0][:], AF.Exp, scale=-1.0)
            P.tt('dve', kt[0:64, 0, :], B[1][0:64, :], B[4][0:64, :], ALU.mult)
            P.tt('dve', kt[64:128, 1, :], B[1][64:128, :], B[4][64:128, :], ALU.mult)
            P.act(B[4][:], B[2][:], AF.Exp, bias=LN8)
            P.tt('dve', qg[:], B[3][:], B[4][:], ALU.mult)
            P.tt('dve', D3, G3[:, :, 63:64].broadcast_to([128, 64, 64]), G3, ALU.subtract)
            P.act(B[4][:], B[0][:], AF.Exp)
            P.tt('dve', kd[:], B[1][:], B[4][:], ALU.mult)
            P.act(dl[:].unsqueeze(2), G3[:, :, 63:64], AF.Exp)
            P.memset('dve', dl[:, 0:1], 0.0)
            KV = B[0]; dfull = B[1]; Sall = B[3]; oT = B[4]
            if STAGE < 1:
                P.dma(A['mixT'][0:128, 0:T], kd[:], 'dbg'); continue
            with ExitStack() as s2:
                ptr = [P.ps(s2, "h_ptr%d" % i, [128, 8, 128], BF16) for i in range(2)]
                for g in range(4):
                    for i in range(8):
                        tl = g * 8 + i
                        P.transpose(ptr[g % 2][:, i, :], kd[:, tl * 128:(tl + 1) * 128], ident[:])
                    P.copy('act', kdt[0:64, g * 8:(g + 1) * 8, 0, :], ptr[g % 2][0:64, :, :])
                    P.copy('dve', kdt[64:128, g * 8:(g + 1) * 8, 1, :], ptr[g % 2][64:128, :, :])
            with ExitStack() as s2:
                pkv = [P.ps(s2, "h_pkv%d" % i, [128, 4, 128], F32) for i in range(2)]
                KV3 = KV[:].rearrange("p (v c) -> p v c", c=64)
                for g in range(16):
                    pk = pkv[g % 2]
                    for i in range(4):
                        c = g * 4 + i
                        tl = c // 2; hf = c % 2
                        P.mm(pk[:, i, :], kdt[:, tl, hf, :], Vt[:, tl, :], start=True, stop=True)
                    for hh in range(2):
                        P.copy('act' if hh else 'dve', KV3[hh * 64:(hh + 1) * 64, :, g * 4:(g + 1) * 4],
                               pk[hh * 64:(hh + 1) * 64, :, hh * 64:(hh + 1) * 64].rearrange("p g v -> p v g"))
            if STAGE < 2:
                P.dma(A['mixT'][0:128, 0:T], kd[:], 'dbg'); continue
            P.copy('pool', dfull[:].rearrange("p (v c) -> p v c", c=64), dl[:].unsqueeze(1).broadcast_to([128, 64, 64]))
            P.scan(Sall[:], dfull[:], KV[:], 0.0, ALU.mult, ALU.add)
            S3 = Sall[:].rearrange("p (v c) -> p v c", c=64)
            for hh in range(2):
                P.copy('dve' if hh else 'act', Sbd[hh * 64:(hh + 1) * 64, 1:64, hh * 64:(hh + 1) * 64],
                       S3[hh * 64:(hh + 1) * 64, :, 0:63].rearrange("p v c -> p c v"))
            if STAGE < 3:
                P.dma(A['mixT'][0:128, 0:T], kd[:], 'dbg'); continue
            with ExitStack() as s2:
                pA = [P.ps(s2, "h_pA%d" % i, [128, 128], F32) for i in range(4)]
                po = [P.ps(s2, "h_po%d" % i, [128, 128], F32) for i in range(2)]
                Am = [P.sb(s2, "h_Am%d" % i, [128, 128], BF16) for i in range(4)]
                for tl in range(32):
                    cols = slice(tl * 128, (tl + 1) * 128)
                    for hh in range(2):
                        i = (tl % 2) * 2 + hh
                        P.mm(pA[i][:], kt[:, hh, cols], qt[:, cols], start=True, stop=True)
                        P.tt('dve', Am[i][:], pA[i][:], mh[:], ALU.mult)
                    p_ = po[tl % 2]
                    P.mm(p_[:], Vz[:, tl, 0, :], Am[(tl % 2) * 2][:], start=True, stop=False)
                    P.mm(p_[:], Vz[:, tl, 1, :], Am[(tl % 2) * 2 + 1][:], start=False, stop=False)
                    P.mm(p_[:, 0:64], Sbd[:, 2 * tl, :], qg[:, tl * 128:tl * 128 + 64], start=False, stop=False)
                    P.mm(p_[:, 64:128], Sbd[:, 2 * tl + 1, :], qg[:, tl * 128 + 64:tl * 128 + 128], start=False, stop=True)
                    P.copy('act', oT[:, cols], p_[:])
            if STAGE < 4:
                P.dma(A['mixT'][0:128, 0:T], kd[:], 'dbg'); continue
            with ExitStack() as s2:
                pss = [P.ps(s2, "h_pss%d" % i, [128, 512], F32) for i in range(2)]
                sq = [P.sb(s2, "h_sq%d" % i, [128, 512], BF16) for i in range(2)]
                rs = [P.sb(s2, "h_rs%d" % i, [128, 512], F32) for i in range(2)]
                ag = [P.sb(s2, "h_ag%d" % i, [128, 512], F32) for i in range(2)]
                ob = [P.sb(s2, "h_ob%d" % i, [128, 512], BF16) for i in range(2)]
                for j in range(8):
                    b = j % 2
                    cols = slice(j * 512, (j + 1) * 512)
                    P.dma(ag[b][:], A['pT'][512 + ct * 128:512 + (ct + 1) * 128, cols], 'hga%d' % b)
                    P.act(sq[b][:], oT[:, cols], AF.Square)
                    P.mm(pss[b][:], bones[:], sq[b][:], start=True, stop=True)
                    P.act(rs[b][:], pss[b][:], AF.Sqrt, bias=1e-6, scale=1.0 / 64)
                    P.op('dve', lambda e, o=rs[b][:]: e.reciprocal(out=o, in_=o), reads=[rs[b][:]], writes=[rs[b][:]])
                    P.act(ag[b][:], ag[b][:], AF.Silu)
                    P.stt('dve', rs[b][:], oT[:, cols], sm[:, 4:5], rs[b][:], ALU.mult, ALU.mult)
                    P.tt('pool', ob[b][:], rs[b][:], ag[b][:], ALU.mult)
                    P.dma(A['mixT'][ct * 128:(ct + 1) * 128, cols], ob[b][:], 'hgo%d' % b, q='pool')

import os
STAGE = int(os.environ.get('STAGE', '99'))

T = 4096
NEG = -30000.0


def phase_nsa(P, A, layer, consts):
    c = consts
    ident = c['ident']
    with ExitStack() as st:
        cx = AttnCtx(P, st, consts)
        lngh = P.sb(st, "n_lngh", [24, T], BF16)
        lngl = P.sb(st, "n_lngl", [24, T], BF16)
        c['selnegb'] = P.sb(st, "n_selnegb", [24, 24 * 64], BF16)
        P.copy('dve', c['selnegb'][:], c['selneg'][:])
        with ExitStack() as s0:
            lng = P.sb(s0, "n_lng", [24, T], F32)
            P.dma(lng[:], A['pT'][1024:1048, :], 'ns0')
            P.act(lng[:], lng[:], AF.Exp, scale=-1.0)
            P.act(lng[:], lng[:], AF.Ln, bias=1.0)
            P.copy('dve', lngh[:], lng[:])
            P.tt('dve', lng[:], lng[:], lngh[:], ALU.subtract)
            P.copy('dve', lngl[:], lng[:])
        lng2 = (lngh, lngl)
        ovaug = P.sb(st, "n_ov", [128, 2, 72], BF16)
        wc = P.sb(st, "n_wc", [128, 3200], BF16)
        eall = P.sb(st, "n_eall", [64, T], BF16)
        addm = P.sb(st, "n_addm", [128, 32, 64], F32)
        P.dma(ovaug[:], A['ovaug'], 'ns0')
        P.dma(wc[:], A['wc'], 'ns0')
        P.dma(eall[:], A['eall'], 'ns0')
        P.dma(addm[:], A['addmask'], 'ns0')
        kcTs = [P.sb(st, "n_kcT%d" % i, [64, 256], BF16) for i in range(2)]
        vcAs = [P.sb(st, "n_vcA%d" % i, [128, 2, 65], BF16) for i in range(2)]
        for g in range(2):
            kcT = kcTs[g]; vcA = vcAs[g]
            with ExitStack() as s2:
                w1 = P.sb(s2, "n_w1", [64, 32, 128], BF16)
                w1f = P.sb(s2, "n_w1f", [64, 32, 128], F32)
                w2 = P.sb(s2, "n_w2", [128, 64], BF16)
                w2f = P.sb(s2, "n_w2f", [128, 64], F32)
                posT = P.sb(s2, "n_posT", [64, 32], BF16)
                posf = P.sb(s2, "n_posf", [64, 32], F32)
                posb = P.sb(s2, "n_posb", [64, 32, 256], BF16)
                srcf = P.sb(s2, "n_srcf", [64, T], F32)
                srcb = P.sb(s2, "n_srcb", [64, T], BF16)
                bias = P.sb(s2, "n_bias", [128, 1], F32)
                xb = P.sb(s2, "n_xb", [128, 256], F32)
                x2 = P.sb(s2, "n_x2", [128, 256], F32)
                hid = P.sb(s2, "n_hid", [128, 256], BF16)
                ktm = P.sb(s2, "n_ktm", [128, 64], F32)
                kts = P.sb(s2, "n_kts", [128, 64], F32)
                ktb = P.sb(s2, "n_ktb", [128, 128], BF16)
                sm = P.sb(s2, "n_sm", [128, 4], F32)
                rt = P.sb(s2, "n_rt", [128, 4, 8], F32)
                kng = P.sb(s2, "n_kng", [128, 64], F32)
                cosc = P.sb(s2, "n_cosc", [128, 2, 8], F32)
                sinc = P.sb(s2, "n_sinc", [128, 2, 8], F32)
                ph = cx.psS[0]; pb = cx.psS[1]; po = cx.psO[0]
                pt = P.ps(s2, "n_pt", [128, 128], BF16)
                P.dma(kng[:], A['kng'][layer].partition_broadcast(128), 'ns1')
                P.dma(cosc[:], A['cosc'], 'ns1')
                P.dma(sinc[:], A['sinc'], 'ns1')
                P.memset('dve', hid[:], 0.0)
                P.memset('dve', vcA[:], 0.0)
                P.memset('dve', kcT[:], 0.0)
                P.memset('dve', ktb[:], 0.0)
                for which in range(2):
                    P.dma(w1f[:], A['w1r'][layer, which], 'ns2')
                    P.dma(w2f[:], A['w2'][layer, which], 'ns2')
                    P.dma(posf[:], A['posT'][layer, which], 'ns2')
                    P.dma(srcf[:], A['pT'][768 + 128 * which + 64 * g:768 + 128 * which + 64 * (g + 1), :], 'ns3', q='act')
                    P.copy('dve', w1[:], w1f[:])
                    P.copy('dve', w2[:], w2f[:])
                    P.copy('dve', posT[:], posf[:])
                    P.copy('dve', posb[:], posT[:].unsqueeze(2).broadcast_to([64, 32, 256]))
                    P.copy('act', srcb[:], srcf[:])
                    for l in range(32):
                        P.mm(ph[:, 0:255], w1[:, l, :], srcb[:].rearrange("p (n s) -> p n s", s=16)[:, (l // 16):(l // 16) + 255, l % 16], start=(l == 0), stop=False)
                    for l in range(32):
                        P.mm(ph[:, 0:255], w1[:, l, :], posb[:, l, 0:255], start=False, stop=(l == 31))
                    P.copy('act', xb[:, 0:255], ph[:, 0:255])
                    P.tt('dve', x2[:, 0:255], xb[:, 0:255], xb[:, 0:255], ALU.mult)
                    P.ts('dve', x2[:, 0:255], x2[:, 0:255], 0.044715, 1.0, ALU.mult, ALU.add)
                    P.tt('dve', x2[:, 0:255], x2[:, 0:255], xb[:, 0:255], ALU.mult)
                    P.act(x2[:, 0:255], x2[:, 0:255], AF.Sigmoid, scale=1.5957691216057308)
                    P.tt('dve', hid[:, 0:255], x2[:, 0:255], xb[:, 0:255], ALU.mult)
                    for nt in range(2):
                        P.mm(po[:, 0:64], hid[:, nt * 128:(nt + 1) * 128], w2[:], start=True, stop=True)
                        if which == 1:
                            nr = 128 if nt == 0 else 127
                            P.copy('act', vcA[0:nr, nt, 0:64], po[0:nr, 0:64])
                            P.memset('dve', vcA[0:nr, nt, 64:65], 1.0)
                        else:
                            P.act(kts[:], po[:, 0:64], AF.Square, accum_out=sm[:, 0:1])
                            P.act(sm[:, 1:2], sm[:, 0:1], AF.Sqrt, bias=1e-6, scale=1.0 / 64)
                            P.op('dve', lambda e, o=sm[:, 1:2]: e.reciprocal(out=o, in_=o), reads=[sm[:, 1:2]], writes=[sm[:, 1:2]])
                            P.stt('dve', ktm[:], po[:, 0:64], sm[:, 1:2], kng[:], ALU.mult, ALU.mult)
                            P.copy('act', ktb[:, 0:64], ktm[:])
                            P.tt('dve', rt[:, 0, :], ktm[:, 0:8], cosc[:, nt, :], ALU.mult)
                            P.tt('dve', rt[:, 1, :], ktm[:, 8:16], sinc[:, nt, :], ALU.mult)
                            P.tt('dve', rt[:, 2, :], ktm[:, 8:16], cosc[:, nt, :], ALU.mult)
                            P.tt('dve', rt[:, 3, :], ktm[:, 0:8], sinc[:, nt, :], ALU.mult)
                            P.tt('dve', ktb[:, 0:8], rt[:, 0, :], rt[:, 1, :], ALU.subtract)
                            P.tt('dve', ktb[:, 8:16], rt[:, 2, :], rt[:, 3, :], ALU.add)
                            P.transpose(pt[:], ktb[:], ident[:])
                            P.copy('dve', kcT[:, nt * 128:(nt + 1) * 128], pt[0:64, :])
            P.memset('dve', kcT[:, 255:256], 0.0)
        if STAGE < 1:
            return
        selT = P.sb(st, "n_selT", [64, T], BF16)
        imp = P.sb(st, "n_imp", [128, 32, 64], F32)
        acc = [P.sb(st, "n_acc%d" % i, [64, T], F32) for i in range(4)]
        Q = [P.sb(st, "n_Q%d" % i, [64, T], BF16) for i in range(4)]
        Ks = P.sb(st, "n_Ks", [64, T], BF16)
        Kw = P.sb(st, "n_Kw", [64, T], BF16)
        Vs = P.sb(st, "n_Vs", [128, 32, 65], BF16)
        Vw = P.sb(st, "n_Vw", [128, 32, 65], BF16)
        for g in range(2):
            kcT = kcTs[g]; vcA = vcAs[g]
            for hh in range(4):
                h = 4 * g + hh
                P.dma(Q[hh][:], A['qkT'][64 * h:64 * (h + 1), :], 'nsq%d' % hh)
            P.dma(Ks[:], A['qkT'][512 + 64 * g:512 + 64 * (g + 1), :], 'nsk')
            P.dma(Kw[:], A['qkT'][640 + 64 * g:640 + 64 * (g + 1), :], 'nsk')
            P.memset('pool', Vs[:, :, 64:65], 1.0)
            P.memset('pool', Vw[:, :, 64:65], 1.0)
            P.dma(Vs[:, :, 0:64], A['vtok'][:, 256 + 64 * g:256 + 64 * (g + 1)].rearrange("(n p) d -> p n d", p=128), 'nsv', q='act')
            P.dma(Vw[:, :, 0:64], A['vtok'][:, 384 + 64 * g:384 + 64 * (g + 1)].rearrange("(n p) d -> p n d", p=128), 'nsv', q='act')
            with ExitStack() as s2:
                pimp = [P.ps(s2, "n_pimp%d" % i, [128, 4, 72], F32) for i in range(1)]
                pTc = [P.sb(s2, "n_pTc%d" % i, [128, 512], BF16) for i in range(2)]
                rinv = P.sb(s2, "n_rinv", [128, 8], F32)
                kc_ = 0
                for hh in range(4):
                    h = 4 * g + hh
                    for j in range(8):
                        q0 = j * 512
                        po = cx.psO[cx.kO % 2]
                        cx.kO += 1
                        P.mm(po[0:65, :], c['zeros'][:, 0:65], c['ident_w'][:, 0:512], start=True, stop=False)
                        tiles = []
                        for nt in range(2):
                            off = 2048 * nt + 31 - 512 * j
                            if -off + 511 < 0:
                                continue
                            tiles.append((nt, off))
                        pim = pimp[0]
                        P.mm(pim[:].rearrange("p a b -> p (a b)"), c['zeros'][:, 0:128], c['ident_w'][:, 0:288], start=True, stop=False)
                        for ti, (nt, off) in enumerate(tiles):
                            ps = cx.psS[cx.kS % 2]
                            cx.kS += 1
                            ptc = pTc[kc_ % 2]
                            kc_ += 1
                            full = (-off >= 2032)
                            P.mm(ps[:], kcT[:, nt * 128:(nt + 1) * 128], Q[hh][:, q0:q0 + 512], start=True, stop=full)
                            if not full:
                                ci0 = -off + 511
                                P.mm(ps[:], ident[:], wc[:, ci0:ci0 + 512], start=False, stop=True)
                            P.act(ptc[:], ps[:], AF.Exp)
                            P.mm(po[0:65, :], vcA[:, nt, :], ptc[:], start=False, stop=(ti == len(tiles) - 1))
                            for m in range(4):
                                P.mm(pim[:, m, :], ptc[:, m * 128:(m + 1) * 128], ovaug[:, nt, :], start=False, stop=(ti == len(tiles) - 1))
                        P.ts('dve', rinv[:, 0:4].unsqueeze(2), pim[:, :, 64:65], 1e-30, None, ALU.max)
                        P.op('dve', lambda e, o=rinv[:, 0:4]: e.reciprocal(out=o, in_=o), reads=[rinv[:, 0:4]], writes=[rinv[:, 0:4]])
                        for m in range(4):
                            tq = j * 4 + m
                            if hh == 0:
                                P.ts('dve', imp[:, tq, :], pim[:, m, 0:64], rinv[:, m:m + 1], None, ALU.mult)
                            else:
                                P.stt('dve', imp[:, tq, :], pim[:, m, 0:64], rinv[:, m:m + 1], imp[:, tq, :], ALU.mult, ALU.add)
                        F = _factor(cx, po, lng2, h, q0)
                        P.tt('dve', acc[hh][:, q0:q0 + 512], po[0:64, :], F[:], ALU.mult)
            if STAGE < 2:
                continue
            with ExitStack() as s2:
                wk = [P.sb(s2, "n_wk%d" % i, [128, 64], F32) for i in range(2)]
                w2_ = [P.sb(s2, "n_wk2%d" % i, [128, 64], F32) for i in range(2)]
                m8 = [P.sb(s2, "n_m8%d" % i, [128, 16], F32) for i in range(2)]
                sb_ = [P.sb(s2, "n_sb%d" % i, [128, 64], BF16) for i in range(2)]
                pts = [P.ps(s2, "n_pts%d" % i, [128, 128], BF16) for i in range(2)]
                for tq in range(32):
                    b = tq % 2
                    P.tt('dve', wk[b][:], imp[:, tq, :], addm[:, tq, :], ALU.add)
                    P.op('dve', lambda e, o=m8[b][:, 0:8], i=wk[b][:]: e.max(out=o, in_=i), reads=[wk[b][:]], writes=[m8[b][:, 0:8]])
                    P.op('dve', lambda e, o=w2_[b][:], r=m8[b][:, 0:8], i=wk[b][:]: e.match_replace(out=o, in_to_replace=r, in_values=i, imm_value=-3.0e38),
                         reads=[m8[b][:, 0:8], wk[b][:]], writes=[w2_[b][:]])
                    P.op('dve', lambda e, o=m8[b][:, 8:16], i=w2_[b][:]: e.max(out=o, in_=i), reads=[w2_[b][:]], writes=[m8[b][:, 8:16]])
                    P.ts('dve', w2_[b][:], wk[b][:], m8[b][:, 15:16], None, ALU.is_ge)
                    P.ts('dve', wk[b][:], wk[b][:], -5.0e29, None, ALU.is_gt)
                    P.tt('dve', wk[b][:], wk[b][:], w2_[b][:], ALU.mult)
                    P.ts('dve', sb_[b][:], wk[b][:], -1.0, -NEG, ALU.add, ALU.mult)
                    P.transpose(pts[b][0:64, :], sb_[b][:], ident[:])
                    P.copy('act', selT[:, tq * 128:(tq + 1) * 128], pts[b][0:64, :])
            if STAGE < 3:
                continue
            with ExitStack() as s2:
                tmp = [P.sb(s2, "n_tmp%d" % i, [64, 512], F32) for i in range(1)] * 2
                ob = [P.sb(s2, "n_ob%d" % i, [64, 512], BF16) for i in range(1)] * 2
                for hh in range(4):
                    h = 4 * g + hh
                    for j in range(8):
                        q0 = j * 512
                        a = acc[hh][:, q0:q0 + 512]
                        po = attn_chunk(cx, Ks, 64, Q[hh], j, Vs, causal_entries(j, c['mc']), extra=(eall, selT), nofactor=True)
                        F = _factor(cx, po, lng2, 8 + h, q0)
                        P.tt('dve', tmp[0][:], po[0:64, :], F[:], ALU.mult)
                        P.tt('pool', a, a, tmp[0][:], ALU.add)
                        po = attn_chunk(cx, Kw, 64, Q[hh], j, Vw, window_entries(j, c['mc'], c['mu']), nofactor=True)
                        F = _factor(cx, po, lng2, 16 + h, q0)
                        P.tt('dve', tmp[1][:], po[0:64, :], F[:], ALU.mult)
                        P.tt('pool', ob[j % 2][:], a, tmp[1][:], ALU.add)
                        P.dma(A['mixT'][256 + 64 * h:256 + 64 * (h + 1), q0:q0 + 512], ob[j % 2][:], 'nso%d' % (j % 2), q='pool')


def _factor(cx, po, lng, gate_c, q0):
    P = cx.P
    c = cx.c
    lr = cx.lr[cx.kF % 2]
    F = cx.F[cx.kF % 2]
    cx.kF += 1
    pb = cx.psB[0]
    P.ts('dve', lr[64:65, :], po[64:65, :], 1e-18, None, ALU.max)
    P.act(lr[64:65, :], lr[64:65, :], AF.Ln)
    pb2 = cx.psB[1]
    G2 = cx.G2[cx.kF % 2]
    P.mm(pb[0:64, :], c['negones'][64:65, :], lr[64:65, :], start=True, stop=True)
    P.mm(pb2[0:64, :], c['selnegb'][0:24, gate_c * 64:(gate_c + 1) * 64], lng[0][0:24, q0:q0 + 512], start=True, stop=False)
    P.mm(pb2[0:64, :], c['selnegb'][0:24, gate_c * 64:(gate_c + 1) * 64], lng[1][0:24, q0:q0 + 512], start=False, stop=True)
    P.act(F[:], pb[0:64, :], AF.Exp)
    P.act(G2[:], pb2[0:64, :], AF.Exp)
    P.tt('pool', F[:], F[:], G2[:], ALU.mult)
    return F


T = 4096
D = 1024
FF = 4096


def phase_wo(P, A, layer, consts, x_in, x_mid):
    ident = consts['ident']
    with ExitStack() as st:
        Wo = P.sb(st, "wo", [128, 8, D], BF16)
        with ExitStack() as s2:
            wst = [P.sb(s2, "wost%d" % i, [128, D], F32) for i in range(2)]
            for kc in range(8):
                b = wst[kc % 2]
                P.dma(b[:], A['wo'][layer, kc * 128:(kc + 1) * 128, :], 'wost%d' % (kc % 2))
                P.copy('act' if kc % 2 else 'dve', Wo[:, kc, :], b[:])
        mx = [P.sb(st, "wo_mx%d" % i, [128, 8, 512], BF16) for i in range(2)]
        xt = [P.sb(st, "wo_xt%d" % i, [128, D], F32) for i in range(2)]
        xm = [P.sb(st, "wo_xm%d" % i, [128, D], F32) for i in range(2)]
        sq = P.sb(st, "wo_sq", [128, D], F32)
        hb = [P.sb(st, "wo_hb%d" % i, [128, D], BF16) for i in range(2)]
        ss = [P.sb(st, "wo_ss%d" % i, [128, 2], F32) for i in range(2)]
        hst = [P.sb(st, "wo_hst%d" % i, [128, 8, 512], BF16) for i in range(2)]
        po = [P.ps(st, "wo_po%d" % i, [128, 512], F32) for i in range(4)]
        ptr = [P.ps(st, "wo_ptr%d" % i, [128, 8, 128], BF16) for i in range(2)]
        for t in range(32):
            j = t // 4
            b = t % 2
            if t % 4 == 0:
                P.dma(mx[j % 2][:], A['mixT'][:, j * 512:(j + 1) * 512].rearrange("(a p) n -> p a n", p=128), 'womx%d' % (j % 2))
            P.dma(xt[b][:], x_in[t * 128:(t + 1) * 128, :], 'woxt%d' % b, q='act')
            for half in range(2):
                pp = po[(t % 2) * 2 + half]
                for kc in range(8):
                    P.mm(pp[:], mx[j % 2][:, kc, (t % 4) * 128:(t % 4 + 1) * 128], Wo[:, kc, half * 512:(half + 1) * 512],
                         start=(kc == 0), stop=(kc == 7))
                P.tt('dve', xm[b][:, half * 512:(half + 1) * 512], pp[:], xt[b][:, half * 512:(half + 1) * 512], ALU.add)
            P.dma(x_mid[t * 128:(t + 1) * 128, :], xm[b][:], 'woxm%d' % b, q='pool')
            P.act(sq[:], xm[b][:], AF.Square, accum_out=ss[b][:, 0:1])
            P.act(ss[b][:, 1:2], ss[b][:, 0:1], AF.Sqrt, bias=1e-6, scale=1.0 / D)
            P.op('dve', lambda e, o=ss[b][:, 1:2]: e.reciprocal(out=o, in_=o), reads=[ss[b][:, 1:2]], writes=[ss[b][:, 1:2]])
            P.ts('dve', hb[b][:], xm[b][:], ss[b][:, 1:2], None, ALU.mult)
            for kc in range(8):
                P.transpose(ptr[b][:, kc, :], hb[b][:, kc * 128:(kc + 1) * 128], ident[:])
            P.copy('act', hst[j % 2][:, :, (t % 4) * 128:(t % 4 + 1) * 128], ptr[b][:])
            if t % 4 == 3:
                P.dma(A['h2T'][:, j * 512:(j + 1) * 512].rearrange("(a p) n -> p a n", p=128), hst[j % 2][:], 'wohst%d' % (j % 2), q='pool')


def phase_ffn(P, A, layer, consts, x_mid, x_out):
    with ExitStack() as st:
        Wu = P.sb(st, "wu", [128, 8, FF], BF16)
        Wd = P.sb(st, "wd", [128, 32, D], BF16)
        g2 = P.sb(st, "g2", [128, 8], F32)
        P.dma(g2[:], A['g2'][layer], 'ff0')
        with ExitStack() as s2:
            wst = [P.sb(s2, "fwst%d" % i, [128, 2048], F32) for i in range(3)]
            k = 0
            for kc in range(8):
                for hf in range(2):
                    b = k % 3
                    P.dma(wst[b][:], A['wup'][layer, kc * 128:(kc + 1) * 128, hf * 2048:(hf + 1) * 2048], 'fwst%d' % b, q='sp' if k % 2 else 'act')
                    if k % 2:
                        P.ts('dve', Wu[:, kc, hf * 2048:(hf + 1) * 2048], wst[b][:], g2[:, kc:kc + 1], None, ALU.mult)
                    else:
                        P.act(Wu[:, kc, hf * 2048:(hf + 1) * 2048], wst[b][:], AF.Copy, scale=g2[:, kc:kc + 1])
                    k += 1
            for fc2 in range(16):
                b = k % 3
                P.dma(wst[b][:].rearrange("p (a n) -> p a n", a=2), A['wdn'][layer, fc2 * 256:(fc2 + 1) * 256, :].rearrange("(a p) n -> p a n", p=128),
                      'fwst%d' % b, q='sp' if k % 2 else 'act')
                P.copy('dve' if k % 2 else 'act', Wd[:, fc2 * 2:(fc2 + 1) * 2, :], wst[b][:].rearrange("p (a n) -> p a n", a=2))
                k += 1
        h2 = [P.sb(st, "ff_h2%d" % i, [128, 8, 512], BF16) for i in range(2)]
        uT = P.sb(st, "ff_uT", [128, 32, 512], BF16)
        rl = [P.sb(st, "ff_rl%d" % i, [128, 512], F32) for i in range(2)]
        xt = [P.sb(st, "ff_xt%d" % i, [128, D], F32) for i in range(2)]
        xo = [P.sb(st, "ff_xo%d" % i, [128, D], F32) for i in range(2)]
        pu = [P.ps(st, "ff_pu%d" % i, [128, 512], F32) for i in range(3)]
        pd = [P.ps(st, "ff_pd%d" % i, [128, 512], F32) for i in range(4)]
        ku = 0
        for j in range(8):
            P.dma(h2[j % 2][:], A['h2T'][:, j * 512:(j + 1) * 512].rearrange("(a p) n -> p a n", p=128), 'ffh2%d' % (j % 2))
            for fc in range(32):
                pp = pu[ku % 3]
                r = rl[ku % 2]
                for kc in range(8):
                    P.mm(pp[:], Wu[:, kc, fc * 128:(fc + 1) * 128], h2[j % 2][:, kc, :], start=(kc == 0), stop=(kc == 7))
                P.act(r[:], pp[:], AF.Relu)
                P.tt('dve' if ku % 2 else 'pool', uT[:, fc, :], r[:], r[:], ALU.mult)
                ku += 1
            for tt in range(4):
                t = j * 4 + tt
                b = t % 2
                P.dma(xt[b][:], x_mid[t * 128:(t + 1) * 128, :], 'ffxt%d' % b, q='act')
                for half in range(2):
                    pp = pd[(t % 2) * 2 + half]
                    for fc in range(32):
                        P.mm(pp[:], uT[:, fc, tt * 128:(tt + 1) * 128], Wd[:, fc, half * 512:(half + 1) * 512], start=(fc == 0), stop=(fc == 31))
                    P.tt('dve', xo[b][:, half * 512:(half + 1) * 512], pp[:], xt[b][:, half * 512:(half + 1) * 512], ALU.add)
                P.dma(x_out[t * 128:(t + 1) * 128, :], xo[b][:], 'ffxo%d' % b, q='pool')

import ml_dtypes
from concourse.bass_utils import run_bass_kernel_spmd

T=4096; D=1024
OFF = {}
_names = ['aq','af','ai','ag','bq','bkc','bvc','bks','bvs','bkw','bvw','bg','cq','ck','cv','cf']
_sizes = [256,256,256,256,512,128,128,128,128,128,128,24,256,256,256,4]
_o = 0
for n_, s_ in zip(_names, _sizes):
    OFF[n_] = (_o, _o + s_); _o += s_
TOK_ORDER = ['bq','bks','bkw','cq','ck','ai','bvs','bvw','cv']
T_ORDER = ['aq','af','ag','bkc','bvc','bg','cf']

def win_layout(w_in):
    L = w_in.shape[0]
    out = np.zeros((L, 1024, 2048 + 1152), np.float32)
    c = 0
    for n_ in TOK_ORDER:
        a, b = OFF[n_]; out[:, :, c:c + b - a] = w_in[:, :, a:b]; c += b - a
    assert c == 2048
    for n_ in T_ORDER:
        a, b = OFF[n_]; out[:, :, c:c + b - a] = w_in[:, :, a:b]; c += b - a
    return out

def rope_tables():
    inv = np.power(np.float32(500000.0), -np.arange(0, 16, 2, dtype=np.float32) / 16).astype(np.float32)
    pos = np.arange(T, dtype=np.float32)
    ang = pos[:, None] * inv[None, :]
    cos = np.cos(ang).astype(np.float32); sin = np.sin(ang).astype(np.float32)
    return (np.ascontiguousarray(cos.reshape(32, 128, 8).transpose(1, 0, 2)),
            np.ascontiguousarray(sin.reshape(32, 128, 8).transpose(1, 0, 2)))

def _skip():
    pass

def const_inputs():
    k = np.arange(128)[:, None]; q = np.arange(128)[None, :]
    mc = np.where(k <= q, 0.0, -30000.0).astype(ml_dtypes.bfloat16)
    mu = np.where(k > q, 0.0, -30000.0).astype(ml_dtypes.bfloat16)
    selneg = np.zeros((24, 24 * 64), np.float32)
    for c in range(24):
        selneg[c, c * 64:(c + 1) * 64] = -1.0
    return dict(ident=np.eye(128, dtype=ml_dtypes.bfloat16), mc=mc, mu=mu, selneg=selneg)

def _unused_ref_proj(inp, layer, x):
    x = x.astype(np.float64)
    h = x / np.sqrt((x * x).mean(-1, keepdims=True) + 1e-6) * inp['norm1_g'][layer]
    return h @ inp['w_in'][layer].astype(np.float64)

def hgrn_consts(inp):
    s = np.arange(128)[:, None]; t = np.arange(128)[None, :]
    mh = ((s // 64 == t // 64) & (s <= t)).astype(ml_dtypes.bfloat16)
    bones = (s // 64 == t // 64).astype(ml_dtypes.bfloat16)
    lbl = np.ascontiguousarray(inp['hgrn_lb_logits'].reshape(2, 2, 128).transpose(1, 2, 0)).astype(np.float32)
    og = np.tile(inp['hgrn_onorm_g'], (1, 2)).reshape(2, 128, 1).astype(np.float32)
    return dict(mh=mh, bones=bones, lbl=lbl, og=og)

def _unused_ref_hgrn(inp, layer, proj):
    def sl(n): a, b = OFF[n]; return proj[:, a:b]
    lbp = np.exp(inp['hgrn_lb_logits'].astype(np.float64)); lbp /= lbp.sum(0, keepdims=True)
    lb_all = np.cumsum(lbp, 0) - lbp[0:1]
    lb = lb_all[layer].reshape(4, 64)
    z = sl('af').reshape(T, 4, 64)
    sig = 1 / (1 + np.exp(-z))
    f = lb + (1 - lb) * sig; logf = np.log(f); k = (1 - lb) * (1 - sig)
    q = sl('aq').reshape(T, 4, 64) * 0.125; v = sl('ai').reshape(T, 4, 64)
    o = np.zeros((T, 4, 64))
    for h in range(4):
        S = np.zeros((64, 64))
        for c in range(64):
            r = slice(c * 64, (c + 1) * 64)
            G = np.cumsum(logf[r, h], 0)
            qc, kc, vc = q[r, h], k[r, h], v[r, h]
            o_inter = (qc * np.exp(G)) @ S
            diff = G[:, None, :] - G[None, :, :]
            mask = np.tril(np.ones((64, 64), bool))
            dec = np.where(mask[:, :, None], np.exp(np.minimum(diff, 0)), 0)
            sc = np.einsum('tk,sk,tsk->ts', qc, kc, dec)
            o[r, h] = o_inter + sc @ vc
            S = S * np.exp(G[-1])[:, None] + (kc * np.exp(G[-1] - G)).T @ vc
    g = sl('ag').reshape(T, 4, 64)
    gate = g / (1 + np.exp(-g))
    on = o / np.sqrt((o * o).mean(-1, keepdims=True) + 1e-6) * inp['hgrn_onorm_g'][layer]
    return (on * gate).reshape(T, 256)

def nsa_consts(inp):
    n_cmp = 255
    ci = np.arange(n_cmp)[:, None]; sj = np.arange(64)[None, :]
    ov = ((ci * 16 <= sj * 64 + 63) & (ci * 16 + 31 >= sj * 64)).astype(np.float32)
    ovaug = np.zeros((256, 72), np.float32); ovaug[:255, :64] = ov; ovaug[:255, 64] = 1.0
    ovaug = np.ascontiguousarray(ovaug.reshape(2, 128, 72).transpose(1, 0, 2)).astype(ml_dtypes.bfloat16)
    nl = np.arange(128)[:, None]; cc = np.arange(3200)[None, :] - 511
    wc = np.where(cc >= 16 * nl, 0.0, -30000.0).astype(ml_dtypes.bfloat16)
    eall = (np.arange(T)[None, :] // 64 == np.arange(64)[:, None]).astype(ml_dtypes.bfloat16)
    q = np.arange(T)[:, None]; j = np.arange(64)[None, :]; cur = q // 64
    am = np.zeros((T, 64), np.float32)
    am[(j == 0) | (j == cur) | (j == cur - 1)] = 1e30
    am[np.broadcast_to(j > cur, am.shape)] = -1e30
    addmask = np.ascontiguousarray(am.reshape(32, 128, 64).transpose(1, 0, 2))
    inv = np.power(np.float32(500000.0), -np.arange(0, 16, 2, dtype=np.float32) / 16).astype(np.float32)
    pos = (np.arange(256, dtype=np.float32) * 16 + 31)
    ang = pos[:, None] * inv[None, :]
    cosc = np.ascontiguousarray(np.cos(ang).astype(np.float32).reshape(2, 128, 8).transpose(1, 0, 2))
    sinc = np.ascontiguousarray(np.sin(ang).astype(np.float32).reshape(2, 128, 8).transpose(1, 0, 2))
    w1r = np.ascontiguousarray(inp['nsa_cmp_w1'].reshape(2, 2, 32, 64, 128).transpose(0, 1, 3, 2, 4)).astype(np.float32)
    posT = np.ascontiguousarray(inp['nsa_cmp_pos'].transpose(0, 1, 3, 2)).astype(np.float32)
    return dict(ovaug=ovaug, wc=wc, eall=eall, addmask=addmask, cosc=cosc, sinc=sinc, w1r=w1r, posT=posT,
                w2=inp['nsa_cmp_w2'].astype(np.float32), kng=inp['nsa_kn_g'].astype(np.float32))

NSA_SHAPES = [('ovaug', [128, 2, 72], BF16), ('wc', [128, 3200], BF16), ('eall', [64, 4096], BF16), ('addmask', [128, 32, 64], F32),
              ('cosc', [128, 2, 8], F32), ('sinc', [128, 2, 8], F32), ('w1r', [2, 2, 64, 32, 128], F32), ('posT', [2, 2, 64, 32], F32),
              ('w2', [2, 2, 128, 64], F32), ('kng', [2, 64], F32)]


import ml_dtypes
from concourse.bass_utils import run_bass_kernel_spmd

_IN_SHAPES = [('x', [T, D], F32), ('win', [2, D, WCOLS], F32), ('g1', [2, 128, 8], F32), ('gq', [2, 1280], F32),
              ('cos', [128, 32, 8], F32), ('sin', [128, 32, 8], F32), ('ident', [128, 128], BF16), ('mc', [128, 128], BF16),
              ('mu', [128, 128], BF16), ('selneg', [24, 1536], F32), ('mh', [128, 128], BF16), ('bones', [128, 128], BF16),
              ('lbl', [2, 128, 2], F32), ('og', [2, 128, 1], F32), ('fb', [2, 4, 1], F32), ('wo', [2, 1024, 1024], F32),
              ('wup', [2, 1024, 4096], F32), ('wdn', [2, 4096, 1024], F32), ('g2', [2, 128, 8], F32)] + NSA_SHAPES

KDEPTH = int(os.environ.get('KDEPTH', '2'))
KPHASES = os.environ.get('KPHASES', '1hnfwf')


def _body(P):
    nc = P.nc
    A = {}
    for k_, shp, dt_ in _IN_SHAPES:
        A[k_] = nc.dram_tensor(k_, shp, dt_, kind="ExternalInput").ap()
    A['y'] = nc.dram_tensor("y", [T, D], F32, kind="ExternalOutput").ap()
    A['qkT'] = nc.dram_tensor("qkT", [1280, T], BF16).ap()
    A['vtok'] = nc.dram_tensor("vtok", [T, 768], BF16).ap()
    A['pT'] = nc.dram_tensor("pT", [TC, T], F32).ap()
    A['mixT'] = nc.dram_tensor("mixT", [1024, T], BF16).ap()
    A['h2T'] = nc.dram_tensor("h2T", [1024, T], BF16).ap()
    xm = nc.dram_tensor("xmid", [T, D], F32).ap()
    x1 = nc.dram_tensor("x1", [T, D], F32).ap()
    xin = A['x']
    for layer in range(KDEPTH):
        A['x'] = xin
        with ExitStack() as st:
            phase1(P, st, A, layer)
        with ExitStack() as st:
            consts = load_consts(P, st, A)
            if 'h' in KPHASES:
                phase_hgrn(P, A, layer, consts)
            if 'n' in KPHASES:
                phase_nsa(P, A, layer, consts)
            if 'f' in KPHASES:
                phase_fox(P, A, layer, consts)
            xout = x1 if layer < KDEPTH - 1 else A['y']
            phase_wo(P, A, layer, consts, xin, xm)
            phase_ffn(P, A, layer, consts, xm, xout)
        xin = xout


def _host_inputs(inp):
    cos, sin = rope_tables()
    gq = np.concatenate([np.tile(inp['nsa_qn_g'], (1, 8)), np.tile(inp['nsa_kn_g'], (1, 4)), np.tile(inp['fox_qn_g'], (1, 4)),
                         np.tile(inp['fox_kn_g'], (1, 4))], axis=1).astype(np.float32)
    base = {"win": win_layout(inp['w_in']), "g1": np.ascontiguousarray(inp['norm1_g'].reshape(2, 8, 128).transpose(0, 2, 1)),
            "g2": np.ascontiguousarray(inp['norm2_g'].reshape(2, 8, 128).transpose(0, 2, 1)),
            "gq": gq, "cos": cos, "sin": sin, "fb": inp['fox_fb'].reshape(2, 4, 1).astype(np.float32),
            "wo": inp['w_o'], "wup": inp['w_up'], "wdn": inp['w_down']}
    base.update(const_inputs()); base.update(hgrn_consts(inp)); base.update(nsa_consts(inp))
    return base


def kernel(**inp):
    inp = {k: np.asarray(v) for k, v in inp.items()}
    nc, plan = build_two_pass(lambda: bass.Bass("TRN2", target_bir_lowering=False), _body)
    base = _host_inputs(inp)
    in_maps = []
    for b in range(8):
        m = dict(base); m['x'] = np.ascontiguousarray(inp['x'][b]); in_maps.append(m)
    res = run_bass_kernel_spmd(nc, in_maps, core_ids=list(range(8)))
    return np.stack([r['y'] for r in res.results], axis=0).astype(np.float32)
```

```python
import numpy as np, sys, time, os, math
import numpy as np
from contextlib import ExitStack
import concourse.bass as bass
import concourse.mybir as mybir

F32 = mybir.dt.float32
BF16 = mybir.dt.bfloat16
AF = mybir.ActivationFunctionType
ALU = mybir.AluOpType
AX = mybir.AxisListType


def _box(ap):
    t = ap.tensor
    dims = ap.ap
    off = int(ap.offset)
    shp = tuple(t.shape)
    rowsize = 1
    for s in shp[1:]:
        rowsize *= int(s)
    r0 = off // rowsize
    f0 = off % rowsize
    rows = 0
    free = 0
    for (st, cnt) in dims:
        st = int(st); cnt = int(cnt)
        if cnt <= 1 or st == 0:
            continue
        if st % rowsize == 0:
            rows += (st // rowsize) * (cnt - 1)
        else:
            free += st * (cnt - 1)
    return t.name, (r0, r0 + rows, f0, f0 + free)


def _ov(a, b):
    return a[0] <= b[1] and b[0] <= a[1] and a[2] <= b[3] and b[2] <= a[3]


def _cont(a, b):
    return a[0] <= b[0] and b[1] <= a[1] and a[2] <= b[2] and b[3] <= a[3]


class Prog:
    def __init__(self, nc, plan=None):
        self.nc = nc
        self.plan = plan
        self.rec = plan is None
        self.eng = dict(pe=nc.tensor, dve=nc.vector, act=nc.scalar, pool=nc.gpsimd, sp=nc.sync)
        self.n = 0
        self.ins = []
        self.track = {}
        self.lane_cnt = {}
        self.freed = {}
        self.uid = 0
        self.stack = ExitStack()
        self.psum_rr = 0
        self.psum_banks = []
        if not self.rec:
            self.sem = {}
            for e in ['pe', 'dve', 'act', 'pool']:
                self.sem[e] = self.stack.enter_context(nc.semaphore("sem_" + e))
            self.lane_sem = {}
            for ln in plan['lanes']:
                self.lane_sem[ln] = self.stack.enter_context(nc.semaphore("ln_" + ln))

    def sb(self, st, name, shape, dtype):
        self.uid += 1
        name = "%s_%d" % (name, self.uid)
        t = st.enter_context(self.nc.sbuf_tensor("s_" + name, list(shape), dtype))
        st.callback(self._free, "s_" + name)
        return t

    def ps(self, st, name, shape, dtype=F32):
        self.uid += 1
        name = "%s_%d" % (name, self.uid)
        t = st.enter_context(self.nc.psum_tensor("p_" + name, list(shape), dtype))
        st.callback(self._free, "p_" + name)
        return t

    def _free(self, name):
        if not self.rec:
            return
        recs = self.track.pop(name, [])
        for (b, i, w) in recs:
            r = self.ins[i]
            key = ('l', r['lane'], i) if r['dma'] else ('e', r['eng'])
            if r['dma']:
                self.freed[key] = i
            else:
                self.freed[key] = max(self.freed.get(key, -1), i)

    def _access(self, idx, eng, dma, ap, write, deps):
        name, box = _box(ap)
        if name not in self.track:
            big = (0, 10 ** 9, 0, 10 ** 9)
            kind = ap.space
            self.track[name] = [] if str(kind) == 'DRAM' else [(big, i, True) for i in sorted(set(self.freed.values()))]
        recs = self.track[name]
        for (b, i, w) in recs:
            if (write or w) and _ov(b, box):
                deps.append((i, (w and not write)))
        if write:
            recs[:] = [r for r in recs if not _cont(box, r[0])]
        elif not dma:
            recs[:] = [r for r in recs if not ((not r[2]) and r[1] < len(self.ins) and self.ins[r[1]]['eng'] == eng
                                               and not self.ins[r[1]]['dma'] and _cont(box, r[0]))]
        recs.append((box, idx, write))

    def op(self, eng, fn, reads=(), writes=(), dma=False, lane=None):
        idx = self.n
        self.n += 1
        if self.rec:
            deps = []
            for ap in reads:
                self._access(idx, eng, dma, ap, False, deps)
            for ap in writes:
                self._access(idx, eng, dma, ap, True, deps)
            lanewaits = {}
            d2 = {}
            for (j, raw) in deps:
                if j == idx:
                    continue
                pj = self.ins[j]
                if pj['dma']:
                    ln = pj['lane']
                    lanewaits[ln] = max(lanewaits.get(ln, 0), pj['lane_val_at'])
                    lanewaits[ln] = max(lanewaits[ln], self.lane_cnt[ln])
                    continue
                if pj['eng'] == eng and not dma:
                    if eng == 'pe':
                        continue
                    if not raw:
                        continue
                d2[j] = True
            rec = dict(eng=eng, deps=list(d2.keys()), lanewaits=lanewaits, dma=dma, lane=lane)
            if dma:
                self.lane_cnt[lane] = self.lane_cnt.get(lane, 0) + 16
                rec['lane_val_at'] = self.lane_cnt[lane]
            self.ins.append(rec)
            return None
        else:
            info = self.plan['ins'][idx]
            e = self.eng[eng]
            for (sname, val) in info['waits']:
                s = self.sem[sname[1]] if sname[0] == 'e' else self.lane_sem[sname[1]]
                e.wait_ge(s, val)
            inst = fn(e)
            if dma:
                inst.then_inc(self.lane_sem[lane], 16)
            elif info['signal']:
                inst.then_inc(self.sem[eng], 1)
            return inst

    def make_plan(self):
        ins = self.ins
        signal = [False] * len(ins)
        for r in ins:
            for j in r['deps']:
                signal[j] = True
        cnt = dict(pe=0, dve=0, act=0, pool=0, sp=0)
        sigval = [0] * len(ins)
        for i, r in enumerate(ins):
            if signal[i] and not r['dma']:
                cnt[r['eng']] += 1
                sigval[i] = cnt[r['eng']]
        seen = {e: {} for e in cnt}
        out = []
        for i, r in enumerate(ins):
            need = {}
            for j in r['deps']:
                k = ('e', ins[j]['eng'])
                need[k] = max(need.get(k, 0), sigval[j])
            for ln, v in r['lanewaits'].items():
                k = ('l', ln)
                need[k] = max(need.get(k, 0), v)
            waits = []
            sd = seen[r['eng']]
            for k, v in need.items():
                if sd.get(k, 0) >= v:
                    continue
                sd[k] = v
                waits.append((k, v))
            out.append(dict(waits=waits, signal=signal[i]))
        return dict(ins=out, lanes=sorted(self.lane_cnt.keys()), lane_final=dict(self.lane_cnt))

    def finish(self):
        if self.rec:
            return
        for ln, v in self.plan['lane_final'].items():
            self.nc.sync.wait_ge(self.lane_sem[ln], v)

    def dma(self, out, in_, lane, q='sp', **kw):
        return self.op(q, lambda e: e.dma_start(out=out, in_=in_, **kw), reads=[in_], writes=[out],
                       dma=True, lane=lane)

    def mm(self, out, lhsT, rhs, start=True, stop=True, **kw):
        return self.op('pe', lambda e: e.matmul(out, lhsT, rhs, start=start, stop=stop, **kw),
                       reads=[lhsT, rhs], writes=[out])

    def transpose(self, out, in_, ident):
        return self.op('pe', lambda e: e.transpose(out, in_, ident), reads=[in_, ident], writes=[out])

    def act(self, out, in_, func, bias=None, scale=None, accum_out=None, eng='act'):
        reads = [in_]
        kw = {}
        if bias is not None:
            kw['bias'] = bias
            if not isinstance(bias, (int, float)):
                reads.append(bias)
        if scale is not None:
            kw['scale'] = scale
            if not isinstance(scale, (int, float)):
                reads.append(scale)
        writes = [out]
        if accum_out is not None:
            kw['accum_out'] = accum_out
            writes.append(accum_out)
        return self.op(eng, lambda e: e.activation(out=out, in_=in_, func=func, **kw), reads=reads, writes=writes)

    def tt(self, eng, out, in0, in1, op):
        return self.op(eng, lambda e: e.tensor_tensor(out=out, in0=in0, in1=in1, op=op), reads=[in0, in1], writes=[out])

    def ts(self, eng, out, in0, s1, s2, op0, op1=None, accum_out=None):
        reads = [in0]
        if not isinstance(s1, (int, float)):
            reads.append(s1)
        if s2 is not None and not isinstance(s2, (int, float)):
            reads.append(s2)
        kw = {}
        writes = [out]
        if op1 is not None:
            kw['op1'] = op1
        if accum_out is not None:
            kw['accum_out'] = accum_out
            writes.append(accum_out)
        return self.op(eng, lambda e: e.tensor_scalar(out=out, in0=in0, scalar1=s1, scalar2=s2, op0=op0, **kw),
                       reads=reads, writes=writes)

    def stt(self, eng, out, in0, scalar, in1, op0, op1):
        reads = [in0, in1]
        if not isinstance(scalar, (int, float)):
            reads.append(scalar)
        return self.op(eng, lambda e: e.scalar_tensor_tensor(out=out, in0=in0, scalar=scalar, in1=in1, op0=op0, op1=op1),
                       reads=reads, writes=[out])

    def copy(self, eng, out, in_):
        if eng == 'act':
            return self.op(eng, lambda e: e.copy(out=out, in_=in_), reads=[in_], writes=[out])
        return self.op(eng, lambda e: e.tensor_copy(out=out, in_=in_), reads=[in_], writes=[out])

    def memset(self, eng, ap, val):
        return self.op(eng, lambda e: e.memset(ap, val), reads=[], writes=[ap])

    def scan(self, out, d0, d1, initial, op0, op1):
        reads = [d0, d1]
        if not isinstance(initial, (int, float)):
            reads.append(initial)
        return self.op('dve', lambda e: e.tensor_tensor_scan(out=out, data0=d0, data1=d1, initial=initial, op0=op0, op1=op1),
                       reads=reads, writes=[out])

    def generic(self, eng, fn, reads, writes):
        return self.op(eng, fn, reads=reads, writes=writes)


def build_two_pass(make_nc, body):
    nc1 = make_nc()
    p1 = Prog(nc1, None)
    body(p1)
    p1.stack.close()
    plan = p1.make_plan()
    nc2 = make_nc()
    p2 = Prog(nc2, plan)
    body(p2)
    p2.finish()
    p2.stack.close()
    return nc2, plan


T = 4096
NT = 32
D = 1024
KC = 8
TOKC = 2048
TC = 1152
WCOLS = TOKC + TC
EPS = 1e-6


def phase1(P, st, A, layer):
    nc = P.nc
    s = ExitStack()
    W = P.sb(s, "w_in", [128, KC, WCOLS], BF16)
    hT = P.sb(s, "hT", [128, KC, T], BF16)
    ident = P.sb(s, "ident", [128, 128], BF16)
    g1 = P.sb(s, "g1", [128, KC], F32)
    G = P.sb(s, "Gq", [128, 1280], F32)
    cos = P.sb(s, "cos", [128, NT, 8], F32)
    sin = P.sb(s, "sin", [128, NT, 8], F32)
    P.dma(ident[:], A['ident'], 'c0')
    P.dma(g1[:], A['g1'][layer], 'c0')
    P.dma(G[:], A['gq'][layer].partition_broadcast(128), 'c0')
    P.dma(cos[:], A['cos'], 'c0')
    P.dma(sin[:], A['sin'], 'c0')
    P.ts('dve', G[:, 0:512], G[:, 0:512], 0.125, None, ALU.mult)
    P.ts('dve', G[:, 768:1024], G[:, 768:1024], 0.125, None, ALU.mult)

    with ExitStack() as s2:
        wst = [P.sb(s2, "wst%d" % i, [128, WCOLS], F32) for i in range(2)]
        for kc in range(KC):
            b = wst[kc % 2]
            P.dma(b[:], A['win'][layer, kc * 128:(kc + 1) * 128, :], 'wst%d' % (kc % 2), q='sp' if kc % 2 == 0 else 'act')
            half = WCOLS // 2
            P.ts('dve', W[:, kc, 0:half], b[:, 0:half], g1[:, kc:kc + 1], None, ALU.mult)
            P.act(W[:, kc, half:WCOLS], b[:, half:WCOLS], AF.Copy, scale=g1[:, kc:kc + 1])

    with ExitStack() as s2:
        xt = [P.sb(s2, "xt%d" % i, [128, D], F32) for i in range(2)]
        sq = P.sb(s2, "sqj", [128, D], F32)
        hb = [P.sb(s2, "hb%d" % i, [128, D], BF16) for i in range(2)]
        ss = [P.sb(s2, "ss%d" % i, [128, 2], F32) for i in range(2)]
        ptr = [P.ps(s2, "ptr%d" % i, [128, KC, 128], BF16) for i in range(2)]
        for t in range(NT):
            b = t % 2
            P.dma(xt[b][:], A['x'][t * 128:(t + 1) * 128, :], 'xt%d' % b)
            P.act(sq[:], xt[b][:], AF.Square, accum_out=ss[b][:, 0:1])
            P.act(ss[b][:, 1:2], ss[b][:, 0:1], AF.Sqrt, bias=EPS_AP(P), scale=1.0 / D)
            P.op('dve', lambda e, o=ss[b][:, 1:2]: e.reciprocal(out=o, in_=o), reads=[ss[b][:, 1:2]], writes=[ss[b][:, 1:2]])
            P.ts('dve', hb[b][:], xt[b][:], ss[b][:, 1:2], None, ALU.mult)
            for kc in range(KC):
                P.transpose(ptr[b][:, kc, :], hb[b][:, kc * 128:(kc + 1) * 128], ident[:])
            P.copy('act' if t % 2 else 'dve', hT[:, :, t * 128:(t + 1) * 128], ptr[b][:])

    with ExitStack() as s2:
        pp = [P.ps(s2, "ppT%d" % i, [128, 512], F32) for i in range(3)]
        so = [P.sb(s2, "soT%d" % i, [128, 512], F32) for i in range(3)]
        k = 0
        for c in range(TC // 128):
            for j in range(T // 512):
                b = k % 3
                for kc in range(KC):
                    P.mm(pp[b][:], W[:, kc, TOKC + c * 128:TOKC + (c + 1) * 128], hT[:, kc, j * 512:(j + 1) * 512],
                         start=(kc == 0), stop=(kc == KC - 1))
                P.copy('act' if k % 2 else 'dve', so[b][:], pp[b][:])
                P.dma(A['pT'][c * 128:(c + 1) * 128, j * 512:(j + 1) * 512], so[b][:], 'soT%d' % b, q='pool')
                k += 1

    with ExitStack() as s2:
        pg = [P.ps(s2, "pg%d" % i, [128, 512], F32) for i in range(3)]
        ptq = [P.ps(s2, "ptq%d" % i, [128, 4, 128], BF16) for i in range(2)]
        sqh = P.sb(s2, "sqh", [128, 512], F32)
        ssh = [P.sb(s2, "ssh%d" % i, [128, 8], F32) for i in range(3)]
        xn = [P.sb(s2, "xn%d" % i, [128, 512], F32) for i in range(2)]
        qb = [P.sb(s2, "qb%d" % i, [128, 512], BF16) for i in range(2)]
        rt = P.sb(s2, "rt", [128, 4, 8, 8], F32)
        qst = [P.sb(s2, "qst%d" % i, [128, 10, 512], BF16) for i in range(2)]
        vst = [P.sb(s2, "vst%d" % i, [128, 768], BF16) for i in range(2)]
        k = 0
        kq = 0
        for t in range(NT):
            sb_ = (t // 4) % 2
            for gi in range(4):
                b = k % 3
                k += 1
                for kc in range(KC):
                    P.mm(pg[b][:], hT[:, kc, t * 128:(t + 1) * 128], W[:, kc, gi * 512:(gi + 1) * 512],
                         start=(kc == 0), stop=(kc == KC - 1))
                nh = [8, 8, 4, 0][gi]
                nr = [8, 4, 0, 0][gi]
                if nh:
                    w = nh * 64
                    q = kq % 2
                    kq += 1
                    P.act(sqh[:, 0:w], pg[b][:, 0:w], AF.Square)
                    P.op('dve', lambda e, o=ssh[b][:, 0:nh], i=sqh[:, 0:w].rearrange("p (h d) -> p h d", d=64):
                         e.tensor_reduce(out=o, in_=i, axis=AX.X, op=ALU.add),
                         reads=[sqh[:, 0:w]], writes=[ssh[b][:, 0:nh]])
                    P.act(ssh[b][:, 0:nh], ssh[b][:, 0:nh], AF.Sqrt, bias=EPS_AP(P), scale=1.0 / 64)
                    P.op('dve', lambda e, o=ssh[b][:, 0:nh]: e.reciprocal(out=o, in_=o), reads=[ssh[b][:, 0:nh]], writes=[ssh[b][:, 0:nh]])
                    P.tt('dve', xn[q][:, 0:w].rearrange("p (h d) -> p h d", d=64),
                         pg[b][:, 0:w].rearrange("p (h d) -> p h d", d=64),
                         ssh[b][:, 0:nh].unsqueeze(2).broadcast_to([128, nh, 64]), ALU.mult)
                    goff = [0, 512, 1024][gi]
                    if nr:
                        P.tt('pool', xn[q][:, 0:w], xn[q][:, 0:w], G[:, goff:goff + w], ALU.mult)
                        P.copy('act', qb[q][:, 0:w], xn[q][:, 0:w])
                        xv = xn[q][:, 0:nr * 64].rearrange("p (h d) -> p h d", d=64)
                        qv = qb[q][:, 0:nr * 64].rearrange("p (h d) -> p h d", d=64)
                        cb = cos[:, t, :].unsqueeze(1).broadcast_to([128, nr, 8])
                        sb2 = sin[:, t, :].unsqueeze(1).broadcast_to([128, nr, 8])
                        P.tt('dve', rt[:, 0, 0:nr, :], xv[:, :, 0:8], cb, ALU.mult)
                        P.tt('dve', rt[:, 1, 0:nr, :], xv[:, :, 8:16], sb2, ALU.mult)
                        P.tt('pool', rt[:, 2, 0:nr, :], xv[:, :, 8:16], cb, ALU.mult)
                        P.tt('pool', rt[:, 3, 0:nr, :], xv[:, :, 0:8], sb2, ALU.mult)
                        P.tt('dve', qv[:, :, 0:8], rt[:, 0, 0:nr, :], rt[:, 1, 0:nr, :], ALU.subtract)
                        P.tt('pool', qv[:, :, 8:16], rt[:, 2, 0:nr, :], rt[:, 3, 0:nr, :], ALU.add)
                    else:
                        P.tt('pool', qb[q][:, 0:w], xn[q][:, 0:w], G[:, goff:goff + w], ALU.mult)
                    npair = nh // 2
                    pbase = [0, 4, 8][gi]
                    for pr in range(npair):
                        P.transpose(ptq[q][:, pr, :], qb[q][:, pr * 128:(pr + 1) * 128], ident[:])
                    P.copy('act' if gi % 2 else 'dve', qst[sb_][:, pbase:pbase + npair, (t % 4) * 128:(t % 4 + 1) * 128], ptq[q][:, 0:npair, :])
                vb = t % 2
                if gi == 2:
                    P.copy('act', vst[vb][:, 0:256], pg[b][:, 256:512])
                if gi == 3:
                    P.copy('act', vst[vb][:, 256:768], pg[b][:, 0:512])
                    P.dma(A['vtok'][t * 128:(t + 1) * 128, :], vst[vb][:], 'vst%d' % vb, q='pool')
            if t % 4 == 3:
                j = t // 4
                P.dma(A['qkT'][:, j * 512:(j + 1) * 512].rearrange("(a p) n -> p a n", p=128), qst[sb_][:], 'qst%d' % sb_, q='pool')
    s.close()


_eps_cache = {}


def EPS_AP(P):
    return EPS


T = 4096
NEG = -30000.0


class AttnCtx:
    def __init__(self, P, st, consts):
        self.P = P
        self.psS = [P.ps(st, "aS%d" % i, [128, 512], F32) for i in range(2)]
        self.psO = [P.ps(st, "aO%d" % i, [128, 512], F32) for i in range(2)]
        self.psB = [P.ps(st, "aB%d" % i, [128, 512], F32) for i in range(2)]
        self.pT = [P.sb(st, "apT%d" % i, [128, 512], BF16) for i in range(3)]
        self.lr = [P.sb(st, "alr%d" % i, [65, 512], F32) for i in range(2)]
        self.F = [P.sb(st, "aF%d" % i, [64, 512], F32) for i in range(2)]
        self.lrh = [P.sb(st, "alrh%d" % i, [65, 512], BF16) for i in range(2)]
        self.lrl = [P.sb(st, "alrl%d" % i, [65, 512], BF16) for i in range(2)]
        self.G2 = [P.sb(st, "aG%d" % i, [64, 512], F32) for i in range(2)]
        self.kS = 0
        self.kO = 0
        self.kF = 0
        self.vm = 65
        self.prev = None
        self.deferred = []
        self.c = consts


def _push_block(cx, s_fn, exp_fn, pv_fn, first=False):
    if first:
        for f in cx.deferred:
            f()
        cx.deferred = []
    s_fn()
    d = cx.deferred
    cx.deferred = []
    if cx.prev is not None:
        e, p, epi = cx.prev
        e()
        p()
        if epi is not None:
            epi[0]()
            cx.deferred.append(epi[1])
    for f in d:
        f()
    cx.prev = (exp_fn, pv_fn, None)


def _end_chunk(cx, epi_a, epi_b):
    cx.prev = (cx.prev[0], cx.prev[1], (epi_a, epi_b))


def attn_flush(cx):
    d = cx.deferred
    cx.deferred = []
    if cx.prev is not None:
        e, p, epi = cx.prev
        e()
        p()
        if epi is not None:
            epi[0]()
            d.append(epi[1])
        cx.prev = None
    for f in d:
        f()


def _mk_block(cx, po, Kaug, kr, Qaug, q0, Vaug, kt, lo, hi, masks, extra, first, last):
    P = cx.P
    c = cx.c
    ps = cx.psS[cx.kS % 2]
    pt = cx.pT[cx.kS % 3]
    cx.kS += 1

    def s_fn():
        if first:
            P.mm(po[0:cx.vm, :], c['zeros'][:, 0:cx.vm], c['ident_w'][:, 0:512], start=True, stop=False)
        P.mm(ps[:, lo:hi], Kaug[0:kr, kt * 128:(kt + 1) * 128], Qaug[0:kr, q0 + lo:q0 + hi], start=True, stop=(len(masks) == 0 and extra is None))
        if extra is not None:
            P.mm(ps[:, lo:hi], extra[0][0:64, kt * 128:(kt + 1) * 128], extra[1][0:64, q0 + lo:q0 + hi], start=False, stop=(len(masks) == 0))
        for mi, (mk, m) in enumerate(masks):
            P.mm(ps[:, m * 128:(m + 1) * 128], c['ident'][:], mk[:], start=False, stop=(mi == len(masks) - 1))

    def exp_fn():
        P.act(pt[:, lo:hi], ps[:, lo:hi], AF.Exp)

    def pv_fn():
        P.mm(po[0:cx.vm, lo:hi], Vaug[:, kt, 0:cx.vm], pt[:, lo:hi], start=False, stop=last)

    return s_fn, exp_fn, pv_fn


def _mk_factor(cx, po, lng2, gate_c, q0, finish):
    P = cx.P
    c = cx.c
    lr = cx.lr[cx.kF % 2]
    F = cx.F[cx.kF % 2]
    G2 = cx.G2[cx.kF % 2]
    cx.kF += 1
    pb = cx.psB[0]
    pb2 = cx.psB[1]

    lrh = cx.lrh[(cx.kF - 1) % 2]
    lrl = cx.lrl[(cx.kF - 1) % 2]

    def epi_a():
        P.ts('dve', lr[64:65, :], po[64:65, :], 1e-18, None, ALU.max)
        P.act(lr[64:65, :], lr[64:65, :], AF.Ln)
        P.copy('dve', lrh[64:65, :], lr[64:65, :])
        P.tt('dve', lrl[64:65, :], lr[64:65, :], lrh[64:65, :], ALU.subtract)

    def epi_b():
        P.mm(pb[0:64, :], c['negonesb'][64:65, :], lrh[64:65, :], start=True, stop=False)
        P.mm(pb[0:64, :], c['negonesb'][64:65, :], lrl[64:65, :], start=False, stop=True)
        if lng2 is not None:
            P.mm(pb2[0:64, :], c['selnegb'][0:24, gate_c * 64:(gate_c + 1) * 64], lng2[0][0:24, q0:q0 + 512], start=True, stop=False)
            P.mm(pb2[0:64, :], c['selnegb'][0:24, gate_c * 64:(gate_c + 1) * 64], lng2[1][0:24, q0:q0 + 512], start=False, stop=True)
        P.act(F[:], pb[0:64, :], AF.Exp)
        if lng2 is not None:
            P.act(G2[:], pb2[0:64, :], AF.Exp)
            P.tt('pool', F[:], F[:], G2[:], ALU.mult)
        finish(po, F)

    return epi_a, epi_b


def attn_chunk(cx, Kaug, kr, Qaug, j, Vaug, entries, finish, lng2=None, gate_c=None, extra=None):
    po = cx.psO[cx.kO % 2]
    cx.kO += 1
    q0 = j * 512
    n = len(entries)
    for ei, (kt, lo, hi, masks) in enumerate(entries):
        fns = _mk_block(cx, po, Kaug, kr, Qaug, q0, Vaug, kt, lo, hi, masks, extra, ei == 0, ei == n - 1)
        _push_block(cx, *fns, first=(ei == 0))
    ea, eb = _mk_factor(cx, po, lng2, gate_c, q0, finish)
    _end_chunk(cx, ea, eb)


def causal_entries(j, mc):
    ent = []
    for kt in range(4 * j + 4):
        if kt < 4 * j:
            ent.append((kt, 0, 512, []))
        else:
            m = kt - 4 * j
            ent.append((kt, 128 * m, 512, [(mc, m)]))
    return ent


def window_entries(j, mc, mu):
    ent = []
    for cc in range(-4, 4):
        kt = 4 * j + cc
        if kt < 0:
            continue
        lo = 128 * max(cc, 0)
        hi = 128 * (min(cc + 4, 3) + 1)
        masks = []
        if 0 <= cc <= 3:
            masks.append((mc, cc))
        if 0 <= cc + 4 <= 3:
            masks.append((mu, cc + 4))
        ent.append((kt, lo, hi, masks))
    return ent


def load_consts(P, st, A):
    c = {}
    c['ident'] = P.sb(st, "c_ident", [128, 128], BF16)
    c['mc'] = P.sb(st, "c_mc", [128, 128], BF16)
    c['mu'] = P.sb(st, "c_mu", [128, 128], BF16)
    c['zeros'] = P.sb(st, "c_zeros", [128, 128], BF16)
    c['ident_w'] = P.sb(st, "c_identw", [128, 512], BF16)
    c['negones'] = P.sb(st, "c_negones", [65, 64], F32)
    c['negonesb'] = P.sb(st, "c_negonesb", [65, 64], BF16)
    c['selneg'] = P.sb(st, "c_selneg", [24, 24 * 64], F32)
    P.dma(c['ident'][:], A['ident'], 'c0')
    P.dma(c['mc'][:], A['mc'], 'c0')
    P.dma(c['mu'][:], A['mu'], 'c0')
    P.dma(c['selneg'][:], A['selneg'], 'c0')
    P.memset('dve', c['zeros'][:], 0.0)
    P.memset('dve', c['ident_w'][:], 0.0)
    P.memset('dve', c['negones'][:], -1.0)
    P.memset('dve', c['negonesb'][:], -1.0)
    return c


def phase_fox(P, A, layer, consts):
    with ExitStack() as st:
        cx = AttnCtx(P, st, consts)
        cf = P.sb(st, "f_cf", [4, T], F32)
        tmp = P.sb(st, "f_tmp", [4, T], F32)
        ones = P.sb(st, "f_ones", [4, T], F32)
        fb = P.sb(st, "f_fb", [4, 2], F32)
        cs = P.sb(st, "f_cs", [4, 3, T], BF16)
        ncs = P.sb(st, "f_ncs", [4, 3, T], BF16)
        P.dma(cf[:], A['pT'][1048:1052, :], 'fx0')
        P.dma(fb[:, 0:1], A['fb'][layer], 'fx0')
        P.ts('dve', fb[:, 1:2], fb[:, 0:1], -1.0, None, ALU.mult)
        P.memset('pool', ones[:], 1.0)
        P.act(tmp[:], cf[:], AF.Exp, bias=fb[:, 1:2], scale=-1.0)
        P.act(tmp[:], tmp[:], AF.Ln, bias=1.0)
        P.scan(cf[:], ones[:], tmp[:], 0.0, ALU.mult, ALU.subtract)
        P.copy('dve', cs[:, 0, :], cf[:])
        P.tt('dve', tmp[:], cf[:], cs[:, 0, :], ALU.subtract)
        P.copy('dve', cs[:, 1, :], tmp[:])
        P.tt('dve', tmp[:], tmp[:], cs[:, 1, :], ALU.subtract)
        P.copy('dve', cs[:, 2, :], tmp[:])
        P.ts('dve', ncs[:].rearrange("p a t -> p (a t)"), cs[:].rearrange("p a t -> p (a t)"), -1.0, None, ALU.mult)
        for h in range(4):
            with ExitStack() as s2:
                Q = P.sb(s2, "f_Q", [128, T], BF16)
                K = P.sb(s2, "f_K", [128, T], BF16)
                V = P.sb(s2, "f_V", [128, 32, 65], BF16)
                ob = [P.sb(s2, "f_ob%d" % i, [64, 512], BF16) for i in range(2)]
                P.dma(Q[0:64, :], A['qkT'][768 + 64 * h:768 + 64 * (h + 1), :], 'fxq')
                P.dma(K[0:64, :], A['qkT'][1024 + 64 * h:1024 + 64 * (h + 1), :], 'fxk')
                P.memset('pool', Q[64:128, :], 0.0)
                P.memset('pool', K[64:128, :], 0.0)
                P.memset('pool', Q[64:70, :], 1.0)
                P.memset('pool', K[64:70, :], 1.0)
                for i in range(3):
                    P.dma(Q[64 + i:65 + i, :], cs[h:h + 1, i, :], 'fxq')
                    P.dma(K[67 + i:68 + i, :], ncs[h:h + 1, i, :], 'fxk')
                P.memset('pool', V[:, :, 64:65], 1.0)
                P.dma(V[:, :, 0:64], A['vtok'][:, 512 + 64 * h:512 + 64 * (h + 1)].rearrange("(n p) d -> p n d", p=128), 'fxv')
                for j in range(8):
                    def fin(po, F, o=ob[j % 2], j=j, h=h):
                        P.tt('dve', o[:], po[0:64, :], F[:], ALU.mult)
                        P.dma(A['mixT'][768 + 64 * h:768 + 64 * (h + 1), j * 512:(j + 1) * 512], o[:], 'fxo%d' % (j % 2), q='pool')
                    attn_chunk(cx, K, 128, Q, j, V, causal_entries(j, consts['mc']), fin)
                attn_flush(cx)

import math, os
STAGE = int(os.environ.get('STAGE', '99'))

T = 4096
LN8 = math.log(0.125)


def phase_hgrn(P, A, layer, consts):
    for ct in range(2):
        with ExitStack() as st:
            B = [P.sb(st, "hB%d" % i, [128, T], F32) for i in range(5)]
            qt = P.sb(st, "h_qt", [128, T], BF16)
            kt = P.sb(st, "h_kt", [128, 2, T], BF16)
            qg = P.sb(st, "h_qg", [128, T], BF16)
            kd = P.sb(st, "h_kd", [128, T], BF16)
            kdt = P.sb(st, "h_kdt", [128, 32, 2, 128], BF16)
            Vt = P.sb(st, "h_Vt", [128, 32, 128], BF16)
            Vz = P.sb(st, "h_Vz", [128, 32, 2, 128], BF16)
            Sbd = P.sb(st, "h_Sbd", [128, 64, 128], BF16)
            rst = P.sb(st, "h_rst", [128, T], BF16)
            sm = P.sb(st, "h_sm", [128, 8], F32)
            dl = P.sb(st, "h_dl", [128, 64], F32)
            mh = P.sb(st, "h_mh", [128, 128], BF16)
            bones = P.sb(st, "h_bones", [128, 128], BF16)
            ident = consts['ident']
            P.dma(mh[:], A['mh'], 'hg0')
            P.dma(bones[:], A['bones'], 'hg0')
            P.dma(sm[:, 0:2], A['lbl'][ct], 'hg0')
            P.dma(sm[:, 4:5], A['og'][layer], 'hg0')
            P.dma(B[0][:], A['pT'][256 + ct * 128:256 + (ct + 1) * 128, :], 'hgz')
            P.dma(B[3][:], A['pT'][ct * 128:(ct + 1) * 128, :], 'hgq', q='act')
            P.dma(Vt[:], A['vtok'][:, ct * 128:(ct + 1) * 128].rearrange("(n p) d -> p n d", p=128), 'hgv', q='pool')
            P.memset('pool', Vz[:], 0.0)
            for hh in range(2):
                P.dma(Vz[:, :, hh, hh * 64:(hh + 1) * 64],
                      A['vtok'][:, ct * 128 + hh * 64:ct * 128 + (hh + 1) * 64].rearrange("(n p) d -> p n d", p=128), 'hgv', q='pool')
            P.memset('pool', Sbd[:], 0.0)
            P.memset('pool', kt[:], 0.0)
            P.memset('pool', kdt[:], 0.0)
            P.memset('pool', rst[:], 1.0)
            P.memset('pool', rst[:].rearrange("p (c s) -> p c s", s=64)[:, :, 0:1], 0.0)
            lb = sm[:, 2:3]; oml = sm[:, 3:4]; noml = sm[:, 5:6]
            if layer == 0:
                P.memset('dve', lb, 0.0)
            else:
                P.act(sm[:, 0:2], sm[:, 0:2], AF.Exp)
                P.tt('dve', sm[:, 6:7], sm[:, 0:1], sm[:, 1:2], ALU.add)
                P.op('dve', lambda e, o=sm[:, 6:7]: e.reciprocal(out=o, in_=o), reads=[sm[:, 6:7]], writes=[sm[:, 6:7]])
                P.tt('dve', lb, sm[:, 1:2], sm[:, 6:7], ALU.mult)
            P.ts('dve', oml, lb, -1.0, 1.0, ALU.mult, ALU.add)
            P.ts('dve', noml, oml, -1.0, None, ALU.mult)
            P.act(B[0][:], B[0][:], AF.Sigmoid)
            P.ts('dve', B[1][:], B[0][:], oml, lb, ALU.mult, ALU.add)
            P.act(B[1][:], B[1][:], AF.Ln)
            P.scan(B[2][:], rst[:], B[1][:], 0.0, ALU.mult, ALU.add)
            P.ts('dve', B[1][:], B[0][:], noml, oml, ALU.mult, ALU.add)
            G3 = B[2][:].rearrange("p (c s) -> p c s", s=64)
            D3 = B[0][:].rearrange("p (c s) -> p c s", s=64)
            P.tt('dve', D3, G3, G3[:, :, 31:32].broadcast_to([128, 64, 64]), ALU.subtract)
            P.act(B[4][:], B[0][:], AF.Exp, bias=LN8)
            P.tt('dve', qt[:], B[3][:], B[4][:], ALU.mult)
            P.act(B[4][:], B[0][:], AF.Exp, scale=-1.0)
            P.tt('dve', kt[0:64, 0, :], B[1][0:64, :], B[4][0:64, :], ALU.mult)
            P.tt('dve', kt[64:128, 1, :], B[1][64:128, :], B[4][64:128, :], ALU.mult)
            P.act(B[4][:], B[2][:], AF.Exp, bias=LN8)
            P.tt('dve', qg[:], B[3][:], B[4][:], ALU.mult)
            P.tt('dve', D3, G3[:, :, 63:64].broadcast_to([128, 64, 64]), G3, ALU.subtract)
            P.act(B[4][:], B[0][:], AF.Exp)
            P.tt('dve', kd[:], B[1][:], B[4][:], ALU.mult)
            P.act(dl[:].unsqueeze(2), G3[:, :, 63:64], AF.Exp)
            P.memset('dve', dl[:, 0:1], 0.0)
            KV = B[0]; dfull = B[1]; Sall = B[3]; oT = B[4]
            if STAGE < 1:
                P.dma(A['mixT'][0:128, 0:T], kd[:], 'dbg'); continue
            with ExitStack() as s2:
                ptr = [P.ps(s2, "h_ptr%d" % i, [128, 8, 128], BF16) for i in range(2)]
                for g in range(4):
                    for i in range(8):
                        tl = g * 8 + i
                        P.transpose(ptr[g % 2][:, i, :], kd[:, tl * 128:(tl + 1) * 128], ident[:])
                    P.copy('act', kdt[0:64, g * 8:(g + 1) * 8, 0, :], ptr[g % 2][0:64, :, :])
                    P.copy('dve', kdt[64:128, g * 8:(g + 1) * 8, 1, :], ptr[g % 2][64:128, :, :])
            with ExitStack() as s2:
                pkv = [P.ps(s2, "h_pkv%d" % i, [128, 4, 128], F32) for i in range(2)]
                KV3 = KV[:].rearrange("p (v c) -> p v c", c=64)
                for g in range(16):
                    pk = pkv[g % 2]
                    for i in range(4):
                        c = g * 4 + i
                        tl = c // 2; hf = c % 2
                        P.mm(pk[:, i, :], kdt[:, tl, hf, :], Vt[:, tl, :], start=True, stop=True)
                    for hh in range(2):
                        P.copy('act' if hh else 'dve', KV3[hh * 64:(hh + 1) * 64, :, g * 4:(g + 1) * 4],
                               pk[hh * 64:(hh + 1) * 64, :, hh * 64:(hh + 1) * 64].rearrange("p g v -> p v g"))
            if STAGE < 2:
                P.dma(A['mixT'][0:128, 0:T], kd[:], 'dbg'); continue
            P.copy('pool', dfull[:].rearrange("p (v c) -> p v c", c=64), dl[:].unsqueeze(1).broadcast_to([128, 64, 64]))
            P.scan(Sall[:], dfull[:], KV[:], 0.0, ALU.mult, ALU.add)
            S3 = Sall[:].rearrange("p (v c) -> p v c", c=64)
            for hh in range(2):
                P.copy('dve' if hh else 'act', Sbd[hh * 64:(hh + 1) * 64, 1:64, hh * 64:(hh + 1) * 64],
                       S3[hh * 64:(hh + 1) * 64, :, 0:63].rearrange("p v c -> p c v"))
            if STAGE < 3:
                P.dma(A['mixT'][0:128, 0:T], kd[:], 'dbg'); continue
            with ExitStack() as s2:
                pA = [P.ps(s2, "h_pA%d" % i, [128, 128], F32) for i in range(4)]
                po = [P.ps(s2, "h_po%d" % i, [128, 128], F32) for i in range(2)]
                Am = [P.sb(s2, "h_Am%d" % i, [128, 128], BF16) for i in range(4)]
                for tl in range(32):
                    cols = slice(tl * 128, (tl + 1) * 128)
                    for hh in range(2):
                        i = (tl % 2) * 2 + hh
                        P.mm(pA[i][:], kt[:, hh, cols], qt[:, cols], start=True, stop=True)
                        P.tt('dve', Am[i][:], pA[i][:], mh[:], ALU.mult)
                    p_ = po[tl % 2]
                    P.mm(p_[:], Vz[:, tl, 0, :], Am[(tl % 2) * 2][:], start=True, stop=False)
                    P.mm(p_[:], Vz[:, tl, 1, :], Am[(tl % 2) * 2 + 1][:], start=False, stop=False)
                    P.mm(p_[:, 0:64], Sbd[:, 2 * tl, :], qg[:, tl * 128:tl * 128 + 64], start=False, stop=False)
                    P.mm(p_[:, 64:128], Sbd[:, 2 * tl + 1, :], qg[:, tl * 128 + 64:tl * 128 + 128], start=False, stop=True)
                    P.copy('act', oT[:, cols], p_[:])
            if STAGE < 4:
                P.dma(A['mixT'][0:128, 0:T], kd[:], 'dbg'); continue
            with ExitStack() as s2:
                pss = [P.ps(s2, "h_pss%d" % i, [128, 512], F32) for i in range(2)]
                sq = [P.sb(s2, "h_sq%d" % i, [128, 512], BF16) for i in range(2)]
                rs = [P.sb(s2, "h_rs%d" % i, [128, 512], F32) for i in range(2)]
                ag = [P.sb(s2, "h_ag%d" % i, [128, 512], F32) for i in range(2)]
                ob = [P.sb(s2, "h_ob%d" % i, [128, 512], BF16) for i in range(2)]
                for j in range(8):
                    b = j % 2
                    cols = slice(j * 512, (j + 1) * 512)
                    P.dma(ag[b][:], A['pT'][512 + ct * 128:512 + (ct + 1) * 128, cols], 'hga%d' % b)
                    P.act(sq[b][:], oT[:, cols], AF.Square)
                    P.mm(pss[b][:], bones[:], sq[b][:], start=True, stop=True)
                    P.act(rs[b][:], pss[b][:], AF.Sqrt, bias=1e-6, scale=1.0 / 64)
                    P.op('dve', lambda e, o=rs[b][:]: e.reciprocal(out=o, in_=o), reads=[rs[b][:]], writes=[rs[b][:]])
                    P.act(ag[b][:], ag[b][:], AF.Silu)
                    P.stt('dve', rs[b][:], oT[:, cols], sm[:, 4:5], rs[b][:], ALU.mult, ALU.mult)
                    P.tt('pool', ob[b][:], rs[b][:], ag[b][:], ALU.mult)
                    P.dma(A['mixT'][ct * 128:(ct + 1) * 128, cols], ob[b][:], 'hgo%d' % b, q='pool')

import os
STAGE = int(os.environ.get('STAGE', '99'))

T = 4096
NEG = -30000.0


def phase_nsa(P, A, layer, consts):
    c = consts
    ident = c['ident']
    with ExitStack() as st:
        cx = AttnCtx(P, st, consts)
        lngh = P.sb(st, "n_lngh", [24, T], BF16)
        lngl = P.sb(st, "n_lngl", [24, T], BF16)
        c['selnegb'] = P.sb(st, "n_selnegb", [24, 24 * 64], BF16)
        P.copy('dve', c['selnegb'][:], c['selneg'][:])
        with ExitStack() as s0:
            lng = P.sb(s0, "n_lng", [24, T], F32)
            P.dma(lng[:], A['pT'][1024:1048, :], 'ns0')
            P.act(lng[:], lng[:], AF.Exp, scale=-1.0)
            P.act(lng[:], lng[:], AF.Ln, bias=1.0)
            P.copy('dve', lngh[:], lng[:])
            P.tt('dve', lng[:], lng[:], lngh[:], ALU.subtract)
            P.copy('dve', lngl[:], lng[:])
        lng2 = (lngh, lngl)
        ovaug = P.sb(st, "n_ov", [128, 2, 72], BF16)
        wc = P.sb(st, "n_wc", [128, 3200], BF16)
        addm = P.sb(st, "n_addm", [128, 32, 64], F32)
        P.dma(ovaug[:], A['ovaug'], 'ns0')
        P.dma(wc[:], A['wc'], 'ns0')
        P.dma(addm[:], A['addmask'], 'ns0')
        kcTs = [P.sb(st, "n_kcT%d" % i, [128, 256], BF16) for i in range(2)]
        vcAs = [P.sb(st, "n_vcA%d" % i, [128, 2, 65], BF16) for i in range(2)]
        for g in range(2):
            kcT = kcTs[g]; vcA = vcAs[g]
            with ExitStack() as s2:
                w1 = P.sb(s2, "n_w1", [64, 32, 128], BF16)
                w1f = P.sb(s2, "n_w1f", [64, 32, 128], F32)
                w2 = P.sb(s2, "n_w2", [128, 64], BF16)
                w2f = P.sb(s2, "n_w2f", [128, 64], F32)
                posT = P.sb(s2, "n_posT", [64, 32], BF16)
                posf = P.sb(s2, "n_posf", [64, 32], F32)
                posb = P.sb(s2, "n_posb", [64, 32, 256], BF16)
                srcf = P.sb(s2, "n_srcf", [64, T], F32)
                srcb = P.sb(s2, "n_srcb", [64, T], BF16)
                bias = P.sb(s2, "n_bias", [128, 1], F32)
                xb = P.sb(s2, "n_xb", [128, 256], F32)
                x2 = P.sb(s2, "n_x2", [128, 256], F32)
                hid = P.sb(s2, "n_hid", [128, 256], BF16)
                ktm = P.sb(s2, "n_ktm", [128, 64], F32)
                kts = P.sb(s2, "n_kts", [128, 64], F32)
                ktb = P.sb(s2, "n_ktb", [128, 128], BF16)
                sm = P.sb(s2, "n_sm", [128, 4], F32)
                rt = P.sb(s2, "n_rt", [128, 4, 8], F32)
                kng = P.sb(s2, "n_kng", [128, 64], F32)
                cosc = P.sb(s2, "n_cosc", [128, 2, 8], F32)
                sinc = P.sb(s2, "n_sinc", [128, 2, 8], F32)
                ph = cx.psS[0]; pb = cx.psS[1]; po = cx.psO[0]
                pt = P.ps(s2, "n_pt", [128, 128], BF16)
                P.dma(kng[:], A['kng'][layer].partition_broadcast(128), 'ns1')
                P.dma(cosc[:], A['cosc'], 'ns1')
                P.dma(sinc[:], A['sinc'], 'ns1')
                P.memset('dve', hid[:], 0.0)
                P.memset('dve', vcA[:], 0.0)
                P.memset('dve', kcT[:], 0.0)
                P.memset('dve', ktb[:], 0.0)
                for which in range(2):
                    P.dma(w1f[:], A['w1r'][layer, which], 'ns2')
                    P.dma(w2f[:], A['w2'][layer, which], 'ns2')
                    P.dma(posf[:], A['posT'][layer, which], 'ns2')
                    P.dma(srcf[:], A['pT'][768 + 128 * which + 64 * g:768 + 128 * which + 64 * (g + 1), :], 'ns3', q='act')
                    P.copy('dve', w1[:], w1f[:])
                    P.copy('dve', w2[:], w2f[:])
                    P.copy('dve', posT[:], posf[:])
                    P.copy('dve', posb[:], posT[:].unsqueeze(2).broadcast_to([64, 32, 256]))
                    P.copy('act', srcb[:], srcf[:])
                    for l in range(32):
                        P.mm(ph[:, 0:255], w1[:, l, :], srcb[:].rearrange("p (n s) -> p n s", s=16)[:, (l // 16):(l // 16) + 255, l % 16], start=(l == 0), stop=False)
                    for l in range(32):
                        P.mm(ph[:, 0:255], w1[:, l, :], posb[:, l, 0:255], start=False, stop=(l == 31))
                    P.copy('act', xb[:, 0:255], ph[:, 0:255])
                    P.tt('dve', x2[:, 0:255], xb[:, 0:255], xb[:, 0:255], ALU.mult)
                    P.ts('dve', x2[:, 0:255], x2[:, 0:255], 0.044715, 1.0, ALU.mult, ALU.add)
                    P.tt('dve', x2[:, 0:255], x2[:, 0:255], xb[:, 0:255], ALU.mult)
                    P.act(x2[:, 0:255], x2[:, 0:255], AF.Sigmoid, scale=1.5957691216057308)
                    P.tt('dve', hid[:, 0:255], x2[:, 0:255], xb[:, 0:255], ALU.mult)
                    for nt in range(2):
                        P.mm(po[:, 0:64], hid[:, nt * 128:(nt + 1) * 128], w2[:], start=True, stop=True)
                        if which == 1:
                            nr = 128 if nt == 0 else 127
                            P.copy('act', vcA[0:nr, nt, 0:64], po[0:nr, 0:64])
                            P.memset('dve', vcA[0:nr, nt, 64:65], 1.0)
                        else:
                            P.act(kts[:], po[:, 0:64], AF.Square, accum_out=sm[:, 0:1])
                            P.act(sm[:, 1:2], sm[:, 0:1], AF.Sqrt, bias=1e-6, scale=1.0 / 64)
                            P.op('dve', lambda e, o=sm[:, 1:2]: e.reciprocal(out=o, in_=o), reads=[sm[:, 1:2]], writes=[sm[:, 1:2]])
                            P.stt('dve', ktm[:], po[:, 0:64], sm[:, 1:2], kng[:], ALU.mult, ALU.mult)
                            P.copy('act', ktb[:, 0:64], ktm[:])
                            P.tt('dve', rt[:, 0, :], ktm[:, 0:8], cosc[:, nt, :], ALU.mult)
                            P.tt('dve', rt[:, 1, :], ktm[:, 8:16], sinc[:, nt, :], ALU.mult)
                            P.tt('dve', rt[:, 2, :], ktm[:, 8:16], cosc[:, nt, :], ALU.mult)
                            P.tt('dve', rt[:, 3, :], ktm[:, 0:8], sinc[:, nt, :], ALU.mult)
                            P.tt('dve', ktb[:, 0:8], rt[:, 0, :], rt[:, 1, :], ALU.subtract)
                            P.tt('dve', ktb[:, 8:16], rt[:, 2, :], rt[:, 3, :], ALU.add)
                            P.transpose(pt[:], ktb[:], ident[:])
                            P.copy('dve', kcT[0:64, nt * 128:(nt + 1) * 128], pt[0:64, :])
            P.memset('dve', kcT[0:64, 255:256], 0.0)
        selT = P.sb(st, "n_selT", [128, T], BF16)
        imp = P.sb(st, "n_imp", [128, 32, 64], F32)
        acc = [P.sb(st, "n_acc%d" % i, [64, T], F32) for i in range(4)]
        Q = [P.sb(st, "n_Q%d" % i, [128, T], BF16) for i in range(4)]
        Ks = P.sb(st, "n_Ks", [128, T], BF16)
        Kw = P.sb(st, "n_Kw", [128, T], BF16)
        Vs = P.sb(st, "n_Vs", [128, 32, 65], BF16)
        Vw = P.sb(st, "n_Vw", [128, 32, 65], BF16)
        for g in range(2):
            kcT = kcTs[g]; vcA = vcAs[g]
            for hh in range(4):
                h = 4 * g + hh
                P.memset('pool', Q[hh][64:128, :], 0.0)
                P.dma(Q[hh][0:64, :], A['qkT'][64 * h:64 * (h + 1), :], 'nsq%d' % hh)
            P.dma(Ks[0:64, :], A['qkT'][512 + 64 * g:512 + 64 * (g + 1), :], 'nsk')
            P.dma(Ks[64:128, :], A['eall'], 'nsk')
            P.dma(Kw[0:64, :], A['qkT'][640 + 64 * g:640 + 64 * (g + 1), :], 'nsk')
            P.memset('pool', Kw[64:128, :], 0.0)
            P.memset('pool', Vs[:, :, 64:65], 1.0)
            P.memset('pool', Vw[:, :, 64:65], 1.0)
            P.dma(Vs[:, :, 0:64], A['vtok'][:, 256 + 64 * g:256 + 64 * (g + 1)].rearrange("(n p) d -> p n d", p=128), 'nsv', q='act')
            P.dma(Vw[:, :, 0:64], A['vtok'][:, 384 + 64 * g:384 + 64 * (g + 1)].rearrange("(n p) d -> p n d", p=128), 'nsv', q='act')
            with ExitStack() as s2:
                pimp = [P.ps(s2, "n_pimp%d" % i, [128, 4, 72], F32) for i in range(2)]
                pTc = [P.sb(s2, "n_pTc%d" % i, [128, 512], BF16) for i in range(3)]
                rinvs = [P.sb(s2, "n_rinv%d" % i, [128, 4], F32) for i in range(2)]
                kc_ = 0
                kch = 0
                for hh in range(4):
                    h = 4 * g + hh
                    for j in range(8):
                        q0 = j * 512
                        po = cx.psO[cx.kO % 2]
                        cx.kO += 1
                        pim = pimp[kch % 2]
                        rinv = rinvs[kch % 2]
                        kch += 1
                        tiles = []
                        for nt in range(2):
                            off = 2048 * nt + 31 - 512 * j
                            if -off + 511 < 0:
                                continue
                            tiles.append((nt, off))
                        for ti, (nt, off) in enumerate(tiles):
                            ps = cx.psS[cx.kS % 2]
                            cx.kS += 1
                            ptc = pTc[kc_ % 3]
                            kc_ += 1
                            first = (ti == 0)
                            last = (ti == len(tiles) - 1)

                            def s_fn(ps=ps, nt=nt, off=off, first=first, po=po, pim=pim, hh=hh, q0=q0):
                                if first:
                                    P.mm(po[0:65, :], c['zeros'][:, 0:65], c['ident_w'][:, 0:512], start=True, stop=False)
                                    P.mm(pim[:].rearrange("p a b -> p (a b)"), c['zeros'][:, 0:128], c['ident_w'][:, 0:288], start=True, stop=False)
                                full = (-off >= 2032)
                                P.mm(ps[:], kcT[:, nt * 128:(nt + 1) * 128], Q[hh][:, q0:q0 + 512], start=True, stop=full)
                                if not full:
                                    ci0 = -off + 511
                                    P.mm(ps[:], ident[:], wc[:, ci0:ci0 + 512], start=False, stop=True)

                            def exp_fn(ps=ps, ptc=ptc):
                                P.act(ptc[:], ps[:], AF.Exp)

                            def pv_fn(po=po, pim=pim, ptc=ptc, nt=nt, last=last):
                                P.mm(po[0:65, :], vcA[:, nt, :], ptc[:], start=False, stop=last)
                                for m in range(4):
                                    P.mm(pim[:, m, :], ptc[:, m * 128:(m + 1) * 128], ovaug[:, nt, :], start=False, stop=last)

                            _push_block(cx, s_fn, exp_fn, pv_fn, first=first)

                        def fin(po_, F, hh=hh, q0=q0, pim=pim, rinv=rinv, j=j):
                            for m in range(4):
                                tq = j * 4 + m
                                if hh == 0:
                                    P.ts('dve', imp[:, tq, :], pim[:, m, 0:64], rinv[:, m:m + 1], None, ALU.mult)
                                else:
                                    P.stt('dve', imp[:, tq, :], pim[:, m, 0:64], rinv[:, m:m + 1], imp[:, tq, :], ALU.mult, ALU.add)
                            P.tt('dve', acc[hh][:, q0:q0 + 512], po_[0:64, :], F[:], ALU.mult)

                        ea, eb = _mk_factor(cx, po, lng2, h, q0, fin)

                        def ea2(ea=ea, pim=pim, rinv=rinv):
                            ea()
                            P.ts('dve', rinv[:, 0:4].unsqueeze(2), pim[:, :, 64:65], 1e-30, None, ALU.max)
                            P.op('dve', lambda e, o=rinv[:, 0:4]: e.reciprocal(out=o, in_=o), reads=[rinv[:, 0:4]], writes=[rinv[:, 0:4]])

                        _end_chunk(cx, ea2, eb)
                attn_flush(cx)
            if STAGE < 2:
                continue
            with ExitStack() as s2:
                wk = [P.sb(s2, "n_wk%d" % i, [128, 64], F32) for i in range(2)]
                w2_ = [P.sb(s2, "n_wk2%d" % i, [128, 64], F32) for i in range(2)]
                m8 = [P.sb(s2, "n_m8%d" % i, [128, 16], F32) for i in range(2)]
                sb_ = [P.sb(s2, "n_sb%d" % i, [128, 128], BF16) for i in range(2)]
                pts = [P.ps(s2, "n_pts%d" % i, [128, 128], BF16) for i in range(2)]
                P.memset('pool', sb_[0][:], 0.0)
                P.memset('pool', sb_[1][:], 0.0)
                for tq in range(32):
                    b = tq % 2
                    P.tt('dve', wk[b][:], imp[:, tq, :], addm[:, tq, :], ALU.add)
                    P.op('dve', lambda e, o=m8[b][:, 0:8], i=wk[b][:]: e.max(out=o, in_=i), reads=[wk[b][:]], writes=[m8[b][:, 0:8]])
                    P.op('dve', lambda e, o=w2_[b][:], r=m8[b][:, 0:8], i=wk[b][:]: e.match_replace(out=o, in_to_replace=r, in_values=i, imm_value=-3.0e38),
                         reads=[m8[b][:, 0:8], wk[b][:]], writes=[w2_[b][:]])
                    P.op('dve', lambda e, o=m8[b][:, 8:16], i=w2_[b][:]: e.max(out=o, in_=i), reads=[w2_[b][:]], writes=[m8[b][:, 8:16]])
                    P.ts('dve', w2_[b][:], wk[b][:], m8[b][:, 15:16], None, ALU.is_ge)
                    P.ts('dve', wk[b][:], wk[b][:], -5.0e29, None, ALU.is_gt)
                    P.tt('dve', wk[b][:], wk[b][:], w2_[b][:], ALU.mult)
                    P.ts('dve', sb_[b][:, 64:128], wk[b][:], -1.0, -NEG, ALU.add, ALU.mult)
                    P.transpose(pts[b][:], sb_[b][:], ident[:])
                    P.copy('act', selT[64:128, tq * 128:(tq + 1) * 128], pts[b][64:128, :])
            for hh in range(4):
                P.dma(Q[hh][64:128, :], selT[64:128, :], 'nsq%d' % hh, q='sp' if hh % 2 else 'act')
            if STAGE < 3:
                continue
            with ExitStack() as s2:
                tmp = [P.sb(s2, "n_tmp%d" % i, [64, 512], F32) for i in range(2)]
                ob = [P.sb(s2, "n_ob%d" % i, [64, 512], BF16) for i in range(1)] * 2
                for hh in range(4):
                    h = 4 * g + hh
                    for j in range(8):
                        q0 = j * 512
                        a = acc[hh][:, q0:q0 + 512]

                        def fin_s(po_, F, a=a):
                            P.tt('dve', tmp[0][:], po_[0:64, :], F[:], ALU.mult)
                            P.tt('pool', a, a, tmp[0][:], ALU.add)

                        def fin_w(po_, F, a=a, j=j, h=h, q0=q0):
                            P.tt('dve', tmp[1][:], po_[0:64, :], F[:], ALU.mult)
                            P.tt('pool', ob[j % 2][:], a, tmp[1][:], ALU.add)
                            if STAGE >= 5:
                                P.dma(A['mixT'][256 + 64 * h:256 + 64 * (h + 1), q0:q0 + 512], ob[j % 2][:], 'nso%d' % (j % 2), q='sp')

                        attn_chunk(cx, Ks, 128, Q[hh], j, Vs, causal_entries(j, c['mc']), fin_s, lng2=lng2, gate_c=8 + h)
                        if STAGE >= 4:
                            attn_chunk(cx, Kw, 128, Q[hh], j, Vw, window_entries(j, c['mc'], c['mu']), fin_w, lng2=lng2, gate_c=16 + h)
                attn_flush(cx)


T = 4096
D = 1024
FF = 4096


def phase_wo(P, A, layer, consts, x_in, x_mid):
    ident = consts['ident']
    with ExitStack() as st:
        Wo = P.sb(st, "wo", [128, 8, D], BF16)
        with ExitStack() as s2:
            wst = [P.sb(s2, "wost%d" % i, [128, D], F32) for i in range(2)]
            for kc in range(8):
                b = wst[kc % 2]
                P.dma(b[:], A['wo'][layer, kc * 128:(kc + 1) * 128, :], 'wost%d' % (kc % 2))
                P.copy('act' if kc % 2 else 'dve', Wo[:, kc, :], b[:])
        mx = [P.sb(st, "wo_mx%d" % i, [128, 8, 512], BF16) for i in range(2)]
        xt = [P.sb(st, "wo_xt%d" % i, [128, D], F32) for i in range(2)]
        xm = [P.sb(st, "wo_xm%d" % i, [128, D], F32) for i in range(2)]
        sq = P.sb(st, "wo_sq", [128, D], F32)
        hb = [P.sb(st, "wo_hb%d" % i, [128, D], BF16) for i in range(2)]
        ss = [P.sb(st, "wo_ss%d" % i, [128, 2], F32) for i in range(2)]
        hst = [P.sb(st, "wo_hst%d" % i, [128, 8, 512], BF16) for i in range(2)]
        po = [P.ps(st, "wo_po%d" % i, [128, 512], F32) for i in range(4)]
        ptr = [P.ps(st, "wo_ptr%d" % i, [128, 8, 128], BF16) for i in range(2)]
        for t in range(32):
            j = t // 4
            b = t % 2
            if t % 4 == 0:
                P.dma(mx[j % 2][:], A['mixT'][:, j * 512:(j + 1) * 512].rearrange("(a p) n -> p a n", p=128), 'womx%d' % (j % 2))
            P.dma(xt[b][:], x_in[t * 128:(t + 1) * 128, :], 'woxt%d' % b, q='act')
            for half in range(2):
                pp = po[(t % 2) * 2 + half]
                for kc in range(8):
                    P.mm(pp[:], mx[j % 2][:, kc, (t % 4) * 128:(t % 4 + 1) * 128], Wo[:, kc, half * 512:(half + 1) * 512],
                         start=(kc == 0), stop=(kc == 7))
                P.tt('dve', xm[b][:, half * 512:(half + 1) * 512], pp[:], xt[b][:, half * 512:(half + 1) * 512], ALU.add)
            P.dma(x_mid[t * 128:(t + 1) * 128, :], xm[b][:], 'woxm%d' % b, q='pool')
            P.act(sq[:], xm[b][:], AF.Square, accum_out=ss[b][:, 0:1])
            P.act(ss[b][:, 1:2], ss[b][:, 0:1], AF.Sqrt, bias=1e-6, scale=1.0 / D)
            P.op('dve', lambda e, o=ss[b][:, 1:2]: e.reciprocal(out=o, in_=o), reads=[ss[b][:, 1:2]], writes=[ss[b][:, 1:2]])
            P.ts('dve', hb[b][:], xm[b][:], ss[b][:, 1:2], None, ALU.mult)
            for kc in range(8):
                P.transpose(ptr[b][:, kc, :], hb[b][:, kc * 128:(kc + 1) * 128], ident[:])
            P.copy('act', hst[j % 2][:, :, (t % 4) * 128:(t % 4 + 1) * 128], ptr[b][:])
            if t % 4 == 3:
                P.dma(A['h2T'][:, j * 512:(j + 1) * 512].rearrange("(a p) n -> p a n", p=128), hst[j % 2][:], 'wohst%d' % (j % 2), q='pool')


def phase_ffn(P, A, layer, consts, x_mid, x_out):
    with ExitStack() as st:
        Wu = P.sb(st, "wu", [128, 8, FF], BF16)
        Wd = P.sb(st, "wd", [128, 32, D], BF16)
        g2 = P.sb(st, "g2", [128, 8], F32)
        P.dma(g2[:], A['g2'][layer], 'ff0')
        with ExitStack() as s2:
            wst = [P.sb(s2, "fwst%d" % i, [128, 2048], F32) for i in range(3)]
            k = 0
            for kc in range(8):
                for hf in range(2):
                    b = k % 3
                    P.dma(wst[b][:], A['wup'][layer, kc * 128:(kc + 1) * 128, hf * 2048:(hf + 1) * 2048], 'fwst%d' % b, q='sp' if k % 2 else 'act')
                    if k % 2:
                        P.ts('dve', Wu[:, kc, hf * 2048:(hf + 1) * 2048], wst[b][:], g2[:, kc:kc + 1], None, ALU.mult)
                    else:
                        P.act(Wu[:, kc, hf * 2048:(hf + 1) * 2048], wst[b][:], AF.Copy, scale=g2[:, kc:kc + 1])
                    k += 1
            for fc2 in range(16):
                b = k % 3
                P.dma(wst[b][:].rearrange("p (a n) -> p a n", a=2), A['wdn'][layer, fc2 * 256:(fc2 + 1) * 256, :].rearrange("(a p) n -> p a n", p=128),
                      'fwst%d' % b, q='sp' if k % 2 else 'act')
                P.copy('dve' if k % 2 else 'act', Wd[:, fc2 * 2:(fc2 + 1) * 2, :], wst[b][:].rearrange("p (a n) -> p a n", a=2))
                k += 1
        h2 = [P.sb(st, "ff_h2%d" % i, [128, 8, 512], BF16) for i in range(2)]
        uT = P.sb(st, "ff_uT", [128, 32, 512], BF16)
        rl = [P.sb(st, "ff_rl%d" % i, [128, 512], F32) for i in range(2)]
        xt = [P.sb(st, "ff_xt%d" % i, [128, D], F32) for i in range(2)]
        xo = [P.sb(st, "ff_xo%d" % i, [128, D], F32) for i in range(2)]
        pu = [P.ps(st, "ff_pu%d" % i, [128, 512], F32) for i in range(3)]
        pd = [P.ps(st, "ff_pd%d" % i, [128, 512], F32) for i in range(4)]
        ku = 0
        for j in range(8):
            P.dma(h2[j % 2][:], A['h2T'][:, j * 512:(j + 1) * 512].rearrange("(a p) n -> p a n", p=128), 'ffh2%d' % (j % 2))
            for fc in range(32):
                pp = pu[ku % 3]
                r = rl[ku % 2]
                for kc in range(8):
                    P.mm(pp[:], Wu[:, kc, fc * 128:(fc + 1) * 128], h2[j % 2][:, kc, :], start=(kc == 0), stop=(kc == 7))
                P.act(r[:], pp[:], AF.Relu)
                P.tt('dve' if ku % 2 else 'pool', uT[:, fc, :], r[:], r[:], ALU.mult)
                ku += 1
            for tt in range(4):
                t = j * 4 + tt
                b = t % 2
                P.dma(xt[b][:], x_mid[t * 128:(t + 1) * 128, :], 'ffxt%d' % b, q='act')
                for half in range(2):
                    pp = pd[(t % 2) * 2 + half]
                    for fc in range(32):
                        P.mm(pp[:], uT[:, fc, tt * 128:(tt + 1) * 128], Wd[:, fc, half * 512:(half + 1) * 512], start=(fc == 0), stop=(fc == 31))
                    P.tt('dve', xo[b][:, half * 512:(half + 1) * 512], pp[:], xt[b][:, half * 512:(half + 1) * 512], ALU.add)
                P.dma(x_out[t * 128:(t + 1) * 128, :], xo[b][:], 'ffxo%d' % b, q='pool')

import ml_dtypes
from concourse.bass_utils import run_bass_kernel_spmd

T=4096; D=1024
OFF = {}
_names = ['aq','af','ai','ag','bq','bkc','bvc','bks','bvs','bkw','bvw','bg','cq','ck','cv','cf']
_sizes = [256,256,256,256,512,128,128,128,128,128,128,24,256,256,256,4]
_o = 0
for n_, s_ in zip(_names, _sizes):
    OFF[n_] = (_o, _o + s_); _o += s_
TOK_ORDER = ['bq','bks','bkw','cq','ck','ai','bvs','bvw','cv']
T_ORDER = ['aq','af','ag','bkc','bvc','bg','cf']

def win_layout(w_in):
    L = w_in.shape[0]
    out = np.zeros((L, 1024, 2048 + 1152), np.float32)
    c = 0
    for n_ in TOK_ORDER:
        a, b = OFF[n_]; out[:, :, c:c + b - a] = w_in[:, :, a:b]; c += b - a
    assert c == 2048
    for n_ in T_ORDER:
        a, b = OFF[n_]; out[:, :, c:c + b - a] = w_in[:, :, a:b]; c += b - a
    return out

def rope_tables():
    inv = np.power(np.float32(500000.0), -np.arange(0, 16, 2, dtype=np.float32) / 16).astype(np.float32)
    pos = np.arange(T, dtype=np.float32)
    ang = pos[:, None] * inv[None, :]
    cos = np.cos(ang).astype(np.float32); sin = np.sin(ang).astype(np.float32)
    return (np.ascontiguousarray(cos.reshape(32, 128, 8).transpose(1, 0, 2)),
            np.ascontiguousarray(sin.reshape(32, 128, 8).transpose(1, 0, 2)))

def _skip():
    pass

def const_inputs():
    k = np.arange(128)[:, None]; q = np.arange(128)[None, :]
    mc = np.where(k <= q, 0.0, -30000.0).astype(ml_dtypes.bfloat16)
    mu = np.where(k > q, 0.0, -30000.0).astype(ml_dtypes.bfloat16)
    selneg = np.zeros((24, 24 * 64), np.float32)
    for c in range(24):
        selneg[c, c * 64:(c + 1) * 64] = -1.0
    return dict(ident=np.eye(128, dtype=ml_dtypes.bfloat16), mc=mc, mu=mu, selneg=selneg)

def _unused_ref_proj(inp, layer, x):
    x = x.astype(np.float64)
    h = x / np.sqrt((x * x).mean(-1, keepdims=True) + 1e-6) * inp['norm1_g'][layer]
    return h @ inp['w_in'][layer].astype(np.float64)

def hgrn_consts(inp):
    s = np.arange(128)[:, None]; t = np.arange(128)[None, :]
    mh = ((s // 64 == t // 64) & (s <= t)).astype(ml_dtypes.bfloat16)
    bones = (s // 64 == t // 64).astype(ml_dtypes.bfloat16)
    lbl = np.ascontiguousarray(inp['hgrn_lb_logits'].reshape(2, 2, 128).transpose(1, 2, 0)).astype(np.float32)
    og = np.tile(inp['hgrn_onorm_g'], (1, 2)).reshape(2, 128, 1).astype(np.float32)
    return dict(mh=mh, bones=bones, lbl=lbl, og=og)

def _unused_ref_hgrn(inp, layer, proj):
    def sl(n): a, b = OFF[n]; return proj[:, a:b]
    lbp = np.exp(inp['hgrn_lb_logits'].astype(np.float64)); lbp /= lbp.sum(0, keepdims=True)
    lb_all = np.cumsum(lbp, 0) - lbp[0:1]
    lb = lb_all[layer].reshape(4, 64)
    z = sl('af').reshape(T, 4, 64)
    sig = 1 / (1 + np.exp(-z))
    f = lb + (1 - lb) * sig; logf = np.log(f); k = (1 - lb) * (1 - sig)
    q = sl('aq').reshape(T, 4, 64) * 0.125; v = sl('ai').reshape(T, 4, 64)
    o = np.zeros((T, 4, 64))
    for h in range(4):
        S = np.zeros((64, 64))
        for c in range(64):
            r = slice(c * 64, (c + 1) * 64)
            G = np.cumsum(logf[r, h], 0)
            qc, kc, vc = q[r, h], k[r, h], v[r, h]
            o_inter = (qc * np.exp(G)) @ S
            diff = G[:, None, :] - G[None, :, :]
            mask = np.tril(np.ones((64, 64), bool))
            dec = np.where(mask[:, :, None], np.exp(np.minimum(diff, 0)), 0)
            sc = np.einsum('tk,sk,tsk->ts', qc, kc, dec)
            o[r, h] = o_inter + sc @ vc
            S = S * np.exp(G[-1])[:, None] + (kc * np.exp(G[-1] - G)).T @ vc
    g = sl('ag').reshape(T, 4, 64)
    gate = g / (1 + np.exp(-g))
    on = o / np.sqrt((o * o).mean(-1, keepdims=True) + 1e-6) * inp['hgrn_onorm_g'][layer]
    return (on * gate).reshape(T, 256)

def nsa_consts(inp):
    n_cmp = 255
    ci = np.arange(n_cmp)[:, None]; sj = np.arange(64)[None, :]
    ov = ((ci * 16 <= sj * 64 + 63) & (ci * 16 + 31 >= sj * 64)).astype(np.float32)
    ovaug = np.zeros((256, 72), np.float32); ovaug[:255, :64] = ov; ovaug[:255, 64] = 1.0
    ovaug = np.ascontiguousarray(ovaug.reshape(2, 128, 72).transpose(1, 0, 2)).astype(ml_dtypes.bfloat16)
    nl = np.arange(128)[:, None]; cc = np.arange(3200)[None, :] - 511
    wc = np.where(cc >= 16 * nl, 0.0, -30000.0).astype(ml_dtypes.bfloat16)
    eall = (np.arange(T)[None, :] // 64 == np.arange(64)[:, None]).astype(ml_dtypes.bfloat16)
    q = np.arange(T)[:, None]; j = np.arange(64)[None, :]; cur = q // 64
    am = np.zeros((T, 64), np.float32)
    am[(j == 0) | (j == cur) | (j == cur - 1)] = 1e30
    am[np.broadcast_to(j > cur, am.shape)] = -1e30
    addmask = np.ascontiguousarray(am.reshape(32, 128, 64).transpose(1, 0, 2))
    inv = np.power(np.float32(500000.0), -np.arange(0, 16, 2, dtype=np.float32) / 16).astype(np.float32)
    pos = (np.arange(256, dtype=np.float32) * 16 + 31)
    ang = pos[:, None] * inv[None, :]
    cosc = np.ascontiguousarray(np.cos(ang).astype(np.float32).reshape(2, 128, 8).transpose(1, 0, 2))
    sinc = np.ascontiguousarray(np.sin(ang).astype(np.float32).reshape(2, 128, 8).transpose(1, 0, 2))
    w1r = np.ascontiguousarray(inp['nsa_cmp_w1'].reshape(2, 2, 32, 64, 128).transpose(0, 1, 3, 2, 4)).astype(np.float32)
    posT = np.ascontiguousarray(inp['nsa_cmp_pos'].transpose(0, 1, 3, 2)).astype(np.float32)
    return dict(ovaug=ovaug, wc=wc, eall=eall, addmask=addmask, cosc=cosc, sinc=sinc, w1r=w1r, posT=posT,
                w2=inp['nsa_cmp_w2'].astype(np.float32), kng=inp['nsa_kn_g'].astype(np.float32))

NSA_SHAPES = [('ovaug', [128, 2, 72], BF16), ('wc', [128, 3200], BF16), ('eall', [64, 4096], BF16), ('addmask', [128, 32, 64], F32),
              ('cosc', [128, 2, 8], F32), ('sinc', [128, 2, 8], F32), ('w1r', [2, 2, 64, 32, 128], F32), ('posT', [2, 2, 64, 32], F32),
              ('w2', [2, 2, 128, 64], F32), ('kng', [2, 64], F32)]


import ml_dtypes
from concourse.bass_utils import run_bass_kernel_spmd

_IN_SHAPES = [('x', [T, D], F32), ('win', [2, D, WCOLS], F32), ('g1', [2, 128, 8], F32), ('gq', [2, 1280], F32),
              ('cos', [128, 32, 8], F32), ('sin', [128, 32, 8], F32), ('ident', [128, 128], BF16), ('mc', [128, 128], BF16),
              ('mu', [128, 128], BF16), ('selneg', [24, 1536], F32), ('mh', [128, 128], BF16), ('bones', [128, 128], BF16),
              ('lbl', [2, 128, 2], F32), ('og', [2, 128, 1], F32), ('fb', [2, 4, 1], F32), ('wo', [2, 1024, 1024], F32),
              ('wup', [2, 1024, 4096], F32), ('wdn', [2, 4096, 1024], F32), ('g2', [2, 128, 8], F32)] + NSA_SHAPES

KDEPTH = int(os.environ.get('KDEPTH', '2'))
KPHASES = os.environ.get('KPHASES', '1hnfwf')


def _body(P):
    nc = P.nc
    A = {}
    for k_, shp, dt_ in _IN_SHAPES:
        A[k_] = nc.dram_tensor(k_, shp, dt_, kind="ExternalInput").ap()
    A['y'] = nc.dram_tensor("y", [T, D], F32, kind="ExternalOutput").ap()
    A['qkT'] = nc.dram_tensor("qkT", [1280, T], BF16).ap()
    A['vtok'] = nc.dram_tensor("vtok", [T, 768], BF16).ap()
    A['pT'] = nc.dram_tensor("pT", [TC, T], F32).ap()
    A['mixT'] = nc.dram_tensor("mixT", [1024, T], BF16).ap()
    A['h2T'] = nc.dram_tensor("h2T", [1024, T], BF16).ap()
    xm = nc.dram_tensor("xmid", [T, D], F32).ap()
    x1 = nc.dram_tensor("x1", [T, D], F32).ap()
    xin = A['x']
    for layer in range(KDEPTH):
        A['x'] = xin
        with ExitStack() as st:
            phase1(P, st, A, layer)
        with ExitStack() as st:
            consts = load_consts(P, st, A)
            if 'h' in KPHASES:
                phase_hgrn(P, A, layer, consts)
            if 'n' in KPHASES:
                phase_nsa(P, A, layer, consts)
            if 'f' in KPHASES:
                phase_fox(P, A, layer, consts)
            xout = x1 if layer < KDEPTH - 1 else A['y']
            phase_wo(P, A, layer, consts, xin, xm)
            phase_ffn(P, A, layer, consts, xm, xout)
        xin = xout


def _host_inputs(inp):
    cos, sin = rope_tables()
    gq = np.concatenate([np.tile(inp['nsa_qn_g'], (1, 8)), np.tile(inp['nsa_kn_g'], (1, 4)), np.tile(inp['fox_qn_g'], (1, 4)),
                         np.tile(inp['fox_kn_g'], (1, 4))], axis=1).astype(np.float32)
    base = {"win": win_layout(inp['w_in']), "g1": np.ascontiguousarray(inp['norm1_g'].reshape(2, 8, 128).transpose(0, 2, 1)),
            "g2": np.ascontiguousarray(inp['norm2_g'].reshape(2, 8, 128).transpose(0, 2, 1)),
            "gq": gq, "cos": cos, "sin": sin, "fb": inp['fox_fb'].reshape(2, 4, 1).astype(np.float32),
            "wo": inp['w_o'], "wup": inp['w_up'], "wdn": inp['w_down']}
    base.update(const_inputs()); base.update(hgrn_consts(inp)); base.update(nsa_consts(inp))
    return base


def kernel(**inp):
    inp = {k: np.asarray(v) for k, v in inp.items()}
    nc, plan = build_two_pass(lambda: bass.Bass("TRN2", target_bir_lowering=False), _body)
    base = _host_inputs(inp)
    in_maps = []
    for b in range(8):
        m = dict(base); m['x'] = np.ascontiguousarray(inp['x'][b]); in_maps.append(m)
    res = run_bass_kernel_spmd(nc, in_maps, core_ids=list(range(8)))
    return np.stack([r['y'] for r in res.results], axis=0).astype(np.float32)
```

```python
import numpy as np, sys, time, os, math
import numpy as np
from contextlib import ExitStack
import concourse.bass as bass
import concourse.mybir as mybir

F32 = mybir.dt.float32
BF16 = mybir.dt.bfloat16
AF = mybir.ActivationFunctionType
ALU = mybir.AluOpType
AX = mybir.AxisListType


def _box(ap):
    t = ap.tensor
    dims = ap.ap
    off = int(ap.offset)
    shp = tuple(t.shape)
    rowsize = 1
    for s in shp[1:]:
        rowsize *= int(s)
    r0 = off // rowsize
    f0 = off % rowsize
    rows = 0
    free = 0
    for (st, cnt) in dims:
        st = int(st); cnt = int(cnt)
        if cnt <= 1 or st == 0:
            continue
        if st % rowsize == 0:
            rows += (st // rowsize) * (cnt - 1)
        else:
            free += st * (cnt - 1)
    return t.name, (r0, r0 + rows, f0, f0 + free)


def _ov(a, b):
    return a[0] <= b[1] and b[0] <= a[1] and a[2] <= b[3] and b[2] <= a[3]


def _cont(a, b):
    return a[0] <= b[0] and b[1] <= a[1] and a[2] <= b[2] and b[3] <= a[3]


class Prog:
    def __init__(self, nc, plan=None):
        self.nc = nc
        self.plan = plan
        self.rec = plan is None
        self.eng = dict(pe=nc.tensor, dve=nc.vector, act=nc.scalar, pool=nc.gpsimd, sp=nc.sync)
        self.n = 0
        self.ins = []
        self.track = {}
        self.lane_cnt = {}
        self.freed = {}
        self.uid = 0
        self.stack = ExitStack()
        self.psum_rr = 0
        self.psum_banks = []
        if not self.rec:
            self.sem = {}
            for e in ['pe', 'dve', 'act', 'pool']:
                self.sem[e] = self.stack.enter_context(nc.semaphore("sem_" + e))
            self.lane_sem = {}
            for ln in plan['lanes']:
                self.lane_sem[ln] = self.stack.enter_context(nc.semaphore("ln_" + ln))

    def sb(self, st, name, shape, dtype):
        self.uid += 1
        name = "%s_%d" % (name, self.uid)
        t = st.enter_context(self.nc.sbuf_tensor("s_" + name, list(shape), dtype))
        st.callback(self._free, "s_" + name)
        return t

    def ps(self, st, name, shape, dtype=F32):
        self.uid += 1
        name = "%s_%d" % (name, self.uid)
        t = st.enter_context(self.nc.psum_tensor("p_" + name, list(shape), dtype))
        st.callback(self._free, "p_" + name)
        return t

    def _free(self, name):
        if not self.rec:
            return
        recs = self.track.pop(name, [])
        for (b, i, w) in recs:
            r = self.ins[i]
            key = ('l', r['lane'], i) if r['dma'] else ('e', r['eng'])
            if r['dma']:
                self.freed[key] = i
            else:
                self.freed[key] = max(self.freed.get(key, -1), i)

    def _access(self, idx, eng, dma, ap, write, deps):
        name, box = _box(ap)
        if name not in self.track:
            big = (0, 10 ** 9, 0, 10 ** 9)
            kind = ap.space
            self.track[name] = [] if str(kind) == 'DRAM' else [(big, i, True) for i in sorted(set(self.freed.values()))]
        recs = self.track[name]
        for (b, i, w) in recs:
            if (write or w) and _ov(b, box):
                deps.append((i, (w and not write)))
        if write:
            recs[:] = [r for r in recs if not _cont(box, r[0])]
        elif not dma:
            recs[:] = [r for r in recs if not ((not r[2]) and r[1] < len(self.ins) and self.ins[r[1]]['eng'] == eng
                                               and not self.ins[r[1]]['dma'] and _cont(box, r[0]))]
        recs.append((box, idx, write))

    def op(self, eng, fn, reads=(), writes=(), dma=False, lane=None):
        idx = self.n
        self.n += 1
        if self.rec:
            deps = []
            for ap in reads:
                self._access(idx, eng, dma, ap, False, deps)
            for ap in writes:
                self._access(idx, eng, dma, ap, True, deps)
            lanewaits = {}
            d2 = {}
            for (j, raw) in deps:
                if j == idx:
                    continue
                pj = self.ins[j]
                if pj['dma']:
                    ln = pj['lane']
                    lanewaits[ln] = max(lanewaits.get(ln, 0), pj['lane_val_at'])
                    lanewaits[ln] = max(lanewaits[ln], self.lane_cnt[ln])
                    continue
                if pj['eng'] == eng and not dma:
                    if eng == 'pe':
                        continue
                    if not raw:
                        continue
                d2[j] = True
            rec = dict(eng=eng, deps=list(d2.keys()), lanewaits=lanewaits, dma=dma, lane=lane)
            if dma:
                self.lane_cnt[lane] = self.lane_cnt.get(lane, 0) + 16
                rec['lane_val_at'] = self.lane_cnt[lane]
            self.ins.append(rec)
            return None
        else:
            info = self.plan['ins'][idx]
            e = self.eng[eng]
            for (sname, val) in info['waits']:
                s = self.sem[sname[1]] if sname[0] == 'e' else self.lane_sem[sname[1]]
                e.wait_ge(s, val)
            inst = fn(e)
            if dma:
                inst.then_inc(self.lane_sem[lane], 16)
            elif info['signal']:
                inst.then_inc(self.sem[eng], 1)
            return inst

    def make_plan(self):
        ins = self.ins
        signal = [False] * len(ins)
        for r in ins:
            for j in r['deps']:
                signal[j] = True
        cnt = dict(pe=0, dve=0, act=0, pool=0, sp=0)
        sigval = [0] * len(ins)
        for i, r in enumerate(ins):
            if signal[i] and not r['dma']:
                cnt[r['eng']] += 1
                sigval[i] = cnt[r['eng']]
        seen = {e: {} for e in cnt}
        out = []
        for i, r in enumerate(ins):
            need = {}
            for j in r['deps']:
                k = ('e', ins[j]['eng'])
                need[k] = max(need.get(k, 0), sigval[j])
            for ln, v in r['lanewaits'].items():
                k = ('l', ln)
                need[k] = max(need.get(k, 0), v)
            waits = []
            sd = seen[r['eng']]
            for k, v in need.items():
                if sd.get(k, 0) >= v:
                    continue
                sd[k] = v
                waits.append((k, v))
            out.append(dict(waits=waits, signal=signal[i]))
        return dict(ins=out, lanes=sorted(self.lane_cnt.keys()), lane_final=dict(self.lane_cnt))

    def finish(self):
        if self.rec:
            return
        for ln, v in self.plan['lane_final'].items():
            self.nc.sync.wait_ge(self.lane_sem[ln], v)

    def dma(self, out, in_, lane, q='sp', **kw):
        return self.op(q, lambda e: e.dma_start(out=out, in_=in_, **kw), reads=[in_], writes=[out],
                       dma=True, lane=lane)

    def mm(self, out, lhsT, rhs, start=True, stop=True, **kw):
        return self.op('pe', lambda e: e.matmul(out, lhsT, rhs, start=start, stop=stop, **kw),
                       reads=[lhsT, rhs], writes=[out])

    def transpose(self, out, in_, ident):
        return self.op('pe', lambda e: e.transpose(out, in_, ident), reads=[in_, ident], writes=[out])

    def act(self, out, in_, func, bias=None, scale=None, accum_out=None, eng='act'):
        reads = [in_]
        kw = {}
        if bias is not None:
            kw['bias'] = bias
            if not isinstance(bias, (int, float)):
                reads.append(bias)
        if scale is not None:
            kw['scale'] = scale
            if not isinstance(scale, (int, float)):
                reads.append(scale)
        writes = [out]
        if accum_out is not None:
            kw['accum_out'] = accum_out
            writes.append(accum_out)
        return self.op(eng, lambda e: e.activation(out=out, in_=in_, func=func, **kw), reads=reads, writes=writes)

    def tt(self, eng, out, in0, in1, op):
        return self.op(eng, lambda e: e.tensor_tensor(out=out, in0=in0, in1=in1, op=op), reads=[in0, in1], writes=[out])

    def ts(self, eng, out, in0, s1, s2, op0, op1=None, accum_out=None):
        reads = [in0]
        if not isinstance(s1, (int, float)):
            reads.append(s1)
        if s2 is not None and not isinstance(s2, (int, float)):
            reads.append(s2)
        kw = {}
        writes = [out]
        if op1 is not None:
            kw['op1'] = op1
        if accum_out is not None:
            kw['accum_out'] = accum_out
            writes.append(accum_out)
        return self.op(eng, lambda e: e.tensor_scalar(out=out, in0=in0, scalar1=s1, scalar2=s2, op0=op0, **kw),
                       reads=reads, writes=writes)

    def stt(self, eng, out, in0, scalar, in1, op0, op1):
        reads = [in0, in1]
        if not isinstance(scalar, (int, float)):
            reads.append(scalar)
        return self.op(eng, lambda e: e.scalar_tensor_tensor(out=out, in0=in0, scalar=scalar, in1=in1, op0=op0, op1=op1),
                       reads=reads, writes=[out])

    def copy(self, eng, out, in_):
        if eng == 'act':
            return self.op(eng, lambda e: e.copy(out=out, in_=in_), reads=[in_], writes=[out])
        return self.op(eng, lambda e: e.tensor_copy(out=out, in_=in_), reads=[in_], writes=[out])

    def memset(self, eng, ap, val):
        return self.op(eng, lambda e: e.memset(ap, val), reads=[], writes=[ap])

    def scan(self, out, d0, d1, initial, op0, op1):
        reads = [d0, d1]
        if not isinstance(initial, (int, float)):
            reads.append(initial)
        return self.op('dve', lambda e: e.tensor_tensor_scan(out=out, data0=d0, data1=d1, initial=initial, op0=op0, op1=op1),
                       reads=reads, writes=[out])

    def generic(self, eng, fn, reads, writes):
        return self.op(eng, fn, reads=reads, writes=writes)


def build_two_pass(make_nc, body):
    nc1 = make_nc()
    p1 = Prog(nc1, None)
    body(p1)
    p1.stack.close()
    plan = p1.make_plan()
    nc2 = make_nc()
    p2 = Prog(nc2, plan)
    body(p2)
    p2.finish()
    p2.stack.close()
    return nc2, plan


T = 4096
NT = 32
D = 1024
KC = 8
TOKC = 2048
TC = 1152
WCOLS = TOKC + TC
EPS = 1e-6


def phase1(P, st, A, layer):
    nc = P.nc
    s = ExitStack()
    W = P.sb(s, "w_in", [128, KC, WCOLS], BF16)
    hT = P.sb(s, "hT", [128, KC, T], BF16)
    ident = P.sb(s, "ident", [128, 128], BF16)
    g1 = P.sb(s, "g1", [128, KC], F32)
    G = P.sb(s, "Gq", [128, 1280], F32)
    cos = P.sb(s, "cos", [128, NT, 8], F32)
    sin = P.sb(s, "sin", [128, NT, 8], F32)
    P.dma(ident[:], A['ident'], 'c0')
    P.dma(g1[:], A['g1'][layer], 'c0')
    P.dma(G[:], A['gq'][layer].partition_broadcast(128), 'c0')
    P.dma(cos[:], A['cos'], 'c0')
    P.dma(sin[:], A['sin'], 'c0')
    P.ts('dve', G[:, 0:512], G[:, 0:512], 0.125, None, ALU.mult)
    P.ts('dve', G[:, 768:1024], G[:, 768:1024], 0.125, None, ALU.mult)

    with ExitStack() as s2:
        wst = [P.sb(s2, "wst%d" % i, [128, WCOLS], F32) for i in range(2)]
        for kc in range(KC):
            b = wst[kc % 2]
            P.dma(b[:], A['win'][layer, kc * 128:(kc + 1) * 128, :], 'wst%d' % (kc % 2), q='sp' if kc % 2 == 0 else 'act')
            half = WCOLS // 2
            P.ts('dve', W[:, kc, 0:half], b[:, 0:half], g1[:, kc:kc + 1], None, ALU.mult)
            P.act(W[:, kc, half:WCOLS], b[:, half:WCOLS], AF.Copy, scale=g1[:, kc:kc + 1])

    with ExitStack() as s2:
        xt = [P.sb(s2, "xt%d" % i, [128, D], F32) for i in range(2)]
        sq = P.sb(s2, "sqj", [128, D], F32)
        hb = [P.sb(s2, "hb%d" % i, [128, D], BF16) for i in range(2)]
        ss = [P.sb(s2, "ss%d" % i, [128, 2], F32) for i in range(2)]
        ptr = [P.ps(s2, "ptr%d" % i, [128, KC, 128], BF16) for i in range(2)]
        for t in range(NT):
            b = t % 2
            P.dma(xt[b][:], A['x'][t * 128:(t + 1) * 128, :], 'xt%d' % b)
            P.act(sq[:], xt[b][:], AF.Square, accum_out=ss[b][:, 0:1])
            P.act(ss[b][:, 1:2], ss[b][:, 0:1], AF.Sqrt, bias=EPS_AP(P), scale=1.0 / D)
            P.op('dve', lambda e, o=ss[b][:, 1:2]: e.reciprocal(out=o, in_=o), reads=[ss[b][:, 1:2]], writes=[ss[b][:, 1:2]])
            P.ts('dve', hb[b][:], xt[b][:], ss[b][:, 1:2], None, ALU.mult)
            for kc in range(KC):
                P.transpose(ptr[b][:, kc, :], hb[b][:, kc * 128:(kc + 1) * 128], ident[:])
            P.copy('act' if t % 2 else 'dve', hT[:, :, t * 128:(t + 1) * 128], ptr[b][:])

    with ExitStack() as s2:
        pp = [P.ps(s2, "ppT%d" % i, [128, 512], F32) for i in range(3)]
        so = [P.sb(s2, "soT%d" % i, [128, 512], F32) for i in range(3)]
        k = 0
        for c in range(TC // 128):
            for j in range(T // 512):
                b = k % 3
                for kc in range(KC):
                    P.mm(pp[b][:], W[:, kc, TOKC + c * 128:TOKC + (c + 1) * 128], hT[:, kc, j * 512:(j + 1) * 512],
                         start=(kc == 0), stop=(kc == KC - 1))
                P.copy('act' if k % 2 else 'dve', so[b][:], pp[b][:])
                P.dma(A['pT'][c * 128:(c + 1) * 128, j * 512:(j + 1) * 512], so[b][:], 'soT%d' % b, q='pool')
                k += 1

    with ExitStack() as s2:
        pg = [P.ps(s2, "pg%d" % i, [128, 512], F32) for i in range(4)]
        ptq = [P.ps(s2, "ptq%d" % i, [128, 4, 128], BF16) for i in range(3)]
        sqhs = [P.sb(s2, "sqh%d" % i, [128, 512], F32) for i in range(3)]
        ssh = [P.sb(s2, "ssh%d" % i, [128, 8], F32) for i in range(4)]
        xn = [P.sb(s2, "xn%d" % i, [128, 512], F32) for i in range(3)]
        qb = [P.sb(s2, "qb%d" % i, [128, 512], BF16) for i in range(3)]
        rts = [P.sb(s2, "rt%d" % i, [128, 4, 8, 8], F32) for i in range(3)]
        qst = [P.sb(s2, "qst%d" % i, [128, 10, 512], BF16) for i in range(2)]
        vst = [P.sb(s2, "vst%d" % i, [128, 768], BF16) for i in range(2)]
        groups = []
        kq = 0
        for t in range(NT):
            for gi in range(4):
                k = t * 4 + gi
                nh = [8, 8, 4, 0][gi]
                qi = None
                if nh:
                    qi = kq % 3
                    kq += 1
                groups.append((t, gi, k, qi))

        def stage(sidx, t, gi, k, q):
            sb_ = (t // 4) % 2
            b = k % 4
            nh = [8, 8, 4, 0][gi]
            nr = [8, 4, 0, 0][gi]
            w = nh * 64
            goff = [0, 512, 1024, 0][gi]
            vb = t % 2
            if sidx == 0:
                for kc in range(KC):
                    P.mm(pg[b][:], hT[:, kc, t * 128:(t + 1) * 128], W[:, kc, gi * 512:(gi + 1) * 512],
                         start=(kc == 0), stop=(kc == KC - 1))
                return
            if sidx == 1:
                if nh:
                    sqh = sqhs[q]
                    P.act(sqh[:, 0:w], pg[b][:, 0:w], AF.Square)
                    P.op('dve', lambda e, o=ssh[b][:, 0:nh], i=sqh[:, 0:w].rearrange("p (h d) -> p h d", d=64):
                         e.tensor_reduce(out=o, in_=i, axis=AX.X, op=ALU.add),
                         reads=[sqh[:, 0:w]], writes=[ssh[b][:, 0:nh]])
                if gi == 2:
                    P.copy('act', vst[vb][:, 0:256], pg[b][:, 256:512])
                if gi == 3:
                    P.copy('act', vst[vb][:, 256:768], pg[b][:, 0:512])
                    P.dma(A['vtok'][t * 128:(t + 1) * 128, :], vst[vb][:], 'vst%d' % vb, q='sp')
                return
            if not nh:
                return
            rt = rts[q]
            xv = xn[q][:, 0:max(nr, 1) * 64].rearrange("p (h d) -> p h d", d=64)
            qv = qb[q][:, 0:max(nr, 1) * 64].rearrange("p (h d) -> p h d", d=64)
            if sidx == 2:
                P.act(ssh[b][:, 0:nh], ssh[b][:, 0:nh], AF.Sqrt, bias=EPS_AP(P), scale=1.0 / 64)
                P.op('dve', lambda e, o=ssh[b][:, 0:nh]: e.reciprocal(out=o, in_=o), reads=[ssh[b][:, 0:nh]], writes=[ssh[b][:, 0:nh]])
                P.tt('dve', xn[q][:, 0:w].rearrange("p (h d) -> p h d", d=64),
                     pg[b][:, 0:w].rearrange("p (h d) -> p h d", d=64),
                     ssh[b][:, 0:nh].unsqueeze(2).broadcast_to([128, nh, 64]), ALU.mult)
            elif sidx == 3:
                if nr:
                    P.tt('pool', xn[q][:, 0:w], xn[q][:, 0:w], G[:, goff:goff + w], ALU.mult)
                    P.copy('act', qb[q][:, 0:w], xn[q][:, 0:w])
                    cb = cos[:, t, :].unsqueeze(1).broadcast_to([128, nr, 8])
                    sb2 = sin[:, t, :].unsqueeze(1).broadcast_to([128, nr, 8])
                    P.tt('dve', rt[:, 0, 0:nr, :], xv[:, :, 0:8], cb, ALU.mult)
                    P.tt('dve', rt[:, 1, 0:nr, :], xv[:, :, 8:16], sb2, ALU.mult)
                    P.tt('pool', rt[:, 2, 0:nr, :], xv[:, :, 8:16], cb, ALU.mult)
                    P.tt('pool', rt[:, 3, 0:nr, :], xv[:, :, 0:8], sb2, ALU.mult)
                else:
                    P.tt('pool', qb[q][:, 0:w], xn[q][:, 0:w], G[:, goff:goff + w], ALU.mult)
            elif sidx == 4:
                if nr:
                    P.tt('dve', qv[:, :, 0:8], rt[:, 0, 0:nr, :], rt[:, 1, 0:nr, :], ALU.subtract)
                    P.tt('pool', qv[:, :, 8:16], rt[:, 2, 0:nr, :], rt[:, 3, 0:nr, :], ALU.add)
                npair = nh // 2
                for pr in range(npair):
                    P.transpose(ptq[q][:, pr, :], qb[q][:, pr * 128:(pr + 1) * 128], ident[:])
            elif sidx == 5:
                npair = nh // 2
                pbase = [0, 4, 8][gi]
                P.copy('act' if gi % 2 else 'dve', qst[sb_][:, pbase:pbase + npair, (t % 4) * 128:(t % 4 + 1) * 128], ptq[q][:, 0:npair, :])
                if t % 4 == 3 and gi == 2:
                    j = t // 4
                    P.dma(A['qkT'][:, j * 512:(j + 1) * 512].rearrange("(a p) n -> p a n", p=128), qst[sb_][:], 'qst%d' % sb_, q='sp')

        NS = 6
        for step in range(len(groups) + NS - 1):
            for sidx in range(NS - 1, -1, -1):
                gidx = step - sidx
                if 0 <= gidx < len(groups):
                    stage(sidx, *groups[gidx])
    s.close()


_eps_cache = {}


def EPS_AP(P):
    return EPS


T = 4096
NEG = -30000.0


class AttnCtx:
    def __init__(self, P, st, consts):
        self.P = P
        self.psS = [P.ps(st, "aS%d" % i, [128, 512], F32) for i in range(2)]
        self.psO = [P.ps(st, "aO%d" % i, [128, 512], F32) for i in range(2)]
        self.psB = [P.ps(st, "aB%d" % i, [128, 512], F32) for i in range(2)]
        self.pT = [P.sb(st, "apT%d" % i, [128, 512], BF16) for i in range(3)]
        self.lr = [P.sb(st, "alr%d" % i, [65, 512], F32) for i in range(2)]
        self.F = [P.sb(st, "aF%d" % i, [64, 512], F32) for i in range(2)]
        self.lrh = [P.sb(st, "alrh%d" % i, [65, 512], BF16) for i in range(2)]
        self.lrl = [P.sb(st, "alrl%d" % i, [65, 512], BF16) for i in range(2)]
        self.G2 = [P.sb(st, "aG%d" % i, [64, 512], F32) for i in range(2)]
        self.kS = 0
        self.kO = 0
        self.kF = 0
        self.vm = 65
        self.prev = None
        self.deferred = []
        self.c = consts


def _push_block(cx, s_fn, exp_fn, pv_fn, first=False):
    if first:
        for f in cx.deferred:
            f()
        cx.deferred = []
    s_fn()
    d = cx.deferred
    cx.deferred = []
    if cx.prev is not None:
        e, p, epi = cx.prev
        e()
        p()
        if epi is not None:
            epi[0]()
            cx.deferred.append(epi[1])
    for f in d:
        f()
    cx.prev = (exp_fn, pv_fn, None)


def _end_chunk(cx, epi_a, epi_b):
    cx.prev = (cx.prev[0], cx.prev[1], (epi_a, epi_b))


def attn_flush(cx):
    d = cx.deferred
    cx.deferred = []
    if cx.prev is not None:
        e, p, epi = cx.prev
        e()
        p()
        if epi is not None:
            epi[0]()
            d.append(epi[1])
        cx.prev = None
    for f in d:
        f()


def _mk_block(cx, po, Kaug, kr, Qaug, q0, Vaug, kt, lo, hi, masks, extra, first, last):
    P = cx.P
    c = cx.c
    ps = cx.psS[cx.kS % 2]
    pt = cx.pT[cx.kS % 3]
    cx.kS += 1

    def s_fn():
        if first:
            P.mm(po[0:cx.vm, :], c['zeros'][:, 0:cx.vm], c['ident_w'][:, 0:512], start=True, stop=False)
        P.mm(ps[:, lo:hi], Kaug[0:kr, kt * 128:(kt + 1) * 128], Qaug[0:kr, q0 + lo:q0 + hi], start=True, stop=(len(masks) == 0 and extra is None))
        if extra is not None:
            P.mm(ps[:, lo:hi], extra[0][0:64, kt * 128:(kt + 1) * 128], extra[1][0:64, q0 + lo:q0 + hi], start=False, stop=(len(masks) == 0))
        for mi, (mk, m) in enumerate(masks):
            P.mm(ps[:, m * 128:(m + 1) * 128], c['ident'][:], mk[:], start=False, stop=(mi == len(masks) - 1))

    def exp_fn():
        P.act(pt[:, lo:hi], ps[:, lo:hi], AF.Exp)

    def pv_fn():
        P.mm(po[0:cx.vm, lo:hi], Vaug[:, kt, 0:cx.vm], pt[:, lo:hi], start=False, stop=last)

    return s_fn, exp_fn, pv_fn


def _mk_factor(cx, po, lng2, gate_c, q0, finish):
    P = cx.P
    c = cx.c
    lr = cx.lr[cx.kF % 2]
    F = cx.F[cx.kF % 2]
    G2 = cx.G2[cx.kF % 2]
    cx.kF += 1
    pb = cx.psB[0]
    pb2 = cx.psB[1]

    lrh = cx.lrh[(cx.kF - 1) % 2]
    lrl = cx.lrl[(cx.kF - 1) % 2]

    def epi_a():
        P.ts('dve', lr[64:65, :], po[64:65, :], 1e-18, None, ALU.max)
        P.act(lr[64:65, :], lr[64:65, :], AF.Ln)
        P.copy('dve', lrh[64:65, :], lr[64:65, :])
        P.tt('dve', lrl[64:65, :], lr[64:65, :], lrh[64:65, :], ALU.subtract)

    def epi_b():
        P.mm(pb[0:64, :], c['negonesb'][64:65, :], lrh[64:65, :], start=True, stop=False)
        P.mm(pb[0:64, :], c['negonesb'][64:65, :], lrl[64:65, :], start=False, stop=True)
        if lng2 is not None:
            P.mm(pb2[0:64, :], c['selnegb'][0:24, gate_c * 64:(gate_c + 1) * 64], lng2[0][0:24, q0:q0 + 512], start=True, stop=False)
            P.mm(pb2[0:64, :], c['selnegb'][0:24, gate_c * 64:(gate_c + 1) * 64], lng2[1][0:24, q0:q0 + 512], start=False, stop=True)
        P.act(F[:], pb[0:64, :], AF.Exp)
        if lng2 is not None:
            P.act(G2[:], pb2[0:64, :], AF.Exp)
            P.tt('pool', F[:], F[:], G2[:], ALU.mult)
        finish(po, F)

    return epi_a, epi_b


def attn_chunk(cx, Kaug, kr, Qaug, j, Vaug, entries, finish, lng2=None, gate_c=None, extra=None):
    po = cx.psO[cx.kO % 2]
    cx.kO += 1
    q0 = j * 512
    n = len(entries)
    for ei, (kt, lo, hi, masks) in enumerate(entries):
        fns = _mk_block(cx, po, Kaug, kr, Qaug, q0, Vaug, kt, lo, hi, masks, extra, ei == 0, ei == n - 1)
        _push_block(cx, *fns, first=(ei == 0))
    ea, eb = _mk_factor(cx, po, lng2, gate_c, q0, finish)
    _end_chunk(cx, ea, eb)


def causal_entries(j, mc):
    ent = []
    for kt in range(4 * j + 4):
        if kt < 4 * j:
            ent.append((kt, 0, 512, []))
        else:
            m = kt - 4 * j
            ent.append((kt, 128 * m, 512, [(mc, m)]))
    return ent


def window_entries(j, mc, mu):
    ent = []
    for cc in range(-4, 4):
        kt = 4 * j + cc
        if kt < 0:
            continue
        lo = 128 * max(cc, 0)
        hi = 128 * (min(cc + 4, 3) + 1)
        masks = []
        if 0 <= cc <= 3:
            masks.append((mc, cc))
        if 0 <= cc + 4 <= 3:
            masks.append((mu, cc + 4))
        ent.append((kt, lo, hi, masks))
    return ent


def load_consts(P, st, A):
    c = {}
    c['ident'] = P.sb(st, "c_ident", [128, 128], BF16)
    c['mc'] = P.sb(st, "c_mc", [128, 128], BF16)
    c['mu'] = P.sb(st, "c_mu", [128, 128], BF16)
    c['zeros'] = P.sb(st, "c_zeros", [128, 128], BF16)
    c['ident_w'] = P.sb(st, "c_identw", [128, 512], BF16)
    c['negones'] = P.sb(st, "c_negones", [65, 64], F32)
    c['negonesb'] = P.sb(st, "c_negonesb", [65, 64], BF16)
    c['selneg'] = P.sb(st, "c_selneg", [24, 24 * 64], F32)
    P.dma(c['ident'][:], A['ident'], 'c0')
    P.dma(c['mc'][:], A['mc'], 'c0')
    P.dma(c['mu'][:], A['mu'], 'c0')
    P.dma(c['selneg'][:], A['selneg'], 'c0')
    P.memset('dve', c['zeros'][:], 0.0)
    P.memset('dve', c['ident_w'][:], 0.0)
    P.memset('dve', c['negones'][:], -1.0)
    P.memset('dve', c['negonesb'][:], -1.0)
    return c


def phase_fox(P, A, layer, consts):
    with ExitStack() as st:
        cx = AttnCtx(P, st, consts)
        cf = P.sb(st, "f_cf", [4, T], F32)
        tmp = P.sb(st, "f_tmp", [4, T], F32)
        ones = P.sb(st, "f_ones", [4, T], F32)
        fb = P.sb(st, "f_fb", [4, 2], F32)
        cs = P.sb(st, "f_cs", [4, 3, T], BF16)
        ncs = P.sb(st, "f_ncs", [4, 3, T], BF16)
        P.dma(cf[:], A['pT'][1048:1052, :], 'fx0')
        P.dma(fb[:, 0:1], A['fb'][layer], 'fx0')
        P.ts('dve', fb[:, 1:2], fb[:, 0:1], -1.0, None, ALU.mult)
        P.memset('pool', ones[:], 1.0)
        P.act(tmp[:], cf[:], AF.Exp, bias=fb[:, 1:2], scale=-1.0)
        P.act(tmp[:], tmp[:], AF.Ln, bias=1.0)
        P.scan(cf[:], ones[:], tmp[:], 0.0, ALU.mult, ALU.subtract)
        P.copy('dve', cs[:, 0, :], cf[:])
        P.tt('dve', tmp[:], cf[:], cs[:, 0, :], ALU.subtract)
        P.copy('dve', cs[:, 1, :], tmp[:])
        P.tt('dve', tmp[:], tmp[:], cs[:, 1, :], ALU.subtract)
        P.copy('dve', cs[:, 2, :], tmp[:])
        P.ts('dve', ncs[:].rearrange("p a t -> p (a t)"), cs[:].rearrange("p a t -> p (a t)"), -1.0, None, ALU.mult)
        for h in range(4):
            with ExitStack() as s2:
                Q = P.sb(s2, "f_Q", [128, T], BF16)
                K = P.sb(s2, "f_K", [128, T], BF16)
                V = P.sb(s2, "f_V", [128, 32, 65], BF16)
                ob = [P.sb(s2, "f_ob%d" % i, [64, 512], BF16) for i in range(2)]
                P.dma(Q[0:64, :], A['qkT'][768 + 64 * h:768 + 64 * (h + 1), :], 'fxq')
                P.dma(K[0:64, :], A['qkT'][1024 + 64 * h:1024 + 64 * (h + 1), :], 'fxk')
                P.memset('pool', Q[64:128, :], 0.0)
                P.memset('pool', K[64:128, :], 0.0)
                P.memset('pool', Q[64:70, :], 1.0)
                P.memset('pool', K[64:70, :], 1.0)
                for i in range(3):
                    P.dma(Q[64 + i:65 + i, :], cs[h:h + 1, i, :], 'fxq')
                    P.dma(K[67 + i:68 + i, :], ncs[h:h + 1, i, :], 'fxk')
                P.memset('pool', V[:, :, 64:65], 1.0)
                P.dma(V[:, :, 0:64], A['vtok'][:, 512 + 64 * h:512 + 64 * (h + 1)].rearrange("(n p) d -> p n d", p=128), 'fxv')
                for j in range(8):
                    def fin(po, F, o=ob[j % 2], j=j, h=h):
                        P.tt('dve', o[:], po[0:64, :], F[:], ALU.mult)
                        P.dma(A['mixT'][768 + 64 * h:768 + 64 * (h + 1), j * 512:(j + 1) * 512], o[:], 'fxo%d' % (j % 2), q='pool')
                    attn_chunk(cx, K, 128, Q, j, V, causal_entries(j, consts['mc']), fin)
                attn_flush(cx)

import math, os
STAGE = int(os.environ.get('STAGE', '99'))

T = 4096
LN8 = math.log(0.125)


def phase_hgrn(P, A, layer, consts):
    for ct in range(2):
        with ExitStack() as st:
            B = [P.sb(st, "hB%d" % i, [128, T], F32) for i in range(5)]
            qt = P.sb(st, "h_qt", [128, T], BF16)
            kt = P.sb(st, "h_kt", [128, 2, T], BF16)
            qg = P.sb(st, "h_qg", [128, T], BF16)
            kd = P.sb(st, "h_kd", [128, T], BF16)
            kdt = P.sb(st, "h_kdt", [128, 32, 2, 128], BF16)
            Vt = P.sb(st, "h_Vt", [128, 32, 128], BF16)
            Vz = P.sb(st, "h_Vz", [128, 32, 2, 128], BF16)
            Sbd = P.sb(st, "h_Sbd", [128, 64, 128], BF16)
            rst = P.sb(st, "h_rst", [128, T], BF16)
            sm = P.sb(st, "h_sm", [128, 8], F32)
            dl = P.sb(st, "h_dl", [128, 64], F32)
            mh = P.sb(st, "h_mh", [128, 128], BF16)
            bones = P.sb(st, "h_bones", [128, 128], BF16)
            ident = consts['ident']
            P.dma(mh[:], A['mh'], 'hg0')
            P.dma(bones[:], A['bones'], 'hg0')
            P.dma(sm[:, 0:2], A['lbl'][ct], 'hg0')
            P.dma(sm[:, 4:5], A['og'][layer], 'hg0')
            P.dma(B[0][:], A['pT'][256 + ct * 128:256 + (ct + 1) * 128, :], 'hgz')
            P.dma(B[3][:], A['pT'][ct * 128:(ct + 1) * 128, :], 'hgq', q='act')
            P.dma(Vt[:], A['vtok'][:, ct * 128:(ct + 1) * 128].rearrange("(n p) d -> p n d", p=128), 'hgv', q='pool')
            P.memset('pool', Vz[:], 0.0)
            for hh in range(2):
                P.dma(Vz[:, :, hh, hh * 64:(hh + 1) * 64],
                      A['vtok'][:, ct * 128 + hh * 64:ct * 128 + (hh + 1) * 64].rearrange("(n p) d -> p n d", p=128), 'hgv', q='pool')
            P.memset('pool', Sbd[:], 0.0)
            P.memset('pool', kt[:], 0.0)
            P.memset('pool', kdt[:], 0.0)
            P.memset('pool', rst[:], 1.0)
            P.memset('pool', rst[:].rearrange("p (c s) -> p c s", s=64)[:, :, 0:1], 0.0)
            lb = sm[:, 2:3]; oml = sm[:, 3:4]; noml = sm[:, 5:6]
            if layer == 0:
                P.memset('dve', lb, 0.0)
            else:
                P.act(sm[:, 0:2], sm[:, 0:2], AF.Exp)
                P.tt('dve', sm[:, 6:7], sm[:, 0:1], sm[:, 1:2], ALU.add)
                P.op('dve', lambda e, o=sm[:, 6:7]: e.reciprocal(out=o, in_=o), reads=[sm[:, 6:7]], writes=[sm[:, 6:7]])
                P.tt('dve', lb, sm[:, 1:2], sm[:, 6:7], ALU.mult)
            P.ts('dve', oml, lb, -1.0, 1.0, ALU.mult, ALU.add)
            P.ts('dve', noml, oml, -1.0, None, ALU.mult)
            P.act(B[0][:], B[0][:], AF.Sigmoid)
            P.ts('dve', B[1][:], B[0][:], oml, lb, ALU.mult, ALU.add)
            P.act(B[1][:], B[1][:], AF.Ln)
            P.scan(B[2][:], rst[:], B[1][:], 0.0, ALU.mult, ALU.add)
            P.ts('dve', B[1][:], B[0][:], noml, oml, ALU.mult, ALU.add)
            G3 = B[2][:].rearrange("p (c s) -> p c s", s=64)
            D3 = B[0][:].rearrange("p (c s) -> p c s", s=64)
            P.tt('dve', D3, G3, G3[:, :, 31:32].broadcast_to([128, 64, 64]), ALU.subtract)
            P.act(B[4][:], B[0][:], AF.Exp, bias=LN8)
            P.tt('dve', qt[:], B[3][:], B[4][:], ALU.mult)
            P.act(B[4][:], B[0][:], AF.Exp, scale=-1.0)
            P.tt('dve', kt[0:64, 0, :], B[1][0:64, :], B[4][0:64, :], ALU.mult)
            P.tt('dve', kt[64:128, 1, :], B[1][64:128, :], B[4][64:128, :], ALU.mult)
            P.act(B[4][:], B[2][:], AF.Exp, bias=LN8)
            P.tt('dve', qg[:], B[3][:], B[4][:], ALU.mult)
            P.tt('dve', D3, G3[:, :, 63:64].broadcast_to([128, 64, 64]), G3, ALU.subtract)
            P.act(B[4][:], B[0][:], AF.Exp)
            P.tt('dve', kd[:], B[1][:], B[4][:], ALU.mult)
            P.act(dl[:].unsqueeze(2), G3[:, :, 63:64], AF.Exp)
            P.memset('dve', dl[:, 0:1], 0.0)
            KV = B[0]; dfull = B[1]; Sall = B[3]; oT = B[4]
            if STAGE < 1:
                P.dma(A['mixT'][0:128, 0:T], kd[:], 'dbg'); continue
            with ExitStack() as s2:
                ptr = [P.ps(s2, "h_ptr%d" % i, [128, 8, 128], BF16) for i in range(2)]
                for g in range(4):
                    for i in range(8):
                        tl = g * 8 + i
                        P.transpose(ptr[g % 2][:, i, :], kd[:, tl * 128:(tl + 1) * 128], ident[:])
                    P.copy('act', kdt[0:64, g * 8:(g + 1) * 8, 0, :], ptr[g % 2][0:64, :, :])
                    P.copy('dve', kdt[64:128, g * 8:(g + 1) * 8, 1, :], ptr[g % 2][64:128, :, :])
            with ExitStack() as s2:
                pkv = [P.ps(s2, "h_pkv%d" % i, [128, 4, 128], F32) for i in range(2)]
                KV3 = KV[:].rearrange("p (v c) -> p v c", c=64)
                for g in range(16):
                    pk = pkv[g % 2]
                    for i in range(4):
                        c = g * 4 + i
                        tl = c // 2; hf = c % 2
                        P.mm(pk[:, i, :], kdt[:, tl, hf, :], Vt[:, tl, :], start=True, stop=True)
                    for hh in range(2):
                        P.copy('act' if hh else 'dve', KV3[hh * 64:(hh + 1) * 64, :, g * 4:(g + 1) * 4],
                               pk[hh * 64:(hh + 1) * 64, :, hh * 64:(hh + 1) * 64].rearrange("p g v -> p v g"))
            if STAGE < 2:
                P.dma(A['mixT'][0:128, 0:T], kd[:], 'dbg'); continue
            P.copy('pool', dfull[:].rearrange("p (v c) -> p v c", c=64), dl[:].unsqueeze(1).broadcast_to([128, 64, 64]))
            P.scan(Sall[:], dfull[:], KV[:], 0.0, ALU.mult, ALU.add)
            S3 = Sall[:].rearrange("p (v c) -> p v c", c=64)
            for hh in range(2):
                P.copy('dve' if hh else 'act', Sbd[hh * 64:(hh + 1) * 64, 1:64, hh * 64:(hh + 1) * 64],
                       S3[hh * 64:(hh + 1) * 64, :, 0:63].rearrange("p v c -> p c v"))
            if STAGE < 3:
                P.dma(A['mixT'][0:128, 0:T], kd[:], 'dbg'); continue
            with ExitStack() as s2:
                pA = [P.ps(s2, "h_pA%d" % i, [128, 128], F32) for i in range(4)]
                po = [P.ps(s2, "h_po%d" % i, [128, 128], F32) for i in range(2)]
                Am = [P.sb(s2, "h_Am%d" % i, [128, 128], BF16) for i in range(4)]
                for tl in range(32):
                    cols = slice(tl * 128, (tl + 1) * 128)
                    for hh in range(2):
                        i = (tl % 2) * 2 + hh
                        P.mm(pA[i][:], kt[:, hh, cols], qt[:, cols], start=True, stop=True)
                        P.tt('dve', Am[i][:], pA[i][:], mh[:], ALU.mult)
                    p_ = po[tl % 2]
                    P.mm(p_[:], Vz[:, tl, 0, :], Am[(tl % 2) * 2][:], start=True, stop=False)
                    P.mm(p_[:], Vz[:, tl, 1, :], Am[(tl % 2) * 2 + 1][:], start=False, stop=False)
                    P.mm(p_[:, 0:64], Sbd[:, 2 * tl, :], qg[:, tl * 128:tl * 128 + 64], start=False, stop=False)
                    P.mm(p_[:, 64:128], Sbd[:, 2 * tl + 1, :], qg[:, tl * 128 + 64:tl * 128 + 128], start=False, stop=True)
                    P.copy('act', oT[:, cols], p_[:])
            if STAGE < 4:
                P.dma(A['mixT'][0:128, 0:T], kd[:], 'dbg'); continue
            with ExitStack() as s2:
                pss = [P.ps(s2, "h_pss%d" % i, [128, 512], F32) for i in range(2)]
                sq = [P.sb(s2, "h_sq%d" % i, [128, 512], BF16) for i in range(2)]
                rs = [P.sb(s2, "h_rs%d" % i, [128, 512], F32) for i in range(2)]
                ag = [P.sb(s2, "h_ag%d" % i, [128, 512], F32) for i in range(2)]
                ob = [P.sb(s2, "h_ob%d" % i, [128, 512], BF16) for i in range(2)]
                for j in range(8):
                    b = j % 2
                    cols = slice(j * 512, (j + 1) * 512)
                    P.dma(ag[b][:], A['pT'][512 + ct * 128:512 + (ct + 1) * 128, cols], 'hga%d' % b)
                    P.act(sq[b][:], oT[:, cols], AF.Square)
                    P.mm(pss[b][:], bones[:], sq[b][:], start=True, stop=True)
                    P.act(rs[b][:], pss[b][:], AF.Sqrt, bias=1e-6, scale=1.0 / 64)
                    P.op('dve', lambda e, o=rs[b][:]: e.reciprocal(out=o, in_=o), reads=[rs[b][:]], writes=[rs[b][:]])
                    P.act(ag[b][:], ag[b][:], AF.Silu)
                    P.stt('dve', rs[b][:], oT[:, cols], sm[:, 4:5], rs[b][:], ALU.mult, ALU.mult)
                    P.tt('pool', ob[b][:], rs[b][:], ag[b][:], ALU.mult)
                    P.dma(A['mixT'][ct * 128:(ct + 1) * 128, cols], ob[b][:], 'hgo%d' % b, q='pool')

import os
STAGE = int(os.environ.get('STAGE', '99'))

T = 4096
NEG = -30000.0


def phase_nsa(P, A, layer, consts):
    c = consts
    ident = c['ident']
    with ExitStack() as st:
        cx = AttnCtx(P, st, consts)
        lngh = P.sb(st, "n_lngh", [24, T], BF16)
        lngl = P.sb(st, "n_lngl", [24, T], BF16)
        c['selnegb'] = P.sb(st, "n_selnegb", [24, 24 * 64], BF16)
        P.copy('dve', c['selnegb'][:], c['selneg'][:])
        with ExitStack() as s0:
            lng = P.sb(s0, "n_lng", [24, T], F32)
            P.dma(lng[:], A['pT'][1024:1048, :], 'ns0')
            P.act(lng[:], lng[:], AF.Exp, scale=-1.0)
            P.act(lng[:], lng[:], AF.Ln, bias=1.0)
            P.copy('dve', lngh[:], lng[:])
            P.tt('dve', lng[:], lng[:], lngh[:], ALU.subtract)
            P.copy('dve', lngl[:], lng[:])
        lng2 = (lngh, lngl)
        ovaug = P.sb(st, "n_ov", [128, 2, 72], BF16)
        wc = P.sb(st, "n_wc", [128, 3200], BF16)
        addm = P.sb(st, "n_addm", [128, 32, 64], F32)
        P.dma(ovaug[:], A['ovaug'], 'ns0')
        P.dma(wc[:], A['wc'], 'ns0')
        P.dma(addm[:], A['addmask'], 'ns0')
        kcTs = [P.sb(st, "n_kcT%d" % i, [128, 256], BF16) for i in range(2)]
        vcAs = [P.sb(st, "n_vcA%d" % i, [128, 2, 65], BF16) for i in range(2)]
        for g in range(2):
            kcT = kcTs[g]; vcA = vcAs[g]
            with ExitStack() as s2:
                w1 = P.sb(s2, "n_w1", [64, 32, 128], BF16)
                w1f = P.sb(s2, "n_w1f", [64, 32, 128], F32)
                w2 = P.sb(s2, "n_w2", [128, 64], BF16)
                w2f = P.sb(s2, "n_w2f", [128, 64], F32)
                posT = P.sb(s2, "n_posT", [64, 32], BF16)
                posf = P.sb(s2, "n_posf", [64, 32], F32)
                posb = P.sb(s2, "n_posb", [64, 32, 256], BF16)
                srcf = P.sb(s2, "n_srcf", [64, T], F32)
                srcb = P.sb(s2, "n_srcb", [64, T], BF16)
                bias = P.sb(s2, "n_bias", [128, 1], F32)
                xb = P.sb(s2, "n_xb", [128, 256], F32)
                x2 = P.sb(s2, "n_x2", [128, 256], F32)
                hid = P.sb(s2, "n_hid", [128, 256], BF16)
                ktm = P.sb(s2, "n_ktm", [128, 64], F32)
                kts = P.sb(s2, "n_kts", [128, 64], F32)
                ktb = P.sb(s2, "n_ktb", [128, 128], BF16)
                sm = P.sb(s2, "n_sm", [128, 4], F32)
                rt = P.sb(s2, "n_rt", [128, 4, 8], F32)
                kng = P.sb(s2, "n_kng", [128, 64], F32)
                cosc = P.sb(s2, "n_cosc", [128, 2, 8], F32)
                sinc = P.sb(s2, "n_sinc", [128, 2, 8], F32)
                ph = cx.psS[0]; pb = cx.psS[1]; po = cx.psO[0]
                pt = P.ps(s2, "n_pt", [128, 128], BF16)
                P.dma(kng[:], A['kng'][layer].partition_broadcast(128), 'ns1')
                P.dma(cosc[:], A['cosc'], 'ns1')
                P.dma(sinc[:], A['sinc'], 'ns1')
                P.memset('dve', hid[:], 0.0)
                P.memset('dve', vcA[:], 0.0)
                P.memset('dve', kcT[:], 0.0)
                P.memset('dve', ktb[:], 0.0)
                for which in range(2):
                    P.dma(w1f[:], A['w1r'][layer, which], 'ns2')
                    P.dma(w2f[:], A['w2'][layer, which], 'ns2')
                    P.dma(posf[:], A['posT'][layer, which], 'ns2')
                    P.dma(srcf[:], A['pT'][768 + 128 * which + 64 * g:768 + 128 * which + 64 * (g + 1), :], 'ns3', q='act')
                    P.copy('dve', w1[:], w1f[:])
                    P.copy('dve', w2[:], w2f[:])
                    P.copy('dve', posT[:], posf[:])
                    P.copy('dve', posb[:], posT[:].unsqueeze(2).broadcast_to([64, 32, 256]))
                    P.copy('act', srcb[:], srcf[:])
                    for l in range(32):
                        P.mm(ph[:, 0:255], w1[:, l, :], srcb[:].rearrange("p (n s) -> p n s", s=16)[:, (l // 16):(l // 16) + 255, l % 16], start=(l == 0), stop=False)
                    for l in range(32):
                        P.mm(ph[:, 0:255], w1[:, l, :], posb[:, l, 0:255], start=False, stop=(l == 31))
                    P.copy('act', xb[:, 0:255], ph[:, 0:255])
                    P.tt('dve', x2[:, 0:255], xb[:, 0:255], xb[:, 0:255], ALU.mult)
                    P.ts('dve', x2[:, 0:255], x2[:, 0:255], 0.044715, 1.0, ALU.mult, ALU.add)
                    P.tt('dve', x2[:, 0:255], x2[:, 0:255], xb[:, 0:255], ALU.mult)
                    P.act(x2[:, 0:255], x2[:, 0:255], AF.Sigmoid, scale=1.5957691216057308)
                    P.tt('dve', hid[:, 0:255], x2[:, 0:255], xb[:, 0:255], ALU.mult)
                    for nt in range(2):
                        P.mm(po[:, 0:64], hid[:, nt * 128:(nt + 1) * 128], w2[:], start=True, stop=True)
                        if which == 1:
                            nr = 128 if nt == 0 else 127
                            P.copy('act', vcA[0:nr, nt, 0:64], po[0:nr, 0:64])
                            P.memset('dve', vcA[0:nr, nt, 64:65], 1.0)
                        else:
                            P.act(kts[:], po[:, 0:64], AF.Square, accum_out=sm[:, 0:1])
                            P.act(sm[:, 1:2], sm[:, 0:1], AF.Sqrt, bias=1e-6, scale=1.0 / 64)
                            P.op('dve', lambda e, o=sm[:, 1:2]: e.reciprocal(out=o, in_=o), reads=[sm[:, 1:2]], writes=[sm[:, 1:2]])
                            P.stt('dve', ktm[:], po[:, 0:64], sm[:, 1:2], kng[:], ALU.mult, ALU.mult)
                            P.copy('act', ktb[:, 0:64], ktm[:])
                            P.tt('dve', rt[:, 0, :], ktm[:, 0:8], cosc[:, nt, :], ALU.mult)
                            P.tt('dve', rt[:, 1, :], ktm[:, 8:16], sinc[:, nt, :], ALU.mult)
                            P.tt('dve', rt[:, 2, :], ktm[:, 8:16], cosc[:, nt, :], ALU.mult)
                            P.tt('dve', rt[:, 3, :], ktm[:, 0:8], sinc[:, nt, :], ALU.mult)
                            P.tt('dve', ktb[:, 0:8], rt[:, 0, :], rt[:, 1, :], ALU.subtract)
                            P.tt('dve', ktb[:, 8:16], rt[:, 2, :], rt[:, 3, :], ALU.add)
                            P.transpose(pt[:], ktb[:], ident[:])
                            P.copy('dve', kcT[0:64, nt * 128:(nt + 1) * 128], pt[0:64, :])
            P.memset('dve', kcT[0:64, 255:256], 0.0)
        selT = P.sb(st, "n_selT", [128, T], BF16)
        imp = P.sb(st, "n_imp", [128, 32, 64], F32)
        acc = [P.sb(st, "n_acc%d" % i, [64, T], F32) for i in range(4)]
        Q = [P.sb(st, "n_Q%d" % i, [128, T], BF16) for i in range(4)]
        Ks = P.sb(st, "n_Ks", [128, T], BF16)
        Kw = P.sb(st, "n_Kw", [128, T], BF16)
        Vs = P.sb(st, "n_Vs", [128, 32, 65], BF16)
        Vw = P.sb(st, "n_Vw", [128, 32, 65], BF16)
        for g in range(2):
            kcT = kcTs[g]; vcA = vcAs[g]
            for hh in range(4):
                h = 4 * g + hh
                P.memset('pool', Q[hh][64:128, :], 0.0)
                P.dma(Q[hh][0:64, :], A['qkT'][64 * h:64 * (h + 1), :], 'nsq%d' % hh)
            P.dma(Ks[0:64, :], A['qkT'][512 + 64 * g:512 + 64 * (g + 1), :], 'nsk')
            P.dma(Ks[64:128, :], A['eall'], 'nsk')
            P.dma(Kw[0:64, :], A['qkT'][640 + 64 * g:640 + 64 * (g + 1), :], 'nsk')
            P.memset('pool', Kw[64:128, :], 0.0)
            P.memset('pool', Vs[:, :, 64:65], 1.0)
            P.memset('pool', Vw[:, :, 64:65], 1.0)
            P.dma(Vs[:, :, 0:64], A['vtok'][:, 256 + 64 * g:256 + 64 * (g + 1)].rearrange("(n p) d -> p n d", p=128), 'nsv', q='act')
            P.dma(Vw[:, :, 0:64], A['vtok'][:, 384 + 64 * g:384 + 64 * (g + 1)].rearrange("(n p) d -> p n d", p=128), 'nsv', q='act')
            with ExitStack() as s2:
                pimp = [P.ps(s2, "n_pimp%d" % i, [128, 4, 72], F32) for i in range(2)]
                pTc = [P.sb(s2, "n_pTc%d" % i, [128, 512], BF16) for i in range(3)]
                rinvs = [P.sb(s2, "n_rinv%d" % i, [128, 4], F32) for i in range(2)]
                kc_ = 0
                kch = 0
                for hh in range(4):
                    h = 4 * g + hh
                    for j in range(8):
                        q0 = j * 512
                        po = cx.psO[cx.kO % 2]
                        cx.kO += 1
                        pim = pimp[kch % 2]
                        rinv = rinvs[kch % 2]
                        kch += 1
                        tiles = []
                        for nt in range(2):
                            off = 2048 * nt + 31 - 512 * j
                            if -off + 511 < 0:
                                continue
                            tiles.append((nt, off))
                        for ti, (nt, off) in enumerate(tiles):
                            ps = cx.psS[cx.kS % 2]
                            cx.kS += 1
                            ptc = pTc[kc_ % 3]
                            kc_ += 1
                            first = (ti == 0)
                            last = (ti == len(tiles) - 1)

                            def s_fn(ps=ps, nt=nt, off=off, first=first, po=po, pim=pim, hh=hh, q0=q0):
                                if first:
                                    P.mm(po[0:65, :], c['zeros'][:, 0:65], c['ident_w'][:, 0:512], start=True, stop=False)
                                    P.mm(pim[:].rearrange("p a b -> p (a b)"), c['zeros'][:, 0:128], c['ident_w'][:, 0:288], start=True, stop=False)
                                full = (-off >= 2032)
                                P.mm(ps[:], kcT[:, nt * 128:(nt + 1) * 128], Q[hh][:, q0:q0 + 512], start=True, stop=full)
                                if not full:
                                    ci0 = -off + 511
                                    P.mm(ps[:], ident[:], wc[:, ci0:ci0 + 512], start=False, stop=True)

                            def exp_fn(ps=ps, ptc=ptc):
                                P.act(ptc[:], ps[:], AF.Exp)

                            def pv_fn(po=po, pim=pim, ptc=ptc, nt=nt, last=last):
                                P.mm(po[0:65, :], vcA[:, nt, :], ptc[:], start=False, stop=last)
                                for m in range(4):
                                    P.mm(pim[:, m, :], ptc[:, m * 128:(m + 1) * 128], ovaug[:, nt, :], start=False, stop=last)

                            _push_block(cx, s_fn, exp_fn, pv_fn, first=first)

                        def fin(po_, F, hh=hh, q0=q0, pim=pim, rinv=rinv, j=j):
                            for m in range(4):
                                tq = j * 4 + m
                                if hh == 0:
                                    P.ts('dve', imp[:, tq, :], pim[:, m, 0:64], rinv[:, m:m + 1], None, ALU.mult)
                                else:
                                    P.stt('dve', imp[:, tq, :], pim[:, m, 0:64], rinv[:, m:m + 1], imp[:, tq, :], ALU.mult, ALU.add)
                            P.tt('dve', acc[hh][:, q0:q0 + 512], po_[0:64, :], F[:], ALU.mult)

                        ea, eb = _mk_factor(cx, po, lng2, h, q0, fin)

                        def ea2(ea=ea, pim=pim, rinv=rinv):
                            ea()
                            P.ts('dve', rinv[:, 0:4].unsqueeze(2), pim[:, :, 64:65], 1e-30, None, ALU.max)
                            P.op('dve', lambda e, o=rinv[:, 0:4]: e.reciprocal(out=o, in_=o), reads=[rinv[:, 0:4]], writes=[rinv[:, 0:4]])

                        _end_chunk(cx, ea2, eb)
                attn_flush(cx)
            if STAGE < 2:
                continue
            with ExitStack() as s2:
                wk = [P.sb(s2, "n_wk%d" % i, [128, 64], F32) for i in range(2)]
                w2_ = [P.sb(s2, "n_wk2%d" % i, [128, 64], F32) for i in range(2)]
                m8 = [P.sb(s2, "n_m8%d" % i, [128, 16], F32) for i in range(2)]
                sb_ = [P.sb(s2, "n_sb%d" % i, [128, 128], BF16) for i in range(2)]
                pts = [P.ps(s2, "n_pts%d" % i, [128, 128], BF16) for i in range(2)]
                P.memset('pool', sb_[0][:], 0.0)
                P.memset('pool', sb_[1][:], 0.0)
                for tq in range(32):
                    b = tq % 2
                    P.tt('dve', wk[b][:], imp[:, tq, :], addm[:, tq, :], ALU.add)
                    P.op('dve', lambda e, o=m8[b][:, 0:8], i=wk[b][:]: e.max(out=o, in_=i), reads=[wk[b][:]], writes=[m8[b][:, 0:8]])
                    P.op('dve', lambda e, o=w2_[b][:], r=m8[b][:, 0:8], i=wk[b][:]: e.match_replace(out=o, in_to_replace=r, in_values=i, imm_value=-3.0e38),
                         reads=[m8[b][:, 0:8], wk[b][:]], writes=[w2_[b][:]])
                    P.op('dve', lambda e, o=m8[b][:, 8:16], i=w2_[b][:]: e.max(out=o, in_=i), reads=[w2_[b][:]], writes=[m8[b][:, 8:16]])
                    P.ts('dve', w2_[b][:], wk[b][:], m8[b][:, 15:16], None, ALU.is_ge)
                    P.ts('dve', wk[b][:], wk[b][:], -5.0e29, None, ALU.is_gt)
                    P.tt('dve', wk[b][:], wk[b][:], w2_[b][:], ALU.mult)
                    P.ts('dve', sb_[b][:, 64:128], wk[b][:], -1.0, -NEG, ALU.add, ALU.mult)
                    P.transpose(pts[b][:], sb_[b][:], ident[:])
                    P.copy('act', selT[64:128, tq * 128:(tq + 1) * 128], pts[b][64:128, :])
            for hh in range(4):
                P.dma(Q[hh][64:128, :], selT[64:128, :], 'nsq%d' % hh, q='sp' if hh % 2 else 'act')
            if STAGE < 3:
                continue
            with ExitStack() as s2:
                tmp = [P.sb(s2, "n_tmp%d" % i, [64, 512], F32) for i in range(2)]
                ob = [P.sb(s2, "n_ob%d" % i, [64, 512], BF16) for i in range(1)] * 2
                for hh in range(4):
                    h = 4 * g + hh
                    for j in range(8):
                        q0 = j * 512
                        a = acc[hh][:, q0:q0 + 512]

                        def fin_s(po_, F, a=a):
                            P.tt('dve', tmp[0][:], po_[0:64, :], F[:], ALU.mult)
                            P.tt('pool', a, a, tmp[0][:], ALU.add)

                        def fin_w(po_, F, a=a, j=j, h=h, q0=q0):
                            P.tt('dve', tmp[1][:], po_[0:64, :], F[:], ALU.mult)
                            P.tt('pool', ob[j % 2][:], a, tmp[1][:], ALU.add)
                            if STAGE >= 5:
                                P.dma(A['mixT'][256 + 64 * h:256 + 64 * (h + 1), q0:q0 + 512], ob[j % 2][:], 'nso%d' % (j % 2), q='sp')

                        attn_chunk(cx, Ks, 128, Q[hh], j, Vs, causal_entries(j, c['mc']), fin_s, lng2=lng2, gate_c=8 + h)
                        if STAGE >= 4:
                            attn_chunk(cx, Kw, 128, Q[hh], j, Vw, window_entries(j, c['mc'], c['mu']), fin_w, lng2=lng2, gate_c=16 + h)
                attn_flush(cx)


T = 4096
D = 1024
FF = 4096


def phase_wo(P, A, layer, consts, x_in, x_mid):
    ident = consts['ident']
    with ExitStack() as st:
        Wo = P.sb(st, "wo", [128, 8, D], BF16)
        with ExitStack() as s2:
            wst = [P.sb(s2, "wost%d" % i, [128, D], F32) for i in range(2)]
            for kc in range(8):
                b = wst[kc % 2]
                P.dma(b[:], A['wo'][layer, kc * 128:(kc + 1) * 128, :], 'wost%d' % (kc % 2))
                P.copy('act' if kc % 2 else 'dve', Wo[:, kc, :], b[:])
        mx = [P.sb(st, "wo_mx%d" % i, [128, 8, 512], BF16) for i in range(2)]
        xt = [P.sb(st, "wo_xt%d" % i, [128, D], F32) for i in range(2)]
        xm = [P.sb(st, "wo_xm%d" % i, [128, D], F32) for i in range(2)]
        sqs = [P.sb(st, "wo_sq%d" % i, [128, D], F32) for i in range(2)]
        hb = [P.sb(st, "wo_hb%d" % i, [128, D], BF16) for i in range(2)]
        ss = [P.sb(st, "wo_ss%d" % i, [128, 2], F32) for i in range(2)]
        hst = [P.sb(st, "wo_hst%d" % i, [128, 8, 512], BF16) for i in range(2)]
        po = [P.ps(st, "wo_po%d" % i, [128, 512], F32) for i in range(4)]
        ptr = [P.ps(st, "wo_ptr%d" % i, [128, 8, 128], BF16) for i in range(2)]
        for t in range(32):
            j = t // 4
            b = t % 2
            sq = sqs[t % 2]
            if t % 4 == 0:
                P.dma(mx[j % 2][:], A['mixT'][:, j * 512:(j + 1) * 512].rearrange("(a p) n -> p a n", p=128), 'womx%d' % (j % 2))
            P.dma(xt[b][:], x_in[t * 128:(t + 1) * 128, :], 'woxt%d' % b, q='act')
            for half in range(2):
                pp = po[(t % 2) * 2 + half]
                for kc in range(8):
                    P.mm(pp[:], mx[j % 2][:, kc, (t % 4) * 128:(t % 4 + 1) * 128], Wo[:, kc, half * 512:(half + 1) * 512],
                         start=(kc == 0), stop=(kc == 7))
                P.tt('dve', xm[b][:, half * 512:(half + 1) * 512], pp[:], xt[b][:, half * 512:(half + 1) * 512], ALU.add)
            P.dma(x_mid[t * 128:(t + 1) * 128, :], xm[b][:], 'woxm%d' % b, q='pool')
            P.act(sq[:], xm[b][:], AF.Square, accum_out=ss[b][:, 0:1])
            P.act(ss[b][:, 1:2], ss[b][:, 0:1], AF.Sqrt, bias=1e-6, scale=1.0 / D)
            P.op('dve', lambda e, o=ss[b][:, 1:2]: e.reciprocal(out=o, in_=o), reads=[ss[b][:, 1:2]], writes=[ss[b][:, 1:2]])
            P.ts('dve', hb[b][:], xm[b][:], ss[b][:, 1:2], None, ALU.mult)
            for kc in range(8):
                P.transpose(ptr[t % 2][:, kc, :], hb[b][:, kc * 128:(kc + 1) * 128], ident[:])
            P.copy('act', hst[j % 2][:, :, (t % 4) * 128:(t % 4 + 1) * 128], ptr[t % 2][:])
            if t % 4 == 3:
                P.dma(A['h2T'][:, j * 512:(j + 1) * 512].rearrange("(a p) n -> p a n", p=128), hst[j % 2][:], 'wohst%d' % (j % 2), q='pool')


def phase_ffn(P, A, layer, consts, x_mid, x_out):
    with ExitStack() as st:
        Wu = P.sb(st, "wu", [128, 8, FF], BF16)
        Wd = P.sb(st, "wd", [128, 32, D], BF16)
        g2 = P.sb(st, "g2", [128, 8], F32)
        P.dma(g2[:], A['g2'][layer], 'ff0')
        with ExitStack() as s2:
            wst = [P.sb(s2, "fwst%d" % i, [128, 2048], F32) for i in range(3)]
            k = 0
            for kc in range(8):
                for hf in range(2):
                    b = k % 3
                    P.dma(wst[b][:], A['wup'][layer, kc * 128:(kc + 1) * 128, hf * 2048:(hf + 1) * 2048], 'fwst%d' % b, q='sp' if k % 2 else 'act')
                    if k % 2:
                        P.ts('dve', Wu[:, kc, hf * 2048:(hf + 1) * 2048], wst[b][:], g2[:, kc:kc + 1], None, ALU.mult)
                    else:
                        P.act(Wu[:, kc, hf * 2048:(hf + 1) * 2048], wst[b][:], AF.Copy, scale=g2[:, kc:kc + 1])
                    k += 1
            for fc2 in range(16):
                b = k % 3
                P.dma(wst[b][:].rearrange("p (a n) -> p a n", a=2), A['wdn'][layer, fc2 * 256:(fc2 + 1) * 256, :].rearrange("(a p) n -> p a n", p=128),
                      'fwst%d' % b, q='sp' if k % 2 else 'act')
                P.copy('dve' if k % 2 else 'act', Wd[:, fc2 * 2:(fc2 + 1) * 2, :], wst[b][:].rearrange("p (a n) -> p a n", a=2))
                k += 1
        h2 = [P.sb(st, "ff_h2%d" % i, [128, 8, 512], BF16) for i in range(2)]
        uT = P.sb(st, "ff_uT", [128, 32, 512], BF16)
        rl = [P.sb(st, "ff_rl%d" % i, [128, 512], F32) for i in range(2)]
        xt = [P.sb(st, "ff_xt%d" % i, [128, D], F32) for i in range(2)]
        xo = [P.sb(st, "ff_xo%d" % i, [128, D], F32) for i in range(2)]
        pu = [P.ps(st, "ff_pu%d" % i, [128, 512], F32) for i in range(3)]
        pd = [P.ps(st, "ff_pd%d" % i, [128, 512], F32) for i in range(4)]
        ku = 0
        for j in range(8):
            P.dma(h2[j % 2][:], A['h2T'][:, j * 512:(j + 1) * 512].rearrange("(a p) n -> p a n", p=128), 'ffh2%d' % (j % 2))
            for fc in range(32):
                pp = pu[ku % 3]
                r = rl[ku % 2]
                for kc in range(8):
                    P.mm(pp[:], Wu[:, kc, fc * 128:(fc + 1) * 128], h2[j % 2][:, kc, :], start=(kc == 0), stop=(kc == 7))
                P.act(r[:], pp[:], AF.Relu)
                P.tt('dve' if ku % 2 else 'pool', uT[:, fc, :], r[:], r[:], ALU.mult)
                ku += 1
            for tt in range(4):
                t = j * 4 + tt
                b = t % 2
                P.dma(xt[b][:], x_mid[t * 128:(t + 1) * 128, :], 'ffxt%d' % b, q='act')
                for half in range(2):
                    pp = pd[(t % 2) * 2 + half]
                    for fc in range(32):
                        P.mm(pp[:], uT[:, fc, tt * 128:(tt + 1) * 128], Wd[:, fc, half * 512:(half + 1) * 512], start=(fc == 0), stop=(fc == 31))
                    P.tt('dve', xo[b][:, half * 512:(half + 1) * 512], pp[:], xt[b][:, half * 512:(half + 1) * 512], ALU.add)
                P.dma(x_out[t * 128:(t + 1) * 128, :], xo[b][:], 'ffxo%d' % b, q='pool')

import ml_dtypes
from concourse.bass_utils import run_bass_kernel_spmd

T=4096; D=1024
OFF = {}
_names = ['aq','af','ai','ag','bq','bkc','bvc','bks','bvs','bkw','bvw','bg','cq','ck','cv','cf']
_sizes = [256,256,256,256,512,128,128,128,128,128,128,24,256,256,256,4]
_o = 0
for n_, s_ in zip(_names, _sizes):
    OFF[n_] = (_o, _o + s_); _o += s_
TOK_ORDER = ['bq','bks','bkw','cq','ck','ai','bvs','bvw','cv']
T_ORDER = ['aq','af','ag','bkc','bvc','bg','cf']

def win_layout(w_in):
    L = w_in.shape[0]
    out = np.zeros((L, 1024, 2048 + 1152), np.float32)
    c = 0
    for n_ in TOK_ORDER:
        a, b = OFF[n_]; out[:, :, c:c + b - a] = w_in[:, :, a:b]; c += b - a
    assert c == 2048
    for n_ in T_ORDER:
        a, b = OFF[n_]; out[:, :, c:c + b - a] = w_in[:, :, a:b]; c += b - a
    return out

def rope_tables():
    inv = np.power(np.float32(500000.0), -np.arange(0, 16, 2, dtype=np.float32) / 16).astype(np.float32)
    pos = np.arange(T, dtype=np.float32)
    ang = pos[:, None] * inv[None, :]
    cos = np.cos(ang).astype(np.float32); sin = np.sin(ang).astype(np.float32)
    return (np.ascontiguousarray(cos.reshape(32, 128, 8).transpose(1, 0, 2)),
            np.ascontiguousarray(sin.reshape(32, 128, 8).transpose(1, 0, 2)))

def _skip():
    pass

def const_inputs():
    k = np.arange(128)[:, None]; q = np.arange(128)[None, :]
    mc = np.where(k <= q, 0.0, -30000.0).astype(ml_dtypes.bfloat16)
    mu = np.where(k > q, 0.0, -30000.0).astype(ml_dtypes.bfloat16)
    selneg = np.zeros((24, 24 * 64), np.float32)
    for c in range(24):
        selneg[c, c * 64:(c + 1) * 64] = -1.0
    return dict(ident=np.eye(128, dtype=ml_dtypes.bfloat16), mc=mc, mu=mu, selneg=selneg)

def _unused_ref_proj(inp, layer, x):
    x = x.astype(np.float64)
    h = x / np.sqrt((x * x).mean(-1, keepdims=True) + 1e-6) * inp['norm1_g'][layer]
    return h @ inp['w_in'][layer].astype(np.float64)

def hgrn_consts(inp):
    s = np.arange(128)[:, None]; t = np.arange(128)[None, :]
    mh = ((s // 64 == t // 64) & (s <= t)).astype(ml_dtypes.bfloat16)
    bones = (s // 64 == t // 64).astype(ml_dtypes.bfloat16)
    lbl = np.ascontiguousarray(inp['hgrn_lb_logits'].reshape(2, 2, 128).transpose(1, 2, 0)).astype(np.float32)
    og = np.tile(inp['hgrn_onorm_g'], (1, 2)).reshape(2, 128, 1).astype(np.float32)
    return dict(mh=mh, bones=bones, lbl=lbl, og=og)

def _unused_ref_hgrn(inp, layer, proj):
    def sl(n): a, b = OFF[n]; return proj[:, a:b]
    lbp = np.exp(inp['hgrn_lb_logits'].astype(np.float64)); lbp /= lbp.sum(0, keepdims=True)
    lb_all = np.cumsum(lbp, 0) - lbp[0:1]
    lb = lb_all[layer].reshape(4, 64)
    z = sl('af').reshape(T, 4, 64)
    sig = 1 / (1 + np.exp(-z))
    f = lb + (1 - lb) * sig; logf = np.log(f); k = (1 - lb) * (1 - sig)
    q = sl('aq').reshape(T, 4, 64) * 0.125; v = sl('ai').reshape(T, 4, 64)
    o = np.zeros((T, 4, 64))
    for h in range(4):
        S = np.zeros((64, 64))
        for c in range(64):
            r = slice(c * 64, (c + 1) * 64)
            G = np.cumsum(logf[r, h], 0)
            qc, kc, vc = q[r, h], k[r, h], v[r, h]
            o_inter = (qc * np.exp(G)) @ S
            diff = G[:, None, :] - G[None, :, :]
            mask = np.tril(np.ones((64, 64), bool))
            dec = np.where(mask[:, :, None], np.exp(np.minimum(diff, 0)), 0)
            sc = np.einsum('tk,sk,tsk->ts', qc, kc, dec)
            o[r, h] = o_inter + sc @ vc
            S = S * np.exp(G[-1])[:, None] + (kc * np.exp(G[-1] - G)).T @ vc
    g = sl('ag').reshape(T, 4, 64)
    gate = g / (1 + np.exp(-g))
    on = o / np.sqrt((o * o).mean(-1, keepdims=True) + 1e-6) * inp['hgrn_onorm_g'][layer]
    return (on * gate).reshape(T, 256)

def nsa_consts(inp):
    n_cmp = 255
    ci = np.arange(n_cmp)[:, None]; sj = np.arange(64)[None, :]
    ov = ((ci * 16 <= sj * 64 + 63) & (ci * 16 + 31 >= sj * 64)).astype(np.float32)
    ovaug = np.zeros((256, 72), np.float32); ovaug[:255, :64] = ov; ovaug[:255, 64] = 1.0
    ovaug = np.ascontiguousarray(ovaug.reshape(2, 128, 72).transpose(1, 0, 2)).astype(ml_dtypes.bfloat16)
    nl = np.arange(128)[:, None]; cc = np.arange(3200)[None, :] - 511
    wc = np.where(cc >= 16 * nl, 0.0, -30000.0).astype(ml_dtypes.bfloat16)
    eall = (np.arange(T)[None, :] // 64 == np.arange(64)[:, None]).astype(ml_dtypes.bfloat16)
    q = np.arange(T)[:, None]; j = np.arange(64)[None, :]; cur = q // 64
    am = np.zeros((T, 64), np.float32)
    am[(j == 0) | (j == cur) | (j == cur - 1)] = 1e30
    am[np.broadcast_to(j > cur, am.shape)] = -1e30
    addmask = np.ascontiguousarray(am.reshape(32, 128, 64).transpose(1, 0, 2))
    inv = np.power(np.float32(500000.0), -np.arange(0, 16, 2, dtype=np.float32) / 16).astype(np.float32)
    pos = (np.arange(256, dtype=np.float32) * 16 + 31)
    ang = pos[:, None] * inv[None, :]
    cosc = np.ascontiguousarray(np.cos(ang).astype(np.float32).reshape(2, 128, 8).transpose(1, 0, 2))
    sinc = np.ascontiguousarray(np.sin(ang).astype(np.float32).reshape(2, 128, 8).transpose(1, 0, 2))
    w1r = np.ascontiguousarray(inp['nsa_cmp_w1'].reshape(2, 2, 32, 64, 128).transpose(0, 1, 3, 2, 4)).astype(np.float32)
    posT = np.ascontiguousarray(inp['nsa_cmp_pos'].transpose(0, 1, 3, 2)).astype(np.float32)
    return dict(ovaug=ovaug, wc=wc, eall=eall, addmask=addmask, cosc=cosc, sinc=sinc, w1r=w1r, posT=posT,
                w2=inp['nsa_cmp_w2'].astype(np.float32), kng=inp['nsa_kn_g'].astype(np.float32))

NSA_SHAPES = [('ovaug', [128, 2, 72], BF16), ('wc', [128, 3200], BF16), ('eall', [64, 4096], BF16), ('addmask', [128, 32, 64], F32),
              ('cosc', [128, 2, 8], F32), ('sinc', [128, 2, 8], F32), ('w1r', [2, 2, 64, 32, 128], F32), ('posT', [2, 2, 64, 32], F32),
              ('w2', [2, 2, 128, 64], F32), ('kng', [2, 64], F32)]


import ml_dtypes
from concourse.bass_utils import run_bass_kernel_spmd

_IN_SHAPES = [('x', [T, D], F32), ('win', [2, D, WCOLS], F32), ('g1', [2, 128, 8], F32), ('gq', [2, 1280], F32),
              ('cos', [128, 32, 8], F32), ('sin', [128, 32, 8], F32), ('ident', [128, 128], BF16), ('mc', [128, 128], BF16),
              ('mu', [128, 128], BF16), ('selneg', [24, 1536], F32), ('mh', [128, 128], BF16), ('bones', [128, 128], BF16),
              ('lbl', [2, 128, 2], F32), ('og', [2, 128, 1], F32), ('fb', [2, 4, 1], F32), ('wo', [2, 1024, 1024], F32),
              ('wup', [2, 1024, 4096], F32), ('wdn', [2, 4096, 1024], F32), ('g2', [2, 128, 8], F32)] + NSA_SHAPES

KDEPTH = int(os.environ.get('KDEPTH', '2'))
KPHASES = os.environ.get('KPHASES', '1hnfwf')


def _body(P):
    nc = P.nc
    A = {}
    for k_, shp, dt_ in _IN_SHAPES:
        A[k_] = nc.dram_tensor(k_, shp, dt_, kind="ExternalInput").ap()
    A['y'] = nc.dram_tensor("y", [T, D], F32, kind="ExternalOutput").ap()
    A['qkT'] = nc.dram_tensor("qkT", [1280, T], BF16).ap()
    A['vtok'] = nc.dram_tensor("vtok", [T, 768], BF16).ap()
    A['pT'] = nc.dram_tensor("pT", [TC, T], F32).ap()
    A['mixT'] = nc.dram_tensor("mixT", [1024, T], BF16).ap()
    A['h2T'] = nc.dram_tensor("h2T", [1024, T], BF16).ap()
    xm = nc.dram_tensor("xmid", [T, D], F32).ap()
    x1 = nc.dram_tensor("x1", [T, D], F32).ap()
    xin = A['x']
    for layer in range(KDEPTH):
        A['x'] = xin
        with ExitStack() as st:
            phase1(P, st, A, layer)
        with ExitStack() as st:
            consts = load_consts(P, st, A)
            if 'h' in KPHASES:
                phase_hgrn(P, A, layer, consts)
            if 'n' in KPHASES:
                phase_nsa(P, A, layer, consts)
            if 'f' in KPHASES:
                phase_fox(P, A, layer, consts)
            xout = x1 if layer < KDEPTH - 1 else A['y']
            phase_wo(P, A, layer, consts, xin, xm)
            phase_ffn(P, A, layer, consts, xm, xout)
        xin = xout


def _host_inputs(inp):
    cos, sin = rope_tables()
    gq = np.concatenate([np.tile(inp['nsa_qn_g'], (1, 8)), np.tile(inp['nsa_kn_g'], (1, 4)), np.tile(inp['fox_qn_g'], (1, 4)),
                         np.tile(inp['fox_kn_g'], (1, 4))], axis=1).astype(np.float32)
    base = {"win": win_layout(inp['w_in']), "g1": np.ascontiguousarray(inp['norm1_g'].reshape(2, 8, 128).transpose(0, 2, 1)),
            "g2": np.ascontiguousarray(inp['norm2_g'].reshape(2, 8, 128).transpose(0, 2, 1)),
            "gq": gq, "cos": cos, "sin": sin, "fb": inp['fox_fb'].reshape(2, 4, 1).astype(np.float32),
            "wo": inp['w_o'], "wup": inp['w_up'], "wdn": inp['w_down']}
    base.update(const_inputs()); base.update(hgrn_consts(inp)); base.update(nsa_consts(inp))
    return base


def kernel(**inp):
    inp = {k: np.asarray(v) for k, v in inp.items()}
    nc, plan = build_two_pass(lambda: bass.Bass("TRN2", target_bir_lowering=False), _body)
    base = _host_inputs(inp)
    in_maps = []
    for b in range(8):
        m = dict(base); m['x'] = np.ascontiguousarray(inp['x'][b]); in_maps.append(m)
    res = run_bass_kernel_spmd(nc, in_maps, core_ids=list(range(8)))
    return np.stack([r['y'] for r in res.results], axis=0).astype(np.float32)
```

```python
import numpy as np, sys, time, os, math
import numpy as np
from contextlib import ExitStack
import concourse.bass as bass
import concourse.mybir as mybir

F32 = mybir.dt.float32
BF16 = mybir.dt.bfloat16
AF = mybir.ActivationFunctionType
ALU = mybir.AluOpType
AX = mybir.AxisListType


def _box(ap):
    t = ap.tensor
    dims = ap.ap
    off = int(ap.offset)
    shp = tuple(t.shape)
    rowsize = 1
    for s in shp[1:]:
        rowsize *= int(s)
    r0 = off // rowsize
    f0 = off % rowsize
    rows = 0
    free = 0
    for (st, cnt) in dims:
        st = int(st); cnt = int(cnt)
        if cnt <= 1 or st == 0:
            continue
        if st % rowsize == 0:
            rows += (st // rowsize) * (cnt - 1)
        else:
            free += st * (cnt - 1)
    return t.name, (r0, r0 + rows, f0, f0 + free)


def _ov(a, b):
    return a[0] <= b[1] and b[0] <= a[1] and a[2] <= b[3] and b[2] <= a[3]


def _cont(a, b):
    return a[0] <= b[0] and b[1] <= a[1] and a[2] <= b[2] and b[3] <= a[3]


class Prog:
    def __init__(self, nc, plan=None):
        self.nc = nc
        self.plan = plan
        self.rec = plan is None
        self.eng = dict(pe=nc.tensor, dve=nc.vector, act=nc.scalar, pool=nc.gpsimd, sp=nc.sync)
        self.n = 0
        self.ins = []
        self.track = {}
        self.lane_cnt = {}
        self.freed = {}
        self.uid = 0
        self.stack = ExitStack()
        self.psum_rr = 0
        self.psum_banks = []
        if not self.rec:
            self.sem = {}
            for e in ['pe', 'dve', 'act', 'pool']:
                self.sem[e] = self.stack.enter_context(nc.semaphore("sem_" + e))
            self.lane_sem = {}
            for ln in plan['lanes']:
                self.lane_sem[ln] = self.stack.enter_context(nc.semaphore("ln_" + ln))

    def sb(self, st, name, shape, dtype):
        self.uid += 1
        name = "%s_%d" % (name, self.uid)
        t = st.enter_context(self.nc.sbuf_tensor("s_" + name, list(shape), dtype))
        st.callback(self._free, "s_" + name)
        return t

    def ps(self, st, name, shape, dtype=F32):
        self.uid += 1
        name = "%s_%d" % (name, self.uid)
        t = st.enter_context(self.nc.psum_tensor("p_" + name, list(shape), dtype))
        st.callback(self._free, "p_" + name)
        return t

    def _free(self, name):
        if not self.rec:
            return
        recs = self.track.pop(name, [])
        for (b, i, w) in recs:
            r = self.ins[i]
            key = ('l', r['lane'], i) if r['dma'] else ('e', r['eng'])
            if r['dma']:
                self.freed[key] = i
            else:
                self.freed[key] = max(self.freed.get(key, -1), i)

    def _access(self, idx, eng, dma, ap, write, deps):
        name, box = _box(ap)
        if name not in self.track:
            big = (0, 10 ** 9, 0, 10 ** 9)
            kind = ap.space
            self.track[name] = [] if str(kind) == 'DRAM' else [(big, i, True) for i in sorted(set(self.freed.values()))]
        recs = self.track[name]
        for (b, i, w) in recs:
            if (write or w) and _ov(b, box):
                deps.append((i, (w and not write)))
        if write:
            recs[:] = [r for r in recs if not _cont(box, r[0])]
        elif not dma:
            recs[:] = [r for r in recs if not ((not r[2]) and r[1] < len(self.ins) and self.ins[r[1]]['eng'] == eng
                                               and not self.ins[r[1]]['dma'] and _cont(box, r[0]))]
        recs.append((box, idx, write))

    def op(self, eng, fn, reads=(), writes=(), dma=False, lane=None):
        idx = self.n
        self.n += 1
        if self.rec:
            deps = []
            for ap in reads:
                self._access(idx, eng, dma, ap, False, deps)
            for ap in writes:
                self._access(idx, eng, dma, ap, True, deps)
            lanewaits = {}
            d2 = {}
            for (j, raw) in deps:
                if j == idx:
                    continue
                pj = self.ins[j]
                if pj['dma']:
                    ln = pj['lane']
                    lanewaits[ln] = max(lanewaits.get(ln, 0), pj['lane_val_at'])
                    lanewaits[ln] = max(lanewaits[ln], self.lane_cnt[ln])
                    continue
                if pj['eng'] == eng and not dma:
                    if eng == 'pe':
                        continue
                    if not raw:
                        continue
                d2[j] = True
            rec = dict(eng=eng, deps=list(d2.keys()), lanewaits=lanewaits, dma=dma, lane=lane)
            if dma:
                self.lane_cnt[lane] = self.lane_cnt.get(lane, 0) + 16
                rec['lane_val_at'] = self.lane_cnt[lane]
            self.ins.append(rec)
            return None
        else:
            info = self.plan['ins'][idx]
            e = self.eng[eng]
            for (sname, val) in info['waits']:
                s = self.sem[sname[1]] if sname[0] == 'e' else self.lane_sem[sname[1]]
                e.wait_ge(s, val)
            inst = fn(e)
            if dma:
                inst.then_inc(self.lane_sem[lane], 16)
            elif info['signal']:
                inst.then_inc(self.sem[eng], 1)
            return inst

    def make_plan(self):
        ins = self.ins
        signal = [False] * len(ins)
        for r in ins:
            for j in r['deps']:
                signal[j] = True
        cnt = dict(pe=0, dve=0, act=0, pool=0, sp=0)
        sigval = [0] * len(ins)
        for i, r in enumerate(ins):
            if signal[i] and not r['dma']:
                cnt[r['eng']] += 1
                sigval[i] = cnt[r['eng']]
        seen = {e: {} for e in cnt}
        out = []
        for i, r in enumerate(ins):
            need = {}
            for j in r['deps']:
                k = ('e', ins[j]['eng'])
                need[k] = max(need.get(k, 0), sigval[j])
            for ln, v in r['lanewaits'].items():
                k = ('l', ln)
                need[k] = max(need.get(k, 0), v)
            waits = []
            sd = seen[r['eng']]
            for k, v in need.items():
                if sd.get(k, 0) >= v:
                    continue
                sd[k] = v
                waits.append((k, v))
            out.append(dict(waits=waits, signal=signal[i]))
        return dict(ins=out, lanes=sorted(self.lane_cnt.keys()), lane_final=dict(self.lane_cnt))

    def finish(self):
        if self.rec:
            return
        for ln, v in self.plan['lane_final'].items():
            self.nc.sync.wait_ge(self.lane_sem[ln], v)

    def dma(self, out, in_, lane, q='sp', **kw):
        return self.op(q, lambda e: e.dma_start(out=out, in_=in_, **kw), reads=[in_], writes=[out],
                       dma=True, lane=lane)

    def mm(self, out, lhsT, rhs, start=True, stop=True, **kw):
        return self.op('pe', lambda e: e.matmul(out, lhsT, rhs, start=start, stop=stop, **kw),
                       reads=[lhsT, rhs], writes=[out])

    def transpose(self, out, in_, ident):
        return self.op('pe', lambda e: e.transpose(out, in_, ident), reads=[in_, ident], writes=[out])

    def act(self, out, in_, func, bias=None, scale=None, accum_out=None, eng='act'):
        reads = [in_]
        kw = {}
        if bias is not None:
            kw['bias'] = bias
            if not isinstance(bias, (int, float)):
                reads.append(bias)
        if scale is not None:
            kw['scale'] = scale
            if not isinstance(scale, (int, float)):
                reads.append(scale)
        writes = [out]
        if accum_out is not None:
            kw['accum_out'] = accum_out
            writes.append(accum_out)
        return self.op(eng, lambda e: e.activation(out=out, in_=in_, func=func, **kw), reads=reads, writes=writes)

    def tt(self, eng, out, in0, in1, op):
        return self.op(eng, lambda e: e.tensor_tensor(out=out, in0=in0, in1=in1, op=op), reads=[in0, in1], writes=[out])

    def ts(self, eng, out, in0, s1, s2, op0, op1=None, accum_out=None):
        reads = [in0]
        if not isinstance(s1, (int, float)):
            reads.append(s1)
        if s2 is not None and not isinstance(s2, (int, float)):
            reads.append(s2)
        kw = {}
        writes = [out]
        if op1 is not None:
            kw['op1'] = op1
        if accum_out is not None:
            kw['accum_out'] = accum_out
            writes.append(accum_out)
        return self.op(eng, lambda e: e.tensor_scalar(out=out, in0=in0, scalar1=s1, scalar2=s2, op0=op0, **kw),
                       reads=reads, writes=writes)

    def stt(self, eng, out, in0, scalar, in1, op0, op1):
        reads = [in0, in1]
        if not isinstance(scalar, (int, float)):
            reads.append(scalar)
        return self.op(eng, lambda e: e.scalar_tensor_tensor(out=out, in0=in0, scalar=scalar, in1=in1, op0=op0, op1=op1),
                       reads=reads, writes=[out])

    def copy(self, eng, out, in_):
        if eng == 'act':
            return self.op(eng, lambda e: e.copy(out=out, in_=in_), reads=[in_], writes=[out])
        return self.op(eng, lambda e: e.tensor_copy(out=out, in_=in_), reads=[in_], writes=[out])

    def memset(self, eng, ap, val):
        return self.op(eng, lambda e: e.memset(ap, val), reads=[], writes=[ap])

    def scan(self, out, d0, d1, initial, op0, op1):
        reads = [d0, d1]
        if not isinstance(initial, (int, float)):
            reads.append(initial)
        return self.op('dve', lambda e: e.tensor_tensor_scan(out=out, data0=d0, data1=d1, initial=initial, op0=op0, op1=op1),
                       reads=reads, writes=[out])

    def generic(self, eng, fn, reads, writes):
        return self.op(eng, fn, reads=reads, writes=writes)


def build_two_pass(make_nc, body):
    nc1 = make_nc()
    p1 = Prog(nc1, None)
    body(p1)
    p1.stack.close()
    plan = p1.make_plan()
    nc2 = make_nc()
    p2 = Prog(nc2, plan)
    body(p2)
    p2.finish()
    p2.stack.close()
    return nc2, plan


T = 4096
NT = 32
D = 1024
KC = 8
TOKC = 2048
TC = 1152
WCOLS = TOKC + TC
EPS = 1e-6


def phase1(P, st, A, layer):
    nc = P.nc
    s = ExitStack()
    W = P.sb(s, "w_in", [128, KC, WCOLS], BF16)
    hT = P.sb(s, "hT", [128, KC, T], BF16)
    ident = P.sb(s, "ident", [128, 128], BF16)
    g1 = P.sb(s, "g1", [128, KC], F32)
    G = P.sb(s, "Gq", [128, 1280], F32)
    cos = P.sb(s, "cos", [128, NT, 8], F32)
    sin = P.sb(s, "sin", [128, NT, 8], F32)
    P.dma(ident[:], A['ident'], 'c0')
    P.dma(g1[:], A['g1'][layer], 'c0')
    P.dma(G[:], A['gq'][layer].partition_broadcast(128), 'c0')
    P.dma(cos[:], A['cos'], 'c0')
    P.dma(sin[:], A['sin'], 'c0')
    P.ts('dve', G[:, 0:512], G[:, 0:512], 0.125, None, ALU.mult)
    P.ts('dve', G[:, 768:1024], G[:, 768:1024], 0.125, None, ALU.mult)

    with ExitStack() as s2:
        wst = [P.sb(s2, "wst%d" % i, [128, WCOLS], F32) for i in range(2)]
        for kc in range(KC):
            b = wst[kc % 2]
            P.dma(b[:], A['win'][layer, kc * 128:(kc + 1) * 128, :], 'wst%d' % (kc % 2), q='sp' if kc % 2 == 0 else 'act')
            half = WCOLS // 2
            P.ts('dve', W[:, kc, 0:half], b[:, 0:half], g1[:, kc:kc + 1], None, ALU.mult)
            P.act(W[:, kc, half:WCOLS], b[:, half:WCOLS], AF.Copy, scale=g1[:, kc:kc + 1])

    with ExitStack() as s2:
        xt = [P.sb(s2, "xt%d" % i, [128, D], F32) for i in range(2)]
        sq = P.sb(s2, "sqj", [128, D], F32)
        hb = [P.sb(s2, "hb%d" % i, [128, D], BF16) for i in range(2)]
        ss = [P.sb(s2, "ss%d" % i, [128, 2], F32) for i in range(2)]
        ptr = [P.ps(s2, "ptr%d" % i, [128, KC, 128], BF16) for i in range(2)]
        for t in range(NT):
            b = t % 2
            P.dma(xt[b][:], A['x'][t * 128:(t + 1) * 128, :], 'xt%d' % b)
            P.act(sq[:], xt[b][:], AF.Square, accum_out=ss[b][:, 0:1])
            P.act(ss[b][:, 1:2], ss[b][:, 0:1], AF.Sqrt, bias=EPS_AP(P), scale=1.0 / D)
            P.op('dve', lambda e, o=ss[b][:, 1:2]: e.reciprocal(out=o, in_=o), reads=[ss[b][:, 1:2]], writes=[ss[b][:, 1:2]])
            P.ts('dve', hb[b][:], xt[b][:], ss[b][:, 1:2], None, ALU.mult)
            for kc in range(KC):
                P.transpose(ptr[b][:, kc, :], hb[b][:, kc * 128:(kc + 1) * 128], ident[:])
            P.copy('act' if t % 2 else 'dve', hT[:, :, t * 128:(t + 1) * 128], ptr[b][:])

    with ExitStack() as s2:
        pp = [P.ps(s2, "ppT%d" % i, [128, 512], F32) for i in range(3)]
        so = [P.sb(s2, "soT%d" % i, [128, 512], F32) for i in range(3)]
        k = 0
        for c in range(TC // 128):
            for j in range(T // 512):
                b = k % 3
                for kc in range(KC):
                    P.mm(pp[b][:], W[:, kc, TOKC + c * 128:TOKC + (c + 1) * 128], hT[:, kc, j * 512:(j + 1) * 512],
                         start=(kc == 0), stop=(kc == KC - 1))
                P.copy('act' if k % 2 else 'dve', so[b][:], pp[b][:])
                P.dma(A['pT'][c * 128:(c + 1) * 128, j * 512:(j + 1) * 512], so[b][:], 'soT%d' % b, q='pool')
                k += 1

    with ExitStack() as s2:
        pg = [P.ps(s2, "pg%d" % i, [128, 512], F32) for i in range(4)]
        ptq = [P.ps(s2, "ptq%d" % i, [128, 4, 128], BF16) for i in range(3)]
        sqhs = [P.sb(s2, "sqh%d" % i, [128, 512], F32) for i in range(3)]
        ssh = [P.sb(s2, "ssh%d" % i, [128, 8], F32) for i in range(4)]
        xn = [P.sb(s2, "xn%d" % i, [128, 512], F32) for i in range(3)]
        qb = [P.sb(s2, "qb%d" % i, [128, 512], BF16) for i in range(3)]
        rts = [P.sb(s2, "rt%d" % i, [128, 4, 8, 8], F32) for i in range(3)]
        qst = [P.sb(s2, "qst%d" % i, [128, 10, 512], BF16) for i in range(2)]
        vst = [P.sb(s2, "vst%d" % i, [128, 768], BF16) for i in range(2)]
        groups = []
        kq = 0
        for t in range(NT):
            for gi in range(4):
                k = t * 4 + gi
                nh = [8, 8, 4, 0][gi]
                qi = None
                if nh:
                    qi = kq % 3
                    kq += 1
                groups.append((t, gi, k, qi))

        def stage(sidx, t, gi, k, q):
            sb_ = (t // 4) % 2
            b = k % 4
            nh = [8, 8, 4, 0][gi]
            nr = [8, 4, 0, 0][gi]
            w = nh * 64
            goff = [0, 512, 1024, 0][gi]
            vb = t % 2
            if sidx == 0:
                for kc in range(KC):
                    P.mm(pg[b][:], hT[:, kc, t * 128:(t + 1) * 128], W[:, kc, gi * 512:(gi + 1) * 512],
                         start=(kc == 0), stop=(kc == KC - 1))
                return
            if sidx == 1:
                if nh:
                    sqh = sqhs[q]
                    P.act(sqh[:, 0:w], pg[b][:, 0:w], AF.Square)
                    P.op('dve', lambda e, o=ssh[b][:, 0:nh], i=sqh[:, 0:w].rearrange("p (h d) -> p h d", d=64):
                         e.tensor_reduce(out=o, in_=i, axis=AX.X, op=ALU.add),
                         reads=[sqh[:, 0:w]], writes=[ssh[b][:, 0:nh]])
                if gi == 2:
                    P.copy('act', vst[vb][:, 0:256], pg[b][:, 256:512])
                if gi == 3:
                    P.copy('act', vst[vb][:, 256:768], pg[b][:, 0:512])
                    P.dma(A['vtok'][t * 128:(t + 1) * 128, :], vst[vb][:], 'vst%d' % vb, q='sp')
                return
            if not nh:
                return
            rt = rts[q]
            xv = xn[q][:, 0:max(nr, 1) * 64].rearrange("p (h d) -> p h d", d=64)
            qv = qb[q][:, 0:max(nr, 1) * 64].rearrange("p (h d) -> p h d", d=64)
            if sidx == 2:
                P.act(ssh[b][:, 0:nh], ssh[b][:, 0:nh], AF.Sqrt, bias=EPS_AP(P), scale=1.0 / 64)
                P.op('dve', lambda e, o=ssh[b][:, 0:nh]: e.reciprocal(out=o, in_=o), reads=[ssh[b][:, 0:nh]], writes=[ssh[b][:, 0:nh]])
                P.tt('dve', xn[q][:, 0:w].rearrange("p (h d) -> p h d", d=64),
                     pg[b][:, 0:w].rearrange("p (h d) -> p h d", d=64),
                     ssh[b][:, 0:nh].unsqueeze(2).broadcast_to([128, nh, 64]), ALU.mult)
            elif sidx == 3:
                if nr:
                    P.tt('pool', xn[q][:, 0:w], xn[q][:, 0:w], G[:, goff:goff + w], ALU.mult)
                    P.copy('act', qb[q][:, 0:w], xn[q][:, 0:w])
                    cb = cos[:, t, :].unsqueeze(1).broadcast_to([128, nr, 8])
                    sb2 = sin[:, t, :].unsqueeze(1).broadcast_to([128, nr, 8])
                    P.tt('dve', rt[:, 0, 0:nr, :], xv[:, :, 0:8], cb, ALU.mult)
                    P.tt('dve', rt[:, 1, 0:nr, :], xv[:, :, 8:16], sb2, ALU.mult)
                    P.tt('pool', rt[:, 2, 0:nr, :], xv[:, :, 8:16], cb, ALU.mult)
                    P.tt('pool', rt[:, 3, 0:nr, :], xv[:, :, 0:8], sb2, ALU.mult)
                else:
                    P.tt('pool', qb[q][:, 0:w], xn[q][:, 0:w], G[:, goff:goff + w], ALU.mult)
            elif sidx == 4:
                if nr:
                    P.tt('dve', qv[:, :, 0:8], rt[:, 0, 0:nr, :], rt[:, 1, 0:nr, :], ALU.subtract)
                    P.tt('pool', qv[:, :, 8:16], rt[:, 2, 0:nr, :], rt[:, 3, 0:nr, :], ALU.add)
                npair = nh // 2
                for pr in range(npair):
                    P.transpose(ptq[q][:, pr, :], qb[q][:, pr * 128:(pr + 1) * 128], ident[:])
            elif sidx == 5:
                npair = nh // 2
                pbase = [0, 4, 8][gi]
                P.copy('act' if gi % 2 else 'dve', qst[sb_][:, pbase:pbase + npair, (t % 4) * 128:(t % 4 + 1) * 128], ptq[q][:, 0:npair, :])
                if t % 4 == 3 and gi == 2:
                    j = t // 4
                    P.dma(A['qkT'][:, j * 512:(j + 1) * 512].rearrange("(a p) n -> p a n", p=128), qst[sb_][:], 'qst%d' % sb_, q='sp')

        NS = 6
        for step in range(len(groups) + NS - 1):
            for sidx in range(NS - 1, -1, -1):
                gidx = step - sidx
                if 0 <= gidx < len(groups):
                    stage(sidx, *groups[gidx])
    s.close()


_eps_cache = {}


def EPS_AP(P):
    return EPS


T = 4096
NEG = -30000.0


class AttnCtx:
    def __init__(self, P, st, consts):
        self.P = P
        self.psS = [P.ps(st, "aS%d" % i, [128, 512], F32) for i in range(2)]
        self.psO = [P.ps(st, "aO%d" % i, [128, 512], F32) for i in range(2)]
        self.psB = [P.ps(st, "aB%d" % i, [128, 512], F32) for i in range(2)]
        self.pT = [P.sb(st, "apT%d" % i, [128, 512], BF16) for i in range(3)]
        self.lr = [P.sb(st, "alr%d" % i, [65, 512], F32) for i in range(2)]
        self.F = [P.sb(st, "aF%d" % i, [64, 512], F32) for i in range(2)]
        self.lrh = [P.sb(st, "alrh%d" % i, [128, 512], BF16) for i in range(2)]
        self.lrl = [P.sb(st, "alrl%d" % i, [128, 512], BF16) for i in range(2)]
        self.G2 = [P.sb(st, "aG%d" % i, [64, 512], F32) for i in range(2)]
        for t_ in self.lrh + self.lrl:
            P.memset('pool', t_[:], 0.0)
        self.kS = 0
        self.kO = 0
        self.kF = 0
        self.vm = 65
        self.prev = None
        self.deferred = []
        self.c = consts


def _push_block(cx, s_fn, exp_fn, pv_fn, first=False):
    if first:
        for f in cx.deferred:
            f()
        cx.deferred = []
    s_fn()
    d = cx.deferred
    cx.deferred = []
    if cx.prev is not None:
        e, p, epi = cx.prev
        e()
        p()
        if epi is not None:
            epi[0]()
            cx.deferred.append(epi[1])
    for f in d:
        f()
    cx.prev = (exp_fn, pv_fn, None)


def _end_chunk(cx, epi_a, epi_b):
    cx.prev = (cx.prev[0], cx.prev[1], (epi_a, epi_b))


def attn_flush(cx):
    d = cx.deferred
    cx.deferred = []
    if cx.prev is not None:
        e, p, epi = cx.prev
        e()
        p()
        if epi is not None:
            epi[0]()
            d.append(epi[1])
        cx.prev = None
    for f in d:
        f()


def _mk_block(cx, po, Kaug, kr, Qaug, q0, Vaug, kt, lo, hi, masks, extra, first, last):
    P = cx.P
    c = cx.c
    ps = cx.psS[cx.kS % 2]
    pt = cx.pT[cx.kS % 3]
    cx.kS += 1

    def s_fn():
        if first:
            P.mm(po[0:cx.vm, :], c['zeros'][:, 0:cx.vm], c['ident_w'][:, 0:512], start=True, stop=False)
        P.mm(ps[:, lo:hi], Kaug[0:kr, kt * 128:(kt + 1) * 128], Qaug[0:kr, q0 + lo:q0 + hi], start=True, stop=(len(masks) == 0 and extra is None))
        if extra is not None:
            P.mm(ps[:, lo:hi], extra[0][0:64, kt * 128:(kt + 1) * 128], extra[1][0:64, q0 + lo:q0 + hi], start=False, stop=(len(masks) == 0))
        for mi, (mk, m) in enumerate(masks):
            P.mm(ps[:, m * 128:(m + 1) * 128], c['ident'][:], mk[:], start=False, stop=(mi == len(masks) - 1))

    def exp_fn():
        P.act(pt[:, lo:hi], ps[:, lo:hi], AF.Exp)

    def pv_fn():
        P.mm(po[0:cx.vm, lo:hi], Vaug[:, kt, 0:cx.vm], pt[:, lo:hi], start=False, stop=last)

    return s_fn, exp_fn, pv_fn


def _mk_factor(cx, po, lng2, gate_c, q0, finish):
    P = cx.P
    c = cx.c
    lr = cx.lr[cx.kF % 2]
    F = cx.F[cx.kF % 2]
    G2 = cx.G2[cx.kF % 2]
    cx.kF += 1
    pb = cx.psB[0]
    pb2 = cx.psB[1]

    lrh = cx.lrh[(cx.kF - 1) % 2]
    lrl = cx.lrl[(cx.kF - 1) % 2]

    def epi_a():
        P.ts('dve', lr[64:65, :], po[64:65, :], 1e-18, None, ALU.max)
        P.act(lr[64:65, :], lr[64:65, :], AF.Ln)
        P.copy('dve', lrh[64:65, :], lr[64:65, :])
        P.tt('dve', lrl[64:65, :], lr[64:65, :], lrh[64:65, :], ALU.subtract)

    def epi_b():
        P.mm(pb[:, :], c['negonesb'][:, :], lrh[:, :], start=True, stop=False)
        P.mm(pb[:, :], c['negonesb'][:, :], lrl[:, :], start=False, stop=True)
        if lng2 is not None:
            P.mm(pb2[:, :], c['selnegb'][:, gate_c * 64:gate_c * 64 + 128], lng2[0][:, q0:q0 + 512], start=True, stop=False)
            P.mm(pb2[:, :], c['selnegb'][:, gate_c * 64:gate_c * 64 + 128], lng2[1][:, q0:q0 + 512], start=False, stop=True)
        P.act(F[:], pb[0:64, :], AF.Exp)
        if lng2 is not None:
            P.act(G2[:], pb2[0:64, :], AF.Exp)
            P.tt('pool', F[:], F[:], G2[:], ALU.mult)
        finish(po, F)

    return epi_a, epi_b


def attn_chunk(cx, Kaug, kr, Qaug, j, Vaug, entries, finish, lng2=None, gate_c=None, extra=None):
    po = cx.psO[cx.kO % 2]
    cx.kO += 1
    q0 = j * 512
    n = len(entries)
    for ei, (kt, lo, hi, masks) in enumerate(entries):
        fns = _mk_block(cx, po, Kaug, kr, Qaug, q0, Vaug, kt, lo, hi, masks, extra, ei == 0, ei == n - 1)
        _push_block(cx, *fns, first=(ei == 0))
    ea, eb = _mk_factor(cx, po, lng2, gate_c, q0, finish)
    _end_chunk(cx, ea, eb)


def causal_entries(j, mc):
    ent = []
    for kt in range(4 * j + 4):
        if kt < 4 * j:
            ent.append((kt, 0, 512, []))
        else:
            m = kt - 4 * j
            ent.append((kt, 128 * m, 512, [(mc, m)]))
    return ent


def window_entries(j, mc, mu):
    ent = []
    for cc in range(-4, 4):
        kt = 4 * j + cc
        if kt < 0:
            continue
        lo = 128 * max(cc, 0)
        hi = 128 * (min(cc + 4, 3) + 1)
        masks = []
        if 0 <= cc <= 3:
            masks.append((mc, cc))
        if 0 <= cc + 4 <= 3:
            masks.append((mu, cc + 4))
        ent.append((kt, lo, hi, masks))
    return ent


def load_consts(P, st, A):
    c = {}
    c['ident'] = P.sb(st, "c_ident", [128, 128], BF16)
    c['mc'] = P.sb(st, "c_mc", [128, 128], BF16)
    c['mu'] = P.sb(st, "c_mu", [128, 128], BF16)
    c['zeros'] = P.sb(st, "c_zeros", [128, 128], BF16)
    c['ident_w'] = P.sb(st, "c_identw", [128, 512], BF16)
    c['negones'] = P.sb(st, "c_negones", [65, 64], F32)
    c['negonesb'] = P.sb(st, "c_negonesb", [128, 128], BF16)
    P.dma(c['ident'][:], A['ident'], 'c0')
    P.dma(c['mc'][:], A['mc'], 'c0')
    P.dma(c['mu'][:], A['mu'], 'c0')
    P.memset('dve', c['zeros'][:], 0.0)
    P.memset('dve', c['ident_w'][:], 0.0)
    P.memset('dve', c['negones'][:], -1.0)
    P.memset('dve', c['negonesb'][:], -1.0)
    return c


def phase_fox(P, A, layer, consts):
    with ExitStack() as st:
        cx = AttnCtx(P, st, consts)
        cf = P.sb(st, "f_cf", [4, T], F32)
        tmp = P.sb(st, "f_tmp", [4, T], F32)
        ones = P.sb(st, "f_ones", [4, T], F32)
        fb = P.sb(st, "f_fb", [4, 2], F32)
        cs = P.sb(st, "f_cs", [4, 3, T], BF16)
        ncs = P.sb(st, "f_ncs", [4, 3, T], BF16)
        P.dma(cf[:], A['pT'][1048:1052, :], 'fx0')
        P.dma(fb[:, 0:1], A['fb'][layer], 'fx0')
        P.ts('dve', fb[:, 1:2], fb[:, 0:1], -1.0, None, ALU.mult)
        P.memset('pool', ones[:], 1.0)
        P.act(tmp[:], cf[:], AF.Exp, bias=fb[:, 1:2], scale=-1.0)
        P.act(tmp[:], tmp[:], AF.Ln, bias=1.0)
        P.scan(cf[:], ones[:], tmp[:], 0.0, ALU.mult, ALU.subtract)
        P.copy('dve', cs[:, 0, :], cf[:])
        P.tt('dve', tmp[:], cf[:], cs[:, 0, :], ALU.subtract)
        P.copy('dve', cs[:, 1, :], tmp[:])
        P.tt('dve', tmp[:], tmp[:], cs[:, 1, :], ALU.subtract)
        P.copy('dve', cs[:, 2, :], tmp[:])
        P.ts('dve', ncs[:].rearrange("p a t -> p (a t)"), cs[:].rearrange("p a t -> p (a t)"), -1.0, None, ALU.mult)
        Qs = [P.sb(st, "f_Q%d" % i, [128, T], BF16) for i in range(2)]
        Ks = [P.sb(st, "f_K%d" % i, [128, T], BF16) for i in range(2)]
        Vs = [P.sb(st, "f_V%d" % i, [128, 32, 65], BF16) for i in range(2)]
        ob = [P.sb(st, "f_ob%d" % i, [64, 512], BF16) for i in range(2)]
        for i in range(2):
            P.memset('pool', Qs[i][64:128, :], 0.0)
            P.memset('pool', Ks[i][64:128, :], 0.0)
            P.memset('pool', Qs[i][64:70, :], 1.0)
            P.memset('pool', Ks[i][64:70, :], 1.0)
            P.memset('pool', Vs[i][:, :, 64:65], 1.0)

        def load(h):
            Q = Qs[h % 2]; K = Ks[h % 2]; V = Vs[h % 2]
            P.dma(Q[0:64, :], A['qkT'][768 + 64 * h:768 + 64 * (h + 1), :], 'fxq%d' % (h % 2))
            P.dma(K[0:64, :], A['qkT'][1024 + 64 * h:1024 + 64 * (h + 1), :], 'fxk%d' % (h % 2), q='act')
            for i in range(3):
                P.dma(Q[64 + i:65 + i, :], cs[h:h + 1, i, :], 'fxq%d' % (h % 2))
                P.dma(K[67 + i:68 + i, :], ncs[h:h + 1, i, :], 'fxk%d' % (h % 2), q='act')
            P.dma(V[:, :, 0:64], A['vtok'][:, 512 + 64 * h:512 + 64 * (h + 1)].rearrange("(n p) d -> p n d", p=128), 'fxv%d' % (h % 2))

        load(0)
        for h in range(4):
            if h + 1 < 4:
                load(h + 1)
            Q = Qs[h % 2]; K = Ks[h % 2]; V = Vs[h % 2]
            for j in range(8):
                def fin(po, F, o=ob[j % 2], j=j, h=h):
                    P.tt('dve', o[:], po[0:64, :], F[:], ALU.mult)
                    P.dma(A['mixT'][768 + 64 * h:768 + 64 * (h + 1), j * 512:(j + 1) * 512], o[:], 'fxo%d' % (j % 2), q='sp')
                attn_chunk(cx, K, 128, Q, j, V, causal_entries(j, consts['mc']), fin)
            attn_flush(cx)

import math, os
STAGE = int(os.environ.get('STAGE', '99'))

T = 4096
LN8 = math.log(0.125)


def phase_hgrn(P, A, layer, consts):
    for ct in range(2):
        with ExitStack() as st:
            B = [P.sb(st, "hB%d" % i, [128, T], F32) for i in range(5)]
            qt = P.sb(st, "h_qt", [128, T], BF16)
            kt = P.sb(st, "h_kt", [128, 2, T], BF16)
            qg = P.sb(st, "h_qg", [128, T], BF16)
            kd = P.sb(st, "h_kd", [128, T], BF16)
            kdt = P.sb(st, "h_kdt", [128, 32, 2, 128], BF16)
            Vt = P.sb(st, "h_Vt", [128, 32, 128], BF16)
            Vz = P.sb(st, "h_Vz", [128, 32, 2, 128], BF16)
            Sbd = P.sb(st, "h_Sbd", [128, 64, 128], BF16)
            rst = P.sb(st, "h_rst", [128, T], BF16)
            sm = P.sb(st, "h_sm", [128, 8], F32)
            dl = P.sb(st, "h_dl", [128, 64], F32)
            mh = P.sb(st, "h_mh", [128, 128], BF16)
            bones = P.sb(st, "h_bones", [128, 128], BF16)
            ident = consts['ident']
            P.dma(mh[:], A['mh'], 'hg0')
            P.dma(bones[:], A['bones'], 'hg0')
            P.dma(sm[:, 0:2], A['lbl'][ct], 'hg0')
            P.dma(sm[:, 4:5], A['og'][layer], 'hg0')
            P.dma(B[0][:], A['pT'][256 + ct * 128:256 + (ct + 1) * 128, :], 'hgz')
            P.dma(B[3][:], A['pT'][ct * 128:(ct + 1) * 128, :], 'hgq', q='act')
            P.dma(Vt[:], A['vtok'][:, ct * 128:(ct + 1) * 128].rearrange("(n p) d -> p n d", p=128), 'hgv', q='pool')
            P.memset('pool', Vz[:], 0.0)
            for hh in range(2):
                P.dma(Vz[:, :, hh, hh * 64:(hh + 1) * 64],
                      A['vtok'][:, ct * 128 + hh * 64:ct * 128 + (hh + 1) * 64].rearrange("(n p) d -> p n d", p=128), 'hgv', q='pool')
            P.memset('pool', Sbd[:], 0.0)
            P.memset('pool', kt[:], 0.0)
            P.memset('pool', kdt[:], 0.0)
            P.memset('pool', rst[:], 1.0)
            P.memset('pool', rst[:].rearrange("p (c s) -> p c s", s=64)[:, :, 0:1], 0.0)
            lb = sm[:, 2:3]; oml = sm[:, 3:4]; noml = sm[:, 5:6]
            if layer == 0:
                P.memset('dve', lb, 0.0)
            else:
                P.act(sm[:, 0:2], sm[:, 0:2], AF.Exp)
                P.tt('dve', sm[:, 6:7], sm[:, 0:1], sm[:, 1:2], ALU.add)
                P.op('dve', lambda e, o=sm[:, 6:7]: e.reciprocal(out=o, in_=o), reads=[sm[:, 6:7]], writes=[sm[:, 6:7]])
                P.tt('dve', lb, sm[:, 1:2], sm[:, 6:7], ALU.mult)
            P.ts('dve', oml, lb, -1.0, 1.0, ALU.mult, ALU.add)
            P.ts('dve', noml, oml, -1.0, None, ALU.mult)
            P.act(B[0][:], B[0][:], AF.Sigmoid)
            P.ts('dve', B[1][:], B[0][:], oml, lb, ALU.mult, ALU.add)
            P.act(B[1][:], B[1][:], AF.Ln)
            P.scan(B[2][:], rst[:], B[1][:], 0.0, ALU.mult, ALU.add)
            P.ts('dve', B[1][:], B[0][:], noml, oml, ALU.mult, ALU.add)
            G3 = B[2][:].rearrange("p (c s) -> p c s", s=64)
            D3 = B[0][:].rearrange("p (c s) -> p c s", s=64)
            P.tt('dve', D3, G3, G3[:, :, 31:32].broadcast_to([128, 64, 64]), ALU.subtract)
            P.act(B[4][:], B[0][:], AF.Exp, bias=LN8)
            P.tt('dve', qt[:], B[3][:], B[4][:], ALU.mult)
            P.act(B[4][:], B[0][:], AF.Exp, scale=-1.0)
            P.tt('dve', kt[0:64, 0, :], B[1][0:64, :], B[4][0:64, :], ALU.mult)
            P.tt('dve', kt[64:128, 1, :], B[1][64:128, :], B[4][64:128, :], ALU.mult)
            P.act(B[4][:], B[2][:], AF.Exp, bias=LN8)
            P.tt('dve', qg[:], B[3][:], B[4][:], ALU.mult)
            P.tt('dve', D3, G3[:, :, 63:64].broadcast_to([128, 64, 64]), G3, ALU.subtract)
            P.act(B[4][:], B[0][:], AF.Exp)
            P.tt('dve', kd[:], B[1][:], B[4][:], ALU.mult)
            P.act(dl[:].unsqueeze(2), G3[:, :, 63:64], AF.Exp)
            P.memset('dve', dl[:, 0:1], 0.0)
            KV = B[0]; dfull = B[1]; Sall = B[3]; oT = B[4]
            if STAGE < 1:
                P.dma(A['mixT'][0:128, 0:T], kd[:], 'dbg'); continue
            with ExitStack() as s2:
                ptr = [P.ps(s2, "h_ptr%d" % i, [128, 8, 128], BF16) for i in range(2)]
                for g in range(4):
                    for i in range(8):
                        tl = g * 8 + i
                        P.transpose(ptr[g % 2][:, i, :], kd[:, tl * 128:(tl + 1) * 128], ident[:])
                    P.copy('act', kdt[0:64, g * 8:(g + 1) * 8, 0, :], ptr[g % 2][0:64, :, :])
                    P.copy('dve', kdt[64:128, g * 8:(g + 1) * 8, 1, :], ptr[g % 2][64:128, :, :])
            with ExitStack() as s2:
                pkv = [P.ps(s2, "h_pkv%d" % i, [128, 4, 128], F32) for i in range(2)]
                KV3 = KV[:].rearrange("p (v c) -> p v c", c=64)
                for g in range(16):
                    pk = pkv[g % 2]
                    for i in range(4):
                        c = g * 4 + i
                        tl = c // 2; hf = c % 2
                        P.mm(pk[:, i, :], kdt[:, tl, hf, :], Vt[:, tl, :], start=True, stop=True)
                    for hh in range(2):
                        P.copy('act' if hh else 'dve', KV3[hh * 64:(hh + 1) * 64, :, g * 4:(g + 1) * 4],
                               pk[hh * 64:(hh + 1) * 64, :, hh * 64:(hh + 1) * 64].rearrange("p g v -> p v g"))
            if STAGE < 2:
                P.dma(A['mixT'][0:128, 0:T], kd[:], 'dbg'); continue
            P.copy('pool', dfull[:].rearrange("p (v c) -> p v c", c=64), dl[:].unsqueeze(1).broadcast_to([128, 64, 64]))
            P.scan(Sall[:], dfull[:], KV[:], 0.0, ALU.mult, ALU.add)
            S3 = Sall[:].rearrange("p (v c) -> p v c", c=64)
            for hh in range(2):
                P.copy('dve' if hh else 'act', Sbd[hh * 64:(hh + 1) * 64, 1:64, hh * 64:(hh + 1) * 64],
                       S3[hh * 64:(hh + 1) * 64, :, 0:63].rearrange("p v c -> p c v"))
            if STAGE < 3:
                P.dma(A['mixT'][0:128, 0:T], kd[:], 'dbg'); continue
            with ExitStack() as s2:
                pA = [P.ps(s2, "h_pA%d" % i, [128, 128], F32) for i in range(4)]
                po = [P.ps(s2, "h_po%d" % i, [128, 128], F32) for i in range(2)]
                Am = [P.sb(s2, "h_Am%d" % i, [128, 128], BF16) for i in range(4)]
                for tl in range(32):
                    cols = slice(tl * 128, (tl + 1) * 128)
                    for hh in range(2):
                        i = (tl % 2) * 2 + hh
                        P.mm(pA[i][:], kt[:, hh, cols], qt[:, cols], start=True, stop=True)
                        P.tt('dve', Am[i][:], pA[i][:], mh[:], ALU.mult)
                    p_ = po[tl % 2]
                    P.mm(p_[:], Vz[:, tl, 0, :], Am[(tl % 2) * 2][:], start=True, stop=False)
                    P.mm(p_[:], Vz[:, tl, 1, :], Am[(tl % 2) * 2 + 1][:], start=False, stop=False)
                    P.mm(p_[:, 0:64], Sbd[:, 2 * tl, :], qg[:, tl * 128:tl * 128 + 64], start=False, stop=False)
                    P.mm(p_[:, 64:128], Sbd[:, 2 * tl + 1, :], qg[:, tl * 128 + 64:tl * 128 + 128], start=False, stop=True)
                    P.copy('act', oT[:, cols], p_[:])
            if STAGE < 4:
                P.dma(A['mixT'][0:128, 0:T], kd[:], 'dbg'); continue
            with ExitStack() as s2:
                pss = [P.ps(s2, "h_pss%d" % i, [128, 512], F32) for i in range(2)]
                sq = [P.sb(s2, "h_sq%d" % i, [128, 512], BF16) for i in range(2)]
                rs = [P.sb(s2, "h_rs%d" % i, [128, 512], F32) for i in range(2)]
                ag = [P.sb(s2, "h_ag%d" % i, [128, 512], F32) for i in range(2)]
                ob = [P.sb(s2, "h_ob%d" % i, [128, 512], BF16) for i in range(2)]
                for j in range(8):
                    b = j % 2
                    cols = slice(j * 512, (j + 1) * 512)
                    P.dma(ag[b][:], A['pT'][512 + ct * 128:512 + (ct + 1) * 128, cols], 'hga%d' % b)
                    P.act(sq[b][:], oT[:, cols], AF.Square)
                    P.mm(pss[b][:], bones[:], sq[b][:], start=True, stop=True)
                    P.act(rs[b][:], pss[b][:], AF.Sqrt, bias=1e-6, scale=1.0 / 64)
                    P.op('dve', lambda e, o=rs[b][:]: e.reciprocal(out=o, in_=o), reads=[rs[b][:]], writes=[rs[b][:]])
                    P.act(ag[b][:], ag[b][:], AF.Silu)
                    P.stt('dve', rs[b][:], oT[:, cols], sm[:, 4:5], rs[b][:], ALU.mult, ALU.mult)
                    P.tt('pool', ob[b][:], rs[b][:], ag[b][:], ALU.mult)
                    P.dma(A['mixT'][ct * 128:(ct + 1) * 128, cols], ob[b][:], 'hgo%d' % b, q='pool')

import os
STAGE = int(os.environ.get('STAGE', '99'))

T = 4096
NEG = -30000.0


def phase_nsa(P, A, layer, consts):
    c = consts
    ident = c['ident']
    with ExitStack() as st:
        cx = AttnCtx(P, st, consts)
        lngh = P.sb(st, "n_lngh", [128, T], BF16)
        lngl = P.sb(st, "n_lngl", [128, T], BF16)
        c['selnegb'] = P.sb(st, "n_selnegb", [128, 24 * 64 + 64], BF16)
        P.memset('pool', c['selnegb'][:], 0.0)
        P.memset('pool', lngh[:], 0.0)
        P.memset('pool', lngl[:], 0.0)
        with ExitStack() as s0:
            lng = P.sb(s0, "n_lng", [24, T], F32)
            selneg = P.sb(s0, "n_selneg", [24, 24 * 64], F32)
            P.dma(selneg[:], A['selneg'], 'ns0')
            P.copy('dve', c['selnegb'][0:24, 0:24 * 64], selneg[:])
            P.dma(lng[:], A['pT'][1024:1048, :], 'ns0')
            P.act(lng[:], lng[:], AF.Exp, scale=-1.0)
            P.act(lng[:], lng[:], AF.Ln, bias=1.0)
            P.copy('dve', lngh[0:24, :], lng[:])
            P.tt('dve', lng[:], lng[:], lngh[0:24, :], ALU.subtract)
            P.copy('dve', lngl[0:24, :], lng[:])
        lng2 = (lngh, lngl)
        ovaug = P.sb(st, "n_ov", [128, 2, 72], BF16)
        wc = P.sb(st, "n_wc", [128, 3200], BF16)
        addm = P.sb(st, "n_addm", [128, 32, 64], F32)
        P.dma(ovaug[:], A['ovaug'], 'ns0')
        P.dma(wc[:], A['wc'], 'ns0')
        P.dma(addm[:], A['addmask'], 'ns0')
        kcTs = [P.sb(st, "n_kcT%d" % i, [128, 256], BF16) for i in range(2)]
        vcAs = [P.sb(st, "n_vcA%d" % i, [128, 2, 65], BF16) for i in range(2)]
        for g in range(2):
            kcT = kcTs[g]; vcA = vcAs[g]
            with ExitStack() as s2:
                w1 = P.sb(s2, "n_w1", [64, 32, 128], BF16)
                w1f = P.sb(s2, "n_w1f", [64, 32, 128], F32)
                w2 = P.sb(s2, "n_w2", [128, 64], BF16)
                w2f = P.sb(s2, "n_w2f", [128, 64], F32)
                posT = P.sb(s2, "n_posT", [64, 32], BF16)
                posf = P.sb(s2, "n_posf", [64, 32], F32)
                posb = P.sb(s2, "n_posb", [64, 32, 256], BF16)
                srcf = P.sb(s2, "n_srcf", [64, T], F32)
                srcb = P.sb(s2, "n_srcb", [64, T], BF16)
                bias = P.sb(s2, "n_bias", [128, 1], F32)
                xb = P.sb(s2, "n_xb", [128, 256], F32)
                x2 = P.sb(s2, "n_x2", [128, 256], F32)
                hid = P.sb(s2, "n_hid", [128, 256], BF16)
                ktm = P.sb(s2, "n_ktm", [128, 64], F32)
                kts = P.sb(s2, "n_kts", [128, 64], F32)
                ktb = P.sb(s2, "n_ktb", [128, 128], BF16)
                sm = P.sb(s2, "n_sm", [128, 4], F32)
                rt = P.sb(s2, "n_rt", [128, 4, 8], F32)
                kng = P.sb(s2, "n_kng", [128, 64], F32)
                cosc = P.sb(s2, "n_cosc", [128, 2, 8], F32)
                sinc = P.sb(s2, "n_sinc", [128, 2, 8], F32)
                ph = cx.psS[0]; pb = cx.psS[1]; po = cx.psO[0]
                pt = P.ps(s2, "n_pt", [128, 128], BF16)
                P.dma(kng[:], A['kng'][layer].partition_broadcast(128), 'ns1')
                P.dma(cosc[:], A['cosc'], 'ns1')
                P.dma(sinc[:], A['sinc'], 'ns1')
                P.memset('dve', hid[:], 0.0)
                P.memset('dve', vcA[:], 0.0)
                P.memset('dve', kcT[:], 0.0)
                P.memset('dve', ktb[:], 0.0)
                for which in range(2):
                    P.dma(w1f[:], A['w1r'][layer, which], 'ns2')
                    P.dma(w2f[:], A['w2'][layer, which], 'ns2')
                    P.dma(posf[:], A['posT'][layer, which], 'ns2')
                    P.dma(srcf[:], A['pT'][768 + 128 * which + 64 * g:768 + 128 * which + 64 * (g + 1), :], 'ns3', q='act')
                    P.copy('dve', w1[:], w1f[:])
                    P.copy('dve', w2[:], w2f[:])
                    P.copy('dve', posT[:], posf[:])
                    P.copy('dve', posb[:], posT[:].unsqueeze(2).broadcast_to([64, 32, 256]))
                    P.copy('act', srcb[:], srcf[:])
                    for l in range(32):
                        P.mm(ph[:, 0:255], w1[:, l, :], srcb[:].rearrange("p (n s) -> p n s", s=16)[:, (l // 16):(l // 16) + 255, l % 16], start=(l == 0), stop=False)
                    for l in range(32):
                        P.mm(ph[:, 0:255], w1[:, l, :], posb[:, l, 0:255], start=False, stop=(l == 31))
                    P.copy('act', xb[:, 0:255], ph[:, 0:255])
                    P.tt('dve', x2[:, 0:255], xb[:, 0:255], xb[:, 0:255], ALU.mult)
                    P.ts('dve', x2[:, 0:255], x2[:, 0:255], 0.044715, 1.0, ALU.mult, ALU.add)
                    P.tt('dve', x2[:, 0:255], x2[:, 0:255], xb[:, 0:255], ALU.mult)
                    P.act(x2[:, 0:255], x2[:, 0:255], AF.Sigmoid, scale=1.5957691216057308)
                    P.tt('dve', hid[:, 0:255], x2[:, 0:255], xb[:, 0:255], ALU.mult)
                    for nt in range(2):
                        P.mm(po[:, 0:64], hid[:, nt * 128:(nt + 1) * 128], w2[:], start=True, stop=True)
                        if which == 1:
                            nr = 128 if nt == 0 else 127
                            P.copy('act', vcA[0:nr, nt, 0:64], po[0:nr, 0:64])
                            P.memset('dve', vcA[0:nr, nt, 64:65], 1.0)
                        else:
                            P.act(kts[:], po[:, 0:64], AF.Square, accum_out=sm[:, 0:1])
                            P.act(sm[:, 1:2], sm[:, 0:1], AF.Sqrt, bias=1e-6, scale=1.0 / 64)
                            P.op('dve', lambda e, o=sm[:, 1:2]: e.reciprocal(out=o, in_=o), reads=[sm[:, 1:2]], writes=[sm[:, 1:2]])
                            P.stt('dve', ktm[:], po[:, 0:64], sm[:, 1:2], kng[:], ALU.mult, ALU.mult)
                            P.copy('act', ktb[:, 0:64], ktm[:])
                            P.tt('dve', rt[:, 0, :], ktm[:, 0:8], cosc[:, nt, :], ALU.mult)
                            P.tt('dve', rt[:, 1, :], ktm[:, 8:16], sinc[:, nt, :], ALU.mult)
                            P.tt('dve', rt[:, 2, :], ktm[:, 8:16], cosc[:, nt, :], ALU.mult)
                            P.tt('dve', rt[:, 3, :], ktm[:, 0:8], sinc[:, nt, :], ALU.mult)
                            P.tt('dve', ktb[:, 0:8], rt[:, 0, :], rt[:, 1, :], ALU.subtract)
                            P.tt('dve', ktb[:, 8:16], rt[:, 2, :], rt[:, 3, :], ALU.add)
                            P.transpose(pt[:], ktb[:], ident[:])
                            P.copy('dve', kcT[0:64, nt * 128:(nt + 1) * 128], pt[0:64, :])
            P.memset('dve', kcT[0:64, 255:256], 0.0)
        selT = P.sb(st, "n_selT", [128, T], BF16)
        imp = P.sb(st, "n_imp", [128, 32, 64], F32)
        acc = [P.sb(st, "n_acc%d" % i, [64, T], F32) for i in range(4)]
        Q = [P.sb(st, "n_Q%d" % i, [128, T], BF16) for i in range(4)]
        Ks = P.sb(st, "n_Ks", [128, T], BF16)
        Kw = P.sb(st, "n_Kw", [128, T], BF16)
        Vs = P.sb(st, "n_Vs", [128, 32, 65], BF16)
        Vw = P.sb(st, "n_Vw", [128, 32, 65], BF16)
        for g in range(2):
            kcT = kcTs[g]; vcA = vcAs[g]
            for hh in range(4):
                h = 4 * g + hh
                P.memset('pool', Q[hh][64:128, :], 0.0)
                P.dma(Q[hh][0:64, :], A['qkT'][64 * h:64 * (h + 1), :], 'nsq%d' % hh)
            P.dma(Ks[0:64, :], A['qkT'][512 + 64 * g:512 + 64 * (g + 1), :], 'nsk')
            P.dma(Ks[64:128, :], A['eall'], 'nsk')
            P.dma(Kw[0:64, :], A['qkT'][640 + 64 * g:640 + 64 * (g + 1), :], 'nsk')
            P.memset('pool', Kw[64:128, :], 0.0)
            P.memset('pool', Vs[:, :, 64:65], 1.0)
            P.memset('pool', Vw[:, :, 64:65], 1.0)
            P.dma(Vs[:, :, 0:64], A['vtok'][:, 256 + 64 * g:256 + 64 * (g + 1)].rearrange("(n p) d -> p n d", p=128), 'nsv', q='act')
            P.dma(Vw[:, :, 0:64], A['vtok'][:, 384 + 64 * g:384 + 64 * (g + 1)].rearrange("(n p) d -> p n d", p=128), 'nsv', q='act')
            with ExitStack() as s2:
                pimp = [P.ps(s2, "n_pimp%d" % i, [128, 4, 72], F32) for i in range(2)]
                pTc = [P.sb(s2, "n_pTc%d" % i, [128, 512], BF16) for i in range(3)]
                rinvs = [P.sb(s2, "n_rinv%d" % i, [128, 4], F32) for i in range(2)]
                kc_ = 0
                kch = 0
                for hh in range(4):
                    h = 4 * g + hh
                    for j in range(8):
                        q0 = j * 512
                        po = cx.psO[cx.kO % 2]
                        cx.kO += 1
                        pim = pimp[kch % 2]
                        rinv = rinvs[kch % 2]
                        kch += 1
                        tiles = []
                        for nt in range(2):
                            off = 2048 * nt + 31 - 512 * j
                            if -off + 511 < 0:
                                continue
                            tiles.append((nt, off))
                        for ti, (nt, off) in enumerate(tiles):
                            ps = cx.psS[cx.kS % 2]
                            cx.kS += 1
                            ptc = pTc[kc_ % 3]
                            kc_ += 1
                            first = (ti == 0)
                            last = (ti == len(tiles) - 1)

                            def s_fn(ps=ps, nt=nt, off=off, first=first, po=po, pim=pim, hh=hh, q0=q0):
                                if first:
                                    P.mm(po[0:65, :], c['zeros'][:, 0:65], c['ident_w'][:, 0:512], start=True, stop=False)
                                    P.mm(pim[:].rearrange("p a b -> p (a b)"), c['zeros'][:, 0:128], c['ident_w'][:, 0:288], start=True, stop=False)
                                full = (-off >= 2032)
                                P.mm(ps[:], kcT[:, nt * 128:(nt + 1) * 128], Q[hh][:, q0:q0 + 512], start=True, stop=full)
                                if not full:
                                    ci0 = -off + 511
                                    P.mm(ps[:], ident[:], wc[:, ci0:ci0 + 512], start=False, stop=True)

                            def exp_fn(ps=ps, ptc=ptc):
                                P.act(ptc[:], ps[:], AF.Exp)

                            def pv_fn(po=po, pim=pim, ptc=ptc, nt=nt, last=last):
                                P.mm(po[0:65, :], vcA[:, nt, :], ptc[:], start=False, stop=last)
                                for m in range(4):
                                    P.mm(pim[:, m, :], ptc[:, m * 128:(m + 1) * 128], ovaug[:, nt, :], start=False, stop=last)

                            _push_block(cx, s_fn, exp_fn, pv_fn, first=first)

                        def fin(po_, F, hh=hh, q0=q0, pim=pim, rinv=rinv, j=j):
                            for m in range(4):
                                tq = j * 4 + m
                                if hh == 0:
                                    P.ts('dve', imp[:, tq, :], pim[:, m, 0:64], rinv[:, m:m + 1], None, ALU.mult)
                                else:
                                    P.stt('dve', imp[:, tq, :], pim[:, m, 0:64], rinv[:, m:m + 1], imp[:, tq, :], ALU.mult, ALU.add)
                            P.tt('dve', acc[hh][:, q0:q0 + 512], po_[0:64, :], F[:], ALU.mult)

                        ea, eb = _mk_factor(cx, po, lng2, h, q0, fin)

                        def ea2(ea=ea, pim=pim, rinv=rinv):
                            ea()
                            P.ts('dve', rinv[:, 0:4].unsqueeze(2), pim[:, :, 64:65], 1e-30, None, ALU.max)
                            P.op('dve', lambda e, o=rinv[:, 0:4]: e.reciprocal(out=o, in_=o), reads=[rinv[:, 0:4]], writes=[rinv[:, 0:4]])

                        _end_chunk(cx, ea2, eb)
                attn_flush(cx)
            if STAGE < 2:
                continue
            with ExitStack() as s2:
                wk = [P.sb(s2, "n_wk%d" % i, [128, 64], F32) for i in range(2)]
                w2_ = [P.sb(s2, "n_wk2%d" % i, [128, 64], F32) for i in range(2)]
                m8 = [P.sb(s2, "n_m8%d" % i, [128, 16], F32) for i in range(2)]
                sb_ = [P.sb(s2, "n_sb%d" % i, [128, 128], BF16) for i in range(2)]
                pts = [P.ps(s2, "n_pts%d" % i, [128, 128], BF16) for i in range(2)]
                P.memset('pool', sb_[0][:], 0.0)
                P.memset('pool', sb_[1][:], 0.0)
                for tq in range(32):
                    b = tq % 2
                    P.tt('dve', wk[b][:], imp[:, tq, :], addm[:, tq, :], ALU.add)
                    P.op('dve', lambda e, o=m8[b][:, 0:8], i=wk[b][:]: e.max(out=o, in_=i), reads=[wk[b][:]], writes=[m8[b][:, 0:8]])
                    P.op('dve', lambda e, o=w2_[b][:], r=m8[b][:, 0:8], i=wk[b][:]: e.match_replace(out=o, in_to_replace=r, in_values=i, imm_value=-3.0e38),
                         reads=[m8[b][:, 0:8], wk[b][:]], writes=[w2_[b][:]])
                    P.op('dve', lambda e, o=m8[b][:, 8:16], i=w2_[b][:]: e.max(out=o, in_=i), reads=[w2_[b][:]], writes=[m8[b][:, 8:16]])
                    P.ts('dve', w2_[b][:], wk[b][:], m8[b][:, 15:16], None, ALU.is_ge)
                    P.ts('dve', wk[b][:], wk[b][:], -5.0e29, None, ALU.is_gt)
                    P.tt('dve', wk[b][:], wk[b][:], w2_[b][:], ALU.mult)
                    P.ts('dve', sb_[b][:, 64:128], wk[b][:], -1.0, -NEG, ALU.add, ALU.mult)
                    P.transpose(pts[b][:], sb_[b][:], ident[:])
                    P.copy('act', selT[64:128, tq * 128:(tq + 1) * 128], pts[b][64:128, :])
            for hh in range(4):
                P.dma(Q[hh][64:128, :], selT[64:128, :], 'nsq%d' % hh, q='sp' if hh % 2 else 'act')
            if STAGE < 3:
                continue
            with ExitStack() as s2:
                tmp = [P.sb(s2, "n_tmp%d" % i, [64, 512], F32) for i in range(2)]
                ob = [P.sb(s2, "n_ob%d" % i, [64, 512], BF16) for i in range(1)] * 2
                for hh in range(4):
                    h = 4 * g + hh
                    for j in range(8):
                        q0 = j * 512
                        a = acc[hh][:, q0:q0 + 512]

                        def fin_s(po_, F, a=a):
                            P.tt('dve', tmp[0][:], po_[0:64, :], F[:], ALU.mult)
                            P.tt('pool', a, a, tmp[0][:], ALU.add)

                        def fin_w(po_, F, a=a, j=j, h=h, q0=q0):
                            P.tt('dve', tmp[1][:], po_[0:64, :], F[:], ALU.mult)
                            P.tt('pool', ob[j % 2][:], a, tmp[1][:], ALU.add)
                            if STAGE >= 5:
                                P.dma(A['mixT'][256 + 64 * h:256 + 64 * (h + 1), q0:q0 + 512], ob[j % 2][:], 'nso%d' % (j % 2), q='sp')

                        attn_chunk(cx, Ks, 128, Q[hh], j, Vs, causal_entries(j, c['mc']), fin_s, lng2=lng2, gate_c=8 + h)
                        if STAGE >= 4:
                            attn_chunk(cx, Kw, 128, Q[hh], j, Vw, window_entries(j, c['mc'], c['mu']), fin_w, lng2=lng2, gate_c=16 + h)
                attn_flush(cx)


T = 4096
D = 1024
FF = 4096


def phase_wo(P, A, layer, consts, x_in, x_mid, per_tile=None):
    ident = consts['ident']
    with ExitStack() as st:
        Wo = P.sb(st, "wo", [128, 8, D], BF16)
        with ExitStack() as s2:
            wst = [P.sb(s2, "wost%d" % i, [128, D], F32) for i in range(2)]
            for kc in range(8):
                b = wst[kc % 2]
                P.dma(b[:], A['wo'][layer, kc * 128:(kc + 1) * 128, :], 'wost%d' % (kc % 2))
                P.copy('act' if kc % 2 else 'dve', Wo[:, kc, :], b[:])
        mx = [P.sb(st, "wo_mx%d" % i, [128, 8, 512], BF16) for i in range(2)]
        xt = [P.sb(st, "wo_xt%d" % i, [128, D], F32) for i in range(2)]
        xm = [P.sb(st, "wo_xm%d" % i, [128, D], F32) for i in range(2)]
        sq = P.sb(st, "wo_sq", [128, D], BF16)
        hb = [P.sb(st, "wo_hb%d" % i, [128, D], BF16) for i in range(2)]
        ss = [P.sb(st, "wo_ss%d" % i, [128, 2], F32) for i in range(2)]
        hst = [P.sb(st, "wo_hst%d" % i, [128, 8, 512], BF16) for i in range(1)] * 2
        po = [P.ps(st, "wo_po%d" % i, [128, 512], F32) for i in range(4)]
        ptr = [P.ps(st, "wo_ptr%d" % i, [128, 8, 128], BF16) for i in range(2)]
        for t in range(32):
            j = t // 4
            b = t % 2
            if t % 4 == 0:
                P.dma(mx[j % 2][:], A['mixT'][:, j * 512:(j + 1) * 512].rearrange("(a p) n -> p a n", p=128), 'womx%d' % (j % 2))
            P.dma(xt[b][:], x_in[t * 128:(t + 1) * 128, :], 'woxt%d' % b, q='act')
            for half in range(2):
                pp = po[(t % 2) * 2 + half]
                for kc in range(8):
                    P.mm(pp[:], mx[j % 2][:, kc, (t % 4) * 128:(t % 4 + 1) * 128], Wo[:, kc, half * 512:(half + 1) * 512],
                         start=(kc == 0), stop=(kc == 7))
                P.tt('dve', xm[b][:, half * 512:(half + 1) * 512], pp[:], xt[b][:, half * 512:(half + 1) * 512], ALU.add)
            P.dma(x_mid[t * 128:(t + 1) * 128, :], xm[b][:], 'woxm%d' % b, q='pool')
            P.act(sq[:], xm[b][:], AF.Square, accum_out=ss[b][:, 0:1])
            P.act(ss[b][:, 1:2], ss[b][:, 0:1], AF.Sqrt, bias=1e-6, scale=1.0 / D)
            P.op('dve', lambda e, o=ss[b][:, 1:2]: e.reciprocal(out=o, in_=o), reads=[ss[b][:, 1:2]], writes=[ss[b][:, 1:2]])
            P.ts('dve', hb[b][:], xm[b][:], ss[b][:, 1:2], None, ALU.mult)
            for kc in range(8):
                P.transpose(ptr[b][:, kc, :], hb[b][:, kc * 128:(kc + 1) * 128], ident[:])
            P.copy('act', hst[j % 2][:, :, (t % 4) * 128:(t % 4 + 1) * 128], ptr[b][:])
            if t % 4 == 3:
                P.dma(A['h2T'][:, j * 512:(j + 1) * 512].rearrange("(a p) n -> p a n", p=128), hst[j % 2][:], 'wohst%d' % (j % 2), q='pool')
            if per_tile is not None:
                per_tile(t)


def phase_wo_ffn(P, A, layer, consts, x_in, x_mid, x_out):
    with ExitStack() as st:
        Wu = P.sb(st, "wu", [128, 8, FF], BF16)
        Wd = P.sb(st, "wd", [128, 32, D], BF16)
        g2 = P.sb(st, "g2", [128, 8], F32)
        P.dma(g2[:], A['g2'][layer], 'ff0')
        with ExitStack() as s2:
            wst = [P.sb(s2, "fwst%d" % i, [128, 1024], F32) for i in range(2)]

            def per_tile(t):
                for u in range(2):
                    ci = 2 * t + u
                    b = u
                    if ci < 32:
                        kc, qt = ci // 4, ci % 4
                        P.dma(wst[b][:], A['wup'][layer, kc * 128:(kc + 1) * 128, qt * 1024:(qt + 1) * 1024], 'fwst%d' % b, q='sp')
                        if u:
                            P.ts('dve', Wu[:, kc, qt * 1024:(qt + 1) * 1024], wst[b][:], g2[:, kc:kc + 1], None, ALU.mult)
                        else:
                            P.act(Wu[:, kc, qt * 1024:(qt + 1) * 1024], wst[b][:], AF.Copy, scale=g2[:, kc:kc + 1])
                    else:
                        fc = ci - 32
                        P.dma(wst[b][:], A['wdn'][layer, fc * 128:(fc + 1) * 128, :], 'fwst%d' % b, q='sp')
                        P.copy('dve' if u else 'act', Wd[:, fc, :], wst[b][:])

            phase_wo(P, A, layer, consts, x_in, x_mid, per_tile=per_tile)
        _ffn_body(P, st, A, Wu, Wd, x_mid, x_out)


def phase_ffn(P, A, layer, consts, x_mid, x_out):
    with ExitStack() as st:
        Wu = P.sb(st, "wu", [128, 8, FF], BF16)
        Wd = P.sb(st, "wd", [128, 32, D], BF16)
        g2 = P.sb(st, "g2", [128, 8], F32)
        P.dma(g2[:], A['g2'][layer], 'ff0')
        with ExitStack() as s2:
            wst = [P.sb(s2, "fwst%d" % i, [128, 2048], F32) for i in range(3)]
            k = 0
            for kc in range(8):
                for hf in range(2):
                    b = k % 3
                    P.dma(wst[b][:], A['wup'][layer, kc * 128:(kc + 1) * 128, hf * 2048:(hf + 1) * 2048], 'fwst%d' % b, q='sp' if k % 2 else 'act')
                    if k % 2:
                        P.ts('dve', Wu[:, kc, hf * 2048:(hf + 1) * 2048], wst[b][:], g2[:, kc:kc + 1], None, ALU.mult)
                    else:
                        P.act(Wu[:, kc, hf * 2048:(hf + 1) * 2048], wst[b][:], AF.Copy, scale=g2[:, kc:kc + 1])
                    k += 1
            for fc2 in range(16):
                b = k % 3
                P.dma(wst[b][:].rearrange("p (a n) -> p a n", a=2), A['wdn'][layer, fc2 * 256:(fc2 + 1) * 256, :].rearrange("(a p) n -> p a n", p=128),
                      'fwst%d' % b, q='sp' if k % 2 else 'act')
                P.copy('dve' if k % 2 else 'act', Wd[:, fc2 * 2:(fc2 + 1) * 2, :], wst[b][:].rearrange("p (a n) -> p a n", a=2))
                k += 1
        _ffn_body(P, st, A, Wu, Wd, x_mid, x_out)


def _ffn_body(P, st, A, Wu, Wd, x_mid, x_out):
    if True:
        h2 = [P.sb(st, "ff_h2%d" % i, [128, 8, 512], BF16) for i in range(2)]
        uT = P.sb(st, "ff_uT", [128, 32, 512], BF16)
        rl = [P.sb(st, "ff_rl%d" % i, [128, 512], F32) for i in range(2)]
        xt = [P.sb(st, "ff_xt%d" % i, [128, D], F32) for i in range(2)]
        xo = [P.sb(st, "ff_xo%d" % i, [128, D], F32) for i in range(2)]
        pu = [P.ps(st, "ff_pu%d" % i, [128, 512], F32) for i in range(3)]
        pd = [P.ps(st, "ff_pd%d" % i, [128, 512], F32) for i in range(4)]
        ku = 0
        for j in range(8):
            P.dma(h2[j % 2][:], A['h2T'][:, j * 512:(j + 1) * 512].rearrange("(a p) n -> p a n", p=128), 'ffh2%d' % (j % 2))
            for fc in range(32):
                pp = pu[ku % 3]
                r = rl[ku % 2]
                for kc in range(8):
                    P.mm(pp[:], Wu[:, kc, fc * 128:(fc + 1) * 128], h2[j % 2][:, kc, :], start=(kc == 0), stop=(kc == 7))
                P.act(r[:], pp[:], AF.Relu)
                P.tt('dve' if ku % 2 else 'pool', uT[:, fc, :], r[:], r[:], ALU.mult)
                ku += 1
            for tt in range(4):
                t = j * 4 + tt
                b = t % 2
                P.dma(xt[b][:], x_mid[t * 128:(t + 1) * 128, :], 'ffxt%d' % b, q='act')
                for half in range(2):
                    pp = pd[(t % 2) * 2 + half]
                    for fc in range(32):
                        P.mm(pp[:], uT[:, fc, tt * 128:(tt + 1) * 128], Wd[:, fc, half * 512:(half + 1) * 512], start=(fc == 0), stop=(fc == 31))
                    P.tt('dve', xo[b][:, half * 512:(half + 1) * 512], pp[:], xt[b][:, half * 512:(half + 1) * 512], ALU.add)
                P.dma(x_out[t * 128:(t + 1) * 128, :], xo[b][:], 'ffxo%d' % b, q='pool')

import ml_dtypes
from concourse.bass_utils import run_bass_kernel_spmd

T=4096; D=1024
OFF = {}
_names = ['aq','af','ai','ag','bq','bkc','bvc','bks','bvs','bkw','bvw','bg','cq','ck','cv','cf']
_sizes = [256,256,256,256,512,128,128,128,128,128,128,24,256,256,256,4]
_o = 0
for n_, s_ in zip(_names, _sizes):
    OFF[n_] = (_o, _o + s_); _o += s_
TOK_ORDER = ['bq','bks','bkw','cq','ck','ai','bvs','bvw','cv']
T_ORDER = ['aq','af','ag','bkc','bvc','bg','cf']

def win_layout(w_in):
    L = w_in.shape[0]
    out = np.zeros((L, 1024, 2048 + 1152), np.float32)
    c = 0
    for n_ in TOK_ORDER:
        a, b = OFF[n_]; out[:, :, c:c + b - a] = w_in[:, :, a:b]; c += b - a
    assert c == 2048
    for n_ in T_ORDER:
        a, b = OFF[n_]; out[:, :, c:c + b - a] = w_in[:, :, a:b]; c += b - a
    return out

def rope_tables():
    inv = np.power(np.float32(500000.0), -np.arange(0, 16, 2, dtype=np.float32) / 16).astype(np.float32)
    pos = np.arange(T, dtype=np.float32)
    ang = pos[:, None] * inv[None, :]
    cos = np.cos(ang).astype(np.float32); sin = np.sin(ang).astype(np.float32)
    return (np.ascontiguousarray(cos.reshape(32, 128, 8).transpose(1, 0, 2)),
            np.ascontiguousarray(sin.reshape(32, 128, 8).transpose(1, 0, 2)))

def _skip():
    pass

def const_inputs():
    k = np.arange(128)[:, None]; q = np.arange(128)[None, :]
    mc = np.where(k <= q, 0.0, -30000.0).astype(ml_dtypes.bfloat16)
    mu = np.where(k > q, 0.0, -30000.0).astype(ml_dtypes.bfloat16)
    selneg = np.zeros((24, 24 * 64), np.float32)
    for c in range(24):
        selneg[c, c * 64:(c + 1) * 64] = -1.0
    return dict(ident=np.eye(128, dtype=ml_dtypes.bfloat16), mc=mc, mu=mu, selneg=selneg)

def _unused_ref_proj(inp, layer, x):
    x = x.astype(np.float64)
    h = x / np.sqrt((x * x).mean(-1, keepdims=True) + 1e-6) * inp['norm1_g'][layer]
    return h @ inp['w_in'][layer].astype(np.float64)

def hgrn_consts(inp):
    s = np.arange(128)[:, None]; t = np.arange(128)[None, :]
    mh = ((s // 64 == t // 64) & (s <= t)).astype(ml_dtypes.bfloat16)
    bones = (s // 64 == t // 64).astype(ml_dtypes.bfloat16)
    lbl = np.ascontiguousarray(inp['hgrn_lb_logits'].reshape(2, 2, 128).transpose(1, 2, 0)).astype(np.float32)
    og = np.tile(inp['hgrn_onorm_g'], (1, 2)).reshape(2, 128, 1).astype(np.float32)
    return dict(mh=mh, bones=bones, lbl=lbl, og=og)

def _unused_ref_hgrn(inp, layer, proj):
    def sl(n): a, b = OFF[n]; return proj[:, a:b]
    lbp = np.exp(inp['hgrn_lb_logits'].astype(np.float64)); lbp /= lbp.sum(0, keepdims=True)
    lb_all = np.cumsum(lbp, 0) - lbp[0:1]
    lb = lb_all[layer].reshape(4, 64)
    z = sl('af').reshape(T, 4, 64)
    sig = 1 / (1 + np.exp(-z))
    f = lb + (1 - lb) * sig; logf = np.log(f); k = (1 - lb) * (1 - sig)
    q = sl('aq').reshape(T, 4, 64) * 0.125; v = sl('ai').reshape(T, 4, 64)
    o = np.zeros((T, 4, 64))
    for h in range(4):
        S = np.zeros((64, 64))
        for c in range(64):
            r = slice(c * 64, (c + 1) * 64)
            G = np.cumsum(logf[r, h], 0)
            qc, kc, vc = q[r, h], k[r, h], v[r, h]
            o_inter = (qc * np.exp(G)) @ S
            diff = G[:, None, :] - G[None, :, :]
            mask = np.tril(np.ones((64, 64), bool))
            dec = np.where(mask[:, :, None], np.exp(np.minimum(diff, 0)), 0)
            sc = np.einsum('tk,sk,tsk->ts', qc, kc, dec)
            o[r, h] = o_inter + sc @ vc
            S = S * np.exp(G[-1])[:, None] + (kc * np.exp(G[-1] - G)).T @ vc
    g = sl('ag').reshape(T, 4, 64)
    gate = g / (1 + np.exp(-g))
    on = o / np.sqrt((o * o).mean(-1, keepdims=True) + 1e-6) * inp['hgrn_onorm_g'][layer]
    return (on * gate).reshape(T, 256)

def nsa_consts(inp):
    n_cmp = 255
    ci = np.arange(n_cmp)[:, None]; sj = np.arange(64)[None, :]
    ov = ((ci * 16 <= sj * 64 + 63) & (ci * 16 + 31 >= sj * 64)).astype(np.float32)
    ovaug = np.zeros((256, 72), np.float32); ovaug[:255, :64] = ov; ovaug[:255, 64] = 1.0
    ovaug = np.ascontiguousarray(ovaug.reshape(2, 128, 72).transpose(1, 0, 2)).astype(ml_dtypes.bfloat16)
    nl = np.arange(128)[:, None]; cc = np.arange(3200)[None, :] - 511
    wc = np.where(cc >= 16 * nl, 0.0, -30000.0).astype(ml_dtypes.bfloat16)
    eall = (np.arange(T)[None, :] // 64 == np.arange(64)[:, None]).astype(ml_dtypes.bfloat16)
    q = np.arange(T)[:, None]; j = np.arange(64)[None, :]; cur = q // 64
    am = np.zeros((T, 64), np.float32)
    am[(j == 0) | (j == cur) | (j == cur - 1)] = 1e30
    am[np.broadcast_to(j > cur, am.shape)] = -1e30
    addmask = np.ascontiguousarray(am.reshape(32, 128, 64).transpose(1, 0, 2))
    inv = np.power(np.float32(500000.0), -np.arange(0, 16, 2, dtype=np.float32) / 16).astype(np.float32)
    pos = (np.arange(256, dtype=np.float32) * 16 + 31)
    ang = pos[:, None] * inv[None, :]
    cosc = np.ascontiguousarray(np.cos(ang).astype(np.float32).reshape(2, 128, 8).transpose(1, 0, 2))
    sinc = np.ascontiguousarray(np.sin(ang).astype(np.float32).reshape(2, 128, 8).transpose(1, 0, 2))
    w1r = np.ascontiguousarray(inp['nsa_cmp_w1'].reshape(2, 2, 32, 64, 128).transpose(0, 1, 3, 2, 4)).astype(np.float32)
    posT = np.ascontiguousarray(inp['nsa_cmp_pos'].transpose(0, 1, 3, 2)).astype(np.float32)
    return dict(ovaug=ovaug, wc=wc, eall=eall, addmask=addmask, cosc=cosc, sinc=sinc, w1r=w1r, posT=posT,
                w2=inp['nsa_cmp_w2'].astype(np.float32), kng=inp['nsa_kn_g'].astype(np.float32))

NSA_SHAPES = [('ovaug', [128, 2, 72], BF16), ('wc', [128, 3200], BF16), ('eall', [64, 4096], BF16), ('addmask', [128, 32, 64], F32),
              ('cosc', [128, 2, 8], F32), ('sinc', [128, 2, 8], F32), ('w1r', [2, 2, 64, 32, 128], F32), ('posT', [2, 2, 64, 32], F32),
              ('w2', [2, 2, 128, 64], F32), ('kng', [2, 64], F32)]


import ml_dtypes
from concourse.bass_utils import run_bass_kernel_spmd

_IN_SHAPES = [('x', [T, D], F32), ('win', [2, D, WCOLS], F32), ('g1', [2, 128, 8], F32), ('gq', [2, 1280], F32),
              ('cos', [128, 32, 8], F32), ('sin', [128, 32, 8], F32), ('ident', [128, 128], BF16), ('mc', [128, 128], BF16),
              ('mu', [128, 128], BF16), ('selneg', [24, 1536], F32), ('mh', [128, 128], BF16), ('bones', [128, 128], BF16),
              ('lbl', [2, 128, 2], F32), ('og', [2, 128, 1], F32), ('fb', [2, 4, 1], F32), ('wo', [2, 1024, 1024], F32),
              ('wup', [2, 1024, 4096], F32), ('wdn', [2, 4096, 1024], F32), ('g2', [2, 128, 8], F32)] + NSA_SHAPES

KDEPTH = int(os.environ.get('KDEPTH', '2'))
KPHASES = os.environ.get('KPHASES', '1hnfwf')


def _body(P):
    nc = P.nc
    A = {}
    for k_, shp, dt_ in _IN_SHAPES:
        A[k_] = nc.dram_tensor(k_, shp, dt_, kind="ExternalInput").ap()
    A['y'] = nc.dram_tensor("y", [T, D], F32, kind="ExternalOutput").ap()
    A['qkT'] = nc.dram_tensor("qkT", [1280, T], BF16).ap()
    A['vtok'] = nc.dram_tensor("vtok", [T, 768], BF16).ap()
    A['pT'] = nc.dram_tensor("pT", [TC, T], F32).ap()
    A['mixT'] = nc.dram_tensor("mixT", [1024, T], BF16).ap()
    A['h2T'] = nc.dram_tensor("h2T", [1024, T], BF16).ap()
    xm = nc.dram_tensor("xmid", [T, D], F32).ap()
    x1 = nc.dram_tensor("x1", [T, D], F32).ap()
    xin = A['x']
    for layer in range(KDEPTH):
        A['x'] = xin
        with ExitStack() as st:
            phase1(P, st, A, layer)
        with ExitStack() as st:
            consts = load_consts(P, st, A)
            if 'h' in KPHASES:
                phase_hgrn(P, A, layer, consts)
            if 'n' in KPHASES:
                phase_nsa(P, A, layer, consts)
            if 'f' in KPHASES:
                phase_fox(P, A, layer, consts)
            xout = x1 if layer < KDEPTH - 1 else A['y']
            phase_wo_ffn(P, A, layer, consts, xin, xm, xout)
        xin = xout


def _host_inputs(inp):
    cos, sin = rope_tables()
    gq = np.concatenate([np.tile(inp['nsa_qn_g'], (1, 8)), np.tile(inp['nsa_kn_g'], (1, 4)), np.tile(inp['fox_qn_g'], (1, 4)),
                         np.tile(inp['fox_kn_g'], (1, 4))], axis=1).astype(np.float32)
    base = {"win": win_layout(inp['w_in']), "g1": np.ascontiguousarray(inp['norm1_g'].reshape(2, 8, 128).transpose(0, 2, 1)),
            "g2": np.ascontiguousarray(inp['norm2_g'].reshape(2, 8, 128).transpose(0, 2, 1)),
            "gq": gq, "cos": cos, "sin": sin, "fb": inp['fox_fb'].reshape(2, 4, 1).astype(np.float32),
            "wo": inp['w_o'], "wup": inp['w_up'], "wdn": inp['w_down']}
    base.update(const_inputs()); base.update(hgrn_consts(inp)); base.update(nsa_consts(inp))
    return base


def kernel(**inp):
    inp = {k: np.asarray(v) for k, v in inp.items()}
    nc, plan = build_two_pass(lambda: bass.Bass("TRN2", target_bir_lowering=False), _body)
    base = _host_inputs(inp)
    in_maps = []
    for b in range(8):
        m = dict(base); m['x'] = np.ascontiguousarray(inp['x'][b]); in_maps.append(m)
    res = run_bass_kernel_spmd(nc, in_maps, core_ids=list(range(8)))
    return np.stack([r['y'] for r in res.results], axis=0).astype(np.float32)
```

```python
import numpy as np, sys, time, os, math
import numpy as np
from contextlib import ExitStack
import concourse.bass as bass
import concourse.mybir as mybir

F32 = mybir.dt.float32
BF16 = mybir.dt.bfloat16
AF = mybir.ActivationFunctionType
ALU = mybir.AluOpType
AX = mybir.AxisListType


def _box(ap):
    t = ap.tensor
    dims = ap.ap
    off = int(ap.offset)
    shp = tuple(t.shape)
    rowsize = 1
    for s in shp[1:]:
        rowsize *= int(s)
    r0 = off // rowsize
    f0 = off % rowsize
    rows = 0
    free = 0
    for (st, cnt) in dims:
        st = int(st); cnt = int(cnt)
        if cnt <= 1 or st == 0:
            continue
        if st % rowsize == 0:
            rows += (st // rowsize) * (cnt - 1)
        else:
            free += st * (cnt - 1)
    return t.name, (r0, r0 + rows, f0, f0 + free)


def _ov(a, b):
    return a[0] <= b[1] and b[0] <= a[1] and a[2] <= b[3] and b[2] <= a[3]


def _cont(a, b):
    return a[0] <= b[0] and b[1] <= a[1] and a[2] <= b[2] and b[3] <= a[3]


class Prog:
    def __init__(self, nc, plan=None):
        self.nc = nc
        self.plan = plan
        self.rec = plan is None
        self.eng = dict(pe=nc.tensor, dve=nc.vector, act=nc.scalar, pool=nc.gpsimd, sp=nc.sync)
        self.n = 0
        self.ins = []
        self.track = {}
        self.lane_cnt = {}
        self.freed = {}
        self.uid = 0
        self.stack = ExitStack()
        self.psum_rr = 0
        self.psum_banks = []
        if not self.rec:
            self.sem = {}
            for e in ['pe', 'dve', 'act', 'pool']:
                self.sem[e] = self.stack.enter_context(nc.semaphore("sem_" + e))
            self.lane_sem = {}
            for ln in plan['lanes']:
                self.lane_sem[ln] = self.stack.enter_context(nc.semaphore("ln_" + ln))

    def sb(self, st, name, shape, dtype):
        self.uid += 1
        name = "%s_%d" % (name, self.uid)
        t = st.enter_context(self.nc.sbuf_tensor("s_" + name, list(shape), dtype))
        st.callback(self._free, "s_" + name)
        return t

    def ps(self, st, name, shape, dtype=F32):
        self.uid += 1
        name = "%s_%d" % (name, self.uid)
        t = st.enter_context(self.nc.psum_tensor("p_" + name, list(shape), dtype))
        st.callback(self._free, "p_" + name)
        return t

    def _free(self, name):
        if not self.rec:
            return
        recs = self.track.pop(name, [])
        for (b, i, w) in recs:
            r = self.ins[i]
            key = ('l', r['lane'], i) if r['dma'] else ('e', r['eng'])
            if r['dma']:
                self.freed[key] = i
            else:
                self.freed[key] = max(self.freed.get(key, -1), i)

    def _access(self, idx, eng, dma, ap, write, deps):
        name, box = _box(ap)
        if name not in self.track:
            big = (0, 10 ** 9, 0, 10 ** 9)
            kind = ap.space
            self.track[name] = [] if str(kind) == 'DRAM' else [(big, i, True) for i in sorted(set(self.freed.values()))]
        recs = self.track[name]
        for (b, i, w) in recs:
            if (write or w) and _ov(b, box):
                deps.append((i, (w and not write)))
        if write:
            recs[:] = [r for r in recs if not _cont(box, r[0])]
        elif not dma:
            recs[:] = [r for r in recs if not ((not r[2]) and r[1] < len(self.ins) and self.ins[r[1]]['eng'] == eng
                                               and not self.ins[r[1]]['dma'] and _cont(box, r[0]))]
        recs.append((box, idx, write))

    def op(self, eng, fn, reads=(), writes=(), dma=False, lane=None):
        idx = self.n
        self.n += 1
        if self.rec:
            deps = []
            for ap in reads:
                self._access(idx, eng, dma, ap, False, deps)
            for ap in writes:
                self._access(idx, eng, dma, ap, True, deps)
            lanewaits = {}
            d2 = {}
            for (j, raw) in deps:
                if j == idx:
                    continue
                pj = self.ins[j]
                if pj['dma']:
                    ln = pj['lane']
                    lanewaits[ln] = max(lanewaits.get(ln, 0), pj['lane_val_at'])
                    lanewaits[ln] = max(lanewaits[ln], self.lane_cnt[ln])
                    continue
                if pj['eng'] == eng and not dma:
                    if eng == 'pe':
                        continue
                    if not raw:
                        continue
                d2[j] = True
            rec = dict(eng=eng, deps=list(d2.keys()), lanewaits=lanewaits, dma=dma, lane=lane)
            if dma:
                self.lane_cnt[lane] = self.lane_cnt.get(lane, 0) + 16
                rec['lane_val_at'] = self.lane_cnt[lane]
            self.ins.append(rec)
            return None
        else:
            info = self.plan['ins'][idx]
            e = self.eng[eng]
            for (sname, val) in info['waits']:
                s = self.sem[sname[1]] if sname[0] == 'e' else self.lane_sem[sname[1]]
                e.wait_ge(s, val)
            inst = fn(e)
            if dma:
                inst.then_inc(self.lane_sem[lane], 16)
            elif info['signal']:
                inst.then_inc(self.sem[eng], 1)
            return inst

    def make_plan(self):
        ins = self.ins
        signal = [False] * len(ins)
        for r in ins:
            for j in r['deps']:
                signal[j] = True
        cnt = dict(pe=0, dve=0, act=0, pool=0, sp=0)
        sigval = [0] * len(ins)
        for i, r in enumerate(ins):
            if signal[i] and not r['dma']:
                cnt[r['eng']] += 1
                sigval[i] = cnt[r['eng']]
        seen = {e: {} for e in cnt}
        out = []
        for i, r in enumerate(ins):
            need = {}
            for j in r['deps']:
                k = ('e', ins[j]['eng'])
                need[k] = max(need.get(k, 0), sigval[j])
            for ln, v in r['lanewaits'].items():
                k = ('l', ln)
                need[k] = max(need.get(k, 0), v)
            waits = []
            sd = seen[r['eng']]
            for k, v in need.items():
                if sd.get(k, 0) >= v:
                    continue
                sd[k] = v
                waits.append((k, v))
            out.append(dict(waits=waits, signal=signal[i]))
        return dict(ins=out, lanes=sorted(self.lane_cnt.keys()), lane_final=dict(self.lane_cnt))

    def finish(self):
        if self.rec:
            return
        for ln, v in self.plan['lane_final'].items():
            self.nc.sync.wait_ge(self.lane_sem[ln], v)

    def dma(self, out, in_, lane, q='sp', **kw):
        return self.op(q, lambda e: e.dma_start(out=out, in_=in_, **kw), reads=[in_], writes=[out],
                       dma=True, lane=lane)

    def mm(self, out, lhsT, rhs, start=True, stop=True, **kw):
        return self.op('pe', lambda e: e.matmul(out, lhsT, rhs, start=start, stop=stop, **kw),
                       reads=[lhsT, rhs], writes=[out])

    def transpose(self, out, in_, ident):
        return self.op('pe', lambda e: e.transpose(out, in_, ident), reads=[in_, ident], writes=[out])

    def act(self, out, in_, func, bias=None, scale=None, accum_out=None, eng='act'):
        reads = [in_]
        kw = {}
        if bias is not None:
            kw['bias'] = bias
            if not isinstance(bias, (int, float)):
                reads.append(bias)
        if scale is not None:
            kw['scale'] = scale
            if not isinstance(scale, (int, float)):
                reads.append(scale)
        writes = [out]
        if accum_out is not None:
            kw['accum_out'] = accum_out
            writes.append(accum_out)
        return self.op(eng, lambda e: e.activation(out=out, in_=in_, func=func, **kw), reads=reads, writes=writes)

    def tt(self, eng, out, in0, in1, op):
        return self.op(eng, lambda e: e.tensor_tensor(out=out, in0=in0, in1=in1, op=op), reads=[in0, in1], writes=[out])

    def ts(self, eng, out, in0, s1, s2, op0, op1=None, accum_out=None):
        reads = [in0]
        if not isinstance(s1, (int, float)):
            reads.append(s1)
        if s2 is not None and not isinstance(s2, (int, float)):
            reads.append(s2)
        kw = {}
        writes = [out]
        if op1 is not None:
            kw['op1'] = op1
        if accum_out is not None:
            kw['accum_out'] = accum_out
            writes.append(accum_out)
        return self.op(eng, lambda e: e.tensor_scalar(out=out, in0=in0, scalar1=s1, scalar2=s2, op0=op0, **kw),
                       reads=reads, writes=writes)

    def stt(self, eng, out, in0, scalar, in1, op0, op1):
        reads = [in0, in1]
        if not isinstance(scalar, (int, float)):
            reads.append(scalar)
        return self.op(eng, lambda e: e.scalar_tensor_tensor(out=out, in0=in0, scalar=scalar, in1=in1, op0=op0, op1=op1),
                       reads=reads, writes=[out])

    def copy(self, eng, out, in_):
        if eng == 'act':
            return self.op(eng, lambda e: e.copy(out=out, in_=in_), reads=[in_], writes=[out])
        return self.op(eng, lambda e: e.tensor_copy(out=out, in_=in_), reads=[in_], writes=[out])

    def memset(self, eng, ap, val):
        return self.op(eng, lambda e: e.memset(ap, val), reads=[], writes=[ap])

    def scan(self, out, d0, d1, initial, op0, op1):
        reads = [d0, d1]
        if not isinstance(initial, (int, float)):
            reads.append(initial)
        return self.op('dve', lambda e: e.tensor_tensor_scan(out=out, data0=d0, data1=d1, initial=initial, op0=op0, op1=op1),
                       reads=reads, writes=[out])

    def generic(self, eng, fn, reads, writes):
        return self.op(eng, fn, reads=reads, writes=writes)


def build_two_pass(make_nc, body):
    nc1 = make_nc()
    p1 = Prog(nc1, None)
    body(p1)
    p1.stack.close()
    plan = p1.make_plan()
    nc2 = make_nc()
    p2 = Prog(nc2, plan)
    body(p2)
    p2.finish()
    p2.stack.close()
    return nc2, plan


T = 4096
NT = 32
D = 1024
KC = 8
TOKC = 2048
TC = 1152
WCOLS = TOKC + TC
EPS = 1e-6


def phase1(P, st, A, layer):
    nc = P.nc
    s = ExitStack()
    W = P.sb(s, "w_in", [128, KC, WCOLS], BF16)
    hT = P.sb(s, "hT", [128, KC, T], BF16)
    ident = P.sb(s, "ident", [128, 128], BF16)
    g1 = P.sb(s, "g1", [128, KC], F32)
    G = P.sb(s, "Gq", [128, 1280], F32)
    cos = P.sb(s, "cos", [128, NT, 8], F32)
    sin = P.sb(s, "sin", [128, NT, 8], F32)
    P.dma(ident[:], A['ident'], 'c0')
    P.dma(g1[:], A['g1'][layer], 'c0')
    P.dma(G[:], A['gq'][layer].partition_broadcast(128), 'c0')
    P.dma(cos[:], A['cos'], 'c0')
    P.dma(sin[:], A['sin'], 'c0')
    P.ts('dve', G[:, 0:512], G[:, 0:512], 0.125, None, ALU.mult)
    P.ts('dve', G[:, 768:1024], G[:, 768:1024], 0.125, None, ALU.mult)

    with ExitStack() as s2:
        wst = [P.sb(s2, "wst%d" % i, [128, WCOLS], F32) for i in range(4)]
        for kc in range(KC):
            b = wst[kc % 4]
            P.dma(b[:], A['win'][layer, kc * 128:(kc + 1) * 128, :], 'wst%d' % (kc % 4), q='sp' if kc % 2 == 0 else 'act')
            half = WCOLS // 2
            P.ts('dve', W[:, kc, 0:half], b[:, 0:half], g1[:, kc:kc + 1], None, ALU.mult)
            P.act(W[:, kc, half:WCOLS], b[:, half:WCOLS], AF.Copy, scale=g1[:, kc:kc + 1])

    with ExitStack() as s2:
        xt = [P.sb(s2, "xt%d" % i, [128, D], F32) for i in range(2)]
        sq = P.sb(s2, "sqj", [128, D], F32)
        hb = [P.sb(s2, "hb%d" % i, [128, D], BF16) for i in range(2)]
        ss = [P.sb(s2, "ss%d" % i, [128, 2], F32) for i in range(2)]
        ptr = [P.ps(s2, "ptr%d" % i, [128, KC, 128], BF16) for i in range(2)]
        for t in range(NT):
            b = t % 2
            P.dma(xt[b][:], A['x'][t * 128:(t + 1) * 128, :], 'xt%d' % b)
            P.act(sq[:], xt[b][:], AF.Square, accum_out=ss[b][:, 0:1])
            P.act(ss[b][:, 1:2], ss[b][:, 0:1], AF.Sqrt, bias=EPS_AP(P), scale=1.0 / D)
            P.op('dve', lambda e, o=ss[b][:, 1:2]: e.reciprocal(out=o, in_=o), reads=[ss[b][:, 1:2]], writes=[ss[b][:, 1:2]])
            P.ts('dve', hb[b][:], xt[b][:], ss[b][:, 1:2], None, ALU.mult)
            for kc in range(KC):
                P.transpose(ptr[b][:, kc, :], hb[b][:, kc * 128:(kc + 1) * 128], ident[:])
            P.copy('act' if t % 2 else 'dve', hT[:, :, t * 128:(t + 1) * 128], ptr[b][:])

    with ExitStack() as s2:
        pp = [P.ps(s2, "ppT%d" % i, [128, 512], F32) for i in range(3)]
        so = [P.sb(s2, "soT%d" % i, [128, 512], F32) for i in range(3)]
        k = 0
        for c in range(TC // 128):
            for j in range(T // 512):
                b = k % 3
                for kc in range(KC):
                    P.mm(pp[b][:], W[:, kc, TOKC + c * 128:TOKC + (c + 1) * 128], hT[:, kc, j * 512:(j + 1) * 512],
                         start=(kc == 0), stop=(kc == KC - 1))
                P.copy('act' if k % 2 else 'dve', so[b][:], pp[b][:])
                P.dma(A['pT'][c * 128:(c + 1) * 128, j * 512:(j + 1) * 512], so[b][:], 'soT%d' % b, q='pool')
                k += 1

    with ExitStack() as s2:
        pg = [P.ps(s2, "pg%d" % i, [128, 512], F32) for i in range(4)]
        ptq = [P.ps(s2, "ptq%d" % i, [128, 4, 128], BF16) for i in range(3)]
        sqhs = [P.sb(s2, "sqh%d" % i, [128, 512], F32) for i in range(3)]
        ssh = [P.sb(s2, "ssh%d" % i, [128, 8], F32) for i in range(4)]
        xn = [P.sb(s2, "xn%d" % i, [128, 512], F32) for i in range(3)]
        qb = [P.sb(s2, "qb%d" % i, [128, 512], BF16) for i in range(3)]
        rts = [P.sb(s2, "rt%d" % i, [128, 4, 8, 8], F32) for i in range(3)]
        qst = [P.sb(s2, "qst%d" % i, [128, 10, 512], BF16) for i in range(2)]
        vst = [P.sb(s2, "vst%d" % i, [128, 768], BF16) for i in range(2)]
        groups = []
        kq = 0
        for t in range(NT):
            for gi in range(4):
                k = t * 4 + gi
                nh = [8, 8, 4, 0][gi]
                qi = None
                if nh:
                    qi = kq % 3
                    kq += 1
                groups.append((t, gi, k, qi))

        def stage(sidx, t, gi, k, q):
            sb_ = (t // 4) % 2
            b = k % 4
            nh = [8, 8, 4, 0][gi]
            nr = [8, 4, 0, 0][gi]
            w = nh * 64
            goff = [0, 512, 1024, 0][gi]
            vb = t % 2
            if sidx == 0:
                for kc in range(KC):
                    P.mm(pg[b][:], hT[:, kc, t * 128:(t + 1) * 128], W[:, kc, gi * 512:(gi + 1) * 512],
                         start=(kc == 0), stop=(kc == KC - 1))
                return
            if sidx == 1:
                if nh:
                    sqh = sqhs[q]
                    P.act(sqh[:, 0:w], pg[b][:, 0:w], AF.Square)
                    P.op('dve', lambda e, o=ssh[b][:, 0:nh], i=sqh[:, 0:w].rearrange("p (h d) -> p h d", d=64):
                         e.tensor_reduce(out=o, in_=i, axis=AX.X, op=ALU.add),
                         reads=[sqh[:, 0:w]], writes=[ssh[b][:, 0:nh]])
                if gi == 2:
                    P.copy('act', vst[vb][:, 0:256], pg[b][:, 256:512])
                if gi == 3:
                    P.copy('act', vst[vb][:, 256:768], pg[b][:, 0:512])
                    P.dma(A['vtok'][t * 128:(t + 1) * 128, :], vst[vb][:], 'vst%d' % vb, q='sp')
                return
            if not nh:
                return
            rt = rts[q]
            xv = xn[q][:, 0:max(nr, 1) * 64].rearrange("p (h d) -> p h d", d=64)
            qv = qb[q][:, 0:max(nr, 1) * 64].rearrange("p (h d) -> p h d", d=64)
            if sidx == 2:
                P.act(ssh[b][:, 0:nh], ssh[b][:, 0:nh], AF.Sqrt, bias=EPS_AP(P), scale=1.0 / 64)
                P.op('dve', lambda e, o=ssh[b][:, 0:nh]: e.reciprocal(out=o, in_=o), reads=[ssh[b][:, 0:nh]], writes=[ssh[b][:, 0:nh]])
                P.tt('dve', xn[q][:, 0:w].rearrange("p (h d) -> p h d", d=64),
                     pg[b][:, 0:w].rearrange("p (h d) -> p h d", d=64),
                     ssh[b][:, 0:nh].unsqueeze(2).broadcast_to([128, nh, 64]), ALU.mult)
            elif sidx == 3:
                if nr:
                    P.tt('pool', xn[q][:, 0:w], xn[q][:, 0:w], G[:, goff:goff + w], ALU.mult)
                    P.copy('act', qb[q][:, 0:w], xn[q][:, 0:w])
                    cb = cos[:, t, :].unsqueeze(1).broadcast_to([128, nr, 8])
                    sb2 = sin[:, t, :].unsqueeze(1).broadcast_to([128, nr, 8])
                    P.tt('dve', rt[:, 0, 0:nr, :], xv[:, :, 0:8], cb, ALU.mult)
                    P.tt('dve', rt[:, 1, 0:nr, :], xv[:, :, 8:16], sb2, ALU.mult)
                    P.tt('pool', rt[:, 2, 0:nr, :], xv[:, :, 8:16], cb, ALU.mult)
                    P.tt('pool', rt[:, 3, 0:nr, :], xv[:, :, 0:8], sb2, ALU.mult)
                else:
                    P.tt('pool', qb[q][:, 0:w], xn[q][:, 0:w], G[:, goff:goff + w], ALU.mult)
            elif sidx == 4:
                if nr:
                    P.tt('dve', qv[:, :, 0:8], rt[:, 0, 0:nr, :], rt[:, 1, 0:nr, :], ALU.subtract)
                    P.tt('pool', qv[:, :, 8:16], rt[:, 2, 0:nr, :], rt[:, 3, 0:nr, :], ALU.add)
                npair = nh // 2
                for pr in range(npair):
                    P.transpose(ptq[q][:, pr, :], qb[q][:, pr * 128:(pr + 1) * 128], ident[:])
            elif sidx == 5:
                npair = nh // 2
                pbase = [0, 4, 8][gi]
                P.copy('act' if gi % 2 else 'dve', qst[sb_][:, pbase:pbase + npair, (t % 4) * 128:(t % 4 + 1) * 128], ptq[q][:, 0:npair, :])
                if t % 4 == 3 and gi == 2:
                    j = t // 4
                    P.dma(A['qkT'][:, j * 512:(j + 1) * 512].rearrange("(a p) n -> p a n", p=128), qst[sb_][:], 'qst%d' % sb_, q='sp')

        NS = 6
        for step in range(len(groups) + NS - 1):
            for sidx in range(NS - 1, -1, -1):
                gidx = step - sidx
                if 0 <= gidx < len(groups):
                    stage(sidx, *groups[gidx])
    s.close()


_eps_cache = {}


def EPS_AP(P):
    return EPS


T = 4096
NEG = -30000.0


class AttnCtx:
    def __init__(self, P, st, consts):
        self.P = P
        self.psS = [P.ps(st, "aS%d" % i, [128, 512], F32) for i in range(2)]
        self.psO = [P.ps(st, "aO%d" % i, [128, 512], F32) for i in range(2)]
        self.psB = [P.ps(st, "aB%d" % i, [128, 512], F32) for i in range(2)]
        self.pT = [P.sb(st, "apT%d" % i, [128, 512], BF16) for i in range(3)]
        self.lr = [P.sb(st, "alr%d" % i, [65, 512], F32) for i in range(2)]
        self.F = [P.sb(st, "aF%d" % i, [64, 512], F32) for i in range(2)]
        self.lrh = [P.sb(st, "alrh%d" % i, [128, 512], BF16) for i in range(2)]
        self.lrl = [P.sb(st, "alrl%d" % i, [128, 512], BF16) for i in range(2)]
        self.G2 = [P.sb(st, "aG%d" % i, [64, 512], F32) for i in range(2)]
        for t_ in self.lrh + self.lrl:
            P.memset('pool', t_[:], 0.0)
        self.kS = 0
        self.kO = 0
        self.kF = 0
        self.vm = 65
        self.prev = None
        self.deferred = []
        self.c = consts


def _push_block(cx, s_fn, exp_fn, pv_fn, first=False):
    if first:
        for f in cx.deferred:
            f()
        cx.deferred = []
    s_fn()
    d = cx.deferred
    cx.deferred = []
    if cx.prev is not None:
        e, p, epi = cx.prev
        e()
        p()
        if epi is not None:
            epi[0]()
            cx.deferred.append(epi[1])
    for f in d:
        f()
    cx.prev = (exp_fn, pv_fn, None)


def _end_chunk(cx, epi_a, epi_b):
    cx.prev = (cx.prev[0], cx.prev[1], (epi_a, epi_b))


def attn_flush(cx):
    d = cx.deferred
    cx.deferred = []
    if cx.prev is not None:
        e, p, epi = cx.prev
        e()
        p()
        if epi is not None:
            epi[0]()
            d.append(epi[1])
        cx.prev = None
    for f in d:
        f()


def _mk_block(cx, po, Kaug, kr, Qaug, q0, Vaug, kt, lo, hi, masks, extra, first, last):
    P = cx.P
    c = cx.c
    ps = cx.psS[cx.kS % 2]
    pt = cx.pT[cx.kS % 3]
    cx.kS += 1

    def s_fn():
        if first:
            P.mm(po[0:cx.vm, :], c['zeros'][:, 0:cx.vm], c['ident_w'][:, 0:512], start=True, stop=False)
        P.mm(ps[:, lo:hi], Kaug[0:kr, kt * 128:(kt + 1) * 128], Qaug[0:kr, q0 + lo:q0 + hi], start=True, stop=(len(masks) == 0 and extra is None))
        if extra is not None:
            P.mm(ps[:, lo:hi], extra[0][0:64, kt * 128:(kt + 1) * 128], extra[1][0:64, q0 + lo:q0 + hi], start=False, stop=(len(masks) == 0))
        for mi, (mk, m) in enumerate(masks):
            P.mm(ps[:, m * 128:(m + 1) * 128], c['ident'][:], mk[:], start=False, stop=(mi == len(masks) - 1))

    def exp_fn():
        P.act(pt[:, lo:hi], ps[:, lo:hi], AF.Exp)

    def pv_fn():
        P.mm(po[0:cx.vm, lo:hi], Vaug[:, kt, 0:cx.vm], pt[:, lo:hi], start=False, stop=last)

    return s_fn, exp_fn, pv_fn


def _mk_factor(cx, po, lng2, gate_c, q0, finish):
    P = cx.P
    c = cx.c
    lr = cx.lr[cx.kF % 2]
    F = cx.F[cx.kF % 2]
    G2 = cx.G2[cx.kF % 2]
    cx.kF += 1
    pb = cx.psB[0]
    pb2 = cx.psB[1]

    lrh = cx.lrh[(cx.kF - 1) % 2]
    lrl = cx.lrl[(cx.kF - 1) % 2]

    def epi_a():
        P.ts('dve', lr[64:65, :], po[64:65, :], 1e-18, None, ALU.max)
        P.act(lr[64:65, :], lr[64:65, :], AF.Ln)
        P.copy('dve', lrh[64:65, :], lr[64:65, :])
        P.tt('dve', lrl[64:65, :], lr[64:65, :], lrh[64:65, :], ALU.subtract)

    def epi_b():
        P.mm(pb[:, :], c['negonesb'][:, :], lrh[:, :], start=True, stop=False)
        P.mm(pb[:, :], c['negonesb'][:, :], lrl[:, :], start=False, stop=True)
        if lng2 is not None:
            P.mm(pb2[:, :], c['selnegb'][:, gate_c * 64:gate_c * 64 + 128], lng2[0][:, q0:q0 + 512], start=True, stop=False)
            P.mm(pb2[:, :], c['selnegb'][:, gate_c * 64:gate_c * 64 + 128], lng2[1][:, q0:q0 + 512], start=False, stop=True)
        P.act(F[:], pb[0:64, :], AF.Exp)
        if lng2 is not None:
            P.act(G2[:], pb2[0:64, :], AF.Exp)
            P.tt('pool', F[:], F[:], G2[:], ALU.mult)
        finish(po, F)

    return epi_a, epi_b


def attn_chunk(cx, Kaug, kr, Qaug, j, Vaug, entries, finish, lng2=None, gate_c=None, extra=None):
    po = cx.psO[cx.kO % 2]
    cx.kO += 1
    q0 = j * 512
    n = len(entries)
    for ei, (kt, lo, hi, masks) in enumerate(entries):
        fns = _mk_block(cx, po, Kaug, kr, Qaug, q0, Vaug, kt, lo, hi, masks, extra, ei == 0, ei == n - 1)
        _push_block(cx, *fns, first=(ei == 0))
    ea, eb = _mk_factor(cx, po, lng2, gate_c, q0, finish)
    _end_chunk(cx, ea, eb)


def causal_entries(j, mc):
    ent = []
    for kt in range(4 * j + 4):
        if kt < 4 * j:
            ent.append((kt, 0, 512, []))
        else:
            m = kt - 4 * j
            ent.append((kt, 128 * m, 512, [(mc, m)]))
    return ent


def window_entries(j, mc, mu):
    ent = []
    for cc in range(-4, 4):
        kt = 4 * j + cc
        if kt < 0:
            continue
        lo = 128 * max(cc, 0)
        hi = 128 * (min(cc + 4, 3) + 1)
        masks = []
        if 0 <= cc <= 3:
            masks.append((mc, cc))
        if 0 <= cc + 4 <= 3:
            masks.append((mu, cc + 4))
        ent.append((kt, lo, hi, masks))
    return ent


def load_consts(P, st, A):
    c = {}
    c['ident'] = P.sb(st, "c_ident", [128, 128], BF16)
    c['mc'] = P.sb(st, "c_mc", [128, 128], BF16)
    c['mu'] = P.sb(st, "c_mu", [128, 128], BF16)
    c['zeros'] = P.sb(st, "c_zeros", [128, 128], BF16)
    c['ident_w'] = P.sb(st, "c_identw", [128, 512], BF16)
    c['negones'] = P.sb(st, "c_negones", [65, 64], F32)
    c['negonesb'] = P.sb(st, "c_negonesb", [128, 128], BF16)
    P.dma(c['ident'][:], A['ident'], 'c0')
    P.dma(c['mc'][:], A['mc'], 'c0')
    P.dma(c['mu'][:], A['mu'], 'c0')
    P.memset('dve', c['zeros'][:], 0.0)
    P.memset('dve', c['ident_w'][:], 0.0)
    P.memset('dve', c['negones'][:], -1.0)
    P.memset('dve', c['negonesb'][:], -1.0)
    return c


def phase_fox(P, A, layer, consts):
    with ExitStack() as st:
        cx = AttnCtx(P, st, consts)
        cf = P.sb(st, "f_cf", [4, T], F32)
        tmp = P.sb(st, "f_tmp", [4, T], F32)
        ones = P.sb(st, "f_ones", [4, T], F32)
        fb = P.sb(st, "f_fb", [4, 2], F32)
        cs = P.sb(st, "f_cs", [4, 3, T], BF16)
        ncs = P.sb(st, "f_ncs", [4, 3, T], BF16)
        P.dma(cf[:], A['pT'][1048:1052, :], 'fx0')
        P.dma(fb[:, 0:1], A['fb'][layer], 'fx0')
        P.ts('dve', fb[:, 1:2], fb[:, 0:1], -1.0, None, ALU.mult)
        P.memset('pool', ones[:], 1.0)
        P.act(tmp[:], cf[:], AF.Exp, bias=fb[:, 1:2], scale=-1.0)
        P.act(tmp[:], tmp[:], AF.Ln, bias=1.0)
        P.scan(cf[:], ones[:], tmp[:], 0.0, ALU.mult, ALU.subtract)
        P.copy('dve', cs[:, 0, :], cf[:])
        P.tt('dve', tmp[:], cf[:], cs[:, 0, :], ALU.subtract)
        P.copy('dve', cs[:, 1, :], tmp[:])
        P.tt('dve', tmp[:], tmp[:], cs[:, 1, :], ALU.subtract)
        P.copy('dve', cs[:, 2, :], tmp[:])
        P.ts('dve', ncs[:].rearrange("p a t -> p (a t)"), cs[:].rearrange("p a t -> p (a t)"), -1.0, None, ALU.mult)
        Qs = [P.sb(st, "f_Q%d" % i, [128, T], BF16) for i in range(2)]
        Ks = [P.sb(st, "f_K%d" % i, [128, T], BF16) for i in range(2)]
        Vs = [P.sb(st, "f_V%d" % i, [128, 32, 65], BF16) for i in range(2)]
        ob = [P.sb(st, "f_ob%d" % i, [64, 512], BF16) for i in range(2)]
        for i in range(2):
            P.memset('pool', Qs[i][64:128, :], 0.0)
            P.memset('pool', Ks[i][64:128, :], 0.0)
            P.memset('pool', Qs[i][64:70, :], 1.0)
            P.memset('pool', Ks[i][64:70, :], 1.0)
            P.memset('pool', Vs[i][:, :, 64:65], 1.0)

        def load(h):
            Q = Qs[h % 2]; K = Ks[h % 2]; V = Vs[h % 2]
            P.dma(Q[0:64, :], A['qkT'][768 + 64 * h:768 + 64 * (h + 1), :], 'fxq%d' % (h % 2))
            P.dma(K[0:64, :], A['qkT'][1024 + 64 * h:1024 + 64 * (h + 1), :], 'fxk%d' % (h % 2), q='act')
            for i in range(3):
                P.dma(Q[64 + i:65 + i, :], cs[h:h + 1, i, :], 'fxq%d' % (h % 2))
                P.dma(K[67 + i:68 + i, :], ncs[h:h + 1, i, :], 'fxk%d' % (h % 2), q='act')
            P.dma(V[:, :, 0:64], A['vtok'][:, 512 + 64 * h:512 + 64 * (h + 1)].rearrange("(n p) d -> p n d", p=128), 'fxv%d' % (h % 2))

        load(0)
        for h in range(4):
            if h + 1 < 4:
                load(h + 1)
            Q = Qs[h % 2]; K = Ks[h % 2]; V = Vs[h % 2]
            for j in range(8):
                def fin(po, F, o=ob[j % 2], j=j, h=h):
                    P.tt('dve', o[:], po[0:64, :], F[:], ALU.mult)
                    P.dma(A['mixT'][768 + 64 * h:768 + 64 * (h + 1), j * 512:(j + 1) * 512], o[:], 'fxo%d' % (j % 2), q='sp')
                attn_chunk(cx, K, 128, Q, j, V, causal_entries(j, consts['mc']), fin)
            attn_flush(cx)

import math, os
STAGE = int(os.environ.get('STAGE', '99'))

T = 4096
LN8 = math.log(0.125)


def phase_hgrn(P, A, layer, consts):
    for ct in range(2):
        with ExitStack() as st:
            B = [P.sb(st, "hB%d" % i, [128, T], F32) for i in range(5)]
            qt = P.sb(st, "h_qt", [128, T], BF16)
            kt = P.sb(st, "h_kt", [128, 2, T], BF16)
            qg = P.sb(st, "h_qg", [128, T], BF16)
            kd = P.sb(st, "h_kd", [128, T], BF16)
            kdt = P.sb(st, "h_kdt", [128, 32, 2, 128], BF16)
            Vt = P.sb(st, "h_Vt", [128, 32, 128], BF16)
            Vz = P.sb(st, "h_Vz", [128, 32, 2, 128], BF16)
            Sbd = P.sb(st, "h_Sbd", [128, 64, 128], BF16)
            rst = P.sb(st, "h_rst", [128, T], BF16)
            sm = P.sb(st, "h_sm", [128, 8], F32)
            dl = P.sb(st, "h_dl", [128, 64], F32)
            mh = P.sb(st, "h_mh", [128, 128], BF16)
            bones = P.sb(st, "h_bones", [128, 128], BF16)
            ident = consts['ident']
            P.dma(mh[:], A['mh'], 'hg0')
            P.dma(bones[:], A['bones'], 'hg0')
            P.dma(sm[:, 0:2], A['lbl'][ct], 'hg0')
            P.dma(sm[:, 4:5], A['og'][layer], 'hg0')
            P.dma(B[0][:], A['pT'][256 + ct * 128:256 + (ct + 1) * 128, :], 'hgz')
            P.dma(B[3][:], A['pT'][ct * 128:(ct + 1) * 128, :], 'hgq', q='act')
            P.dma(Vt[:], A['vtok'][:, ct * 128:(ct + 1) * 128].rearrange("(n p) d -> p n d", p=128), 'hgv', q='pool')
            P.memset('pool', Vz[:], 0.0)
            for hh in range(2):
                P.dma(Vz[:, :, hh, hh * 64:(hh + 1) * 64],
                      A['vtok'][:, ct * 128 + hh * 64:ct * 128 + (hh + 1) * 64].rearrange("(n p) d -> p n d", p=128), 'hgv', q='pool')
            P.memset('pool', Sbd[:], 0.0)
            P.memset('pool', kt[:], 0.0)
            P.memset('pool', kdt[:], 0.0)
            P.memset('pool', rst[:], 1.0)
            P.memset('pool', rst[:].rearrange("p (c s) -> p c s", s=64)[:, :, 0:1], 0.0)
            lb = sm[:, 2:3]; oml = sm[:, 3:4]; noml = sm[:, 5:6]
            if layer == 0:
                P.memset('dve', lb, 0.0)
            else:
                P.act(sm[:, 0:2], sm[:, 0:2], AF.Exp)
                P.tt('dve', sm[:, 6:7], sm[:, 0:1], sm[:, 1:2], ALU.add)
                P.op('dve', lambda e, o=sm[:, 6:7]: e.reciprocal(out=o, in_=o), reads=[sm[:, 6:7]], writes=[sm[:, 6:7]])
                P.tt('dve', lb, sm[:, 1:2], sm[:, 6:7], ALU.mult)
            P.ts('dve', oml, lb, -1.0, 1.0, ALU.mult, ALU.add)
            P.ts('dve', noml, oml, -1.0, None, ALU.mult)
            P.act(B[0][:], B[0][:], AF.Sigmoid)
            P.ts('dve', B[1][:], B[0][:], oml, lb, ALU.mult, ALU.add)
            P.act(B[1][:], B[1][:], AF.Ln)
            P.scan(B[2][:], rst[:], B[1][:], 0.0, ALU.mult, ALU.add)
            P.ts('dve', B[1][:], B[0][:], noml, oml, ALU.mult, ALU.add)
            G3 = B[2][:].rearrange("p (c s) -> p c s", s=64)
            D3 = B[0][:].rearrange("p (c s) -> p c s", s=64)
            P.tt('dve', D3, G3, G3[:, :, 31:32].broadcast_to([128, 64, 64]), ALU.subtract)
            P.act(B[4][:], B[0][:], AF.Exp, bias=LN8)
            P.tt('dve', qt[:], B[3][:], B[4][:], ALU.mult)
            P.act(B[4][:], B[0][:], AF.Exp, scale=-1.0)
            P.tt('dve', kt[0:64, 0, :], B[1][0:64, :], B[4][0:64, :], ALU.mult)
            P.tt('dve', kt[64:128, 1, :], B[1][64:128, :], B[4][64:128, :], ALU.mult)
            P.act(B[4][:], B[2][:], AF.Exp, bias=LN8)
            P.tt('dve', qg[:], B[3][:], B[4][:], ALU.mult)
            P.tt('dve', D3, G3[:, :, 63:64].broadcast_to([128, 64, 64]), G3, ALU.subtract)
            P.act(B[4][:], B[0][:], AF.Exp)
            P.tt('dve', kd[:], B[1][:], B[4][:], ALU.mult)
            P.act(dl[:].unsqueeze(2), G3[:, :, 63:64], AF.Exp)
            P.memset('dve', dl[:, 0:1], 0.0)
            KV = B[0]; dfull = B[1]; Sall = B[3]; oT = B[4]
            if STAGE < 1:
                P.dma(A['mixT'][0:128, 0:T], kd[:], 'dbg'); continue
            with ExitStack() as s2:
                ptr = [P.ps(s2, "h_ptr%d" % i, [128, 8, 128], BF16) for i in range(2)]
                for g in range(4):
                    for i in range(8):
                        tl = g * 8 + i
                        P.transpose(ptr[g % 2][:, i, :], kd[:, tl * 128:(tl + 1) * 128], ident[:])
                    P.copy('act', kdt[0:64, g * 8:(g + 1) * 8, 0, :], ptr[g % 2][0:64, :, :])
                    P.copy('dve', kdt[64:128, g * 8:(g + 1) * 8, 1, :], ptr[g % 2][64:128, :, :])
            with ExitStack() as s2:
                pkv = [P.ps(s2, "h_pkv%d" % i, [128, 4, 128], F32) for i in range(2)]
                KV3 = KV[:].rearrange("p (v c) -> p v c", c=64)
                for g in range(16):
                    pk = pkv[g % 2]
                    for i in range(4):
                        c = g * 4 + i
                        tl = c // 2; hf = c % 2
                        P.mm(pk[:, i, :], kdt[:, tl, hf, :], Vt[:, tl, :], start=True, stop=True)
                    for hh in range(2):
                        P.copy('act' if hh else 'dve', KV3[hh * 64:(hh + 1) * 64, :, g * 4:(g + 1) * 4],
                               pk[hh * 64:(hh + 1) * 64, :, hh * 64:(hh + 1) * 64].rearrange("p g v -> p v g"))
            if STAGE < 2:
                P.dma(A['mixT'][0:128, 0:T], kd[:], 'dbg'); continue
            P.copy('pool', dfull[:].rearrange("p (v c) -> p v c", c=64), dl[:].unsqueeze(1).broadcast_to([128, 64, 64]))
            P.scan(Sall[:], dfull[:], KV[:], 0.0, ALU.mult, ALU.add)
            S3 = Sall[:].rearrange("p (v c) -> p v c", c=64)
            for hh in range(2):
                P.copy('dve' if hh else 'act', Sbd[hh * 64:(hh + 1) * 64, 1:64, hh * 64:(hh + 1) * 64],
                       S3[hh * 64:(hh + 1) * 64, :, 0:63].rearrange("p v c -> p c v"))
            if STAGE < 3:
                P.dma(A['mixT'][0:128, 0:T], kd[:], 'dbg'); continue
            with ExitStack() as s2:
                pA = [P.ps(s2, "h_pA%d" % i, [128, 128], F32) for i in range(4)]
                po = [P.ps(s2, "h_po%d" % i, [128, 128], F32) for i in range(2)]
                Am = [P.sb(s2, "h_Am%d" % i, [128, 128], BF16) for i in range(4)]
                def scores(tl):
                    cols = slice(tl * 128, (tl + 1) * 128)
                    for hh in range(2):
                        i = (tl % 2) * 2 + hh
                        P.mm(pA[i][:], kt[:, hh, cols], qt[:, cols], start=True, stop=True)
                        P.tt('dve', Am[i][:], pA[i][:], mh[:], ALU.mult)

                def outs(tl):
                    cols = slice(tl * 128, (tl + 1) * 128)
                    p_ = po[tl % 2]
                    P.mm(p_[:], Vz[:, tl, 0, :], Am[(tl % 2) * 2][:], start=True, stop=False)
                    P.mm(p_[:], Vz[:, tl, 1, :], Am[(tl % 2) * 2 + 1][:], start=False, stop=False)
                    P.mm(p_[:, 0:64], Sbd[:, 2 * tl, :], qg[:, tl * 128:tl * 128 + 64], start=False, stop=False)
                    P.mm(p_[:, 64:128], Sbd[:, 2 * tl + 1, :], qg[:, tl * 128 + 64:tl * 128 + 128], start=False, stop=True)
                    P.copy('act', oT[:, cols], p_[:])

                for tl in range(33):
                    if tl < 32:
                        scores(tl)
                    if tl >= 1:
                        outs(tl - 1)
            if STAGE < 4:
                P.dma(A['mixT'][0:128, 0:T], kd[:], 'dbg'); continue
            with ExitStack() as s2:
                pss = [P.ps(s2, "h_pss%d" % i, [128, 512], F32) for i in range(2)]
                sq = [P.sb(s2, "h_sq%d" % i, [128, 512], BF16) for i in range(2)]
                rs = [P.sb(s2, "h_rs%d" % i, [128, 512], F32) for i in range(2)]
                ag = [P.sb(s2, "h_ag%d" % i, [128, 512], F32) for i in range(2)]
                ob = [P.sb(s2, "h_ob%d" % i, [128, 512], BF16) for i in range(2)]
                for j in range(8):
                    b = j % 2
                    cols = slice(j * 512, (j + 1) * 512)
                    P.dma(ag[b][:], A['pT'][512 + ct * 128:512 + (ct + 1) * 128, cols], 'hga%d' % b)
                    P.act(sq[b][:], oT[:, cols], AF.Square)
                    P.mm(pss[b][:], bones[:], sq[b][:], start=True, stop=True)
                    P.act(rs[b][:], pss[b][:], AF.Sqrt, bias=1e-6, scale=1.0 / 64)
                    P.op('dve', lambda e, o=rs[b][:]: e.reciprocal(out=o, in_=o), reads=[rs[b][:]], writes=[rs[b][:]])
                    P.act(ag[b][:], ag[b][:], AF.Silu)
                    P.stt('dve', rs[b][:], oT[:, cols], sm[:, 4:5], rs[b][:], ALU.mult, ALU.mult)
                    P.tt('pool', ob[b][:], rs[b][:], ag[b][:], ALU.mult)
                    P.dma(A['mixT'][ct * 128:(ct + 1) * 128, cols], ob[b][:], 'hgo%d' % b, q='pool')

import os
STAGE = int(os.environ.get('STAGE', '99'))

T = 4096
NEG = -30000.0


def phase_nsa(P, A, layer, consts):
    c = consts
    ident = c['ident']
    with ExitStack() as st:
        cx = AttnCtx(P, st, consts)
        lngh = P.sb(st, "n_lngh", [128, T], BF16)
        lngl = P.sb(st, "n_lngl", [128, T], BF16)
        c['selnegb'] = P.sb(st, "n_selnegb", [128, 24 * 64 + 64], BF16)
        P.memset('pool', c['selnegb'][:], 0.0)
        P.memset('pool', lngh[:], 0.0)
        P.memset('pool', lngl[:], 0.0)
        with ExitStack() as s0:
            lng = P.sb(s0, "n_lng", [24, T], F32)
            selneg = P.sb(s0, "n_selneg", [24, 24 * 64], F32)
            P.dma(selneg[:], A['selneg'], 'ns0')
            P.copy('dve', c['selnegb'][0:24, 0:24 * 64], selneg[:])
            P.dma(lng[:], A['pT'][1024:1048, :], 'ns0')
            P.act(lng[:], lng[:], AF.Exp, scale=-1.0)
            P.act(lng[:], lng[:], AF.Ln, bias=1.0)
            P.copy('dve', lngh[0:24, :], lng[:])
            P.tt('dve', lng[:], lng[:], lngh[0:24, :], ALU.subtract)
            P.copy('dve', lngl[0:24, :], lng[:])
        lng2 = (lngh, lngl)
        ovaug = P.sb(st, "n_ov", [128, 2, 72], BF16)
        wc = P.sb(st, "n_wc", [128, 3200], BF16)
        addm = P.sb(st, "n_addm", [128, 32, 64], F32)
        P.dma(ovaug[:], A['ovaug'], 'ns0')
        P.dma(wc[:], A['wc'], 'ns0')
        P.dma(addm[:], A['addmask'], 'ns0')
        kcTs = [P.sb(st, "n_kcT%d" % i, [128, 256], BF16) for i in range(2)]
        vcAs = [P.sb(st, "n_vcA%d" % i, [128, 2, 65], BF16) for i in range(2)]
        for g in range(2):
            kcT = kcTs[g]; vcA = vcAs[g]
            with ExitStack() as s2:
                w1 = P.sb(s2, "n_w1", [64, 32, 128], BF16)
                w1f = P.sb(s2, "n_w1f", [64, 32, 128], F32)
                w2 = P.sb(s2, "n_w2", [128, 64], BF16)
                w2f = P.sb(s2, "n_w2f", [128, 64], F32)
                posT = P.sb(s2, "n_posT", [64, 32], BF16)
                posf = P.sb(s2, "n_posf", [64, 32], F32)
                posb = P.sb(s2, "n_posb", [64, 32, 256], BF16)
                srcf = P.sb(s2, "n_srcf", [64, T], F32)
                srcb = P.sb(s2, "n_srcb", [64, T], BF16)
                bias = P.sb(s2, "n_bias", [128, 1], F32)
                xb = P.sb(s2, "n_xb", [128, 256], F32)
                x2 = P.sb(s2, "n_x2", [128, 256], F32)
                hid = P.sb(s2, "n_hid", [128, 256], BF16)
                ktm = P.sb(s2, "n_ktm", [128, 64], F32)
                kts = P.sb(s2, "n_kts", [128, 64], F32)
                ktb = P.sb(s2, "n_ktb", [128, 128], BF16)
                sm = P.sb(s2, "n_sm", [128, 4], F32)
                rt = P.sb(s2, "n_rt", [128, 4, 8], F32)
                kng = P.sb(s2, "n_kng", [128, 64], F32)
                cosc = P.sb(s2, "n_cosc", [128, 2, 8], F32)
                sinc = P.sb(s2, "n_sinc", [128, 2, 8], F32)
                ph = cx.psS[0]; pb = cx.psS[1]; po = cx.psO[0]
                pt = P.ps(s2, "n_pt", [128, 128], BF16)
                P.dma(kng[:], A['kng'][layer].partition_broadcast(128), 'ns1')
                P.dma(cosc[:], A['cosc'], 'ns1')
                P.dma(sinc[:], A['sinc'], 'ns1')
                P.memset('dve', hid[:], 0.0)
                P.memset('dve', vcA[:], 0.0)
                P.memset('dve', kcT[:], 0.0)
                P.memset('dve', ktb[:], 0.0)
                for which in range(2):
                    P.dma(w1f[:], A['w1r'][layer, which], 'ns2')
                    P.dma(w2f[:], A['w2'][layer, which], 'ns2')
                    P.dma(posf[:], A['posT'][layer, which], 'ns2')
                    P.dma(srcf[:], A['pT'][768 + 128 * which + 64 * g:768 + 128 * which + 64 * (g + 1), :], 'ns3', q='act')
                    P.copy('dve', w1[:], w1f[:])
                    P.copy('dve', w2[:], w2f[:])
                    P.copy('dve', posT[:], posf[:])
                    P.copy('dve', posb[:], posT[:].unsqueeze(2).broadcast_to([64, 32, 256]))
                    P.copy('act', srcb[:], srcf[:])
                    for l in range(32):
                        P.mm(ph[:, 0:255], w1[:, l, :], srcb[:].rearrange("p (n s) -> p n s", s=16)[:, (l // 16):(l // 16) + 255, l % 16], start=(l == 0), stop=False)
                    for l in range(32):
                        P.mm(ph[:, 0:255], w1[:, l, :], posb[:, l, 0:255], start=False, stop=(l == 31))
                    P.copy('act', xb[:, 0:255], ph[:, 0:255])
                    P.tt('dve', x2[:, 0:255], xb[:, 0:255], xb[:, 0:255], ALU.mult)
                    P.ts('dve', x2[:, 0:255], x2[:, 0:255], 0.044715, 1.0, ALU.mult, ALU.add)
                    P.tt('dve', x2[:, 0:255], x2[:, 0:255], xb[:, 0:255], ALU.mult)
                    P.act(x2[:, 0:255], x2[:, 0:255], AF.Sigmoid, scale=1.5957691216057308)
                    P.tt('dve', hid[:, 0:255], x2[:, 0:255], xb[:, 0:255], ALU.mult)
                    for nt in range(2):
                        P.mm(po[:, 0:64], hid[:, nt * 128:(nt + 1) * 128], w2[:], start=True, stop=True)
                        if which == 1:
                            nr = 128 if nt == 0 else 127
                            P.copy('act', vcA[0:nr, nt, 0:64], po[0:nr, 0:64])
                            P.memset('dve', vcA[0:nr, nt, 64:65], 1.0)
                        else:
                            P.act(kts[:], po[:, 0:64], AF.Square, accum_out=sm[:, 0:1])
                            P.act(sm[:, 1:2], sm[:, 0:1], AF.Sqrt, bias=1e-6, scale=1.0 / 64)
                            P.op('dve', lambda e, o=sm[:, 1:2]: e.reciprocal(out=o, in_=o), reads=[sm[:, 1:2]], writes=[sm[:, 1:2]])
                            P.stt('dve', ktm[:], po[:, 0:64], sm[:, 1:2], kng[:], ALU.mult, ALU.mult)
                            P.copy('act', ktb[:, 0:64], ktm[:])
                            P.tt('dve', rt[:, 0, :], ktm[:, 0:8], cosc[:, nt, :], ALU.mult)
                            P.tt('dve', rt[:, 1, :], ktm[:, 8:16], sinc[:, nt, :], ALU.mult)
                            P.tt('dve', rt[:, 2, :], ktm[:, 8:16], cosc[:, nt, :], ALU.mult)
                            P.tt('dve', rt[:, 3, :], ktm[:, 0:8], sinc[:, nt, :], ALU.mult)
                            P.tt('dve', ktb[:, 0:8], rt[:, 0, :], rt[:, 1, :], ALU.subtract)
                            P.tt('dve', ktb[:, 8:16], rt[:, 2, :], rt[:, 3, :], ALU.add)
                            P.transpose(pt[:], ktb[:], ident[:])
                            P.copy('dve', kcT[0:64, nt * 128:(nt + 1) * 128], pt[0:64, :])
            P.memset('dve', kcT[0:64, 255:256], 0.0)
        selT = P.sb(st, "n_selT", [128, T], BF16)
        imp = P.sb(st, "n_imp", [128, 32, 64], F32)
        acc = [P.sb(st, "n_acc%d" % i, [64, T], F32) for i in range(4)]
        Q = [P.sb(st, "n_Q%d" % i, [128, T], BF16) for i in range(4)]
        Ks = P.sb(st, "n_Ks", [128, T], BF16)
        Kw = P.sb(st, "n_Kw", [128, T], BF16)
        Vs = P.sb(st, "n_Vs", [128, 32, 65], BF16)
        Vw = P.sb(st, "n_Vw", [128, 32, 65], BF16)
        for g in range(2):
            kcT = kcTs[g]; vcA = vcAs[g]
            for hh in range(4):
                h = 4 * g + hh
                P.memset('pool', Q[hh][64:128, :], 0.0)
                P.dma(Q[hh][0:64, :], A['qkT'][64 * h:64 * (h + 1), :], 'nsq%d' % hh)
            P.dma(Ks[0:64, :], A['qkT'][512 + 64 * g:512 + 64 * (g + 1), :], 'nsk')
            P.dma(Ks[64:128, :], A['eall'], 'nsk')
            P.dma(Kw[0:64, :], A['qkT'][640 + 64 * g:640 + 64 * (g + 1), :], 'nsk')
            P.memset('pool', Kw[64:128, :], 0.0)
            P.memset('pool', Vs[:, :, 64:65], 1.0)
            P.memset('pool', Vw[:, :, 64:65], 1.0)
            P.dma(Vs[:, :, 0:64], A['vtok'][:, 256 + 64 * g:256 + 64 * (g + 1)].rearrange("(n p) d -> p n d", p=128), 'nsv', q='act')
            P.dma(Vw[:, :, 0:64], A['vtok'][:, 384 + 64 * g:384 + 64 * (g + 1)].rearrange("(n p) d -> p n d", p=128), 'nsv', q='act')
            with ExitStack() as s2:
                pimp = [P.ps(s2, "n_pimp%d" % i, [128, 4, 72], F32) for i in range(2)]
                pTc = [P.sb(s2, "n_pTc%d" % i, [128, 512], BF16) for i in range(3)]
                rinvs = [P.sb(s2, "n_rinv%d" % i, [128, 4], F32) for i in range(2)]
                kc_ = 0
                kch = 0
                for hh in range(4):
                    h = 4 * g + hh
                    for j in range(8):
                        q0 = j * 512
                        po = cx.psO[cx.kO % 2]
                        cx.kO += 1
                        pim = pimp[kch % 2]
                        rinv = rinvs[kch % 2]
                        kch += 1
                        tiles = []
                        for nt in range(2):
                            off = 2048 * nt + 31 - 512 * j
                            if -off + 511 < 0:
                                continue
                            tiles.append((nt, off))
                        for ti, (nt, off) in enumerate(tiles):
                            ps = cx.psS[cx.kS % 2]
                            cx.kS += 1
                            ptc = pTc[kc_ % 3]
                            kc_ += 1
                            first = (ti == 0)
                            last = (ti == len(tiles) - 1)

                            def s_fn(ps=ps, nt=nt, off=off, first=first, po=po, pim=pim, hh=hh, q0=q0):
                                if first:
                                    P.mm(po[0:65, :], c['zeros'][:, 0:65], c['ident_w'][:, 0:512], start=True, stop=False)
                                    P.mm(pim[:].rearrange("p a b -> p (a b)"), c['zeros'][:, 0:128], c['ident_w'][:, 0:288], start=True, stop=False)
                                full = (-off >= 2032)
                                P.mm(ps[:], kcT[:, nt * 128:(nt + 1) * 128], Q[hh][:, q0:q0 + 512], start=True, stop=full)
                                if not full:
                                    ci0 = -off + 511
                                    P.mm(ps[:], ident[:], wc[:, ci0:ci0 + 512], start=False, stop=True)

                            def exp_fn(ps=ps, ptc=ptc):
                                P.act(ptc[:], ps[:], AF.Exp)

                            def pv_fn(po=po, pim=pim, ptc=ptc, nt=nt, last=last):
                                P.mm(po[0:65, :], vcA[:, nt, :], ptc[:], start=False, stop=last)
                                for m in range(4):
                                    P.mm(pim[:, m, :], ptc[:, m * 128:(m + 1) * 128], ovaug[:, nt, :], start=False, stop=last)

                            _push_block(cx, s_fn, exp_fn, pv_fn, first=first)

                        def fin(po_, F, hh=hh, q0=q0, pim=pim, rinv=rinv, j=j):
                            for m in range(4):
                                tq = j * 4 + m
                                if hh == 0:
                                    P.ts('dve', imp[:, tq, :], pim[:, m, 0:64], rinv[:, m:m + 1], None, ALU.mult)
                                else:
                                    P.stt('dve', imp[:, tq, :], pim[:, m, 0:64], rinv[:, m:m + 1], imp[:, tq, :], ALU.mult, ALU.add)
                            P.tt('dve', acc[hh][:, q0:q0 + 512], po_[0:64, :], F[:], ALU.mult)

                        ea, eb = _mk_factor(cx, po, lng2, h, q0, fin)

                        def ea2(ea=ea, pim=pim, rinv=rinv):
                            ea()
                            P.ts('dve', rinv[:, 0:4].unsqueeze(2), pim[:, :, 64:65], 1e-30, None, ALU.max)
                            P.op('dve', lambda e, o=rinv[:, 0:4]: e.reciprocal(out=o, in_=o), reads=[rinv[:, 0:4]], writes=[rinv[:, 0:4]])

                        _end_chunk(cx, ea2, eb)
                attn_flush(cx)
            if STAGE < 2:
                continue
            with ExitStack() as s2:
                wk = [P.sb(s2, "n_wk%d" % i, [128, 64], F32) for i in range(2)]
                w2_ = [P.sb(s2, "n_wk2%d" % i, [128, 64], F32) for i in range(2)]
                m8 = [P.sb(s2, "n_m8%d" % i, [128, 16], F32) for i in range(2)]
                sb_ = [P.sb(s2, "n_sb%d" % i, [128, 128], BF16) for i in range(2)]
                pts = [P.ps(s2, "n_pts%d" % i, [128, 128], BF16) for i in range(2)]
                P.memset('pool', sb_[0][:], 0.0)
                P.memset('pool', sb_[1][:], 0.0)
                for tq in range(32):
                    b = tq % 2
                    P.tt('dve', wk[b][:], imp[:, tq, :], addm[:, tq, :], ALU.add)
                    P.op('dve', lambda e, o=m8[b][:, 0:8], i=wk[b][:]: e.max(out=o, in_=i), reads=[wk[b][:]], writes=[m8[b][:, 0:8]])
                    P.op('dve', lambda e, o=w2_[b][:], r=m8[b][:, 0:8], i=wk[b][:]: e.match_replace(out=o, in_to_replace=r, in_values=i, imm_value=-3.0e38),
                         reads=[m8[b][:, 0:8], wk[b][:]], writes=[w2_[b][:]])
                    P.op('dve', lambda e, o=m8[b][:, 8:16], i=w2_[b][:]: e.max(out=o, in_=i), reads=[w2_[b][:]], writes=[m8[b][:, 8:16]])
                    P.ts('dve', w2_[b][:], wk[b][:], m8[b][:, 15:16], None, ALU.is_ge)
                    P.ts('dve', wk[b][:], wk[b][:], -5.0e29, None, ALU.is_gt)
                    P.tt('dve', wk[b][:], wk[b][:], w2_[b][:], ALU.mult)
                    P.ts('dve', sb_[b][:, 64:128], wk[b][:], -1.0, -NEG, ALU.add, ALU.mult)
                    P.transpose(pts[b][:], sb_[b][:], ident[:])
                    P.copy('act', selT[64:128, tq * 128:(tq + 1) * 128], pts[b][64:128, :])
            for hh in range(4):
                P.dma(Q[hh][64:128, :], selT[64:128, :], 'nsq%d' % hh, q='sp' if hh % 2 else 'act')
            if STAGE < 3:
                continue
            with ExitStack() as s2:
                tmp = [P.sb(s2, "n_tmp%d" % i, [64, 512], F32) for i in range(2)]
                ob = [P.sb(s2, "n_ob%d" % i, [64, 512], BF16) for i in range(1)] * 2
                for hh in range(4):
                    h = 4 * g + hh
                    for j in range(8):
                        q0 = j * 512
                        a = acc[hh][:, q0:q0 + 512]

                        def fin_s(po_, F, a=a):
                            P.tt('dve', tmp[0][:], po_[0:64, :], F[:], ALU.mult)
                            P.tt('pool', a, a, tmp[0][:], ALU.add)

                        def fin_w(po_, F, a=a, j=j, h=h, q0=q0):
                            P.tt('dve', tmp[1][:], po_[0:64, :], F[:], ALU.mult)
                            P.tt('pool', ob[j % 2][:], a, tmp[1][:], ALU.add)
                            if STAGE >= 5:
                                P.dma(A['mixT'][256 + 64 * h:256 + 64 * (h + 1), q0:q0 + 512], ob[j % 2][:], 'nso%d' % (j % 2), q='sp')

                        attn_chunk(cx, Ks, 128, Q[hh], j, Vs, causal_entries(j, c['mc']), fin_s, lng2=lng2, gate_c=8 + h)
                        if STAGE >= 4:
                            attn_chunk(cx, Kw, 128, Q[hh], j, Vw, window_entries(j, c['mc'], c['mu']), fin_w, lng2=lng2, gate_c=16 + h)
                attn_flush(cx)


T = 4096
D = 1024
FF = 4096


def phase_wo(P, A, layer, consts, x_in, x_mid, per_tile=None):
    ident = consts['ident']
    with ExitStack() as st:
        Wo = P.sb(st, "wo", [128, 8, D], BF16)
        with ExitStack() as s2:
            wst = [P.sb(s2, "wost%d" % i, [128, D], F32) for i in range(2)]
            for kc in range(8):
                b = wst[kc % 2]
                P.dma(b[:], A['wo'][layer, kc * 128:(kc + 1) * 128, :], 'wost%d' % (kc % 2))
                P.copy('act' if kc % 2 else 'dve', Wo[:, kc, :], b[:])
        mx = [P.sb(st, "wo_mx%d" % i, [128, 8, 512], BF16) for i in range(2)]
        xt = [P.sb(st, "wo_xt%d" % i, [128, D], F32) for i in range(2)]
        xm = [P.sb(st, "wo_xm%d" % i, [128, D], F32) for i in range(2)]
        sq = P.sb(st, "wo_sq", [128, D], BF16)
        hb = [P.sb(st, "wo_hb%d" % i, [128, D], BF16) for i in range(2)]
        ss = [P.sb(st, "wo_ss%d" % i, [128, 2], F32) for i in range(2)]
        hst = [P.sb(st, "wo_hst%d" % i, [128, 8, 512], BF16) for i in range(1)] * 2
        po = [P.ps(st, "wo_po%d" % i, [128, 512], F32) for i in range(4)]
        ptr = [P.ps(st, "wo_ptr%d" % i, [128, 8, 128], BF16) for i in range(2)]
        for t in range(32):
            j = t // 4
            b = t % 2
            if t % 4 == 0:
                P.dma(mx[j % 2][:], A['mixT'][:, j * 512:(j + 1) * 512].rearrange("(a p) n -> p a n", p=128), 'womx%d' % (j % 2))
            P.dma(xt[b][:], x_in[t * 128:(t + 1) * 128, :], 'woxt%d' % b, q='act')
            for half in range(2):
                pp = po[(t % 2) * 2 + half]
                for kc in range(8):
                    P.mm(pp[:], mx[j % 2][:, kc, (t % 4) * 128:(t % 4 + 1) * 128], Wo[:, kc, half * 512:(half + 1) * 512],
                         start=(kc == 0), stop=(kc == 7))
                P.tt('dve', xm[b][:, half * 512:(half + 1) * 512], pp[:], xt[b][:, half * 512:(half + 1) * 512], ALU.add)
            P.dma(x_mid[t * 128:(t + 1) * 128, :], xm[b][:], 'woxm%d' % b, q='pool')
            P.act(sq[:], xm[b][:], AF.Square, accum_out=ss[b][:, 0:1])
            P.act(ss[b][:, 1:2], ss[b][:, 0:1], AF.Sqrt, bias=1e-6, scale=1.0 / D)
            P.op('dve', lambda e, o=ss[b][:, 1:2]: e.reciprocal(out=o, in_=o), reads=[ss[b][:, 1:2]], writes=[ss[b][:, 1:2]])
            P.ts('dve', hb[b][:], xm[b][:], ss[b][:, 1:2], None, ALU.mult)
            for kc in range(8):
                P.transpose(ptr[b][:, kc, :], hb[b][:, kc * 128:(kc + 1) * 128], ident[:])
            P.copy('act', hst[j % 2][:, :, (t % 4) * 128:(t % 4 + 1) * 128], ptr[b][:])
            if t % 4 == 3:
                P.dma(A['h2T'][:, j * 512:(j + 1) * 512].rearrange("(a p) n -> p a n", p=128), hst[j % 2][:], 'wohst%d' % (j % 2), q='pool')
            if per_tile is not None:
                per_tile(t)


def phase_wo_ffn(P, A, layer, consts, x_in, x_mid, x_out):
    with ExitStack() as st:
        Wu = P.sb(st, "wu", [128, 8, FF], BF16)
        Wd = P.sb(st, "wd", [128, 32, D], BF16)
        g2 = P.sb(st, "g2", [128, 8], F32)
        P.dma(g2[:], A['g2'][layer], 'ff0')
        with ExitStack() as s2:
            wst = [P.sb(s2, "fwst%d" % i, [128, 1024], F32) for i in range(2)]

            def per_tile(t):
                for u in range(2):
                    ci = 2 * t + u
                    b = u
                    if ci < 32:
                        kc, qt = ci // 4, ci % 4
                        P.dma(wst[b][:], A['wup'][layer, kc * 128:(kc + 1) * 128, qt * 1024:(qt + 1) * 1024], 'fwst%d' % b, q='sp')
                        if u:
                            P.ts('dve', Wu[:, kc, qt * 1024:(qt + 1) * 1024], wst[b][:], g2[:, kc:kc + 1], None, ALU.mult)
                        else:
                            P.act(Wu[:, kc, qt * 1024:(qt + 1) * 1024], wst[b][:], AF.Copy, scale=g2[:, kc:kc + 1])
                    else:
                        fc = ci - 32
                        P.dma(wst[b][:], A['wdn'][layer, fc * 128:(fc + 1) * 128, :], 'fwst%d' % b, q='sp')
                        P.copy('dve' if u else 'act', Wd[:, fc, :], wst[b][:])

            phase_wo(P, A, layer, consts, x_in, x_mid, per_tile=per_tile)
        _ffn_body(P, st, A, Wu, Wd, x_mid, x_out)


def phase_ffn(P, A, layer, consts, x_mid, x_out):
    with ExitStack() as st:
        Wu = P.sb(st, "wu", [128, 8, FF], BF16)
        Wd = P.sb(st, "wd", [128, 32, D], BF16)
        g2 = P.sb(st, "g2", [128, 8], F32)
        P.dma(g2[:], A['g2'][layer], 'ff0')
        with ExitStack() as s2:
            wst = [P.sb(s2, "fwst%d" % i, [128, 2048], F32) for i in range(3)]
            k = 0
            for kc in range(8):
                for hf in range(2):
                    b = k % 3
                    P.dma(wst[b][:], A['wup'][layer, kc * 128:(kc + 1) * 128, hf * 2048:(hf + 1) * 2048], 'fwst%d' % b, q='sp' if k % 2 else 'act')
                    if k % 2:
                        P.ts('dve', Wu[:, kc, hf * 2048:(hf + 1) * 2048], wst[b][:], g2[:, kc:kc + 1], None, ALU.mult)
                    else:
                        P.act(Wu[:, kc, hf * 2048:(hf + 1) * 2048], wst[b][:], AF.Copy, scale=g2[:, kc:kc + 1])
                    k += 1
            for fc2 in range(16):
                b = k % 3
                P.dma(wst[b][:].rearrange("p (a n) -> p a n", a=2), A['wdn'][layer, fc2 * 256:(fc2 + 1) * 256, :].rearrange("(a p) n -> p a n", p=128),
                      'fwst%d' % b, q='sp' if k % 2 else 'act')
                P.copy('dve' if k % 2 else 'act', Wd[:, fc2 * 2:(fc2 + 1) * 2, :], wst[b][:].rearrange("p (a n) -> p a n", a=2))
                k += 1
        _ffn_body(P, st, A, Wu, Wd, x_mid, x_out)


def _ffn_body(P, st, A, Wu, Wd, x_mid, x_out):
    if True:
        h2 = [P.sb(st, "ff_h2%d" % i, [128, 8, 512], BF16) for i in range(2)]
        uT = P.sb(st, "ff_uT", [128, 32, 512], BF16)
        rl = [P.sb(st, "ff_rl%d" % i, [128, 512], F32) for i in range(2)]
        xt = [P.sb(st, "ff_xt%d" % i, [128, D], F32) for i in range(2)]
        xo = [P.sb(st, "ff_xo%d" % i, [128, D], F32) for i in range(2)]
        pu = [P.ps(st, "ff_pu%d" % i, [128, 512], F32) for i in range(3)]
        pd = [P.ps(st, "ff_pd%d" % i, [128, 512], F32) for i in range(4)]
        ku = 0
        for j in range(8):
            P.dma(h2[j % 2][:], A['h2T'][:, j * 512:(j + 1) * 512].rearrange("(a p) n -> p a n", p=128), 'ffh2%d' % (j % 2))
            for fc in range(32):
                pp = pu[ku % 3]
                r = rl[ku % 2]
                for kc in range(8):
                    P.mm(pp[:], Wu[:, kc, fc * 128:(fc + 1) * 128], h2[j % 2][:, kc, :], start=(kc == 0), stop=(kc == 7))
                P.act(r[:], pp[:], AF.Relu)
                P.tt('dve' if ku % 2 else 'pool', uT[:, fc, :], r[:], r[:], ALU.mult)
                ku += 1
            for tt in range(4):
                t = j * 4 + tt
                b = t % 2
                P.dma(xt[b][:], x_mid[t * 128:(t + 1) * 128, :], 'ffxt%d' % b, q='act')
                for half in range(2):
                    pp = pd[(t % 2) * 2 + half]
                    for fc in range(32):
                        P.mm(pp[:], uT[:, fc, tt * 128:(tt + 1) * 128], Wd[:, fc, half * 512:(half + 1) * 512], start=(fc == 0), stop=(fc == 31))
                    P.tt('dve', xo[b][:, half * 512:(half + 1) * 512], pp[:], xt[b][:, half * 512:(half + 1) * 512], ALU.add)
                P.dma(x_out[t * 128:(t + 1) * 128, :], xo[b][:], 'ffxo%d' % b, q='pool')

import ml_dtypes
from concourse.bass_utils import run_bass_kernel_spmd

T=4096; D=1024
OFF = {}
_names = ['aq','af','ai','ag','bq','bkc','bvc','bks','bvs','bkw','bvw','bg','cq','ck','cv','cf']
_sizes = [256,256,256,256,512,128,128,128,128,128,128,24,256,256,256,4]
_o = 0
for n_, s_ in zip(_names, _sizes):
    OFF[n_] = (_o, _o + s_); _o += s_
TOK_ORDER = ['bq','bks','bkw','cq','ck','ai','bvs','bvw','cv']
T_ORDER = ['aq','af','ag','bkc','bvc','bg','cf']

def win_layout(w_in):
    L = w_in.shape[0]
    out = np.zeros((L, 1024, 2048 + 1152), np.float32)
    c = 0
    for n_ in TOK_ORDER:
        a, b = OFF[n_]; out[:, :, c:c + b - a] = w_in[:, :, a:b]; c += b - a
    assert c == 2048
    for n_ in T_ORDER:
        a, b = OFF[n_]; out[:, :, c:c + b - a] = w_in[:, :, a:b]; c += b - a
    return out

def rope_tables():
    inv = np.power(np.float32(500000.0), -np.arange(0, 16, 2, dtype=np.float32) / 16).astype(np.float32)
    pos = np.arange(T, dtype=np.float32)
    ang = pos[:, None] * inv[None, :]
    cos = np.cos(ang).astype(np.float32); sin = np.sin(ang).astype(np.float32)
    return (np.ascontiguousarray(cos.reshape(32, 128, 8).transpose(1, 0, 2)),
            np.ascontiguousarray(sin.reshape(32, 128, 8).transpose(1, 0, 2)))

def _skip():
    pass

def const_inputs():
    k = np.arange(128)[:, None]; q = np.arange(128)[None, :]
    mc = np.where(k <= q, 0.0, -30000.0).astype(ml_dtypes.bfloat16)
    mu = np.where(k > q, 0.0, -30000.0).astype(ml_dtypes.bfloat16)
    selneg = np.zeros((24, 24 * 64), np.float32)
    for c in range(24):
        selneg[c, c * 64:(c + 1) * 64] = -1.0
    return dict(ident=np.eye(128, dtype=ml_dtypes.bfloat16), mc=mc, mu=mu, selneg=selneg)

def _unused_ref_proj(inp, layer, x):
    x = x.astype(np.float64)
    h = x / np.sqrt((x * x).mean(-1, keepdims=True) + 1e-6) * inp['norm1_g'][layer]
    return h @ inp['w_in'][layer].astype(np.float64)

def hgrn_consts(inp):
    s = np.arange(128)[:, None]; t = np.arange(128)[None, :]
    mh = ((s // 64 == t // 64) & (s <= t)).astype(ml_dtypes.bfloat16)
    bones = (s // 64 == t // 64).astype(ml_dtypes.bfloat16)
    lbl = np.ascontiguousarray(inp['hgrn_lb_logits'].reshape(2, 2, 128).transpose(1, 2, 0)).astype(np.float32)
    og = np.tile(inp['hgrn_onorm_g'], (1, 2)).reshape(2, 128, 1).astype(np.float32)
    return dict(mh=mh, bones=bones, lbl=lbl, og=og)

def _unused_ref_hgrn(inp, layer, proj):
    def sl(n): a, b = OFF[n]; return proj[:, a:b]
    lbp = np.exp(inp['hgrn_lb_logits'].astype(np.float64)); lbp /= lbp.sum(0, keepdims=True)
    lb_all = np.cumsum(lbp, 0) - lbp[0:1]
    lb = lb_all[layer].reshape(4, 64)
    z = sl('af').reshape(T, 4, 64)
    sig = 1 / (1 + np.exp(-z))
    f = lb + (1 - lb) * sig; logf = np.log(f); k = (1 - lb) * (1 - sig)
    q = sl('aq').reshape(T, 4, 64) * 0.125; v = sl('ai').reshape(T, 4, 64)
    o = np.zeros((T, 4, 64))
    for h in range(4):
        S = np.zeros((64, 64))
        for c in range(64):
            r = slice(c * 64, (c + 1) * 64)
            G = np.cumsum(logf[r, h], 0)
            qc, kc, vc = q[r, h], k[r, h], v[r, h]
            o_inter = (qc * np.exp(G)) @ S
            diff = G[:, None, :] - G[None, :, :]
            mask = np.tril(np.ones((64, 64), bool))
            dec = np.where(mask[:, :, None], np.exp(np.minimum(diff, 0)), 0)
            sc = np.einsum('tk,sk,tsk->ts', qc, kc, dec)
            o[r, h] = o_inter + sc @ vc
            S = S * np.exp(G[-1])[:, None] + (kc * np.exp(G[-1] - G)).T @ vc
    g = sl('ag').reshape(T, 4, 64)
    gate = g / (1 + np.exp(-g))
    on = o / np.sqrt((o * o).mean(-1, keepdims=True) + 1e-6) * inp['hgrn_onorm_g'][layer]
    return (on * gate).reshape(T, 256)

def nsa_consts(inp):
    n_cmp = 255
    ci = np.arange(n_cmp)[:, None]; sj = np.arange(64)[None, :]
    ov = ((ci * 16 <= sj * 64 + 63) & (ci * 16 + 31 >= sj * 64)).astype(np.float32)
    ovaug = np.zeros((256, 72), np.float32); ovaug[:255, :64] = ov; ovaug[:255, 64] = 1.0
    ovaug = np.ascontiguousarray(ovaug.reshape(2, 128, 72).transpose(1, 0, 2)).astype(ml_dtypes.bfloat16)
    nl = np.arange(128)[:, None]; cc = np.arange(3200)[None, :] - 511
    wc = np.where(cc >= 16 * nl, 0.0, -30000.0).astype(ml_dtypes.bfloat16)
    eall = (np.arange(T)[None, :] // 64 == np.arange(64)[:, None]).astype(ml_dtypes.bfloat16)
    q = np.arange(T)[:, None]; j = np.arange(64)[None, :]; cur = q // 64
    am = np.zeros((T, 64), np.float32)
    am[(j == 0) | (j == cur) | (j == cur - 1)] = 1e30
    am[np.broadcast_to(j > cur, am.shape)] = -1e30
    addmask = np.ascontiguousarray(am.reshape(32, 128, 64).transpose(1, 0, 2))
    inv = np.power(np.float32(500000.0), -np.arange(0, 16, 2, dtype=np.float32) / 16).astype(np.float32)
    pos = (np.arange(256, dtype=np.float32) * 16 + 31)
    ang = pos[:, None] * inv[None, :]
    cosc = np.ascontiguousarray(np.cos(ang).astype(np.float32).reshape(2, 128, 8).transpose(1, 0, 2))
    sinc = np.ascontiguousarray(np.sin(ang).astype(np.float32).reshape(2, 128, 8).transpose(1, 0, 2))
    w1r = np.ascontiguousarray(inp['nsa_cmp_w1'].reshape(2, 2, 32, 64, 128).transpose(0, 1, 3, 2, 4)).astype(np.float32)
    posT = np.ascontiguousarray(inp['nsa_cmp_pos'].transpose(0, 1, 3, 2)).astype(np.float32)
    return dict(ovaug=ovaug, wc=wc, eall=eall, addmask=addmask, cosc=cosc, sinc=sinc, w1r=w1r, posT=posT,
                w2=inp['nsa_cmp_w2'].astype(np.float32), kng=inp['nsa_kn_g'].astype(np.float32))

NSA_SHAPES = [('ovaug', [128, 2, 72], BF16), ('wc', [128, 3200], BF16), ('eall', [64, 4096], BF16), ('addmask', [128, 32, 64], F32),
              ('cosc', [128, 2, 8], F32), ('sinc', [128, 2, 8], F32), ('w1r', [2, 2, 64, 32, 128], F32), ('posT', [2, 2, 64, 32], F32),
              ('w2', [2, 2, 128, 64], F32), ('kng', [2, 64], F32)]


import ml_dtypes
from concourse.bass_utils import run_bass_kernel_spmd

_IN_SHAPES = [('x', [T, D], F32), ('win', [2, D, WCOLS], F32), ('g1', [2, 128, 8], F32), ('gq', [2, 1280], F32),
              ('cos', [128, 32, 8], F32), ('sin', [128, 32, 8], F32), ('ident', [128, 128], BF16), ('mc', [128, 128], BF16),
              ('mu', [128, 128], BF16), ('selneg', [24, 1536], F32), ('mh', [128, 128], BF16), ('bones', [128, 128], BF16),
              ('lbl', [2, 128, 2], F32), ('og', [2, 128, 1], F32), ('fb', [2, 4, 1], F32), ('wo', [2, 1024, 1024], F32),
              ('wup', [2, 1024, 4096], F32), ('wdn', [2, 4096, 1024], F32), ('g2', [2, 128, 8], F32)] + NSA_SHAPES

KDEPTH = int(os.environ.get('KDEPTH', '2'))
KPHASES = os.environ.get('KPHASES', '1hnfwf')


def _body(P):
    nc = P.nc
    A = {}
    for k_, shp, dt_ in _IN_SHAPES:
        A[k_] = nc.dram_tensor(k_, shp, dt_, kind="ExternalInput").ap()
    A['y'] = nc.dram_tensor("y", [T, D], F32, kind="ExternalOutput").ap()
    A['qkT'] = nc.dram_tensor("qkT", [1280, T], BF16).ap()
    A['vtok'] = nc.dram_tensor("vtok", [T, 768], BF16).ap()
    A['pT'] = nc.dram_tensor("pT", [TC, T], F32).ap()
    A['mixT'] = nc.dram_tensor("mixT", [1024, T], BF16).ap()
    A['h2T'] = nc.dram_tensor("h2T", [1024, T], BF16).ap()
    xm = nc.dram_tensor("xmid", [T, D], F32).ap()
    x1 = nc.dram_tensor("x1", [T, D], F32).ap()
    xin = A['x']
    for layer in range(KDEPTH):
        A['x'] = xin
        with ExitStack() as st:
            phase1(P, st, A, layer)
        with ExitStack() as st:
            consts = load_consts(P, st, A)
            if 'h' in KPHASES:
                phase_hgrn(P, A, layer, consts)
            if 'n' in KPHASES:
                phase_nsa(P, A, layer, consts)
            if 'f' in KPHASES:
                phase_fox(P, A, layer, consts)
            xout = x1 if layer < KDEPTH - 1 else A['y']
            phase_wo_ffn(P, A, layer, consts, xin, xm, xout)
        xin = xout


def _host_inputs(inp):
    cos, sin = rope_tables()
    gq = np.concatenate([np.tile(inp['nsa_qn_g'], (1, 8)), np.tile(inp['nsa_kn_g'], (1, 4)), np.tile(inp['fox_qn_g'], (1, 4)),
                         np.tile(inp['fox_kn_g'], (1, 4))], axis=1).astype(np.float32)
    base = {"win": win_layout(inp['w_in']), "g1": np.ascontiguousarray(inp['norm1_g'].reshape(2, 8, 128).transpose(0, 2, 1)),
            "g2": np.ascontiguousarray(inp['norm2_g'].reshape(2, 8, 128).transpose(0, 2, 1)),
            "gq": gq, "cos": cos, "sin": sin, "fb": inp['fox_fb'].reshape(2, 4, 1).astype(np.float32),
            "wo": inp['w_o'], "wup": inp['w_up'], "wdn": inp['w_down']}
    base.update(const_inputs()); base.update(hgrn_consts(inp)); base.update(nsa_consts(inp))
    return base


def kernel(**inp):
    inp = {k: np.asarray(v) for k, v in inp.items()}
    nc, plan = build_two_pass(lambda: bass.Bass("TRN2", target_bir_lowering=False), _body)
    base = _host_inputs(inp)
    in_maps = []
    for b in range(8):
        m = dict(base); m['x'] = np.ascontiguousarray(inp['x'][b]); in_maps.append(m)
    res = run_bass_kernel_spmd(nc, in_maps, core_ids=list(range(8)))
    return np.stack([r['y'] for r in res.results], axis=0).astype(np.float32)
```

```python
import numpy as np, sys, time, os, math
import numpy as np
from contextlib import ExitStack
import concourse.bass as bass
import concourse.mybir as mybir

F32 = mybir.dt.float32
BF16 = mybir.dt.bfloat16
AF = mybir.ActivationFunctionType
ALU = mybir.AluOpType
AX = mybir.AxisListType


def _box(ap):
    t = ap.tensor
    dims = ap.ap
    off = int(ap.offset)
    shp = tuple(t.shape)
    rowsize = 1
    for s in shp[1:]:
        rowsize *= int(s)
    r0 = off // rowsize
    f0 = off % rowsize
    rows = 0
    free = 0
    for (st, cnt) in dims:
        st = int(st); cnt = int(cnt)
        if cnt <= 1 or st == 0:
            continue
        if st % rowsize == 0:
            rows += (st // rowsize) * (cnt - 1)
        else:
            free += st * (cnt - 1)
    return t.name, (r0, r0 + rows, f0, f0 + free)


def _ov(a, b):
    return a[0] <= b[1] and b[0] <= a[1] and a[2] <= b[3] and b[2] <= a[3]


def _cont(a, b):
    return a[0] <= b[0] and b[1] <= a[1] and a[2] <= b[2] and b[3] <= a[3]


class Prog:
    def __init__(self, nc, plan=None):
        self.nc = nc
        self.plan = plan
        self.rec = plan is None
        self.eng = dict(pe=nc.tensor, dve=nc.vector, act=nc.scalar, pool=nc.gpsimd, sp=nc.sync)
        self.n = 0
        self.ins = []
        self.track = {}
        self.lane_cnt = {}
        self.freed = {}
        self.uid = 0
        self.stack = ExitStack()
        self.psum_rr = 0
        self.psum_banks = []
        if not self.rec:
            self.sem = {}
            for e in ['pe', 'dve', 'act', 'pool']:
                self.sem[e] = self.stack.enter_context(nc.semaphore("sem_" + e))
            self.lane_sem = {}
            for ln in plan['lanes']:
                self.lane_sem[ln] = self.stack.enter_context(nc.semaphore("ln_" + ln))

    def sb(self, st, name, shape, dtype):
        self.uid += 1
        name = "%s_%d" % (name, self.uid)
        t = st.enter_context(self.nc.sbuf_tensor("s_" + name, list(shape), dtype))
        st.callback(self._free, "s_" + name)
        return t

    def ps(self, st, name, shape, dtype=F32):
        self.uid += 1
        name = "%s_%d" % (name, self.uid)
        t = st.enter_context(self.nc.psum_tensor("p_" + name, list(shape), dtype))
        st.callback(self._free, "p_" + name)
        return t

    def _free(self, name):
        if not self.rec:
            return
        recs = self.track.pop(name, [])
        for (b, i, w) in recs:
            r = self.ins[i]
            key = ('l', r['lane'], i) if r['dma'] else ('e', r['eng'])
            if r['dma']:
                self.freed[key] = i
            else:
                self.freed[key] = max(self.freed.get(key, -1), i)

    def _access(self, idx, eng, dma, ap, write, deps):
        name, box = _box(ap)
        if name not in self.track:
            big = (0, 10 ** 9, 0, 10 ** 9)
            kind = ap.space
            self.track[name] = [] if str(kind) == 'DRAM' else [(big, i, True) for i in sorted(set(self.freed.values()))]
        recs = self.track[name]
        for (b, i, w) in recs:
            if (write or w) and _ov(b, box):
                deps.append((i, (w and not write)))
        if write:
            recs[:] = [r for r in recs if not _cont(box, r[0])]
        elif not dma:
            recs[:] = [r for r in recs if not ((not r[2]) and r[1] < len(self.ins) and self.ins[r[1]]['eng'] == eng
                                               and not self.ins[r[1]]['dma'] and _cont(box, r[0]))]
        recs.append((box, idx, write))

    def op(self, eng, fn, reads=(), writes=(), dma=False, lane=None):
        idx = self.n
        self.n += 1
        if self.rec:
            deps = []
            for ap in reads:
                self._access(idx, eng, dma, ap, False, deps)
            for ap in writes:
                self._access(idx, eng, dma, ap, True, deps)
            lanewaits = {}
            d2 = {}
            for (j, raw) in deps:
                if j == idx:
                    continue
                pj = self.ins[j]
                if pj['dma']:
                    ln = pj['lane']
                    lanewaits[ln] = max(lanewaits.get(ln, 0), pj['lane_val_at'])
                    lanewaits[ln] = max(lanewaits[ln], self.lane_cnt[ln])
                    continue
                if pj['eng'] == eng and not dma:
                    if eng == 'pe':
                        continue
                    if not raw and eng != 'pool':
                        continue
                d2[j] = True
            rec = dict(eng=eng, deps=list(d2.keys()), lanewaits=lanewaits, dma=dma, lane=lane)
            if dma:
                self.lane_cnt[lane] = self.lane_cnt.get(lane, 0) + 16
                rec['lane_val_at'] = self.lane_cnt[lane]
            self.ins.append(rec)
            return None
        else:
            info = self.plan['ins'][idx]
            e = self.eng[eng]
            for (sname, val) in info['waits']:
                s = self.sem[sname[1]] if sname[0] == 'e' else self.lane_sem[sname[1]]
                e.wait_ge(s, val)
            inst = fn(e)
            if dma:
                inst.then_inc(self.lane_sem[lane], 16)
            elif info['signal']:
                inst.then_inc(self.sem[eng], 1)
            return inst

    def make_plan(self):
        ins = self.ins
        signal = [False] * len(ins)
        for r in ins:
            for j in r['deps']:
                signal[j] = True
        cnt = dict(pe=0, dve=0, act=0, pool=0, sp=0)
        sigval = [0] * len(ins)
        for i, r in enumerate(ins):
            if signal[i] and not r['dma']:
                cnt[r['eng']] += 1
                sigval[i] = cnt[r['eng']]
        seen = {e: {} for e in cnt}
        out = []
        for i, r in enumerate(ins):
            need = {}
            for j in r['deps']:
                k = ('e', ins[j]['eng'])
                need[k] = max(need.get(k, 0), sigval[j])
            for ln, v in r['lanewaits'].items():
                k = ('l', ln)
                need[k] = max(need.get(k, 0), v)
            waits = []
            sd = seen[r['eng']]
            for k, v in need.items():
                if sd.get(k, 0) >= v:
                    continue
                sd[k] = v
                waits.append((k, v))
            out.append(dict(waits=waits, signal=signal[i]))
        return dict(ins=out, lanes=sorted(self.lane_cnt.keys()), lane_final=dict(self.lane_cnt))

    def finish(self):
        if self.rec:
            return
        for ln, v in self.plan['lane_final'].items():
            self.nc.sync.wait_ge(self.lane_sem[ln], v)

    def dma(self, out, in_, lane, q='sp', **kw):
        return self.op(q, lambda e: e.dma_start(out=out, in_=in_, **kw), reads=[in_], writes=[out],
                       dma=True, lane=lane)

    def mm(self, out, lhsT, rhs, start=True, stop=True, **kw):
        return self.op('pe', lambda e: e.matmul(out, lhsT, rhs, start=start, stop=stop, **kw),
                       reads=[lhsT, rhs], writes=[out])

    def transpose(self, out, in_, ident):
        return self.op('pe', lambda e: e.transpose(out, in_, ident), reads=[in_, ident], writes=[out])

    def act(self, out, in_, func, bias=None, scale=None, accum_out=None, eng='act'):
        reads = [in_]
        kw = {}
        if bias is not None:
            kw['bias'] = bias
            if not isinstance(bias, (int, float)):
                reads.append(bias)
        if scale is not None:
            kw['scale'] = scale
            if not isinstance(scale, (int, float)):
                reads.append(scale)
        writes = [out]
        if accum_out is not None:
            kw['accum_out'] = accum_out
            writes.append(accum_out)
        return self.op(eng, lambda e: e.activation(out=out, in_=in_, func=func, **kw), reads=reads, writes=writes)

    def tt(self, eng, out, in0, in1, op):
        return self.op(eng, lambda e: e.tensor_tensor(out=out, in0=in0, in1=in1, op=op), reads=[in0, in1], writes=[out])

    def ts(self, eng, out, in0, s1, s2, op0, op1=None, accum_out=None):
        reads = [in0]
        if not isinstance(s1, (int, float)):
            reads.append(s1)
        if s2 is not None and not isinstance(s2, (int, float)):
            reads.append(s2)
        kw = {}
        writes = [out]
        if op1 is not None:
            kw['op1'] = op1
        if accum_out is not None:
            kw['accum_out'] = accum_out
            writes.append(accum_out)
        return self.op(eng, lambda e: e.tensor_scalar(out=out, in0=in0, scalar1=s1, scalar2=s2, op0=op0, **kw),
                       reads=reads, writes=writes)

    def stt(self, eng, out, in0, scalar, in1, op0, op1):
        reads = [in0, in1]
        if not isinstance(scalar, (int, float)):
            reads.append(scalar)
        return self.op(eng, lambda e: e.scalar_tensor_tensor(out=out, in0=in0, scalar=scalar, in1=in1, op0=op0, op1=op1),
                       reads=reads, writes=[out])

    def copy(self, eng, out, in_):
        if eng == 'act':
            return self.op(eng, lambda e: e.copy(out=out, in_=in_), reads=[in_], writes=[out])
        return self.op(eng, lambda e: e.tensor_copy(out=out, in_=in_), reads=[in_], writes=[out])

    def memset(self, eng, ap, val):
        return self.op(eng, lambda e: e.memset(ap, val), reads=[], writes=[ap])

    def scan(self, out, d0, d1, initial, op0, op1):
        reads = [d0, d1]
        if not isinstance(initial, (int, float)):
            reads.append(initial)
        return self.op('dve', lambda e: e.tensor_tensor_scan(out=out, data0=d0, data1=d1, initial=initial, op0=op0, op1=op1),
                       reads=reads, writes=[out])

    def generic(self, eng, fn, reads, writes):
        return self.op(eng, fn, reads=reads, writes=writes)


def build_two_pass(make_nc, body):
    nc1 = make_nc()
    p1 = Prog(nc1, None)
    body(p1)
    p1.stack.close()
    plan = p1.make_plan()
    nc2 = make_nc()
    p2 = Prog(nc2, plan)
    body(p2)
    p2.finish()
    p2.stack.close()
    return nc2, plan


T = 4096
NT = 32
D = 1024
KC = 8
TOKC = 2048
TC = 1152
WCOLS = TOKC + TC
EPS = 1e-6


def phase1(P, st, A, layer):
    nc = P.nc
    s = ExitStack()
    W = P.sb(s, "w_in", [128, KC, WCOLS], BF16)
    hT = P.sb(s, "hT", [128, KC, T], BF16)
    ident = P.sb(s, "ident", [128, 128], BF16)
    g1 = P.sb(s, "g1", [128, KC], F32)
    G = P.sb(s, "Gq", [128, 1280], F32)
    cos = P.sb(s, "cos", [128, NT, 8], F32)
    sin = P.sb(s, "sin", [128, NT, 8], F32)
    P.dma(ident[:], A['ident'], 'c0')
    P.dma(g1[:], A['g1'][layer], 'c0')
    P.dma(G[:], A['gq'][layer].partition_broadcast(128), 'c0')
    P.dma(cos[:], A['cos'], 'c0')
    P.dma(sin[:], A['sin'], 'c0')
    P.ts('dve', G[:, 0:512], G[:, 0:512], 0.125, None, ALU.mult)
    P.ts('dve', G[:, 768:1024], G[:, 768:1024], 0.125, None, ALU.mult)

    with ExitStack() as s2:
        wst = [P.sb(s2, "wst%d" % i, [128, WCOLS], F32) for i in range(4)]
        for kc in range(KC):
            b = wst[kc % 4]
            P.dma(b[:], A['win'][layer, kc * 128:(kc + 1) * 128, :], 'wst%d' % (kc % 4), q='sp' if kc % 2 == 0 else 'act')
            half = WCOLS // 2
            P.ts('dve', W[:, kc, 0:half], b[:, 0:half], g1[:, kc:kc + 1], None, ALU.mult)
            P.act(W[:, kc, half:WCOLS], b[:, half:WCOLS], AF.Copy, scale=g1[:, kc:kc + 1])

    with ExitStack() as s2:
        xt = [P.sb(s2, "xt%d" % i, [128, D], F32) for i in range(2)]
        sq = P.sb(s2, "sqj", [128, D], F32)
        hb = [P.sb(s2, "hb%d" % i, [128, D], BF16) for i in range(2)]
        ss = [P.sb(s2, "ss%d" % i, [128, 2], F32) for i in range(2)]
        ptr = [P.ps(s2, "ptr%d" % i, [128, KC, 128], BF16) for i in range(2)]
        for t in range(NT):
            b = t % 2
            P.dma(xt[b][:], A['x'][t * 128:(t + 1) * 128, :], 'xt%d' % b)
            P.act(sq[:], xt[b][:], AF.Square, accum_out=ss[b][:, 0:1])
            P.act(ss[b][:, 1:2], ss[b][:, 0:1], AF.Sqrt, bias=EPS_AP(P), scale=1.0 / D)
            P.op('dve', lambda e, o=ss[b][:, 1:2]: e.reciprocal(out=o, in_=o), reads=[ss[b][:, 1:2]], writes=[ss[b][:, 1:2]])
            P.ts('dve', hb[b][:], xt[b][:], ss[b][:, 1:2], None, ALU.mult)
            for kc in range(KC):
                P.transpose(ptr[b][:, kc, :], hb[b][:, kc * 128:(kc + 1) * 128], ident[:])
            P.copy('act' if t % 2 else 'dve', hT[:, :, t * 128:(t + 1) * 128], ptr[b][:])

    with ExitStack() as s2:
        pp = [P.ps(s2, "ppT%d" % i, [128, 512], F32) for i in range(3)]
        so = [P.sb(s2, "soT%d" % i, [128, 512], F32) for i in range(3)]
        k = 0
        for c in range(TC // 128):
            for j in range(T // 512):
                b = k % 3
                for kc in range(KC):
                    P.mm(pp[b][:], W[:, kc, TOKC + c * 128:TOKC + (c + 1) * 128], hT[:, kc, j * 512:(j + 1) * 512],
                         start=(kc == 0), stop=(kc == KC - 1))
                P.copy('act' if k % 2 else 'dve', so[b][:], pp[b][:])
                P.dma(A['pT'][c * 128:(c + 1) * 128, j * 512:(j + 1) * 512], so[b][:], 'soT%d' % b, q='pool')
                k += 1

    with ExitStack() as s2:
        pg = [P.ps(s2, "pg%d" % i, [128, 512], F32) for i in range(4)]
        ptq = [P.ps(s2, "ptq%d" % i, [128, 4, 128], BF16) for i in range(3)]
        sqhs = [P.sb(s2, "sqh%d" % i, [128, 512], F32) for i in range(3)]
        ssh = [P.sb(s2, "ssh%d" % i, [128, 8], F32) for i in range(4)]
        xn = [P.sb(s2, "xn%d" % i, [128, 512], F32) for i in range(3)]
        qb = [P.sb(s2, "qb%d" % i, [128, 512], BF16) for i in range(3)]
        rts = [P.sb(s2, "rt%d" % i, [128, 4, 8, 8], F32) for i in range(3)]
        qst = [P.sb(s2, "qst%d" % i, [128, 10, 512], BF16) for i in range(2)]
        vst = [P.sb(s2, "vst%d" % i, [128, 768], BF16) for i in range(2)]
        groups = []
        kq = 0
        for t in range(NT):
            for gi in range(4):
                k = t * 4 + gi
                nh = [8, 8, 4, 0][gi]
                qi = None
                if nh:
                    qi = kq % 3
                    kq += 1
                groups.append((t, gi, k, qi))

        def stage(sidx, t, gi, k, q):
            sb_ = (t // 4) % 2
            b = k % 4
            nh = [8, 8, 4, 0][gi]
            nr = [8, 4, 0, 0][gi]
            w = nh * 64
            goff = [0, 512, 1024, 0][gi]
            vb = t % 2
            if sidx == 0:
                for kc in range(KC):
                    P.mm(pg[b][:], hT[:, kc, t * 128:(t + 1) * 128], W[:, kc, gi * 512:(gi + 1) * 512],
                         start=(kc == 0), stop=(kc == KC - 1))
                return
            if sidx == 1:
                if nh:
                    sqh = sqhs[q]
                    P.act(sqh[:, 0:w], pg[b][:, 0:w], AF.Square)
                    P.op('dve', lambda e, o=ssh[b][:, 0:nh], i=sqh[:, 0:w].rearrange("p (h d) -> p h d", d=64):
                         e.tensor_reduce(out=o, in_=i, axis=AX.X, op=ALU.add),
                         reads=[sqh[:, 0:w]], writes=[ssh[b][:, 0:nh]])
                if gi == 2:
                    P.copy('act', vst[vb][:, 0:256], pg[b][:, 256:512])
                if gi == 3:
                    P.copy('act', vst[vb][:, 256:768], pg[b][:, 0:512])
                    P.dma(A['vtok'][t * 128:(t + 1) * 128, :], vst[vb][:], 'vst%d' % vb, q='sp')
                return
            if not nh:
                return
            rt = rts[q]
            xv = xn[q][:, 0:max(nr, 1) * 64].rearrange("p (h d) -> p h d", d=64)
            qv = qb[q][:, 0:max(nr, 1) * 64].rearrange("p (h d) -> p h d", d=64)
            if sidx == 2:
                P.act(ssh[b][:, 0:nh], ssh[b][:, 0:nh], AF.Sqrt, bias=EPS_AP(P), scale=1.0 / 64)
                P.op('dve', lambda e, o=ssh[b][:, 0:nh]: e.reciprocal(out=o, in_=o), reads=[ssh[b][:, 0:nh]], writes=[ssh[b][:, 0:nh]])
                P.tt('dve', xn[q][:, 0:w].rearrange("p (h d) -> p h d", d=64),
                     pg[b][:, 0:w].rearrange("p (h d) -> p h d", d=64),
                     ssh[b][:, 0:nh].unsqueeze(2).broadcast_to([128, nh, 64]), ALU.mult)
            elif sidx == 3:
                if nr:
                    P.tt('pool', xn[q][:, 0:w], xn[q][:, 0:w], G[:, goff:goff + w], ALU.mult)
                    P.copy('act', qb[q][:, 0:w], xn[q][:, 0:w])
                    cb = cos[:, t, :].unsqueeze(1).broadcast_to([128, nr, 8])
                    sb2 = sin[:, t, :].unsqueeze(1).broadcast_to([128, nr, 8])
                    P.tt('dve', rt[:, 0, 0:nr, :], xv[:, :, 0:8], cb, ALU.mult)
                    P.tt('dve', rt[:, 1, 0:nr, :], xv[:, :, 8:16], sb2, ALU.mult)
                    P.tt('pool', rt[:, 2, 0:nr, :], xv[:, :, 8:16], cb, ALU.mult)
                    P.tt('pool', rt[:, 3, 0:nr, :], xv[:, :, 0:8], sb2, ALU.mult)
                else:
                    P.tt('pool', qb[q][:, 0:w], xn[q][:, 0:w], G[:, goff:goff + w], ALU.mult)
            elif sidx == 4:
                if nr:
                    P.tt('dve', qv[:, :, 0:8], rt[:, 0, 0:nr, :], rt[:, 1, 0:nr, :], ALU.subtract)
                    P.tt('pool', qv[:, :, 8:16], rt[:, 2, 0:nr, :], rt[:, 3, 0:nr, :], ALU.add)
                npair = nh // 2
                for pr in range(npair):
                    P.transpose(ptq[q][:, pr, :], qb[q][:, pr * 128:(pr + 1) * 128], ident[:])
            elif sidx == 5:
                npair = nh // 2
                pbase = [0, 4, 8][gi]
                P.copy('act' if gi % 2 else 'dve', qst[sb_][:, pbase:pbase + npair, (t % 4) * 128:(t % 4 + 1) * 128], ptq[q][:, 0:npair, :])
                if t % 4 == 3 and gi == 2:
                    j = t // 4
                    P.dma(A['qkT'][:, j * 512:(j + 1) * 512].rearrange("(a p) n -> p a n", p=128), qst[sb_][:], 'qst%d' % sb_, q='sp')

        NS = 6
        for step in range(len(groups) + NS - 1):
            for sidx in range(NS - 1, -1, -1):
                gidx = step - sidx
                if 0 <= gidx < len(groups):
                    stage(sidx, *groups[gidx])
    s.close()


_eps_cache = {}


def EPS_AP(P):
    return EPS


T = 4096
NEG = -30000.0


class AttnCtx:
    def __init__(self, P, st, consts):
        self.P = P
        self.psS = [P.ps(st, "aS%d" % i, [128, 512], F32) for i in range(2)]
        self.psO = [P.ps(st, "aO%d" % i, [128, 512], F32) for i in range(2)]
        self.psB = [P.ps(st, "aB%d" % i, [128, 512], F32) for i in range(2)]
        self.pT = [P.sb(st, "apT%d" % i, [128, 512], BF16) for i in range(3)]
        self.lr = [P.sb(st, "alr%d" % i, [65, 512], F32) for i in range(2)]
        self.F = [P.sb(st, "aF%d" % i, [64, 512], F32) for i in range(2)]
        self.lrh = [P.sb(st, "alrh%d" % i, [128, 512], BF16) for i in range(2)]
        self.lrl = [P.sb(st, "alrl%d" % i, [128, 512], BF16) for i in range(2)]
        self.G2 = [P.sb(st, "aG%d" % i, [64, 512], F32) for i in range(2)]
        for t_ in self.lrh + self.lrl:
            P.memset('pool', t_[:], 0.0)
        self.kS = 0
        self.kO = 0
        self.kF = 0
        self.vm = 65
        self.prev = None
        self.deferred = []
        self.c = consts


def _push_block(cx, s_fn, exp_fn, pv_fn, first=False):
    if first:
        for f in cx.deferred:
            f()
        cx.deferred = []
    s_fn()
    d = cx.deferred
    cx.deferred = []
    if cx.prev is not None:
        e, p, epi = cx.prev
        e()
        p()
        if epi is not None:
            epi[0]()
            cx.deferred.append(epi[1])
    for f in d:
        f()
    cx.prev = (exp_fn, pv_fn, None)


def _end_chunk(cx, epi_a, epi_b):
    cx.prev = (cx.prev[0], cx.prev[1], (epi_a, epi_b))


def attn_flush(cx):
    d = cx.deferred
    cx.deferred = []
    if cx.prev is not None:
        e, p, epi = cx.prev
        e()
        p()
        if epi is not None:
            epi[0]()
            d.append(epi[1])
        cx.prev = None
    for f in d:
        f()


def _mk_block(cx, po, Kaug, kr, Qaug, q0, Vaug, kt, lo, hi, masks, extra, first, last):
    P = cx.P
    c = cx.c
    ps = cx.psS[cx.kS % 2]
    pt = cx.pT[cx.kS % 3]
    cx.kS += 1

    def s_fn():
        P.mm(ps[:, lo:hi], Kaug[0:kr, kt * 128:(kt + 1) * 128], Qaug[0:kr, q0 + lo:q0 + hi], start=True, stop=(len(masks) == 0 and extra is None))
        if extra is not None:
            P.mm(ps[:, lo:hi], extra[0][0:64, kt * 128:(kt + 1) * 128], extra[1][0:64, q0 + lo:q0 + hi], start=False, stop=(len(masks) == 0))
        for mi, (mk, m) in enumerate(masks):
            P.mm(ps[:, m * 128:(m + 1) * 128], c['ident'][:], mk[:], start=False, stop=(mi == len(masks) - 1))

    def exp_fn():
        P.act(pt[:, lo:hi], ps[:, lo:hi], AF.Exp)

    def pv_fn():
        P.mm(po[0:cx.vm, lo:hi], Vaug[:, kt, 0:cx.vm], pt[:, lo:hi], start=first, stop=last)

    return s_fn, exp_fn, pv_fn


def _mk_factor(cx, po, lng2, gate_c, q0, finish):
    P = cx.P
    c = cx.c
    lr = cx.lr[cx.kF % 2]
    F = cx.F[cx.kF % 2]
    G2 = cx.G2[cx.kF % 2]
    cx.kF += 1
    pb = cx.psB[0]
    pb2 = cx.psB[1]

    lrh = cx.lrh[(cx.kF - 1) % 2]
    lrl = cx.lrl[(cx.kF - 1) % 2]

    def epi_a():
        P.ts('dve', lr[64:65, :], po[64:65, :], 1e-18, None, ALU.max)
        P.act(lr[64:65, :], lr[64:65, :], AF.Ln)
        P.copy('dve', lrh[64:65, :], lr[64:65, :])
        P.tt('dve', lrl[64:65, :], lr[64:65, :], lrh[64:65, :], ALU.subtract)

    def epi_b():
        P.mm(pb[:, :], c['negonesb'][:, :], lrh[:, :], start=True, stop=False)
        P.mm(pb[:, :], c['negonesb'][:, :], lrl[:, :], start=False, stop=True)
        if lng2 is not None:
            P.mm(pb2[:, :], c['selnegb'][:, gate_c * 64:gate_c * 64 + 128], lng2[0][:, q0:q0 + 512], start=True, stop=False)
            P.mm(pb2[:, :], c['selnegb'][:, gate_c * 64:gate_c * 64 + 128], lng2[1][:, q0:q0 + 512], start=False, stop=True)
        P.act(F[:], pb[0:64, :], AF.Exp)
        if lng2 is not None:
            P.act(G2[:], pb2[0:64, :], AF.Exp)
            P.tt('pool', F[:], F[:], G2[:], ALU.mult)
        finish(po, F)

    return epi_a, epi_b


def attn_chunk(cx, Kaug, kr, Qaug, j, Vaug, entries, finish, lng2=None, gate_c=None, extra=None):
    po = cx.psO[cx.kO % 2]
    cx.kO += 1
    q0 = j * 512
    n = len(entries)
    for ei, (kt, lo, hi, masks) in enumerate(entries):
        fns = _mk_block(cx, po, Kaug, kr, Qaug, q0, Vaug, kt, lo, hi, masks, extra, ei == 0, ei == n - 1)
        _push_block(cx, *fns, first=(ei == 0))
    ea, eb = _mk_factor(cx, po, lng2, gate_c, q0, finish)
    _end_chunk(cx, ea, eb)


def causal_entries(j, mc):
    ent = []
    for kt in range(4 * j + 4):
        if kt < 4 * j:
            ent.append((kt, 0, 512, []))
        else:
            m = kt - 4 * j
            ent.append((kt, 128 * m, 512, [(mc, m)]))
    return ent


def window_entries(j, mc, mu):
    ent = []
    for cc in range(-4, 4):
        kt = 4 * j + cc
        if kt < 0:
            continue
        lo = 128 * max(cc, 0)
        hi = 128 * (min(cc + 4, 3) + 1)
        masks = []
        if 0 <= cc <= 3:
            masks.append((mc, cc))
        if 0 <= cc + 4 <= 3:
            masks.append((mu, cc + 4))
        ent.append((kt, lo, hi, masks))
    return ent


def load_consts(P, st, A):
    c = {}
    c['ident'] = P.sb(st, "c_ident", [128, 128], BF16)
    c['mc'] = P.sb(st, "c_mc", [128, 128], BF16)
    c['mu'] = P.sb(st, "c_mu", [128, 128], BF16)
    c['zeros'] = P.sb(st, "c_zeros", [128, 128], BF16)
    c['ident_w'] = P.sb(st, "c_identw", [128, 512], BF16)
    c['negones'] = P.sb(st, "c_negones", [65, 64], F32)
    c['negonesb'] = P.sb(st, "c_negonesb", [128, 128], BF16)
    P.dma(c['ident'][:], A['ident'], 'c0')
    P.dma(c['mc'][:], A['mc'], 'c0')
    P.dma(c['mu'][:], A['mu'], 'c0')
    P.memset('dve', c['zeros'][:], 0.0)
    P.memset('dve', c['ident_w'][:], 0.0)
    P.memset('dve', c['negones'][:], -1.0)
    P.memset('dve', c['negonesb'][:], -1.0)
    return c


def phase_fox(P, A, layer, consts):
    with ExitStack() as st:
        cx = AttnCtx(P, st, consts)
        cf = P.sb(st, "f_cf", [4, T], F32)
        tmp = P.sb(st, "f_tmp", [4, T], F32)
        ones = P.sb(st, "f_ones", [4, T], F32)
        fb = P.sb(st, "f_fb", [4, 2], F32)
        cs = P.sb(st, "f_cs", [4, 3, T], BF16)
        ncs = P.sb(st, "f_ncs", [4, 3, T], BF16)
        P.dma(cf[:], A['pT'][1048:1052, :], 'fx0')
        P.dma(fb[:, 0:1], A['fb'][layer], 'fx0')
        P.ts('dve', fb[:, 1:2], fb[:, 0:1], -1.0, None, ALU.mult)
        P.memset('pool', ones[:], 1.0)
        P.act(tmp[:], cf[:], AF.Exp, bias=fb[:, 1:2], scale=-1.0)
        P.act(tmp[:], tmp[:], AF.Ln, bias=1.0)
        P.scan(cf[:], ones[:], tmp[:], 0.0, ALU.mult, ALU.subtract)
        P.copy('dve', cs[:, 0, :], cf[:])
        P.tt('dve', tmp[:], cf[:], cs[:, 0, :], ALU.subtract)
        P.copy('dve', cs[:, 1, :], tmp[:])
        P.tt('dve', tmp[:], tmp[:], cs[:, 1, :], ALU.subtract)
        P.copy('dve', cs[:, 2, :], tmp[:])
        P.ts('dve', ncs[:].rearrange("p a t -> p (a t)"), cs[:].rearrange("p a t -> p (a t)"), -1.0, None, ALU.mult)
        Qs = [P.sb(st, "f_Q%d" % i, [128, T], BF16) for i in range(2)]
        Ks = [P.sb(st, "f_K%d" % i, [128, T], BF16) for i in range(2)]
        Vs = [P.sb(st, "f_V%d" % i, [128, 32, 65], BF16) for i in range(2)]
        ob = [P.sb(st, "f_ob%d" % i, [64, 512], BF16) for i in range(2)]
        for i in range(2):
            P.memset('pool', Qs[i][64:128, :], 0.0)
            P.memset('pool', Ks[i][64:128, :], 0.0)
            P.memset('pool', Qs[i][64:70, :], 1.0)
            P.memset('pool', Ks[i][64:70, :], 1.0)
            P.memset('pool', Vs[i][:, :, 64:65], 1.0)

        def load(h):
            Q = Qs[h % 2]; K = Ks[h % 2]; V = Vs[h % 2]
            P.dma(Q[0:64, :], A['qkT'][768 + 64 * h:768 + 64 * (h + 1), :], 'fxq%d' % (h % 2))
            P.dma(K[0:64, :], A['qkT'][1024 + 64 * h:1024 + 64 * (h + 1), :], 'fxk%d' % (h % 2), q='act')
            for i in range(3):
                P.dma(Q[64 + i:65 + i, :], cs[h:h + 1, i, :], 'fxq%d' % (h % 2))
                P.dma(K[67 + i:68 + i, :], ncs[h:h + 1, i, :], 'fxk%d' % (h % 2), q='act')
            P.dma(V[:, :, 0:64], A['vtok'][:, 512 + 64 * h:512 + 64 * (h + 1)].rearrange("(n p) d -> p n d", p=128), 'fxv%d' % (h % 2))

        load(0)
        for h in range(4):
            if h + 1 < 4:
                load(h + 1)
            Q = Qs[h % 2]; K = Ks[h % 2]; V = Vs[h % 2]
            for j in range(8):
                def fin(po, F, o=ob[j % 2], j=j, h=h):
                    P.tt('dve', o[:], po[0:64, :], F[:], ALU.mult)
                    P.dma(A['mixT'][768 + 64 * h:768 + 64 * (h + 1), j * 512:(j + 1) * 512], o[:], 'fxo%d' % (j % 2), q='sp')
                attn_chunk(cx, K, 128, Q, j, V, causal_entries(j, consts['mc']), fin)
            attn_flush(cx)

import math, os
STAGE = int(os.environ.get('STAGE', '99'))

T = 4096
LN8 = math.log(0.125)


def phase_hgrn(P, A, layer, consts):
    for ct in range(2):
        with ExitStack() as st:
            B = [P.sb(st, "hB%d" % i, [128, T], F32) for i in range(5)]
            qt = P.sb(st, "h_qt", [128, T], BF16)
            kt = P.sb(st, "h_kt", [128, 2, T], BF16)
            qg = P.sb(st, "h_qg", [128, T], BF16)
            kd = P.sb(st, "h_kd", [128, T], BF16)
            kdt = P.sb(st, "h_kdt", [128, 32, 2, 128], BF16)
            Vt = P.sb(st, "h_Vt", [128, 32, 128], BF16)
            Vz = P.sb(st, "h_Vz", [128, 32, 2, 128], BF16)
            Sbd = P.sb(st, "h_Sbd", [128, 64, 128], BF16)
            rst = P.sb(st, "h_rst", [128, T], BF16)
            sm = P.sb(st, "h_sm", [128, 8], F32)
            dl = P.sb(st, "h_dl", [128, 64], F32)
            mh = P.sb(st, "h_mh", [128, 128], BF16)
            bones = P.sb(st, "h_bones", [128, 128], BF16)
            ident = consts['ident']
            P.dma(mh[:], A['mh'], 'hg0')
            P.dma(bones[:], A['bones'], 'hg0')
            P.dma(sm[:, 0:2], A['lbl'][ct], 'hg0')
            P.dma(sm[:, 4:5], A['og'][layer], 'hg0')
            P.dma(B[0][:], A['pT'][256 + ct * 128:256 + (ct + 1) * 128, :], 'hgz')
            P.dma(B[3][:], A['pT'][ct * 128:(ct + 1) * 128, :], 'hgq', q='act')
            P.dma(Vt[:], A['vtok'][:, ct * 128:(ct + 1) * 128].rearrange("(n p) d -> p n d", p=128), 'hgv', q='pool')
            P.memset('pool', Vz[:], 0.0)
            for hh in range(2):
                P.dma(Vz[:, :, hh, hh * 64:(hh + 1) * 64],
                      A['vtok'][:, ct * 128 + hh * 64:ct * 128 + (hh + 1) * 64].rearrange("(n p) d -> p n d", p=128), 'hgv', q='pool')
            P.memset('pool', Sbd[:], 0.0)
            P.memset('pool', kt[:], 0.0)
            P.memset('pool', kdt[:], 0.0)
            P.memset('pool', rst[:], 1.0)
            P.memset('pool', rst[:].rearrange("p (c s) -> p c s", s=64)[:, :, 0:1], 0.0)
            lb = sm[:, 2:3]; oml = sm[:, 3:4]; noml = sm[:, 5:6]
            if layer == 0:
                P.memset('dve', lb, 0.0)
            else:
                P.act(sm[:, 0:2], sm[:, 0:2], AF.Exp)
                P.tt('dve', sm[:, 6:7], sm[:, 0:1], sm[:, 1:2], ALU.add)
                P.op('dve', lambda e, o=sm[:, 6:7]: e.reciprocal(out=o, in_=o), reads=[sm[:, 6:7]], writes=[sm[:, 6:7]])
                P.tt('dve', lb, sm[:, 1:2], sm[:, 6:7], ALU.mult)
            P.ts('dve', oml, lb, -1.0, 1.0, ALU.mult, ALU.add)
            P.ts('dve', noml, oml, -1.0, None, ALU.mult)
            P.act(B[0][:], B[0][:], AF.Sigmoid)
            P.ts('dve', B[1][:], B[0][:], oml, lb, ALU.mult, ALU.add)
            P.act(B[1][:], B[1][:], AF.Ln)
            P.scan(B[2][:], rst[:], B[1][:], 0.0, ALU.mult, ALU.add)
            P.ts('dve', B[1][:], B[0][:], noml, oml, ALU.mult, ALU.add)
            G3 = B[2][:].rearrange("p (c s) -> p c s", s=64)
            D3 = B[0][:].rearrange("p (c s) -> p c s", s=64)
            P.tt('dve', D3, G3, G3[:, :, 31:32].broadcast_to([128, 64, 64]), ALU.subtract)
            P.act(B[4][:], B[0][:], AF.Exp, bias=LN8)
            P.tt('dve', qt[:], B[3][:], B[4][:], ALU.mult)
            P.act(B[4][:], B[0][:], AF.Exp, scale=-1.0)
            P.tt('dve', kt[0:64, 0, :], B[1][0:64, :], B[4][0:64, :], ALU.mult)
            P.tt('dve', kt[64:128, 1, :], B[1][64:128, :], B[4][64:128, :], ALU.mult)
            P.act(B[4][:], B[2][:], AF.Exp, bias=LN8)
            P.tt('dve', qg[:], B[3][:], B[4][:], ALU.mult)
            P.tt('dve', D3, G3[:, :, 63:64].broadcast_to([128, 64, 64]), G3, ALU.subtract)
            P.act(B[4][:], B[0][:], AF.Exp)
            P.tt('dve', kd[:], B[1][:], B[4][:], ALU.mult)
            P.act(dl[:].unsqueeze(2), G3[:, :, 63:64], AF.Exp)
            P.memset('dve', dl[:, 0:1], 0.0)
            KV = B[0]; dfull = B[1]; Sall = B[3]; oT = B[4]
            if STAGE < 1:
                P.dma(A['mixT'][0:128, 0:T], kd[:], 'dbg'); continue
            with ExitStack() as s2:
                ptr = [P.ps(s2, "h_ptr%d" % i, [128, 8, 128], BF16) for i in range(2)]
                for g in range(4):
                    for i in range(8):
                        tl = g * 8 + i
                        P.transpose(ptr[g % 2][:, i, :], kd[:, tl * 128:(tl + 1) * 128], ident[:])
                    P.copy('act', kdt[0:64, g * 8:(g + 1) * 8, 0, :], ptr[g % 2][0:64, :, :])
                    P.copy('dve', kdt[64:128, g * 8:(g + 1) * 8, 1, :], ptr[g % 2][64:128, :, :])
            with ExitStack() as s2:
                pkv = [P.ps(s2, "h_pkv%d" % i, [128, 4, 128], F32) for i in range(2)]
                KV3 = KV[:].rearrange("p (v c) -> p v c", c=64)
                for g in range(16):
                    pk = pkv[g % 2]
                    for i in range(4):
                        c = g * 4 + i
                        tl = c // 2; hf = c % 2
                        P.mm(pk[:, i, :], kdt[:, tl, hf, :], Vt[:, tl, :], start=True, stop=True)
                    for hh in range(2):
                        P.copy('act' if hh else 'dve', KV3[hh * 64:(hh + 1) * 64, :, g * 4:(g + 1) * 4],
                               pk[hh * 64:(hh + 1) * 64, :, hh * 64:(hh + 1) * 64].rearrange("p g v -> p v g"))
            if STAGE < 2:
                P.dma(A['mixT'][0:128, 0:T], kd[:], 'dbg'); continue
            P.copy('pool', dfull[:].rearrange("p (v c) -> p v c", c=64), dl[:].unsqueeze(1).broadcast_to([128, 64, 64]))
            P.scan(Sall[:], dfull[:], KV[:], 0.0, ALU.mult, ALU.add)
            S3 = Sall[:].rearrange("p (v c) -> p v c", c=64)
            for hh in range(2):
                P.copy('dve' if hh else 'act', Sbd[hh * 64:(hh + 1) * 64, 1:64, hh * 64:(hh + 1) * 64],
                       S3[hh * 64:(hh + 1) * 64, :, 0:63].rearrange("p v c -> p c v"))
            if STAGE < 3:
                P.dma(A['mixT'][0:128, 0:T], kd[:], 'dbg'); continue
            with ExitStack() as s2:
                pA = [P.ps(s2, "h_pA%d" % i, [128, 128], F32) for i in range(4)]
                po = [P.ps(s2, "h_po%d" % i, [128, 128], F32) for i in range(2)]
                Am = [P.sb(s2, "h_Am%d" % i, [128, 128], BF16) for i in range(4)]
                def scores(tl):
                    cols = slice(tl * 128, (tl + 1) * 128)
                    for hh in range(2):
                        i = (tl % 2) * 2 + hh
                        P.mm(pA[i][:], kt[:, hh, cols], qt[:, cols], start=True, stop=True)
                        P.tt('dve', Am[i][:], pA[i][:], mh[:], ALU.mult)

                def outs(tl):
                    cols = slice(tl * 128, (tl + 1) * 128)
                    p_ = po[tl % 2]
                    P.mm(p_[:], Vz[:, tl, 0, :], Am[(tl % 2) * 2][:], start=True, stop=False)
                    P.mm(p_[:], Vz[:, tl, 1, :], Am[(tl % 2) * 2 + 1][:], start=False, stop=False)
                    P.mm(p_[:, 0:64], Sbd[:, 2 * tl, :], qg[:, tl * 128:tl * 128 + 64], start=False, stop=False)
                    P.mm(p_[:, 64:128], Sbd[:, 2 * tl + 1, :], qg[:, tl * 128 + 64:tl * 128 + 128], start=False, stop=True)
                    P.copy('act', oT[:, cols], p_[:])

                for tl in range(33):
                    if tl < 32:
                        scores(tl)
                    if tl >= 1:
                        outs(tl - 1)
            if STAGE < 4:
                P.dma(A['mixT'][0:128, 0:T], kd[:], 'dbg'); continue
            with ExitStack() as s2:
                pss = [P.ps(s2, "h_pss%d" % i, [128, 512], F32) for i in range(2)]
                sq = [P.sb(s2, "h_sq%d" % i, [128, 512], BF16) for i in range(2)]
                rs = [P.sb(s2, "h_rs%d" % i, [128, 512], F32) for i in range(2)]
                ag = [P.sb(s2, "h_ag%d" % i, [128, 512], F32) for i in range(2)]
                ob = [P.sb(s2, "h_ob%d" % i, [128, 512], BF16) for i in range(2)]
                for j in range(8):
                    b = j % 2
                    cols = slice(j * 512, (j + 1) * 512)
                    P.dma(ag[b][:], A['pT'][512 + ct * 128:512 + (ct + 1) * 128, cols], 'hga%d' % b)
                    P.act(sq[b][:], oT[:, cols], AF.Square)
                    P.mm(pss[b][:], bones[:], sq[b][:], start=True, stop=True)
                    P.act(rs[b][:], pss[b][:], AF.Sqrt, bias=1e-6, scale=1.0 / 64)
                    P.op('dve', lambda e, o=rs[b][:]: e.reciprocal(out=o, in_=o), reads=[rs[b][:]], writes=[rs[b][:]])
                    P.act(ag[b][:], ag[b][:], AF.Silu)
                    P.stt('dve', rs[b][:], oT[:, cols], sm[:, 4:5], rs[b][:], ALU.mult, ALU.mult)
                    P.tt('pool', ob[b][:], rs[b][:], ag[b][:], ALU.mult)
                    P.dma(A['mixT'][ct * 128:(ct + 1) * 128, cols], ob[b][:], 'hgo%d' % b, q='pool')

import os
STAGE = int(os.environ.get('STAGE', '99'))

T = 4096
NEG = -30000.0


def phase_nsa(P, A, layer, consts):
    c = consts
    ident = c['ident']
    with ExitStack() as st:
        cx = AttnCtx(P, st, consts)
        lngh = P.sb(st, "n_lngh", [128, T], BF16)
        lngl = P.sb(st, "n_lngl", [128, T], BF16)
        c['selnegb'] = P.sb(st, "n_selnegb", [128, 24 * 64 + 64], BF16)
        P.memset('pool', c['selnegb'][:], 0.0)
        P.memset('pool', lngh[:], 0.0)
        P.memset('pool', lngl[:], 0.0)
        with ExitStack() as s0:
            lng = P.sb(s0, "n_lng", [24, T], F32)
            selneg = P.sb(s0, "n_selneg", [24, 24 * 64], F32)
            P.dma(selneg[:], A['selneg'], 'ns0')
            P.copy('dve', c['selnegb'][0:24, 0:24 * 64], selneg[:])
            P.dma(lng[:], A['pT'][1024:1048, :], 'ns0')
            P.act(lng[:], lng[:], AF.Exp, scale=-1.0)
            P.act(lng[:], lng[:], AF.Ln, bias=1.0)
            P.copy('dve', lngh[0:24, :], lng[:])
            P.tt('dve', lng[:], lng[:], lngh[0:24, :], ALU.subtract)
            P.copy('dve', lngl[0:24, :], lng[:])
        lng2 = (lngh, lngl)
        ovaug = P.sb(st, "n_ov", [128, 2, 72], BF16)
        wc = P.sb(st, "n_wc", [128, 3200], BF16)
        addm = P.sb(st, "n_addm", [128, 32, 64], F32)
        P.dma(ovaug[:], A['ovaug'], 'ns0')
        P.dma(wc[:], A['wc'], 'ns0')
        P.dma(addm[:], A['addmask'], 'ns0')
        kcTs = [P.sb(st, "n_kcT%d" % i, [128, 256], BF16) for i in range(2)]
        vcAs = [P.sb(st, "n_vcA%d" % i, [128, 2, 65], BF16) for i in range(2)]
        for g in range(2):
            kcT = kcTs[g]; vcA = vcAs[g]
            with ExitStack() as s2:
                w1 = P.sb(s2, "n_w1", [64, 32, 128], BF16)
                w1f = P.sb(s2, "n_w1f", [64, 32, 128], F32)
                w2 = P.sb(s2, "n_w2", [128, 64], BF16)
                w2f = P.sb(s2, "n_w2f", [128, 64], F32)
                posT = P.sb(s2, "n_posT", [64, 32], BF16)
                posf = P.sb(s2, "n_posf", [64, 32], F32)
                posb = P.sb(s2, "n_posb", [64, 32, 256], BF16)
                srcf = P.sb(s2, "n_srcf", [64, T], F32)
                srcb = P.sb(s2, "n_srcb", [64, T], BF16)
                bias = P.sb(s2, "n_bias", [128, 1], F32)
                xb = P.sb(s2, "n_xb", [128, 256], F32)
                x2 = P.sb(s2, "n_x2", [128, 256], F32)
                hid = P.sb(s2, "n_hid", [128, 256], BF16)
                ktm = P.sb(s2, "n_ktm", [128, 64], F32)
                kts = P.sb(s2, "n_kts", [128, 64], F32)
                ktb = P.sb(s2, "n_ktb", [128, 128], BF16)
                sm = P.sb(s2, "n_sm", [128, 4], F32)
                rt = P.sb(s2, "n_rt", [128, 4, 8], F32)
                kng = P.sb(s2, "n_kng", [128, 64], F32)
                cosc = P.sb(s2, "n_cosc", [128, 2, 8], F32)
                sinc = P.sb(s2, "n_sinc", [128, 2, 8], F32)
                ph = cx.psS[0]; pb = cx.psS[1]; po = cx.psO[0]
                pt = P.ps(s2, "n_pt", [128, 128], BF16)
                P.dma(kng[:], A['kng'][layer].partition_broadcast(128), 'ns1')
                P.dma(cosc[:], A['cosc'], 'ns1')
                P.dma(sinc[:], A['sinc'], 'ns1')
                P.memset('dve', hid[:], 0.0)
                P.memset('dve', vcA[:], 0.0)
                P.memset('dve', kcT[:], 0.0)
                P.memset('dve', ktb[:], 0.0)
                for which in range(2):
                    P.dma(w1f[:], A['w1r'][layer, which], 'ns2')
                    P.dma(w2f[:], A['w2'][layer, which], 'ns2')
                    P.dma(posf[:], A['posT'][layer, which], 'ns2')
                    P.dma(srcf[:], A['pT'][768 + 128 * which + 64 * g:768 + 128 * which + 64 * (g + 1), :], 'ns3', q='act')
                    P.copy('dve', w1[:], w1f[:])
                    P.copy('dve', w2[:], w2f[:])
                    P.copy('dve', posT[:], posf[:])
                    P.copy('dve', posb[:], posT[:].unsqueeze(2).broadcast_to([64, 32, 256]))
                    P.copy('act', srcb[:], srcf[:])
                    for l in range(32):
                        P.mm(ph[:, 0:255], w1[:, l, :], srcb[:].rearrange("p (n s) -> p n s", s=16)[:, (l // 16):(l // 16) + 255, l % 16], start=(l == 0), stop=False)
                    for l in range(32):
                        P.mm(ph[:, 0:255], w1[:, l, :], posb[:, l, 0:255], start=False, stop=(l == 31))
                    P.copy('act', xb[:, 0:255], ph[:, 0:255])
                    P.tt('dve', x2[:, 0:255], xb[:, 0:255], xb[:, 0:255], ALU.mult)
                    P.ts('dve', x2[:, 0:255], x2[:, 0:255], 0.044715, 1.0, ALU.mult, ALU.add)
                    P.tt('dve', x2[:, 0:255], x2[:, 0:255], xb[:, 0:255], ALU.mult)
                    P.act(x2[:, 0:255], x2[:, 0:255], AF.Sigmoid, scale=1.5957691216057308)
                    P.tt('dve', hid[:, 0:255], x2[:, 0:255], xb[:, 0:255], ALU.mult)
                    for nt in range(2):
                        P.mm(po[:, 0:64], hid[:, nt * 128:(nt + 1) * 128], w2[:], start=True, stop=True)
                        if which == 1:
                            nr = 128 if nt == 0 else 127
                            P.copy('act', vcA[0:nr, nt, 0:64], po[0:nr, 0:64])
                            P.memset('dve', vcA[0:nr, nt, 64:65], 1.0)
                        else:
                            P.act(kts[:], po[:, 0:64], AF.Square, accum_out=sm[:, 0:1])
                            P.act(sm[:, 1:2], sm[:, 0:1], AF.Sqrt, bias=1e-6, scale=1.0 / 64)
                            P.op('dve', lambda e, o=sm[:, 1:2]: e.reciprocal(out=o, in_=o), reads=[sm[:, 1:2]], writes=[sm[:, 1:2]])
                            P.stt('dve', ktm[:], po[:, 0:64], sm[:, 1:2], kng[:], ALU.mult, ALU.mult)
                            P.copy('act', ktb[:, 0:64], ktm[:])
                            P.tt('dve', rt[:, 0, :], ktm[:, 0:8], cosc[:, nt, :], ALU.mult)
                            P.tt('dve', rt[:, 1, :], ktm[:, 8:16], sinc[:, nt, :], ALU.mult)
                            P.tt('dve', rt[:, 2, :], ktm[:, 8:16], cosc[:, nt, :], ALU.mult)
                            P.tt('dve', rt[:, 3, :], ktm[:, 0:8], sinc[:, nt, :], ALU.mult)
                            P.tt('dve', ktb[:, 0:8], rt[:, 0, :], rt[:, 1, :], ALU.subtract)
                            P.tt('dve', ktb[:, 8:16], rt[:, 2, :], rt[:, 3, :], ALU.add)
                            P.transpose(pt[:], ktb[:], ident[:])
                            P.copy('dve', kcT[0:64, nt * 128:(nt + 1) * 128], pt[0:64, :])
            P.memset('dve', kcT[0:64, 255:256], 0.0)
        selT = P.sb(st, "n_selT", [128, T], BF16)
        imp = P.sb(st, "n_imp", [128, 32, 64], F32)
        acc = [P.sb(st, "n_acc%d" % i, [64, T], F32) for i in range(4)]
        Q = [P.sb(st, "n_Q%d" % i, [128, T], BF16) for i in range(4)]
        Ks = P.sb(st, "n_Ks", [128, T], BF16)
        Kw = P.sb(st, "n_Kw", [128, T], BF16)
        Vs = P.sb(st, "n_Vs", [128, 32, 65], BF16)
        Vw = P.sb(st, "n_Vw", [128, 32, 65], BF16)
        for g in range(2):
            kcT = kcTs[g]; vcA = vcAs[g]
            for hh in range(4):
                h = 4 * g + hh
                P.memset('pool', Q[hh][64:128, :], 0.0)
                P.dma(Q[hh][0:64, :], A['qkT'][64 * h:64 * (h + 1), :], 'nsq%d' % hh)
            P.dma(Ks[0:64, :], A['qkT'][512 + 64 * g:512 + 64 * (g + 1), :], 'nsk')
            P.dma(Ks[64:128, :], A['eall'], 'nsk')
            P.dma(Kw[0:64, :], A['qkT'][640 + 64 * g:640 + 64 * (g + 1), :], 'nsk')
            P.memset('pool', Kw[64:128, :], 0.0)
            P.memset('pool', Vs[:, :, 64:65], 1.0)
            P.memset('pool', Vw[:, :, 64:65], 1.0)
            P.dma(Vs[:, :, 0:64], A['vtok'][:, 256 + 64 * g:256 + 64 * (g + 1)].rearrange("(n p) d -> p n d", p=128), 'nsv', q='act')
            P.dma(Vw[:, :, 0:64], A['vtok'][:, 384 + 64 * g:384 + 64 * (g + 1)].rearrange("(n p) d -> p n d", p=128), 'nsv', q='act')
            with ExitStack() as s2:
                pimp = [P.ps(s2, "n_pimp%d" % i, [128, 4, 72], F32) for i in range(2)]
                pTc = [P.sb(s2, "n_pTc%d" % i, [128, 512], BF16) for i in range(3)]
                rinvs = [P.sb(s2, "n_rinv%d" % i, [128, 4], F32) for i in range(2)]
                kc_ = 0
                kch = 0
                for hh in range(4):
                    h = 4 * g + hh
                    for j in range(8):
                        q0 = j * 512
                        po = cx.psO[cx.kO % 2]
                        cx.kO += 1
                        pim = pimp[kch % 2]
                        rinv = rinvs[kch % 2]
                        kch += 1
                        tiles = []
                        for nt in range(2):
                            off = 2048 * nt + 31 - 512 * j
                            if -off + 511 < 0:
                                continue
                            tiles.append((nt, off))
                        for ti, (nt, off) in enumerate(tiles):
                            ps = cx.psS[cx.kS % 2]
                            cx.kS += 1
                            ptc = pTc[kc_ % 3]
                            kc_ += 1
                            first = (ti == 0)
                            last = (ti == len(tiles) - 1)

                            def s_fn(ps=ps, nt=nt, off=off, first=first, po=po, pim=pim, hh=hh, q0=q0):
                                full = (-off >= 2032)
                                P.mm(ps[:], kcT[:, nt * 128:(nt + 1) * 128], Q[hh][:, q0:q0 + 512], start=True, stop=full)
                                if not full:
                                    ci0 = -off + 511
                                    P.mm(ps[:], ident[:], wc[:, ci0:ci0 + 512], start=False, stop=True)

                            def exp_fn(ps=ps, ptc=ptc):
                                P.act(ptc[:], ps[:], AF.Exp)

                            def pv_fn(po=po, pim=pim, ptc=ptc, nt=nt, last=last, first=first):
                                P.mm(po[0:65, :], vcA[:, nt, :], ptc[:], start=first, stop=last)
                                for m in range(4):
                                    P.mm(pim[:, m, :], ptc[:, m * 128:(m + 1) * 128], ovaug[:, nt, :], start=(first and m == 0), stop=last)

                            _push_block(cx, s_fn, exp_fn, pv_fn, first=first)

                        def fin(po_, F, hh=hh, q0=q0, pim=pim, rinv=rinv, j=j):
                            for m in range(4):
                                tq = j * 4 + m
                                if hh == 0:
                                    P.ts('dve', imp[:, tq, :], pim[:, m, 0:64], rinv[:, m:m + 1], None, ALU.mult)
                                else:
                                    P.stt('dve', imp[:, tq, :], pim[:, m, 0:64], rinv[:, m:m + 1], imp[:, tq, :], ALU.mult, ALU.add)
                            P.tt('dve', acc[hh][:, q0:q0 + 512], po_[0:64, :], F[:], ALU.mult)

                        ea, eb = _mk_factor(cx, po, lng2, h, q0, fin)

                        def ea2(ea=ea, pim=pim, rinv=rinv):
                            ea()
                            P.ts('dve', rinv[:, 0:4].unsqueeze(2), pim[:, :, 64:65], 1e-30, None, ALU.max)
                            P.op('dve', lambda e, o=rinv[:, 0:4]: e.reciprocal(out=o, in_=o), reads=[rinv[:, 0:4]], writes=[rinv[:, 0:4]])

                        _end_chunk(cx, ea2, eb)
                attn_flush(cx)
            if STAGE < 2:
                continue
            with ExitStack() as s2:
                wk = [P.sb(s2, "n_wk%d" % i, [128, 64], F32) for i in range(2)]
                w2_ = [P.sb(s2, "n_wk2%d" % i, [128, 64], F32) for i in range(2)]
                m8 = [P.sb(s2, "n_m8%d" % i, [128, 16], F32) for i in range(2)]
                sb_ = [P.sb(s2, "n_sb%d" % i, [128, 128], BF16) for i in range(2)]
                pts = [P.ps(s2, "n_pts%d" % i, [128, 128], BF16) for i in range(2)]
                P.memset('pool', sb_[0][:], 0.0)
                P.memset('pool', sb_[1][:], 0.0)
                for tq in range(32):
                    b = tq % 2
                    P.tt('dve', wk[b][:], imp[:, tq, :], addm[:, tq, :], ALU.add)
                    P.op('dve', lambda e, o=m8[b][:, 0:8], i=wk[b][:]: e.max(out=o, in_=i), reads=[wk[b][:]], writes=[m8[b][:, 0:8]])
                    P.op('dve', lambda e, o=w2_[b][:], r=m8[b][:, 0:8], i=wk[b][:]: e.match_replace(out=o, in_to_replace=r, in_values=i, imm_value=-3.0e38),
                         reads=[m8[b][:, 0:8], wk[b][:]], writes=[w2_[b][:]])
                    P.op('dve', lambda e, o=m8[b][:, 8:16], i=w2_[b][:]: e.max(out=o, in_=i), reads=[w2_[b][:]], writes=[m8[b][:, 8:16]])
                    P.ts('dve', w2_[b][:], wk[b][:], m8[b][:, 15:16], None, ALU.is_ge)
                    P.ts('dve', wk[b][:], wk[b][:], -5.0e29, None, ALU.is_gt)
                    P.tt('dve', wk[b][:], wk[b][:], w2_[b][:], ALU.mult)
                    P.ts('dve', sb_[b][:, 64:128], wk[b][:], -1.0, -NEG, ALU.add, ALU.mult)
                    P.transpose(pts[b][:], sb_[b][:], ident[:])
                    P.copy('act', selT[64:128, tq * 128:(tq + 1) * 128], pts[b][64:128, :])
            for hh in range(4):
                P.dma(Q[hh][64:128, :], selT[64:128, :], 'nsq%d' % hh, q='sp' if hh % 2 else 'act')
            if STAGE < 3:
                continue
            with ExitStack() as s2:
                tmp = [P.sb(s2, "n_tmp%d" % i, [64, 512], F32) for i in range(2)]
                ob = [P.sb(s2, "n_ob%d" % i, [64, 512], BF16) for i in range(1)] * 2
                for hh in range(4):
                    h = 4 * g + hh
                    for j in range(8):
                        q0 = j * 512
                        a = acc[hh][:, q0:q0 + 512]

                        def fin_s(po_, F, a=a):
                            P.tt('dve', tmp[0][:], po_[0:64, :], F[:], ALU.mult)
                            P.tt('pool', a, a, tmp[0][:], ALU.add)

                        def fin_w(po_, F, a=a, j=j, h=h, q0=q0):
                            P.tt('dve', tmp[1][:], po_[0:64, :], F[:], ALU.mult)
                            P.tt('pool', ob[j % 2][:], a, tmp[1][:], ALU.add)
                            if STAGE >= 5:
                                P.dma(A['mixT'][256 + 64 * h:256 + 64 * (h + 1), q0:q0 + 512], ob[j % 2][:], 'nso%d' % (j % 2), q='sp')

                        attn_chunk(cx, Ks, 128, Q[hh], j, Vs, causal_entries(j, c['mc']), fin_s, lng2=lng2, gate_c=8 + h)
                        if STAGE >= 4:
                            attn_chunk(cx, Kw, 128, Q[hh], j, Vw, window_entries(j, c['mc'], c['mu']), fin_w, lng2=lng2, gate_c=16 + h)
                attn_flush(cx)


T = 4096
D = 1024
FF = 4096


def phase_wo(P, A, layer, consts, x_in, x_mid, per_tile=None):
    ident = consts['ident']
    with ExitStack() as st:
        Wo = P.sb(st, "wo", [128, 8, D], BF16)
        with ExitStack() as s2:
            wst = [P.sb(s2, "wost%d" % i, [128, D], F32) for i in range(2)]
            for kc in range(8):
                b = wst[kc % 2]
                P.dma(b[:], A['wo'][layer, kc * 128:(kc + 1) * 128, :], 'wost%d' % (kc % 2))
                P.copy('act' if kc % 2 else 'dve', Wo[:, kc, :], b[:])
        mx = [P.sb(st, "wo_mx%d" % i, [128, 8, 512], BF16) for i in range(2)]
        xt = [P.sb(st, "wo_xt%d" % i, [128, D], F32) for i in range(2)]
        xm = [P.sb(st, "wo_xm%d" % i, [128, D], F32) for i in range(2)]
        sq = P.sb(st, "wo_sq", [128, D], BF16)
        hb = [P.sb(st, "wo_hb%d" % i, [128, D], BF16) for i in range(2)]
        ss = [P.sb(st, "wo_ss%d" % i, [128, 2], F32) for i in range(2)]
        hst = [P.sb(st, "wo_hst%d" % i, [128, 8, 512], BF16) for i in range(1)] * 2
        po = [P.ps(st, "wo_po%d" % i, [128, 512], F32) for i in range(4)]
        ptr = [P.ps(st, "wo_ptr%d" % i, [128, 8, 128], BF16) for i in range(2)]
        for t in range(32):
            j = t // 4
            b = t % 2
            if t % 4 == 0:
                P.dma(mx[j % 2][:], A['mixT'][:, j * 512:(j + 1) * 512].rearrange("(a p) n -> p a n", p=128), 'womx%d' % (j % 2))
            P.dma(xt[b][:], x_in[t * 128:(t + 1) * 128, :], 'woxt%d' % b, q='act')
            for half in range(2):
                pp = po[(t % 2) * 2 + half]
                for kc in range(8):
                    P.mm(pp[:], mx[j % 2][:, kc, (t % 4) * 128:(t % 4 + 1) * 128], Wo[:, kc, half * 512:(half + 1) * 512],
                         start=(kc == 0), stop=(kc == 7))
                P.tt('dve', xm[b][:, half * 512:(half + 1) * 512], pp[:], xt[b][:, half * 512:(half + 1) * 512], ALU.add)
            P.dma(x_mid[t * 128:(t + 1) * 128, :], xm[b][:], 'woxm%d' % b, q='pool')
            P.act(sq[:], xm[b][:], AF.Square, accum_out=ss[b][:, 0:1])
            P.act(ss[b][:, 1:2], ss[b][:, 0:1], AF.Sqrt, bias=1e-6, scale=1.0 / D)
            P.op('dve', lambda e, o=ss[b][:, 1:2]: e.reciprocal(out=o, in_=o), reads=[ss[b][:, 1:2]], writes=[ss[b][:, 1:2]])
            P.ts('dve', hb[b][:], xm[b][:], ss[b][:, 1:2], None, ALU.mult)
            for kc in range(8):
                P.transpose(ptr[b][:, kc, :], hb[b][:, kc * 128:(kc + 1) * 128], ident[:])
            P.copy('act', hst[j % 2][:, :, (t % 4) * 128:(t % 4 + 1) * 128], ptr[b][:])
            if t % 4 == 3:
                P.dma(A['h2T'][:, j * 512:(j + 1) * 512].rearrange("(a p) n -> p a n", p=128), hst[j % 2][:], 'wohst%d' % (j % 2), q='pool')
            if per_tile is not None:
                per_tile(t)


def phase_wo_ffn(P, A, layer, consts, x_in, x_mid, x_out):
    with ExitStack() as st:
        Wu = P.sb(st, "wu", [128, 8, FF], BF16)
        Wd = P.sb(st, "wd", [128, 32, D], BF16)
        g2 = P.sb(st, "g2", [128, 8], F32)
        P.dma(g2[:], A['g2'][layer], 'ff0')
        with ExitStack() as s2:
            wst = [P.sb(s2, "fwst%d" % i, [128, 1024], F32) for i in range(2)]

            def per_tile(t):
                for u in range(2):
                    ci = 2 * t + u
                    b = u
                    if ci < 32:
                        kc, qt = ci // 4, ci % 4
                        P.dma(wst[b][:], A['wup'][layer, kc * 128:(kc + 1) * 128, qt * 1024:(qt + 1) * 1024], 'fwst%d' % b, q='sp')
                        if u:
                            P.ts('dve', Wu[:, kc, qt * 1024:(qt + 1) * 1024], wst[b][:], g2[:, kc:kc + 1], None, ALU.mult)
                        else:
                            P.act(Wu[:, kc, qt * 1024:(qt + 1) * 1024], wst[b][:], AF.Copy, scale=g2[:, kc:kc + 1])
                    else:
                        fc = ci - 32
                        P.dma(wst[b][:], A['wdn'][layer, fc * 128:(fc + 1) * 128, :], 'fwst%d' % b, q='sp')
                        P.copy('dve' if u else 'act', Wd[:, fc, :], wst[b][:])

            phase_wo(P, A, layer, consts, x_in, x_mid, per_tile=per_tile)
        _ffn_body(P, st, A, Wu, Wd, x_mid, x_out)


def phase_ffn(P, A, layer, consts, x_mid, x_out):
    with ExitStack() as st:
        Wu = P.sb(st, "wu", [128, 8, FF], BF16)
        Wd = P.sb(st, "wd", [128, 32, D], BF16)
        g2 = P.sb(st, "g2", [128, 8], F32)
        P.dma(g2[:], A['g2'][layer], 'ff0')
        with ExitStack() as s2:
            wst = [P.sb(s2, "fwst%d" % i, [128, 2048], F32) for i in range(3)]
            k = 0
            for kc in range(8):
                for hf in range(2):
                    b = k % 3
                    P.dma(wst[b][:], A['wup'][layer, kc * 128:(kc + 1) * 128, hf * 2048:(hf + 1) * 2048], 'fwst%d' % b, q='sp' if k % 2 else 'act')
                    if k % 2:
                        P.ts('dve', Wu[:, kc, hf * 2048:(hf + 1) * 2048], wst[b][:], g2[:, kc:kc + 1], None, ALU.mult)
                    else:
                        P.act(Wu[:, kc, hf * 2048:(hf + 1) * 2048], wst[b][:], AF.Copy, scale=g2[:, kc:kc + 1])
                    k += 1
            for fc2 in range(16):
                b = k % 3
                P.dma(wst[b][:].rearrange("p (a n) -> p a n", a=2), A['wdn'][layer, fc2 * 256:(fc2 + 1) * 256, :].rearrange("(a p) n -> p a n", p=128),
                      'fwst%d' % b, q='sp' if k % 2 else 'act')
                P.copy('dve' if k % 2 else 'act', Wd[:, fc2 * 2:(fc2 + 1) * 2, :], wst[b][:].rearrange("p (a n) -> p a n", a=2))
                k += 1
        _ffn_body(P, st, A, Wu, Wd, x_mid, x_out)


def _ffn_body(P, st, A, Wu, Wd, x_mid, x_out):
    if True:
        h2 = [P.sb(st, "ff_h2%d" % i, [128, 8, 512], BF16) for i in range(2)]
        uT = P.sb(st, "ff_uT", [128, 32, 512], BF16)
        rl = [P.sb(st, "ff_rl%d" % i, [128, 512], F32) for i in range(2)]
        xt = [P.sb(st, "ff_xt%d" % i, [128, D], F32) for i in range(2)]
        xo = [P.sb(st, "ff_xo%d" % i, [128, D], F32) for i in range(2)]
        pu = [P.ps(st, "ff_pu%d" % i, [128, 512], F32) for i in range(3)]
        pd = [P.ps(st, "ff_pd%d" % i, [128, 512], F32) for i in range(4)]
        ku = 0
        for j in range(8):
            P.dma(h2[j % 2][:], A['h2T'][:, j * 512:(j + 1) * 512].rearrange("(a p) n -> p a n", p=128), 'ffh2%d' % (j % 2))
            for fc in range(32):
                pp = pu[ku % 3]
                r = rl[ku % 2]
                for kc in range(8):
                    P.mm(pp[:], Wu[:, kc, fc * 128:(fc + 1) * 128], h2[j % 2][:, kc, :], start=(kc == 0), stop=(kc == 7))
                P.act(r[:], pp[:], AF.Relu)
                P.tt('dve' if ku % 2 else 'pool', uT[:, fc, :], r[:], r[:], ALU.mult)
                ku += 1
            for tt in range(4):
                t = j * 4 + tt
                b = t % 2
                P.dma(xt[b][:], x_mid[t * 128:(t + 1) * 128, :], 'ffxt%d' % b, q='act')
                for half in range(2):
                    pp = pd[(t % 2) * 2 + half]
                    for fc in range(32):
                        P.mm(pp[:], uT[:, fc, tt * 128:(tt + 1) * 128], Wd[:, fc, half * 512:(half + 1) * 512], start=(fc == 0), stop=(fc == 31))
                    P.tt('dve', xo[b][:, half * 512:(half + 1) * 512], pp[:], xt[b][:, half * 512:(half + 1) * 512], ALU.add)
                P.dma(x_out[t * 128:(t + 1) * 128, :], xo[b][:], 'ffxo%d' % b, q='pool')

import ml_dtypes
from concourse.bass_utils import run_bass_kernel_spmd

T=4096; D=1024
OFF = {}
_names = ['aq','af','ai','ag','bq','bkc','bvc','bks','bvs','bkw','bvw','bg','cq','ck','cv','cf']
_sizes = [256,256,256,256,512,128,128,128,128,128,128,24,256,256,256,4]
_o = 0
for n_, s_ in zip(_names, _sizes):
    OFF[n_] = (_o, _o + s_); _o += s_
TOK_ORDER = ['bq','bks','bkw','cq','ck','ai','bvs','bvw','cv']
T_ORDER = ['aq','af','ag','bkc','bvc','bg','cf']

def win_layout(w_in):
    L = w_in.shape[0]
    out = np.zeros((L, 1024, 2048 + 1152), np.float32)
    c = 0
    for n_ in TOK_ORDER:
        a, b = OFF[n_]; out[:, :, c:c + b - a] = w_in[:, :, a:b]; c += b - a
    assert c == 2048
    for n_ in T_ORDER:
        a, b = OFF[n_]; out[:, :, c:c + b - a] = w_in[:, :, a:b]; c += b - a
    return out

def rope_tables():
    inv = np.power(np.float32(500000.0), -np.arange(0, 16, 2, dtype=np.float32) / 16).astype(np.float32)
    pos = np.arange(T, dtype=np.float32)
    ang = pos[:, None] * inv[None, :]
    cos = np.cos(ang).astype(np.float32); sin = np.sin(ang).astype(np.float32)
    return (np.ascontiguousarray(cos.reshape(32, 128, 8).transpose(1, 0, 2)),
            np.ascontiguousarray(sin.reshape(32, 128, 8).transpose(1, 0, 2)))

def _skip():
    pass

def const_inputs():
    k = np.arange(128)[:, None]; q = np.arange(128)[None, :]
    mc = np.where(k <= q, 0.0, -30000.0).astype(ml_dtypes.bfloat16)
    mu = np.where(k > q, 0.0, -30000.0).astype(ml_dtypes.bfloat16)
    selneg = np.zeros((24, 24 * 64), np.float32)
    for c in range(24):
        selneg[c, c * 64:(c + 1) * 64] = -1.0
    return dict(ident=np.eye(128, dtype=ml_dtypes.bfloat16), mc=mc, mu=mu, selneg=selneg)

def _unused_ref_proj(inp, layer, x):
    x = x.astype(np.float64)
    h = x / np.sqrt((x * x).mean(-1, keepdims=True) + 1e-6) * inp['norm1_g'][layer]
    return h @ inp['w_in'][layer].astype(np.float64)

def hgrn_consts(inp):
    s = np.arange(128)[:, None]; t = np.arange(128)[None, :]
    mh = ((s // 64 == t // 64) & (s <= t)).astype(ml_dtypes.bfloat16)
    bones = (s // 64 == t // 64).astype(ml_dtypes.bfloat16)
    lbl = np.ascontiguousarray(inp['hgrn_lb_logits'].reshape(2, 2, 128).transpose(1, 2, 0)).astype(np.float32)
    og = np.tile(inp['hgrn_onorm_g'], (1, 2)).reshape(2, 128, 1).astype(np.float32)
    return dict(mh=mh, bones=bones, lbl=lbl, og=og)

def _unused_ref_hgrn(inp, layer, proj):
    def sl(n): a, b = OFF[n]; return proj[:, a:b]
    lbp = np.exp(inp['hgrn_lb_logits'].astype(np.float64)); lbp /= lbp.sum(0, keepdims=True)
    lb_all = np.cumsum(lbp, 0) - lbp[0:1]
    lb = lb_all[layer].reshape(4, 64)
    z = sl('af').reshape(T, 4, 64)
    sig = 1 / (1 + np.exp(-z))
    f = lb + (1 - lb) * sig; logf = np.log(f); k = (1 - lb) * (1 - sig)
    q = sl('aq').reshape(T, 4, 64) * 0.125; v = sl('ai').reshape(T, 4, 64)
    o = np.zeros((T, 4, 64))
    for h in range(4):
        S = np.zeros((64, 64))
        for c in range(64):
            r = slice(c * 64, (c + 1) * 64)
            G = np.cumsum(logf[r, h], 0)
            qc, kc, vc = q[r, h], k[r, h], v[r, h]
            o_inter = (qc * np.exp(G)) @ S
            diff = G[:, None, :] - G[None, :, :]
            mask = np.tril(np.ones((64, 64), bool))
            dec = np.where(mask[:, :, None], np.exp(np.minimum(diff, 0)), 0)
            sc = np.einsum('tk,sk,tsk->ts', qc, kc, dec)
            o[r, h] = o_inter + sc @ vc
            S = S * np.exp(G[-1])[:, None] + (kc * np.exp(G[-1] - G)).T @ vc
    g = sl('ag').reshape(T, 4, 64)
    gate = g / (1 + np.exp(-g))
    on = o / np.sqrt((o * o).mean(-1, keepdims=True) + 1e-6) * inp['hgrn_onorm_g'][layer]
    return (on * gate).reshape(T, 256)

def nsa_consts(inp):
    n_cmp = 255
    ci = np.arange(n_cmp)[:, None]; sj = np.arange(64)[None, :]
    ov = ((ci * 16 <= sj * 64 + 63) & (ci * 16 + 31 >= sj * 64)).astype(np.float32)
    ovaug = np.zeros((256, 72), np.float32); ovaug[:255, :64] = ov; ovaug[:255, 64] = 1.0
    ovaug = np.ascontiguousarray(ovaug.reshape(2, 128, 72).transpose(1, 0, 2)).astype(ml_dtypes.bfloat16)
    nl = np.arange(128)[:, None]; cc = np.arange(3200)[None, :] - 511
    wc = np.where(cc >= 16 * nl, 0.0, -30000.0).astype(ml_dtypes.bfloat16)
    eall = (np.arange(T)[None, :] // 64 == np.arange(64)[:, None]).astype(ml_dtypes.bfloat16)
    q = np.arange(T)[:, None]; j = np.arange(64)[None, :]; cur = q // 64
    am = np.zeros((T, 64), np.float32)
    am[(j == 0) | (j == cur) | (j == cur - 1)] = 1e30
    am[np.broadcast_to(j > cur, am.shape)] = -1e30
    addmask = np.ascontiguousarray(am.reshape(32, 128, 64).transpose(1, 0, 2))
    inv = np.power(np.float32(500000.0), -np.arange(0, 16, 2, dtype=np.float32) / 16).astype(np.float32)
    pos = (np.arange(256, dtype=np.float32) * 16 + 31)
    ang = pos[:, None] * inv[None, :]
    cosc = np.ascontiguousarray(np.cos(ang).astype(np.float32).reshape(2, 128, 8).transpose(1, 0, 2))
    sinc = np.ascontiguousarray(np.sin(ang).astype(np.float32).reshape(2, 128, 8).transpose(1, 0, 2))
    w1r = np.ascontiguousarray(inp['nsa_cmp_w1'].reshape(2, 2, 32, 64, 128).transpose(0, 1, 3, 2, 4)).astype(np.float32)
    posT = np.ascontiguousarray(inp['nsa_cmp_pos'].transpose(0, 1, 3, 2)).astype(np.float32)
    return dict(ovaug=ovaug, wc=wc, eall=eall, addmask=addmask, cosc=cosc, sinc=sinc, w1r=w1r, posT=posT,
                w2=inp['nsa_cmp_w2'].astype(np.float32), kng=inp['nsa_kn_g'].astype(np.float32))

NSA_SHAPES = [('ovaug', [128, 2, 72], BF16), ('wc', [128, 3200], BF16), ('eall', [64, 4096], BF16), ('addmask', [128, 32, 64], F32),
              ('cosc', [128, 2, 8], F32), ('sinc', [128, 2, 8], F32), ('w1r', [2, 2, 64, 32, 128], F32), ('posT', [2, 2, 64, 32], F32),
              ('w2', [2, 2, 128, 64], F32), ('kng', [2, 64], F32)]


import ml_dtypes
from concourse.bass_utils import run_bass_kernel_spmd

_IN_SHAPES = [('x', [T, D], F32), ('win', [2, D, WCOLS], F32), ('g1', [2, 128, 8], F32), ('gq', [2, 1280], F32),
              ('cos', [128, 32, 8], F32), ('sin', [128, 32, 8], F32), ('ident', [128, 128], BF16), ('mc', [128, 128], BF16),
              ('mu', [128, 128], BF16), ('selneg', [24, 1536], F32), ('mh', [128, 128], BF16), ('bones', [128, 128], BF16),
              ('lbl', [2, 128, 2], F32), ('og', [2, 128, 1], F32), ('fb', [2, 4, 1], F32), ('wo', [2, 1024, 1024], F32),
              ('wup', [2, 1024, 4096], F32), ('wdn', [2, 4096, 1024], F32), ('g2', [2, 128, 8], F32)] + NSA_SHAPES

KDEPTH = int(os.environ.get('KDEPTH', '2'))
KPHASES = os.environ.get('KPHASES', '1hnfwf')


def _body(P):
    nc = P.nc
    A = {}
    for k_, shp, dt_ in _IN_SHAPES:
        A[k_] = nc.dram_tensor(k_, shp, dt_, kind="ExternalInput").ap()
    A['y'] = nc.dram_tensor("y", [T, D], F32, kind="ExternalOutput").ap()
    A['qkT'] = nc.dram_tensor("qkT", [1280, T], BF16).ap()
    A['vtok'] = nc.dram_tensor("vtok", [T, 768], BF16).ap()
    A['pT'] = nc.dram_tensor("pT", [TC, T], F32).ap()
    A['mixT'] = nc.dram_tensor("mixT", [1024, T], BF16).ap()
    A['h2T'] = nc.dram_tensor("h2T", [1024, T], BF16).ap()
    xm = nc.dram_tensor("xmid", [T, D], F32).ap()
    x1 = nc.dram_tensor("x1", [T, D], F32).ap()
    xin = A['x']
    for layer in range(KDEPTH):
        A['x'] = xin
        with ExitStack() as st:
            phase1(P, st, A, layer)
        with ExitStack() as st:
            consts = load_consts(P, st, A)
            if 'h' in KPHASES:
                phase_hgrn(P, A, layer, consts)
            if 'n' in KPHASES:
                phase_nsa(P, A, layer, consts)
            if 'f' in KPHASES:
                phase_fox(P, A, layer, consts)
            xout = x1 if layer < KDEPTH - 1 else A['y']
            phase_wo_ffn(P, A, layer, consts, xin, xm, xout)
        xin = xout


def _host_inputs(inp):
    cos, sin = rope_tables()
    gq = np.concatenate([np.tile(inp['nsa_qn_g'], (1, 8)), np.tile(inp['nsa_kn_g'], (1, 4)), np.tile(inp['fox_qn_g'], (1, 4)),
                         np.tile(inp['fox_kn_g'], (1, 4))], axis=1).astype(np.float32)
    base = {"win": win_layout(inp['w_in']), "g1": np.ascontiguousarray(inp['norm1_g'].reshape(2, 8, 128).transpose(0, 2, 1)),
            "g2": np.ascontiguousarray(inp['norm2_g'].reshape(2, 8, 128).transpose(0, 2, 1)),
            "gq": gq, "cos": cos, "sin": sin, "fb": inp['fox_fb'].reshape(2, 4, 1).astype(np.float32),
            "wo": inp['w_o'], "wup": inp['w_up'], "wdn": inp['w_down']}
    base.update(const_inputs()); base.update(hgrn_consts(inp)); base.update(nsa_consts(inp))
    return base


def kernel(**inp):
    inp = {k: np.asarray(v) for k, v in inp.items()}
    nc, plan = build_two_pass(lambda: bass.Bass("TRN2", target_bir_lowering=False), _body)
    base = _host_inputs(inp)
    in_maps = []
    for b in range(8):
        m = dict(base); m['x'] = np.ascontiguousarray(inp['x'][b]); in_maps.append(m)
    res = run_bass_kernel_spmd(nc, in_maps, core_ids=list(range(8)))
    return np.stack([r['y'] for r in res.results], axis=0).astype(np.float32)
```

```python
import numpy as np, sys, time, os, math
import numpy as np
from contextlib import ExitStack
import concourse.bass as bass
import concourse.mybir as mybir

F32 = mybir.dt.float32
BF16 = mybir.dt.bfloat16
AF = mybir.ActivationFunctionType
ALU = mybir.AluOpType
AX = mybir.AxisListType


def _box(ap):
    t = ap.tensor
    dims = ap.ap
    off = int(ap.offset)
    shp = tuple(t.shape)
    rowsize = 1
    for s in shp[1:]:
        rowsize *= int(s)
    r0 = off // rowsize
    f0 = off % rowsize
    rows = 0
    free = 0
    for (st, cnt) in dims:
        st = int(st); cnt = int(cnt)
        if cnt <= 1 or st == 0:
            continue
        if st % rowsize == 0:
            rows += (st // rowsize) * (cnt - 1)
        else:
            free += st * (cnt - 1)
    return t.name, (r0, r0 + rows, f0, f0 + free)


def _ov(a, b):
    return a[0] <= b[1] and b[0] <= a[1] and a[2] <= b[3] and b[2] <= a[3]


def _cont(a, b):
    return a[0] <= b[0] and b[1] <= a[1] and a[2] <= b[2] and b[3] <= a[3]


class Prog:
    def __init__(self, nc, plan=None):
        self.nc = nc
        self.plan = plan
        self.rec = plan is None
        self.eng = dict(pe=nc.tensor, dve=nc.vector, act=nc.scalar, pool=nc.gpsimd, sp=nc.sync)
        self.n = 0
        self.ins = []
        self.track = {}
        self.lane_cnt = {}
        self.freed = {}
        self.uid = 0
        self.stack = ExitStack()
        self.psum_rr = 0
        self.psum_banks = []
        if not self.rec:
            self.sem = {}
            for e in ['pe', 'dve', 'act', 'pool']:
                self.sem[e] = self.stack.enter_context(nc.semaphore("sem_" + e))
            self.lane_sem = {}
            for ln in plan['lanes']:
                self.lane_sem[ln] = self.stack.enter_context(nc.semaphore("ln_" + ln))

    def sb(self, st, name, shape, dtype):
        self.uid += 1
        name = "%s_%d" % (name, self.uid)
        t = st.enter_context(self.nc.sbuf_tensor("s_" + name, list(shape), dtype))
        st.callback(self._free, "s_" + name)
        return t

    def ps(self, st, name, shape, dtype=F32):
        self.uid += 1
        name = "%s_%d" % (name, self.uid)
        t = st.enter_context(self.nc.psum_tensor("p_" + name, list(shape), dtype))
        st.callback(self._free, "p_" + name)
        return t

    def _free(self, name):
        if not self.rec:
            return
        recs = self.track.pop(name, [])
        for (b, i, w) in recs:
            r = self.ins[i]
            key = ('l', r['lane'], i) if r['dma'] else ('e', r['eng'])
            if r['dma']:
                self.freed[key] = i
            else:
                self.freed[key] = max(self.freed.get(key, -1), i)

    def _access(self, idx, eng, dma, ap, write, deps):
        name, box = _box(ap)
        if name not in self.track:
            big = (0, 10 ** 9, 0, 10 ** 9)
            kind = ap.space
            self.track[name] = [] if str(kind) == 'DRAM' else [(big, i, True) for i in sorted(set(self.freed.values()))]
        recs = self.track[name]
        for (b, i, w) in recs:
            if (write or w) and _ov(b, box):
                deps.append((i, (w and not write)))
        if write:
            recs[:] = [r for r in recs if not _cont(box, r[0])]
        elif not dma:
            recs[:] = [r for r in recs if not ((not r[2]) and r[1] < len(self.ins) and self.ins[r[1]]['eng'] == eng
                                               and not self.ins[r[1]]['dma'] and _cont(box, r[0]))]
        recs.append((box, idx, write))

    def op(self, eng, fn, reads=(), writes=(), dma=False, lane=None):
        idx = self.n
        self.n += 1
        if self.rec:
            deps = []
            for ap in reads:
                self._access(idx, eng, dma, ap, False, deps)
            for ap in writes:
                self._access(idx, eng, dma, ap, True, deps)
            lanewaits = {}
            d2 = {}
            for (j, raw) in deps:
                if j == idx:
                    continue
                pj = self.ins[j]
                if pj['dma']:
                    ln = pj['lane']
                    lanewaits[ln] = max(lanewaits.get(ln, 0), pj['lane_val_at'])
                    lanewaits[ln] = max(lanewaits[ln], self.lane_cnt[ln])
                    continue
                if pj['eng'] == eng and not dma:
                    if eng == 'pe':
                        continue
                    if not raw and eng != 'pool':
                        continue
                d2[j] = True
            rec = dict(eng=eng, deps=list(d2.keys()), lanewaits=lanewaits, dma=dma, lane=lane)
            if dma:
                self.lane_cnt[lane] = self.lane_cnt.get(lane, 0) + 16
                rec['lane_val_at'] = self.lane_cnt[lane]
            self.ins.append(rec)
            return None
        else:
            info = self.plan['ins'][idx]
            e = self.eng[eng]
            for (sname, val) in info['waits']:
                s = self.sem[sname[1]] if sname[0] == 'e' else self.lane_sem[sname[1]]
                e.wait_ge(s, val)
            inst = fn(e)
            if dma:
                inst.then_inc(self.lane_sem[lane], 16)
            elif info['signal']:
                inst.then_inc(self.sem[eng], 1)
            return inst

    def make_plan(self):
        ins = self.ins
        signal = [False] * len(ins)
        for r in ins:
            for j in r['deps']:
                signal[j] = True
        cnt = dict(pe=0, dve=0, act=0, pool=0, sp=0)
        sigval = [0] * len(ins)
        for i, r in enumerate(ins):
            if signal[i] and not r['dma']:
                cnt[r['eng']] += 1
                sigval[i] = cnt[r['eng']]
        seen = {e: {} for e in cnt}
        out = []
        for i, r in enumerate(ins):
            need = {}
            for j in r['deps']:
                k = ('e', ins[j]['eng'])
                need[k] = max(need.get(k, 0), sigval[j])
            for ln, v in r['lanewaits'].items():
                k = ('l', ln)
                need[k] = max(need.get(k, 0), v)
            waits = []
            sd = seen[r['eng']]
            for k, v in need.items():
                if sd.get(k, 0) >= v:
                    continue
                sd[k] = v
                waits.append((k, v))
            out.append(dict(waits=waits, signal=signal[i]))
        return dict(ins=out, lanes=sorted(self.lane_cnt.keys()), lane_final=dict(self.lane_cnt))

    def finish(self):
        if self.rec:
            return
        for ln, v in self.plan['lane_final'].items():
            self.nc.sync.wait_ge(self.lane_sem[ln], v)

    def dma(self, out, in_, lane, q='sp', **kw):
        return self.op(q, lambda e: e.dma_start(out=out, in_=in_, **kw), reads=[in_], writes=[out],
                       dma=True, lane=lane)

    def mm(self, out, lhsT, rhs, start=True, stop=True, **kw):
        return self.op('pe', lambda e: e.matmul(out, lhsT, rhs, start=start, stop=stop, **kw),
                       reads=[lhsT, rhs], writes=[out])

    def transpose(self, out, in_, ident):
        return self.op('pe', lambda e: e.transpose(out, in_, ident), reads=[in_, ident], writes=[out])

    def act(self, out, in_, func, bias=None, scale=None, accum_out=None, eng='act'):
        reads = [in_]
        kw = {}
        if bias is not None:
            kw['bias'] = bias
            if not isinstance(bias, (int, float)):
                reads.append(bias)
        if scale is not None:
            kw['scale'] = scale
            if not isinstance(scale, (int, float)):
                reads.append(scale)
        writes = [out]
        if accum_out is not None:
            kw['accum_out'] = accum_out
            writes.append(accum_out)
        return self.op(eng, lambda e: e.activation(out=out, in_=in_, func=func, **kw), reads=reads, writes=writes)

    def tt(self, eng, out, in0, in1, op):
        return self.op(eng, lambda e: e.tensor_tensor(out=out, in0=in0, in1=in1, op=op), reads=[in0, in1], writes=[out])

    def ts(self, eng, out, in0, s1, s2, op0, op1=None, accum_out=None):
        reads = [in0]
        if not isinstance(s1, (int, float)):
            reads.append(s1)
        if s2 is not None and not isinstance(s2, (int, float)):
            reads.append(s2)
        kw = {}
        writes = [out]
        if op1 is not None:
            kw['op1'] = op1
        if accum_out is not None:
            kw['accum_out'] = accum_out
            writes.append(accum_out)
        return self.op(eng, lambda e: e.tensor_scalar(out=out, in0=in0, scalar1=s1, scalar2=s2, op0=op0, **kw),
                       reads=reads, writes=writes)

    def stt(self, eng, out, in0, scalar, in1, op0, op1):
        reads = [in0, in1]
        if not isinstance(scalar, (int, float)):
            reads.append(scalar)
        return self.op(eng, lambda e: e.scalar_tensor_tensor(out=out, in0=in0, scalar=scalar, in1=in1, op0=op0, op1=op1),
                       reads=reads, writes=[out])

    def copy(self, eng, out, in_):
        if eng == 'act':
            return self.op(eng, lambda e: e.copy(out=out, in_=in_), reads=[in_], writes=[out])
        return self.op(eng, lambda e: e.tensor_copy(out=out, in_=in_), reads=[in_], writes=[out])

    def memset(self, eng, ap, val):
        return self.op(eng, lambda e: e.memset(ap, val), reads=[], writes=[ap])

    def scan(self, out, d0, d1, initial, op0, op1):
        reads = [d0, d1]
        if not isinstance(initial, (int, float)):
            reads.append(initial)
        return self.op('dve', lambda e: e.tensor_tensor_scan(out=out, data0=d0, data1=d1, initial=initial, op0=op0, op1=op1),
                       reads=reads, writes=[out])

    def generic(self, eng, fn, reads, writes):
        return self.op(eng, fn, reads=reads, writes=writes)


def build_two_pass(make_nc, body):
    nc1 = make_nc()
    p1 = Prog(nc1, None)
    body(p1)
    p1.stack.close()
    plan = p1.make_plan()
    nc2 = make_nc()
    p2 = Prog(nc2, plan)
    body(p2)
    p2.finish()
    p2.stack.close()
    return nc2, plan


T = 4096
NT = 32
D = 1024
KC = 8
TOKC = 2048
TC = 1152
WCOLS = TOKC + TC
EPS = 1e-6


def phase1(P, st, A, layer):
    nc = P.nc
    s = ExitStack()
    W = P.sb(s, "w_in", [128, KC, WCOLS], BF16)
    hT = P.sb(s, "hT", [128, KC, T], BF16)
    ident = P.sb(s, "ident", [128, 128], BF16)
    g1 = P.sb(s, "g1", [128, KC], F32)
    G = P.sb(s, "Gq", [128, 1280], F32)
    cos = P.sb(s, "cos", [128, NT, 8], F32)
    sin = P.sb(s, "sin", [128, NT, 8], F32)
    P.dma(ident[:], A['ident'], 'c0')
    P.dma(g1[:], A['g1'][layer], 'c0')
    P.dma(G[:], A['gq'][layer].partition_broadcast(128), 'c0')
    P.dma(cos[:], A['cos'], 'c0')
    P.dma(sin[:], A['sin'], 'c0')
    P.ts('dve', G[:, 0:512], G[:, 0:512], 0.125, None, ALU.mult)
    P.ts('dve', G[:, 768:1024], G[:, 768:1024], 0.125, None, ALU.mult)

    with ExitStack() as s2:
        wst = [P.sb(s2, "wst%d" % i, [128, WCOLS], F32) for i in range(4)]
        for kc in range(KC):
            b = wst[kc % 4]
            P.dma(b[:], A['win'][layer, kc * 128:(kc + 1) * 128, :], 'wst%d' % (kc % 4), q='sp' if kc % 2 == 0 else 'act')
            half = WCOLS // 2
            P.ts('dve', W[:, kc, 0:half], b[:, 0:half], g1[:, kc:kc + 1], None, ALU.mult)
            P.act(W[:, kc, half:WCOLS], b[:, half:WCOLS], AF.Copy, scale=g1[:, kc:kc + 1])

    with ExitStack() as s2:
        xt = [P.sb(s2, "xt%d" % i, [128, D], F32) for i in range(2)]
        sq = P.sb(s2, "sqj", [128, D], F32)
        hb = [P.sb(s2, "hb%d" % i, [128, D], BF16) for i in range(2)]
        ss = [P.sb(s2, "ss%d" % i, [128, 2], F32) for i in range(2)]
        ptr = [P.ps(s2, "ptr%d" % i, [128, KC, 128], BF16) for i in range(2)]
        for t in range(NT):
            b = t % 2
            P.dma(xt[b][:], A['x'][t * 128:(t + 1) * 128, :], 'xt%d' % b)
            P.act(sq[:], xt[b][:], AF.Square, accum_out=ss[b][:, 0:1])
            P.act(ss[b][:, 1:2], ss[b][:, 0:1], AF.Sqrt, bias=EPS_AP(P), scale=1.0 / D)
            P.op('dve', lambda e, o=ss[b][:, 1:2]: e.reciprocal(out=o, in_=o), reads=[ss[b][:, 1:2]], writes=[ss[b][:, 1:2]])
            P.ts('dve', hb[b][:], xt[b][:], ss[b][:, 1:2], None, ALU.mult)
            for kc in range(KC):
                P.transpose(ptr[b][:, kc, :], hb[b][:, kc * 128:(kc + 1) * 128], ident[:])
            P.copy('act' if t % 2 else 'dve', hT[:, :, t * 128:(t + 1) * 128], ptr[b][:])

    with ExitStack() as s2:
        pp = [P.ps(s2, "ppT%d" % i, [128, 512], F32) for i in range(3)]
        so = [P.sb(s2, "soT%d" % i, [128, 512], F32) for i in range(3)]
        k = 0
        for c in range(TC // 128):
            for j in range(T // 512):
                b = k % 3
                for kc in range(KC):
                    P.mm(pp[b][:], W[:, kc, TOKC + c * 128:TOKC + (c + 1) * 128], hT[:, kc, j * 512:(j + 1) * 512],
                         start=(kc == 0), stop=(kc == KC - 1))
                P.copy('act' if k % 2 else 'dve', so[b][:], pp[b][:])
                P.dma(A['pT'][c * 128:(c + 1) * 128, j * 512:(j + 1) * 512], so[b][:], 'soT%d' % b, q='pool')
                k += 1

    with ExitStack() as s2:
        pg = [P.ps(s2, "pg%d" % i, [128, 512], F32) for i in range(4)]
        ptq = [P.ps(s2, "ptq%d" % i, [128, 4, 128], BF16) for i in range(3)]
        sqhs = [P.sb(s2, "sqh%d" % i, [128, 512], F32) for i in range(3)]
        ssh = [P.sb(s2, "ssh%d" % i, [128, 8], F32) for i in range(4)]
        xn = [P.sb(s2, "xn%d" % i, [128, 512], F32) for i in range(3)]
        qb = [P.sb(s2, "qb%d" % i, [128, 512], BF16) for i in range(3)]
        rts = [P.sb(s2, "rt%d" % i, [128, 4, 8, 8], F32) for i in range(3)]
        qst = [P.sb(s2, "qst%d" % i, [128, 10, 512], BF16) for i in range(2)]
        vst = [P.sb(s2, "vst%d" % i, [128, 768], BF16) for i in range(2)]
        groups = []
        kq = 0
        for t in range(NT):
            for gi in range(4):
                k = t * 4 + gi
                nh = [8, 8, 4, 0][gi]
                qi = None
                if nh:
                    qi = kq % 3
                    kq += 1
                groups.append((t, gi, k, qi))

        def stage(sidx, t, gi, k, q):
            sb_ = (t // 4) % 2
            b = k % 4
            nh = [8, 8, 4, 0][gi]
            nr = [8, 4, 0, 0][gi]
            w = nh * 64
            goff = [0, 512, 1024, 0][gi]
            vb = t % 2
            if sidx == 0:
                for kc in range(KC):
                    P.mm(pg[b][:], hT[:, kc, t * 128:(t + 1) * 128], W[:, kc, gi * 512:(gi + 1) * 512],
                         start=(kc == 0), stop=(kc == KC - 1))
                return
            if sidx == 1:
                if nh:
                    sqh = sqhs[q]
                    P.act(sqh[:, 0:w], pg[b][:, 0:w], AF.Square)
                    P.op('dve', lambda e, o=ssh[b][:, 0:nh], i=sqh[:, 0:w].rearrange("p (h d) -> p h d", d=64):
                         e.tensor_reduce(out=o, in_=i, axis=AX.X, op=ALU.add),
                         reads=[sqh[:, 0:w]], writes=[ssh[b][:, 0:nh]])
                if gi == 2:
                    P.copy('act', vst[vb][:, 0:256], pg[b][:, 256:512])
                if gi == 3:
                    P.copy('act', vst[vb][:, 256:768], pg[b][:, 0:512])
                    P.dma(A['vtok'][t * 128:(t + 1) * 128, :], vst[vb][:], 'vst%d' % vb, q='sp')
                return
            if not nh:
                return
            rt = rts[q]
            xv = xn[q][:, 0:max(nr, 1) * 64].rearrange("p (h d) -> p h d", d=64)
            qv = qb[q][:, 0:max(nr, 1) * 64].rearrange("p (h d) -> p h d", d=64)
            if sidx == 2:
                P.act(ssh[b][:, 0:nh], ssh[b][:, 0:nh], AF.Sqrt, bias=EPS_AP(P), scale=1.0 / 64)
                P.op('dve', lambda e, o=ssh[b][:, 0:nh]: e.reciprocal(out=o, in_=o), reads=[ssh[b][:, 0:nh]], writes=[ssh[b][:, 0:nh]])
                P.tt('dve', xn[q][:, 0:w].rearrange("p (h d) -> p h d", d=64),
                     pg[b][:, 0:w].rearrange("p (h d) -> p h d", d=64),
                     ssh[b][:, 0:nh].unsqueeze(2).broadcast_to([128, nh, 64]), ALU.mult)
            elif sidx == 3:
                if nr:
                    P.tt('pool', xn[q][:, 0:w], xn[q][:, 0:w], G[:, goff:goff + w], ALU.mult)
                    P.copy('act', qb[q][:, 0:w], xn[q][:, 0:w])
                    cb = cos[:, t, :].unsqueeze(1).broadcast_to([128, nr, 8])
                    sb2 = sin[:, t, :].unsqueeze(1).broadcast_to([128, nr, 8])
                    P.tt('dve', rt[:, 0, 0:nr, :], xv[:, :, 0:8], cb, ALU.mult)
                    P.tt('dve', rt[:, 1, 0:nr, :], xv[:, :, 8:16], sb2, ALU.mult)
                    P.tt('pool', rt[:, 2, 0:nr, :], xv[:, :, 8:16], cb, ALU.mult)
                    P.tt('pool', rt[:, 3, 0:nr, :], xv[:, :, 0:8], sb2, ALU.mult)
                else:
                    P.tt('pool', qb[q][:, 0:w], xn[q][:, 0:w], G[:, goff:goff + w], ALU.mult)
            elif sidx == 4:
                if nr:
                    P.tt('dve', qv[:, :, 0:8], rt[:, 0, 0:nr, :], rt[:, 1, 0:nr, :], ALU.subtract)
                    P.tt('pool', qv[:, :, 8:16], rt[:, 2, 0:nr, :], rt[:, 3, 0:nr, :], ALU.add)
                npair = nh // 2
                for pr in range(npair):
                    P.transpose(ptq[q][:, pr, :], qb[q][:, pr * 128:(pr + 1) * 128], ident[:])
            elif sidx == 5:
                npair = nh // 2
                pbase = [0, 4, 8][gi]
                P.copy('act' if gi % 2 else 'dve', qst[sb_][:, pbase:pbase + npair, (t % 4) * 128:(t % 4 + 1) * 128], ptq[q][:, 0:npair, :])
                if t % 4 == 3 and gi == 2:
                    j = t // 4
                    P.dma(A['qkT'][:, j * 512:(j + 1) * 512].rearrange("(a p) n -> p a n", p=128), qst[sb_][:], 'qst%d' % sb_, q='sp')

        NS = 6
        for step in range(len(groups) + NS - 1):
            for sidx in range(NS - 1, -1, -1):
                gidx = step - sidx
                if 0 <= gidx < len(groups):
                    stage(sidx, *groups[gidx])
    s.close()


_eps_cache = {}


def EPS_AP(P):
    return EPS


T = 4096
NEG = -30000.0


class AttnCtx:
    def __init__(self, P, st, consts):
        self.P = P
        self.psS = [P.ps(st, "aS%d" % i, [128, 512], F32) for i in range(2)]
        self.psO = [P.ps(st, "aO%d" % i, [128, 512], F32) for i in range(2)]
        self.psB = [P.ps(st, "aB%d" % i, [128, 512], F32) for i in range(2)]
        self.pT = [P.sb(st, "apT%d" % i, [128, 512], BF16) for i in range(3)]
        self.lr = [P.sb(st, "alr%d" % i, [65, 512], F32) for i in range(2)]
        self.F = [P.sb(st, "aF%d" % i, [64, 512], F32) for i in range(2)]
        self.lrh = [P.sb(st, "alrh%d" % i, [128, 512], BF16) for i in range(2)]
        self.lrl = [P.sb(st, "alrl%d" % i, [128, 512], BF16) for i in range(2)]
        self.G2 = [P.sb(st, "aG%d" % i, [64, 512], F32) for i in range(2)]
        for t_ in self.lrh + self.lrl:
            P.memset('pool', t_[:], 0.0)
        self.kF_ids = {id(g): i for i, g in enumerate(self.G2)}
        self.kS = 0
        self.kO = 0
        self.kF = 0
        self.vm = 65
        self.prev = None
        self.deferred = []
        self.c = consts


def _push_block(cx, s_fn, exp_fn, pv_fn, first=False):
    if first:
        for f in cx.deferred:
            f()
        cx.deferred = []
    s_fn()
    d = cx.deferred
    cx.deferred = []
    if cx.prev is not None:
        e, p, epi = cx.prev
        e()
        p()
        if epi is not None:
            epi[0]()
            cx.deferred.append(epi[1])
    for f in d:
        f()
    cx.prev = (exp_fn, pv_fn, None)


def _end_chunk(cx, epi_a, epi_b):
    cx.prev = (cx.prev[0], cx.prev[1], (epi_a, epi_b))


def attn_flush(cx):
    d = cx.deferred
    cx.deferred = []
    if cx.prev is not None:
        e, p, epi = cx.prev
        e()
        p()
        if epi is not None:
            epi[0]()
            d.append(epi[1])
        cx.prev = None
    for f in d:
        f()


def _mk_block(cx, po, Kaug, kr, Qaug, q0, Vaug, kt, lo, hi, masks, extra, first, last):
    P = cx.P
    c = cx.c
    ps = cx.psS[cx.kS % 2]
    pt = cx.pT[cx.kS % 3]
    cx.kS += 1

    def s_fn():
        P.mm(ps[:, lo:hi], Kaug[0:kr, kt * 128:(kt + 1) * 128], Qaug[0:kr, q0 + lo:q0 + hi], start=True, stop=(len(masks) == 0 and extra is None))
        if extra is not None:
            P.mm(ps[:, lo:hi], extra[0][0:64, kt * 128:(kt + 1) * 128], extra[1][0:64, q0 + lo:q0 + hi], start=False, stop=(len(masks) == 0))
        for mi, (mk, m) in enumerate(masks):
            P.mm(ps[:, m * 128:(m + 1) * 128], c['ident'][:], mk[:], start=False, stop=(mi == len(masks) - 1))

    def exp_fn():
        P.act(pt[:, lo:hi], ps[:, lo:hi], AF.Exp)

    def pv_fn():
        P.mm(po[0:cx.vm, lo:hi], Vaug[:, kt, 0:cx.vm], pt[:, lo:hi], start=first, stop=last)

    return s_fn, exp_fn, pv_fn


def _mk_factor(cx, po, lng2, gate_c, q0, finish):
    P = cx.P
    c = cx.c
    lr = cx.lr[cx.kF % 2]
    F = cx.F[cx.kF % 2]
    G2 = cx.G2[cx.kF % 2]
    cx.kF += 1
    pb = cx.psB[0]
    pb2 = cx.psB[1]

    lrh = cx.lrh[(cx.kF - 1) % 2]
    lrl = cx.lrl[(cx.kF - 1) % 2]

    def epi_a():
        if lng2 is not None:
            P.dma(G2[:], lng2[gate_c, q0:q0 + 512].partition_broadcast(64), 'ag%d' % ((cx.kF_ids[id(G2)])), q='sp')
        P.ts('dve', lr[64:65, :], po[64:65, :], 1e-18, None, ALU.max)
        P.act(lr[64:65, :], lr[64:65, :], AF.Ln)
        P.copy('dve', lrh[64:65, :], lr[64:65, :])
        P.tt('dve', lrl[64:65, :], lr[64:65, :], lrh[64:65, :], ALU.subtract)

    def epi_b():
        P.mm(pb[:, :], c['negonesb'][:, :], lrh[:, :], start=True, stop=False)
        P.mm(pb[:, :], c['negonesb'][:, :], lrl[:, :], start=False, stop=True)
        P.act(F[:], pb[0:64, :], AF.Exp)
        if lng2 is not None:
            P.tt('pool', F[:], F[:], G2[:], ALU.mult)
        finish(po, F)

    return epi_a, epi_b


def attn_chunk(cx, Kaug, kr, Qaug, j, Vaug, entries, finish, lng2=None, gate_c=None, extra=None):
    po = cx.psO[cx.kO % 2]
    cx.kO += 1
    q0 = j * 512
    n = len(entries)
    for ei, (kt, lo, hi, masks) in enumerate(entries):
        fns = _mk_block(cx, po, Kaug, kr, Qaug, q0, Vaug, kt, lo, hi, masks, extra, ei == 0, ei == n - 1)
        _push_block(cx, *fns, first=(ei == 0))
    ea, eb = _mk_factor(cx, po, lng2, gate_c, q0, finish)
    _end_chunk(cx, ea, eb)


def causal_entries(j, mc):
    ent = []
    for kt in range(4 * j + 4):
        if kt < 4 * j:
            ent.append((kt, 0, 512, []))
        else:
            m = kt - 4 * j
            ent.append((kt, 128 * m, 512, [(mc, m)]))
    return ent


def window_entries(j, mc, mu):
    ent = []
    for cc in range(-4, 4):
        kt = 4 * j + cc
        if kt < 0:
            continue
        lo = 128 * max(cc, 0)
        hi = 128 * (min(cc + 4, 3) + 1)
        masks = []
        if 0 <= cc <= 3:
            masks.append((mc, cc))
        if 0 <= cc + 4 <= 3:
            masks.append((mu, cc + 4))
        ent.append((kt, lo, hi, masks))
    return ent


def load_consts(P, st, A):
    c = {}
    c['ident'] = P.sb(st, "c_ident", [128, 128], BF16)
    c['mc'] = P.sb(st, "c_mc", [128, 128], BF16)
    c['mu'] = P.sb(st, "c_mu", [128, 128], BF16)
    c['zeros'] = P.sb(st, "c_zeros", [128, 128], BF16)
    c['ident_w'] = P.sb(st, "c_identw", [128, 512], BF16)
    c['negones'] = P.sb(st, "c_negones", [65, 64], F32)
    c['negonesb'] = P.sb(st, "c_negonesb", [128, 128], BF16)
    P.dma(c['ident'][:], A['ident'], 'c0')
    P.dma(c['mc'][:], A['mc'], 'c0')
    P.dma(c['mu'][:], A['mu'], 'c0')
    P.memset('dve', c['zeros'][:], 0.0)
    P.memset('dve', c['ident_w'][:], 0.0)
    P.memset('dve', c['negones'][:], -1.0)
    P.memset('dve', c['negonesb'][:], -1.0)
    return c


def phase_fox(P, A, layer, consts):
    with ExitStack() as st:
        cx = AttnCtx(P, st, consts)
        cf = P.sb(st, "f_cf", [4, T], F32)
        tmp = P.sb(st, "f_tmp", [4, T], F32)
        ones = P.sb(st, "f_ones", [4, T], F32)
        fb = P.sb(st, "f_fb", [4, 2], F32)
        cs = P.sb(st, "f_cs", [4, 3, T], BF16)
        ncs = P.sb(st, "f_ncs", [4, 3, T], BF16)
        P.dma(cf[:], A['pT'][1048:1052, :], 'fx0')
        P.dma(fb[:, 0:1], A['fb'][layer], 'fx0')
        P.ts('dve', fb[:, 1:2], fb[:, 0:1], -1.0, None, ALU.mult)
        P.memset('pool', ones[:], 1.0)
        P.act(tmp[:], cf[:], AF.Exp, bias=fb[:, 1:2], scale=-1.0)
        P.act(tmp[:], tmp[:], AF.Ln, bias=1.0)
        P.scan(cf[:], ones[:], tmp[:], 0.0, ALU.mult, ALU.subtract)
        P.copy('dve', cs[:, 0, :], cf[:])
        P.tt('dve', tmp[:], cf[:], cs[:, 0, :], ALU.subtract)
        P.copy('dve', cs[:, 1, :], tmp[:])
        P.tt('dve', tmp[:], tmp[:], cs[:, 1, :], ALU.subtract)
        P.copy('dve', cs[:, 2, :], tmp[:])
        P.ts('dve', ncs[:].rearrange("p a t -> p (a t)"), cs[:].rearrange("p a t -> p (a t)"), -1.0, None, ALU.mult)
        Qs = [P.sb(st, "f_Q%d" % i, [128, T], BF16) for i in range(2)]
        Ks = [P.sb(st, "f_K%d" % i, [128, T], BF16) for i in range(2)]
        Vs = [P.sb(st, "f_V%d" % i, [128, 32, 65], BF16) for i in range(2)]
        ob = [P.sb(st, "f_ob%d" % i, [64, 512], BF16) for i in range(2)]
        for i in range(2):
            P.memset('pool', Qs[i][64:128, :], 0.0)
            P.memset('pool', Ks[i][64:128, :], 0.0)
            P.memset('pool', Qs[i][64:70, :], 1.0)
            P.memset('pool', Ks[i][64:70, :], 1.0)
            P.memset('pool', Vs[i][:, :, 64:65], 1.0)

        def load(h):
            Q = Qs[h % 2]; K = Ks[h % 2]; V = Vs[h % 2]
            P.dma(Q[0:64, :], A['qkT'][768 + 64 * h:768 + 64 * (h + 1), :], 'fxq%d' % (h % 2))
            P.dma(K[0:64, :], A['qkT'][1024 + 64 * h:1024 + 64 * (h + 1), :], 'fxk%d' % (h % 2), q='act')
            for i in range(3):
                P.dma(Q[64 + i:65 + i, :], cs[h:h + 1, i, :], 'fxq%d' % (h % 2))
                P.dma(K[67 + i:68 + i, :], ncs[h:h + 1, i, :], 'fxk%d' % (h % 2), q='act')
            P.dma(V[:, :, 0:64], A['vtok'][:, 512 + 64 * h:512 + 64 * (h + 1)].rearrange("(n p) d -> p n d", p=128), 'fxv%d' % (h % 2))

        load(0)
        for h in range(4):
            if h + 1 < 4:
                load(h + 1)
            Q = Qs[h % 2]; K = Ks[h % 2]; V = Vs[h % 2]
            for j in range(8):
                def fin(po, F, o=ob[j % 2], j=j, h=h):
                    P.tt('dve', o[:], po[0:64, :], F[:], ALU.mult)
                    P.dma(A['mixT'][768 + 64 * h:768 + 64 * (h + 1), j * 512:(j + 1) * 512], o[:], 'fxo%d' % (j % 2), q='sp')
                attn_chunk(cx, K, 128, Q, j, V, causal_entries(j, consts['mc']), fin)
            attn_flush(cx)

import math, os
STAGE = int(os.environ.get('STAGE', '99'))

T = 4096
LN8 = math.log(0.125)


def phase_hgrn(P, A, layer, consts):
    for ct in range(2):
        with ExitStack() as st:
            B = [P.sb(st, "hB%d" % i, [128, T], F32) for i in range(5)]
            qt = P.sb(st, "h_qt", [128, T], BF16)
            kt = P.sb(st, "h_kt", [128, 2, T], BF16)
            qg = P.sb(st, "h_qg", [128, T], BF16)
            kd = P.sb(st, "h_kd", [128, T], BF16)
            kdt = P.sb(st, "h_kdt", [128, 32, 2, 128], BF16)
            Vt = P.sb(st, "h_Vt", [128, 32, 128], BF16)
            Vz = P.sb(st, "h_Vz", [128, 32, 2, 128], BF16)
            Sbd = P.sb(st, "h_Sbd", [128, 64, 128], BF16)
            rst = P.sb(st, "h_rst", [128, T], BF16)
            sm = P.sb(st, "h_sm", [128, 8], F32)
            dl = P.sb(st, "h_dl", [128, 64], F32)
            mh = P.sb(st, "h_mh", [128, 128], BF16)
            bones = P.sb(st, "h_bones", [128, 128], BF16)
            ident = consts['ident']
            P.dma(mh[:], A['mh'], 'hg0')
            P.dma(bones[:], A['bones'], 'hg0')
            P.dma(sm[:, 0:2], A['lbl'][ct], 'hg0')
            P.dma(sm[:, 4:5], A['og'][layer], 'hg0')
            P.dma(B[0][:], A['pT'][256 + ct * 128:256 + (ct + 1) * 128, :], 'hgz')
            P.dma(B[3][:], A['pT'][ct * 128:(ct + 1) * 128, :], 'hgq', q='act')
            P.dma(Vt[:], A['vtok'][:, ct * 128:(ct + 1) * 128].rearrange("(n p) d -> p n d", p=128), 'hgv', q='pool')
            P.memset('pool', Vz[:], 0.0)
            for hh in range(2):
                P.dma(Vz[:, :, hh, hh * 64:(hh + 1) * 64],
                      A['vtok'][:, ct * 128 + hh * 64:ct * 128 + (hh + 1) * 64].rearrange("(n p) d -> p n d", p=128), 'hgv', q='pool')
            P.memset('pool', Sbd[:], 0.0)
            P.memset('pool', kt[:], 0.0)
            P.memset('pool', kdt[:], 0.0)
            P.memset('pool', rst[:], 1.0)
            P.memset('pool', rst[:].rearrange("p (c s) -> p c s", s=64)[:, :, 0:1], 0.0)
            lb = sm[:, 2:3]; oml = sm[:, 3:4]; noml = sm[:, 5:6]
            if layer == 0:
                P.memset('dve', lb, 0.0)
            else:
                P.act(sm[:, 0:2], sm[:, 0:2], AF.Exp)
                P.tt('dve', sm[:, 6:7], sm[:, 0:1], sm[:, 1:2], ALU.add)
                P.op('dve', lambda e, o=sm[:, 6:7]: e.reciprocal(out=o, in_=o), reads=[sm[:, 6:7]], writes=[sm[:, 6:7]])
                P.tt('dve', lb, sm[:, 1:2], sm[:, 6:7], ALU.mult)
            P.ts('dve', oml, lb, -1.0, 1.0, ALU.mult, ALU.add)
            P.ts('dve', noml, oml, -1.0, None, ALU.mult)
            P.act(B[0][:], B[0][:], AF.Sigmoid)
            P.ts('dve', B[1][:], B[0][:], oml, lb, ALU.mult, ALU.add)
            P.act(B[1][:], B[1][:], AF.Ln)
            P.scan(B[2][:], rst[:], B[1][:], 0.0, ALU.mult, ALU.add)
            P.ts('dve', B[1][:], B[0][:], noml, oml, ALU.mult, ALU.add)
            G3 = B[2][:].rearrange("p (c s) -> p c s", s=64)
            D3 = B[0][:].rearrange("p (c s) -> p c s", s=64)
            P.tt('dve', D3, G3, G3[:, :, 31:32].broadcast_to([128, 64, 64]), ALU.subtract)
            P.act(B[4][:], B[0][:], AF.Exp, bias=LN8)
            P.tt('dve', qt[:], B[3][:], B[4][:], ALU.mult)
            P.act(B[4][:], B[0][:], AF.Exp, scale=-1.0)
            P.tt('dve', kt[0:64, 0, :], B[1][0:64, :], B[4][0:64, :], ALU.mult)
            P.tt('dve', kt[64:128, 1, :], B[1][64:128, :], B[4][64:128, :], ALU.mult)
            P.act(B[4][:], B[2][:], AF.Exp, bias=LN8)
            P.tt('dve', qg[:], B[3][:], B[4][:], ALU.mult)
            P.tt('dve', D3, G3[:, :, 63:64].broadcast_to([128, 64, 64]), G3, ALU.subtract)
            P.act(B[4][:], B[0][:], AF.Exp)
            P.tt('dve', kd[:], B[1][:], B[4][:], ALU.mult)
            P.act(dl[:].unsqueeze(2), G3[:, :, 63:64], AF.Exp)
            P.memset('dve', dl[:, 0:1], 0.0)
            KV = B[0]; dfull = B[1]; Sall = B[3]; oT = B[4]
            if STAGE < 1:
                P.dma(A['mixT'][0:128, 0:T], kd[:], 'dbg'); continue
            with ExitStack() as s2:
                ptr = [P.ps(s2, "h_ptr%d" % i, [128, 8, 128], BF16) for i in range(2)]
                for g in range(4):
                    for i in range(8):
                        tl = g * 8 + i
                        P.transpose(ptr[g % 2][:, i, :], kd[:, tl * 128:(tl + 1) * 128], ident[:])
                    P.copy('act', kdt[0:64, g * 8:(g + 1) * 8, 0, :], ptr[g % 2][0:64, :, :])
                    P.copy('dve', kdt[64:128, g * 8:(g + 1) * 8, 1, :], ptr[g % 2][64:128, :, :])
            with ExitStack() as s2:
                pkv = [P.ps(s2, "h_pkv%d" % i, [128, 4, 128], F32) for i in range(2)]
                KV3 = KV[:].rearrange("p (v c) -> p v c", c=64)
                for g in range(16):
                    pk = pkv[g % 2]
                    for i in range(4):
                        c = g * 4 + i
                        tl = c // 2; hf = c % 2
                        P.mm(pk[:, i, :], kdt[:, tl, hf, :], Vt[:, tl, :], start=True, stop=True)
                    for hh in range(2):
                        P.copy('act' if hh else 'dve', KV3[hh * 64:(hh + 1) * 64, :, g * 4:(g + 1) * 4],
                               pk[hh * 64:(hh + 1) * 64, :, hh * 64:(hh + 1) * 64].rearrange("p g v -> p v g"))
            if STAGE < 2:
                P.dma(A['mixT'][0:128, 0:T], kd[:], 'dbg'); continue
            P.copy('pool', dfull[:].rearrange("p (v c) -> p v c", c=64), dl[:].unsqueeze(1).broadcast_to([128, 64, 64]))
            P.scan(Sall[:], dfull[:], KV[:], 0.0, ALU.mult, ALU.add)
            S3 = Sall[:].rearrange("p (v c) -> p v c", c=64)
            for hh in range(2):
                P.copy('dve' if hh else 'act', Sbd[hh * 64:(hh + 1) * 64, 1:64, hh * 64:(hh + 1) * 64],
                       S3[hh * 64:(hh + 1) * 64, :, 0:63].rearrange("p v c -> p c v"))
            if STAGE < 3:
                P.dma(A['mixT'][0:128, 0:T], kd[:], 'dbg'); continue
            with ExitStack() as s2:
                pA = [P.ps(s2, "h_pA%d" % i, [128, 128], F32) for i in range(4)]
                po = [P.ps(s2, "h_po%d" % i, [128, 128], F32) for i in range(2)]
                Am = [P.sb(s2, "h_Am%d" % i, [128, 128], BF16) for i in range(4)]
                def scores(tl):
                    cols = slice(tl * 128, (tl + 1) * 128)
                    for hh in range(2):
                        i = (tl % 2) * 2 + hh
                        P.mm(pA[i][:], kt[:, hh, cols], qt[:, cols], start=True, stop=True)
                        P.tt('dve', Am[i][:], pA[i][:], mh[:], ALU.mult)

                def outs(tl):
                    cols = slice(tl * 128, (tl + 1) * 128)
                    p_ = po[tl % 2]
                    P.mm(p_[:], Vz[:, tl, 0, :], Am[(tl % 2) * 2][:], start=True, stop=False)
                    P.mm(p_[:], Vz[:, tl, 1, :], Am[(tl % 2) * 2 + 1][:], start=False, stop=False)
                    P.mm(p_[:, 0:64], Sbd[:, 2 * tl, :], qg[:, tl * 128:tl * 128 + 64], start=False, stop=False)
                    P.mm(p_[:, 64:128], Sbd[:, 2 * tl + 1, :], qg[:, tl * 128 + 64:tl * 128 + 128], start=False, stop=True)
                    P.copy('act', oT[:, cols], p_[:])

                for tl in range(33):
                    if tl < 32:
                        scores(tl)
                    if tl >= 1:
                        outs(tl - 1)
            if STAGE < 4:
                P.dma(A['mixT'][0:128, 0:T], kd[:], 'dbg'); continue
            with ExitStack() as s2:
                pss = [P.ps(s2, "h_pss%d" % i, [128, 512], F32) for i in range(2)]
                sq = [P.sb(s2, "h_sq%d" % i, [128, 512], BF16) for i in range(2)]
                rs = [P.sb(s2, "h_rs%d" % i, [128, 512], F32) for i in range(2)]
                ag = [P.sb(s2, "h_ag%d" % i, [128, 512], F32) for i in range(2)]
                ob = [P.sb(s2, "h_ob%d" % i, [128, 512], BF16) for i in range(2)]
                for j in range(8):
                    b = j % 2
                    cols = slice(j * 512, (j + 1) * 512)
                    P.dma(ag[b][:], A['pT'][512 + ct * 128:512 + (ct + 1) * 128, cols], 'hga%d' % b)
                    P.act(sq[b][:], oT[:, cols], AF.Square)
                    P.mm(pss[b][:], bones[:], sq[b][:], start=True, stop=True)
                    P.act(rs[b][:], pss[b][:], AF.Sqrt, bias=1e-6, scale=1.0 / 64)
                    P.op('dve', lambda e, o=rs[b][:]: e.reciprocal(out=o, in_=o), reads=[rs[b][:]], writes=[rs[b][:]])
                    P.act(ag[b][:], ag[b][:], AF.Silu)
                    P.stt('dve', rs[b][:], oT[:, cols], sm[:, 4:5], rs[b][:], ALU.mult, ALU.mult)
                    P.tt('pool', ob[b][:], rs[b][:], ag[b][:], ALU.mult)
                    P.dma(A['mixT'][ct * 128:(ct + 1) * 128, cols], ob[b][:], 'hgo%d' % b, q='pool')

import os
STAGE = int(os.environ.get('STAGE', '99'))

T = 4096
NEG = -30000.0


def phase_nsa(P, A, layer, consts):
    c = consts
    ident = c['ident']
    with ExitStack() as st:
        cx = AttnCtx(P, st, consts)
        with ExitStack() as s0:
            lng = P.sb(s0, "n_lng", [24, T], F32)
            P.dma(lng[:], A['pT'][1024:1048, :], 'ns0')
            P.act(lng[:], lng[:], AF.Sigmoid)
            P.dma(A['gsig'], lng[:], 'ns0')
        lng2 = A['gsig']
        ovaug = P.sb(st, "n_ov", [128, 2, 72], BF16)
        wc = P.sb(st, "n_wc", [128, 3200], BF16)
        addm = P.sb(st, "n_addm", [128, 32, 64], F32)
        P.dma(ovaug[:], A['ovaug'], 'ns0')
        P.dma(wc[:], A['wc'], 'ns0')
        P.dma(addm[:], A['addmask'], 'ns0')
        kcTs = [P.sb(st, "n_kcT%d" % i, [128, 256], BF16) for i in range(2)]
        vcAs = [P.sb(st, "n_vcA%d" % i, [128, 2, 65], BF16) for i in range(2)]
        for g in range(2):
            kcT = kcTs[g]; vcA = vcAs[g]
            with ExitStack() as s2:
                w1 = P.sb(s2, "n_w1", [64, 32, 128], BF16)
                w1f = P.sb(s2, "n_w1f", [64, 32, 128], F32)
                w2 = P.sb(s2, "n_w2", [128, 64], BF16)
                w2f = P.sb(s2, "n_w2f", [128, 64], F32)
                posT = P.sb(s2, "n_posT", [64, 32], BF16)
                posf = P.sb(s2, "n_posf", [64, 32], F32)
                posb = P.sb(s2, "n_posb", [64, 32, 256], BF16)
                srcf = P.sb(s2, "n_srcf", [64, T], F32)
                srcb = P.sb(s2, "n_srcb", [64, T], BF16)
                bias = P.sb(s2, "n_bias", [128, 1], F32)
                xb = P.sb(s2, "n_xb", [128, 256], F32)
                x2 = P.sb(s2, "n_x2", [128, 256], F32)
                hid = P.sb(s2, "n_hid", [128, 256], BF16)
                ktm = P.sb(s2, "n_ktm", [128, 64], F32)
                kts = P.sb(s2, "n_kts", [128, 64], F32)
                ktb = P.sb(s2, "n_ktb", [128, 128], BF16)
                sm = P.sb(s2, "n_sm", [128, 4], F32)
                rt = P.sb(s2, "n_rt", [128, 4, 8], F32)
                kng = P.sb(s2, "n_kng", [128, 64], F32)
                cosc = P.sb(s2, "n_cosc", [128, 2, 8], F32)
                sinc = P.sb(s2, "n_sinc", [128, 2, 8], F32)
                ph = cx.psS[0]; pb = cx.psS[1]; po = cx.psO[0]
                pt = P.ps(s2, "n_pt", [128, 128], BF16)
                P.dma(kng[:], A['kng'][layer].partition_broadcast(128), 'ns1')
                P.dma(cosc[:], A['cosc'], 'ns1')
                P.dma(sinc[:], A['sinc'], 'ns1')
                P.memset('dve', hid[:], 0.0)
                P.memset('dve', vcA[:], 0.0)
                P.memset('dve', kcT[:], 0.0)
                P.memset('dve', ktb[:], 0.0)
                for which in range(2):
                    P.dma(w1f[:], A['w1r'][layer, which], 'ns2')
                    P.dma(w2f[:], A['w2'][layer, which], 'ns2')
                    P.dma(posf[:], A['posT'][layer, which], 'ns2')
                    P.dma(srcf[:], A['pT'][768 + 128 * which + 64 * g:768 + 128 * which + 64 * (g + 1), :], 'ns3', q='act')
                    P.copy('dve', w1[:], w1f[:])
                    P.copy('dve', w2[:], w2f[:])
                    P.copy('dve', posT[:], posf[:])
                    P.copy('dve', posb[:], posT[:].unsqueeze(2).broadcast_to([64, 32, 256]))
                    P.copy('act', srcb[:], srcf[:])
                    for l in range(32):
                        P.mm(ph[:, 0:255], w1[:, l, :], srcb[:].rearrange("p (n s) -> p n s", s=16)[:, (l // 16):(l // 16) + 255, l % 16], start=(l == 0), stop=False)
                    for l in range(32):
                        P.mm(ph[:, 0:255], w1[:, l, :], posb[:, l, 0:255], start=False, stop=(l == 31))
                    P.copy('act', xb[:, 0:255], ph[:, 0:255])
                    P.tt('dve', x2[:, 0:255], xb[:, 0:255], xb[:, 0:255], ALU.mult)
                    P.ts('dve', x2[:, 0:255], x2[:, 0:255], 0.044715, 1.0, ALU.mult, ALU.add)
                    P.tt('dve', x2[:, 0:255], x2[:, 0:255], xb[:, 0:255], ALU.mult)
                    P.act(x2[:, 0:255], x2[:, 0:255], AF.Sigmoid, scale=1.5957691216057308)
                    P.tt('dve', hid[:, 0:255], x2[:, 0:255], xb[:, 0:255], ALU.mult)
                    for nt in range(2):
                        P.mm(po[:, 0:64], hid[:, nt * 128:(nt + 1) * 128], w2[:], start=True, stop=True)
                        if which == 1:
                            nr = 128 if nt == 0 else 127
                            P.copy('act', vcA[0:nr, nt, 0:64], po[0:nr, 0:64])
                            P.memset('dve', vcA[0:nr, nt, 64:65], 1.0)
                        else:
                            P.act(kts[:], po[:, 0:64], AF.Square, accum_out=sm[:, 0:1])
                            P.act(sm[:, 1:2], sm[:, 0:1], AF.Sqrt, bias=1e-6, scale=1.0 / 64)
                            P.op('dve', lambda e, o=sm[:, 1:2]: e.reciprocal(out=o, in_=o), reads=[sm[:, 1:2]], writes=[sm[:, 1:2]])
                            P.stt('dve', ktm[:], po[:, 0:64], sm[:, 1:2], kng[:], ALU.mult, ALU.mult)
                            P.copy('act', ktb[:, 0:64], ktm[:])
                            P.tt('dve', rt[:, 0, :], ktm[:, 0:8], cosc[:, nt, :], ALU.mult)
                            P.tt('dve', rt[:, 1, :], ktm[:, 8:16], sinc[:, nt, :], ALU.mult)
                            P.tt('dve', rt[:, 2, :], ktm[:, 8:16], cosc[:, nt, :], ALU.mult)
                            P.tt('dve', rt[:, 3, :], ktm[:, 0:8], sinc[:, nt, :], ALU.mult)
                            P.tt('dve', ktb[:, 0:8], rt[:, 0, :], rt[:, 1, :], ALU.subtract)
                            P.tt('dve', ktb[:, 8:16], rt[:, 2, :], rt[:, 3, :], ALU.add)
                            P.transpose(pt[:], ktb[:], ident[:])
                            P.copy('dve', kcT[0:64, nt * 128:(nt + 1) * 128], pt[0:64, :])
            P.memset('dve', kcT[0:64, 255:256], 0.0)
        selT = P.sb(st, "n_selT", [128, T], BF16)
        imp = P.sb(st, "n_imp", [128, 32, 64], F32)
        acc = [P.sb(st, "n_acc%d" % i, [64, T], F32) for i in range(4)]
        Q = [P.sb(st, "n_Q%d" % i, [128, T], BF16) for i in range(4)]
        Ks = P.sb(st, "n_Ks", [128, T], BF16)
        Kw = P.sb(st, "n_Kw", [128, T], BF16)
        Vs = P.sb(st, "n_Vs", [128, 32, 65], BF16)
        Vw = P.sb(st, "n_Vw", [128, 32, 65], BF16)
        for g in range(2):
            kcT = kcTs[g]; vcA = vcAs[g]
            for hh in range(4):
                h = 4 * g + hh
                P.memset('pool', Q[hh][64:128, :], 0.0)
                P.dma(Q[hh][0:64, :], A['qkT'][64 * h:64 * (h + 1), :], 'nsq%d' % hh)
            P.dma(Ks[0:64, :], A['qkT'][512 + 64 * g:512 + 64 * (g + 1), :], 'nsk')
            P.dma(Ks[64:128, :], A['eall'], 'nsk')
            P.dma(Kw[0:64, :], A['qkT'][640 + 64 * g:640 + 64 * (g + 1), :], 'nsk')
            P.memset('pool', Kw[64:128, :], 0.0)
            P.memset('pool', Vs[:, :, 64:65], 1.0)
            P.memset('pool', Vw[:, :, 64:65], 1.0)
            P.dma(Vs[:, :, 0:64], A['vtok'][:, 256 + 64 * g:256 + 64 * (g + 1)].rearrange("(n p) d -> p n d", p=128), 'nsv', q='act')
            P.dma(Vw[:, :, 0:64], A['vtok'][:, 384 + 64 * g:384 + 64 * (g + 1)].rearrange("(n p) d -> p n d", p=128), 'nsv', q='act')
            with ExitStack() as s2:
                pimp = [P.ps(s2, "n_pimp%d" % i, [128, 4, 72], F32) for i in range(2)]
                pTc = [P.sb(s2, "n_pTc%d" % i, [128, 512], BF16) for i in range(3)]
                rinvs = [P.sb(s2, "n_rinv%d" % i, [128, 4], F32) for i in range(2)]
                kc_ = 0
                kch = 0
                for hh in range(4):
                    h = 4 * g + hh
                    for j in range(8):
                        q0 = j * 512
                        po = cx.psO[cx.kO % 2]
                        cx.kO += 1
                        pim = pimp[kch % 2]
                        rinv = rinvs[kch % 2]
                        kch += 1
                        tiles = []
                        for nt in range(2):
                            off = 2048 * nt + 31 - 512 * j
                            if -off + 511 < 0:
                                continue
                            tiles.append((nt, off))
                        for ti, (nt, off) in enumerate(tiles):
                            ps = cx.psS[cx.kS % 2]
                            cx.kS += 1
                            ptc = pTc[kc_ % 3]
                            kc_ += 1
                            first = (ti == 0)
                            last = (ti == len(tiles) - 1)

                            def s_fn(ps=ps, nt=nt, off=off, first=first, po=po, pim=pim, hh=hh, q0=q0):
                                full = (-off >= 2032)
                                P.mm(ps[:], kcT[:, nt * 128:(nt + 1) * 128], Q[hh][:, q0:q0 + 512], start=True, stop=full)
                                if not full:
                                    ci0 = -off + 511
                                    P.mm(ps[:], ident[:], wc[:, ci0:ci0 + 512], start=False, stop=True)

                            def exp_fn(ps=ps, ptc=ptc):
                                P.act(ptc[:], ps[:], AF.Exp)

                            def pv_fn(po=po, pim=pim, ptc=ptc, nt=nt, last=last, first=first):
                                P.mm(po[0:65, :], vcA[:, nt, :], ptc[:], start=first, stop=last)
                                for m in range(4):
                                    P.mm(pim[:, m, :], ptc[:, m * 128:(m + 1) * 128], ovaug[:, nt, :], start=(first and m == 0), stop=last)

                            _push_block(cx, s_fn, exp_fn, pv_fn, first=first)

                        def fin(po_, F, hh=hh, q0=q0, pim=pim, rinv=rinv, j=j):
                            for m in range(4):
                                tq = j * 4 + m
                                if hh == 0:
                                    P.ts('dve', imp[:, tq, :], pim[:, m, 0:64], rinv[:, m:m + 1], None, ALU.mult)
                                else:
                                    P.stt('dve', imp[:, tq, :], pim[:, m, 0:64], rinv[:, m:m + 1], imp[:, tq, :], ALU.mult, ALU.add)
                            P.tt('dve', acc[hh][:, q0:q0 + 512], po_[0:64, :], F[:], ALU.mult)

                        ea, eb = _mk_factor(cx, po, lng2, h, q0, fin)

                        def ea2(ea=ea, pim=pim, rinv=rinv):
                            ea()
                            P.ts('dve', rinv[:, 0:4].unsqueeze(2), pim[:, :, 64:65], 1e-30, None, ALU.max)
                            P.op('dve', lambda e, o=rinv[:, 0:4]: e.reciprocal(out=o, in_=o), reads=[rinv[:, 0:4]], writes=[rinv[:, 0:4]])

                        _end_chunk(cx, ea2, eb)
                attn_flush(cx)
            if STAGE < 2:
                continue
            with ExitStack() as s2:
                wk = [P.sb(s2, "n_wk%d" % i, [128, 64], F32) for i in range(2)]
                w2_ = [P.sb(s2, "n_wk2%d" % i, [128, 64], F32) for i in range(2)]
                m8 = [P.sb(s2, "n_m8%d" % i, [128, 16], F32) for i in range(2)]
                sb_ = [P.sb(s2, "n_sb%d" % i, [128, 128], BF16) for i in range(2)]
                pts = [P.ps(s2, "n_pts%d" % i, [128, 128], BF16) for i in range(2)]
                P.memset('pool', sb_[0][:], 0.0)
                P.memset('pool', sb_[1][:], 0.0)
                for tq in range(32):
                    b = tq % 2
                    P.tt('dve', wk[b][:], imp[:, tq, :], addm[:, tq, :], ALU.add)
                    P.op('dve', lambda e, o=m8[b][:, 0:8], i=wk[b][:]: e.max(out=o, in_=i), reads=[wk[b][:]], writes=[m8[b][:, 0:8]])
                    P.op('dve', lambda e, o=w2_[b][:], r=m8[b][:, 0:8], i=wk[b][:]: e.match_replace(out=o, in_to_replace=r, in_values=i, imm_value=-3.0e38),
                         reads=[m8[b][:, 0:8], wk[b][:]], writes=[w2_[b][:]])
                    P.op('dve', lambda e, o=m8[b][:, 8:16], i=w2_[b][:]: e.max(out=o, in_=i), reads=[w2_[b][:]], writes=[m8[b][:, 8:16]])
                    P.ts('dve', w2_[b][:], wk[b][:], m8[b][:, 15:16], None, ALU.is_ge)
                    P.ts('dve', wk[b][:], wk[b][:], -5.0e29, None, ALU.is_gt)
                    P.tt('dve', wk[b][:], wk[b][:], w2_[b][:], ALU.mult)
                    P.ts('dve', sb_[b][:, 64:128], wk[b][:], -1.0, -NEG, ALU.add, ALU.mult)
                    P.transpose(pts[b][:], sb_[b][:], ident[:])
                    P.copy('act', selT[64:128, tq * 128:(tq + 1) * 128], pts[b][64:128, :])
            for hh in range(4):
                P.dma(Q[hh][64:128, :], selT[64:128, :], 'nsq%d' % hh, q='sp' if hh % 2 else 'act')
            if STAGE < 3:
                continue
            with ExitStack() as s2:
                tmp = [P.sb(s2, "n_tmp%d" % i, [64, 512], F32) for i in range(2)]
                ob = [P.sb(s2, "n_ob%d" % i, [64, 512], BF16) for i in range(1)] * 2
                for hh in range(4):
                    h = 4 * g + hh
                    for j in range(8):
                        q0 = j * 512
                        a = acc[hh][:, q0:q0 + 512]

                        def fin_s(po_, F, a=a):
                            P.tt('dve', tmp[0][:], po_[0:64, :], F[:], ALU.mult)
                            P.tt('pool', a, a, tmp[0][:], ALU.add)

                        def fin_w(po_, F, a=a, j=j, h=h, q0=q0):
                            P.tt('dve', tmp[1][:], po_[0:64, :], F[:], ALU.mult)
                            P.tt('pool', ob[j % 2][:], a, tmp[1][:], ALU.add)
                            if STAGE >= 5:
                                P.dma(A['mixT'][256 + 64 * h:256 + 64 * (h + 1), q0:q0 + 512], ob[j % 2][:], 'nso%d' % (j % 2), q='sp')

                        attn_chunk(cx, Ks, 128, Q[hh], j, Vs, causal_entries(j, c['mc']), fin_s, lng2=lng2, gate_c=8 + h)
                        if STAGE >= 4:
                            attn_chunk(cx, Kw, 128, Q[hh], j, Vw, window_entries(j, c['mc'], c['mu']), fin_w, lng2=lng2, gate_c=16 + h)
                attn_flush(cx)


T = 4096
D = 1024
FF = 4096


def phase_wo(P, A, layer, consts, x_in, x_mid, per_tile=None):
    ident = consts['ident']
    with ExitStack() as st:
        Wo = P.sb(st, "wo", [128, 8, D], BF16)
        with ExitStack() as s2:
            wst = [P.sb(s2, "wost%d" % i, [128, D], F32) for i in range(2)]
            for kc in range(8):
                b = wst[kc % 2]
                P.dma(b[:], A['wo'][layer, kc * 128:(kc + 1) * 128, :], 'wost%d' % (kc % 2))
                P.copy('act' if kc % 2 else 'dve', Wo[:, kc, :], b[:])
        mx = [P.sb(st, "wo_mx%d" % i, [128, 8, 512], BF16) for i in range(2)]
        xt = [P.sb(st, "wo_xt%d" % i, [128, D], F32) for i in range(2)]
        xm = [P.sb(st, "wo_xm%d" % i, [128, D], F32) for i in range(2)]
        sq = P.sb(st, "wo_sq", [128, D], BF16)
        hb = [P.sb(st, "wo_hb%d" % i, [128, D], BF16) for i in range(2)]
        ss = [P.sb(st, "wo_ss%d" % i, [128, 2], F32) for i in range(2)]
        hst = [P.sb(st, "wo_hst%d" % i, [128, 8, 512], BF16) for i in range(1)] * 2
        po = [P.ps(st, "wo_po%d" % i, [128, 512], F32) for i in range(4)]
        ptr = [P.ps(st, "wo_ptr%d" % i, [128, 8, 128], BF16) for i in range(2)]
        for t in range(32):
            j = t // 4
            b = t % 2
            if t % 4 == 0:
                P.dma(mx[j % 2][:], A['mixT'][:, j * 512:(j + 1) * 512].rearrange("(a p) n -> p a n", p=128), 'womx%d' % (j % 2))
            P.dma(xt[b][:], x_in[t * 128:(t + 1) * 128, :], 'woxt%d' % b, q='act')
            for half in range(2):
                pp = po[(t % 2) * 2 + half]
                for kc in range(8):
                    P.mm(pp[:], mx[j % 2][:, kc, (t % 4) * 128:(t % 4 + 1) * 128], Wo[:, kc, half * 512:(half + 1) * 512],
                         start=(kc == 0), stop=(kc == 7))
                P.tt('dve', xm[b][:, half * 512:(half + 1) * 512], pp[:], xt[b][:, half * 512:(half + 1) * 512], ALU.add)
            P.dma(x_mid[t * 128:(t + 1) * 128, :], xm[b][:], 'woxm%d' % b, q='pool')
            P.act(sq[:], xm[b][:], AF.Square, accum_out=ss[b][:, 0:1])
            P.act(ss[b][:, 1:2], ss[b][:, 0:1], AF.Sqrt, bias=1e-6, scale=1.0 / D)
            P.op('dve', lambda e, o=ss[b][:, 1:2]: e.reciprocal(out=o, in_=o), reads=[ss[b][:, 1:2]], writes=[ss[b][:, 1:2]])
            P.ts('dve', hb[b][:], xm[b][:], ss[b][:, 1:2], None, ALU.mult)
            for kc in range(8):
                P.transpose(ptr[b][:, kc, :], hb[b][:, kc * 128:(kc + 1) * 128], ident[:])
            P.copy('act', hst[j % 2][:, :, (t % 4) * 128:(t % 4 + 1) * 128], ptr[b][:])
            if t % 4 == 3:
                P.dma(A['h2T'][:, j * 512:(j + 1) * 512].rearrange("(a p) n -> p a n", p=128), hst[j % 2][:], 'wohst%d' % (j % 2), q='pool')
            if per_tile is not None:
                per_tile(t)


def phase_wo_ffn(P, A, layer, consts, x_in, x_mid, x_out):
    with ExitStack() as st:
        Wu = P.sb(st, "wu", [128, 8, FF], BF16)
        Wd = P.sb(st, "wd", [128, 32, D], BF16)
        g2 = P.sb(st, "g2", [128, 8], F32)
        P.dma(g2[:], A['g2'][layer], 'ff0')
        with ExitStack() as s2:
            wst = [P.sb(s2, "fwst%d" % i, [128, 1024], F32) for i in range(2)]

            def per_tile(t):
                for u in range(2):
                    ci = 2 * t + u
                    b = u
                    if ci < 32:
                        kc, qt = ci // 4, ci % 4
                        P.dma(wst[b][:], A['wup'][layer, kc * 128:(kc + 1) * 128, qt * 1024:(qt + 1) * 1024], 'fwst%d' % b, q='sp')
                        if u:
                            P.ts('dve', Wu[:, kc, qt * 1024:(qt + 1) * 1024], wst[b][:], g2[:, kc:kc + 1], None, ALU.mult)
                        else:
                            P.act(Wu[:, kc, qt * 1024:(qt + 1) * 1024], wst[b][:], AF.Copy, scale=g2[:, kc:kc + 1])
                    else:
                        fc = ci - 32
                        P.dma(wst[b][:], A['wdn'][layer, fc * 128:(fc + 1) * 128, :], 'fwst%d' % b, q='sp')
                        P.copy('dve' if u else 'act', Wd[:, fc, :], wst[b][:])

            phase_wo(P, A, layer, consts, x_in, x_mid, per_tile=per_tile)
        _ffn_body(P, st, A, Wu, Wd, x_mid, x_out)


def phase_ffn(P, A, layer, consts, x_mid, x_out):
    with ExitStack() as st:
        Wu = P.sb(st, "wu", [128, 8, FF], BF16)
        Wd = P.sb(st, "wd", [128, 32, D], BF16)
        g2 = P.sb(st, "g2", [128, 8], F32)
        P.dma(g2[:], A['g2'][layer], 'ff0')
        with ExitStack() as s2:
            wst = [P.sb(s2, "fwst%d" % i, [128, 2048], F32) for i in range(3)]
            k = 0
            for kc in range(8):
                for hf in range(2):
                    b = k % 3
                    P.dma(wst[b][:], A['wup'][layer, kc * 128:(kc + 1) * 128, hf * 2048:(hf + 1) * 2048], 'fwst%d' % b, q='sp' if k % 2 else 'act')
                    if k % 2:
                        P.ts('dve', Wu[:, kc, hf * 2048:(hf + 1) * 2048], wst[b][:], g2[:, kc:kc + 1], None, ALU.mult)
                    else:
                        P.act(Wu[:, kc, hf * 2048:(hf + 1) * 2048], wst[b][:], AF.Copy, scale=g2[:, kc:kc + 1])
                    k += 1
            for fc2 in range(16):
                b = k % 3
                P.dma(wst[b][:].rearrange("p (a n) -> p a n", a=2), A['wdn'][layer, fc2 * 256:(fc2 + 1) * 256, :].rearrange("(a p) n -> p a n", p=128),
                      'fwst%d' % b, q='sp' if k % 2 else 'act')
                P.copy('dve' if k % 2 else 'act', Wd[:, fc2 * 2:(fc2 + 1) * 2, :], wst[b][:].rearrange("p (a n) -> p a n", a=2))
                k += 1
        _ffn_body(P, st, A, Wu, Wd, x_mid, x_out)


def _ffn_body(P, st, A, Wu, Wd, x_mid, x_out):
    if True:
        h2 = [P.sb(st, "ff_h2%d" % i, [128, 8, 512], BF16) for i in range(2)]
        uT = P.sb(st, "ff_uT", [128, 32, 512], BF16)
        rl = [P.sb(st, "ff_rl%d" % i, [128, 512], F32) for i in range(2)]
        xt = [P.sb(st, "ff_xt%d" % i, [128, D], F32) for i in range(2)]
        xo = [P.sb(st, "ff_xo%d" % i, [128, D], F32) for i in range(2)]
        pu = [P.ps(st, "ff_pu%d" % i, [128, 512], F32) for i in range(3)]
        pd = [P.ps(st, "ff_pd%d" % i, [128, 512], F32) for i in range(4)]
        ku = 0
        for j in range(8):
            P.dma(h2[j % 2][:], A['h2T'][:, j * 512:(j + 1) * 512].rearrange("(a p) n -> p a n", p=128), 'ffh2%d' % (j % 2))
            for fc in range(32):
                pp = pu[ku % 3]
                r = rl[ku % 2]
                for kc in range(8):
                    P.mm(pp[:], Wu[:, kc, fc * 128:(fc + 1) * 128], h2[j % 2][:, kc, :], start=(kc == 0), stop=(kc == 7))
                P.act(r[:], pp[:], AF.Relu)
                P.tt('dve' if ku % 2 else 'pool', uT[:, fc, :], r[:], r[:], ALU.mult)
                ku += 1
            for tt in range(4):
                t = j * 4 + tt
                b = t % 2
                P.dma(xt[b][:], x_mid[t * 128:(t + 1) * 128, :], 'ffxt%d' % b, q='act')
                for half in range(2):
                    pp = pd[(t % 2) * 2 + half]
                    for fc in range(32):
                        P.mm(pp[:], uT[:, fc, tt * 128:(tt + 1) * 128], Wd[:, fc, half * 512:(half + 1) * 512], start=(fc == 0), stop=(fc == 31))
                    P.tt('dve', xo[b][:, half * 512:(half + 1) * 512], pp[:], xt[b][:, half * 512:(half + 1) * 512], ALU.add)
                P.dma(x_out[t * 128:(t + 1) * 128, :], xo[b][:], 'ffxo%d' % b, q='pool')

import ml_dtypes
from concourse.bass_utils import run_bass_kernel_spmd

T=4096; D=1024
OFF = {}
_names = ['aq','af','ai','ag','bq','bkc','bvc','bks','bvs','bkw','bvw','bg','cq','ck','cv','cf']
_sizes = [256,256,256,256,512,128,128,128,128,128,128,24,256,256,256,4]
_o = 0
for n_, s_ in zip(_names, _sizes):
    OFF[n_] = (_o, _o + s_); _o += s_
TOK_ORDER = ['bq','bks','bkw','cq','ck','ai','bvs','bvw','cv']
T_ORDER = ['aq','af','ag','bkc','bvc','bg','cf']

def win_layout(w_in):
    L = w_in.shape[0]
    out = np.zeros((L, 1024, 2048 + 1152), np.float32)
    c = 0
    for n_ in TOK_ORDER:
        a, b = OFF[n_]; out[:, :, c:c + b - a] = w_in[:, :, a:b]; c += b - a
    assert c == 2048
    for n_ in T_ORDER:
        a, b = OFF[n_]; out[:, :, c:c + b - a] = w_in[:, :, a:b]; c += b - a
    return out

def rope_tables():
    inv = np.power(np.float32(500000.0), -np.arange(0, 16, 2, dtype=np.float32) / 16).astype(np.float32)
    pos = np.arange(T, dtype=np.float32)
    ang = pos[:, None] * inv[None, :]
    cos = np.cos(ang).astype(np.float32); sin = np.sin(ang).astype(np.float32)
    return (np.ascontiguousarray(cos.reshape(32, 128, 8).transpose(1, 0, 2)),
            np.ascontiguousarray(sin.reshape(32, 128, 8).transpose(1, 0, 2)))

def _skip():
    pass

def const_inputs():
    k = np.arange(128)[:, None]; q = np.arange(128)[None, :]
    mc = np.where(k <= q, 0.0, -30000.0).astype(ml_dtypes.bfloat16)
    mu = np.where(k > q, 0.0, -30000.0).astype(ml_dtypes.bfloat16)
    selneg = np.zeros((24, 24 * 64), np.float32)
    for c in range(24):
        selneg[c, c * 64:(c + 1) * 64] = -1.0
    return dict(ident=np.eye(128, dtype=ml_dtypes.bfloat16), mc=mc, mu=mu, selneg=selneg)

def _unused_ref_proj(inp, layer, x):
    x = x.astype(np.float64)
    h = x / np.sqrt((x * x).mean(-1, keepdims=True) + 1e-6) * inp['norm1_g'][layer]
    return h @ inp['w_in'][layer].astype(np.float64)

def hgrn_consts(inp):
    s = np.arange(128)[:, None]; t = np.arange(128)[None, :]
    mh = ((s // 64 == t // 64) & (s <= t)).astype(ml_dtypes.bfloat16)
    bones = (s // 64 == t // 64).astype(ml_dtypes.bfloat16)
    lbl = np.ascontiguousarray(inp['hgrn_lb_logits'].reshape(2, 2, 128).transpose(1, 2, 0)).astype(np.float32)
    og = np.tile(inp['hgrn_onorm_g'], (1, 2)).reshape(2, 128, 1).astype(np.float32)
    return dict(mh=mh, bones=bones, lbl=lbl, og=og)

def _unused_ref_hgrn(inp, layer, proj):
    def sl(n): a, b = OFF[n]; return proj[:, a:b]
    lbp = np.exp(inp['hgrn_lb_logits'].astype(np.float64)); lbp /= lbp.sum(0, keepdims=True)
    lb_all = np.cumsum(lbp, 0) - lbp[0:1]
    lb = lb_all[layer].reshape(4, 64)
    z = sl('af').reshape(T, 4, 64)
    sig = 1 / (1 + np.exp(-z))
    f = lb + (1 - lb) * sig; logf = np.log(f); k = (1 - lb) * (1 - sig)
    q = sl('aq').reshape(T, 4, 64) * 0.125; v = sl('ai').reshape(T, 4, 64)
    o = np.zeros((T, 4, 64))
    for h in range(4):
        S = np.zeros((64, 64))
        for c in range(64):
            r = slice(c * 64, (c + 1) * 64)
            G = np.cumsum(logf[r, h], 0)
            qc, kc, vc = q[r, h], k[r, h], v[r, h]
            o_inter = (qc * np.exp(G)) @ S
            diff = G[:, None, :] - G[None, :, :]
            mask = np.tril(np.ones((64, 64), bool))
            dec = np.where(mask[:, :, None], np.exp(np.minimum(diff, 0)), 0)
            sc = np.einsum('tk,sk,tsk->ts', qc, kc, dec)
            o[r, h] = o_inter + sc @ vc
            S = S * np.exp(G[-1])[:, None] + (kc * np.exp(G[-1] - G)).T @ vc
    g = sl('ag').reshape(T, 4, 64)
    gate = g / (1 + np.exp(-g))
    on = o / np.sqrt((o * o).mean(-1, keepdims=True) + 1e-6) * inp['hgrn_onorm_g'][layer]
    return (on * gate).reshape(T, 256)

def nsa_consts(inp):
    n_cmp = 255
    ci = np.arange(n_cmp)[:, None]; sj = np.arange(64)[None, :]
    ov = ((ci * 16 <= sj * 64 + 63) & (ci * 16 + 31 >= sj * 64)).astype(np.float32)
    ovaug = np.zeros((256, 72), np.float32); ovaug[:255, :64] = ov; ovaug[:255, 64] = 1.0
    ovaug = np.ascontiguousarray(ovaug.reshape(2, 128, 72).transpose(1, 0, 2)).astype(ml_dtypes.bfloat16)
    nl = np.arange(128)[:, None]; cc = np.arange(3200)[None, :] - 511
    wc = np.where(cc >= 16 * nl, 0.0, -30000.0).astype(ml_dtypes.bfloat16)
    eall = (np.arange(T)[None, :] // 64 == np.arange(64)[:, None]).astype(ml_dtypes.bfloat16)
    q = np.arange(T)[:, None]; j = np.arange(64)[None, :]; cur = q // 64
    am = np.zeros((T, 64), np.float32)
    am[(j == 0) | (j == cur) | (j == cur - 1)] = 1e30
    am[np.broadcast_to(j > cur, am.shape)] = -1e30
    addmask = np.ascontiguousarray(am.reshape(32, 128, 64).transpose(1, 0, 2))
    inv = np.power(np.float32(500000.0), -np.arange(0, 16, 2, dtype=np.float32) / 16).astype(np.float32)
    pos = (np.arange(256, dtype=np.float32) * 16 + 31)
    ang = pos[:, None] * inv[None, :]
    cosc = np.ascontiguousarray(np.cos(ang).astype(np.float32).reshape(2, 128, 8).transpose(1, 0, 2))
    sinc = np.ascontiguousarray(np.sin(ang).astype(np.float32).reshape(2, 128, 8).transpose(1, 0, 2))
    w1r = np.ascontiguousarray(inp['nsa_cmp_w1'].reshape(2, 2, 32, 64, 128).transpose(0, 1, 3, 2, 4)).astype(np.float32)
    posT = np.ascontiguousarray(inp['nsa_cmp_pos'].transpose(0, 1, 3, 2)).astype(np.float32)
    return dict(ovaug=ovaug, wc=wc, eall=eall, addmask=addmask, cosc=cosc, sinc=sinc, w1r=w1r, posT=posT,
                w2=inp['nsa_cmp_w2'].astype(np.float32), kng=inp['nsa_kn_g'].astype(np.float32))

NSA_SHAPES = [('ovaug', [128, 2, 72], BF16), ('wc', [128, 3200], BF16), ('eall', [64, 4096], BF16), ('addmask', [128, 32, 64], F32),
              ('cosc', [128, 2, 8], F32), ('sinc', [128, 2, 8], F32), ('w1r', [2, 2, 64, 32, 128], F32), ('posT', [2, 2, 64, 32], F32),
              ('w2', [2, 2, 128, 64], F32), ('kng', [2, 64], F32)]


import ml_dtypes
from concourse.bass_utils import run_bass_kernel_spmd

_IN_SHAPES = [('x', [T, D], F32), ('win', [2, D, WCOLS], F32), ('g1', [2, 128, 8], F32), ('gq', [2, 1280], F32),
              ('cos', [128, 32, 8], F32), ('sin', [128, 32, 8], F32), ('ident', [128, 128], BF16), ('mc', [128, 128], BF16),
              ('mu', [128, 128], BF16), ('selneg', [24, 1536], F32), ('mh', [128, 128], BF16), ('bones', [128, 128], BF16),
              ('lbl', [2, 128, 2], F32), ('og', [2, 128, 1], F32), ('fb', [2, 4, 1], F32), ('wo', [2, 1024, 1024], F32),
              ('wup', [2, 1024, 4096], F32), ('wdn', [2, 4096, 1024], F32), ('g2', [2, 128, 8], F32)] + NSA_SHAPES

KDEPTH = int(os.environ.get('KDEPTH', '2'))
KPHASES = os.environ.get('KPHASES', '1hnfwf')


def _body(P):
    nc = P.nc
    A = {}
    for k_, shp, dt_ in _IN_SHAPES:
        A[k_] = nc.dram_tensor(k_, shp, dt_, kind="ExternalInput").ap()
    A['y'] = nc.dram_tensor("y", [T, D], F32, kind="ExternalOutput").ap()
    A['qkT'] = nc.dram_tensor("qkT", [1280, T], BF16).ap()
    A['vtok'] = nc.dram_tensor("vtok", [T, 768], BF16).ap()
    A['pT'] = nc.dram_tensor("pT", [TC, T], F32).ap()
    A['mixT'] = nc.dram_tensor("mixT", [1024, T], BF16).ap()
    A['h2T'] = nc.dram_tensor("h2T", [1024, T], BF16).ap()
    A['gsig'] = nc.dram_tensor("gsig", [24, T], F32).ap()
    xm = nc.dram_tensor("xmid", [T, D], F32).ap()
    x1 = nc.dram_tensor("x1", [T, D], F32).ap()
    xin = A['x']
    for layer in range(KDEPTH):
        A['x'] = xin
        with ExitStack() as st:
            phase1(P, st, A, layer)
        with ExitStack() as st:
            consts = load_consts(P, st, A)
            if 'h' in KPHASES:
                phase_hgrn(P, A, layer, consts)
            if 'n' in KPHASES:
                phase_nsa(P, A, layer, consts)
            if 'f' in KPHASES:
                phase_fox(P, A, layer, consts)
            xout = x1 if layer < KDEPTH - 1 else A['y']
            phase_wo_ffn(P, A, layer, consts, xin, xm, xout)
        xin = xout


def _host_inputs(inp):
    cos, sin = rope_tables()
    gq = np.concatenate([np.tile(inp['nsa_qn_g'], (1, 8)), np.tile(inp['nsa_kn_g'], (1, 4)), np.tile(inp['fox_qn_g'], (1, 4)),
                         np.tile(inp['fox_kn_g'], (1, 4))], axis=1).astype(np.float32)
    base = {"win": win_layout(inp['w_in']), "g1": np.ascontiguousarray(inp['norm1_g'].reshape(2, 8, 128).transpose(0, 2, 1)),
            "g2": np.ascontiguousarray(inp['norm2_g'].reshape(2, 8, 128).transpose(0, 2, 1)),
            "gq": gq, "cos": cos, "sin": sin, "fb": inp['fox_fb'].reshape(2, 4, 1).astype(np.float32),
            "wo": inp['w_o'], "wup": inp['w_up'], "wdn": inp['w_down']}
    base.update(const_inputs()); base.update(hgrn_consts(inp)); base.update(nsa_consts(inp))
    return base


def kernel(**inp):
    inp = {k: np.asarray(v) for k, v in inp.items()}
    nc, plan = build_two_pass(lambda: bass.Bass("TRN2", target_bir_lowering=False), _body)
    base = _host_inputs(inp)
    in_maps = []
    for b in range(8):
        m = dict(base); m['x'] = np.ascontiguousarray(inp['x'][b]); in_maps.append(m)
    res = run_bass_kernel_spmd(nc, in_maps, core_ids=list(range(8)))
    return np.stack([r['y'] for r in res.results], axis=0).astype(np.float32)
```

```python
import numpy as np, sys, time, os, math
import numpy as np
from contextlib import ExitStack
import concourse.bass as bass
import concourse.mybir as mybir

F32 = mybir.dt.float32
BF16 = mybir.dt.bfloat16
AF = mybir.ActivationFunctionType
ALU = mybir.AluOpType
AX = mybir.AxisListType


def _box(ap):
    t = ap.tensor
    dims = ap.ap
    off = int(ap.offset)
    shp = tuple(t.shape)
    rowsize = 1
    for s in shp[1:]:
        rowsize *= int(s)
    r0 = off // rowsize
    f0 = off % rowsize
    rows = 0
    free = 0
    for (st, cnt) in dims:
        st = int(st); cnt = int(cnt)
        if cnt <= 1 or st == 0:
            continue
        if st % rowsize == 0:
            rows += (st // rowsize) * (cnt - 1)
        else:
            free += st * (cnt - 1)
    return t.name, (r0, r0 + rows, f0, f0 + free)


def _ov(a, b):
    return a[0] <= b[1] and b[0] <= a[1] and a[2] <= b[3] and b[2] <= a[3]


def _cont(a, b):
    return a[0] <= b[0] and b[1] <= a[1] and a[2] <= b[2] and b[3] <= a[3]


class Prog:
    def __init__(self, nc, plan=None):
        self.nc = nc
        self.plan = plan
        self.rec = plan is None
        self.eng = dict(pe=nc.tensor, dve=nc.vector, act=nc.scalar, pool=nc.gpsimd, sp=nc.sync)
        self.n = 0
        self.ins = []
        self.track = {}
        self.lane_cnt = {}
        self.freed = {}
        self.uid = 0
        self.stack = ExitStack()
        self.psum_rr = 0
        self.psum_banks = []
        if not self.rec:
            self.sem = {}
            for e in ['pe', 'dve', 'act', 'pool']:
                self.sem[e] = self.stack.enter_context(nc.semaphore("sem_" + e))
            self.lane_sem = {}
            for ln in plan['lanes']:
                self.lane_sem[ln] = self.stack.enter_context(nc.semaphore("ln_" + ln))

    def sb(self, st, name, shape, dtype):
        self.uid += 1
        name = "%s_%d" % (name, self.uid)
        t = st.enter_context(self.nc.sbuf_tensor("s_" + name, list(shape), dtype))
        st.callback(self._free, "s_" + name)
        return t

    def ps(self, st, name, shape, dtype=F32):
        self.uid += 1
        name = "%s_%d" % (name, self.uid)
        t = st.enter_context(self.nc.psum_tensor("p_" + name, list(shape), dtype))
        st.callback(self._free, "p_" + name)
        return t

    def _free(self, name):
        if not self.rec:
            return
        recs = self.track.pop(name, [])
        for (b, i, w) in recs:
            r = self.ins[i]
            key = ('l', r['lane'], i) if r['dma'] else ('e', r['eng'])
            if r['dma']:
                self.freed[key] = i
            else:
                self.freed[key] = max(self.freed.get(key, -1), i)

    def _access(self, idx, eng, dma, ap, write, deps):
        name, box = _box(ap)
        if name not in self.track:
            big = (0, 10 ** 9, 0, 10 ** 9)
            kind = ap.space
            self.track[name] = [] if str(kind) == 'DRAM' else [(big, i, True) for i in sorted(set(self.freed.values()))]
        recs = self.track[name]
        for (b, i, w) in recs:
            if (write or w) and _ov(b, box):
                deps.append((i, (w and not write)))
        if write:
            recs[:] = [r for r in recs if not _cont(box, r[0])]
        elif not dma:
            recs[:] = [r for r in recs if not ((not r[2]) and r[1] < len(self.ins) and self.ins[r[1]]['eng'] == eng
                                               and not self.ins[r[1]]['dma'] and _cont(box, r[0]))]
        recs.append((box, idx, write))

    def op(self, eng, fn, reads=(), writes=(), dma=False, lane=None):
        idx = self.n
        self.n += 1
        if self.rec:
            deps = []
            for ap in reads:
                self._access(idx, eng, dma, ap, False, deps)
            for ap in writes:
                self._access(idx, eng, dma, ap, True, deps)
            lanewaits = {}
            d2 = {}
            for (j, raw) in deps:
                if j == idx:
                    continue
                pj = self.ins[j]
                if pj['dma']:
                    ln = pj['lane']
                    lanewaits[ln] = max(lanewaits.get(ln, 0), pj['lane_val_at'])
                    lanewaits[ln] = max(lanewaits[ln], self.lane_cnt[ln])
                    continue
                if pj['eng'] == eng and not dma:
                    if eng == 'pe':
                        continue
                    if not raw and eng != 'pool':
                        continue
                d2[j] = True
            rec = dict(eng=eng, deps=list(d2.keys()), lanewaits=lanewaits, dma=dma, lane=lane)
            if dma:
                self.lane_cnt[lane] = self.lane_cnt.get(lane, 0) + 16
                rec['lane_val_at'] = self.lane_cnt[lane]
            self.ins.append(rec)
            return None
        else:
            info = self.plan['ins'][idx]
            e = self.eng[eng]
            for (sname, val) in info['waits']:
                s = self.sem[sname[1]] if sname[0] == 'e' else self.lane_sem[sname[1]]
                e.wait_ge(s, val)
            inst = fn(e)
            if dma:
                inst.then_inc(self.lane_sem[lane], 16)
            elif info['signal']:
                inst.then_inc(self.sem[eng], 1)
            return inst

    def make_plan(self):
        ins = self.ins
        signal = [False] * len(ins)
        for r in ins:
            for j in r['deps']:
                signal[j] = True
        cnt = dict(pe=0, dve=0, act=0, pool=0, sp=0)
        sigval = [0] * len(ins)
        for i, r in enumerate(ins):
            if signal[i] and not r['dma']:
                cnt[r['eng']] += 1
                sigval[i] = cnt[r['eng']]
        seen = {e: {} for e in cnt}
        out = []
        for i, r in enumerate(ins):
            need = {}
            for j in r['deps']:
                k = ('e', ins[j]['eng'])
                need[k] = max(need.get(k, 0), sigval[j])
            for ln, v in r['lanewaits'].items():
                k = ('l', ln)
                need[k] = max(need.get(k, 0), v)
            waits = []
            sd = seen[r['eng']]
            for k, v in need.items():
                if sd.get(k, 0) >= v:
                    continue
                sd[k] = v
                waits.append((k, v))
            out.append(dict(waits=waits, signal=signal[i]))
        return dict(ins=out, lanes=sorted(self.lane_cnt.keys()), lane_final=dict(self.lane_cnt))

    def finish(self):
        if self.rec:
            return
        for ln, v in self.plan['lane_final'].items():
            self.nc.sync.wait_ge(self.lane_sem[ln], v)

    def dma(self, out, in_, lane, q='sp', **kw):
        return self.op(q, lambda e: e.dma_start(out=out, in_=in_, **kw), reads=[in_], writes=[out],
                       dma=True, lane=lane)

    def mm(self, out, lhsT, rhs, start=True, stop=True, **kw):
        return self.op('pe', lambda e: e.matmul(out, lhsT, rhs, start=start, stop=stop, **kw),
                       reads=[lhsT, rhs], writes=[out])

    def transpose(self, out, in_, ident):
        return self.op('pe', lambda e: e.transpose(out, in_, ident), reads=[in_, ident], writes=[out])

    def act(self, out, in_, func, bias=None, scale=None, accum_out=None, eng='act'):
        reads = [in_]
        kw = {}
        if bias is not None:
            kw['bias'] = bias
            if not isinstance(bias, (int, float)):
                reads.append(bias)
        if scale is not None:
            kw['scale'] = scale
            if not isinstance(scale, (int, float)):
                reads.append(scale)
        writes = [out]
        if accum_out is not None:
            kw['accum_out'] = accum_out
            writes.append(accum_out)
        return self.op(eng, lambda e: e.activation(out=out, in_=in_, func=func, **kw), reads=reads, writes=writes)

    def tt(self, eng, out, in0, in1, op):
        return self.op(eng, lambda e: e.tensor_tensor(out=out, in0=in0, in1=in1, op=op), reads=[in0, in1], writes=[out])

    def ts(self, eng, out, in0, s1, s2, op0, op1=None, accum_out=None):
        reads = [in0]
        if not isinstance(s1, (int, float)):
            reads.append(s1)
        if s2 is not None and not isinstance(s2, (int, float)):
            reads.append(s2)
        kw = {}
        writes = [out]
        if op1 is not None:
            kw['op1'] = op1
        if accum_out is not None:
            kw['accum_out'] = accum_out
            writes.append(accum_out)
        return self.op(eng, lambda e: e.tensor_scalar(out=out, in0=in0, scalar1=s1, scalar2=s2, op0=op0, **kw),
                       reads=reads, writes=writes)

    def stt(self, eng, out, in0, scalar, in1, op0, op1):
        reads = [in0, in1]
        if not isinstance(scalar, (int, float)):
            reads.append(scalar)
        return self.op(eng, lambda e: e.scalar_tensor_tensor(out=out, in0=in0, scalar=scalar, in1=in1, op0=op0, op1=op1),
                       reads=reads, writes=[out])

    def copy(self, eng, out, in_):
        if eng == 'act':
            return self.op(eng, lambda e: e.copy(out=out, in_=in_), reads=[in_], writes=[out])
        return self.op(eng, lambda e: e.tensor_copy(out=out, in_=in_), reads=[in_], writes=[out])

    def memset(self, eng, ap, val):
        return self.op(eng, lambda e: e.memset(ap, val), reads=[], writes=[ap])

    def scan(self, out, d0, d1, initial, op0, op1):
        reads = [d0, d1]
        if not isinstance(initial, (int, float)):
            reads.append(initial)
        return self.op('dve', lambda e: e.tensor_tensor_scan(out=out, data0=d0, data1=d1, initial=initial, op0=op0, op1=op1),
                       reads=reads, writes=[out])

    def generic(self, eng, fn, reads, writes):
        return self.op(eng, fn, reads=reads, writes=writes)


def build_two_pass(make_nc, body):
    nc1 = make_nc()
    p1 = Prog(nc1, None)
    body(p1)
    p1.stack.close()
    plan = p1.make_plan()
    nc2 = make_nc()
    p2 = Prog(nc2, plan)
    body(p2)
    p2.finish()
    p2.stack.close()
    return nc2, plan


T = 4096
NT = 32
D = 1024
KC = 8
TOKC = 2048
TC = 1152
WCOLS = TOKC + TC
EPS = 1e-6


def phase1(P, st, A, layer):
    nc = P.nc
    s = ExitStack()
    W = P.sb(s, "w_in", [128, KC, WCOLS], BF16)
    hT = P.sb(s, "hT", [128, KC, T], BF16)
    ident = P.sb(s, "ident", [128, 128], BF16)
    g1 = P.sb(s, "g1", [128, KC], F32)
    G = P.sb(s, "Gq", [128, 1280], F32)
    cos = P.sb(s, "cos", [128, NT, 8], F32)
    sin = P.sb(s, "sin", [128, NT, 8], F32)
    P.dma(ident[:], A['ident'], 'c0')
    P.dma(g1[:], A['g1'][layer], 'c0')
    P.dma(G[:], A['gq'][layer].partition_broadcast(128), 'c0')
    P.dma(cos[:], A['cos'], 'c0')
    P.dma(sin[:], A['sin'], 'c0')
    P.ts('dve', G[:, 0:512], G[:, 0:512], 0.125, None, ALU.mult)
    P.ts('dve', G[:, 768:1024], G[:, 768:1024], 0.125, None, ALU.mult)

    with ExitStack() as s2:
        wst = [P.sb(s2, "wst%d" % i, [128, WCOLS], F32) for i in range(4)]
        for kc in range(KC):
            b = wst[kc % 4]
            P.dma(b[:], A['win'][layer, kc * 128:(kc + 1) * 128, :], 'wst%d' % (kc % 4), q='sp' if kc % 2 == 0 else 'act')
            half = WCOLS // 2
            P.ts('dve', W[:, kc, 0:half], b[:, 0:half], g1[:, kc:kc + 1], None, ALU.mult)
            P.act(W[:, kc, half:WCOLS], b[:, half:WCOLS], AF.Copy, scale=g1[:, kc:kc + 1])

    with ExitStack() as s2:
        xt = [P.sb(s2, "xt%d" % i, [128, D], F32) for i in range(2)]
        sq = P.sb(s2, "sqj", [128, D], F32)
        hb = [P.sb(s2, "hb%d" % i, [128, D], BF16) for i in range(2)]
        ss = [P.sb(s2, "ss%d" % i, [128, 2], F32) for i in range(2)]
        ptr = [P.ps(s2, "ptr%d" % i, [128, KC, 128], BF16) for i in range(2)]
        for t in range(NT):
            b = t % 2
            P.dma(xt[b][:], A['x'][t * 128:(t + 1) * 128, :], 'xt%d' % b)
            P.act(sq[:], xt[b][:], AF.Square, accum_out=ss[b][:, 0:1])
            P.act(ss[b][:, 1:2], ss[b][:, 0:1], AF.Sqrt, bias=EPS_AP(P), scale=1.0 / D)
            P.op('dve', lambda e, o=ss[b][:, 1:2]: e.reciprocal(out=o, in_=o), reads=[ss[b][:, 1:2]], writes=[ss[b][:, 1:2]])
            P.ts('dve', hb[b][:], xt[b][:], ss[b][:, 1:2], None, ALU.mult)
            for kc in range(KC):
                P.transpose(ptr[b][:, kc, :], hb[b][:, kc * 128:(kc + 1) * 128], ident[:])
            P.copy('act' if t % 2 else 'dve', hT[:, :, t * 128:(t + 1) * 128], ptr[b][:])

    with ExitStack() as s2:
        pp = [P.ps(s2, "ppT%d" % i, [128, 512], F32) for i in range(3)]
        so = [P.sb(s2, "soT%d" % i, [128, 512], F32) for i in range(3)]
        k = 0
        for c in range(TC // 128):
            for j in range(T // 512):
                b = k % 3
                for kc in range(KC):
                    P.mm(pp[b][:], W[:, kc, TOKC + c * 128:TOKC + (c + 1) * 128], hT[:, kc, j * 512:(j + 1) * 512],
                         start=(kc == 0), stop=(kc == KC - 1))
                P.copy('act' if k % 2 else 'dve', so[b][:], pp[b][:])
                P.dma(A['pT'][c * 128:(c + 1) * 128, j * 512:(j + 1) * 512], so[b][:], 'soT%d' % b, q='pool')
                k += 1

    with ExitStack() as s2:
        pg = [P.ps(s2, "pg%d" % i, [128, 512], F32) for i in range(4)]
        ptq = [P.ps(s2, "ptq%d" % i, [128, 4, 128], BF16) for i in range(3)]
        sqhs = [P.sb(s2, "sqh%d" % i, [128, 512], F32) for i in range(3)]
        ssh = [P.sb(s2, "ssh%d" % i, [128, 8], F32) for i in range(4)]
        xn = [P.sb(s2, "xn%d" % i, [128, 512], F32) for i in range(3)]
        qb = [P.sb(s2, "qb%d" % i, [128, 512], BF16) for i in range(3)]
        rts = [P.sb(s2, "rt%d" % i, [128, 4, 8, 8], F32) for i in range(3)]
        qst = [P.sb(s2, "qst%d" % i, [128, 10, 512], BF16) for i in range(2)]
        vst = [P.sb(s2, "vst%d" % i, [128, 768], BF16) for i in range(2)]
        groups = []
        kq = 0
        for t in range(NT):
            for gi in range(4):
                k = t * 4 + gi
                nh = [8, 8, 4, 0][gi]
                qi = None
                if nh:
                    qi = kq % 3
                    kq += 1
                groups.append((t, gi, k, qi))

        def stage(sidx, t, gi, k, q):
            sb_ = (t // 4) % 2
            b = k % 4
            nh = [8, 8, 4, 0][gi]
            nr = [8, 4, 0, 0][gi]
            w = nh * 64
            goff = [0, 512, 1024, 0][gi]
            vb = t % 2
            if sidx == 0:
                for kc in range(KC):
                    P.mm(pg[b][:], hT[:, kc, t * 128:(t + 1) * 128], W[:, kc, gi * 512:(gi + 1) * 512],
                         start=(kc == 0), stop=(kc == KC - 1))
                return
            if sidx == 1:
                if nh:
                    sqh = sqhs[q]
                    P.act(sqh[:, 0:w], pg[b][:, 0:w], AF.Square)
                    P.op('dve', lambda e, o=ssh[b][:, 0:nh], i=sqh[:, 0:w].rearrange("p (h d) -> p h d", d=64):
                         e.tensor_reduce(out=o, in_=i, axis=AX.X, op=ALU.add),
                         reads=[sqh[:, 0:w]], writes=[ssh[b][:, 0:nh]])
                if gi == 2:
                    P.copy('act', vst[vb][:, 0:256], pg[b][:, 256:512])
                if gi == 3:
                    P.copy('act', vst[vb][:, 256:768], pg[b][:, 0:512])
                    P.dma(A['vtok'][t * 128:(t + 1) * 128, :], vst[vb][:], 'vst%d' % vb, q='sp')
                return
            if not nh:
                return
            rt = rts[q]
            xv = xn[q][:, 0:max(nr, 1) * 64].rearrange("p (h d) -> p h d", d=64)
            qv = qb[q][:, 0:max(nr, 1) * 64].rearrange("p (h d) -> p h d", d=64)
            if sidx == 2:
                P.act(ssh[b][:, 0:nh], ssh[b][:, 0:nh], AF.Sqrt, bias=EPS_AP(P), scale=1.0 / 64)
                P.op('dve', lambda e, o=ssh[b][:, 0:nh]: e.reciprocal(out=o, in_=o), reads=[ssh[b][:, 0:nh]], writes=[ssh[b][:, 0:nh]])
                P.tt('dve', xn[q][:, 0:w].rearrange("p (h d) -> p h d", d=64),
                     pg[b][:, 0:w].rearrange("p (h d) -> p h d", d=64),
                     ssh[b][:, 0:nh].unsqueeze(2).broadcast_to([128, nh, 64]), ALU.mult)
            elif sidx == 3:
                if nr:
                    P.tt('pool', xn[q][:, 0:w], xn[q][:, 0:w], G[:, goff:goff + w], ALU.mult)
                    P.copy('act', qb[q][:, 0:w], xn[q][:, 0:w])
                    cb = cos[:, t, :].unsqueeze(1).broadcast_to([128, nr, 8])
                    sb2 = sin[:, t, :].unsqueeze(1).broadcast_to([128, nr, 8])
                    P.tt('dve', rt[:, 0, 0:nr, :], xv[:, :, 0:8], cb, ALU.mult)
                    P.tt('dve', rt[:, 1, 0:nr, :], xv[:, :, 8:16], sb2, ALU.mult)
                    P.tt('pool', rt[:, 2, 0:nr, :], xv[:, :, 8:16], cb, ALU.mult)
                    P.tt('pool', rt[:, 3, 0:nr, :], xv[:, :, 0:8], sb2, ALU.mult)
                else:
                    P.tt('pool', qb[q][:, 0:w], xn[q][:, 0:w], G[:, goff:goff + w], ALU.mult)
            elif sidx == 4:
                if nr:
                    P.tt('dve', qv[:, :, 0:8], rt[:, 0, 0:nr, :], rt[:, 1, 0:nr, :], ALU.subtract)
                    P.tt('pool', qv[:, :, 8:16], rt[:, 2, 0:nr, :], rt[:, 3, 0:nr, :], ALU.add)
                npair = nh // 2
                for pr in range(npair):
                    P.transpose(ptq[q][:, pr, :], qb[q][:, pr * 128:(pr + 1) * 128], ident[:])
            elif sidx == 5:
                npair = nh // 2
                pbase = [0, 4, 8][gi]
                P.copy('act' if gi % 2 else 'dve', qst[sb_][:, pbase:pbase + npair, (t % 4) * 128:(t % 4 + 1) * 128], ptq[q][:, 0:npair, :])
                if t % 4 == 3 and gi == 2:
                    j = t // 4
                    P.dma(A['qkT'][:, j * 512:(j + 1) * 512].rearrange("(a p) n -> p a n", p=128), qst[sb_][:], 'qst%d' % sb_, q='sp')

        NS = 6
        for step in range(len(groups) + NS - 1):
            for sidx in range(NS - 1, -1, -1):
                gidx = step - sidx
                if 0 <= gidx < len(groups):
                    stage(sidx, *groups[gidx])
    s.close()


_eps_cache = {}


def EPS_AP(P):
    return EPS


T = 4096
NEG = -30000.0


class AttnCtx:
    def __init__(self, P, st, consts):
        self.P = P
        self.psS = [P.ps(st, "aS%d" % i, [128, 1024], F32) for i in range(2)]
        self.psO = [P.ps(st, "aO%d" % i, [128, 512], F32) for i in range(2)]
        self.psB = [P.ps(st, "aB%d" % i, [128, 512], F32) for i in range(1)]
        self.pT = [P.sb(st, "apT%d" % i, [128, 1024], BF16) for i in range(3)]
        self.lr = [P.sb(st, "alr%d" % i, [65, 512], F32) for i in range(2)]
        self.F = [P.sb(st, "aF%d" % i, [64, 512], F32) for i in range(2)]
        self.lrh = [P.sb(st, "alrh%d" % i, [128, 512], BF16) for i in range(2)]
        self.lrl = [P.sb(st, "alrl%d" % i, [128, 512], BF16) for i in range(2)]
        self.G2 = [P.sb(st, "aG%d" % i, [64, 512], F32) for i in range(2)]
        for t_ in self.lrh + self.lrl:
            P.memset('pool', t_[:], 0.0)
        self.kF_ids = {id(g): i for i, g in enumerate(self.G2)}
        self.kS = 0
        self.kO = 0
        self.kF = 0
        self.vm = 65
        self.prev = None
        self.deferred = []
        self.c = consts


def _push_block(cx, s_fn, exp_fn, pv_fn, first=False):
    if first:
        for f in cx.deferred:
            f()
        cx.deferred = []
    s_fn()
    d = cx.deferred
    cx.deferred = []
    if cx.prev is not None:
        e, p, epi = cx.prev
        e()
        p()
        if epi is not None:
            epi[0]()
            cx.deferred.append(epi[1])
    for f in d:
        f()
    cx.prev = (exp_fn, pv_fn, None)


def _end_chunk(cx, epi_a, epi_b):
    cx.prev = (cx.prev[0], cx.prev[1], (epi_a, epi_b))


def attn_flush(cx):
    d = cx.deferred
    cx.deferred = []
    if cx.prev is not None:
        e, p, epi = cx.prev
        e()
        p()
        if epi is not None:
            epi[0]()
            d.append(epi[1])
        cx.prev = None
    for f in d:
        f()


def _mk_block(cx, po, Kaug, kr, Qaug, q0, Vaug, kt, lo, hi, masks, extra, first, last):
    P = cx.P
    c = cx.c
    ps = cx.psS[cx.kS % 2]
    pt = cx.pT[cx.kS % 3]
    cx.kS += 1

    def s_fn():
        P.mm(ps[:, lo:hi], Kaug[0:kr, kt * 128:(kt + 1) * 128], Qaug[0:kr, q0 + lo:q0 + hi], start=True, stop=(len(masks) == 0 and extra is None))
        if extra is not None:
            P.mm(ps[:, lo:hi], extra[0][0:64, kt * 128:(kt + 1) * 128], extra[1][0:64, q0 + lo:q0 + hi], start=False, stop=(len(masks) == 0))
        for mi, (mk, m) in enumerate(masks):
            P.mm(ps[:, m * 128:(m + 1) * 128], c['ident'][:], mk[:], start=False, stop=(mi == len(masks) - 1))

    def exp_fn():
        P.act(pt[:, lo:hi], ps[:, lo:hi], AF.Exp)

    def pv_fn():
        P.mm(po[0:cx.vm, lo:hi], Vaug[:, kt, 0:cx.vm], pt[:, lo:hi], start=first, stop=last)

    return s_fn, exp_fn, pv_fn


def _mk_pair(cx, po, Kaug, kr, Qaug, q0, Vaug, kta, ktb, first, last):
    P = cx.P
    ps = cx.psS[cx.kS % 2]
    pt = cx.pT[cx.kS % 3]
    cx.kS += 1

    def s_fn():
        P.mm(ps[:, 0:512], Kaug[0:kr, kta * 128:(kta + 1) * 128], Qaug[0:kr, q0:q0 + 512], start=True, stop=True)
        P.mm(ps[:, 512:1024], Kaug[0:kr, ktb * 128:(ktb + 1) * 128], Qaug[0:kr, q0:q0 + 512], start=True, stop=True)

    def exp_fn():
        P.act(pt[:, 0:1024], ps[:, 0:1024], AF.Exp)

    def pv_fn():
        P.mm(po[0:cx.vm, 0:512], Vaug[:, kta, 0:cx.vm], pt[:, 0:512], start=first, stop=False)
        P.mm(po[0:cx.vm, 0:512], Vaug[:, ktb, 0:cx.vm], pt[:, 512:1024], start=False, stop=last)

    return s_fn, exp_fn, pv_fn


def _mk_factor(cx, po, lng2, gate_c, q0, finish):
    P = cx.P
    c = cx.c
    lr = cx.lr[cx.kF % 2]
    F = cx.F[cx.kF % 2]
    G2 = cx.G2[cx.kF % 2]
    cx.kF += 1
    pb = cx.psB[0]

    lrh = cx.lrh[(cx.kF - 1) % 2]
    lrl = cx.lrl[(cx.kF - 1) % 2]

    def epi_a():
        if lng2 is not None:
            P.dma(G2[:], lng2[gate_c, q0:q0 + 512].partition_broadcast(64), 'ag%d' % ((cx.kF_ids[id(G2)])), q='sp')
        P.ts('dve', lr[64:65, :], po[64:65, :], 1e-18, None, ALU.max)
        P.act(lr[64:65, :], lr[64:65, :], AF.Ln)
        P.copy('dve', lrh[64:65, :], lr[64:65, :])
        P.tt('dve', lrl[64:65, :], lr[64:65, :], lrh[64:65, :], ALU.subtract)

    def epi_b():
        P.mm(pb[:, :], c['negonesb'][:, :], lrh[:, :], start=True, stop=False)
        P.mm(pb[:, :], c['negonesb'][:, :], lrl[:, :], start=False, stop=True)
        P.act(F[:], pb[0:64, :], AF.Exp)
        if lng2 is not None:
            P.tt('pool', F[:], F[:], G2[:], ALU.mult)
        finish(po, F)

    return epi_a, epi_b


def attn_chunk(cx, Kaug, kr, Qaug, j, Vaug, entries, finish, lng2=None, gate_c=None, extra=None):
    po = cx.psO[cx.kO % 2]
    cx.kO += 1
    q0 = j * 512
    n = len(entries)
    ei = 0
    while ei < n:
        kt, lo, hi, masks = entries[ei]
        full = (lo == 0 and hi == 512 and not masks and extra is None)
        if full and ei + 1 < n:
            kt2, lo2, hi2, masks2 = entries[ei + 1]
            if lo2 == 0 and hi2 == 512 and not masks2:
                fns = _mk_pair(cx, po, Kaug, kr, Qaug, q0, Vaug, kt, kt2, ei == 0, ei + 1 == n - 1)
                _push_block(cx, *fns, first=(ei == 0))
                ei += 2
                continue
        fns = _mk_block(cx, po, Kaug, kr, Qaug, q0, Vaug, kt, lo, hi, masks, extra, ei == 0, ei == n - 1)
        _push_block(cx, *fns, first=(ei == 0))
        ei += 1
    ea, eb = _mk_factor(cx, po, lng2, gate_c, q0, finish)
    _end_chunk(cx, ea, eb)


def causal_entries(j, mc):
    ent = []
    for kt in range(4 * j + 4):
        if kt < 4 * j:
            ent.append((kt, 0, 512, []))
        else:
            m = kt - 4 * j
            ent.append((kt, 128 * m, 512, [(mc, m)]))
    return ent


def window_entries(j, mc, mu):
    ent = []
    for cc in range(-4, 4):
        kt = 4 * j + cc
        if kt < 0:
            continue
        lo = 128 * max(cc, 0)
        hi = 128 * (min(cc + 4, 3) + 1)
        masks = []
        if 0 <= cc <= 3:
            masks.append((mc, cc))
        if 0 <= cc + 4 <= 3:
            masks.append((mu, cc + 4))
        ent.append((kt, lo, hi, masks))
    return ent


def load_consts(P, st, A):
    c = {}
    c['ident'] = P.sb(st, "c_ident", [128, 128], BF16)
    c['mc'] = P.sb(st, "c_mc", [128, 128], BF16)
    c['mu'] = P.sb(st, "c_mu", [128, 128], BF16)
    c['zeros'] = P.sb(st, "c_zeros", [128, 128], BF16)
    c['ident_w'] = P.sb(st, "c_identw", [128, 512], BF16)
    c['negones'] = P.sb(st, "c_negones", [65, 64], F32)
    c['negonesb'] = P.sb(st, "c_negonesb", [128, 128], BF16)
    P.dma(c['ident'][:], A['ident'], 'c0')
    P.dma(c['mc'][:], A['mc'], 'c0')
    P.dma(c['mu'][:], A['mu'], 'c0')
    P.memset('dve', c['zeros'][:], 0.0)
    P.memset('dve', c['ident_w'][:], 0.0)
    P.memset('dve', c['negones'][:], -1.0)
    P.memset('dve', c['negonesb'][:], -1.0)
    return c


def phase_fox(P, A, layer, consts):
    with ExitStack() as st:
        cx = AttnCtx(P, st, consts)
        cf = P.sb(st, "f_cf", [4, T], F32)
        tmp = P.sb(st, "f_tmp", [4, T], F32)
        ones = P.sb(st, "f_ones", [4, T], F32)
        fb = P.sb(st, "f_fb", [4, 2], F32)
        cs = P.sb(st, "f_cs", [4, 3, T], BF16)
        ncs = P.sb(st, "f_ncs", [4, 3, T], BF16)
        P.dma(cf[:], A['pT'][1048:1052, :], 'fx0')
        P.dma(fb[:, 0:1], A['fb'][layer], 'fx0')
        P.ts('dve', fb[:, 1:2], fb[:, 0:1], -1.0, None, ALU.mult)
        P.memset('pool', ones[:], 1.0)
        P.act(tmp[:], cf[:], AF.Exp, bias=fb[:, 1:2], scale=-1.0)
        P.act(tmp[:], tmp[:], AF.Ln, bias=1.0)
        P.scan(cf[:], ones[:], tmp[:], 0.0, ALU.mult, ALU.subtract)
        P.copy('dve', cs[:, 0, :], cf[:])
        P.tt('dve', tmp[:], cf[:], cs[:, 0, :], ALU.subtract)
        P.copy('dve', cs[:, 1, :], tmp[:])
        P.tt('dve', tmp[:], tmp[:], cs[:, 1, :], ALU.subtract)
        P.copy('dve', cs[:, 2, :], tmp[:])
        P.ts('dve', ncs[:].rearrange("p a t -> p (a t)"), cs[:].rearrange("p a t -> p (a t)"), -1.0, None, ALU.mult)
        Qs = [P.sb(st, "f_Q%d" % i, [128, T], BF16) for i in range(2)]
        Ks = [P.sb(st, "f_K%d" % i, [128, T], BF16) for i in range(2)]
        Vs = [P.sb(st, "f_V%d" % i, [128, 32, 65], BF16) for i in range(2)]
        ob = [P.sb(st, "f_ob%d" % i, [64, 512], BF16) for i in range(2)]
        for i in range(2):
            P.memset('pool', Qs[i][64:128, :], 0.0)
            P.memset('pool', Ks[i][64:128, :], 0.0)
            P.memset('pool', Qs[i][64:70, :], 1.0)
            P.memset('pool', Ks[i][64:70, :], 1.0)
            P.memset('pool', Vs[i][:, :, 64:65], 1.0)

        def load(h):
            Q = Qs[h % 2]; K = Ks[h % 2]; V = Vs[h % 2]
            P.dma(Q[0:64, :], A['qkT'][768 + 64 * h:768 + 64 * (h + 1), :], 'fxq%d' % (h % 2))
            P.dma(K[0:64, :], A['qkT'][1024 + 64 * h:1024 + 64 * (h + 1), :], 'fxk%d' % (h % 2), q='act')
            for i in range(3):
                P.dma(Q[64 + i:65 + i, :], cs[h:h + 1, i, :], 'fxq%d' % (h % 2))
                P.dma(K[67 + i:68 + i, :], ncs[h:h + 1, i, :], 'fxk%d' % (h % 2), q='act')
            P.dma(V[:, :, 0:64], A['vtok'][:, 512 + 64 * h:512 + 64 * (h + 1)].rearrange("(n p) d -> p n d", p=128), 'fxv%d' % (h % 2))

        load(0)
        for h in range(4):
            if h + 1 < 4:
                load(h + 1)
            Q = Qs[h % 2]; K = Ks[h % 2]; V = Vs[h % 2]
            for j in range(8):
                def fin(po, F, o=ob[j % 2], j=j, h=h):
                    P.tt('dve', o[:], po[0:64, :], F[:], ALU.mult)
                    P.dma(A['mixT'][768 + 64 * h:768 + 64 * (h + 1), j * 512:(j + 1) * 512], o[:], 'fxo%d' % (j % 2), q='sp')
                attn_chunk(cx, K, 128, Q, j, V, causal_entries(j, consts['mc']), fin)
            attn_flush(cx)

import math, os
STAGE = int(os.environ.get('STAGE', '99'))

T = 4096
LN8 = math.log(0.125)


def phase_hgrn(P, A, layer, consts):
    for ct in range(2):
        with ExitStack() as st:
            B = [P.sb(st, "hB%d" % i, [128, T], F32) for i in range(5)]
            qt = P.sb(st, "h_qt", [128, T], BF16)
            kt = P.sb(st, "h_kt", [128, 2, T], BF16)
            qg = P.sb(st, "h_qg", [128, T], BF16)
            kd = P.sb(st, "h_kd", [128, T], BF16)
            kdt = P.sb(st, "h_kdt", [128, 32, 2, 128], BF16)
            Vt = P.sb(st, "h_Vt", [128, 32, 128], BF16)
            Vz = P.sb(st, "h_Vz", [128, 32, 2, 128], BF16)
            Sbd = P.sb(st, "h_Sbd", [128, 64, 128], BF16)
            rst = P.sb(st, "h_rst", [128, T], BF16)
            sm = P.sb(st, "h_sm", [128, 8], F32)
            dl = P.sb(st, "h_dl", [128, 64], F32)
            mh = P.sb(st, "h_mh", [128, 128], BF16)
            bones = P.sb(st, "h_bones", [128, 128], BF16)
            ident = consts['ident']
            P.dma(mh[:], A['mh'], 'hg0')
            P.dma(bones[:], A['bones'], 'hg0')
            P.dma(sm[:, 0:2], A['lbl'][ct], 'hg0')
            P.dma(sm[:, 4:5], A['og'][layer], 'hg0')
            P.dma(B[0][:], A['pT'][256 + ct * 128:256 + (ct + 1) * 128, :], 'hgz')
            P.dma(B[3][:], A['pT'][ct * 128:(ct + 1) * 128, :], 'hgq', q='act')
            P.dma(Vt[:], A['vtok'][:, ct * 128:(ct + 1) * 128].rearrange("(n p) d -> p n d", p=128), 'hgv', q='pool')
            P.memset('pool', Vz[:], 0.0)
            for hh in range(2):
                P.dma(Vz[:, :, hh, hh * 64:(hh + 1) * 64],
                      A['vtok'][:, ct * 128 + hh * 64:ct * 128 + (hh + 1) * 64].rearrange("(n p) d -> p n d", p=128), 'hgv', q='pool')
            P.memset('pool', Sbd[:], 0.0)
            P.memset('pool', kt[:], 0.0)
            P.memset('pool', kdt[:], 0.0)
            P.memset('pool', rst[:], 1.0)
            P.memset('pool', rst[:].rearrange("p (c s) -> p c s", s=64)[:, :, 0:1], 0.0)
            lb = sm[:, 2:3]; oml = sm[:, 3:4]; noml = sm[:, 5:6]
            if layer == 0:
                P.memset('dve', lb, 0.0)
            else:
                P.act(sm[:, 0:2], sm[:, 0:2], AF.Exp)
                P.tt('dve', sm[:, 6:7], sm[:, 0:1], sm[:, 1:2], ALU.add)
                P.op('dve', lambda e, o=sm[:, 6:7]: e.reciprocal(out=o, in_=o), reads=[sm[:, 6:7]], writes=[sm[:, 6:7]])
                P.tt('dve', lb, sm[:, 1:2], sm[:, 6:7], ALU.mult)
            P.ts('dve', oml, lb, -1.0, 1.0, ALU.mult, ALU.add)
            P.ts('dve', noml, oml, -1.0, None, ALU.mult)
            P.act(B[0][:], B[0][:], AF.Sigmoid)
            P.ts('dve', B[1][:], B[0][:], oml, lb, ALU.mult, ALU.add)
            P.act(B[1][:], B[1][:], AF.Ln)
            P.scan(B[2][:], rst[:], B[1][:], 0.0, ALU.mult, ALU.add)
            P.ts('dve', B[1][:], B[0][:], noml, oml, ALU.mult, ALU.add)
            G3 = B[2][:].rearrange("p (c s) -> p c s", s=64)
            D3 = B[0][:].rearrange("p (c s) -> p c s", s=64)
            P.tt('dve', D3, G3, G3[:, :, 31:32].broadcast_to([128, 64, 64]), ALU.subtract)
            P.act(B[4][:], B[0][:], AF.Exp, bias=LN8)
            P.tt('dve', qt[:], B[3][:], B[4][:], ALU.mult)
            P.act(B[4][:], B[0][:], AF.Exp, scale=-1.0)
            P.tt('dve', kt[0:64, 0, :], B[1][0:64, :], B[4][0:64, :], ALU.mult)
            P.tt('dve', kt[64:128, 1, :], B[1][64:128, :], B[4][64:128, :], ALU.mult)
            P.act(B[4][:], B[2][:], AF.Exp, bias=LN8)
            P.tt('dve', qg[:], B[3][:], B[4][:], ALU.mult)
            P.tt('dve', D3, G3[:, :, 63:64].broadcast_to([128, 64, 64]), G3, ALU.subtract)
            P.act(B[4][:], B[0][:], AF.Exp)
            P.tt('dve', kd[:], B[1][:], B[4][:], ALU.mult)
            P.act(dl[:].unsqueeze(2), G3[:, :, 63:64], AF.Exp)
            P.memset('dve', dl[:, 0:1], 0.0)
            KV = B[0]; dfull = B[1]; Sall = B[3]; oT = B[4]
            if STAGE < 1:
                P.dma(A['mixT'][0:128, 0:T], kd[:], 'dbg'); continue
            with ExitStack() as s2:
                ptr = [P.ps(s2, "h_ptr%d" % i, [128, 8, 128], BF16) for i in range(2)]
                for g in range(4):
                    for i in range(8):
                        tl = g * 8 + i
                        P.transpose(ptr[g % 2][:, i, :], kd[:, tl * 128:(tl + 1) * 128], ident[:])
                    P.copy('act', kdt[0:64, g * 8:(g + 1) * 8, 0, :], ptr[g % 2][0:64, :, :])
                    P.copy('dve', kdt[64:128, g * 8:(g + 1) * 8, 1, :], ptr[g % 2][64:128, :, :])
            with ExitStack() as s2:
                pkv = [P.ps(s2, "h_pkv%d" % i, [128, 4, 128], F32) for i in range(2)]
                KV3 = KV[:].rearrange("p (v c) -> p v c", c=64)
                for g in range(16):
                    pk = pkv[g % 2]
                    for i in range(4):
                        c = g * 4 + i
                        tl = c // 2; hf = c % 2
                        P.mm(pk[:, i, :], kdt[:, tl, hf, :], Vt[:, tl, :], start=True, stop=True)
                    for hh in range(2):
                        P.copy('act' if hh else 'dve', KV3[hh * 64:(hh + 1) * 64, :, g * 4:(g + 1) * 4],
                               pk[hh * 64:(hh + 1) * 64, :, hh * 64:(hh + 1) * 64].rearrange("p g v -> p v g"))
            if STAGE < 2:
                P.dma(A['mixT'][0:128, 0:T], kd[:], 'dbg'); continue
            P.copy('pool', dfull[:].rearrange("p (v c) -> p v c", c=64), dl[:].unsqueeze(1).broadcast_to([128, 64, 64]))
            P.scan(Sall[:], dfull[:], KV[:], 0.0, ALU.mult, ALU.add)
            S3 = Sall[:].rearrange("p (v c) -> p v c", c=64)
            for hh in range(2):
                P.copy('dve' if hh else 'act', Sbd[hh * 64:(hh + 1) * 64, 1:64, hh * 64:(hh + 1) * 64],
                       S3[hh * 64:(hh + 1) * 64, :, 0:63].rearrange("p v c -> p c v"))
            if STAGE < 3:
                P.dma(A['mixT'][0:128, 0:T], kd[:], 'dbg'); continue
            with ExitStack() as s2:
                pA = [P.ps(s2, "h_pA%d" % i, [128, 128], F32) for i in range(4)]
                po = [P.ps(s2, "h_po%d" % i, [128, 128], F32) for i in range(2)]
                Am = [P.sb(s2, "h_Am%d" % i, [128, 128], BF16) for i in range(4)]
                def scores(tl):
                    cols = slice(tl * 128, (tl + 1) * 128)
                    for hh in range(2):
                        i = (tl % 2) * 2 + hh
                        P.mm(pA[i][:], kt[:, hh, cols], qt[:, cols], start=True, stop=True)
                        P.tt('dve', Am[i][:], pA[i][:], mh[:], ALU.mult)

                def outs(tl):
                    cols = slice(tl * 128, (tl + 1) * 128)
                    p_ = po[tl % 2]
                    P.mm(p_[:], Vz[:, tl, 0, :], Am[(tl % 2) * 2][:], start=True, stop=False)
                    P.mm(p_[:], Vz[:, tl, 1, :], Am[(tl % 2) * 2 + 1][:], start=False, stop=False)
                    P.mm(p_[:, 0:64], Sbd[:, 2 * tl, :], qg[:, tl * 128:tl * 128 + 64], start=False, stop=False)
                    P.mm(p_[:, 64:128], Sbd[:, 2 * tl + 1, :], qg[:, tl * 128 + 64:tl * 128 + 128], start=False, stop=True)
                    P.copy('act', oT[:, cols], p_[:])

                for tl in range(33):
                    if tl < 32:
                        scores(tl)
                    if tl >= 1:
                        outs(tl - 1)
            if STAGE < 4:
                P.dma(A['mixT'][0:128, 0:T], kd[:], 'dbg'); continue
            with ExitStack() as s2:
                pss = [P.ps(s2, "h_pss%d" % i, [128, 512], F32) for i in range(2)]
                sq = [P.sb(s2, "h_sq%d" % i, [128, 512], BF16) for i in range(2)]
                rs = [P.sb(s2, "h_rs%d" % i, [128, 512], F32) for i in range(2)]
                ag = [P.sb(s2, "h_ag%d" % i, [128, 512], F32) for i in range(2)]
                ob = [P.sb(s2, "h_ob%d" % i, [128, 512], BF16) for i in range(2)]
                for j in range(8):
                    b = j % 2
                    cols = slice(j * 512, (j + 1) * 512)
                    P.dma(ag[b][:], A['pT'][512 + ct * 128:512 + (ct + 1) * 128, cols], 'hga%d' % b)
                    P.act(sq[b][:], oT[:, cols], AF.Square)
                    P.mm(pss[b][:], bones[:], sq[b][:], start=True, stop=True)
                    P.act(rs[b][:], pss[b][:], AF.Sqrt, bias=1e-6, scale=1.0 / 64)
                    P.op('dve', lambda e, o=rs[b][:]: e.reciprocal(out=o, in_=o), reads=[rs[b][:]], writes=[rs[b][:]])
                    P.act(ag[b][:], ag[b][:], AF.Silu)
                    P.stt('dve', rs[b][:], oT[:, cols], sm[:, 4:5], rs[b][:], ALU.mult, ALU.mult)
                    P.tt('pool', ob[b][:], rs[b][:], ag[b][:], ALU.mult)
                    P.dma(A['mixT'][ct * 128:(ct + 1) * 128, cols], ob[b][:], 'hgo%d' % b, q='pool')

import os
STAGE = int(os.environ.get('STAGE', '99'))

T = 4096
NEG = -30000.0


def phase_nsa(P, A, layer, consts):
    c = consts
    ident = c['ident']
    with ExitStack() as st:
        cx = AttnCtx(P, st, consts)
        with ExitStack() as s0:
            lng = P.sb(s0, "n_lng", [24, T], F32)
            P.dma(lng[:], A['pT'][1024:1048, :], 'ns0')
            P.act(lng[:], lng[:], AF.Sigmoid)
            P.dma(A['gsig'], lng[:], 'ns0')
        lng2 = A['gsig']
        ovaug = P.sb(st, "n_ov", [128, 2, 72], BF16)
        wc = P.sb(st, "n_wc", [128, 3200], BF16)
        addm = P.sb(st, "n_addm", [128, 32, 64], F32)
        P.dma(ovaug[:], A['ovaug'], 'ns0')
        P.dma(wc[:], A['wc'], 'ns0')
        P.dma(addm[:], A['addmask'], 'ns0')
        kcTs = [P.sb(st, "n_kcT%d" % i, [128, 256], BF16) for i in range(2)]
        vcAs = [P.sb(st, "n_vcA%d" % i, [128, 2, 65], BF16) for i in range(2)]
        for g in range(2):
            kcT = kcTs[g]; vcA = vcAs[g]
            with ExitStack() as s2:
                w1 = P.sb(s2, "n_w1", [64, 32, 128], BF16)
                w1f = P.sb(s2, "n_w1f", [64, 32, 128], F32)
                w2 = P.sb(s2, "n_w2", [128, 64], BF16)
                w2f = P.sb(s2, "n_w2f", [128, 64], F32)
                posT = P.sb(s2, "n_posT", [64, 32], BF16)
                posf = P.sb(s2, "n_posf", [64, 32], F32)
                posb = P.sb(s2, "n_posb", [64, 32, 256], BF16)
                srcf = P.sb(s2, "n_srcf", [64, T], F32)
                srcb = P.sb(s2, "n_srcb", [64, T], BF16)
                bias = P.sb(s2, "n_bias", [128, 1], F32)
                xb = P.sb(s2, "n_xb", [128, 256], F32)
                x2 = P.sb(s2, "n_x2", [128, 256], F32)
                hid = P.sb(s2, "n_hid", [128, 256], BF16)
                ktm = P.sb(s2, "n_ktm", [128, 64], F32)
                kts = P.sb(s2, "n_kts", [128, 64], F32)
                ktb = P.sb(s2, "n_ktb", [128, 128], BF16)
                sm = P.sb(s2, "n_sm", [128, 4], F32)
                rt = P.sb(s2, "n_rt", [128, 4, 8], F32)
                kng = P.sb(s2, "n_kng", [128, 64], F32)
                cosc = P.sb(s2, "n_cosc", [128, 2, 8], F32)
                sinc = P.sb(s2, "n_sinc", [128, 2, 8], F32)
                ph = cx.psS[0]; pb = cx.psS[1]; po = cx.psO[0]
                pt = P.ps(s2, "n_pt", [128, 128], BF16)
                P.dma(kng[:], A['kng'][layer].partition_broadcast(128), 'ns1')
                P.dma(cosc[:], A['cosc'], 'ns1')
                P.dma(sinc[:], A['sinc'], 'ns1')
                P.memset('dve', hid[:], 0.0)
                P.memset('dve', vcA[:], 0.0)
                P.memset('dve', kcT[:], 0.0)
                P.memset('dve', ktb[:], 0.0)
                for which in range(2):
                    P.dma(w1f[:], A['w1r'][layer, which], 'ns2')
                    P.dma(w2f[:], A['w2'][layer, which], 'ns2')
                    P.dma(posf[:], A['posT'][layer, which], 'ns2')
                    P.dma(srcf[:], A['pT'][768 + 128 * which + 64 * g:768 + 128 * which + 64 * (g + 1), :], 'ns3', q='act')
                    P.copy('dve', w1[:], w1f[:])
                    P.copy('dve', w2[:], w2f[:])
                    P.copy('dve', posT[:], posf[:])
                    P.copy('dve', posb[:], posT[:].unsqueeze(2).broadcast_to([64, 32, 256]))
                    P.copy('act', srcb[:], srcf[:])
                    for l in range(32):
                        P.mm(ph[:, 0:255], w1[:, l, :], srcb[:].rearrange("p (n s) -> p n s", s=16)[:, (l // 16):(l // 16) + 255, l % 16], start=(l == 0), stop=False)
                    for l in range(32):
                        P.mm(ph[:, 0:255], w1[:, l, :], posb[:, l, 0:255], start=False, stop=(l == 31))
                    P.copy('act', xb[:, 0:255], ph[:, 0:255])
                    P.tt('dve', x2[:, 0:255], xb[:, 0:255], xb[:, 0:255], ALU.mult)
                    P.ts('dve', x2[:, 0:255], x2[:, 0:255], 0.044715, 1.0, ALU.mult, ALU.add)
                    P.tt('dve', x2[:, 0:255], x2[:, 0:255], xb[:, 0:255], ALU.mult)
                    P.act(x2[:, 0:255], x2[:, 0:255], AF.Sigmoid, scale=1.5957691216057308)
                    P.tt('dve', hid[:, 0:255], x2[:, 0:255], xb[:, 0:255], ALU.mult)
                    for nt in range(2):
                        P.mm(po[:, 0:64], hid[:, nt * 128:(nt + 1) * 128], w2[:], start=True, stop=True)
                        if which == 1:
                            nr = 128 if nt == 0 else 127
                            P.copy('act', vcA[0:nr, nt, 0:64], po[0:nr, 0:64])
                            P.memset('dve', vcA[0:nr, nt, 64:65], 1.0)
                        else:
                            P.act(kts[:], po[:, 0:64], AF.Square, accum_out=sm[:, 0:1])
                            P.act(sm[:, 1:2], sm[:, 0:1], AF.Sqrt, bias=1e-6, scale=1.0 / 64)
                            P.op('dve', lambda e, o=sm[:, 1:2]: e.reciprocal(out=o, in_=o), reads=[sm[:, 1:2]], writes=[sm[:, 1:2]])
                            P.stt('dve', ktm[:], po[:, 0:64], sm[:, 1:2], kng[:], ALU.mult, ALU.mult)
                            P.copy('act', ktb[:, 0:64], ktm[:])
                            P.tt('dve', rt[:, 0, :], ktm[:, 0:8], cosc[:, nt, :], ALU.mult)
                            P.tt('dve', rt[:, 1, :], ktm[:, 8:16], sinc[:, nt, :], ALU.mult)
                            P.tt('dve', rt[:, 2, :], ktm[:, 8:16], cosc[:, nt, :], ALU.mult)
                            P.tt('dve', rt[:, 3, :], ktm[:, 0:8], sinc[:, nt, :], ALU.mult)
                            P.tt('dve', ktb[:, 0:8], rt[:, 0, :], rt[:, 1, :], ALU.subtract)
                            P.tt('dve', ktb[:, 8:16], rt[:, 2, :], rt[:, 3, :], ALU.add)
                            P.transpose(pt[:], ktb[:], ident[:])
                            P.copy('dve', kcT[0:64, nt * 128:(nt + 1) * 128], pt[0:64, :])
            P.memset('dve', kcT[0:64, 255:256], 0.0)
        selT = P.sb(st, "n_selT", [128, T], BF16)
        imp = P.sb(st, "n_imp", [128, 32, 64], F32)
        acc = [P.sb(st, "n_acc%d" % i, [64, T], F32) for i in range(4)]
        Q = [P.sb(st, "n_Q%d" % i, [128, T], BF16) for i in range(4)]
        Ks = P.sb(st, "n_Ks", [128, T], BF16)
        Kw = P.sb(st, "n_Kw", [128, T], BF16)
        Vs = P.sb(st, "n_Vs", [128, 32, 65], BF16)
        Vw = P.sb(st, "n_Vw", [128, 32, 65], BF16)
        for g in range(2):
            kcT = kcTs[g]; vcA = vcAs[g]
            for hh in range(4):
                h = 4 * g + hh
                P.memset('pool', Q[hh][64:128, :], 0.0)
                P.dma(Q[hh][0:64, :], A['qkT'][64 * h:64 * (h + 1), :], 'nsq%d' % hh)
            P.dma(Ks[0:64, :], A['qkT'][512 + 64 * g:512 + 64 * (g + 1), :], 'nsk')
            P.dma(Ks[64:128, :], A['eall'], 'nsk')
            P.dma(Kw[0:64, :], A['qkT'][640 + 64 * g:640 + 64 * (g + 1), :], 'nsk')
            P.memset('pool', Kw[64:128, :], 0.0)
            P.memset('pool', Vs[:, :, 64:65], 1.0)
            P.memset('pool', Vw[:, :, 64:65], 1.0)
            P.dma(Vs[:, :, 0:64], A['vtok'][:, 256 + 64 * g:256 + 64 * (g + 1)].rearrange("(n p) d -> p n d", p=128), 'nsv', q='act')
            P.dma(Vw[:, :, 0:64], A['vtok'][:, 384 + 64 * g:384 + 64 * (g + 1)].rearrange("(n p) d -> p n d", p=128), 'nsv', q='act')
            with ExitStack() as s2:
                pimp = [cx.psS[i][:, 512:800].rearrange("p (a b) -> p a b", b=72) for i in range(2)]
                pTc = [P.sb(s2, "n_pTc%d" % i, [128, 512], BF16) for i in range(3)]
                rinvs = [P.sb(s2, "n_rinv%d" % i, [128, 4], F32) for i in range(2)]
                kc_ = 0
                kch = 0
                for hh in range(4):
                    h = 4 * g + hh
                    for j in range(8):
                        q0 = j * 512
                        po = cx.psO[cx.kO % 2]
                        cx.kO += 1
                        pim = pimp[kch % 2]
                        rinv = rinvs[kch % 2]
                        kch += 1
                        tiles = []
                        for nt in range(2):
                            off = 2048 * nt + 31 - 512 * j
                            if -off + 511 < 0:
                                continue
                            tiles.append((nt, off))
                        for ti, (nt, off) in enumerate(tiles):
                            ps = cx.psS[cx.kS % 2][:, 0:512]
                            cx.kS += 1
                            ptc = pTc[kc_ % 3]
                            kc_ += 1
                            first = (ti == 0)
                            last = (ti == len(tiles) - 1)

                            def s_fn(ps=ps, nt=nt, off=off, first=first, po=po, pim=pim, hh=hh, q0=q0):
                                full = (-off >= 2032)
                                P.mm(ps[:], kcT[:, nt * 128:(nt + 1) * 128], Q[hh][:, q0:q0 + 512], start=True, stop=full)
                                if not full:
                                    ci0 = -off + 511
                                    P.mm(ps[:], ident[:], wc[:, ci0:ci0 + 512], start=False, stop=True)

                            def exp_fn(ps=ps, ptc=ptc):
                                P.act(ptc[:], ps[:], AF.Exp)

                            def pv_fn(po=po, pim=pim, ptc=ptc, nt=nt, last=last, first=first):
                                P.mm(po[0:65, :], vcA[:, nt, :], ptc[:], start=first, stop=last)
                                for m in range(4):
                                    P.mm(pim[:, m, :], ptc[:, m * 128:(m + 1) * 128], ovaug[:, nt, :], start=(first and m == 0), stop=last)

                            _push_block(cx, s_fn, exp_fn, pv_fn, first=first)

                        def fin(po_, F, hh=hh, q0=q0, pim=pim, rinv=rinv, j=j):
                            for m in range(4):
                                tq = j * 4 + m
                                if hh == 0:
                                    P.ts('dve', imp[:, tq, :], pim[:, m, 0:64], rinv[:, m:m + 1], None, ALU.mult)
                                else:
                                    P.stt('dve', imp[:, tq, :], pim[:, m, 0:64], rinv[:, m:m + 1], imp[:, tq, :], ALU.mult, ALU.add)
                            P.tt('dve', acc[hh][:, q0:q0 + 512], po_[0:64, :], F[:], ALU.mult)

                        ea, eb = _mk_factor(cx, po, lng2, h, q0, fin)

                        def ea2(ea=ea, pim=pim, rinv=rinv):
                            ea()
                            P.ts('dve', rinv[:, 0:4].unsqueeze(2), pim[:, :, 64:65], 1e-30, None, ALU.max)
                            P.op('dve', lambda e, o=rinv[:, 0:4]: e.reciprocal(out=o, in_=o), reads=[rinv[:, 0:4]], writes=[rinv[:, 0:4]])

                        _end_chunk(cx, ea2, eb)
                attn_flush(cx)
            if STAGE < 2:
                continue
            with ExitStack() as s2:
                wk = [P.sb(s2, "n_wk%d" % i, [128, 64], F32) for i in range(2)]
                w2_ = [P.sb(s2, "n_wk2%d" % i, [128, 64], F32) for i in range(2)]
                m8 = [P.sb(s2, "n_m8%d" % i, [128, 16], F32) for i in range(2)]
                sb_ = [P.sb(s2, "n_sb%d" % i, [128, 128], BF16) for i in range(2)]
                pts = [P.ps(s2, "n_pts%d" % i, [128, 128], BF16) for i in range(1)] * 2
                P.memset('pool', sb_[0][:], 0.0)
                P.memset('pool', sb_[1][:], 0.0)
                for tq in range(32):
                    b = tq % 2
                    P.tt('dve', wk[b][:], imp[:, tq, :], addm[:, tq, :], ALU.add)
                    P.op('dve', lambda e, o=m8[b][:, 0:8], i=wk[b][:]: e.max(out=o, in_=i), reads=[wk[b][:]], writes=[m8[b][:, 0:8]])
                    P.op('dve', lambda e, o=w2_[b][:], r=m8[b][:, 0:8], i=wk[b][:]: e.match_replace(out=o, in_to_replace=r, in_values=i, imm_value=-3.0e38),
                         reads=[m8[b][:, 0:8], wk[b][:]], writes=[w2_[b][:]])
                    P.op('dve', lambda e, o=m8[b][:, 8:16], i=w2_[b][:]: e.max(out=o, in_=i), reads=[w2_[b][:]], writes=[m8[b][:, 8:16]])
                    P.ts('dve', w2_[b][:], wk[b][:], m8[b][:, 15:16], None, ALU.is_ge)
                    P.ts('dve', wk[b][:], wk[b][:], -5.0e29, None, ALU.is_gt)
                    P.tt('dve', wk[b][:], wk[b][:], w2_[b][:], ALU.mult)
                    P.ts('dve', sb_[b][:, 64:128], wk[b][:], -1.0, -NEG, ALU.add, ALU.mult)
                    P.transpose(pts[b][:], sb_[b][:], ident[:])
                    P.copy('act', selT[64:128, tq * 128:(tq + 1) * 128], pts[b][64:128, :])
            for hh in range(4):
                P.dma(Q[hh][64:128, :], selT[64:128, :], 'nsq%d' % hh, q='sp' if hh % 2 else 'act')
            if STAGE < 3:
                continue
            with ExitStack() as s2:
                tmp = [P.sb(s2, "n_tmp%d" % i, [64, 512], F32) for i in range(2)]
                ob = [P.sb(s2, "n_ob%d" % i, [64, 512], BF16) for i in range(1)] * 2
                for hh in range(4):
                    h = 4 * g + hh
                    for j in range(8):
                        q0 = j * 512
                        a = acc[hh][:, q0:q0 + 512]

                        def fin_s(po_, F, a=a):
                            P.tt('dve', tmp[0][:], po_[0:64, :], F[:], ALU.mult)
                            P.tt('pool', a, a, tmp[0][:], ALU.add)

                        def fin_w(po_, F, a=a, j=j, h=h, q0=q0):
                            P.tt('dve', tmp[1][:], po_[0:64, :], F[:], ALU.mult)
                            P.tt('pool', ob[j % 2][:], a, tmp[1][:], ALU.add)
                            if STAGE >= 5:
                                P.dma(A['mixT'][256 + 64 * h:256 + 64 * (h + 1), q0:q0 + 512], ob[j % 2][:], 'nso%d' % (j % 2), q='sp')

                        attn_chunk(cx, Ks, 128, Q[hh], j, Vs, causal_entries(j, c['mc']), fin_s, lng2=lng2, gate_c=8 + h)
                        if STAGE >= 4:
                            attn_chunk(cx, Kw, 128, Q[hh], j, Vw, window_entries(j, c['mc'], c['mu']), fin_w, lng2=lng2, gate_c=16 + h)
                attn_flush(cx)


T = 4096
D = 1024
FF = 4096


def phase_wo(P, A, layer, consts, x_in, x_mid, per_tile=None):
    ident = consts['ident']
    with ExitStack() as st:
        Wo = P.sb(st, "wo", [128, 8, D], BF16)
        with ExitStack() as s2:
            wst = [P.sb(s2, "wost%d" % i, [128, D], F32) for i in range(2)]
            for kc in range(8):
                b = wst[kc % 2]
                P.dma(b[:], A['wo'][layer, kc * 128:(kc + 1) * 128, :], 'wost%d' % (kc % 2))
                P.copy('act' if kc % 2 else 'dve', Wo[:, kc, :], b[:])
        mx = [P.sb(st, "wo_mx%d" % i, [128, 8, 512], BF16) for i in range(2)]
        xt = [P.sb(st, "wo_xt%d" % i, [128, D], F32) for i in range(2)]
        xm = [P.sb(st, "wo_xm%d" % i, [128, D], F32) for i in range(2)]
        sq = P.sb(st, "wo_sq", [128, D], BF16)
        hb = [P.sb(st, "wo_hb%d" % i, [128, D], BF16) for i in range(2)]
        ss = [P.sb(st, "wo_ss%d" % i, [128, 2], F32) for i in range(2)]
        hst = [P.sb(st, "wo_hst%d" % i, [128, 8, 512], BF16) for i in range(1)] * 2
        po = [P.ps(st, "wo_po%d" % i, [128, 512], F32) for i in range(4)]
        ptr = [P.ps(st, "wo_ptr%d" % i, [128, 8, 128], BF16) for i in range(2)]
        for t in range(32):
            j = t // 4
            b = t % 2
            if t % 4 == 0:
                P.dma(mx[j % 2][:], A['mixT'][:, j * 512:(j + 1) * 512].rearrange("(a p) n -> p a n", p=128), 'womx%d' % (j % 2))
            P.dma(xt[b][:], x_in[t * 128:(t + 1) * 128, :], 'woxt%d' % b, q='act')
            for half in range(2):
                pp = po[(t % 2) * 2 + half]
                for kc in range(8):
                    P.mm(pp[:], mx[j % 2][:, kc, (t % 4) * 128:(t % 4 + 1) * 128], Wo[:, kc, half * 512:(half + 1) * 512],
                         start=(kc == 0), stop=(kc == 7))
                P.tt('dve', xm[b][:, half * 512:(half + 1) * 512], pp[:], xt[b][:, half * 512:(half + 1) * 512], ALU.add)
            P.dma(x_mid[t * 128:(t + 1) * 128, :], xm[b][:], 'woxm%d' % b, q='pool')
            P.act(sq[:], xm[b][:], AF.Square, accum_out=ss[b][:, 0:1])
            P.act(ss[b][:, 1:2], ss[b][:, 0:1], AF.Sqrt, bias=1e-6, scale=1.0 / D)
            P.op('dve', lambda e, o=ss[b][:, 1:2]: e.reciprocal(out=o, in_=o), reads=[ss[b][:, 1:2]], writes=[ss[b][:, 1:2]])
            P.ts('dve', hb[b][:], xm[b][:], ss[b][:, 1:2], None, ALU.mult)
            for kc in range(8):
                P.transpose(ptr[b][:, kc, :], hb[b][:, kc * 128:(kc + 1) * 128], ident[:])
            P.copy('act', hst[j % 2][:, :, (t % 4) * 128:(t % 4 + 1) * 128], ptr[b][:])
            if t % 4 == 3:
                P.dma(A['h2T'][:, j * 512:(j + 1) * 512].rearrange("(a p) n -> p a n", p=128), hst[j % 2][:], 'wohst%d' % (j % 2), q='pool')
            if per_tile is not None:
                per_tile(t)


def phase_wo_ffn(P, A, layer, consts, x_in, x_mid, x_out):
    with ExitStack() as st:
        Wu = P.sb(st, "wu", [128, 8, FF], BF16)
        Wd = P.sb(st, "wd", [128, 32, D], BF16)
        g2 = P.sb(st, "g2", [128, 8], F32)
        P.dma(g2[:], A['g2'][layer], 'ff0')
        with ExitStack() as s2:
            wst = [P.sb(s2, "fwst%d" % i, [128, 1024], F32) for i in range(2)]

            def per_tile(t):
                for u in range(2):
                    ci = 2 * t + u
                    b = u
                    if ci < 32:
                        kc, qt = ci // 4, ci % 4
                        P.dma(wst[b][:], A['wup'][layer, kc * 128:(kc + 1) * 128, qt * 1024:(qt + 1) * 1024], 'fwst%d' % b, q='sp')
                        if u:
                            P.ts('dve', Wu[:, kc, qt * 1024:(qt + 1) * 1024], wst[b][:], g2[:, kc:kc + 1], None, ALU.mult)
                        else:
                            P.act(Wu[:, kc, qt * 1024:(qt + 1) * 1024], wst[b][:], AF.Copy, scale=g2[:, kc:kc + 1])
                    else:
                        fc = ci - 32
                        P.dma(wst[b][:], A['wdn'][layer, fc * 128:(fc + 1) * 128, :], 'fwst%d' % b, q='sp')
                        P.copy('dve' if u else 'act', Wd[:, fc, :], wst[b][:])

            phase_wo(P, A, layer, consts, x_in, x_mid, per_tile=per_tile)
        _ffn_body(P, st, A, Wu, Wd, x_mid, x_out)


def phase_ffn(P, A, layer, consts, x_mid, x_out):
    with ExitStack() as st:
        Wu = P.sb(st, "wu", [128, 8, FF], BF16)
        Wd = P.sb(st, "wd", [128, 32, D], BF16)
        g2 = P.sb(st, "g2", [128, 8], F32)
        P.dma(g2[:], A['g2'][layer], 'ff0')
        with ExitStack() as s2:
            wst = [P.sb(s2, "fwst%d" % i, [128, 2048], F32) for i in range(3)]
            k = 0
            for kc in range(8):
                for hf in range(2):
                    b = k % 3
                    P.dma(wst[b][:], A['wup'][layer, kc * 128:(kc + 1) * 128, hf * 2048:(hf + 1) * 2048], 'fwst%d' % b, q='sp' if k % 2 else 'act')
                    if k % 2:
                        P.ts('dve', Wu[:, kc, hf * 2048:(hf + 1) * 2048], wst[b][:], g2[:, kc:kc + 1], None, ALU.mult)
                    else:
                        P.act(Wu[:, kc, hf * 2048:(hf + 1) * 2048], wst[b][:], AF.Copy, scale=g2[:, kc:kc + 1])
                    k += 1
            for fc2 in range(16):
                b = k % 3
                P.dma(wst[b][:].rearrange("p (a n) -> p a n", a=2), A['wdn'][layer, fc2 * 256:(fc2 + 1) * 256, :].rearrange("(a p) n -> p a n", p=128),
                      'fwst%d' % b, q='sp' if k % 2 else 'act')
                P.copy('dve' if k % 2 else 'act', Wd[:, fc2 * 2:(fc2 + 1) * 2, :], wst[b][:].rearrange("p (a n) -> p a n", a=2))
                k += 1
        _ffn_body(P, st, A, Wu, Wd, x_mid, x_out)


def _ffn_body(P, st, A, Wu, Wd, x_mid, x_out):
    if True:
        h2 = [P.sb(st, "ff_h2%d" % i, [128, 8, 512], BF16) for i in range(2)]
        uT = P.sb(st, "ff_uT", [128, 32, 512], BF16)
        rl = [P.sb(st, "ff_rl%d" % i, [128, 512], F32) for i in range(2)]
        xt = [P.sb(st, "ff_xt%d" % i, [128, D], F32) for i in range(2)]
        xo = [P.sb(st, "ff_xo%d" % i, [128, D], F32) for i in range(2)]
        pu = [P.ps(st, "ff_pu%d" % i, [128, 512], F32) for i in range(3)]
        pd = [P.ps(st, "ff_pd%d" % i, [128, 512], F32) for i in range(4)]
        ku = 0
        for j in range(8):
            P.dma(h2[j % 2][:], A['h2T'][:, j * 512:(j + 1) * 512].rearrange("(a p) n -> p a n", p=128), 'ffh2%d' % (j % 2))
            for fc in range(32):
                pp = pu[ku % 3]
                r = rl[ku % 2]
                for kc in range(8):
                    P.mm(pp[:], Wu[:, kc, fc * 128:(fc + 1) * 128], h2[j % 2][:, kc, :], start=(kc == 0), stop=(kc == 7))
                P.act(r[:], pp[:], AF.Relu)
                P.tt('dve' if ku % 2 else 'pool', uT[:, fc, :], r[:], r[:], ALU.mult)
                ku += 1
            for tt in range(4):
                t = j * 4 + tt
                b = t % 2
                P.dma(xt[b][:], x_mid[t * 128:(t + 1) * 128, :], 'ffxt%d' % b, q='act')
                for half in range(2):
                    pp = pd[(t % 2) * 2 + half]
                    for fc in range(32):
                        P.mm(pp[:], uT[:, fc, tt * 128:(tt + 1) * 128], Wd[:, fc, half * 512:(half + 1) * 512], start=(fc == 0), stop=(fc == 31))
                    P.tt('dve', xo[b][:, half * 512:(half + 1) * 512], pp[:], xt[b][:, half * 512:(half + 1) * 512], ALU.add)
                P.dma(x_out[t * 128:(t + 1) * 128, :], xo[b][:], 'ffxo%d' % b, q='pool')

import ml_dtypes
from concourse.bass_utils import run_bass_kernel_spmd

T=4096; D=1024
OFF = {}
_names = ['aq','af','ai','ag','bq','bkc','bvc','bks','bvs','bkw','bvw','bg','cq','ck','cv','cf']
_sizes = [256,256,256,256,512,128,128,128,128,128,128,24,256,256,256,4]
_o = 0
for n_, s_ in zip(_names, _sizes):
    OFF[n_] = (_o, _o + s_); _o += s_
TOK_ORDER = ['bq','bks','bkw','cq','ck','ai','bvs','bvw','cv']
T_ORDER = ['aq','af','ag','bkc','bvc','bg','cf']

def win_layout(w_in):
    L = w_in.shape[0]
    out = np.zeros((L, 1024, 2048 + 1152), np.float32)
    c = 0
    for n_ in TOK_ORDER:
        a, b = OFF[n_]; out[:, :, c:c + b - a] = w_in[:, :, a:b]; c += b - a
    assert c == 2048
    for n_ in T_ORDER:
        a, b = OFF[n_]; out[:, :, c:c + b - a] = w_in[:, :, a:b]; c += b - a
    return out

def rope_tables():
    inv = np.power(np.float32(500000.0), -np.arange(0, 16, 2, dtype=np.float32) / 16).astype(np.float32)
    pos = np.arange(T, dtype=np.float32)
    ang = pos[:, None] * inv[None, :]
    cos = np.cos(ang).astype(np.float32); sin = np.sin(ang).astype(np.float32)
    return (np.ascontiguousarray(cos.reshape(32, 128, 8).transpose(1, 0, 2)),
            np.ascontiguousarray(sin.reshape(32, 128, 8).transpose(1, 0, 2)))

def _skip():
    pass

def const_inputs():
    k = np.arange(128)[:, None]; q = np.arange(128)[None, :]
    mc = np.where(k <= q, 0.0, -30000.0).astype(ml_dtypes.bfloat16)
    mu = np.where(k > q, 0.0, -30000.0).astype(ml_dtypes.bfloat16)
    selneg = np.zeros((24, 24 * 64), np.float32)
    for c in range(24):
        selneg[c, c * 64:(c + 1) * 64] = -1.0
    return dict(ident=np.eye(128, dtype=ml_dtypes.bfloat16), mc=mc, mu=mu, selneg=selneg)

def _unused_ref_proj(inp, layer, x):
    x = x.astype(np.float64)
    h = x / np.sqrt((x * x).mean(-1, keepdims=True) + 1e-6) * inp['norm1_g'][layer]
    return h @ inp['w_in'][layer].astype(np.float64)

def hgrn_consts(inp):
    s = np.arange(128)[:, None]; t = np.arange(128)[None, :]
    mh = ((s // 64 == t // 64) & (s <= t)).astype(ml_dtypes.bfloat16)
    bones = (s // 64 == t // 64).astype(ml_dtypes.bfloat16)
    lbl = np.ascontiguousarray(inp['hgrn_lb_logits'].reshape(2, 2, 128).transpose(1, 2, 0)).astype(np.float32)
    og = np.tile(inp['hgrn_onorm_g'], (1, 2)).reshape(2, 128, 1).astype(np.float32)
    return dict(mh=mh, bones=bones, lbl=lbl, og=og)

def _unused_ref_hgrn(inp, layer, proj):
    def sl(n): a, b = OFF[n]; return proj[:, a:b]
    lbp = np.exp(inp['hgrn_lb_logits'].astype(np.float64)); lbp /= lbp.sum(0, keepdims=True)
    lb_all = np.cumsum(lbp, 0) - lbp[0:1]
    lb = lb_all[layer].reshape(4, 64)
    z = sl('af').reshape(T, 4, 64)
    sig = 1 / (1 + np.exp(-z))
    f = lb + (1 - lb) * sig; logf = np.log(f); k = (1 - lb) * (1 - sig)
    q = sl('aq').reshape(T, 4, 64) * 0.125; v = sl('ai').reshape(T, 4, 64)
    o = np.zeros((T, 4, 64))
    for h in range(4):
        S = np.zeros((64, 64))
        for c in range(64):
            r = slice(c * 64, (c + 1) * 64)
            G = np.cumsum(logf[r, h], 0)
            qc, kc, vc = q[r, h], k[r, h], v[r, h]
            o_inter = (qc * np.exp(G)) @ S
            diff = G[:, None, :] - G[None, :, :]
            mask = np.tril(np.ones((64, 64), bool))
            dec = np.where(mask[:, :, None], np.exp(np.minimum(diff, 0)), 0)
            sc = np.einsum('tk,sk,tsk->ts', qc, kc, dec)
            o[r, h] = o_inter + sc @ vc
            S = S * np.exp(G[-1])[:, None] + (kc * np.exp(G[-1] - G)).T @ vc
    g = sl('ag').reshape(T, 4, 64)
    gate = g / (1 + np.exp(-g))
    on = o / np.sqrt((o * o).mean(-1, keepdims=True) + 1e-6) * inp['hgrn_onorm_g'][layer]
    return (on * gate).reshape(T, 256)

def nsa_consts(inp):
    n_cmp = 255
    ci = np.arange(n_cmp)[:, None]; sj = np.arange(64)[None, :]
    ov = ((ci * 16 <= sj * 64 + 63) & (ci * 16 + 31 >= sj * 64)).astype(np.float32)
    ovaug = np.zeros((256, 72), np.float32); ovaug[:255, :64] = ov; ovaug[:255, 64] = 1.0
    ovaug = np.ascontiguousarray(ovaug.reshape(2, 128, 72).transpose(1, 0, 2)).astype(ml_dtypes.bfloat16)
    nl = np.arange(128)[:, None]; cc = np.arange(3200)[None, :] - 511
    wc = np.where(cc >= 16 * nl, 0.0, -30000.0).astype(ml_dtypes.bfloat16)
    eall = (np.arange(T)[None, :] // 64 == np.arange(64)[:, None]).astype(ml_dtypes.bfloat16)
    q = np.arange(T)[:, None]; j = np.arange(64)[None, :]; cur = q // 64
    am = np.zeros((T, 64), np.float32)
    am[(j == 0) | (j == cur) | (j == cur - 1)] = 1e30
    am[np.broadcast_to(j > cur, am.shape)] = -1e30
    addmask = np.ascontiguousarray(am.reshape(32, 128, 64).transpose(1, 0, 2))
    inv = np.power(np.float32(500000.0), -np.arange(0, 16, 2, dtype=np.float32) / 16).astype(np.float32)
    pos = (np.arange(256, dtype=np.float32) * 16 + 31)
    ang = pos[:, None] * inv[None, :]
    cosc = np.ascontiguousarray(np.cos(ang).astype(np.float32).reshape(2, 128, 8).transpose(1, 0, 2))
    sinc = np.ascontiguousarray(np.sin(ang).astype(np.float32).reshape(2, 128, 8).transpose(1, 0, 2))
    w1r = np.ascontiguousarray(inp['nsa_cmp_w1'].reshape(2, 2, 32, 64, 128).transpose(0, 1, 3, 2, 4)).astype(np.float32)
    posT = np.ascontiguousarray(inp['nsa_cmp_pos'].transpose(0, 1, 3, 2)).astype(np.float32)
    return dict(ovaug=ovaug, wc=wc, eall=eall, addmask=addmask, cosc=cosc, sinc=sinc, w1r=w1r, posT=posT,
                w2=inp['nsa_cmp_w2'].astype(np.float32), kng=inp['nsa_kn_g'].astype(np.float32))

NSA_SHAPES = [('ovaug', [128, 2, 72], BF16), ('wc', [128, 3200], BF16), ('eall', [64, 4096], BF16), ('addmask', [128, 32, 64], F32),
              ('cosc', [128, 2, 8], F32), ('sinc', [128, 2, 8], F32), ('w1r', [2, 2, 64, 32, 128], F32), ('posT', [2, 2, 64, 32], F32),
              ('w2', [2, 2, 128, 64], F32), ('kng', [2, 64], F32)]


import ml_dtypes
from concourse.bass_utils import run_bass_kernel_spmd

_IN_SHAPES = [('x', [T, D], F32), ('win', [2, D, WCOLS], F32), ('g1', [2, 128, 8], F32), ('gq', [2, 1280], F32),
              ('cos', [128, 32, 8], F32), ('sin', [128, 32, 8], F32), ('ident', [128, 128], BF16), ('mc', [128, 128], BF16),
              ('mu', [128, 128], BF16), ('selneg', [24, 1536], F32), ('mh', [128, 128], BF16), ('bones', [128, 128], BF16),
              ('lbl', [2, 128, 2], F32), ('og', [2, 128, 1], F32), ('fb', [2, 4, 1], F32), ('wo', [2, 1024, 1024], F32),
              ('wup', [2, 1024, 4096], F32), ('wdn', [2, 4096, 1024], F32), ('g2', [2, 128, 8], F32)] + NSA_SHAPES

KDEPTH = int(os.environ.get('KDEPTH', '2'))
KPHASES = os.environ.get('KPHASES', '1hnfwf')


def _body(P):
    nc = P.nc
    A = {}
    for k_, shp, dt_ in _IN_SHAPES:
        A[k_] = nc.dram_tensor(k_, shp, dt_, kind="ExternalInput").ap()
    A['y'] = nc.dram_tensor("y", [T, D], F32, kind="ExternalOutput").ap()
    A['qkT'] = nc.dram_tensor("qkT", [1280, T], BF16).ap()
    A['vtok'] = nc.dram_tensor("vtok", [T, 768], BF16).ap()
    A['pT'] = nc.dram_tensor("pT", [TC, T], F32).ap()
    A['mixT'] = nc.dram_tensor("mixT", [1024, T], BF16).ap()
    A['h2T'] = nc.dram_tensor("h2T", [1024, T], BF16).ap()
    A['gsig'] = nc.dram_tensor("gsig", [24, T], F32).ap()
    xm = nc.dram_tensor("xmid", [T, D], F32).ap()
    x1 = nc.dram_tensor("x1", [T, D], F32).ap()
    xin = A['x']
    for layer in range(KDEPTH):
        A['x'] = xin
        with ExitStack() as st:
            phase1(P, st, A, layer)
        with ExitStack() as st:
            consts = load_consts(P, st, A)
            if 'h' in KPHASES:
                phase_hgrn(P, A, layer, consts)
            if 'n' in KPHASES:
                phase_nsa(P, A, layer, consts)
            if 'f' in KPHASES:
                phase_fox(P, A, layer, consts)
            xout = x1 if layer < KDEPTH - 1 else A['y']
            phase_wo_ffn(P, A, layer, consts, xin, xm, xout)
        xin = xout


def _host_inputs(inp):
    cos, sin = rope_tables()
    gq = np.concatenate([np.tile(inp['nsa_qn_g'], (1, 8)), np.tile(inp['nsa_kn_g'], (1, 4)), np.tile(inp['fox_qn_g'], (1, 4)),
                         np.tile(inp['fox_kn_g'], (1, 4))], axis=1).astype(np.float32)
    base = {"win": win_layout(inp['w_in']), "g1": np.ascontiguousarray(inp['norm1_g'].reshape(2, 8, 128).transpose(0, 2, 1)),
            "g2": np.ascontiguousarray(inp['norm2_g'].reshape(2, 8, 128).transpose(0, 2, 1)),
            "gq": gq, "cos": cos, "sin": sin, "fb": inp['fox_fb'].reshape(2, 4, 1).astype(np.float32),
            "wo": inp['w_o'], "wup": inp['w_up'], "wdn": inp['w_down']}
    base.update(const_inputs()); base.update(hgrn_consts(inp)); base.update(nsa_consts(inp))
    return base


def kernel(**inp):
    inp = {k: np.asarray(v) for k, v in inp.items()}
    nc, plan = build_two_pass(lambda: bass.Bass("TRN2", target_bir_lowering=False), _body)
    base = _host_inputs(inp)
    in_maps = []
    for b in range(8):
        m = dict(base); m['x'] = np.ascontiguousarray(inp['x'][b]); in_maps.append(m)
    res = run_bass_kernel_spmd(nc, in_maps, core_ids=list(range(8)))
    return np.stack([r['y'] for r in res.results], axis=0).astype(np.float32)
```

```python
import numpy as np, sys, time, os, math
import numpy as np
from contextlib import ExitStack
import concourse.bass as bass
import concourse.mybir as mybir

F32 = mybir.dt.float32
BF16 = mybir.dt.bfloat16
AF = mybir.ActivationFunctionType
ALU = mybir.AluOpType
AX = mybir.AxisListType


def _box(ap):
    t = ap.tensor
    dims = ap.ap
    off = int(ap.offset)
    shp = tuple(t.shape)
    rowsize = 1
    for s in shp[1:]:
        rowsize *= int(s)
    r0 = off // rowsize
    f0 = off % rowsize
    rows = 0
    free = 0
    for (st, cnt) in dims:
        st = int(st); cnt = int(cnt)
        if cnt <= 1 or st == 0:
            continue
        if st % rowsize == 0:
            rows += (st // rowsize) * (cnt - 1)
        else:
            free += st * (cnt - 1)
    return t.name, (r0, r0 + rows, f0, f0 + free)


def _ov(a, b):
    return a[0] <= b[1] and b[0] <= a[1] and a[2] <= b[3] and b[2] <= a[3]


def _cont(a, b):
    return a[0] <= b[0] and b[1] <= a[1] and a[2] <= b[2] and b[3] <= a[3]


class Prog:
    def __init__(self, nc, plan=None):
        self.nc = nc
        self.plan = plan
        self.rec = plan is None
        self.eng = dict(pe=nc.tensor, dve=nc.vector, act=nc.scalar, pool=nc.gpsimd, sp=nc.sync)
        self.n = 0
        self.ins = []
        self.track = {}
        self.lane_cnt = {}
        self.freed = {}
        self.uid = 0
        self.stack = ExitStack()
        self.psum_rr = 0
        self.psum_banks = []
        if not self.rec:
            self.sem = {}
            for e in ['pe', 'dve', 'act', 'pool']:
                self.sem[e] = self.stack.enter_context(nc.semaphore("sem_" + e))
            self.lane_sem = {}
            for ln in plan['lanes']:
                self.lane_sem[ln] = self.stack.enter_context(nc.semaphore("ln_" + ln))

    def sb(self, st, name, shape, dtype):
        self.uid += 1
        name = "%s_%d" % (name, self.uid)
        t = st.enter_context(self.nc.sbuf_tensor("s_" + name, list(shape), dtype))
        st.callback(self._free, "s_" + name)
        return t

    def ps(self, st, name, shape, dtype=F32):
        self.uid += 1
        name = "%s_%d" % (name, self.uid)
        t = st.enter_context(self.nc.psum_tensor("p_" + name, list(shape), dtype))
        st.callback(self._free, "p_" + name)
        return t

    def _free(self, name):
        if not self.rec:
            return
        recs = self.track.pop(name, [])
        for (b, i, w) in recs:
            r = self.ins[i]
            key = ('l', r['lane'], i) if r['dma'] else ('e', r['eng'])
            if r['dma']:
                self.freed[key] = i
            else:
                self.freed[key] = max(self.freed.get(key, -1), i)

    def _access(self, idx, eng, dma, ap, write, deps):
        name, box = _box(ap)
        if name not in self.track:
            big = (0, 10 ** 9, 0, 10 ** 9)
            kind = ap.space
            self.track[name] = [] if str(kind) == 'DRAM' else [(big, i, True) for i in sorted(set(self.freed.values()))]
        recs = self.track[name]
        for (b, i, w) in recs:
            if (write or w) and _ov(b, box):
                deps.append((i, (w and not write)))
        if write:
            recs[:] = [r for r in recs if not _cont(box, r[0])]
        elif not dma:
            recs[:] = [r for r in recs if not ((not r[2]) and r[1] < len(self.ins) and self.ins[r[1]]['eng'] == eng
                                               and not self.ins[r[1]]['dma'] and _cont(box, r[0]))]
        recs.append((box, idx, write))

    def op(self, eng, fn, reads=(), writes=(), dma=False, lane=None):
        idx = self.n
        self.n += 1
        if self.rec:
            deps = []
            for ap in reads:
                self._access(idx, eng, dma, ap, False, deps)
            for ap in writes:
                self._access(idx, eng, dma, ap, True, deps)
            lanewaits = {}
            d2 = {}
            for (j, raw) in deps:
                if j == idx:
                    continue
                pj = self.ins[j]
                if pj['dma']:
                    ln = pj['lane']
                    lanewaits[ln] = max(lanewaits.get(ln, 0), pj['lane_val_at'])
                    lanewaits[ln] = max(lanewaits[ln], self.lane_cnt[ln])
                    continue
                if pj['eng'] == eng and not dma:
                    if eng == 'pe':
                        continue
                    if not raw and eng != 'pool':
                        continue
                d2[j] = True
            rec = dict(eng=eng, deps=list(d2.keys()), lanewaits=lanewaits, dma=dma, lane=lane)
            if dma:
                self.lane_cnt[lane] = self.lane_cnt.get(lane, 0) + 16
                rec['lane_val_at'] = self.lane_cnt[lane]
            self.ins.append(rec)
            return None
        else:
            info = self.plan['ins'][idx]
            e = self.eng[eng]
            for (sname, val) in info['waits']:
                s = self.sem[sname[1]] if sname[0] == 'e' else self.lane_sem[sname[1]]
                e.wait_ge(s, val)
            inst = fn(e)
            if dma:
                inst.then_inc(self.lane_sem[lane], 16)
            elif info['signal']:
                inst.then_inc(self.sem[eng], 1)
            return inst

    def make_plan(self):
        ins = self.ins
        signal = [False] * len(ins)
        for r in ins:
            for j in r['deps']:
                signal[j] = True
        cnt = dict(pe=0, dve=0, act=0, pool=0, sp=0)
        sigval = [0] * len(ins)
        for i, r in enumerate(ins):
            if signal[i] and not r['dma']:
                cnt[r['eng']] += 1
                sigval[i] = cnt[r['eng']]
        seen = {e: {} for e in cnt}
        out = []
        for i, r in enumerate(ins):
            need = {}
            for j in r['deps']:
                k = ('e', ins[j]['eng'])
                need[k] = max(need.get(k, 0), sigval[j])
            for ln, v in r['lanewaits'].items():
                k = ('l', ln)
                need[k] = max(need.get(k, 0), v)
            waits = []
            sd = seen[r['eng']]
            for k, v in need.items():
                if sd.get(k, 0) >= v:
                    continue
                sd[k] = v
                waits.append((k, v))
            out.append(dict(waits=waits, signal=signal[i]))
        return dict(ins=out, lanes=sorted(self.lane_cnt.keys()), lane_final=dict(self.lane_cnt))

    def finish(self):
        if self.rec:
            return
        for ln, v in self.plan['lane_final'].items():
            self.nc.sync.wait_ge(self.lane_sem[ln], v)

    def dma(self, out, in_, lane, q='sp', **kw):
        return self.op(q, lambda e: e.dma_start(out=out, in_=in_, **kw), reads=[in_], writes=[out],
                       dma=True, lane=lane)

    def mm(self, out, lhsT, rhs, start=True, stop=True, **kw):
        return self.op('pe', lambda e: e.matmul(out, lhsT, rhs, start=start, stop=stop, **kw),
                       reads=[lhsT, rhs], writes=[out])

    def transpose(self, out, in_, ident):
        return self.op('pe', lambda e: e.transpose(out, in_, ident), reads=[in_, ident], writes=[out])

    def act(self, out, in_, func, bias=None, scale=None, accum_out=None, eng='act'):
        reads = [in_]
        kw = {}
        if bias is not None:
            kw['bias'] = bias
            if not isinstance(bias, (int, float)):
                reads.append(bias)
        if scale is not None:
            kw['scale'] = scale
            if not isinstance(scale, (int, float)):
                reads.append(scale)
        writes = [out]
        if accum_out is not None:
            kw['accum_out'] = accum_out
            writes.append(accum_out)
        return self.op(eng, lambda e: e.activation(out=out, in_=in_, func=func, **kw), reads=reads, writes=writes)

    def tt(self, eng, out, in0, in1, op):
        return self.op(eng, lambda e: e.tensor_tensor(out=out, in0=in0, in1=in1, op=op), reads=[in0, in1], writes=[out])

    def ts(self, eng, out, in0, s1, s2, op0, op1=None, accum_out=None):
        reads = [in0]
        if not isinstance(s1, (int, float)):
            reads.append(s1)
        if s2 is not None and not isinstance(s2, (int, float)):
            reads.append(s2)
        kw = {}
        writes = [out]
        if op1 is not None:
            kw['op1'] = op1
        if accum_out is not None:
            kw['accum_out'] = accum_out
            writes.append(accum_out)
        return self.op(eng, lambda e: e.tensor_scalar(out=out, in0=in0, scalar1=s1, scalar2=s2, op0=op0, **kw),
                       reads=reads, writes=writes)

    def stt(self, eng, out, in0, scalar, in1, op0, op1):
        reads = [in0, in1]
        if not isinstance(scalar, (int, float)):
            reads.append(scalar)
        return self.op(eng, lambda e: e.scalar_tensor_tensor(out=out, in0=in0, scalar=scalar, in1=in1, op0=op0, op1=op1),
                       reads=reads, writes=[out])

    def copy(self, eng, out, in_):
        if eng == 'act':
            return self.op(eng, lambda e: e.copy(out=out, in_=in_), reads=[in_], writes=[out])
        return self.op(eng, lambda e: e.tensor_copy(out=out, in_=in_), reads=[in_], writes=[out])

    def memset(self, eng, ap, val):
        return self.op(eng, lambda e: e.memset(ap, val), reads=[], writes=[ap])

    def scan(self, out, d0, d1, initial, op0, op1):
        reads = [d0, d1]
        if not isinstance(initial, (int, float)):
            reads.append(initial)
        return self.op('dve', lambda e: e.tensor_tensor_scan(out=out, data0=d0, data1=d1, initial=initial, op0=op0, op1=op1),
                       reads=reads, writes=[out])

    def generic(self, eng, fn, reads, writes):
        return self.op(eng, fn, reads=reads, writes=writes)


def build_two_pass(make_nc, body):
    nc1 = make_nc()
    p1 = Prog(nc1, None)
    body(p1)
    p1.stack.close()
    plan = p1.make_plan()
    nc2 = make_nc()
    p2 = Prog(nc2, plan)
    body(p2)
    p2.finish()
    p2.stack.close()
    return nc2, plan


T = 4096
NT = 32
D = 1024
KC = 8
TOKC = 2048
TC = 1152
WCOLS = TOKC + TC
EPS = 1e-6


def phase1(P, st, A, layer):
    nc = P.nc
    s = ExitStack()
    W = P.sb(s, "w_in", [128, KC, WCOLS], BF16)
    hT = P.sb(s, "hT", [128, KC, T], BF16)
    ident = P.sb(s, "ident", [128, 128], BF16)
    g1 = P.sb(s, "g1", [128, KC], F32)
    G = P.sb(s, "Gq", [128, 1280], F32)
    cos = P.sb(s, "cos", [128, NT, 8], F32)
    sin = P.sb(s, "sin", [128, NT, 8], F32)
    P.dma(ident[:], A['ident'], 'c0')
    P.dma(g1[:], A['g1'][layer], 'c0')
    P.dma(G[:], A['gq'][layer].partition_broadcast(128), 'c0')
    P.dma(cos[:], A['cos'], 'c0')
    P.dma(sin[:], A['sin'], 'c0')
    P.ts('dve', G[:, 0:512], G[:, 0:512], 0.125, None, ALU.mult)
    P.ts('dve', G[:, 768:1024], G[:, 768:1024], 0.125, None, ALU.mult)

    with ExitStack() as s2:
        wst = [P.sb(s2, "wst%d" % i, [128, WCOLS], F32) for i in range(4)]
        for kc in range(KC):
            b = wst[kc % 4]
            P.dma(b[:], A['win'][layer, kc * 128:(kc + 1) * 128, :], 'wst%d' % (kc % 4), q='sp' if kc % 2 == 0 else 'act')
            half = WCOLS // 2
            P.ts('dve', W[:, kc, 0:half], b[:, 0:half], g1[:, kc:kc + 1], None, ALU.mult)
            P.act(W[:, kc, half:WCOLS], b[:, half:WCOLS], AF.Copy, scale=g1[:, kc:kc + 1])

    with ExitStack() as s2:
        xt = [P.sb(s2, "xt%d" % i, [128, D], F32) for i in range(2)]
        sq = P.sb(s2, "sqj", [128, D], F32)
        hb = [P.sb(s2, "hb%d" % i, [128, D], BF16) for i in range(2)]
        ss = [P.sb(s2, "ss%d" % i, [128, 2], F32) for i in range(2)]
        ptr = [P.ps(s2, "ptr%d" % i, [128, KC, 128], BF16) for i in range(2)]
        for t in range(NT):
            b = t % 2
            P.dma(xt[b][:], A['x'][t * 128:(t + 1) * 128, :], 'xt%d' % b)
            P.act(sq[:], xt[b][:], AF.Square, accum_out=ss[b][:, 0:1])
            P.act(ss[b][:, 1:2], ss[b][:, 0:1], AF.Sqrt, bias=EPS_AP(P), scale=1.0 / D)
            P.op('dve', lambda e, o=ss[b][:, 1:2]: e.reciprocal(out=o, in_=o), reads=[ss[b][:, 1:2]], writes=[ss[b][:, 1:2]])
            P.ts('dve', hb[b][:], xt[b][:], ss[b][:, 1:2], None, ALU.mult)
            for kc in range(KC):
                P.transpose(ptr[b][:, kc, :], hb[b][:, kc * 128:(kc + 1) * 128], ident[:])
            P.copy('act' if t % 2 else 'dve', hT[:, :, t * 128:(t + 1) * 128], ptr[b][:])

    with ExitStack() as s2:
        pp = [P.ps(s2, "ppT%d" % i, [128, 512], F32) for i in range(3)]
        so = [P.sb(s2, "soT%d" % i, [128, 512], F32) for i in range(3)]
        k = 0
        for c in range(TC // 128):
            for j in range(T // 512):
                b = k % 3
                for kc in range(KC):
                    P.mm(pp[b][:], W[:, kc, TOKC + c * 128:TOKC + (c + 1) * 128], hT[:, kc, j * 512:(j + 1) * 512],
                         start=(kc == 0), stop=(kc == KC - 1))
                P.copy('act' if k % 2 else 'dve', so[b][:], pp[b][:])
                P.dma(A['pT'][c * 128:(c + 1) * 128, j * 512:(j + 1) * 512], so[b][:], 'soT%d' % b, q='pool')
                k += 1

    with ExitStack() as s2:
        pg = [P.ps(s2, "pg%d" % i, [128, 512], F32) for i in range(4)]
        ptq = [P.ps(s2, "ptq%d" % i, [128, 4, 128], BF16) for i in range(3)]
        sqhs = [P.sb(s2, "sqh%d" % i, [128, 512], F32) for i in range(3)]
        ssh = [P.sb(s2, "ssh%d" % i, [128, 8], F32) for i in range(4)]
        xn = [P.sb(s2, "xn%d" % i, [128, 512], F32) for i in range(3)]
        qb = [P.sb(s2, "qb%d" % i, [128, 512], BF16) for i in range(3)]
        rts = [P.sb(s2, "rt%d" % i, [128, 4, 8, 8], F32) for i in range(3)]
        qst = [P.sb(s2, "qst%d" % i, [128, 10, 512], BF16) for i in range(2)]
        vst = [P.sb(s2, "vst%d" % i, [128, 768], BF16) for i in range(2)]
        groups = []
        kq = 0
        for t in range(NT):
            for gi in range(4):
                k = t * 4 + gi
                nh = [8, 8, 4, 0][gi]
                qi = None
                if nh:
                    qi = kq % 3
                    kq += 1
                groups.append((t, gi, k, qi))

        def stage(sidx, t, gi, k, q):
            sb_ = (t // 4) % 2
            b = k % 4
            nh = [8, 8, 4, 0][gi]
            nr = [8, 4, 0, 0][gi]
            w = nh * 64
            goff = [0, 512, 1024, 0][gi]
            vb = t % 2
            if sidx == 0:
                for kc in range(KC):
                    P.mm(pg[b][:], hT[:, kc, t * 128:(t + 1) * 128], W[:, kc, gi * 512:(gi + 1) * 512],
                         start=(kc == 0), stop=(kc == KC - 1))
                return
            if sidx == 1:
                if nh:
                    sqh = sqhs[q]
                    P.act(sqh[:, 0:w], pg[b][:, 0:w], AF.Square)
                    P.op('dve', lambda e, o=ssh[b][:, 0:nh], i=sqh[:, 0:w].rearrange("p (h d) -> p h d", d=64):
                         e.tensor_reduce(out=o, in_=i, axis=AX.X, op=ALU.add),
                         reads=[sqh[:, 0:w]], writes=[ssh[b][:, 0:nh]])
                if gi == 2:
                    P.copy('act', vst[vb][:, 0:256], pg[b][:, 256:512])
                if gi == 3:
                    P.copy('act', vst[vb][:, 256:768], pg[b][:, 0:512])
                    P.dma(A['vtok'][t * 128:(t + 1) * 128, :], vst[vb][:], 'vst%d' % vb, q='sp')
                return
            if not nh:
                return
            rt = rts[q]
            xv = xn[q][:, 0:max(nr, 1) * 64].rearrange("p (h d) -> p h d", d=64)
            qv = qb[q][:, 0:max(nr, 1) * 64].rearrange("p (h d) -> p h d", d=64)
            if sidx == 2:
                P.act(ssh[b][:, 0:nh], ssh[b][:, 0:nh], AF.Sqrt, bias=EPS_AP(P), scale=1.0 / 64)
                P.op('dve', lambda e, o=ssh[b][:, 0:nh]: e.reciprocal(out=o, in_=o), reads=[ssh[b][:, 0:nh]], writes=[ssh[b][:, 0:nh]])
                P.tt('dve', xn[q][:, 0:w].rearrange("p (h d) -> p h d", d=64),
                     pg[b][:, 0:w].rearrange("p (h d) -> p h d", d=64),
                     ssh[b][:, 0:nh].unsqueeze(2).broadcast_to([128, nh, 64]), ALU.mult)
            elif sidx == 3:
                if nr:
                    P.tt('pool', xn[q][:, 0:w], xn[q][:, 0:w], G[:, goff:goff + w], ALU.mult)
                    P.copy('act', qb[q][:, 0:w], xn[q][:, 0:w])
                    cb = cos[:, t, :].unsqueeze(1).broadcast_to([128, nr, 8])
                    sb2 = sin[:, t, :].unsqueeze(1).broadcast_to([128, nr, 8])
                    P.tt('dve', rt[:, 0, 0:nr, :], xv[:, :, 0:8], cb, ALU.mult)
                    P.tt('dve', rt[:, 1, 0:nr, :], xv[:, :, 8:16], sb2, ALU.mult)
                    P.tt('pool', rt[:, 2, 0:nr, :], xv[:, :, 8:16], cb, ALU.mult)
                    P.tt('pool', rt[:, 3, 0:nr, :], xv[:, :, 0:8], sb2, ALU.mult)
                else:
                    P.tt('pool', qb[q][:, 0:w], xn[q][:, 0:w], G[:, goff:goff + w], ALU.mult)
            elif sidx == 4:
                if nr:
                    P.tt('dve', qv[:, :, 0:8], rt[:, 0, 0:nr, :], rt[:, 1, 0:nr, :], ALU.subtract)
                    P.tt('pool', qv[:, :, 8:16], rt[:, 2, 0:nr, :], rt[:, 3, 0:nr, :], ALU.add)
                npair = nh // 2
                for pr in range(npair):
                    P.transpose(ptq[q][:, pr, :], qb[q][:, pr * 128:(pr + 1) * 128], ident[:])
            elif sidx == 5:
                npair = nh // 2
                pbase = [0, 4, 8][gi]
                P.copy('act' if gi % 2 else 'dve', qst[sb_][:, pbase:pbase + npair, (t % 4) * 128:(t % 4 + 1) * 128], ptq[q][:, 0:npair, :])
                if t % 4 == 3 and gi == 2:
                    j = t // 4
                    P.dma(A['qkT'][:, j * 512:(j + 1) * 512].rearrange("(a p) n -> p a n", p=128), qst[sb_][:], 'qst%d' % sb_, q='sp')

        NS = 6
        for step in range(len(groups) + NS - 1):
            for sidx in range(NS - 1, -1, -1):
                gidx = step - sidx
                if 0 <= gidx < len(groups):
                    stage(sidx, *groups[gidx])
    s.close()


_eps_cache = {}


def EPS_AP(P):
    return EPS


T = 4096
NEG = -30000.0


class AttnCtx:
    def __init__(self, P, st, consts):
        self.P = P
        self.psS = [P.ps(st, "aS%d" % i, [128, 1024], F32) for i in range(2)]
        self.psO = [P.ps(st, "aO%d" % i, [128, 512], F32) for i in range(2)]
        self.psB = [P.ps(st, "aB%d" % i, [128, 512], F32) for i in range(1)]
        self.pT = [P.sb(st, "apT%d" % i, [128, 1024], BF16) for i in range(3)]
        self.lr = [P.sb(st, "alr%d" % i, [65, 512], F32) for i in range(2)]
        self.F = [P.sb(st, "aF%d" % i, [64, 512], F32) for i in range(2)]
        self.lrh = [P.sb(st, "alrh%d" % i, [128, 512], BF16) for i in range(2)]
        self.lrl = [P.sb(st, "alrl%d" % i, [128, 512], BF16) for i in range(2)]
        self.G2 = [P.sb(st, "aG%d" % i, [64, 512], F32) for i in range(2)]
        for t_ in self.lrh + self.lrl:
            P.memset('pool', t_[:], 0.0)
        self.kF_ids = {id(g): i for i, g in enumerate(self.G2)}
        self.kS = 0
        self.kO = 0
        self.kF = 0
        self.vm = 65
        self.prev = None
        self.deferred = []
        self.c = consts


def _push_block(cx, s_fn, exp_fn, pv_fn, first=False):
    if first:
        for f in cx.deferred:
            f()
        cx.deferred = []
    s_fn()
    d = cx.deferred
    cx.deferred = []
    if cx.prev is not None:
        e, p, epi = cx.prev
        e()
        p()
        if epi is not None:
            epi[0]()
            cx.deferred.append(epi[1])
    for f in d:
        f()
    cx.prev = (exp_fn, pv_fn, None)


def _end_chunk(cx, epi_a, epi_b):
    cx.prev = (cx.prev[0], cx.prev[1], (epi_a, epi_b))


def attn_flush(cx):
    d = cx.deferred
    cx.deferred = []
    if cx.prev is not None:
        e, p, epi = cx.prev
        e()
        p()
        if epi is not None:
            epi[0]()
            d.append(epi[1])
        cx.prev = None
    for f in d:
        f()


def _mk_block(cx, po, Kaug, kr, Qaug, q0, Vaug, kt, lo, hi, masks, extra, first, last):
    P = cx.P
    c = cx.c
    ps = cx.psS[cx.kS % 2]
    pt = cx.pT[cx.kS % 3]
    cx.kS += 1

    def s_fn():
        P.mm(ps[:, lo:hi], Kaug[0:kr, kt * 128:(kt + 1) * 128], Qaug[0:kr, q0 + lo:q0 + hi], start=True, stop=(len(masks) == 0 and extra is None))
        if extra is not None:
            P.mm(ps[:, lo:hi], extra[0][0:64, kt * 128:(kt + 1) * 128], extra[1][0:64, q0 + lo:q0 + hi], start=False, stop=(len(masks) == 0))
        for mi, (mk, m) in enumerate(masks):
            P.mm(ps[:, m * 128:(m + 1) * 128], c['ident'][:], mk[:], start=False, stop=(mi == len(masks) - 1))

    def exp_fn():
        P.act(pt[:, lo:hi], ps[:, lo:hi], AF.Exp)

    def pv_fn():
        P.mm(po[0:cx.vm, lo:hi], Vaug[:, kt, 0:cx.vm], pt[:, lo:hi], start=first, stop=last)

    return s_fn, exp_fn, pv_fn


def _mk_pair(cx, po, Kaug, kr, Qaug, q0, Vaug, ea, eb, first, last):
    P = cx.P
    c = cx.c
    ps = cx.psS[cx.kS % 2]
    pt = cx.pT[cx.kS % 3]
    cx.kS += 1
    (kta, loa, hia, ma), (ktb, lob, hib, mb) = ea, eb

    def s_fn():
        for o, (kt, lo, hi, masks) in ((0, ea), (512, eb)):
            P.mm(ps[:, o + lo:o + hi], Kaug[0:kr, kt * 128:(kt + 1) * 128], Qaug[0:kr, q0 + lo:q0 + hi], start=True, stop=(len(masks) == 0))
            for mi, (mk, m) in enumerate(masks):
                P.mm(ps[:, o + m * 128:o + (m + 1) * 128], c['ident'][:], mk[:], start=False, stop=(mi == len(masks) - 1))

    def exp_fn():
        P.act(pt[:, loa:512 + hib], ps[:, loa:512 + hib], AF.Exp)

    def pv_fn():
        P.mm(po[0:cx.vm, loa:hia], Vaug[:, kta, 0:cx.vm], pt[:, loa:hia], start=first, stop=False)
        P.mm(po[0:cx.vm, lob:hib], Vaug[:, ktb, 0:cx.vm], pt[:, 512 + lob:512 + hib], start=False, stop=last)

    return s_fn, exp_fn, pv_fn


def _mk_factor(cx, po, lng2, gate_c, q0, finish):
    P = cx.P
    c = cx.c
    lr = cx.lr[cx.kF % 2]
    F = cx.F[cx.kF % 2]
    G2 = cx.G2[cx.kF % 2]
    cx.kF += 1
    pb = cx.psB[0]

    lrh = cx.lrh[(cx.kF - 1) % 2]
    lrl = cx.lrl[(cx.kF - 1) % 2]

    def epi_a():
        if lng2 is not None:
            P.dma(G2[:], lng2[gate_c, q0:q0 + 512].partition_broadcast(64), 'ag%d' % ((cx.kF_ids[id(G2)])), q='sp')
        P.ts('dve', lr[64:65, :], po[64:65, :], 1e-18, None, ALU.max)
        P.act(lr[64:65, :], lr[64:65, :], AF.Ln)
        P.copy('dve', lrh[64:65, :], lr[64:65, :])
        P.tt('dve', lrl[64:65, :], lr[64:65, :], lrh[64:65, :], ALU.subtract)

    def epi_b():
        P.mm(pb[:, :], c['negonesb'][:, :], lrh[:, :], start=True, stop=False)
        P.mm(pb[:, :], c['negonesb'][:, :], lrl[:, :], start=False, stop=True)
        P.act(F[:], pb[0:64, :], AF.Exp)
        if lng2 is not None:
            P.tt('pool', F[:], F[:], G2[:], ALU.mult)
        finish(po, F)

    return epi_a, epi_b


def attn_chunk(cx, Kaug, kr, Qaug, j, Vaug, entries, finish, lng2=None, gate_c=None, extra=None):
    po = cx.psO[cx.kO % 2]
    cx.kO += 1
    q0 = j * 512
    n = len(entries)
    ei = 0
    while ei < n:
        if extra is None and ei + 1 < n:
            fns = _mk_pair(cx, po, Kaug, kr, Qaug, q0, Vaug, entries[ei], entries[ei + 1], ei == 0, ei + 1 == n - 1)
            _push_block(cx, *fns, first=(ei == 0))
            ei += 2
            continue
        kt, lo, hi, masks = entries[ei]
        fns = _mk_block(cx, po, Kaug, kr, Qaug, q0, Vaug, kt, lo, hi, masks, extra, ei == 0, ei == n - 1)
        _push_block(cx, *fns, first=(ei == 0))
        ei += 1
    ea, eb = _mk_factor(cx, po, lng2, gate_c, q0, finish)
    _end_chunk(cx, ea, eb)


def causal_entries(j, mc):
    ent = []
    for kt in range(4 * j + 4):
        if kt < 4 * j:
            ent.append((kt, 0, 512, []))
        else:
            m = kt - 4 * j
            ent.append((kt, 128 * m, 512, [(mc, m)]))
    return ent


def window_entries(j, mc, mu):
    ent = []
    for cc in range(-4, 4):
        kt = 4 * j + cc
        if kt < 0:
            continue
        lo = 128 * max(cc, 0)
        hi = 128 * (min(cc + 4, 3) + 1)
        masks = []
        if 0 <= cc <= 3:
            masks.append((mc, cc))
        if 0 <= cc + 4 <= 3:
            masks.append((mu, cc + 4))
        ent.append((kt, lo, hi, masks))
    return ent


def load_consts(P, st, A):
    c = {}
    c['ident'] = P.sb(st, "c_ident", [128, 128], BF16)
    c['mc'] = P.sb(st, "c_mc", [128, 128], BF16)
    c['mu'] = P.sb(st, "c_mu", [128, 128], BF16)
    c['zeros'] = P.sb(st, "c_zeros", [128, 128], BF16)
    c['ident_w'] = P.sb(st, "c_identw", [128, 512], BF16)
    c['negones'] = P.sb(st, "c_negones", [65, 64], F32)
    c['negonesb'] = P.sb(st, "c_negonesb", [128, 128], BF16)
    P.dma(c['ident'][:], A['ident'], 'c0')
    P.dma(c['mc'][:], A['mc'], 'c0')
    P.dma(c['mu'][:], A['mu'], 'c0')
    P.memset('dve', c['zeros'][:], 0.0)
    P.memset('dve', c['ident_w'][:], 0.0)
    P.memset('dve', c['negones'][:], -1.0)
    P.memset('dve', c['negonesb'][:], -1.0)
    return c


def phase_fox(P, A, layer, consts):
    with ExitStack() as st:
        cx = AttnCtx(P, st, consts)
        cf = P.sb(st, "f_cf", [4, T], F32)
        tmp = P.sb(st, "f_tmp", [4, T], F32)
        ones = P.sb(st, "f_ones", [4, T], F32)
        fb = P.sb(st, "f_fb", [4, 2], F32)
        cs = P.sb(st, "f_cs", [4, 3, T], BF16)
        ncs = P.sb(st, "f_ncs", [4, 3, T], BF16)
        P.dma(cf[:], A['pT'][1048:1052, :], 'fx0')
        P.dma(fb[:, 0:1], A['fb'][layer], 'fx0')
        P.ts('dve', fb[:, 1:2], fb[:, 0:1], -1.0, None, ALU.mult)
        P.memset('pool', ones[:], 1.0)
        P.act(tmp[:], cf[:], AF.Exp, bias=fb[:, 1:2], scale=-1.0)
        P.act(tmp[:], tmp[:], AF.Ln, bias=1.0)
        P.scan(cf[:], ones[:], tmp[:], 0.0, ALU.mult, ALU.subtract)
        P.copy('dve', cs[:, 0, :], cf[:])
        P.tt('dve', tmp[:], cf[:], cs[:, 0, :], ALU.subtract)
        P.copy('dve', cs[:, 1, :], tmp[:])
        P.tt('dve', tmp[:], tmp[:], cs[:, 1, :], ALU.subtract)
        P.copy('dve', cs[:, 2, :], tmp[:])
        P.ts('dve', ncs[:].rearrange("p a t -> p (a t)"), cs[:].rearrange("p a t -> p (a t)"), -1.0, None, ALU.mult)
        Qs = [P.sb(st, "f_Q%d" % i, [128, T], BF16) for i in range(2)]
        Ks = [P.sb(st, "f_K%d" % i, [128, T], BF16) for i in range(2)]
        Vs = [P.sb(st, "f_V%d" % i, [128, 32, 65], BF16) for i in range(2)]
        ob = [P.sb(st, "f_ob%d" % i, [64, 512], BF16) for i in range(2)]
        for i in range(2):
            P.memset('pool', Qs[i][64:128, :], 0.0)
            P.memset('pool', Ks[i][64:128, :], 0.0)
            P.memset('pool', Qs[i][64:70, :], 1.0)
            P.memset('pool', Ks[i][64:70, :], 1.0)
            P.memset('pool', Vs[i][:, :, 64:65], 1.0)

        def load(h):
            Q = Qs[h % 2]; K = Ks[h % 2]; V = Vs[h % 2]
            P.dma(Q[0:64, :], A['qkT'][768 + 64 * h:768 + 64 * (h + 1), :], 'fxq%d' % (h % 2))
            P.dma(K[0:64, :], A['qkT'][1024 + 64 * h:1024 + 64 * (h + 1), :], 'fxk%d' % (h % 2), q='act')
            for i in range(3):
                P.dma(Q[64 + i:65 + i, :], cs[h:h + 1, i, :], 'fxq%d' % (h % 2))
                P.dma(K[67 + i:68 + i, :], ncs[h:h + 1, i, :], 'fxk%d' % (h % 2), q='act')
            P.dma(V[:, :, 0:64], A['vtok'][:, 512 + 64 * h:512 + 64 * (h + 1)].rearrange("(n p) d -> p n d", p=128), 'fxv%d' % (h % 2))

        load(0)
        for h in range(4):
            if h + 1 < 4:
                load(h + 1)
            Q = Qs[h % 2]; K = Ks[h % 2]; V = Vs[h % 2]
            for j in range(8):
                def fin(po, F, o=ob[j % 2], j=j, h=h):
                    P.tt('dve', o[:], po[0:64, :], F[:], ALU.mult)
                    P.dma(A['mixT'][768 + 64 * h:768 + 64 * (h + 1), j * 512:(j + 1) * 512], o[:], 'fxo%d' % (j % 2), q='sp')
                attn_chunk(cx, K, 128, Q, j, V, causal_entries(j, consts['mc']), fin)
            attn_flush(cx)

import math, os
STAGE = int(os.environ.get('STAGE', '99'))

T = 4096
LN8 = math.log(0.125)


def phase_hgrn(P, A, layer, consts):
    for ct in range(2):
        with ExitStack() as st:
            B = [P.sb(st, "hB%d" % i, [128, T], F32) for i in range(5)]
            qt = P.sb(st, "h_qt", [128, T], BF16)
            kt = P.sb(st, "h_kt", [128, 2, T], BF16)
            qg = P.sb(st, "h_qg", [128, T], BF16)
            kd = P.sb(st, "h_kd", [128, T], BF16)
            kdt = P.sb(st, "h_kdt", [128, 32, 2, 128], BF16)
            Vt = P.sb(st, "h_Vt", [128, 32, 128], BF16)
            Vz = P.sb(st, "h_Vz", [128, 32, 2, 128], BF16)
            Sbd = P.sb(st, "h_Sbd", [128, 64, 128], BF16)
            rst = P.sb(st, "h_rst", [128, T], BF16)
            sm = P.sb(st, "h_sm", [128, 8], F32)
            dl = P.sb(st, "h_dl", [128, 64], F32)
            mh = P.sb(st, "h_mh", [128, 128], BF16)
            bones = P.sb(st, "h_bones", [128, 128], BF16)
            ident = consts['ident']
            P.dma(mh[:], A['mh'], 'hg0')
            P.dma(bones[:], A['bones'], 'hg0')
            P.dma(sm[:, 0:2], A['lbl'][ct], 'hg0')
            P.dma(sm[:, 4:5], A['og'][layer], 'hg0')
            P.dma(B[0][:], A['pT'][256 + ct * 128:256 + (ct + 1) * 128, :], 'hgz')
            P.dma(B[3][:], A['pT'][ct * 128:(ct + 1) * 128, :], 'hgq', q='act')
            P.dma(Vt[:], A['vtok'][:, ct * 128:(ct + 1) * 128].rearrange("(n p) d -> p n d", p=128), 'hgv', q='pool')
            P.memset('pool', Vz[:], 0.0)
            for hh in range(2):
                P.dma(Vz[:, :, hh, hh * 64:(hh + 1) * 64],
                      A['vtok'][:, ct * 128 + hh * 64:ct * 128 + (hh + 1) * 64].rearrange("(n p) d -> p n d", p=128), 'hgv', q='pool')
            P.memset('pool', Sbd[:], 0.0)
            P.memset('pool', kt[:], 0.0)
            P.memset('pool', kdt[:], 0.0)
            P.memset('pool', rst[:], 1.0)
            P.memset('pool', rst[:].rearrange("p (c s) -> p c s", s=64)[:, :, 0:1], 0.0)
            lb = sm[:, 2:3]; oml = sm[:, 3:4]; noml = sm[:, 5:6]
            if layer == 0:
                P.memset('dve', lb, 0.0)
            else:
                P.act(sm[:, 0:2], sm[:, 0:2], AF.Exp)
                P.tt('dve', sm[:, 6:7], sm[:, 0:1], sm[:, 1:2], ALU.add)
                P.op('dve', lambda e, o=sm[:, 6:7]: e.reciprocal(out=o, in_=o), reads=[sm[:, 6:7]], writes=[sm[:, 6:7]])
                P.tt('dve', lb, sm[:, 1:2], sm[:, 6:7], ALU.mult)
            P.ts('dve', oml, lb, -1.0, 1.0, ALU.mult, ALU.add)
            P.ts('dve', noml, oml, -1.0, None, ALU.mult)
            P.act(B[0][:], B[0][:], AF.Sigmoid)
            P.ts('dve', B[1][:], B[0][:], oml, lb, ALU.mult, ALU.add)
            P.act(B[1][:], B[1][:], AF.Ln)
            P.scan(B[2][:], rst[:], B[1][:], 0.0, ALU.mult, ALU.add)
            P.ts('dve', B[1][:], B[0][:], noml, oml, ALU.mult, ALU.add)
            G3 = B[2][:].rearrange("p (c s) -> p c s", s=64)
            D3 = B[0][:].rearrange("p (c s) -> p c s", s=64)
            P.tt('dve', D3, G3, G3[:, :, 31:32].broadcast_to([128, 64, 64]), ALU.subtract)
            P.act(B[4][:], B[0][:], AF.Exp, bias=LN8)
            P.tt('dve', qt[:], B[3][:], B[4][:], ALU.mult)
            P.act(B[4][:], B[0][:], AF.Exp, scale=-1.0)
            P.tt('dve', kt[0:64, 0, :], B[1][0:64, :], B[4][0:64, :], ALU.mult)
            P.tt('dve', kt[64:128, 1, :], B[1][64:128, :], B[4][64:128, :], ALU.mult)
            P.act(B[4][:], B[2][:], AF.Exp, bias=LN8)
            P.tt('dve', qg[:], B[3][:], B[4][:], ALU.mult)
            P.tt('dve', D3, G3[:, :, 63:64].broadcast_to([128, 64, 64]), G3, ALU.subtract)
            P.act(B[4][:], B[0][:], AF.Exp)
            P.tt('dve', kd[:], B[1][:], B[4][:], ALU.mult)
            P.act(dl[:].unsqueeze(2), G3[:, :, 63:64], AF.Exp)
            P.memset('dve', dl[:, 0:1], 0.0)
            KV = B[0]; dfull = B[1]; Sall = B[3]; oT = B[4]
            if STAGE < 1:
                P.dma(A['mixT'][0:128, 0:T], kd[:], 'dbg'); continue
            with ExitStack() as s2:
                ptr = [P.ps(s2, "h_ptr%d" % i, [128, 8, 128], BF16) for i in range(2)]
                for g in range(4):
                    for i in range(8):
                        tl = g * 8 + i
                        P.transpose(ptr[g % 2][:, i, :], kd[:, tl * 128:(tl + 1) * 128], ident[:])
                    P.copy('act', kdt[0:64, g * 8:(g + 1) * 8, 0, :], ptr[g % 2][0:64, :, :])
                    P.copy('dve', kdt[64:128, g * 8:(g + 1) * 8, 1, :], ptr[g % 2][64:128, :, :])
            with ExitStack() as s2:
                pkv = [P.ps(s2, "h_pkv%d" % i, [128, 4, 128], F32) for i in range(2)]
                KV3 = KV[:].rearrange("p (v c) -> p v c", c=64)
                for g in range(16):
                    pk = pkv[g % 2]
                    for i in range(4):
                        c = g * 4 + i
                        tl = c // 2; hf = c % 2
                        P.mm(pk[:, i, :], kdt[:, tl, hf, :], Vt[:, tl, :], start=True, stop=True)
                    for hh in range(2):
                        P.copy('act' if hh else 'dve', KV3[hh * 64:(hh + 1) * 64, :, g * 4:(g + 1) * 4],
                               pk[hh * 64:(hh + 1) * 64, :, hh * 64:(hh + 1) * 64].rearrange("p g v -> p v g"))
            if STAGE < 2:
                P.dma(A['mixT'][0:128, 0:T], kd[:], 'dbg'); continue
            P.copy('pool', dfull[:].rearrange("p (v c) -> p v c", c=64), dl[:].unsqueeze(1).broadcast_to([128, 64, 64]))
            P.scan(Sall[:], dfull[:], KV[:], 0.0, ALU.mult, ALU.add)
            S3 = Sall[:].rearrange("p (v c) -> p v c", c=64)
            for hh in range(2):
                P.copy('dve' if hh else 'act', Sbd[hh * 64:(hh + 1) * 64, 1:64, hh * 64:(hh + 1) * 64],
                       S3[hh * 64:(hh + 1) * 64, :, 0:63].rearrange("p v c -> p c v"))
            if STAGE < 3:
                P.dma(A['mixT'][0:128, 0:T], kd[:], 'dbg'); continue
            with ExitStack() as s2:
                pA = [P.ps(s2, "h_pA%d" % i, [128, 128], F32) for i in range(4)]
                po = [P.ps(s2, "h_po%d" % i, [128, 128], F32) for i in range(2)]
                Am = [P.sb(s2, "h_Am%d" % i, [128, 128], BF16) for i in range(4)]
                def scores(tl):
                    cols = slice(tl * 128, (tl + 1) * 128)
                    for hh in range(2):
                        i = (tl % 2) * 2 + hh
                        P.mm(pA[i][:], kt[:, hh, cols], qt[:, cols], start=True, stop=True)
                        P.tt('dve', Am[i][:], pA[i][:], mh[:], ALU.mult)

                def outs(tl):
                    cols = slice(tl * 128, (tl + 1) * 128)
                    p_ = po[tl % 2]
                    P.mm(p_[:], Vz[:, tl, 0, :], Am[(tl % 2) * 2][:], start=True, stop=False)
                    P.mm(p_[:], Vz[:, tl, 1, :], Am[(tl % 2) * 2 + 1][:], start=False, stop=False)
                    P.mm(p_[:, 0:64], Sbd[:, 2 * tl, :], qg[:, tl * 128:tl * 128 + 64], start=False, stop=False)
                    P.mm(p_[:, 64:128], Sbd[:, 2 * tl + 1, :], qg[:, tl * 128 + 64:tl * 128 + 128], start=False, stop=True)
                    P.copy('act', oT[:, cols], p_[:])

                for tl in range(33):
                    if tl < 32:
                        scores(tl)
                    if tl >= 1:
                        outs(tl - 1)
            if STAGE < 4:
                P.dma(A['mixT'][0:128, 0:T], kd[:], 'dbg'); continue
            with ExitStack() as s2:
                pss = [P.ps(s2, "h_pss%d" % i, [128, 512], F32) for i in range(2)]
                sq = [P.sb(s2, "h_sq%d" % i, [128, 512], BF16) for i in range(2)]
                rs = [P.sb(s2, "h_rs%d" % i, [128, 512], F32) for i in range(2)]
                ag = [P.sb(s2, "h_ag%d" % i, [128, 512], F32) for i in range(2)]
                ob = [P.sb(s2, "h_ob%d" % i, [128, 512], BF16) for i in range(2)]
                for j in range(8):
                    b = j % 2
                    cols = slice(j * 512, (j + 1) * 512)
                    P.dma(ag[b][:], A['pT'][512 + ct * 128:512 + (ct + 1) * 128, cols], 'hga%d' % b)
                    P.act(sq[b][:], oT[:, cols], AF.Square)
                    P.mm(pss[b][:], bones[:], sq[b][:], start=True, stop=True)
                    P.act(rs[b][:], pss[b][:], AF.Sqrt, bias=1e-6, scale=1.0 / 64)
                    P.op('dve', lambda e, o=rs[b][:]: e.reciprocal(out=o, in_=o), reads=[rs[b][:]], writes=[rs[b][:]])
                    P.act(ag[b][:], ag[b][:], AF.Silu)
                    P.stt('dve', rs[b][:], oT[:, cols], sm[:, 4:5], rs[b][:], ALU.mult, ALU.mult)
                    P.tt('pool', ob[b][:], rs[b][:], ag[b][:], ALU.mult)
                    P.dma(A['mixT'][ct * 128:(ct + 1) * 128, cols], ob[b][:], 'hgo%d' % b, q='pool')

import os
STAGE = int(os.environ.get('STAGE', '99'))

T = 4096
NEG = -30000.0


def phase_nsa(P, A, layer, consts):
    c = consts
    ident = c['ident']
    with ExitStack() as st:
        cx = AttnCtx(P, st, consts)
        with ExitStack() as s0:
            lng = P.sb(s0, "n_lng", [24, T], F32)
            P.dma(lng[:], A['pT'][1024:1048, :], 'ns0')
            P.act(lng[:], lng[:], AF.Sigmoid)
            P.dma(A['gsig'], lng[:], 'ns0')
        lng2 = A['gsig']
        ovaug = P.sb(st, "n_ov", [128, 2, 72], BF16)
        wc = P.sb(st, "n_wc", [128, 3200], BF16)
        addm = P.sb(st, "n_addm", [128, 32, 64], F32)
        P.dma(ovaug[:], A['ovaug'], 'ns0')
        P.dma(wc[:], A['wc'], 'ns0')
        P.dma(addm[:], A['addmask'], 'ns0')
        kcTs = [P.sb(st, "n_kcT%d" % i, [128, 256], BF16) for i in range(2)]
        vcAs = [P.sb(st, "n_vcA%d" % i, [128, 2, 65], BF16) for i in range(2)]
        for g in range(2):
            kcT = kcTs[g]; vcA = vcAs[g]
            with ExitStack() as s2:
                w1 = P.sb(s2, "n_w1", [64, 32, 128], BF16)
                w1f = P.sb(s2, "n_w1f", [64, 32, 128], F32)
                w2 = P.sb(s2, "n_w2", [128, 64], BF16)
                w2f = P.sb(s2, "n_w2f", [128, 64], F32)
                posT = P.sb(s2, "n_posT", [64, 32], BF16)
                posf = P.sb(s2, "n_posf", [64, 32], F32)
                posb = P.sb(s2, "n_posb", [64, 32, 256], BF16)
                srcf = P.sb(s2, "n_srcf", [64, T], F32)
                srcb = P.sb(s2, "n_srcb", [64, T], BF16)
                bias = P.sb(s2, "n_bias", [128, 1], F32)
                xb = P.sb(s2, "n_xb", [128, 256], F32)
                x2 = P.sb(s2, "n_x2", [128, 256], F32)
                hid = P.sb(s2, "n_hid", [128, 256], BF16)
                ktm = P.sb(s2, "n_ktm", [128, 64], F32)
                kts = P.sb(s2, "n_kts", [128, 64], F32)
                ktb = P.sb(s2, "n_ktb", [128, 128], BF16)
                sm = P.sb(s2, "n_sm", [128, 4], F32)
                rt = P.sb(s2, "n_rt", [128, 4, 8], F32)
                kng = P.sb(s2, "n_kng", [128, 64], F32)
                cosc = P.sb(s2, "n_cosc", [128, 2, 8], F32)
                sinc = P.sb(s2, "n_sinc", [128, 2, 8], F32)
                ph = cx.psS[0]; pb = cx.psS[1]; po = cx.psO[0]
                pt = P.ps(s2, "n_pt", [128, 128], BF16)
                P.dma(kng[:], A['kng'][layer].partition_broadcast(128), 'ns1')
                P.dma(cosc[:], A['cosc'], 'ns1')
                P.dma(sinc[:], A['sinc'], 'ns1')
                P.memset('dve', hid[:], 0.0)
                P.memset('dve', vcA[:], 0.0)
                P.memset('dve', kcT[:], 0.0)
                P.memset('dve', ktb[:], 0.0)
                for which in range(2):
                    P.dma(w1f[:], A['w1r'][layer, which], 'ns2')
                    P.dma(w2f[:], A['w2'][layer, which], 'ns2')
                    P.dma(posf[:], A['posT'][layer, which], 'ns2')
                    P.dma(srcf[:], A['pT'][768 + 128 * which + 64 * g:768 + 128 * which + 64 * (g + 1), :], 'ns3', q='act')
                    P.copy('dve', w1[:], w1f[:])
                    P.copy('dve', w2[:], w2f[:])
                    P.copy('dve', posT[:], posf[:])
                    P.copy('dve', posb[:], posT[:].unsqueeze(2).broadcast_to([64, 32, 256]))
                    P.copy('act', srcb[:], srcf[:])
                    for l in range(32):
                        P.mm(ph[:, 0:255], w1[:, l, :], srcb[:].rearrange("p (n s) -> p n s", s=16)[:, (l // 16):(l // 16) + 255, l % 16], start=(l == 0), stop=False)
                    for l in range(32):
                        P.mm(ph[:, 0:255], w1[:, l, :], posb[:, l, 0:255], start=False, stop=(l == 31))
                    P.copy('act', xb[:, 0:255], ph[:, 0:255])
                    P.tt('dve', x2[:, 0:255], xb[:, 0:255], xb[:, 0:255], ALU.mult)
                    P.ts('dve', x2[:, 0:255], x2[:, 0:255], 0.044715, 1.0, ALU.mult, ALU.add)
                    P.tt('dve', x2[:, 0:255], x2[:, 0:255], xb[:, 0:255], ALU.mult)
                    P.act(x2[:, 0:255], x2[:, 0:255], AF.Sigmoid, scale=1.5957691216057308)
                    P.tt('dve', hid[:, 0:255], x2[:, 0:255], xb[:, 0:255], ALU.mult)
                    for nt in range(2):
                        P.mm(po[:, 0:64], hid[:, nt * 128:(nt + 1) * 128], w2[:], start=True, stop=True)
                        if which == 1:
                            nr = 128 if nt == 0 else 127
                            P.copy('act', vcA[0:nr, nt, 0:64], po[0:nr, 0:64])
                            P.memset('dve', vcA[0:nr, nt, 64:65], 1.0)
                        else:
                            P.act(kts[:], po[:, 0:64], AF.Square, accum_out=sm[:, 0:1])
                            P.act(sm[:, 1:2], sm[:, 0:1], AF.Sqrt, bias=1e-6, scale=1.0 / 64)
                            P.op('dve', lambda e, o=sm[:, 1:2]: e.reciprocal(out=o, in_=o), reads=[sm[:, 1:2]], writes=[sm[:, 1:2]])
                            P.stt('dve', ktm[:], po[:, 0:64], sm[:, 1:2], kng[:], ALU.mult, ALU.mult)
                            P.copy('act', ktb[:, 0:64], ktm[:])
                            P.tt('dve', rt[:, 0, :], ktm[:, 0:8], cosc[:, nt, :], ALU.mult)
                            P.tt('dve', rt[:, 1, :], ktm[:, 8:16], sinc[:, nt, :], ALU.mult)
                            P.tt('dve', rt[:, 2, :], ktm[:, 8:16], cosc[:, nt, :], ALU.mult)
                            P.tt('dve', rt[:, 3, :], ktm[:, 0:8], sinc[:, nt, :], ALU.mult)
                            P.tt('dve', ktb[:, 0:8], rt[:, 0, :], rt[:, 1, :], ALU.subtract)
                            P.tt('dve', ktb[:, 8:16], rt[:, 2, :], rt[:, 3, :], ALU.add)
                            P.transpose(pt[:], ktb[:], ident[:])
                            P.copy('dve', kcT[0:64, nt * 128:(nt + 1) * 128], pt[0:64, :])
            P.memset('dve', kcT[0:64, 255:256], 0.0)
        selT = P.sb(st, "n_selT", [128, T], BF16)
        imp = P.sb(st, "n_imp", [128, 32, 64], F32)
        acc = [P.sb(st, "n_acc%d" % i, [64, T], F32) for i in range(4)]
        Q = [P.sb(st, "n_Q%d" % i, [128, T], BF16) for i in range(4)]
        Ks = P.sb(st, "n_Ks", [128, T], BF16)
        Kw = P.sb(st, "n_Kw", [128, T], BF16)
        Vs = P.sb(st, "n_Vs", [128, 32, 65], BF16)
        Vw = P.sb(st, "n_Vw", [128, 32, 65], BF16)
        for g in range(2):
            kcT = kcTs[g]; vcA = vcAs[g]
            for hh in range(4):
                h = 4 * g + hh
                P.memset('pool', Q[hh][64:128, :], 0.0)
                P.dma(Q[hh][0:64, :], A['qkT'][64 * h:64 * (h + 1), :], 'nsq%d' % hh)
            P.dma(Ks[0:64, :], A['qkT'][512 + 64 * g:512 + 64 * (g + 1), :], 'nsk')
            P.dma(Ks[64:128, :], A['eall'], 'nsk')
            P.dma(Kw[0:64, :], A['qkT'][640 + 64 * g:640 + 64 * (g + 1), :], 'nsk')
            P.memset('pool', Kw[64:128, :], 0.0)
            P.memset('pool', Vs[:, :, 64:65], 1.0)
            P.memset('pool', Vw[:, :, 64:65], 1.0)
            P.dma(Vs[:, :, 0:64], A['vtok'][:, 256 + 64 * g:256 + 64 * (g + 1)].rearrange("(n p) d -> p n d", p=128), 'nsv', q='act')
            P.dma(Vw[:, :, 0:64], A['vtok'][:, 384 + 64 * g:384 + 64 * (g + 1)].rearrange("(n p) d -> p n d", p=128), 'nsv', q='act')
            with ExitStack() as s2:
                pimp = [cx.psS[i][:, 512:800].rearrange("p (a b) -> p a b", b=72) for i in range(2)]
                pTc = [P.sb(s2, "n_pTc%d" % i, [128, 512], BF16) for i in range(3)]
                rinvs = [P.sb(s2, "n_rinv%d" % i, [128, 4], F32) for i in range(2)]
                kc_ = 0
                kch = 0
                for hh in range(4):
                    h = 4 * g + hh
                    for j in range(8):
                        q0 = j * 512
                        po = cx.psO[cx.kO % 2]
                        cx.kO += 1
                        pim = pimp[kch % 2]
                        rinv = rinvs[kch % 2]
                        kch += 1
                        tiles = []
                        for nt in range(2):
                            off = 2048 * nt + 31 - 512 * j
                            if -off + 511 < 0:
                                continue
                            tiles.append((nt, off))
                        for ti, (nt, off) in enumerate(tiles):
                            ps = cx.psS[cx.kS % 2][:, 0:512]
                            cx.kS += 1
                            ptc = pTc[kc_ % 3]
                            kc_ += 1
                            first = (ti == 0)
                            last = (ti == len(tiles) - 1)

                            def s_fn(ps=ps, nt=nt, off=off, first=first, po=po, pim=pim, hh=hh, q0=q0):
                                full = (-off >= 2032)
                                P.mm(ps[:], kcT[:, nt * 128:(nt + 1) * 128], Q[hh][:, q0:q0 + 512], start=True, stop=full)
                                if not full:
                                    ci0 = -off + 511
                                    P.mm(ps[:], ident[:], wc[:, ci0:ci0 + 512], start=False, stop=True)

                            def exp_fn(ps=ps, ptc=ptc):
                                P.act(ptc[:], ps[:], AF.Exp)

                            def pv_fn(po=po, pim=pim, ptc=ptc, nt=nt, last=last, first=first):
                                P.mm(po[0:65, :], vcA[:, nt, :], ptc[:], start=first, stop=last)
                                for m in range(4):
                                    P.mm(pim[:, m, :], ptc[:, m * 128:(m + 1) * 128], ovaug[:, nt, :], start=(first and m == 0), stop=last)

                            _push_block(cx, s_fn, exp_fn, pv_fn, first=first)

                        def fin(po_, F, hh=hh, q0=q0, pim=pim, rinv=rinv, j=j):
                            for m in range(4):
                                tq = j * 4 + m
                                if hh == 0:
                                    P.ts('dve', imp[:, tq, :], pim[:, m, 0:64], rinv[:, m:m + 1], None, ALU.mult)
                                else:
                                    P.stt('dve', imp[:, tq, :], pim[:, m, 0:64], rinv[:, m:m + 1], imp[:, tq, :], ALU.mult, ALU.add)
                            P.tt('dve', acc[hh][:, q0:q0 + 512], po_[0:64, :], F[:], ALU.mult)

                        ea, eb = _mk_factor(cx, po, lng2, h, q0, fin)

                        def ea2(ea=ea, pim=pim, rinv=rinv):
                            ea()
                            P.ts('dve', rinv[:, 0:4].unsqueeze(2), pim[:, :, 64:65], 1e-30, None, ALU.max)
                            P.op('dve', lambda e, o=rinv[:, 0:4]: e.reciprocal(out=o, in_=o), reads=[rinv[:, 0:4]], writes=[rinv[:, 0:4]])

                        _end_chunk(cx, ea2, eb)
                attn_flush(cx)
            if STAGE < 2:
                continue
            with ExitStack() as s2:
                wk = [P.sb(s2, "n_wk%d" % i, [128, 64], F32) for i in range(2)]
                w2_ = [P.sb(s2, "n_wk2%d" % i, [128, 64], F32) for i in range(2)]
                m8 = [P.sb(s2, "n_m8%d" % i, [128, 16], F32) for i in range(2)]
                sb_ = [P.sb(s2, "n_sb%d" % i, [128, 128], BF16) for i in range(2)]
                pts = [P.ps(s2, "n_pts%d" % i, [128, 128], BF16) for i in range(1)] * 2
                P.memset('pool', sb_[0][:], 0.0)
                P.memset('pool', sb_[1][:], 0.0)
                for tq in range(32):
                    b = tq % 2
                    P.tt('dve', wk[b][:], imp[:, tq, :], addm[:, tq, :], ALU.add)
                    P.op('dve', lambda e, o=m8[b][:, 0:8], i=wk[b][:]: e.max(out=o, in_=i), reads=[wk[b][:]], writes=[m8[b][:, 0:8]])
                    P.op('dve', lambda e, o=w2_[b][:], r=m8[b][:, 0:8], i=wk[b][:]: e.match_replace(out=o, in_to_replace=r, in_values=i, imm_value=-3.0e38),
                         reads=[m8[b][:, 0:8], wk[b][:]], writes=[w2_[b][:]])
                    P.op('dve', lambda e, o=m8[b][:, 8:16], i=w2_[b][:]: e.max(out=o, in_=i), reads=[w2_[b][:]], writes=[m8[b][:, 8:16]])
                    P.ts('dve', w2_[b][:], wk[b][:], m8[b][:, 15:16], None, ALU.is_ge)
                    P.ts('dve', wk[b][:], wk[b][:], -5.0e29, None, ALU.is_gt)
                    P.tt('dve', wk[b][:], wk[b][:], w2_[b][:], ALU.mult)
                    P.ts('dve', sb_[b][:, 64:128], wk[b][:], -1.0, -NEG, ALU.add, ALU.mult)
                    P.transpose(pts[b][:], sb_[b][:], ident[:])
                    P.copy('act', selT[64:128, tq * 128:(tq + 1) * 128], pts[b][64:128, :])
            for hh in range(4):
                P.dma(Q[hh][64:128, :], selT[64:128, :], 'nsq%d' % hh, q='sp' if hh % 2 else 'act')
            if STAGE < 3:
                continue
            with ExitStack() as s2:
                tmp = [P.sb(s2, "n_tmp%d" % i, [64, 512], F32) for i in range(2)]
                ob = [P.sb(s2, "n_ob%d" % i, [64, 512], BF16) for i in range(1)] * 2
                for hh in range(4):
                    h = 4 * g + hh
                    for j in range(8):
                        q0 = j * 512
                        a = acc[hh][:, q0:q0 + 512]

                        def fin_s(po_, F, a=a):
                            P.tt('dve', tmp[0][:], po_[0:64, :], F[:], ALU.mult)
                            P.tt('pool', a, a, tmp[0][:], ALU.add)

                        def fin_w(po_, F, a=a, j=j, h=h, q0=q0):
                            P.tt('dve', tmp[1][:], po_[0:64, :], F[:], ALU.mult)
                            P.tt('pool', ob[j % 2][:], a, tmp[1][:], ALU.add)
                            if STAGE >= 5:
                                P.dma(A['mixT'][256 + 64 * h:256 + 64 * (h + 1), q0:q0 + 512], ob[j % 2][:], 'nso%d' % (j % 2), q='sp')

                        attn_chunk(cx, Ks, 128, Q[hh], j, Vs, causal_entries(j, c['mc']), fin_s, lng2=lng2, gate_c=8 + h)
                        if STAGE >= 4:
                            attn_chunk(cx, Kw, 128, Q[hh], j, Vw, window_entries(j, c['mc'], c['mu']), fin_w, lng2=lng2, gate_c=16 + h)
                attn_flush(cx)


T = 4096
D = 1024
FF = 4096


def phase_wo(P, A, layer, consts, x_in, x_mid, per_tile=None):
    ident = consts['ident']
    with ExitStack() as st:
        Wo = P.sb(st, "wo", [128, 8, D], BF16)
        with ExitStack() as s2:
            wst = [P.sb(s2, "wost%d" % i, [128, D], F32) for i in range(2)]
            for kc in range(8):
                b = wst[kc % 2]
                P.dma(b[:], A['wo'][layer, kc * 128:(kc + 1) * 128, :], 'wost%d' % (kc % 2))
                P.copy('act' if kc % 2 else 'dve', Wo[:, kc, :], b[:])
        mx = [P.sb(st, "wo_mx%d" % i, [128, 8, 512], BF16) for i in range(2)]
        xt = [P.sb(st, "wo_xt%d" % i, [128, D], F32) for i in range(2)]
        xm = [P.sb(st, "wo_xm%d" % i, [128, D], F32) for i in range(2)]
        sq = P.sb(st, "wo_sq", [128, D], BF16)
        hb = [P.sb(st, "wo_hb%d" % i, [128, D], BF16) for i in range(2)]
        ss = [P.sb(st, "wo_ss%d" % i, [128, 2], F32) for i in range(2)]
        hst = [P.sb(st, "wo_hst%d" % i, [128, 8, 512], BF16) for i in range(1)] * 2
        po = [P.ps(st, "wo_po%d" % i, [128, 512], F32) for i in range(4)]
        ptr = [P.ps(st, "wo_ptr%d" % i, [128, 8, 128], BF16) for i in range(2)]
        for t in range(32):
            j = t // 4
            b = t % 2
            if t % 4 == 0:
                P.dma(mx[j % 2][:], A['mixT'][:, j * 512:(j + 1) * 512].rearrange("(a p) n -> p a n", p=128), 'womx%d' % (j % 2))
            P.dma(xt[b][:], x_in[t * 128:(t + 1) * 128, :], 'woxt%d' % b, q='act')
            for half in range(2):
                pp = po[(t % 2) * 2 + half]
                for kc in range(8):
                    P.mm(pp[:], mx[j % 2][:, kc, (t % 4) * 128:(t % 4 + 1) * 128], Wo[:, kc, half * 512:(half + 1) * 512],
                         start=(kc == 0), stop=(kc == 7))
                P.tt('dve', xm[b][:, half * 512:(half + 1) * 512], pp[:], xt[b][:, half * 512:(half + 1) * 512], ALU.add)
            P.dma(x_mid[t * 128:(t + 1) * 128, :], xm[b][:], 'woxm%d' % b, q='pool')
            P.act(sq[:], xm[b][:], AF.Square, accum_out=ss[b][:, 0:1])
            P.act(ss[b][:, 1:2], ss[b][:, 0:1], AF.Sqrt, bias=1e-6, scale=1.0 / D)
            P.op('dve', lambda e, o=ss[b][:, 1:2]: e.reciprocal(out=o, in_=o), reads=[ss[b][:, 1:2]], writes=[ss[b][:, 1:2]])
            P.ts('dve', hb[b][:], xm[b][:], ss[b][:, 1:2], None, ALU.mult)
            for kc in range(8):
                P.transpose(ptr[b][:, kc, :], hb[b][:, kc * 128:(kc + 1) * 128], ident[:])
            P.copy('act', hst[j % 2][:, :, (t % 4) * 128:(t % 4 + 1) * 128], ptr[b][:])
            if t % 4 == 3:
                P.dma(A['h2T'][:, j * 512:(j + 1) * 512].rearrange("(a p) n -> p a n", p=128), hst[j % 2][:], 'wohst%d' % (j % 2), q='pool')
            if per_tile is not None:
                per_tile(t)


def phase_wo_ffn(P, A, layer, consts, x_in, x_mid, x_out):
    with ExitStack() as st:
        Wu = P.sb(st, "wu", [128, 8, FF], BF16)
        Wd = P.sb(st, "wd", [128, 32, D], BF16)
        g2 = P.sb(st, "g2", [128, 8], F32)
        P.dma(g2[:], A['g2'][layer], 'ff0')
        with ExitStack() as s2:
            wst = [P.sb(s2, "fwst%d" % i, [128, 1024], F32) for i in range(2)]

            def per_tile(t):
                for u in range(2):
                    ci = 2 * t + u
                    b = u
                    if ci < 32:
                        kc, qt = ci // 4, ci % 4
                        P.dma(wst[b][:], A['wup'][layer, kc * 128:(kc + 1) * 128, qt * 1024:(qt + 1) * 1024], 'fwst%d' % b, q='sp')
                        if u:
                            P.ts('dve', Wu[:, kc, qt * 1024:(qt + 1) * 1024], wst[b][:], g2[:, kc:kc + 1], None, ALU.mult)
                        else:
                            P.act(Wu[:, kc, qt * 1024:(qt + 1) * 1024], wst[b][:], AF.Copy, scale=g2[:, kc:kc + 1])
                    else:
                        fc = ci - 32
                        P.dma(wst[b][:], A['wdn'][layer, fc * 128:(fc + 1) * 128, :], 'fwst%d' % b, q='sp')
                        P.copy('dve' if u else 'act', Wd[:, fc, :], wst[b][:])

            phase_wo(P, A, layer, consts, x_in, x_mid, per_tile=per_tile)
        _ffn_body(P, st, A, Wu, Wd, x_mid, x_out)


def phase_ffn(P, A, layer, consts, x_mid, x_out):
    with ExitStack() as st:
        Wu = P.sb(st, "wu", [128, 8, FF], BF16)
        Wd = P.sb(st, "wd", [128, 32, D], BF16)
        g2 = P.sb(st, "g2", [128, 8], F32)
        P.dma(g2[:], A['g2'][layer], 'ff0')
        with ExitStack() as s2:
            wst = [P.sb(s2, "fwst%d" % i, [128, 2048], F32) for i in range(3)]
            k = 0
            for kc in range(8):
                for hf in range(2):
                    b = k % 3
                    P.dma(wst[b][:], A['wup'][layer, kc * 128:(kc + 1) * 128, hf * 2048:(hf + 1) * 2048], 'fwst%d' % b, q='sp' if k % 2 else 'act')
                    if k % 2:
                        P.ts('dve', Wu[:, kc, hf * 2048:(hf + 1) * 2048], wst[b][:], g2[:, kc:kc + 1], None, ALU.mult)
                    else:
                        P.act(Wu[:, kc, hf * 2048:(hf + 1) * 2048], wst[b][:], AF.Copy, scale=g2[:, kc:kc + 1])
                    k += 1
            for fc2 in range(16):
                b = k % 3
                P.dma(wst[b][:].rearrange("p (a n) -> p a n", a=2), A['wdn'][layer, fc2 * 256:(fc2 + 1) * 256, :].rearrange("(a p) n -> p a n", p=128),
                      'fwst%d' % b, q='sp' if k % 2 else 'act')
                P.copy('dve' if k % 2 else 'act', Wd[:, fc2 * 2:(fc2 + 1) * 2, :], wst[b][:].rearrange("p (a n) -> p a n", a=2))
                k += 1
        _ffn_body(P, st, A, Wu, Wd, x_mid, x_out)


def _ffn_body(P, st, A, Wu, Wd, x_mid, x_out):
    if True:
        h2 = [P.sb(st, "ff_h2%d" % i, [128, 8, 512], BF16) for i in range(2)]
        uT = P.sb(st, "ff_uT", [128, 32, 512], BF16)
        rl = [P.sb(st, "ff_rl%d" % i, [128, 512], F32) for i in range(2)]
        xt = [P.sb(st, "ff_xt%d" % i, [128, D], F32) for i in range(2)]
        xo = [P.sb(st, "ff_xo%d" % i, [128, D], F32) for i in range(2)]
        pu = [P.ps(st, "ff_pu%d" % i, [128, 512], F32) for i in range(3)]
        pd = [P.ps(st, "ff_pd%d" % i, [128, 512], F32) for i in range(4)]
        ku = 0
        for j in range(8):
            P.dma(h2[j % 2][:], A['h2T'][:, j * 512:(j + 1) * 512].rearrange("(a p) n -> p a n", p=128), 'ffh2%d' % (j % 2))
            for fc in range(32):
                pp = pu[ku % 3]
                r = rl[ku % 2]
                for kc in range(8):
                    P.mm(pp[:], Wu[:, kc, fc * 128:(fc + 1) * 128], h2[j % 2][:, kc, :], start=(kc == 0), stop=(kc == 7))
                P.act(r[:], pp[:], AF.Relu)
                P.tt('dve' if ku % 2 else 'pool', uT[:, fc, :], r[:], r[:], ALU.mult)
                ku += 1
            for tt in range(4):
                t = j * 4 + tt
                b = t % 2
                P.dma(xt[b][:], x_mid[t * 128:(t + 1) * 128, :], 'ffxt%d' % b, q='act')
                for half in range(2):
                    pp = pd[(t % 2) * 2 + half]
                    for fc in range(32):
                        P.mm(pp[:], uT[:, fc, tt * 128:(tt + 1) * 128], Wd[:, fc, half * 512:(half + 1) * 512], start=(fc == 0), stop=(fc == 31))
                    P.tt('dve', xo[b][:, half * 512:(half + 1) * 512], pp[:], xt[b][:, half * 512:(half + 1) * 512], ALU.add)
                P.dma(x_out[t * 128:(t + 1) * 128, :], xo[b][:], 'ffxo%d' % b, q='pool')

import ml_dtypes
from concourse.bass_utils import run_bass_kernel_spmd

T=4096; D=1024
OFF = {}
_names = ['aq','af','ai','ag','bq','bkc','bvc','bks','bvs','bkw','bvw','bg','cq','ck','cv','cf']
_sizes = [256,256,256,256,512,128,128,128,128,128,128,24,256,256,256,4]
_o = 0
for n_, s_ in zip(_names, _sizes):
    OFF[n_] = (_o, _o + s_); _o += s_
TOK_ORDER = ['bq','bks','bkw','cq','ck','ai','bvs','bvw','cv']
T_ORDER = ['aq','af','ag','bkc','bvc','bg','cf']

def win_layout(w_in):
    L = w_in.shape[0]
    out = np.zeros((L, 1024, 2048 + 1152), np.float32)
    c = 0
    for n_ in TOK_ORDER:
        a, b = OFF[n_]; out[:, :, c:c + b - a] = w_in[:, :, a:b]; c += b - a
    assert c == 2048
    for n_ in T_ORDER:
        a, b = OFF[n_]; out[:, :, c:c + b - a] = w_in[:, :, a:b]; c += b - a
    return out

def rope_tables():
    inv = np.power(np.float32(500000.0), -np.arange(0, 16, 2, dtype=np.float32) / 16).astype(np.float32)
    pos = np.arange(T, dtype=np.float32)
    ang = pos[:, None] * inv[None, :]
    cos = np.cos(ang).astype(np.float32); sin = np.sin(ang).astype(np.float32)
    return (np.ascontiguousarray(cos.reshape(32, 128, 8).transpose(1, 0, 2)),
            np.ascontiguousarray(sin.reshape(32, 128, 8).transpose(1, 0, 2)))

def _skip():
    pass

def const_inputs():
    k = np.arange(128)[:, None]; q = np.arange(128)[None, :]
    mc = np.where(k <= q, 0.0, -30000.0).astype(ml_dtypes.bfloat16)
    mu = np.where(k > q, 0.0, -30000.0).astype(ml_dtypes.bfloat16)
    selneg = np.zeros((24, 24 * 64), np.float32)
    for c in range(24):
        selneg[c, c * 64:(c + 1) * 64] = -1.0
    return dict(ident=np.eye(128, dtype=ml_dtypes.bfloat16), mc=mc, mu=mu, selneg=selneg)

def _unused_ref_proj(inp, layer, x):
    x = x.astype(np.float64)
    h = x / np.sqrt((x * x).mean(-1, keepdims=True) + 1e-6) * inp['norm1_g'][layer]
    return h @ inp['w_in'][layer].astype(np.float64)

def hgrn_consts(inp):
    s = np.arange(128)[:, None]; t = np.arange(128)[None, :]
    mh = ((s // 64 == t // 64) & (s <= t)).astype(ml_dtypes.bfloat16)
    bones = (s // 64 == t // 64).astype(ml_dtypes.bfloat16)
    lbl = np.ascontiguousarray(inp['hgrn_lb_logits'].reshape(2, 2, 128).transpose(1, 2, 0)).astype(np.float32)
    og = np.tile(inp['hgrn_onorm_g'], (1, 2)).reshape(2, 128, 1).astype(np.float32)
    return dict(mh=mh, bones=bones, lbl=lbl, og=og)

def _unused_ref_hgrn(inp, layer, proj):
    def sl(n): a, b = OFF[n]; return proj[:, a:b]
    lbp = np.exp(inp['hgrn_lb_logits'].astype(np.float64)); lbp /= lbp.sum(0, keepdims=True)
    lb_all = np.cumsum(lbp, 0) - lbp[0:1]
    lb = lb_all[layer].reshape(4, 64)
    z = sl('af').reshape(T, 4, 64)
    sig = 1 / (1 + np.exp(-z))
    f = lb + (1 - lb) * sig; logf = np.log(f); k = (1 - lb) * (1 - sig)
    q = sl('aq').reshape(T, 4, 64) * 0.125; v = sl('ai').reshape(T, 4, 64)
    o = np.zeros((T, 4, 64))
    for h in range(4):
        S = np.zeros((64, 64))
        for c in range(64):
            r = slice(c * 64, (c + 1) * 64)
            G = np.cumsum(logf[r, h], 0)
            qc, kc, vc = q[r, h], k[r, h], v[r, h]
            o_inter = (qc * np.exp(G)) @ S
            diff = G[:, None, :] - G[None, :, :]
            mask = np.tril(np.ones((64, 64), bool))
            dec = np.where(mask[:, :, None], np.exp(np.minimum(diff, 0)), 0)
            sc = np.einsum('tk,sk,tsk->ts', qc, kc, dec)
            o[r, h] = o_inter + sc @ vc
            S = S * np.exp(G[-1])[:, None] + (kc * np.exp(G[-1] - G)).T @ vc
    g = sl('ag').reshape(T, 4, 64)
    gate = g / (1 + np.exp(-g))
    on = o / np.sqrt((o * o).mean(-1, keepdims=True) + 1e-6) * inp['hgrn_onorm_g'][layer]
    return (on * gate).reshape(T, 256)

def nsa_consts(inp):
    n_cmp = 255
    ci = np.arange(n_cmp)[:, None]; sj = np.arange(64)[None, :]
    ov = ((ci * 16 <= sj * 64 + 63) & (ci * 16 + 31 >= sj * 64)).astype(np.float32)
    ovaug = np.zeros((256, 72), np.float32); ovaug[:255, :64] = ov; ovaug[:255, 64] = 1.0
    ovaug = np.ascontiguousarray(ovaug.reshape(2, 128, 72).transpose(1, 0, 2)).astype(ml_dtypes.bfloat16)
    nl = np.arange(128)[:, None]; cc = np.arange(3200)[None, :] - 511
    wc = np.where(cc >= 16 * nl, 0.0, -30000.0).astype(ml_dtypes.bfloat16)
    eall = (np.arange(T)[None, :] // 64 == np.arange(64)[:, None]).astype(ml_dtypes.bfloat16)
    q = np.arange(T)[:, None]; j = np.arange(64)[None, :]; cur = q // 64
    am = np.zeros((T, 64), np.float32)
    am[(j == 0) | (j == cur) | (j == cur - 1)] = 1e30
    am[np.broadcast_to(j > cur, am.shape)] = -1e30
    addmask = np.ascontiguousarray(am.reshape(32, 128, 64).transpose(1, 0, 2))
    inv = np.power(np.float32(500000.0), -np.arange(0, 16, 2, dtype=np.float32) / 16).astype(np.float32)
    pos = (np.arange(256, dtype=np.float32) * 16 + 31)
    ang = pos[:, None] * inv[None, :]
    cosc = np.ascontiguousarray(np.cos(ang).astype(np.float32).reshape(2, 128, 8).transpose(1, 0, 2))
    sinc = np.ascontiguousarray(np.sin(ang).astype(np.float32).reshape(2, 128, 8).transpose(1, 0, 2))
    w1r = np.ascontiguousarray(inp['nsa_cmp_w1'].reshape(2, 2, 32, 64, 128).transpose(0, 1, 3, 2, 4)).astype(np.float32)
    posT = np.ascontiguousarray(inp['nsa_cmp_pos'].transpose(0, 1, 3, 2)).astype(np.float32)
    return dict(ovaug=ovaug, wc=wc, eall=eall, addmask=addmask, cosc=cosc, sinc=sinc, w1r=w1r, posT=posT,
                w2=inp['nsa_cmp_w2'].astype(np.float32), kng=inp['nsa_kn_g'].astype(np.float32))

NSA_SHAPES = [('ovaug', [128, 2, 72], BF16), ('wc', [128, 3200], BF16), ('eall', [64, 4096], BF16), ('addmask', [128, 32, 64], F32),
              ('cosc', [128, 2, 8], F32), ('sinc', [128, 2, 8], F32), ('w1r', [2, 2, 64, 32, 128], F32), ('posT', [2, 2, 64, 32], F32),
              ('w2', [2, 2, 128, 64], F32), ('kng', [2, 64], F32)]


import ml_dtypes
from concourse.bass_utils import run_bass_kernel_spmd

_IN_SHAPES = [('x', [T, D], F32), ('win', [2, D, WCOLS], F32), ('g1', [2, 128, 8], F32), ('gq', [2, 1280], F32),
              ('cos', [128, 32, 8], F32), ('sin', [128, 32, 8], F32), ('ident', [128, 128], BF16), ('mc', [128, 128], BF16),
              ('mu', [128, 128], BF16), ('selneg', [24, 1536], F32), ('mh', [128, 128], BF16), ('bones', [128, 128], BF16),
              ('lbl', [2, 128, 2], F32), ('og', [2, 128, 1], F32), ('fb', [2, 4, 1], F32), ('wo', [2, 1024, 1024], F32),
              ('wup', [2, 1024, 4096], F32), ('wdn', [2, 4096, 1024], F32), ('g2', [2, 128, 8], F32)] + NSA_SHAPES

KDEPTH = int(os.environ.get('KDEPTH', '2'))
KPHASES = os.environ.get('KPHASES', '1hnfwf')


def _body(P):
    nc = P.nc
    A = {}
    for k_, shp, dt_ in _IN_SHAPES:
        A[k_] = nc.dram_tensor(k_, shp, dt_, kind="ExternalInput").ap()
    A['y'] = nc.dram_tensor("y", [T, D], F32, kind="ExternalOutput").ap()
    A['qkT'] = nc.dram_tensor("qkT", [1280, T], BF16).ap()
    A['vtok'] = nc.dram_tensor("vtok", [T, 768], BF16).ap()
    A['pT'] = nc.dram_tensor("pT", [TC, T], F32).ap()
    A['mixT'] = nc.dram_tensor("mixT", [1024, T], BF16).ap()
    A['h2T'] = nc.dram_tensor("h2T", [1024, T], BF16).ap()
    A['gsig'] = nc.dram_tensor("gsig", [24, T], F32).ap()
    xm = nc.dram_tensor("xmid", [T, D], F32).ap()
    x1 = nc.dram_tensor("x1", [T, D], F32).ap()
    xin = A['x']
    for layer in range(KDEPTH):
        A['x'] = xin
        with ExitStack() as st:
            phase1(P, st, A, layer)
        with ExitStack() as st:
            consts = load_consts(P, st, A)
            if 'h' in KPHASES:
                phase_hgrn(P, A, layer, consts)
            if 'n' in KPHASES:
                phase_nsa(P, A, layer, consts)
            if 'f' in KPHASES:
                phase_fox(P, A, layer, consts)
            xout = x1 if layer < KDEPTH - 1 else A['y']
            phase_wo_ffn(P, A, layer, consts, xin, xm, xout)
        xin = xout


def _host_inputs(inp):
    cos, sin = rope_tables()
    gq = np.concatenate([np.tile(inp['nsa_qn_g'], (1, 8)), np.tile(inp['nsa_kn_g'], (1, 4)), np.tile(inp['fox_qn_g'], (1, 4)),
                         np.tile(inp['fox_kn_g'], (1, 4))], axis=1).astype(np.float32)
    base = {"win": win_layout(inp['w_in']), "g1": np.ascontiguousarray(inp['norm1_g'].reshape(2, 8, 128).transpose(0, 2, 1)),
            "g2": np.ascontiguousarray(inp['norm2_g'].reshape(2, 8, 128).transpose(0, 2, 1)),
            "gq": gq, "cos": cos, "sin": sin, "fb": inp['fox_fb'].reshape(2, 4, 1).astype(np.float32),
            "wo": inp['w_o'], "wup": inp['w_up'], "wdn": inp['w_down']}
    base.update(const_inputs()); base.update(hgrn_consts(inp)); base.update(nsa_consts(inp))
    return base


def kernel(**inp):
    inp = {k: np.asarray(v) for k, v in inp.items()}
    nc, plan = build_two_pass(lambda: bass.Bass("TRN2", target_bir_lowering=False), _body)
    base = _host_inputs(inp)
    in_maps = []
    for b in range(8):
        m = dict(base); m['x'] = np.ascontiguousarray(inp['x'][b]); in_maps.append(m)
    res = run_bass_kernel_spmd(nc, in_maps, core_ids=list(range(8)))
    return np.stack([r['y'] for r in res.results], axis=0).astype(np.float32)
```

```python
import numpy as np, sys, time, os, math
import numpy as np
from contextlib import ExitStack
import concourse.bass as bass
import concourse.mybir as mybir

F32 = mybir.dt.float32
BF16 = mybir.dt.bfloat16
AF = mybir.ActivationFunctionType
ALU = mybir.AluOpType
AX = mybir.AxisListType


def _box(ap):
    t = ap.tensor
    dims = ap.ap
    off = int(ap.offset)
    shp = tuple(t.shape)
    rowsize = 1
    for s in shp[1:]:
        rowsize *= int(s)
    r0 = off // rowsize
    f0 = off % rowsize
    rows = 0
    free = 0
    for (st, cnt) in dims:
        st = int(st); cnt = int(cnt)
        if cnt <= 1 or st == 0:
            continue
        if st % rowsize == 0:
            rows += (st // rowsize) * (cnt - 1)
        else:
            free += st * (cnt - 1)
    return t.name, (r0, r0 + rows, f0, f0 + free)


def _ov(a, b):
    return a[0] <= b[1] and b[0] <= a[1] and a[2] <= b[3] and b[2] <= a[3]


def _cont(a, b):
    return a[0] <= b[0] and b[1] <= a[1] and a[2] <= b[2] and b[3] <= a[3]


class Prog:
    def __init__(self, nc, plan=None):
        self.nc = nc
        self.plan = plan
        self.rec = plan is None
        self.eng = dict(pe=nc.tensor, dve=nc.vector, act=nc.scalar, pool=nc.gpsimd, sp=nc.sync)
        self.n = 0
        self.ins = []
        self.track = {}
        self.lane_cnt = {}
        self.freed = {}
        self.uid = 0
        self.stack = ExitStack()
        self.psum_rr = 0
        self.psum_banks = []
        if not self.rec:
            self.sem = {}
            for e in ['pe', 'dve', 'act', 'pool']:
                self.sem[e] = self.stack.enter_context(nc.semaphore("sem_" + e))
            self.lane_sem = {}
            for ln in plan['lanes']:
                self.lane_sem[ln] = self.stack.enter_context(nc.semaphore("ln_" + ln))

    def sb(self, st, name, shape, dtype):
        self.uid += 1
        name = "%s_%d" % (name, self.uid)
        t = st.enter_context(self.nc.sbuf_tensor("s_" + name, list(shape), dtype))
        st.callback(self._free, "s_" + name)
        return t

    def ps(self, st, name, shape, dtype=F32):
        self.uid += 1
        name = "%s_%d" % (name, self.uid)
        t = st.enter_context(self.nc.psum_tensor("p_" + name, list(shape), dtype))
        st.callback(self._free, "p_" + name)
        return t

    def _free(self, name):
        if not self.rec:
            return
        recs = self.track.pop(name, [])
        for (b, i, w) in recs:
            r = self.ins[i]
            key = ('l', r['lane'], i) if r['dma'] else ('e', r['eng'])
            if r['dma']:
                self.freed[key] = i
            else:
                self.freed[key] = max(self.freed.get(key, -1), i)

    def _access(self, idx, eng, dma, ap, write, deps):
        name, box = _box(ap)
        if name not in self.track:
            big = (0, 10 ** 9, 0, 10 ** 9)
            kind = ap.space
            self.track[name] = [] if str(kind) == 'DRAM' else [(big, i, True) for i in sorted(set(self.freed.values()))]
        recs = self.track[name]
        for (b, i, w) in recs:
            if (write or w) and _ov(b, box):
                deps.append((i, (w and not write)))
        if write:
            recs[:] = [r for r in recs if not _cont(box, r[0])]
        elif not dma:
            recs[:] = [r for r in recs if not ((not r[2]) and r[1] < len(self.ins) and self.ins[r[1]]['eng'] == eng
                                               and not self.ins[r[1]]['dma'] and _cont(box, r[0]))]
        recs.append((box, idx, write))

    def op(self, eng, fn, reads=(), writes=(), dma=False, lane=None):
        idx = self.n
        self.n += 1
        if self.rec:
            deps = []
            for ap in reads:
                self._access(idx, eng, dma, ap, False, deps)
            for ap in writes:
                self._access(idx, eng, dma, ap, True, deps)
            lanewaits = {}
            d2 = {}
            for (j, raw) in deps:
                if j == idx:
                    continue
                pj = self.ins[j]
                if pj['dma']:
                    ln = pj['lane']
                    lanewaits[ln] = max(lanewaits.get(ln, 0), pj['lane_val_at'])
                    lanewaits[ln] = max(lanewaits[ln], self.lane_cnt[ln])
                    continue
                if pj['eng'] == eng and not dma:
                    if eng == 'pe':
                        continue
                    if not raw and eng != 'pool':
                        continue
                d2[j] = True
            rec = dict(eng=eng, deps=list(d2.keys()), lanewaits=lanewaits, dma=dma, lane=lane)
            if dma:
                self.lane_cnt[lane] = self.lane_cnt.get(lane, 0) + 16
                rec['lane_val_at'] = self.lane_cnt[lane]
            self.ins.append(rec)
            return None
        else:
            info = self.plan['ins'][idx]
            e = self.eng[eng]
            for (sname, val) in info['waits']:
                s = self.sem[sname[1]] if sname[0] == 'e' else self.lane_sem[sname[1]]
                e.wait_ge(s, val)
            inst = fn(e)
            if dma:
                inst.then_inc(self.lane_sem[lane], 16)
            elif info['signal']:
                inst.then_inc(self.sem[eng], 1)
            return inst

    def make_plan(self):
        ins = self.ins
        signal = [False] * len(ins)
        for r in ins:
            for j in r['deps']:
                signal[j] = True
        cnt = dict(pe=0, dve=0, act=0, pool=0, sp=0)
        sigval = [0] * len(ins)
        for i, r in enumerate(ins):
            if signal[i] and not r['dma']:
                cnt[r['eng']] += 1
                sigval[i] = cnt[r['eng']]
        seen = {e: {} for e in cnt}
        out = []
        for i, r in enumerate(ins):
            need = {}
            for j in r['deps']:
                k = ('e', ins[j]['eng'])
                need[k] = max(need.get(k, 0), sigval[j])
            for ln, v in r['lanewaits'].items():
                k = ('l', ln)
                need[k] = max(need.get(k, 0), v)
            waits = []
            sd = seen[r['eng']]
            for k, v in need.items():
                if sd.get(k, 0) >= v:
                    continue
                sd[k] = v
                waits.append((k, v))
            out.append(dict(waits=waits, signal=signal[i]))
        return dict(ins=out, lanes=sorted(self.lane_cnt.keys()), lane_final=dict(self.lane_cnt))

    def finish(self):
        if self.rec:
            return
        for ln, v in self.plan['lane_final'].items():
            self.nc.sync.wait_ge(self.lane_sem[ln], v)

    def dma(self, out, in_, lane, q='sp', **kw):
        return self.op(q, lambda e: e.dma_start(out=out, in_=in_, **kw), reads=[in_], writes=[out],
                       dma=True, lane=lane)

    def mm(self, out, lhsT, rhs, start=True, stop=True, **kw):
        return self.op('pe', lambda e: e.matmul(out, lhsT, rhs, start=start, stop=stop, **kw),
                       reads=[lhsT, rhs], writes=[out])

    def transpose(self, out, in_, ident):
        return self.op('pe', lambda e: e.transpose(out, in_, ident), reads=[in_, ident], writes=[out])

    def act(self, out, in_, func, bias=None, scale=None, accum_out=None, eng='act'):
        reads = [in_]
        kw = {}
        if bias is not None:
            kw['bias'] = bias
            if not isinstance(bias, (int, float)):
                reads.append(bias)
        if scale is not None:
            kw['scale'] = scale
            if not isinstance(scale, (int, float)):
                reads.append(scale)
        writes = [out]
        if accum_out is not None:
            kw['accum_out'] = accum_out
            writes.append(accum_out)
        return self.op(eng, lambda e: e.activation(out=out, in_=in_, func=func, **kw), reads=reads, writes=writes)

    def tt(self, eng, out, in0, in1, op):
        return self.op(eng, lambda e: e.tensor_tensor(out=out, in0=in0, in1=in1, op=op), reads=[in0, in1], writes=[out])

    def ts(self, eng, out, in0, s1, s2, op0, op1=None, accum_out=None):
        reads = [in0]
        if not isinstance(s1, (int, float)):
            reads.append(s1)
        if s2 is not None and not isinstance(s2, (int, float)):
            reads.append(s2)
        kw = {}
        writes = [out]
        if op1 is not None:
            kw['op1'] = op1
        if accum_out is not None:
            kw['accum_out'] = accum_out
            writes.append(accum_out)
        return self.op(eng, lambda e: e.tensor_scalar(out=out, in0=in0, scalar1=s1, scalar2=s2, op0=op0, **kw),
                       reads=reads, writes=writes)

    def stt(self, eng, out, in0, scalar, in1, op0, op1):
        reads = [in0, in1]
        if not isinstance(scalar, (int, float)):
            reads.append(scalar)
        return self.op(eng, lambda e: e.scalar_tensor_tensor(out=out, in0=in0, scalar=scalar, in1=in1, op0=op0, op1=op1),
                       reads=reads, writes=[out])

    def copy(self, eng, out, in_):
        if eng == 'act':
            return self.op(eng, lambda e: e.copy(out=out, in_=in_), reads=[in_], writes=[out])
        return self.op(eng, lambda e: e.tensor_copy(out=out, in_=in_), reads=[in_], writes=[out])

    def memset(self, eng, ap, val):
        return self.op(eng, lambda e: e.memset(ap, val), reads=[], writes=[ap])

    def scan(self, out, d0, d1, initial, op0, op1):
        reads = [d0, d1]
        if not isinstance(initial, (int, float)):
            reads.append(initial)
        return self.op('dve', lambda e: e.tensor_tensor_scan(out=out, data0=d0, data1=d1, initial=initial, op0=op0, op1=op1),
                       reads=reads, writes=[out])

    def generic(self, eng, fn, reads, writes):
        return self.op(eng, fn, reads=reads, writes=writes)


def build_two_pass(make_nc, body):
    nc1 = make_nc()
    p1 = Prog(nc1, None)
    body(p1)
    p1.stack.close()
    plan = p1.make_plan()
    nc2 = make_nc()
    p2 = Prog(nc2, plan)
    body(p2)
    p2.finish()
    p2.stack.close()
    return nc2, plan


T = 4096
NT = 32
D = 1024
KC = 8
TOKC = 2048
TC = 1152
WCOLS = TOKC + TC
EPS = 1e-6


def phase1(P, st, A, layer):
    nc = P.nc
    s = ExitStack()
    W = P.sb(s, "w_in", [128, KC, WCOLS], BF16)
    hT = P.sb(s, "hT", [128, KC, T], BF16)
    ident = P.sb(s, "ident", [128, 128], BF16)
    g1 = P.sb(s, "g1", [128, KC], F32)
    G = P.sb(s, "Gq", [128, 1280], F32)
    cos = P.sb(s, "cos", [128, NT, 8], F32)
    sin = P.sb(s, "sin", [128, NT, 8], F32)
    P.dma(ident[:], A['ident'], 'c0')
    P.dma(g1[:], A['g1'][layer], 'c0')
    P.dma(G[:], A['gq'][layer].partition_broadcast(128), 'c0')
    P.dma(cos[:], A['cos'], 'c0')
    P.dma(sin[:], A['sin'], 'c0')
    P.ts('dve', G[:, 0:512], G[:, 0:512], 0.125, None, ALU.mult)
    P.ts('dve', G[:, 768:1024], G[:, 768:1024], 0.125, None, ALU.mult)

    with ExitStack() as s2:
        wst = [P.sb(s2, "wst%d" % i, [128, WCOLS], F32) for i in range(4)]
        for kc in range(KC):
            b = wst[kc % 4]
            P.dma(b[:], A['win'][layer, kc * 128:(kc + 1) * 128, :], 'wst%d' % (kc % 4), q='sp' if kc % 2 == 0 else 'act')
            half = WCOLS // 2
            P.ts('dve', W[:, kc, 0:half], b[:, 0:half], g1[:, kc:kc + 1], None, ALU.mult)
            P.act(W[:, kc, half:WCOLS], b[:, half:WCOLS], AF.Copy, scale=g1[:, kc:kc + 1])

    with ExitStack() as s2:
        xt = [P.sb(s2, "xt%d" % i, [128, D], F32) for i in range(2)]
        sq = P.sb(s2, "sqj", [128, D], F32)
        hb = [P.sb(s2, "hb%d" % i, [128, D], BF16) for i in range(2)]
        ss = [P.sb(s2, "ss%d" % i, [128, 2], F32) for i in range(2)]
        ptr = [P.ps(s2, "ptr%d" % i, [128, KC, 128], BF16) for i in range(2)]
        for t in range(NT):
            b = t % 2
            P.dma(xt[b][:], A['x'][t * 128:(t + 1) * 128, :], 'xt%d' % b)
            P.act(sq[:], xt[b][:], AF.Square, accum_out=ss[b][:, 0:1])
            P.act(ss[b][:, 1:2], ss[b][:, 0:1], AF.Sqrt, bias=EPS_AP(P), scale=1.0 / D)
            P.op('dve', lambda e, o=ss[b][:, 1:2]: e.reciprocal(out=o, in_=o), reads=[ss[b][:, 1:2]], writes=[ss[b][:, 1:2]])
            P.ts('dve', hb[b][:], xt[b][:], ss[b][:, 1:2], None, ALU.mult)
            for kc in range(KC):
                P.transpose(ptr[b][:, kc, :], hb[b][:, kc * 128:(kc + 1) * 128], ident[:])
            P.copy('act' if t % 2 else 'dve', hT[:, :, t * 128:(t + 1) * 128], ptr[b][:])

    with ExitStack() as s2:
        pp = [P.ps(s2, "ppT%d" % i, [128, 512], F32) for i in range(3)]
        so = [P.sb(s2, "soT%d" % i, [128, 512], F32) for i in range(3)]
        k = 0
        for c in range(TC // 128):
            for j in range(T // 512):
                b = k % 3
                for kc in range(KC):
                    P.mm(pp[b][:], W[:, kc, TOKC + c * 128:TOKC + (c + 1) * 128], hT[:, kc, j * 512:(j + 1) * 512],
                         start=(kc == 0), stop=(kc == KC - 1))
                P.copy('act' if k % 2 else 'dve', so[b][:], pp[b][:])
                P.dma(A['pT'][c * 128:(c + 1) * 128, j * 512:(j + 1) * 512], so[b][:], 'soT%d' % b, q='pool')
                k += 1

    with ExitStack() as s2:
        pg = [P.ps(s2, "pg%d" % i, [128, 512], F32) for i in range(4)]
        ptq = [P.ps(s2, "ptq%d" % i, [128, 4, 128], BF16) for i in range(3)]
        sqhs = [P.sb(s2, "sqh%d" % i, [128, 512], F32) for i in range(3)]
        ssh = [P.sb(s2, "ssh%d" % i, [128, 8], F32) for i in range(4)]
        xn = [P.sb(s2, "xn%d" % i, [128, 512], F32) for i in range(3)]
        qb = [P.sb(s2, "qb%d" % i, [128, 512], BF16) for i in range(3)]
        rts = [P.sb(s2, "rt%d" % i, [128, 4, 8, 8], F32) for i in range(3)]
        qst = [P.sb(s2, "qst%d" % i, [128, 10, 512], BF16) for i in range(2)]
        vst = [P.sb(s2, "vst%d" % i, [128, 768], BF16) for i in range(2)]
        groups = []
        kq = 0
        for t in range(NT):
            for gi in range(4):
                k = t * 4 + gi
                nh = [8, 8, 4, 0][gi]
                qi = None
                if nh:
                    qi = kq % 3
                    kq += 1
                groups.append((t, gi, k, qi))

        def stage(sidx, t, gi, k, q):
            sb_ = (t // 4) % 2
            b = k % 4
            nh = [8, 8, 4, 0][gi]
            nr = [8, 4, 0, 0][gi]
            w = nh * 64
            goff = [0, 512, 1024, 0][gi]
            vb = t % 2
            if sidx == 0:
                for kc in range(KC):
                    P.mm(pg[b][:], hT[:, kc, t * 128:(t + 1) * 128], W[:, kc, gi * 512:(gi + 1) * 512],
                         start=(kc == 0), stop=(kc == KC - 1))
                return
            if sidx == 1:
                if nh:
                    sqh = sqhs[q]
                    P.act(sqh[:, 0:w], pg[b][:, 0:w], AF.Square)
                    P.op('dve', lambda e, o=ssh[b][:, 0:nh], i=sqh[:, 0:w].rearrange("p (h d) -> p h d", d=64):
                         e.tensor_reduce(out=o, in_=i, axis=AX.X, op=ALU.add),
                         reads=[sqh[:, 0:w]], writes=[ssh[b][:, 0:nh]])
                if gi == 2:
                    P.copy('act', vst[vb][:, 0:256], pg[b][:, 256:512])
                if gi == 3:
                    P.copy('act', vst[vb][:, 256:768], pg[b][:, 0:512])
                    P.dma(A['vtok'][t * 128:(t + 1) * 128, :], vst[vb][:], 'vst%d' % vb, q='sp')
                return
            if not nh:
                return
            rt = rts[q]
            xv = xn[q][:, 0:max(nr, 1) * 64].rearrange("p (h d) -> p h d", d=64)
            qv = qb[q][:, 0:max(nr, 1) * 64].rearrange("p (h d) -> p h d", d=64)
            if sidx == 2:
                P.act(ssh[b][:, 0:nh], ssh[b][:, 0:nh], AF.Sqrt, bias=EPS_AP(P), scale=1.0 / 64)
                P.op('dve', lambda e, o=ssh[b][:, 0:nh]: e.reciprocal(out=o, in_=o), reads=[ssh[b][:, 0:nh]], writes=[ssh[b][:, 0:nh]])
                P.tt('dve', xn[q][:, 0:w].rearrange("p (h d) -> p h d", d=64),
                     pg[b][:, 0:w].rearrange("p (h d) -> p h d", d=64),
                     ssh[b][:, 0:nh].unsqueeze(2).broadcast_to([128, nh, 64]), ALU.mult)
            elif sidx == 3:
                if nr:
                    P.tt('pool', xn[q][:, 0:w], xn[q][:, 0:w], G[:, goff:goff + w], ALU.mult)
                    P.copy('act', qb[q][:, 0:w], xn[q][:, 0:w])
                    cb = cos[:, t, :].unsqueeze(1).broadcast_to([128, nr, 8])
                    sb2 = sin[:, t, :].unsqueeze(1).broadcast_to([128, nr, 8])
                    P.tt('dve', rt[:, 0, 0:nr, :], xv[:, :, 0:8], cb, ALU.mult)
                    P.tt('dve', rt[:, 1, 0:nr, :], xv[:, :, 8:16], sb2, ALU.mult)
                    P.tt('pool', rt[:, 2, 0:nr, :], xv[:, :, 8:16], cb, ALU.mult)
                    P.tt('pool', rt[:, 3, 0:nr, :], xv[:, :, 0:8], sb2, ALU.mult)
                else:
                    P.tt('pool', qb[q][:, 0:w], xn[q][:, 0:w], G[:, goff:goff + w], ALU.mult)
            elif sidx == 4:
                if nr:
                    P.tt('dve', qv[:, :, 0:8], rt[:, 0, 0:nr, :], rt[:, 1, 0:nr, :], ALU.subtract)
                    P.tt('pool', qv[:, :, 8:16], rt[:, 2, 0:nr, :], rt[:, 3, 0:nr, :], ALU.add)
                npair = nh // 2
                for pr in range(npair):
                    P.transpose(ptq[q][:, pr, :], qb[q][:, pr * 128:(pr + 1) * 128], ident[:])
            elif sidx == 5:
                npair = nh // 2
                pbase = [0, 4, 8][gi]
                P.copy('act' if gi % 2 else 'dve', qst[sb_][:, pbase:pbase + npair, (t % 4) * 128:(t % 4 + 1) * 128], ptq[q][:, 0:npair, :])
                if t % 4 == 3 and gi == 2:
                    j = t // 4
                    P.dma(A['qkT'][:, j * 512:(j + 1) * 512].rearrange("(a p) n -> p a n", p=128), qst[sb_][:], 'qst%d' % sb_, q='sp')

        NS = 6
        for step in range(len(groups) + NS - 1):
            for sidx in range(NS - 1, -1, -1):
                gidx = step - sidx
                if 0 <= gidx < len(groups):
                    stage(sidx, *groups[gidx])
    s.close()


_eps_cache = {}


def EPS_AP(P):
    return EPS


T = 4096
NEG = -30000.0


class AttnCtx:
    def __init__(self, P, st, consts):
        self.P = P
        self.psS = [P.ps(st, "aS%d" % i, [128, 1024], F32) for i in range(2)]
        self.psO = [P.ps(st, "aO%d" % i, [128, 512], F32) for i in range(2)]
        self.psB = [P.ps(st, "aB%d" % i, [128, 512], F32) for i in range(1)]
        self.pT = [P.sb(st, "apT%d" % i, [128, 1024], BF16) for i in range(3)]
        self.lr = [P.sb(st, "alr%d" % i, [65, 512], F32) for i in range(2)]
        self.F = [P.sb(st, "aF%d" % i, [64, 512], F32) for i in range(2)]
        self.lrh = [P.sb(st, "alrh%d" % i, [128, 512], BF16) for i in range(2)]
        self.lrl = [P.sb(st, "alrl%d" % i, [128, 512], BF16) for i in range(2)]
        self.G2 = [P.sb(st, "aG%d" % i, [64, 512], F32) for i in range(2)]
        for t_ in self.lrh + self.lrl:
            P.memset('pool', t_[:], 0.0)
        self.kF_ids = {id(g): i for i, g in enumerate(self.G2)}
        self.kS = 0
        self.kO = 0
        self.kF = 0
        self.vm = 65
        self.prev = None
        self.deferred = []
        self.c = consts


def _push_block(cx, s_fn, exp_fn, pv_fn, first=False):
    if first:
        for f in cx.deferred:
            f()
        cx.deferred = []
    s_fn()
    d = cx.deferred
    cx.deferred = []
    if cx.prev is not None:
        e, p, epi = cx.prev
        e()
        p()
        if epi is not None:
            epi[0]()
            cx.deferred.append(epi[1])
    for f in d:
        f()
    cx.prev = (exp_fn, pv_fn, None)


def _end_chunk(cx, epi_a, epi_b):
    cx.prev = (cx.prev[0], cx.prev[1], (epi_a, epi_b))


def attn_flush(cx):
    d = cx.deferred
    cx.deferred = []
    if cx.prev is not None:
        e, p, epi = cx.prev
        e()
        p()
        if epi is not None:
            epi[0]()
            d.append(epi[1])
        cx.prev = None
    for f in d:
        f()


def _mk_block(cx, po, Kaug, kr, Qaug, q0, Vaug, kt, lo, hi, masks, extra, first, last):
    P = cx.P
    c = cx.c
    ps = cx.psS[cx.kS % 2]
    pt = cx.pT[cx.kS % 3]
    cx.kS += 1

    def s_fn():
        P.mm(ps[:, lo:hi], Kaug[0:kr, kt * 128:(kt + 1) * 128], Qaug[0:kr, q0 + lo:q0 + hi], start=True, stop=(len(masks) == 0 and extra is None))
        if extra is not None:
            P.mm(ps[:, lo:hi], extra[0][0:64, kt * 128:(kt + 1) * 128], extra[1][0:64, q0 + lo:q0 + hi], start=False, stop=(len(masks) == 0))
        for mi, (mk, m) in enumerate(masks):
            P.mm(ps[:, m * 128:(m + 1) * 128], c['ident'][:], mk[:], start=False, stop=(mi == len(masks) - 1))

    def exp_fn():
        P.act(pt[:, lo:hi], ps[:, lo:hi], AF.Exp)

    def pv_fn():
        P.mm(po[0:cx.vm, lo:hi], Vaug[:, kt, 0:cx.vm], pt[:, lo:hi], start=first, stop=last)

    return s_fn, exp_fn, pv_fn


def _mk_pair(cx, po, Kaug, kr, Qaug, q0, Vaug, ea, eb, first, last):
    P = cx.P
    c = cx.c
    ps = cx.psS[cx.kS % 2]
    pt = cx.pT[cx.kS % 3]
    cx.kS += 1
    (kta, loa, hia, ma), (ktb, lob, hib, mb) = ea, eb

    def s_fn():
        for o, (kt, lo, hi, masks) in ((0, ea), (512, eb)):
            P.mm(ps[:, o + lo:o + hi], Kaug[0:kr, kt * 128:(kt + 1) * 128], Qaug[0:kr, q0 + lo:q0 + hi], start=True, stop=(len(masks) == 0))
            for mi, (mk, m) in enumerate(masks):
                P.mm(ps[:, o + m * 128:o + (m + 1) * 128], c['ident'][:], mk[:], start=False, stop=(mi == len(masks) - 1))

    def exp_fn():
        P.act(pt[:, loa:512 + hib], ps[:, loa:512 + hib], AF.Exp)

    def pv_fn():
        P.mm(po[0:cx.vm, loa:hia], Vaug[:, kta, 0:cx.vm], pt[:, loa:hia], start=first, stop=False)
        P.mm(po[0:cx.vm, lob:hib], Vaug[:, ktb, 0:cx.vm], pt[:, 512 + lob:512 + hib], start=False, stop=last)

    return s_fn, exp_fn, pv_fn


def _mk_factor(cx, po, lng2, gate_c, q0, finish):
    P = cx.P
    c = cx.c
    lr = cx.lr[cx.kF % 2]
    F = cx.F[cx.kF % 2]
    G2 = cx.G2[cx.kF % 2]
    cx.kF += 1
    pb = cx.psB[0]

    lrh = cx.lrh[(cx.kF - 1) % 2]
    lrl = cx.lrl[(cx.kF - 1) % 2]

    def epi_a():
        if lng2 is not None:
            P.dma(G2[:], lng2[gate_c, q0:q0 + 512].partition_broadcast(64), 'ag%d' % ((cx.kF_ids[id(G2)])), q='sp')
        P.ts('dve', lr[64:65, :], po[64:65, :], 1e-18, None, ALU.max)
        P.act(lr[64:65, :], lr[64:65, :], AF.Ln)
        P.copy('dve', lrh[64:65, :], lr[64:65, :])
        P.tt('dve', lrl[64:65, :], lr[64:65, :], lrh[64:65, :], ALU.subtract)

    def epi_b():
        P.mm(pb[:, :], c['negonesb'][:, :], lrh[:, :], start=True, stop=False)
        P.mm(pb[:, :], c['negonesb'][:, :], lrl[:, :], start=False, stop=True)
        P.act(F[:], pb[0:64, :], AF.Exp)
        if lng2 is not None:
            P.tt('pool', F[:], F[:], G2[:], ALU.mult)
        finish(po, F)

    return epi_a, epi_b


def attn_chunk(cx, Kaug, kr, Qaug, j, Vaug, entries, finish, lng2=None, gate_c=None, extra=None):
    po = cx.psO[cx.kO % 2]
    cx.kO += 1
    q0 = j * 512
    n = len(entries)
    ei = 0
    while ei < n:
        if extra is None and ei + 1 < n:
            fns = _mk_pair(cx, po, Kaug, kr, Qaug, q0, Vaug, entries[ei], entries[ei + 1], ei == 0, ei + 1 == n - 1)
            _push_block(cx, *fns, first=(ei == 0))
            ei += 2
            continue
        kt, lo, hi, masks = entries[ei]
        fns = _mk_block(cx, po, Kaug, kr, Qaug, q0, Vaug, kt, lo, hi, masks, extra, ei == 0, ei == n - 1)
        _push_block(cx, *fns, first=(ei == 0))
        ei += 1
    ea, eb = _mk_factor(cx, po, lng2, gate_c, q0, finish)
    _end_chunk(cx, ea, eb)


def causal_entries(j, mc):
    ent = []
    for kt in range(4 * j + 4):
        if kt < 4 * j:
            ent.append((kt, 0, 512, []))
        else:
            m = kt - 4 * j
            ent.append((kt, 128 * m, 512, [(mc, m)]))
    return ent


def window_entries(j, mc, mu):
    ent = []
    for cc in range(-4, 4):
        kt = 4 * j + cc
        if kt < 0:
            continue
        lo = 128 * max(cc, 0)
        hi = 128 * (min(cc + 4, 3) + 1)
        masks = []
        if 0 <= cc <= 3:
            masks.append((mc, cc))
        if 0 <= cc + 4 <= 3:
            masks.append((mu, cc + 4))
        ent.append((kt, lo, hi, masks))
    return ent


def load_consts(P, st, A):
    c = {}
    c['ident'] = P.sb(st, "c_ident", [128, 128], BF16)
    c['mc'] = P.sb(st, "c_mc", [128, 128], BF16)
    c['mu'] = P.sb(st, "c_mu", [128, 128], BF16)
    c['zeros'] = P.sb(st, "c_zeros", [128, 128], BF16)
    c['ident_w'] = P.sb(st, "c_identw", [128, 512], BF16)
    c['negones'] = P.sb(st, "c_negones", [65, 64], F32)
    c['negonesb'] = P.sb(st, "c_negonesb", [128, 128], BF16)
    P.dma(c['ident'][:], A['ident'], 'c0')
    P.dma(c['mc'][:], A['mc'], 'c0')
    P.dma(c['mu'][:], A['mu'], 'c0')
    P.memset('dve', c['zeros'][:], 0.0)
    P.memset('dve', c['ident_w'][:], 0.0)
    P.memset('dve', c['negones'][:], -1.0)
    P.memset('dve', c['negonesb'][:], -1.0)
    return c


def phase_fox(P, A, layer, consts):
    with ExitStack() as st:
        cx = AttnCtx(P, st, consts)
        cf = P.sb(st, "f_cf", [4, T], F32)
        tmp = P.sb(st, "f_tmp", [4, T], F32)
        ones = P.sb(st, "f_ones", [4, T], F32)
        fb = P.sb(st, "f_fb", [4, 2], F32)
        cs = P.sb(st, "f_cs", [4, 3, T], BF16)
        ncs = P.sb(st, "f_ncs", [4, 3, T], BF16)
        P.dma(cf[:], A['pT'][1048:1052, :], 'fx0')
        P.dma(fb[:, 0:1], A['fb'][layer], 'fx0')
        P.ts('dve', fb[:, 1:2], fb[:, 0:1], -1.0, None, ALU.mult)
        P.memset('pool', ones[:], 1.0)
        P.act(tmp[:], cf[:], AF.Exp, bias=fb[:, 1:2], scale=-1.0)
        P.act(tmp[:], tmp[:], AF.Ln, bias=1.0)
        P.scan(cf[:], ones[:], tmp[:], 0.0, ALU.mult, ALU.subtract)
        P.copy('dve', cs[:, 0, :], cf[:])
        P.tt('dve', tmp[:], cf[:], cs[:, 0, :], ALU.subtract)
        P.copy('dve', cs[:, 1, :], tmp[:])
        P.tt('dve', tmp[:], tmp[:], cs[:, 1, :], ALU.subtract)
        P.copy('dve', cs[:, 2, :], tmp[:])
        P.ts('dve', ncs[:].rearrange("p a t -> p (a t)"), cs[:].rearrange("p a t -> p (a t)"), -1.0, None, ALU.mult)
        Qs = [P.sb(st, "f_Q%d" % i, [128, T], BF16) for i in range(2)]
        Ks = [P.sb(st, "f_K%d" % i, [128, T], BF16) for i in range(2)]
        Vs = [P.sb(st, "f_V%d" % i, [128, 32, 65], BF16) for i in range(2)]
        ob = [P.sb(st, "f_ob%d" % i, [64, 512], BF16) for i in range(2)]
        for i in range(2):
            P.memset('pool', Qs[i][64:128, :], 0.0)
            P.memset('pool', Ks[i][64:128, :], 0.0)
            P.memset('pool', Qs[i][64:70, :], 1.0)
            P.memset('pool', Ks[i][64:70, :], 1.0)
            P.memset('pool', Vs[i][:, :, 64:65], 1.0)

        def load(h):
            Q = Qs[h % 2]; K = Ks[h % 2]; V = Vs[h % 2]
            P.dma(Q[0:64, :], A['qkT'][768 + 64 * h:768 + 64 * (h + 1), :], 'fxq%d' % (h % 2))
            P.dma(K[0:64, :], A['qkT'][1024 + 64 * h:1024 + 64 * (h + 1), :], 'fxk%d' % (h % 2), q='act')
            for i in range(3):
                P.dma(Q[64 + i:65 + i, :], cs[h:h + 1, i, :], 'fxq%d' % (h % 2))
                P.dma(K[67 + i:68 + i, :], ncs[h:h + 1, i, :], 'fxk%d' % (h % 2), q='act')
            P.dma(V[:, :, 0:64], A['vtok'][:, 512 + 64 * h:512 + 64 * (h + 1)].rearrange("(n p) d -> p n d", p=128), 'fxv%d' % (h % 2))

        load(0)
        for h in range(4):
            if h + 1 < 4:
                load(h + 1)
            Q = Qs[h % 2]; K = Ks[h % 2]; V = Vs[h % 2]
            for j in range(8):
                def fin(po, F, o=ob[j % 2], j=j, h=h):
                    P.tt('dve', o[:], po[0:64, :], F[:], ALU.mult)
                    P.dma(A['mixT'][768 + 64 * h:768 + 64 * (h + 1), j * 512:(j + 1) * 512], o[:], 'fxo%d' % (j % 2), q='sp')
                attn_chunk(cx, K, 128, Q, j, V, causal_entries(j, consts['mc']), fin)
            attn_flush(cx)

import math, os
STAGE = int(os.environ.get('STAGE', '99'))

T = 4096
LN8 = math.log(0.125)


def phase_hgrn(P, A, layer, consts):
    for ct in range(2):
        with ExitStack() as st:
            B = [P.sb(st, "hB%d" % i, [128, T], F32) for i in range(5)]
            qt = P.sb(st, "h_qt", [128, T], BF16)
            kt = P.sb(st, "h_kt", [128, 2, T], BF16)
            qg = P.sb(st, "h_qg", [128, T], BF16)
            kd = P.sb(st, "h_kd", [128, T], BF16)
            kdt = P.sb(st, "h_kdt", [128, 32, 2, 128], BF16)
            Vt = P.sb(st, "h_Vt", [128, 32, 128], BF16)
            Vz = P.sb(st, "h_Vz", [128, 32, 2, 128], BF16)
            Sbd = P.sb(st, "h_Sbd", [128, 64, 128], BF16)
            rst = P.sb(st, "h_rst", [128, T], BF16)
            sm = P.sb(st, "h_sm", [128, 8], F32)
            dl = P.sb(st, "h_dl", [128, 64], F32)
            mh = P.sb(st, "h_mh", [128, 128], BF16)
            bones = P.sb(st, "h_bones", [128, 128], BF16)
            ident = consts['ident']
            P.dma(mh[:], A['mh'], 'hg0')
            P.dma(bones[:], A['bones'], 'hg0')
            P.dma(sm[:, 0:2], A['lbl'][ct], 'hg0')
            P.dma(sm[:, 4:5], A['og'][layer], 'hg0')
            P.dma(B[0][:], A['pT'][256 + ct * 128:256 + (ct + 1) * 128, :], 'hgz')
            P.dma(B[3][:], A['pT'][ct * 128:(ct + 1) * 128, :], 'hgq', q='act')
            P.dma(Vt[:], A['vtok'][:, ct * 128:(ct + 1) * 128].rearrange("(n p) d -> p n d", p=128), 'hgv', q='pool')
            P.memset('pool', Vz[:], 0.0)
            for hh in range(2):
                P.dma(Vz[:, :, hh, hh * 64:(hh + 1) * 64],
                      A['vtok'][:, ct * 128 + hh * 64:ct * 128 + (hh + 1) * 64].rearrange("(n p) d -> p n d", p=128), 'hgv', q='pool')
            P.memset('pool', Sbd[:], 0.0)
            P.memset('pool', kt[:], 0.0)
            P.memset('pool', kdt[:], 0.0)
            P.memset('pool', rst[:], 1.0)
            P.memset('pool', rst[:].rearrange("p (c s) -> p c s", s=64)[:, :, 0:1], 0.0)
            lb = sm[:, 2:3]; oml = sm[:, 3:4]; noml = sm[:, 5:6]
            if layer == 0:
                P.memset('dve', lb, 0.0)
            else:
                P.act(sm[:, 0:2], sm[:, 0:2], AF.Exp)
                P.tt('dve', sm[:, 6:7], sm[:, 0:1], sm[:, 1:2], ALU.add)
                P.op('dve', lambda e, o=sm[:, 6:7]: e.reciprocal(out=o, in_=o), reads=[sm[:, 6:7]], writes=[sm[:, 6:7]])
                P.tt('dve', lb, sm[:, 1:2], sm[:, 6:7], ALU.mult)
            P.ts('dve', oml, lb, -1.0, 1.0, ALU.mult, ALU.add)
            P.ts('dve', noml, oml, -1.0, None, ALU.mult)
            P.act(B[0][:], B[0][:], AF.Sigmoid)
            P.ts('dve', B[1][:], B[0][:], oml, lb, ALU.mult, ALU.add)
            P.act(B[1][:], B[1][:], AF.Ln)
            P.scan(B[2][:], rst[:], B[1][:], 0.0, ALU.mult, ALU.add)
            P.ts('dve', B[1][:], B[0][:], noml, oml, ALU.mult, ALU.add)
            G3 = B[2][:].rearrange("p (c s) -> p c s", s=64)
            D3 = B[0][:].rearrange("p (c s) -> p c s", s=64)
            P.tt('dve', D3, G3, G3[:, :, 31:32].broadcast_to([128, 64, 64]), ALU.subtract)
            P.act(B[4][:], B[0][:], AF.Exp, bias=LN8)
            P.tt('dve', qt[:], B[3][:], B[4][:], ALU.mult)
            P.act(B[4][:], B[0][:], AF.Exp, scale=-1.0)
            P.tt('dve', kt[0:64, 0, :], B[1][0:64, :], B[4][0:64, :], ALU.mult)
            P.tt('dve', kt[64:128, 1, :], B[1][64:128, :], B[4][64:128, :], ALU.mult)
            P.act(B[4][:], B[2][:], AF.Exp, bias=LN8)
            P.tt('dve', qg[:], B[3][:], B[4][:], ALU.mult)
            P.tt('dve', D3, G3[:, :, 63:64].broadcast_to([128, 64, 64]), G3, ALU.subtract)
            P.act(B[4][:], B[0][:], AF.Exp)
            P.tt('dve', kd[:], B[1][:], B[4][:], ALU.mult)
            P.act(dl[:].unsqueeze(2), G3[:, :, 63:64], AF.Exp)
            P.memset('dve', dl[:, 0:1], 0.0)
            KV = B[0]; dfull = B[1]; Sall = B[3]; oT = B[4]
            if STAGE < 1:
                P.dma(A['mixT'][0:128, 0:T], kd[:], 'dbg'); continue
            with ExitStack() as s2:
                ptr = [P.ps(s2, "h_ptr%d" % i, [128, 8, 128], BF16) for i in range(2)]
                for g in range(4):
                    for i in range(8):
                        tl = g * 8 + i
                        P.transpose(ptr[g % 2][:, i, :], kd[:, tl * 128:(tl + 1) * 128], ident[:])
                    P.copy('act', kdt[0:64, g * 8:(g + 1) * 8, 0, :], ptr[g % 2][0:64, :, :])
                    P.copy('dve', kdt[64:128, g * 8:(g + 1) * 8, 1, :], ptr[g % 2][64:128, :, :])
            with ExitStack() as s2:
                pkv = [P.ps(s2, "h_pkv%d" % i, [128, 4, 128], F32) for i in range(2)]
                KV3 = KV[:].rearrange("p (v c) -> p v c", c=64)
                for g in range(16):
                    pk = pkv[g % 2]
                    for i in range(4):
                        c = g * 4 + i
                        tl = c // 2; hf = c % 2
                        P.mm(pk[:, i, :], kdt[:, tl, hf, :], Vt[:, tl, :], start=True, stop=True)
                    for hh in range(2):
                        P.copy('act' if hh else 'dve', KV3[hh * 64:(hh + 1) * 64, :, g * 4:(g + 1) * 4],
                               pk[hh * 64:(hh + 1) * 64, :, hh * 64:(hh + 1) * 64].rearrange("p g v -> p v g"))
            if STAGE < 2:
                P.dma(A['mixT'][0:128, 0:T], kd[:], 'dbg'); continue
            P.copy('pool', dfull[:].rearrange("p (v c) -> p v c", c=64), dl[:].unsqueeze(1).broadcast_to([128, 64, 64]))
            P.scan(Sall[:], dfull[:], KV[:], 0.0, ALU.mult, ALU.add)
            S3 = Sall[:].rearrange("p (v c) -> p v c", c=64)
            for hh in range(2):
                P.copy('dve' if hh else 'act', Sbd[hh * 64:(hh + 1) * 64, 1:64, hh * 64:(hh + 1) * 64],
                       S3[hh * 64:(hh + 1) * 64, :, 0:63].rearrange("p v c -> p c v"))
            if STAGE < 3:
                P.dma(A['mixT'][0:128, 0:T], kd[:], 'dbg'); continue
            with ExitStack() as s2:
                pA = [P.ps(s2, "h_pA%d" % i, [128, 128], F32) for i in range(4)]
                po = [P.ps(s2, "h_po%d" % i, [128, 128], F32) for i in range(2)]
                Am = [P.sb(s2, "h_Am%d" % i, [128, 128], BF16) for i in range(4)]
                def scores(tl):
                    cols = slice(tl * 128, (tl + 1) * 128)
                    for hh in range(2):
                        i = (tl % 2) * 2 + hh
                        P.mm(pA[i][:], kt[:, hh, cols], qt[:, cols], start=True, stop=True)
                        P.tt('dve', Am[i][:], pA[i][:], mh[:], ALU.mult)

                def outs(tl):
                    cols = slice(tl * 128, (tl + 1) * 128)
                    p_ = po[tl % 2]
                    P.mm(p_[:], Vz[:, tl, 0, :], Am[(tl % 2) * 2][:], start=True, stop=False)
                    P.mm(p_[:], Vz[:, tl, 1, :], Am[(tl % 2) * 2 + 1][:], start=False, stop=False)
                    P.mm(p_[:, 0:64], Sbd[:, 2 * tl, :], qg[:, tl * 128:tl * 128 + 64], start=False, stop=False)
                    P.mm(p_[:, 64:128], Sbd[:, 2 * tl + 1, :], qg[:, tl * 128 + 64:tl * 128 + 128], start=False, stop=True)
                    P.copy('act', oT[:, cols], p_[:])

                for tl in range(33):
                    if tl < 32:
                        scores(tl)
                    if tl >= 1:
                        outs(tl - 1)
            if STAGE < 4:
                P.dma(A['mixT'][0:128, 0:T], kd[:], 'dbg'); continue
            with ExitStack() as s2:
                pss = [P.ps(s2, "h_pss%d" % i, [128, 512], F32) for i in range(2)]
                sq = [P.sb(s2, "h_sq%d" % i, [128, 512], BF16) for i in range(2)]
                rs = [P.sb(s2, "h_rs%d" % i, [128, 512], F32) for i in range(2)]
                ag = [P.sb(s2, "h_ag%d" % i, [128, 512], F32) for i in range(2)]
                ob = [P.sb(s2, "h_ob%d" % i, [128, 512], BF16) for i in range(2)]
                for j in range(8):
                    b = j % 2
                    cols = slice(j * 512, (j + 1) * 512)
                    P.dma(ag[b][:], A['pT'][512 + ct * 128:512 + (ct + 1) * 128, cols], 'hga%d' % b)
                    P.act(sq[b][:], oT[:, cols], AF.Square)
                    P.mm(pss[b][:], bones[:], sq[b][:], start=True, stop=True)
                    P.act(rs[b][:], pss[b][:], AF.Sqrt, bias=1e-6, scale=1.0 / 64)
                    P.op('dve', lambda e, o=rs[b][:]: e.reciprocal(out=o, in_=o), reads=[rs[b][:]], writes=[rs[b][:]])
                    P.act(ag[b][:], ag[b][:], AF.Silu)
                    P.stt('dve', rs[b][:], oT[:, cols], sm[:, 4:5], rs[b][:], ALU.mult, ALU.mult)
                    P.tt('pool', ob[b][:], rs[b][:], ag[b][:], ALU.mult)
                    P.dma(A['mixT'][ct * 128:(ct + 1) * 128, cols], ob[b][:], 'hgo%d' % b, q='pool')

import os
STAGE = int(os.environ.get('STAGE', '99'))

T = 4096
NEG = -30000.0


def phase_nsa(P, A, layer, consts):
    c = consts
    ident = c['ident']
    with ExitStack() as st:
        cx = AttnCtx(P, st, consts)
        with ExitStack() as s0:
            lng = P.sb(s0, "n_lng", [24, T], F32)
            P.dma(lng[:], A['pT'][1024:1048, :], 'ns0')
            P.act(lng[:], lng[:], AF.Sigmoid)
            P.dma(A['gsig'], lng[:], 'ns0')
        lng2 = A['gsig']
        ovaug = P.sb(st, "n_ov", [128, 2, 72], BF16)
        wc = P.sb(st, "n_wc", [128, 3200], BF16)
        addm = P.sb(st, "n_addm", [128, 32, 64], F32)
        P.dma(ovaug[:], A['ovaug'], 'ns0')
        P.dma(wc[:], A['wc'], 'ns0')
        P.dma(addm[:], A['addmask'], 'ns0')
        kcTs = [P.sb(st, "n_kcT%d" % i, [128, 256], BF16) for i in range(2)]
        vcAs = [P.sb(st, "n_vcA%d" % i, [128, 2, 65], BF16) for i in range(2)]
        for g in range(2):
            kcT = kcTs[g]; vcA = vcAs[g]
            with ExitStack() as s2:
                w1 = P.sb(s2, "n_w1", [64, 32, 128], BF16)
                w1f = P.sb(s2, "n_w1f", [64, 32, 128], F32)
                w2 = P.sb(s2, "n_w2", [128, 64], BF16)
                w2f = P.sb(s2, "n_w2f", [128, 64], F32)
                posT = P.sb(s2, "n_posT", [64, 32], BF16)
                posf = P.sb(s2, "n_posf", [64, 32], F32)
                posb = P.sb(s2, "n_posb", [64, 32, 256], BF16)
                srcf = P.sb(s2, "n_srcf", [64, T], F32)
                srcb = P.sb(s2, "n_srcb", [64, T], BF16)
                bias = P.sb(s2, "n_bias", [128, 1], F32)
                xb = P.sb(s2, "n_xb", [128, 256], F32)
                x2 = P.sb(s2, "n_x2", [128, 256], F32)
                hid = P.sb(s2, "n_hid", [128, 256], BF16)
                ktm = P.sb(s2, "n_ktm", [128, 64], F32)
                kts = P.sb(s2, "n_kts", [128, 64], F32)
                ktb = P.sb(s2, "n_ktb", [128, 128], BF16)
                sm = P.sb(s2, "n_sm", [128, 4], F32)
                rt = P.sb(s2, "n_rt", [128, 4, 8], F32)
                kng = P.sb(s2, "n_kng", [128, 64], F32)
                cosc = P.sb(s2, "n_cosc", [128, 2, 8], F32)
                sinc = P.sb(s2, "n_sinc", [128, 2, 8], F32)
                ph = cx.psS[0]; pb = cx.psS[1]; po = cx.psO[0]
                pt = P.ps(s2, "n_pt", [128, 128], BF16)
                P.dma(kng[:], A['kng'][layer].partition_broadcast(128), 'ns1')
                P.dma(cosc[:], A['cosc'], 'ns1')
                P.dma(sinc[:], A['sinc'], 'ns1')
                P.memset('dve', hid[:], 0.0)
                P.memset('dve', vcA[:], 0.0)
                P.memset('dve', kcT[:], 0.0)
                P.memset('dve', ktb[:], 0.0)
                for which in range(2):
                    P.dma(w1f[:], A['w1r'][layer, which], 'ns2')
                    P.dma(w2f[:], A['w2'][layer, which], 'ns2')
                    P.dma(posf[:], A['posT'][layer, which], 'ns2')
                    P.dma(srcf[:], A['pT'][768 + 128 * which + 64 * g:768 + 128 * which + 64 * (g + 1), :], 'ns3', q='act')
                    P.copy('dve', w1[:], w1f[:])
                    P.copy('dve', w2[:], w2f[:])
                    P.copy('dve', posT[:], posf[:])
                    P.copy('dve', posb[:], posT[:].unsqueeze(2).broadcast_to([64, 32, 256]))
                    P.copy('act', srcb[:], srcf[:])
                    for l in range(32):
                        P.mm(ph[:, 0:255], w1[:, l, :], srcb[:].rearrange("p (n s) -> p n s", s=16)[:, (l // 16):(l // 16) + 255, l % 16], start=(l == 0), stop=False)
                    for l in range(32):
                        P.mm(ph[:, 0:255], w1[:, l, :], posb[:, l, 0:255], start=False, stop=(l == 31))
                    P.copy('act', xb[:, 0:255], ph[:, 0:255])
                    P.tt('dve', x2[:, 0:255], xb[:, 0:255], xb[:, 0:255], ALU.mult)
                    P.ts('dve', x2[:, 0:255], x2[:, 0:255], 0.044715, 1.0, ALU.mult, ALU.add)
                    P.tt('dve', x2[:, 0:255], x2[:, 0:255], xb[:, 0:255], ALU.mult)
                    P.act(x2[:, 0:255], x2[:, 0:255], AF.Sigmoid, scale=1.5957691216057308)
                    P.tt('dve', hid[:, 0:255], x2[:, 0:255], xb[:, 0:255], ALU.mult)
                    for nt in range(2):
                        P.mm(po[:, 0:64], hid[:, nt * 128:(nt + 1) * 128], w2[:], start=True, stop=True)
                        if which == 1:
                            nr = 128 if nt == 0 else 127
                            P.copy('act', vcA[0:nr, nt, 0:64], po[0:nr, 0:64])
                            P.memset('dve', vcA[0:nr, nt, 64:65], 1.0)
                        else:
                            P.act(kts[:], po[:, 0:64], AF.Square, accum_out=sm[:, 0:1])
                            P.act(sm[:, 1:2], sm[:, 0:1], AF.Sqrt, bias=1e-6, scale=1.0 / 64)
                            P.op('dve', lambda e, o=sm[:, 1:2]: e.reciprocal(out=o, in_=o), reads=[sm[:, 1:2]], writes=[sm[:, 1:2]])
                            P.stt('dve', ktm[:], po[:, 0:64], sm[:, 1:2], kng[:], ALU.mult, ALU.mult)
                            P.copy('act', ktb[:, 0:64], ktm[:])
                            P.tt('dve', rt[:, 0, :], ktm[:, 0:8], cosc[:, nt, :], ALU.mult)
                            P.tt('dve', rt[:, 1, :], ktm[:, 8:16], sinc[:, nt, :], ALU.mult)
                            P.tt('dve', rt[:, 2, :], ktm[:, 8:16], cosc[:, nt, :], ALU.mult)
                            P.tt('dve', rt[:, 3, :], ktm[:, 0:8], sinc[:, nt, :], ALU.mult)
                            P.tt('dve', ktb[:, 0:8], rt[:, 0, :], rt[:, 1, :], ALU.subtract)
                            P.tt('dve', ktb[:, 8:16], rt[:, 2, :], rt[:, 3, :], ALU.add)
                            P.transpose(pt[:], ktb[:], ident[:])
                            P.copy('dve', kcT[0:64, nt * 128:(nt + 1) * 128], pt[0:64, :])
            P.memset('dve', kcT[0:64, 255:256], 0.0)
        selT = P.sb(st, "n_selT", [128, T], BF16)
        imp = P.sb(st, "n_imp", [128, 32, 64], F32)
        acc = [P.sb(st, "n_acc%d" % i, [64, T], F32) for i in range(4)]
        Q = [P.sb(st, "n_Q%d" % i, [128, T], BF16) for i in range(4)]
        Ks = P.sb(st, "n_Ks", [128, T], BF16)
        Kw = P.sb(st, "n_Kw", [128, T], BF16)
        Vs = P.sb(st, "n_Vs", [128, 32, 65], BF16)
        Vw = P.sb(st, "n_Vw", [128, 32, 65], BF16)
        for g in range(2):
            kcT = kcTs[g]; vcA = vcAs[g]
            for hh in range(4):
                h = 4 * g + hh
                P.memset('pool', Q[hh][64:128, :], 0.0)
                P.dma(Q[hh][0:64, :], A['qkT'][64 * h:64 * (h + 1), :], 'nsq%d' % hh)
            P.dma(Ks[0:64, :], A['qkT'][512 + 64 * g:512 + 64 * (g + 1), :], 'nsk')
            P.dma(Ks[64:128, :], A['eall'], 'nsk')
            P.dma(Kw[0:64, :], A['qkT'][640 + 64 * g:640 + 64 * (g + 1), :], 'nsk')
            P.memset('pool', Kw[64:128, :], 0.0)
            P.memset('pool', Vs[:, :, 64:65], 1.0)
            P.memset('pool', Vw[:, :, 64:65], 1.0)
            P.dma(Vs[:, :, 0:64], A['vtok'][:, 256 + 64 * g:256 + 64 * (g + 1)].rearrange("(n p) d -> p n d", p=128), 'nsv', q='act')
            P.dma(Vw[:, :, 0:64], A['vtok'][:, 384 + 64 * g:384 + 64 * (g + 1)].rearrange("(n p) d -> p n d", p=128), 'nsv', q='act')
            with ExitStack() as s2:
                pimp = [cx.psS[i][:, 512:800].rearrange("p (a b) -> p a b", b=72) for i in range(2)]
                pTc = [P.sb(s2, "n_pTc%d" % i, [128, 512], BF16) for i in range(3)]
                rinvs = [P.sb(s2, "n_rinv%d" % i, [128, 4], F32) for i in range(2)]
                kc_ = 0
                kch = 0
                for hh in range(4):
                    h = 4 * g + hh
                    for j in range(8):
                        q0 = j * 512
                        po = cx.psO[cx.kO % 2]
                        cx.kO += 1
                        pim = pimp[kch % 2]
                        rinv = rinvs[kch % 2]
                        kch += 1
                        tiles = []
                        for nt in range(2):
                            off = 2048 * nt + 31 - 512 * j
                            if -off + 511 < 0:
                                continue
                            tiles.append((nt, off))
                        for ti, (nt, off) in enumerate(tiles):
                            ps = cx.psS[cx.kS % 2][:, 0:512]
                            cx.kS += 1
                            ptc = pTc[kc_ % 3]
                            kc_ += 1
                            first = (ti == 0)
                            last = (ti == len(tiles) - 1)

                            def s_fn(ps=ps, nt=nt, off=off, first=first, po=po, pim=pim, hh=hh, q0=q0):
                                full = (-off >= 2032)
                                P.mm(ps[:], kcT[:, nt * 128:(nt + 1) * 128], Q[hh][:, q0:q0 + 512], start=True, stop=full)
                                if not full:
                                    ci0 = -off + 511
                                    P.mm(ps[:], ident[:], wc[:, ci0:ci0 + 512], start=False, stop=True)

                            def exp_fn(ps=ps, ptc=ptc):
                                P.act(ptc[:], ps[:], AF.Exp)

                            def pv_fn(po=po, pim=pim, ptc=ptc, nt=nt, last=last, first=first):
                                P.mm(po[0:65, :], vcA[:, nt, :], ptc[:], start=first, stop=last)
                                for m in range(4):
                                    P.mm(pim[:, m, :], ptc[:, m * 128:(m + 1) * 128], ovaug[:, nt, :], start=(first and m == 0), stop=last)

                            _push_block(cx, s_fn, exp_fn, pv_fn, first=first)

                        def fin(po_, F, hh=hh, q0=q0, pim=pim, rinv=rinv, j=j):
                            for m in range(4):
                                tq = j * 4 + m
                                if hh == 0:
                                    P.ts('dve', imp[:, tq, :], pim[:, m, 0:64], rinv[:, m:m + 1], None, ALU.mult)
                                else:
                                    P.stt('dve', imp[:, tq, :], pim[:, m, 0:64], rinv[:, m:m + 1], imp[:, tq, :], ALU.mult, ALU.add)
                            P.tt('dve', acc[hh][:, q0:q0 + 512], po_[0:64, :], F[:], ALU.mult)

                        ea, eb = _mk_factor(cx, po, lng2, h, q0, fin)

                        def ea2(ea=ea, pim=pim, rinv=rinv):
                            ea()
                            P.ts('dve', rinv[:, 0:4].unsqueeze(2), pim[:, :, 64:65], 1e-30, None, ALU.max)
                            P.op('dve', lambda e, o=rinv[:, 0:4]: e.reciprocal(out=o, in_=o), reads=[rinv[:, 0:4]], writes=[rinv[:, 0:4]])

                        _end_chunk(cx, ea2, eb)
                attn_flush(cx)
            if STAGE < 2:
                continue
            with ExitStack() as s2:
                wk = [P.sb(s2, "n_wk%d" % i, [128, 64], F32) for i in range(2)]
                w2_ = [P.sb(s2, "n_wk2%d" % i, [128, 64], F32) for i in range(2)]
                m8 = [P.sb(s2, "n_m8%d" % i, [128, 16], F32) for i in range(2)]
                sb_ = [P.sb(s2, "n_sb%d" % i, [128, 128], BF16) for i in range(2)]
                pts = [P.ps(s2, "n_pts%d" % i, [128, 128], BF16) for i in range(1)] * 2
                P.memset('pool', sb_[0][:], 0.0)
                P.memset('pool', sb_[1][:], 0.0)
                for tq in range(32):
                    b = tq % 2
                    P.tt('dve', wk[b][:], imp[:, tq, :], addm[:, tq, :], ALU.add)
                    P.op('dve', lambda e, o=m8[b][:, 0:8], i=wk[b][:]: e.max(out=o, in_=i), reads=[wk[b][:]], writes=[m8[b][:, 0:8]])
                    P.op('dve', lambda e, o=w2_[b][:], r=m8[b][:, 0:8], i=wk[b][:]: e.match_replace(out=o, in_to_replace=r, in_values=i, imm_value=-3.0e38),
                         reads=[m8[b][:, 0:8], wk[b][:]], writes=[w2_[b][:]])
                    P.op('dve', lambda e, o=m8[b][:, 8:16], i=w2_[b][:]: e.max(out=o, in_=i), reads=[w2_[b][:]], writes=[m8[b][:, 8:16]])
                    P.ts('dve', w2_[b][:], wk[b][:], m8[b][:, 15:16], None, ALU.is_ge)
                    P.ts('dve', wk[b][:], wk[b][:], -5.0e29, None, ALU.is_gt)
                    P.tt('dve', wk[b][:], wk[b][:], w2_[b][:], ALU.mult)
                    P.ts('dve', sb_[b][:, 64:128], wk[b][:], -1.0, -NEG, ALU.add, ALU.mult)
                    P.transpose(pts[b][:], sb_[b][:], ident[:])
                    P.copy('act', selT[64:128, tq * 128:(tq + 1) * 128], pts[b][64:128, :])
            for hh in range(4):
                P.dma(Q[hh][64:128, :], selT[64:128, :], 'nsq%d' % hh, q='sp' if hh % 2 else 'act')
            if STAGE < 3:
                continue
            with ExitStack() as s2:
                tmp = [P.sb(s2, "n_tmp%d" % i, [64, 512], F32) for i in range(2)]
                ob = [P.sb(s2, "n_ob%d" % i, [64, 512], BF16) for i in range(1)] * 2
                for hh in range(4):
                    h = 4 * g + hh
                    for j in range(8):
                        q0 = j * 512
                        a = acc[hh][:, q0:q0 + 512]

                        def fin_s(po_, F, a=a):
                            P.tt('dve', tmp[0][:], po_[0:64, :], F[:], ALU.mult)
                            P.tt('pool', a, a, tmp[0][:], ALU.add)

                        def fin_w(po_, F, a=a, j=j, h=h, q0=q0):
                            P.tt('dve', tmp[1][:], po_[0:64, :], F[:], ALU.mult)
                            P.tt('pool', ob[j % 2][:], a, tmp[1][:], ALU.add)
                            if STAGE >= 5:
                                P.dma(A['mixT'][256 + 64 * h:256 + 64 * (h + 1), q0:q0 + 512], ob[j % 2][:], 'nso%d' % (j % 2), q='sp')

                        attn_chunk(cx, Ks, 128, Q[hh], j, Vs, causal_entries(j, c['mc']), fin_s, lng2=lng2, gate_c=8 + h)
                        if STAGE >= 4:
                            attn_chunk(cx, Kw, 128, Q[hh], j, Vw, window_entries(j, c['mc'], c['mu']), fin_w, lng2=lng2, gate_c=16 + h)
                attn_flush(cx)


T = 4096
D = 1024
FF = 4096


def phase_wo(P, A, layer, consts, x_in, x_mid, per_tile=None):
    ident = consts['ident']
    with ExitStack() as st:
        Wo = P.sb(st, "wo", [128, 8, D], BF16)
        with ExitStack() as s2:
            wst = [P.sb(s2, "wost%d" % i, [128, D], F32) for i in range(2)]
            for kc in range(8):
                b = wst[kc % 2]
                P.dma(b[:], A['wo'][layer, kc * 128:(kc + 1) * 128, :], 'wost%d' % (kc % 2))
                P.copy('act' if kc % 2 else 'dve', Wo[:, kc, :], b[:])
        mx = [P.sb(st, "wo_mx%d" % i, [128, 8, 512], BF16) for i in range(2)]
        xt = [P.sb(st, "wo_xt%d" % i, [128, D], F32) for i in range(2)]
        xm = [P.sb(st, "wo_xm%d" % i, [128, D], F32) for i in range(2)]
        sq = P.sb(st, "wo_sq", [128, D], BF16)
        hb = [P.sb(st, "wo_hb%d" % i, [128, D], BF16) for i in range(2)]
        ss = [P.sb(st, "wo_ss%d" % i, [128, 2], F32) for i in range(2)]
        hst = [P.sb(st, "wo_hst%d" % i, [128, 8, 512], BF16) for i in range(1)] * 2
        po = [P.ps(st, "wo_po%d" % i, [128, 512], F32) for i in range(4)]
        ptr = [P.ps(st, "wo_ptr%d" % i, [128, 8, 128], BF16) for i in range(2)]
        def part_a(t):
            j = t // 4
            b = t % 2
            if t % 4 == 0:
                P.dma(mx[j % 2][:], A['mixT'][:, j * 512:(j + 1) * 512].rearrange("(a p) n -> p a n", p=128), 'womx%d' % (j % 2))
            P.dma(xt[b][:], x_in[t * 128:(t + 1) * 128, :], 'woxt%d' % b, q='act')
            for half in range(2):
                pp = po[(t % 2) * 2 + half]
                for kc in range(8):
                    P.mm(pp[:], mx[j % 2][:, kc, (t % 4) * 128:(t % 4 + 1) * 128], Wo[:, kc, half * 512:(half + 1) * 512],
                         start=(kc == 0), stop=(kc == 7))
                P.tt('dve', xm[b][:, half * 512:(half + 1) * 512], pp[:], xt[b][:, half * 512:(half + 1) * 512], ALU.add)
            P.dma(x_mid[t * 128:(t + 1) * 128, :], xm[b][:], 'woxm%d' % b, q='pool')
            P.act(sq[:], xm[b][:], AF.Square, accum_out=ss[b][:, 0:1])
            P.act(ss[b][:, 1:2], ss[b][:, 0:1], AF.Sqrt, bias=1e-6, scale=1.0 / D)
            P.op('dve', lambda e, o=ss[b][:, 1:2]: e.reciprocal(out=o, in_=o), reads=[ss[b][:, 1:2]], writes=[ss[b][:, 1:2]])
            P.ts('dve', hb[b][:], xm[b][:], ss[b][:, 1:2], None, ALU.mult)

        def part_b(t):
            j = t // 4
            b = t % 2
            for kc in range(8):
                P.transpose(ptr[b][:, kc, :], hb[b][:, kc * 128:(kc + 1) * 128], ident[:])
            P.copy('act', hst[j % 2][:, :, (t % 4) * 128:(t % 4 + 1) * 128], ptr[b][:])
            if t % 4 == 3:
                P.dma(A['h2T'][:, j * 512:(j + 1) * 512].rearrange("(a p) n -> p a n", p=128), hst[j % 2][:], 'wohst%d' % (j % 2), q='pool')
            if per_tile is not None:
                per_tile(t)

        for t in range(33):
            if t < 32:
                part_a(t)
            if t >= 1:
                part_b(t - 1)


def phase_wo_ffn(P, A, layer, consts, x_in, x_mid, x_out):
    with ExitStack() as st:
        Wu = P.sb(st, "wu", [128, 8, FF], BF16)
        Wd = P.sb(st, "wd", [128, 32, D], BF16)
        g2 = P.sb(st, "g2", [128, 8], F32)
        P.dma(g2[:], A['g2'][layer], 'ff0')
        with ExitStack() as s2:
            wst = [P.sb(s2, "fwst%d" % i, [128, 1024], F32) for i in range(2)]

            def per_tile(t):
                for u in range(2):
                    ci = 2 * t + u
                    b = u
                    if ci < 32:
                        kc, qt = ci // 4, ci % 4
                        P.dma(wst[b][:], A['wup'][layer, kc * 128:(kc + 1) * 128, qt * 1024:(qt + 1) * 1024], 'fwst%d' % b, q='sp')
                        if u:
                            P.ts('dve', Wu[:, kc, qt * 1024:(qt + 1) * 1024], wst[b][:], g2[:, kc:kc + 1], None, ALU.mult)
                        else:
                            P.act(Wu[:, kc, qt * 1024:(qt + 1) * 1024], wst[b][:], AF.Copy, scale=g2[:, kc:kc + 1])
                    else:
                        fc = ci - 32
                        P.dma(wst[b][:], A['wdn'][layer, fc * 128:(fc + 1) * 128, :], 'fwst%d' % b, q='sp')
                        P.copy('dve' if u else 'act', Wd[:, fc, :], wst[b][:])

            phase_wo(P, A, layer, consts, x_in, x_mid, per_tile=per_tile)
        _ffn_body(P, st, A, Wu, Wd, x_mid, x_out)


def phase_ffn(P, A, layer, consts, x_mid, x_out):
    with ExitStack() as st:
        Wu = P.sb(st, "wu", [128, 8, FF], BF16)
        Wd = P.sb(st, "wd", [128, 32, D], BF16)
        g2 = P.sb(st, "g2", [128, 8], F32)
        P.dma(g2[:], A['g2'][layer], 'ff0')
        with ExitStack() as s2:
            wst = [P.sb(s2, "fwst%d" % i, [128, 2048], F32) for i in range(3)]
            k = 0
            for kc in range(8):
                for hf in range(2):
                    b = k % 3
                    P.dma(wst[b][:], A['wup'][layer, kc * 128:(kc + 1) * 128, hf * 2048:(hf + 1) * 2048], 'fwst%d' % b, q='sp' if k % 2 else 'act')
                    if k % 2:
                        P.ts('dve', Wu[:, kc, hf * 2048:(hf + 1) * 2048], wst[b][:], g2[:, kc:kc + 1], None, ALU.mult)
                    else:
                        P.act(Wu[:, kc, hf * 2048:(hf + 1) * 2048], wst[b][:], AF.Copy, scale=g2[:, kc:kc + 1])
                    k += 1
            for fc2 in range(16):
                b = k % 3
                P.dma(wst[b][:].rearrange("p (a n) -> p a n", a=2), A['wdn'][layer, fc2 * 256:(fc2 + 1) * 256, :].rearrange("(a p) n -> p a n", p=128),
                      'fwst%d' % b, q='sp' if k % 2 else 'act')
                P.copy('dve' if k % 2 else 'act', Wd[:, fc2 * 2:(fc2 + 1) * 2, :], wst[b][:].rearrange("p (a n) -> p a n", a=2))
                k += 1
        _ffn_body(P, st, A, Wu, Wd, x_mid, x_out)


def _ffn_body(P, st, A, Wu, Wd, x_mid, x_out):
    if True:
        h2 = [P.sb(st, "ff_h2%d" % i, [128, 8, 512], BF16) for i in range(2)]
        uT = P.sb(st, "ff_uT", [128, 32, 512], BF16)
        rl = [P.sb(st, "ff_rl%d" % i, [128, 512], F32) for i in range(2)]
        xt = [P.sb(st, "ff_xt%d" % i, [128, D], F32) for i in range(2)]
        xo = [P.sb(st, "ff_xo%d" % i, [128, D], F32) for i in range(2)]
        pu = [P.ps(st, "ff_pu%d" % i, [128, 512], F32) for i in range(3)]
        pd = [P.ps(st, "ff_pd%d" % i, [128, 512], F32) for i in range(4)]
        ku = 0
        for j in range(8):
            P.dma(h2[j % 2][:], A['h2T'][:, j * 512:(j + 1) * 512].rearrange("(a p) n -> p a n", p=128), 'ffh2%d' % (j % 2))
            for fc in range(32):
                pp = pu[ku % 3]
                r = rl[ku % 2]
                for kc in range(8):
                    P.mm(pp[:], Wu[:, kc, fc * 128:(fc + 1) * 128], h2[j % 2][:, kc, :], start=(kc == 0), stop=(kc == 7))
                P.act(r[:], pp[:], AF.Relu)
                P.tt('dve' if ku % 2 else 'pool', uT[:, fc, :], r[:], r[:], ALU.mult)
                ku += 1
            for tt in range(4):
                t = j * 4 + tt
                b = t % 2
                P.dma(xt[b][:], x_mid[t * 128:(t + 1) * 128, :], 'ffxt%d' % b, q='act')
                for half in range(2):
                    pp = pd[(t % 2) * 2 + half]
                    for fc in range(32):
                        P.mm(pp[:], uT[:, fc, tt * 128:(tt + 1) * 128], Wd[:, fc, half * 512:(half + 1) * 512], start=(fc == 0), stop=(fc == 31))
                    P.tt('dve', xo[b][:, half * 512:(half + 1) * 512], pp[:], xt[b][:, half * 512:(half + 1) * 512], ALU.add)
                P.dma(x_out[t * 128:(t + 1) * 128, :], xo[b][:], 'ffxo%d' % b, q='pool')

import ml_dtypes
from concourse.bass_utils import run_bass_kernel_spmd

T=4096; D=1024
OFF = {}
_names = ['aq','af','ai','ag','bq','bkc','bvc','bks','bvs','bkw','bvw','bg','cq','ck','cv','cf']
_sizes = [256,256,256,256,512,128,128,128,128,128,128,24,256,256,256,4]
_o = 0
for n_, s_ in zip(_names, _sizes):
    OFF[n_] = (_o, _o + s_); _o += s_
TOK_ORDER = ['bq','bks','bkw','cq','ck','ai','bvs','bvw','cv']
T_ORDER = ['aq','af','ag','bkc','bvc','bg','cf']

def win_layout(w_in):
    L = w_in.shape[0]
    out = np.zeros((L, 1024, 2048 + 1152), np.float32)
    c = 0
    for n_ in TOK_ORDER:
        a, b = OFF[n_]; out[:, :, c:c + b - a] = w_in[:, :, a:b]; c += b - a
    assert c == 2048
    for n_ in T_ORDER:
        a, b = OFF[n_]; out[:, :, c:c + b - a] = w_in[:, :, a:b]; c += b - a
    return out

def rope_tables():
    inv = np.power(np.float32(500000.0), -np.arange(0, 16, 2, dtype=np.float32) / 16).astype(np.float32)
    pos = np.arange(T, dtype=np.float32)
    ang = pos[:, None] * inv[None, :]
    cos = np.cos(ang).astype(np.float32); sin = np.sin(ang).astype(np.float32)
    return (np.ascontiguousarray(cos.reshape(32, 128, 8).transpose(1, 0, 2)),
            np.ascontiguousarray(sin.reshape(32, 128, 8).transpose(1, 0, 2)))

def _skip():
    pass

def const_inputs():
    k = np.arange(128)[:, None]; q = np.arange(128)[None, :]
    mc = np.where(k <= q, 0.0, -30000.0).astype(ml_dtypes.bfloat16)
    mu = np.where(k > q, 0.0, -30000.0).astype(ml_dtypes.bfloat16)
    selneg = np.zeros((24, 24 * 64), np.float32)
    for c in range(24):
        selneg[c, c * 64:(c + 1) * 64] = -1.0
    return dict(ident=np.eye(128, dtype=ml_dtypes.bfloat16), mc=mc, mu=mu, selneg=selneg)

def _unused_ref_proj(inp, layer, x):
    x = x.astype(np.float64)
    h = x / np.sqrt((x * x).mean(-1, keepdims=True) + 1e-6) * inp['norm1_g'][layer]
    return h @ inp['w_in'][layer].astype(np.float64)

def hgrn_consts(inp):
    s = np.arange(128)[:, None]; t = np.arange(128)[None, :]
    mh = ((s // 64 == t // 64) & (s <= t)).astype(ml_dtypes.bfloat16)
    bones = (s // 64 == t // 64).astype(ml_dtypes.bfloat16)
    lbl = np.ascontiguousarray(inp['hgrn_lb_logits'].reshape(2, 2, 128).transpose(1, 2, 0)).astype(np.float32)
    og = np.tile(inp['hgrn_onorm_g'], (1, 2)).reshape(2, 128, 1).astype(np.float32)
    return dict(mh=mh, bones=bones, lbl=lbl, og=og)

def _unused_ref_hgrn(inp, layer, proj):
    def sl(n): a, b = OFF[n]; return proj[:, a:b]
    lbp = np.exp(inp['hgrn_lb_logits'].astype(np.float64)); lbp /= lbp.sum(0, keepdims=True)
    lb_all = np.cumsum(lbp, 0) - lbp[0:1]
    lb = lb_all[layer].reshape(4, 64)
    z = sl('af').reshape(T, 4, 64)
    sig = 1 / (1 + np.exp(-z))
    f = lb + (1 - lb) * sig; logf = np.log(f); k = (1 - lb) * (1 - sig)
    q = sl('aq').reshape(T, 4, 64) * 0.125; v = sl('ai').reshape(T, 4, 64)
    o = np.zeros((T, 4, 64))
    for h in range(4):
        S = np.zeros((64, 64))
        for c in range(64):
            r = slice(c * 64, (c + 1) * 64)
            G = np.cumsum(logf[r, h], 0)
            qc, kc, vc = q[r, h], k[r, h], v[r, h]
            o_inter = (qc * np.exp(G)) @ S
            diff = G[:, None, :] - G[None, :, :]
            mask = np.tril(np.ones((64, 64), bool))
            dec = np.where(mask[:, :, None], np.exp(np.minimum(diff, 0)), 0)
            sc = np.einsum('tk,sk,tsk->ts', qc, kc, dec)
            o[r, h] = o_inter + sc @ vc
            S = S * np.exp(G[-1])[:, None] + (kc * np.exp(G[-1] - G)).T @ vc
    g = sl('ag').reshape(T, 4, 64)
    gate = g / (1 + np.exp(-g))
    on = o / np.sqrt((o * o).mean(-1, keepdims=True) + 1e-6) * inp['hgrn_onorm_g'][layer]
    return (on * gate).reshape(T, 256)

def nsa_consts(inp):
    n_cmp = 255
    ci = np.arange(n_cmp)[:, None]; sj = np.arange(64)[None, :]
    ov = ((ci * 16 <= sj * 64 + 63) & (ci * 16 + 31 >= sj * 64)).astype(np.float32)
    ovaug = np.zeros((256, 72), np.float32); ovaug[:255, :64] = ov; ovaug[:255, 64] = 1.0
    ovaug = np.ascontiguousarray(ovaug.reshape(2, 128, 72).transpose(1, 0, 2)).astype(ml_dtypes.bfloat16)
    nl = np.arange(128)[:, None]; cc = np.arange(3200)[None, :] - 511
    wc = np.where(cc >= 16 * nl, 0.0, -30000.0).astype(ml_dtypes.bfloat16)
    eall = (np.arange(T)[None, :] // 64 == np.arange(64)[:, None]).astype(ml_dtypes.bfloat16)
    q = np.arange(T)[:, None]; j = np.arange(64)[None, :]; cur = q // 64
    am = np.zeros((T, 64), np.float32)
    am[(j == 0) | (j == cur) | (j == cur - 1)] = 1e30
    am[np.broadcast_to(j > cur, am.shape)] = -1e30
    addmask = np.ascontiguousarray(am.reshape(32, 128, 64).transpose(1, 0, 2))
    inv = np.power(np.float32(500000.0), -np.arange(0, 16, 2, dtype=np.float32) / 16).astype(np.float32)
    pos = (np.arange(256, dtype=np.float32) * 16 + 31)
    ang = pos[:, None] * inv[None, :]
    cosc = np.ascontiguousarray(np.cos(ang).astype(np.float32).reshape(2, 128, 8).transpose(1, 0, 2))
    sinc = np.ascontiguousarray(np.sin(ang).astype(np.float32).reshape(2, 128, 8).transpose(1, 0, 2))
    w1r = np.ascontiguousarray(inp['nsa_cmp_w1'].reshape(2, 2, 32, 64, 128).transpose(0, 1, 3, 2, 4)).astype(np.float32)
    posT = np.ascontiguousarray(inp['nsa_cmp_pos'].transpose(0, 1, 3, 2)).astype(np.float32)
    return dict(ovaug=ovaug, wc=wc, eall=eall, addmask=addmask, cosc=cosc, sinc=sinc, w1r=w1r, posT=posT,
                w2=inp['nsa_cmp_w2'].astype(np.float32), kng=inp['nsa_kn_g'].astype(np.float32))

NSA_SHAPES = [('ovaug', [128, 2, 72], BF16), ('wc', [128, 3200], BF16), ('eall', [64, 4096], BF16), ('addmask', [128, 32, 64], F32),
              ('cosc', [128, 2, 8], F32), ('sinc', [128, 2, 8], F32), ('w1r', [2, 2, 64, 32, 128], F32), ('posT', [2, 2, 64, 32], F32),
              ('w2', [2, 2, 128, 64], F32), ('kng', [2, 64], F32)]


import ml_dtypes
from concourse.bass_utils import run_bass_kernel_spmd

_IN_SHAPES = [('x', [T, D], F32), ('win', [2, D, WCOLS], F32), ('g1', [2, 128, 8], F32), ('gq', [2, 1280], F32),
              ('cos', [128, 32, 8], F32), ('sin', [128, 32, 8], F32), ('ident', [128, 128], BF16), ('mc', [128, 128], BF16),
              ('mu', [128, 128], BF16), ('selneg', [24, 1536], F32), ('mh', [128, 128], BF16), ('bones', [128, 128], BF16),
              ('lbl', [2, 128, 2], F32), ('og', [2, 128, 1], F32), ('fb', [2, 4, 1], F32), ('wo', [2, 1024, 1024], F32),
              ('wup', [2, 1024, 4096], F32), ('wdn', [2, 4096, 1024], F32), ('g2', [2, 128, 8], F32)] + NSA_SHAPES

KDEPTH = int(os.environ.get('KDEPTH', '2'))
KPHASES = os.environ.get('KPHASES', '1hnfwf')


def _body(P):
    nc = P.nc
    A = {}
    for k_, shp, dt_ in _IN_SHAPES:
        A[k_] = nc.dram_tensor(k_, shp, dt_, kind="ExternalInput").ap()
    A['y'] = nc.dram_tensor("y", [T, D], F32, kind="ExternalOutput").ap()
    A['qkT'] = nc.dram_tensor("qkT", [1280, T], BF16).ap()
    A['vtok'] = nc.dram_tensor("vtok", [T, 768], BF16).ap()
    A['pT'] = nc.dram_tensor("pT", [TC, T], F32).ap()
    A['mixT'] = nc.dram_tensor("mixT", [1024, T], BF16).ap()
    A['h2T'] = nc.dram_tensor("h2T", [1024, T], BF16).ap()
    A['gsig'] = nc.dram_tensor("gsig", [24, T], F32).ap()
    xm = nc.dram_tensor("xmid", [T, D], F32).ap()
    x1 = nc.dram_tensor("x1", [T, D], F32).ap()
    xin = A['x']
    for layer in range(KDEPTH):
        A['x'] = xin
        with ExitStack() as st:
            phase1(P, st, A, layer)
        with ExitStack() as st:
            consts = load_consts(P, st, A)
            if 'h' in KPHASES:
                phase_hgrn(P, A, layer, consts)
            if 'n' in KPHASES:
                phase_nsa(P, A, layer, consts)
            if 'f' in KPHASES:
                phase_fox(P, A, layer, consts)
            xout = x1 if layer < KDEPTH - 1 else A['y']
            phase_wo_ffn(P, A, layer, consts, xin, xm, xout)
        xin = xout


def _host_inputs(inp):
    cos, sin = rope_tables()
    gq = np.concatenate([np.tile(inp['nsa_qn_g'], (1, 8)), np.tile(inp['nsa_kn_g'], (1, 4)), np.tile(inp['fox_qn_g'], (1, 4)),
                         np.tile(inp['fox_kn_g'], (1, 4))], axis=1).astype(np.float32)
    base = {"win": win_layout(inp['w_in']), "g1": np.ascontiguousarray(inp['norm1_g'].reshape(2, 8, 128).transpose(0, 2, 1)),
            "g2": np.ascontiguousarray(inp['norm2_g'].reshape(2, 8, 128).transpose(0, 2, 1)),
            "gq": gq, "cos": cos, "sin": sin, "fb": inp['fox_fb'].reshape(2, 4, 1).astype(np.float32),
            "wo": inp['w_o'], "wup": inp['w_up'], "wdn": inp['w_down']}
    base.update(const_inputs()); base.update(hgrn_consts(inp)); base.update(nsa_consts(inp))
    return base


def kernel(**inp):
    inp = {k: np.asarray(v) for k, v in inp.items()}
    nc, plan = build_two_pass(lambda: bass.Bass("TRN2", target_bir_lowering=False), _body)
    base = _host_inputs(inp)
    in_maps = []
    for b in range(8):
        m = dict(base); m['x'] = np.ascontiguousarray(inp['x'][b]); in_maps.append(m)
    res = run_bass_kernel_spmd(nc, in_maps, core_ids=list(range(8)))
    return np.stack([r['y'] for r in res.results], axis=0).astype(np.float32)
```

```python
import numpy as np, sys, time, os, math
import numpy as np
from contextlib import ExitStack
import concourse.bass as bass
import concourse.mybir as mybir

F32 = mybir.dt.float32
BF16 = mybir.dt.bfloat16
AF = mybir.ActivationFunctionType
ALU = mybir.AluOpType
AX = mybir.AxisListType


def _box(ap):
    t = ap.tensor
    dims = ap.ap
    off = int(ap.offset)
    shp = tuple(t.shape)
    rowsize = 1
    for s in shp[1:]:
        rowsize *= int(s)
    r0 = off // rowsize
    f0 = off % rowsize
    rows = 0
    free = 0
    for (st, cnt) in dims:
        st = int(st); cnt = int(cnt)
        if cnt <= 1 or st == 0:
            continue
        if st % rowsize == 0:
            rows += (st // rowsize) * (cnt - 1)
        else:
            free += st * (cnt - 1)
    return t.name, (r0, r0 + rows, f0, f0 + free)


def _ov(a, b):
    return a[0] <= b[1] and b[0] <= a[1] and a[2] <= b[3] and b[2] <= a[3]


def _cont(a, b):
    return a[0] <= b[0] and b[1] <= a[1] and a[2] <= b[2] and b[3] <= a[3]


class Prog:
    def __init__(self, nc, plan=None):
        self.nc = nc
        self.plan = plan
        self.rec = plan is None
        self.eng = dict(pe=nc.tensor, dve=nc.vector, act=nc.scalar, pool=nc.gpsimd, sp=nc.sync)
        self.n = 0
        self.ins = []
        self.track = {}
        self.lane_cnt = {}
        self.freed = {}
        self.uid = 0
        self.stack = ExitStack()
        self.psum_rr = 0
        self.psum_banks = []
        if not self.rec:
            self.sem = {}
            for e in ['pe', 'dve', 'act', 'pool']:
                self.sem[e] = self.stack.enter_context(nc.semaphore("sem_" + e))
            self.lane_sem = {}
            for ln in plan['lanes']:
                self.lane_sem[ln] = self.stack.enter_context(nc.semaphore("ln_" + ln))

    def sb(self, st, name, shape, dtype):
        self.uid += 1
        name = "%s_%d" % (name, self.uid)
        t = st.enter_context(self.nc.sbuf_tensor("s_" + name, list(shape), dtype))
        st.callback(self._free, "s_" + name)
        return t

    def ps(self, st, name, shape, dtype=F32):
        self.uid += 1
        name = "%s_%d" % (name, self.uid)
        t = st.enter_context(self.nc.psum_tensor("p_" + name, list(shape), dtype))
        st.callback(self._free, "p_" + name)
        return t

    def _free(self, name):
        if not self.rec:
            return
        recs = self.track.pop(name, [])
        for (b, i, w) in recs:
            r = self.ins[i]
            key = ('l', r['lane'], i) if r['dma'] else ('e', r['eng'])
            if r['dma']:
                self.freed[key] = i
            else:
                self.freed[key] = max(self.freed.get(key, -1), i)

    def _access(self, idx, eng, dma, ap, write, deps):
        name, box = _box(ap)
        if name not in self.track:
            big = (0, 10 ** 9, 0, 10 ** 9)
            kind = ap.space
            self.track[name] = [] if str(kind) == 'DRAM' else [(big, i, True) for i in sorted(set(self.freed.values()))]
        recs = self.track[name]
        for (b, i, w) in recs:
            if (write or w) and _ov(b, box):
                deps.append((i, (w and not write)))
        if write:
            recs[:] = [r for r in recs if not _cont(box, r[0])]
        elif not dma:
            recs[:] = [r for r in recs if not ((not r[2]) and r[1] < len(self.ins) and self.ins[r[1]]['eng'] == eng
                                               and not self.ins[r[1]]['dma'] and _cont(box, r[0]))]
        recs.append((box, idx, write))

    def op(self, eng, fn, reads=(), writes=(), dma=False, lane=None):
        idx = self.n
        self.n += 1
        if self.rec:
            deps = []
            for ap in reads:
                self._access(idx, eng, dma, ap, False, deps)
            for ap in writes:
                self._access(idx, eng, dma, ap, True, deps)
            lanewaits = {}
            d2 = {}
            for (j, raw) in deps:
                if j == idx:
                    continue
                pj = self.ins[j]
                if pj['dma']:
                    ln = pj['lane']
                    lanewaits[ln] = max(lanewaits.get(ln, 0), pj['lane_val_at'])
                    lanewaits[ln] = max(lanewaits[ln], self.lane_cnt[ln])
                    continue
                if pj['eng'] == eng and not dma:
                    if eng == 'pe':
                        continue
                    if not raw and eng != 'pool':
                        continue
                d2[j] = True
            rec = dict(eng=eng, deps=list(d2.keys()), lanewaits=lanewaits, dma=dma, lane=lane)
            if dma:
                self.lane_cnt[lane] = self.lane_cnt.get(lane, 0) + 16
                rec['lane_val_at'] = self.lane_cnt[lane]
            self.ins.append(rec)
            return None
        else:
            info = self.plan['ins'][idx]
            e = self.eng[eng]
            for (sname, val) in info['waits']:
                s = self.sem[sname[1]] if sname[0] == 'e' else self.lane_sem[sname[1]]
                e.wait_ge(s, val)
            inst = fn(e)
            if dma:
                inst.then_inc(self.lane_sem[lane], 16)
            elif info['signal']:
                inst.then_inc(self.sem[eng], 1)
            return inst

    def make_plan(self):
        ins = self.ins
        signal = [False] * len(ins)
        for r in ins:
            for j in r['deps']:
                signal[j] = True
        cnt = dict(pe=0, dve=0, act=0, pool=0, sp=0)
        sigval = [0] * len(ins)
        for i, r in enumerate(ins):
            if signal[i] and not r['dma']:
                cnt[r['eng']] += 1
                sigval[i] = cnt[r['eng']]
        seen = {e: {} for e in cnt}
        out = []
        for i, r in enumerate(ins):
            need = {}
            for j in r['deps']:
                k = ('e', ins[j]['eng'])
                need[k] = max(need.get(k, 0), sigval[j])
            for ln, v in r['lanewaits'].items():
                k = ('l', ln)
                need[k] = max(need.get(k, 0), v)
            waits = []
            sd = seen[r['eng']]
            for k, v in need.items():
                if sd.get(k, 0) >= v:
                    continue
                sd[k] = v
                waits.append((k, v))
            out.append(dict(waits=waits, signal=signal[i]))
        return dict(ins=out, lanes=sorted(self.lane_cnt.keys()), lane_final=dict(self.lane_cnt))

    def finish(self):
        if self.rec:
            return
        for ln, v in self.plan['lane_final'].items():
            self.nc.sync.wait_ge(self.lane_sem[ln], v)

    def dma(self, out, in_, lane, q='sp', **kw):
        return self.op(q, lambda e: e.dma_start(out=out, in_=in_, **kw), reads=[in_], writes=[out],
                       dma=True, lane=lane)

    def mm(self, out, lhsT, rhs, start=True, stop=True, **kw):
        return self.op('pe', lambda e: e.matmul(out, lhsT, rhs, start=start, stop=stop, **kw),
                       reads=[lhsT, rhs], writes=[out])

    def transpose(self, out, in_, ident):
        return self.op('pe', lambda e: e.transpose(out, in_, ident), reads=[in_, ident], writes=[out])

    def act(self, out, in_, func, bias=None, scale=None, accum_out=None, eng='act'):
        reads = [in_]
        kw = {}
        if bias is not None:
            kw['bias'] = bias
            if not isinstance(bias, (int, float)):
                reads.append(bias)
        if scale is not None:
            kw['scale'] = scale
            if not isinstance(scale, (int, float)):
                reads.append(scale)
        writes = [out]
        if accum_out is not None:
            kw['accum_out'] = accum_out
            writes.append(accum_out)
        return self.op(eng, lambda e: e.activation(out=out, in_=in_, func=func, **kw), reads=reads, writes=writes)

    def tt(self, eng, out, in0, in1, op):
        return self.op(eng, lambda e: e.tensor_tensor(out=out, in0=in0, in1=in1, op=op), reads=[in0, in1], writes=[out])

    def ts(self, eng, out, in0, s1, s2, op0, op1=None, accum_out=None):
        reads = [in0]
        if not isinstance(s1, (int, float)):
            reads.append(s1)
        if s2 is not None and not isinstance(s2, (int, float)):
            reads.append(s2)
        kw = {}
        writes = [out]
        if op1 is not None:
            kw['op1'] = op1
        if accum_out is not None:
            kw['accum_out'] = accum_out
            writes.append(accum_out)
        return self.op(eng, lambda e: e.tensor_scalar(out=out, in0=in0, scalar1=s1, scalar2=s2, op0=op0, **kw),
                       reads=reads, writes=writes)

    def stt(self, eng, out, in0, scalar, in1, op0, op1):
        reads = [in0, in1]
        if not isinstance(scalar, (int, float)):
            reads.append(scalar)
        return self.op(eng, lambda e: e.scalar_tensor_tensor(out=out, in0=in0, scalar=scalar, in1=in1, op0=op0, op1=op1),
                       reads=reads, writes=[out])

    def copy(self, eng, out, in_):
        if eng == 'act':
            return self.op(eng, lambda e: e.copy(out=out, in_=in_), reads=[in_], writes=[out])
        return self.op(eng, lambda e: e.tensor_copy(out=out, in_=in_), reads=[in_], writes=[out])

    def memset(self, eng, ap, val):
        return self.op(eng, lambda e: e.memset(ap, val), reads=[], writes=[ap])

    def scan(self, out, d0, d1, initial, op0, op1):
        reads = [d0, d1]
        if not isinstance(initial, (int, float)):
            reads.append(initial)
        return self.op('dve', lambda e: e.tensor_tensor_scan(out=out, data0=d0, data1=d1, initial=initial, op0=op0, op1=op1),
                       reads=reads, writes=[out])

    def generic(self, eng, fn, reads, writes):
        return self.op(eng, fn, reads=reads, writes=writes)


def build_two_pass(make_nc, body):
    nc1 = make_nc()
    p1 = Prog(nc1, None)
    body(p1)
    p1.stack.close()
    plan = p1.make_plan()
    nc2 = make_nc()
    p2 = Prog(nc2, plan)
    body(p2)
    p2.finish()
    p2.stack.close()
    return nc2, plan


T = 4096
NT = 32
D = 1024
KC = 8
TOKC = 2048
TC = 1152
WCOLS = TOKC + TC
EPS = 1e-6


def phase1(P, st, A, layer):
    nc = P.nc
    s = ExitStack()
    W = P.sb(s, "w_in", [128, KC, WCOLS], BF16)
    hT = P.sb(s, "hT", [128, KC, T], BF16)
    ident = P.sb(s, "ident", [128, 128], BF16)
    g1 = P.sb(s, "g1", [128, KC], F32)
    G = P.sb(s, "Gq", [128, 1280], F32)
    cos = P.sb(s, "cos", [128, NT, 8], F32)
    sin = P.sb(s, "sin", [128, NT, 8], F32)
    P.dma(ident[:], A['ident'], 'c0')
    P.dma(g1[:], A['g1'][layer], 'c0')
    P.dma(G[:], A['gq'][layer].partition_broadcast(128), 'c0')
    P.dma(cos[:], A['cos'], 'c0')
    P.dma(sin[:], A['sin'], 'c0')
    P.ts('dve', G[:, 0:512], G[:, 0:512], 0.125, None, ALU.mult)
    P.ts('dve', G[:, 768:1024], G[:, 768:1024], 0.125, None, ALU.mult)

    with ExitStack() as s2:
        wst = [P.sb(s2, "wst%d" % i, [128, WCOLS], F32) for i in range(4)]
        for kc in range(KC):
            b = wst[kc % 4]
            P.dma(b[:], A['win'][layer, kc * 128:(kc + 1) * 128, :], 'wst%d' % (kc % 4), q='sp' if kc % 2 == 0 else 'act')
            half = WCOLS // 2
            P.ts('dve', W[:, kc, 0:half], b[:, 0:half], g1[:, kc:kc + 1], None, ALU.mult)
            P.act(W[:, kc, half:WCOLS], b[:, half:WCOLS], AF.Copy, scale=g1[:, kc:kc + 1])

    with ExitStack() as s2:
        xt = [P.sb(s2, "xt%d" % i, [128, D], F32) for i in range(2)]
        sq = P.sb(s2, "sqj", [128, D], F32)
        hb = [P.sb(s2, "hb%d" % i, [128, D], BF16) for i in range(2)]
        ss = [P.sb(s2, "ss%d" % i, [128, 2], F32) for i in range(2)]
        ptr = [P.ps(s2, "ptr%d" % i, [128, KC, 128], BF16) for i in range(2)]
        for t in range(NT):
            b = t % 2
            P.dma(xt[b][:], A['x'][t * 128:(t + 1) * 128, :], 'xt%d' % b)
            P.act(sq[:], xt[b][:], AF.Square, accum_out=ss[b][:, 0:1])
            P.act(ss[b][:, 1:2], ss[b][:, 0:1], AF.Sqrt, bias=EPS_AP(P), scale=1.0 / D)
            P.op('dve', lambda e, o=ss[b][:, 1:2]: e.reciprocal(out=o, in_=o), reads=[ss[b][:, 1:2]], writes=[ss[b][:, 1:2]])
            P.ts('dve', hb[b][:], xt[b][:], ss[b][:, 1:2], None, ALU.mult)
            for kc in range(KC):
                P.transpose(ptr[b][:, kc, :], hb[b][:, kc * 128:(kc + 1) * 128], ident[:])
            P.copy('act' if t % 2 else 'dve', hT[:, :, t * 128:(t + 1) * 128], ptr[b][:])

    with ExitStack() as s2:
        pp = [P.ps(s2, "ppT%d" % i, [128, 512], F32) for i in range(3)]
        so = [P.sb(s2, "soT%d" % i, [128, 512], F32) for i in range(3)]
        k = 0
        for c in range(TC // 128):
            for j in range(T // 512):
                b = k % 3
                for kc in range(KC):
                    P.mm(pp[b][:], W[:, kc, TOKC + c * 128:TOKC + (c + 1) * 128], hT[:, kc, j * 512:(j + 1) * 512],
                         start=(kc == 0), stop=(kc == KC - 1))
                P.copy('act' if k % 2 else 'dve', so[b][:], pp[b][:])
                P.dma(A['pT'][c * 128:(c + 1) * 128, j * 512:(j + 1) * 512], so[b][:], 'soT%d' % b, q='pool')
                k += 1

    with ExitStack() as s2:
        pg = [P.ps(s2, "pg%d" % i, [128, 512], F32) for i in range(4)]
        ptq = [P.ps(s2, "ptq%d" % i, [128, 4, 128], BF16) for i in range(3)]
        sqhs = [P.sb(s2, "sqh%d" % i, [128, 512], F32) for i in range(3)]
        ssh = [P.sb(s2, "ssh%d" % i, [128, 8], F32) for i in range(4)]
        xn = [P.sb(s2, "xn%d" % i, [128, 512], F32) for i in range(3)]
        qb = [P.sb(s2, "qb%d" % i, [128, 512], BF16) for i in range(3)]
        rts = [P.sb(s2, "rt%d" % i, [128, 4, 8, 8], F32) for i in range(3)]
        qst = [P.sb(s2, "qst%d" % i, [128, 10, 512], BF16) for i in range(2)]
        vst = [P.sb(s2, "vst%d" % i, [128, 768], BF16) for i in range(2)]
        groups = []
        kq = 0
        for t in range(NT):
            for gi in range(4):
                k = t * 4 + gi
                nh = [8, 8, 4, 0][gi]
                qi = None
                if nh:
                    qi = kq % 3
                    kq += 1
                groups.append((t, gi, k, qi))

        def stage(sidx, t, gi, k, q):
            sb_ = (t // 4) % 2
            b = k % 4
            nh = [8, 8, 4, 0][gi]
            nr = [8, 4, 0, 0][gi]
            w = nh * 64
            goff = [0, 512, 1024, 0][gi]
            vb = t % 2
            if sidx == 0:
                for kc in range(KC):
                    P.mm(pg[b][:], hT[:, kc, t * 128:(t + 1) * 128], W[:, kc, gi * 512:(gi + 1) * 512],
                         start=(kc == 0), stop=(kc == KC - 1))
                return
            if sidx == 1:
                if nh:
                    sqh = sqhs[q]
                    P.act(sqh[:, 0:w], pg[b][:, 0:w], AF.Square)
                    P.op('dve', lambda e, o=ssh[b][:, 0:nh], i=sqh[:, 0:w].rearrange("p (h d) -> p h d", d=64):
                         e.tensor_reduce(out=o, in_=i, axis=AX.X, op=ALU.add),
                         reads=[sqh[:, 0:w]], writes=[ssh[b][:, 0:nh]])
                if gi == 2:
                    P.copy('act', vst[vb][:, 0:256], pg[b][:, 256:512])
                if gi == 3:
                    P.copy('act', vst[vb][:, 256:768], pg[b][:, 0:512])
                    P.dma(A['vtok'][t * 128:(t + 1) * 128, :], vst[vb][:], 'vst%d' % vb, q='sp')
                return
            if not nh:
                return
            rt = rts[q]
            xv = xn[q][:, 0:max(nr, 1) * 64].rearrange("p (h d) -> p h d", d=64)
            qv = qb[q][:, 0:max(nr, 1) * 64].rearrange("p (h d) -> p h d", d=64)
            if sidx == 2:
                P.act(ssh[b][:, 0:nh], ssh[b][:, 0:nh], AF.Sqrt, bias=EPS_AP(P), scale=1.0 / 64)
                P.op('dve', lambda e, o=ssh[b][:, 0:nh]: e.reciprocal(out=o, in_=o), reads=[ssh[b][:, 0:nh]], writes=[ssh[b][:, 0:nh]])
                P.tt('dve', xn[q][:, 0:w].rearrange("p (h d) -> p h d", d=64),
                     pg[b][:, 0:w].rearrange("p (h d) -> p h d", d=64),
                     ssh[b][:, 0:nh].unsqueeze(2).broadcast_to([128, nh, 64]), ALU.mult)
            elif sidx == 3:
                if nr:
                    P.tt('pool', xn[q][:, 0:w], xn[q][:, 0:w], G[:, goff:goff + w], ALU.mult)
                    P.copy('act', qb[q][:, 0:w], xn[q][:, 0:w])
                    cb = cos[:, t, :].unsqueeze(1).broadcast_to([128, nr, 8])
                    sb2 = sin[:, t, :].unsqueeze(1).broadcast_to([128, nr, 8])
                    P.tt('dve', rt[:, 0, 0:nr, :], xv[:, :, 0:8], cb, ALU.mult)
                    P.tt('dve', rt[:, 1, 0:nr, :], xv[:, :, 8:16], sb2, ALU.mult)
                    P.tt('pool', rt[:, 2, 0:nr, :], xv[:, :, 8:16], cb, ALU.mult)
                    P.tt('pool', rt[:, 3, 0:nr, :], xv[:, :, 0:8], sb2, ALU.mult)
                else:
                    P.tt('pool', qb[q][:, 0:w], xn[q][:, 0:w], G[:, goff:goff + w], ALU.mult)
            elif sidx == 4:
                if nr:
                    P.tt('dve', qv[:, :, 0:8], rt[:, 0, 0:nr, :], rt[:, 1, 0:nr, :], ALU.subtract)
                    P.tt('pool', qv[:, :, 8:16], rt[:, 2, 0:nr, :], rt[:, 3, 0:nr, :], ALU.add)
                npair = nh // 2
                for pr in range(npair):
                    P.transpose(ptq[q][:, pr, :], qb[q][:, pr * 128:(pr + 1) * 128], ident[:])
            elif sidx == 5:
                npair = nh // 2
                pbase = [0, 4, 8][gi]
                P.copy('act' if gi % 2 else 'dve', qst[sb_][:, pbase:pbase + npair, (t % 4) * 128:(t % 4 + 1) * 128], ptq[q][:, 0:npair, :])
                if t % 4 == 3 and gi == 2:
                    j = t // 4
                    P.dma(A['qkT'][:, j * 512:(j + 1) * 512].rearrange("(a p) n -> p a n", p=128), qst[sb_][:], 'qst%d' % sb_, q='sp')

        NS = 6
        for step in range(len(groups) + NS - 1):
            for sidx in range(NS - 1, -1, -1):
                gidx = step - sidx
                if 0 <= gidx < len(groups):
                    stage(sidx, *groups[gidx])
    s.close()


_eps_cache = {}


def EPS_AP(P):
    return EPS


T = 4096
NEG = -30000.0


class AttnCtx:
    def __init__(self, P, st, consts):
        self.P = P
        self.psS = [P.ps(st, "aS%d" % i, [128, 1024], F32) for i in range(2)]
        self.psO = [P.ps(st, "aO%d" % i, [128, 512], F32) for i in range(2)]
        self.psB = [P.ps(st, "aB%d" % i, [128, 512], F32) for i in range(1)]
        self.pT = [P.sb(st, "apT%d" % i, [128, 1024], BF16) for i in range(3)]
        self.lr = [P.sb(st, "alr%d" % i, [65, 512], F32) for i in range(2)]
        self.F = [P.sb(st, "aF%d" % i, [64, 512], F32) for i in range(2)]
        self.lrh = [P.sb(st, "alrh%d" % i, [128, 512], BF16) for i in range(2)]
        self.lrl = [P.sb(st, "alrl%d" % i, [128, 512], BF16) for i in range(2)]
        self.G2 = [P.sb(st, "aG%d" % i, [64, 512], F32) for i in range(2)]
        for t_ in self.lrh + self.lrl:
            P.memset('pool', t_[:], 0.0)
        self.kF_ids = {id(g): i for i, g in enumerate(self.G2)}
        self.kS = 0
        self.kO = 0
        self.kF = 0
        self.vm = 65
        self.prev = None
        self.deferred = []
        self.c = consts


def _push_block(cx, s_fn, exp_fn, pv_fn, first=False):
    if first:
        for f in cx.deferred:
            f()
        cx.deferred = []
    s_fn()
    d = cx.deferred
    cx.deferred = []
    if cx.prev is not None:
        e, p, epi = cx.prev
        e()
        p()
        if epi is not None:
            epi[0]()
            cx.deferred.append(epi[1])
    for f in d:
        f()
    cx.prev = (exp_fn, pv_fn, None)


def _end_chunk(cx, epi_a, epi_b):
    cx.prev = (cx.prev[0], cx.prev[1], (epi_a, epi_b))


def attn_flush(cx):
    d = cx.deferred
    cx.deferred = []
    if cx.prev is not None:
        e, p, epi = cx.prev
        e()
        p()
        if epi is not None:
            epi[0]()
            d.append(epi[1])
        cx.prev = None
    for f in d:
        f()


def _mk_block(cx, po, Kaug, kr, Qaug, q0, Vaug, kt, lo, hi, masks, extra, first, last):
    P = cx.P
    c = cx.c
    ps = cx.psS[cx.kS % 2]
    pt = cx.pT[cx.kS % 3]
    cx.kS += 1

    def s_fn():
        P.mm(ps[:, lo:hi], Kaug[0:kr, kt * 128:(kt + 1) * 128], Qaug[0:kr, q0 + lo:q0 + hi], start=True, stop=(len(masks) == 0 and extra is None))
        if extra is not None:
            P.mm(ps[:, lo:hi], extra[0][0:64, kt * 128:(kt + 1) * 128], extra[1][0:64, q0 + lo:q0 + hi], start=False, stop=(len(masks) == 0))
        for mi, (mk, m) in enumerate(masks):
            P.mm(ps[:, m * 128:(m + 1) * 128], c['ident'][:], mk[:], start=False, stop=(mi == len(masks) - 1))

    def exp_fn():
        P.act(pt[:, lo:hi], ps[:, lo:hi], AF.Exp)

    def pv_fn():
        P.mm(po[0:cx.vm, lo:hi], Vaug[:, kt, 0:cx.vm], pt[:, lo:hi], start=first, stop=last)

    return s_fn, exp_fn, pv_fn


def _mk_pair(cx, po, Kaug, kr, Qaug, q0, Vaug, ea, eb, first, last):
    P = cx.P
    c = cx.c
    ps = cx.psS[cx.kS % 2]
    pt = cx.pT[cx.kS % 3]
    cx.kS += 1
    (kta, loa, hia, ma), (ktb, lob, hib, mb) = ea, eb

    def s_fn():
        for o, (kt, lo, hi, masks) in ((0, ea), (512, eb)):
            P.mm(ps[:, o + lo:o + hi], Kaug[0:kr, kt * 128:(kt + 1) * 128], Qaug[0:kr, q0 + lo:q0 + hi], start=True, stop=(len(masks) == 0))
            for mi, (mk, m) in enumerate(masks):
                P.mm(ps[:, o + m * 128:o + (m + 1) * 128], c['ident'][:], mk[:], start=False, stop=(mi == len(masks) - 1))

    def exp_fn():
        P.act(pt[:, loa:512 + hib], ps[:, loa:512 + hib], AF.Exp)

    def pv_fn():
        P.mm(po[0:cx.vm, loa:hia], Vaug[:, kta, 0:cx.vm], pt[:, loa:hia], start=first, stop=False)
        P.mm(po[0:cx.vm, lob:hib], Vaug[:, ktb, 0:cx.vm], pt[:, 512 + lob:512 + hib], start=False, stop=last)

    return s_fn, exp_fn, pv_fn


def _mk_factor(cx, po, lng2, gate_c, q0, finish):
    P = cx.P
    c = cx.c
    lr = cx.lr[cx.kF % 2]
    F = cx.F[cx.kF % 2]
    G2 = cx.G2[cx.kF % 2]
    cx.kF += 1
    pb = cx.psB[0]

    lrh = cx.lrh[(cx.kF - 1) % 2]
    lrl = cx.lrl[(cx.kF - 1) % 2]

    def epi_a():
        if lng2 is not None:
            P.dma(G2[:], lng2[gate_c, q0:q0 + 512].partition_broadcast(64), 'ag%d' % ((cx.kF_ids[id(G2)])), q='sp')
        P.ts('dve', lr[64:65, :], po[64:65, :], 1e-18, None, ALU.max)
        P.act(lr[64:65, :], lr[64:65, :], AF.Ln)
        P.copy('dve', lrh[64:65, :], lr[64:65, :])
        P.tt('dve', lrl[64:65, :], lr[64:65, :], lrh[64:65, :], ALU.subtract)

    def epi_b():
        P.mm(pb[:, :], c['negonesb'][:, :], lrh[:, :], start=True, stop=False)
        P.mm(pb[:, :], c['negonesb'][:, :], lrl[:, :], start=False, stop=True)
        P.act(F[:], pb[0:64, :], AF.Exp)
        if lng2 is not None:
            P.tt('pool', F[:], F[:], G2[:], ALU.mult)
        finish(po, F)

    return epi_a, epi_b


def attn_chunk(cx, Kaug, kr, Qaug, j, Vaug, entries, finish, lng2=None, gate_c=None, extra=None):
    po = cx.psO[cx.kO % 2]
    cx.kO += 1
    q0 = j * 512
    n = len(entries)
    ei = 0
    while ei < n:
        if extra is None and ei + 1 < n:
            fns = _mk_pair(cx, po, Kaug, kr, Qaug, q0, Vaug, entries[ei], entries[ei + 1], ei == 0, ei + 1 == n - 1)
            _push_block(cx, *fns, first=(ei == 0))
            ei += 2
            continue
        kt, lo, hi, masks = entries[ei]
        fns = _mk_block(cx, po, Kaug, kr, Qaug, q0, Vaug, kt, lo, hi, masks, extra, ei == 0, ei == n - 1)
        _push_block(cx, *fns, first=(ei == 0))
        ei += 1
    ea, eb = _mk_factor(cx, po, lng2, gate_c, q0, finish)
    _end_chunk(cx, ea, eb)


def causal_entries(j, mc):
    ent = []
    for kt in range(4 * j + 4):
        if kt < 4 * j:
            ent.append((kt, 0, 512, []))
        else:
            m = kt - 4 * j
            ent.append((kt, 128 * m, 512, [(mc, m)]))
    return ent


def window_entries(j, mc, mu):
    ent = []
    for cc in range(-4, 4):
        kt = 4 * j + cc
        if kt < 0:
            continue
        lo = 128 * max(cc, 0)
        hi = 128 * (min(cc + 4, 3) + 1)
        masks = []
        if 0 <= cc <= 3:
            masks.append((mc, cc))
        if 0 <= cc + 4 <= 3:
            masks.append((mu, cc + 4))
        ent.append((kt, lo, hi, masks))
    return ent


def load_consts(P, st, A):
    c = {}
    c['ident'] = P.sb(st, "c_ident", [128, 128], BF16)
    c['mc'] = P.sb(st, "c_mc", [128, 128], BF16)
    c['mu'] = P.sb(st, "c_mu", [128, 128], BF16)
    c['zeros'] = P.sb(st, "c_zeros", [128, 128], BF16)
    c['ident_w'] = P.sb(st, "c_identw", [128, 512], BF16)
    c['negones'] = P.sb(st, "c_negones", [65, 64], F32)
    c['negonesb'] = P.sb(st, "c_negonesb", [128, 128], BF16)
    P.dma(c['ident'][:], A['ident'], 'c0')
    P.dma(c['mc'][:], A['mc'], 'c0')
    P.dma(c['mu'][:], A['mu'], 'c0')
    P.memset('dve', c['zeros'][:], 0.0)
    P.memset('dve', c['ident_w'][:], 0.0)
    P.memset('dve', c['negones'][:], -1.0)
    P.memset('dve', c['negonesb'][:], -1.0)
    return c


def phase_fox(P, A, layer, consts):
    with ExitStack() as st:
        cx = AttnCtx(P, st, consts)
        cf = P.sb(st, "f_cf", [4, T], F32)
        tmp = P.sb(st, "f_tmp", [4, T], F32)
        ones = P.sb(st, "f_ones", [4, T], F32)
        fb = P.sb(st, "f_fb", [4, 2], F32)
        cs = P.sb(st, "f_cs", [4, 3, T], BF16)
        ncs = P.sb(st, "f_ncs", [4, 3, T], BF16)
        P.dma(cf[:], A['pT'][1048:1052, :], 'fx0')
        P.dma(fb[:, 0:1], A['fb'][layer], 'fx0')
        P.ts('dve', fb[:, 1:2], fb[:, 0:1], -1.0, None, ALU.mult)
        P.memset('pool', ones[:], 1.0)
        P.act(tmp[:], cf[:], AF.Exp, bias=fb[:, 1:2], scale=-1.0)
        P.act(tmp[:], tmp[:], AF.Ln, bias=1.0)
        P.scan(cf[:], ones[:], tmp[:], 0.0, ALU.mult, ALU.subtract)
        P.copy('dve', cs[:, 0, :], cf[:])
        P.tt('dve', tmp[:], cf[:], cs[:, 0, :], ALU.subtract)
        P.copy('dve', cs[:, 1, :], tmp[:])
        P.tt('dve', tmp[:], tmp[:], cs[:, 1, :], ALU.subtract)
        P.copy('dve', cs[:, 2, :], tmp[:])
        P.ts('dve', ncs[:].rearrange("p a t -> p (a t)"), cs[:].rearrange("p a t -> p (a t)"), -1.0, None, ALU.mult)
        Qs = [P.sb(st, "f_Q%d" % i, [128, T], BF16) for i in range(2)]
        Ks = [P.sb(st, "f_K%d" % i, [128, T], BF16) for i in range(2)]
        Vs = [P.sb(st, "f_V%d" % i, [128, 32, 65], BF16) for i in range(2)]
        ob = [P.sb(st, "f_ob%d" % i, [64, 512], BF16) for i in range(2)]
        for i in range(2):
            P.memset('pool', Qs[i][64:128, :], 0.0)
            P.memset('pool', Ks[i][64:128, :], 0.0)
            P.memset('pool', Qs[i][64:70, :], 1.0)
            P.memset('pool', Ks[i][64:70, :], 1.0)
            P.memset('pool', Vs[i][:, :, 64:65], 1.0)

        def load(h):
            Q = Qs[h % 2]; K = Ks[h % 2]; V = Vs[h % 2]
            P.dma(Q[0:64, :], A['qkT'][768 + 64 * h:768 + 64 * (h + 1), :], 'fxq%d' % (h % 2))
            P.dma(K[0:64, :], A['qkT'][1024 + 64 * h:1024 + 64 * (h + 1), :], 'fxk%d' % (h % 2), q='act')
            for i in range(3):
                P.dma(Q[64 + i:65 + i, :], cs[h:h + 1, i, :], 'fxq%d' % (h % 2))
                P.dma(K[67 + i:68 + i, :], ncs[h:h + 1, i, :], 'fxk%d' % (h % 2), q='act')
            P.dma(V[:, :, 0:64], A['vtok'][:, 512 + 64 * h:512 + 64 * (h + 1)].rearrange("(n p) d -> p n d", p=128), 'fxv%d' % (h % 2))

        load(0)
        for h in range(4):
            if h + 1 < 4:
                load(h + 1)
            Q = Qs[h % 2]; K = Ks[h % 2]; V = Vs[h % 2]
            for j in range(8):
                def fin(po, F, o=ob[j % 2], j=j, h=h):
                    P.tt('dve', o[:], po[0:64, :], F[:], ALU.mult)
                    P.dma(A['mixT'][768 + 64 * h:768 + 64 * (h + 1), j * 512:(j + 1) * 512], o[:], 'fxo%d' % (j % 2), q='sp')
                attn_chunk(cx, K, 128, Q, j, V, causal_entries(j, consts['mc']), fin)
            attn_flush(cx)

import math, os
STAGE = int(os.environ.get('STAGE', '99'))

T = 4096
LN8 = math.log(0.125)


def phase_hgrn(P, A, layer, consts):
    for ct in range(2):
        with ExitStack() as st:
            B = [P.sb(st, "hB%d" % i, [128, T], F32) for i in range(5)]
            qt = P.sb(st, "h_qt", [128, T], BF16)
            kt = P.sb(st, "h_kt", [128, 2, T], BF16)
            qg = P.sb(st, "h_qg", [128, T], BF16)
            kd = P.sb(st, "h_kd", [128, T], BF16)
            kdt = P.sb(st, "h_kdt", [128, 32, 2, 128], BF16)
            Vt = P.sb(st, "h_Vt", [128, 32, 128], BF16)
            Vz = P.sb(st, "h_Vz", [128, 32, 2, 128], BF16)
            Sbd = P.sb(st, "h_Sbd", [128, 64, 128], BF16)
            rst = P.sb(st, "h_rst", [128, T], BF16)
            sm = P.sb(st, "h_sm", [128, 8], F32)
            dl = P.sb(st, "h_dl", [128, 64], F32)
            mh = P.sb(st, "h_mh", [128, 128], BF16)
            bones = P.sb(st, "h_bones", [128, 128], BF16)
            ident = consts['ident']
            P.dma(mh[:], A['mh'], 'hg0')
            P.dma(bones[:], A['bones'], 'hg0')
            P.dma(sm[:, 0:2], A['lbl'][ct], 'hg0')
            P.dma(sm[:, 4:5], A['og'][layer], 'hg0')
            P.dma(B[0][:], A['pT'][256 + ct * 128:256 + (ct + 1) * 128, :], 'hgz')
            P.dma(B[3][:], A['pT'][ct * 128:(ct + 1) * 128, :], 'hgq', q='act')
            P.dma(Vt[:], A['vtok'][:, ct * 128:(ct + 1) * 128].rearrange("(n p) d -> p n d", p=128), 'hgv', q='pool')
            P.memset('pool', Vz[:], 0.0)
            for hh in range(2):
                P.dma(Vz[:, :, hh, hh * 64:(hh + 1) * 64],
                      A['vtok'][:, ct * 128 + hh * 64:ct * 128 + (hh + 1) * 64].rearrange("(n p) d -> p n d", p=128), 'hgv', q='pool')
            P.memset('pool', Sbd[:], 0.0)
            P.memset('pool', kt[:], 0.0)
            P.memset('pool', kdt[:], 0.0)
            P.memset('pool', rst[:], 1.0)
            P.memset('pool', rst[:].rearrange("p (c s) -> p c s", s=64)[:, :, 0:1], 0.0)
            lb = sm[:, 2:3]; oml = sm[:, 3:4]; noml = sm[:, 5:6]
            if layer == 0:
                P.memset('dve', lb, 0.0)
            else:
                P.act(sm[:, 0:2], sm[:, 0:2], AF.Exp)
                P.tt('dve', sm[:, 6:7], sm[:, 0:1], sm[:, 1:2], ALU.add)
                P.op('dve', lambda e, o=sm[:, 6:7]: e.reciprocal(out=o, in_=o), reads=[sm[:, 6:7]], writes=[sm[:, 6:7]])
                P.tt('dve', lb, sm[:, 1:2], sm[:, 6:7], ALU.mult)
            P.ts('dve', oml, lb, -1.0, 1.0, ALU.mult, ALU.add)
            P.ts('dve', noml, oml, -1.0, None, ALU.mult)
            P.act(B[0][:], B[0][:], AF.Sigmoid)
            P.ts('dve', B[1][:], B[0][:], oml, lb, ALU.mult, ALU.add)
            P.act(B[1][:], B[1][:], AF.Ln)
            P.scan(B[2][:], rst[:], B[1][:], 0.0, ALU.mult, ALU.add)
            P.ts('dve', B[1][:], B[0][:], noml, oml, ALU.mult, ALU.add)
            G3 = B[2][:].rearrange("p (c s) -> p c s", s=64)
            D3 = B[0][:].rearrange("p (c s) -> p c s", s=64)
            P.tt('dve', D3, G3, G3[:, :, 31:32].broadcast_to([128, 64, 64]), ALU.subtract)
            P.act(B[4][:], B[0][:], AF.Exp, bias=LN8)
            P.tt('dve', qt[:], B[3][:], B[4][:], ALU.mult)
            P.act(B[4][:], B[0][:], AF.Exp, scale=-1.0)
            P.tt('dve', kt[0:64, 0, :], B[1][0:64, :], B[4][0:64, :], ALU.mult)
            P.tt('dve', kt[64:128, 1, :], B[1][64:128, :], B[4][64:128, :], ALU.mult)
            P.act(B[4][:], B[2][:], AF.Exp, bias=LN8)
            P.tt('dve', qg[:], B[3][:], B[4][:], ALU.mult)
            P.tt('dve', D3, G3[:, :, 63:64].broadcast_to([128, 64, 64]), G3, ALU.subtract)
            P.act(B[4][:], B[0][:], AF.Exp)
            P.tt('dve', kd[:], B[1][:], B[4][:], ALU.mult)
            P.act(dl[:].unsqueeze(2), G3[:, :, 63:64], AF.Exp)
            P.memset('dve', dl[:, 0:1], 0.0)
            KV = B[0]; dfull = B[1]; Sall = B[3]; oT = B[4]
            if STAGE < 1:
                P.dma(A['mixT'][0:128, 0:T], kd[:], 'dbg'); continue
            with ExitStack() as s2:
                ptr = [P.ps(s2, "h_ptr%d" % i, [128, 8, 128], BF16) for i in range(2)]
                for g in range(4):
                    for i in range(8):
                        tl = g * 8 + i
                        P.transpose(ptr[g % 2][:, i, :], kd[:, tl * 128:(tl + 1) * 128], ident[:])
                    P.copy('act', kdt[0:64, g * 8:(g + 1) * 8, 0, :], ptr[g % 2][0:64, :, :])
                    P.copy('dve', kdt[64:128, g * 8:(g + 1) * 8, 1, :], ptr[g % 2][64:128, :, :])
            with ExitStack() as s2:
                pkv = [P.ps(s2, "h_pkv%d" % i, [128, 4, 128], F32) for i in range(2)]
                KV3 = KV[:].rearrange("p (v c) -> p v c", c=64)
                for g in range(16):
                    pk = pkv[g % 2]
                    for i in range(4):
                        c = g * 4 + i
                        tl = c // 2; hf = c % 2
                        P.mm(pk[:, i, :], kdt[:, tl, hf, :], Vt[:, tl, :], start=True, stop=True)
                    for hh in range(2):
                        P.copy('act' if hh else 'dve', KV3[hh * 64:(hh + 1) * 64, :, g * 4:(g + 1) * 4],
                               pk[hh * 64:(hh + 1) * 64, :, hh * 64:(hh + 1) * 64].rearrange("p g v -> p v g"))
            if STAGE < 2:
                P.dma(A['mixT'][0:128, 0:T], kd[:], 'dbg'); continue
            P.copy('pool', dfull[:].rearrange("p (v c) -> p v c", c=64), dl[:].unsqueeze(1).broadcast_to([128, 64, 64]))
            P.scan(Sall[:], dfull[:], KV[:], 0.0, ALU.mult, ALU.add)
            S3 = Sall[:].rearrange("p (v c) -> p v c", c=64)
            for hh in range(2):
                P.copy('dve' if hh else 'act', Sbd[hh * 64:(hh + 1) * 64, 1:64, hh * 64:(hh + 1) * 64],
                       S3[hh * 64:(hh + 1) * 64, :, 0:63].rearrange("p v c -> p c v"))
            if STAGE < 3:
                P.dma(A['mixT'][0:128, 0:T], kd[:], 'dbg'); continue
            with ExitStack() as s2:
                pA = [P.ps(s2, "h_pA%d" % i, [128, 128], F32) for i in range(4)]
                po = [P.ps(s2, "h_po%d" % i, [128, 128], F32) for i in range(2)]
                Am = [P.sb(s2, "h_Am%d" % i, [128, 128], BF16) for i in range(4)]
                def scores(tl):
                    cols = slice(tl * 128, (tl + 1) * 128)
                    for hh in range(2):
                        i = (tl % 2) * 2 + hh
                        P.mm(pA[i][:], kt[:, hh, cols], qt[:, cols], start=True, stop=True)
                        P.tt('dve', Am[i][:], pA[i][:], mh[:], ALU.mult)

                def outs(tl):
                    cols = slice(tl * 128, (tl + 1) * 128)
                    p_ = po[tl % 2]
                    P.mm(p_[:], Vz[:, tl, 0, :], Am[(tl % 2) * 2][:], start=True, stop=False)
                    P.mm(p_[:], Vz[:, tl, 1, :], Am[(tl % 2) * 2 + 1][:], start=False, stop=False)
                    P.mm(p_[:, 0:64], Sbd[:, 2 * tl, :], qg[:, tl * 128:tl * 128 + 64], start=False, stop=False)
                    P.mm(p_[:, 64:128], Sbd[:, 2 * tl + 1, :], qg[:, tl * 128 + 64:tl * 128 + 128], start=False, stop=True)
                    P.copy('act', oT[:, cols], p_[:])

                for tl in range(33):
                    if tl < 32:
                        scores(tl)
                    if tl >= 1:
                        outs(tl - 1)
            if STAGE < 4:
                P.dma(A['mixT'][0:128, 0:T], kd[:], 'dbg'); continue
            with ExitStack() as s2:
                pss = [P.ps(s2, "h_pss%d" % i, [128, 512], F32) for i in range(2)]
                sq = [P.sb(s2, "h_sq%d" % i, [128, 512], BF16) for i in range(2)]
                rs = [P.sb(s2, "h_rs%d" % i, [128, 512], F32) for i in range(2)]
                ag = [P.sb(s2, "h_ag%d" % i, [128, 512], F32) for i in range(2)]
                ob = [P.sb(s2, "h_ob%d" % i, [128, 512], BF16) for i in range(2)]
                for j in range(8):
                    b = j % 2
                    cols = slice(j * 512, (j + 1) * 512)
                    P.dma(ag[b][:], A['pT'][512 + ct * 128:512 + (ct + 1) * 128, cols], 'hga%d' % b)
                    P.act(sq[b][:], oT[:, cols], AF.Square)
                    P.mm(pss[b][:], bones[:], sq[b][:], start=True, stop=True)
                    P.act(rs[b][:], pss[b][:], AF.Sqrt, bias=1e-6, scale=1.0 / 64)
                    P.op('dve', lambda e, o=rs[b][:]: e.reciprocal(out=o, in_=o), reads=[rs[b][:]], writes=[rs[b][:]])
                    P.act(ag[b][:], ag[b][:], AF.Silu)
                    P.stt('dve', rs[b][:], oT[:, cols], sm[:, 4:5], rs[b][:], ALU.mult, ALU.mult)
                    P.tt('pool', ob[b][:], rs[b][:], ag[b][:], ALU.mult)
                    P.dma(A['mixT'][ct * 128:(ct + 1) * 128, cols], ob[b][:], 'hgo%d' % b, q='pool')

import os
STAGE = int(os.environ.get('STAGE', '99'))

T = 4096
NEG = -30000.0


def phase_nsa(P, A, layer, consts):
    c = consts
    ident = c['ident']
    with ExitStack() as st:
        cx = AttnCtx(P, st, consts)
        with ExitStack() as s0:
            lng = P.sb(s0, "n_lng", [24, T], F32)
            P.dma(lng[:], A['pT'][1024:1048, :], 'ns0')
            P.act(lng[:], lng[:], AF.Sigmoid)
            P.dma(A['gsig'], lng[:], 'ns0')
        lng2 = A['gsig']
        ovaug = P.sb(st, "n_ov", [128, 2, 72], BF16)
        wc = P.sb(st, "n_wc", [128, 3200], BF16)
        addm = P.sb(st, "n_addm", [128, 32, 64], F32)
        P.dma(ovaug[:], A['ovaug'], 'ns0')
        P.dma(wc[:], A['wc'], 'ns0')
        P.dma(addm[:], A['addmask'], 'ns0')
        kcTs = [P.sb(st, "n_kcT%d" % i, [128, 256], BF16) for i in range(2)]
        vcAs = [P.sb(st, "n_vcA%d" % i, [128, 2, 65], BF16) for i in range(2)]
        for g in range(2):
            kcT = kcTs[g]; vcA = vcAs[g]
            with ExitStack() as s2:
                w1 = P.sb(s2, "n_w1", [64, 32, 128], BF16)
                w1f = P.sb(s2, "n_w1f", [64, 32, 128], F32)
                w2 = P.sb(s2, "n_w2", [128, 64], BF16)
                w2f = P.sb(s2, "n_w2f", [128, 64], F32)
                posT = P.sb(s2, "n_posT", [64, 32], BF16)
                posf = P.sb(s2, "n_posf", [64, 32], F32)
                posb = P.sb(s2, "n_posb", [64, 32, 256], BF16)
                srcf = P.sb(s2, "n_srcf", [64, T], F32)
                srcb = P.sb(s2, "n_srcb", [64, T], BF16)
                bias = P.sb(s2, "n_bias", [128, 1], F32)
                xb = P.sb(s2, "n_xb", [128, 256], F32)
                x2 = P.sb(s2, "n_x2", [128, 256], F32)
                hid = P.sb(s2, "n_hid", [128, 256], BF16)
                ktm = P.sb(s2, "n_ktm", [128, 64], F32)
                kts = P.sb(s2, "n_kts", [128, 64], F32)
                ktb = P.sb(s2, "n_ktb", [128, 128], BF16)
                sm = P.sb(s2, "n_sm", [128, 4], F32)
                rt = P.sb(s2, "n_rt", [128, 4, 8], F32)
                kng = P.sb(s2, "n_kng", [128, 64], F32)
                cosc = P.sb(s2, "n_cosc", [128, 2, 8], F32)
                sinc = P.sb(s2, "n_sinc", [128, 2, 8], F32)
                ph = cx.psS[0]; pb = cx.psS[1]; po = cx.psO[0]
                pt = P.ps(s2, "n_pt", [128, 128], BF16)
                P.dma(kng[:], A['kng'][layer].partition_broadcast(128), 'ns1')
                P.dma(cosc[:], A['cosc'], 'ns1')
                P.dma(sinc[:], A['sinc'], 'ns1')
                P.memset('dve', hid[:], 0.0)
                P.memset('dve', vcA[:], 0.0)
                P.memset('dve', kcT[:], 0.0)
                P.memset('dve', ktb[:], 0.0)
                for which in range(2):
                    P.dma(w1f[:], A['w1r'][layer, which], 'ns2')
                    P.dma(w2f[:], A['w2'][layer, which], 'ns2')
                    P.dma(posf[:], A['posT'][layer, which], 'ns2')
                    P.dma(srcf[:], A['pT'][768 + 128 * which + 64 * g:768 + 128 * which + 64 * (g + 1), :], 'ns3', q='act')
                    P.copy('dve', w1[:], w1f[:])
                    P.copy('dve', w2[:], w2f[:])
                    P.copy('dve', posT[:], posf[:])
                    P.copy('dve', posb[:], posT[:].unsqueeze(2).broadcast_to([64, 32, 256]))
                    P.copy('act', srcb[:], srcf[:])
                    for l in range(32):
                        P.mm(ph[:, 0:255], w1[:, l, :], srcb[:].rearrange("p (n s) -> p n s", s=16)[:, (l // 16):(l // 16) + 255, l % 16], start=(l == 0), stop=False)
                    for l in range(32):
                        P.mm(ph[:, 0:255], w1[:, l, :], posb[:, l, 0:255], start=False, stop=(l == 31))
                    P.copy('act', xb[:, 0:255], ph[:, 0:255])
                    P.tt('dve', x2[:, 0:255], xb[:, 0:255], xb[:, 0:255], ALU.mult)
                    P.ts('dve', x2[:, 0:255], x2[:, 0:255], 0.044715, 1.0, ALU.mult, ALU.add)
                    P.tt('dve', x2[:, 0:255], x2[:, 0:255], xb[:, 0:255], ALU.mult)
                    P.act(x2[:, 0:255], x2[:, 0:255], AF.Sigmoid, scale=1.5957691216057308)
                    P.tt('dve', hid[:, 0:255], x2[:, 0:255], xb[:, 0:255], ALU.mult)
                    for nt in range(2):
                        P.mm(po[:, 0:64], hid[:, nt * 128:(nt + 1) * 128], w2[:], start=True, stop=True)
                        if which == 1:
                            nr = 128 if nt == 0 else 127
                            P.copy('act', vcA[0:nr, nt, 0:64], po[0:nr, 0:64])
                            P.memset('dve', vcA[0:nr, nt, 64:65], 1.0)
                        else:
                            P.act(kts[:], po[:, 0:64], AF.Square, accum_out=sm[:, 0:1])
                            P.act(sm[:, 1:2], sm[:, 0:1], AF.Sqrt, bias=1e-6, scale=1.0 / 64)
                            P.op('dve', lambda e, o=sm[:, 1:2]: e.reciprocal(out=o, in_=o), reads=[sm[:, 1:2]], writes=[sm[:, 1:2]])
                            P.stt('dve', ktm[:], po[:, 0:64], sm[:, 1:2], kng[:], ALU.mult, ALU.mult)
                            P.copy('act', ktb[:, 0:64], ktm[:])
                            P.tt('dve', rt[:, 0, :], ktm[:, 0:8], cosc[:, nt, :], ALU.mult)
                            P.tt('dve', rt[:, 1, :], ktm[:, 8:16], sinc[:, nt, :], ALU.mult)
                            P.tt('dve', rt[:, 2, :], ktm[:, 8:16], cosc[:, nt, :], ALU.mult)
                            P.tt('dve', rt[:, 3, :], ktm[:, 0:8], sinc[:, nt, :], ALU.mult)
                            P.tt('dve', ktb[:, 0:8], rt[:, 0, :], rt[:, 1, :], ALU.subtract)
                            P.tt('dve', ktb[:, 8:16], rt[:, 2, :], rt[:, 3, :], ALU.add)
                            P.transpose(pt[:], ktb[:], ident[:])
                            P.copy('dve', kcT[0:64, nt * 128:(nt + 1) * 128], pt[0:64, :])
            P.memset('dve', kcT[0:64, 255:256], 0.0)
        selT = P.sb(st, "n_selT", [128, T], BF16)
        imp = P.sb(st, "n_imp", [128, 32, 64], F32)
        acc = [P.sb(st, "n_acc%d" % i, [64, T], F32) for i in range(4)]
        Q = [P.sb(st, "n_Q%d" % i, [128, T], BF16) for i in range(4)]
        Ks = P.sb(st, "n_Ks", [128, T], BF16)
        Kw = P.sb(st, "n_Kw", [128, T], BF16)
        Vs = P.sb(st, "n_Vs", [128, 32, 65], BF16)
        Vw = P.sb(st, "n_Vw", [128, 32, 65], BF16)
        for g in range(2):
            kcT = kcTs[g]; vcA = vcAs[g]
            for hh in range(4):
                h = 4 * g + hh
                P.memset('pool', Q[hh][64:128, :], 0.0)
                P.dma(Q[hh][0:64, :], A['qkT'][64 * h:64 * (h + 1), :], 'nsq%d' % hh)
            P.dma(Ks[0:64, :], A['qkT'][512 + 64 * g:512 + 64 * (g + 1), :], 'nsk')
            P.dma(Ks[64:128, :], A['eall'], 'nsk')
            P.dma(Kw[0:64, :], A['qkT'][640 + 64 * g:640 + 64 * (g + 1), :], 'nsk')
            P.memset('pool', Kw[64:128, :], 0.0)
            P.memset('pool', Vs[:, :, 64:65], 1.0)
            P.memset('pool', Vw[:, :, 64:65], 1.0)
            P.dma(Vs[:, :, 0:64], A['vtok'][:, 256 + 64 * g:256 + 64 * (g + 1)].rearrange("(n p) d -> p n d", p=128), 'nsv', q='act')
            P.dma(Vw[:, :, 0:64], A['vtok'][:, 384 + 64 * g:384 + 64 * (g + 1)].rearrange("(n p) d -> p n d", p=128), 'nsv', q='act')
            with ExitStack() as s2:
                pimp = [cx.psS[i][:, 512:800].rearrange("p (a b) -> p a b", b=72) for i in range(2)]
                pTc = [P.sb(s2, "n_pTc%d" % i, [128, 512], BF16) for i in range(3)]
                rinvs = [P.sb(s2, "n_rinv%d" % i, [128, 4], F32) for i in range(2)]
                kc_ = 0
                kch = 0
                for hh in range(4):
                    h = 4 * g + hh
                    for j in range(8):
                        q0 = j * 512
                        po = cx.psO[cx.kO % 2]
                        cx.kO += 1
                        pim = pimp[kch % 2]
                        rinv = rinvs[kch % 2]
                        kch += 1
                        tiles = []
                        for nt in range(2):
                            off = 2048 * nt + 31 - 512 * j
                            if -off + 511 < 0:
                                continue
                            tiles.append((nt, off))
                        for ti, (nt, off) in enumerate(tiles):
                            ps = cx.psS[cx.kS % 2][:, 0:512]
                            cx.kS += 1
                            ptc = pTc[kc_ % 3]
                            kc_ += 1
                            first = (ti == 0)
                            last = (ti == len(tiles) - 1)

                            def s_fn(ps=ps, nt=nt, off=off, first=first, po=po, pim=pim, hh=hh, q0=q0):
                                full = (-off >= 2032)
                                P.mm(ps[:], kcT[:, nt * 128:(nt + 1) * 128], Q[hh][:, q0:q0 + 512], start=True, stop=full)
                                if not full:
                                    ci0 = -off + 511
                                    P.mm(ps[:], ident[:], wc[:, ci0:ci0 + 512], start=False, stop=True)

                            def exp_fn(ps=ps, ptc=ptc):
                                P.act(ptc[:], ps[:], AF.Exp)

                            def pv_fn(po=po, pim=pim, ptc=ptc, nt=nt, last=last, first=first):
                                P.mm(po[0:65, :], vcA[:, nt, :], ptc[:], start=first, stop=last)
                                for m in range(4):
                                    P.mm(pim[:, m, :], ptc[:, m * 128:(m + 1) * 128], ovaug[:, nt, :], start=(first and m == 0), stop=last)

                            _push_block(cx, s_fn, exp_fn, pv_fn, first=first)

                        def fin(po_, F, hh=hh, q0=q0, pim=pim, rinv=rinv, j=j):
                            for m in range(4):
                                tq = j * 4 + m
                                if hh == 0:
                                    P.ts('dve', imp[:, tq, :], pim[:, m, 0:64], rinv[:, m:m + 1], None, ALU.mult)
                                else:
                                    P.stt('dve', imp[:, tq, :], pim[:, m, 0:64], rinv[:, m:m + 1], imp[:, tq, :], ALU.mult, ALU.add)
                            P.tt('dve', acc[hh][:, q0:q0 + 512], po_[0:64, :], F[:], ALU.mult)

                        ea, eb = _mk_factor(cx, po, lng2, h, q0, fin)

                        def ea2(ea=ea, pim=pim, rinv=rinv):
                            ea()
                            P.ts('dve', rinv[:, 0:4].unsqueeze(2), pim[:, :, 64:65], 1e-30, None, ALU.max)
                            P.op('dve', lambda e, o=rinv[:, 0:4]: e.reciprocal(out=o, in_=o), reads=[rinv[:, 0:4]], writes=[rinv[:, 0:4]])

                        _end_chunk(cx, ea2, eb)
                attn_flush(cx)
            if STAGE < 2:
                continue
            with ExitStack() as s2:
                wk = [P.sb(s2, "n_wk%d" % i, [128, 64], F32) for i in range(2)]
                w2_ = [P.sb(s2, "n_wk2%d" % i, [128, 64], F32) for i in range(2)]
                m8 = [P.sb(s2, "n_m8%d" % i, [128, 16], F32) for i in range(2)]
                sb_ = [P.sb(s2, "n_sb%d" % i, [128, 128], BF16) for i in range(2)]
                pts = [P.ps(s2, "n_pts%d" % i, [128, 128], BF16) for i in range(1)] * 2
                P.memset('pool', sb_[0][:], 0.0)
                P.memset('pool', sb_[1][:], 0.0)
                def steps(tq):
                    b = tq % 2
                    return [
                        lambda: P.tt('dve', wk[b][:], imp[:, tq, :], addm[:, tq, :], ALU.add),
                        lambda: P.op('dve', lambda e, o=m8[b][:, 0:8], i=wk[b][:]: e.max(out=o, in_=i), reads=[wk[b][:]], writes=[m8[b][:, 0:8]]),
                        lambda: P.op('dve', lambda e, o=w2_[b][:], r=m8[b][:, 0:8], i=wk[b][:]: e.match_replace(out=o, in_to_replace=r, in_values=i, imm_value=-3.0e38),
                                     reads=[m8[b][:, 0:8], wk[b][:]], writes=[w2_[b][:]]),
                        lambda: P.op('dve', lambda e, o=m8[b][:, 8:16], i=w2_[b][:]: e.max(out=o, in_=i), reads=[w2_[b][:]], writes=[m8[b][:, 8:16]]),
                        lambda: P.ts('dve', w2_[b][:], wk[b][:], m8[b][:, 15:16], None, ALU.is_ge),
                        lambda: P.ts('dve', wk[b][:], wk[b][:], -5.0e29, None, ALU.is_gt),
                        lambda: P.tt('dve', wk[b][:], wk[b][:], w2_[b][:], ALU.mult),
                        lambda: P.ts('dve', sb_[b][:, 64:128], wk[b][:], -1.0, -NEG, ALU.add, ALU.mult),
                        lambda: P.transpose(pts[b][:], sb_[b][:], ident[:]),
                        lambda: P.copy('act', selT[64:128, tq * 128:(tq + 1) * 128], pts[b][64:128, :]),
                    ]

                for tq in range(0, 32, 2):
                    sa, sb2 = steps(tq), steps(tq + 1)
                    for fa, fb in zip(sa[:8], sb2[:8]):
                        fa()
                        fb()
                    for f_ in sa[8:] + sb2[8:]:
                        f_()
            for hh in range(4):
                P.dma(Q[hh][64:128, :], selT[64:128, :], 'nsq%d' % hh, q='sp' if hh % 2 else 'act')
            if STAGE < 3:
                continue
            with ExitStack() as s2:
                tmp = [P.sb(s2, "n_tmp%d" % i, [64, 512], F32) for i in range(2)]
                ob = [P.sb(s2, "n_ob%d" % i, [64, 512], BF16) for i in range(1)] * 2
                for hh in range(4):
                    h = 4 * g + hh
                    for j in range(8):
                        q0 = j * 512
                        a = acc[hh][:, q0:q0 + 512]

                        def fin_s(po_, F, a=a):
                            P.tt('dve', tmp[0][:], po_[0:64, :], F[:], ALU.mult)
                            P.tt('pool', a, a, tmp[0][:], ALU.add)

                        def fin_w(po_, F, a=a, j=j, h=h, q0=q0):
                            P.tt('dve', tmp[1][:], po_[0:64, :], F[:], ALU.mult)
                            P.tt('pool', ob[j % 2][:], a, tmp[1][:], ALU.add)
                            if STAGE >= 5:
                                P.dma(A['mixT'][256 + 64 * h:256 + 64 * (h + 1), q0:q0 + 512], ob[j % 2][:], 'nso%d' % (j % 2), q='sp')

                        attn_chunk(cx, Ks, 128, Q[hh], j, Vs, causal_entries(j, c['mc']), fin_s, lng2=lng2, gate_c=8 + h)
                        if STAGE >= 4:
                            attn_chunk(cx, Kw, 128, Q[hh], j, Vw, window_entries(j, c['mc'], c['mu']), fin_w, lng2=lng2, gate_c=16 + h)
                attn_flush(cx)


T = 4096
D = 1024
FF = 4096


def phase_wo(P, A, layer, consts, x_in, x_mid, per_tile=None):
    ident = consts['ident']
    with ExitStack() as st:
        Wo = P.sb(st, "wo", [128, 8, D], BF16)
        with ExitStack() as s2:
            wst = [P.sb(s2, "wost%d" % i, [128, D], F32) for i in range(2)]
            for kc in range(8):
                b = wst[kc % 2]
                P.dma(b[:], A['wo'][layer, kc * 128:(kc + 1) * 128, :], 'wost%d' % (kc % 2))
                P.copy('act' if kc % 2 else 'dve', Wo[:, kc, :], b[:])
        mx = [P.sb(st, "wo_mx%d" % i, [128, 8, 512], BF16) for i in range(2)]
        xt = [P.sb(st, "wo_xt%d" % i, [128, D], F32) for i in range(2)]
        xm = [P.sb(st, "wo_xm%d" % i, [128, D], F32) for i in range(2)]
        sq = P.sb(st, "wo_sq", [128, D], BF16)
        hb = [P.sb(st, "wo_hb%d" % i, [128, D], BF16) for i in range(2)]
        ss = [P.sb(st, "wo_ss%d" % i, [128, 2], F32) for i in range(2)]
        hst = [P.sb(st, "wo_hst%d" % i, [128, 8, 512], BF16) for i in range(1)] * 2
        po = [P.ps(st, "wo_po%d" % i, [128, 512], F32) for i in range(4)]
        ptr = [P.ps(st, "wo_ptr%d" % i, [128, 8, 128], BF16) for i in range(2)]
        def part_a(t):
            j = t // 4
            b = t % 2
            if t % 4 == 0:
                P.dma(mx[j % 2][:], A['mixT'][:, j * 512:(j + 1) * 512].rearrange("(a p) n -> p a n", p=128), 'womx%d' % (j % 2))
            P.dma(xt[b][:], x_in[t * 128:(t + 1) * 128, :], 'woxt%d' % b, q='act')
            for half in range(2):
                pp = po[(t % 2) * 2 + half]
                for kc in range(8):
                    P.mm(pp[:], mx[j % 2][:, kc, (t % 4) * 128:(t % 4 + 1) * 128], Wo[:, kc, half * 512:(half + 1) * 512],
                         start=(kc == 0), stop=(kc == 7))
                P.tt('dve', xm[b][:, half * 512:(half + 1) * 512], pp[:], xt[b][:, half * 512:(half + 1) * 512], ALU.add)
            P.dma(x_mid[t * 128:(t + 1) * 128, :], xm[b][:], 'woxm%d' % b, q='pool')
            P.act(sq[:], xm[b][:], AF.Square, accum_out=ss[b][:, 0:1])
            P.act(ss[b][:, 1:2], ss[b][:, 0:1], AF.Sqrt, bias=1e-6, scale=1.0 / D)
            P.op('dve', lambda e, o=ss[b][:, 1:2]: e.reciprocal(out=o, in_=o), reads=[ss[b][:, 1:2]], writes=[ss[b][:, 1:2]])
            P.ts('dve', hb[b][:], xm[b][:], ss[b][:, 1:2], None, ALU.mult)

        def part_b(t):
            j = t // 4
            b = t % 2
            for kc in range(8):
                P.transpose(ptr[b][:, kc, :], hb[b][:, kc * 128:(kc + 1) * 128], ident[:])
            P.copy('act', hst[j % 2][:, :, (t % 4) * 128:(t % 4 + 1) * 128], ptr[b][:])
            if t % 4 == 3:
                P.dma(A['h2T'][:, j * 512:(j + 1) * 512].rearrange("(a p) n -> p a n", p=128), hst[j % 2][:], 'wohst%d' % (j % 2), q='pool')
            if per_tile is not None:
                per_tile(t)

        for t in range(33):
            if t < 32:
                part_a(t)
            if t >= 1:
                part_b(t - 1)


def phase_wo_ffn(P, A, layer, consts, x_in, x_mid, x_out):
    with ExitStack() as st:
        Wu = P.sb(st, "wu", [128, 8, FF], BF16)
        Wd = P.sb(st, "wd", [128, 32, D], BF16)
        g2 = P.sb(st, "g2", [128, 8], F32)
        P.dma(g2[:], A['g2'][layer], 'ff0')
        with ExitStack() as s2:
            wst = [P.sb(s2, "fwst%d" % i, [128, 1024], F32) for i in range(2)]

            def per_tile(t):
                for u in range(2):
                    ci = 2 * t + u
                    b = u
                    if ci < 32:
                        kc, qt = ci // 4, ci % 4
                        P.dma(wst[b][:], A['wup'][layer, kc * 128:(kc + 1) * 128, qt * 1024:(qt + 1) * 1024], 'fwst%d' % b, q='sp')
                        if u:
                            P.ts('dve', Wu[:, kc, qt * 1024:(qt + 1) * 1024], wst[b][:], g2[:, kc:kc + 1], None, ALU.mult)
                        else:
                            P.act(Wu[:, kc, qt * 1024:(qt + 1) * 1024], wst[b][:], AF.Copy, scale=g2[:, kc:kc + 1])
                    else:
                        fc = ci - 32
                        P.dma(wst[b][:], A['wdn'][layer, fc * 128:(fc + 1) * 128, :], 'fwst%d' % b, q='sp')
                        P.copy('dve' if u else 'act', Wd[:, fc, :], wst[b][:])

            phase_wo(P, A, layer, consts, x_in, x_mid, per_tile=per_tile)
        _ffn_body(P, st, A, Wu, Wd, x_mid, x_out)


def phase_ffn(P, A, layer, consts, x_mid, x_out):
    with ExitStack() as st:
        Wu = P.sb(st, "wu", [128, 8, FF], BF16)
        Wd = P.sb(st, "wd", [128, 32, D], BF16)
        g2 = P.sb(st, "g2", [128, 8], F32)
        P.dma(g2[:], A['g2'][layer], 'ff0')
        with ExitStack() as s2:
            wst = [P.sb(s2, "fwst%d" % i, [128, 2048], F32) for i in range(3)]
            k = 0
            for kc in range(8):
                for hf in range(2):
                    b = k % 3
                    P.dma(wst[b][:], A['wup'][layer, kc * 128:(kc + 1) * 128, hf * 2048:(hf + 1) * 2048], 'fwst%d' % b, q='sp' if k % 2 else 'act')
                    if k % 2:
                        P.ts('dve', Wu[:, kc, hf * 2048:(hf + 1) * 2048], wst[b][:], g2[:, kc:kc + 1], None, ALU.mult)
                    else:
                        P.act(Wu[:, kc, hf * 2048:(hf + 1) * 2048], wst[b][:], AF.Copy, scale=g2[:, kc:kc + 1])
                    k += 1
            for fc2 in range(16):
                b = k % 3
                P.dma(wst[b][:].rearrange("p (a n) -> p a n", a=2), A['wdn'][layer, fc2 * 256:(fc2 + 1) * 256, :].rearrange("(a p) n -> p a n", p=128),
                      'fwst%d' % b, q='sp' if k % 2 else 'act')
                P.copy('dve' if k % 2 else 'act', Wd[:, fc2 * 2:(fc2 + 1) * 2, :], wst[b][:].rearrange("p (a n) -> p a n", a=2))
                k += 1
        _ffn_body(P, st, A, Wu, Wd, x_mid, x_out)


def _ffn_body(P, st, A, Wu, Wd, x_mid, x_out):
    if True:
        h2 = [P.sb(st, "ff_h2%d" % i, [128, 8, 512], BF16) for i in range(2)]
        uT = P.sb(st, "ff_uT", [128, 32, 512], BF16)
        rl = [P.sb(st, "ff_rl%d" % i, [128, 512], F32) for i in range(2)]
        xt = [P.sb(st, "ff_xt%d" % i, [128, D], F32) for i in range(2)]
        xo = [P.sb(st, "ff_xo%d" % i, [128, D], F32) for i in range(2)]
        pu = [P.ps(st, "ff_pu%d" % i, [128, 512], F32) for i in range(3)]
        pd = [P.ps(st, "ff_pd%d" % i, [128, 512], F32) for i in range(4)]
        ku = 0
        for j in range(8):
            P.dma(h2[j % 2][:], A['h2T'][:, j * 512:(j + 1) * 512].rearrange("(a p) n -> p a n", p=128), 'ffh2%d' % (j % 2))
            for fc in range(32):
                pp = pu[ku % 3]
                r = rl[ku % 2]
                for kc in range(8):
                    P.mm(pp[:], Wu[:, kc, fc * 128:(fc + 1) * 128], h2[j % 2][:, kc, :], start=(kc == 0), stop=(kc == 7))
                P.act(r[:], pp[:], AF.Relu)
                P.tt('dve' if ku % 2 else 'pool', uT[:, fc, :], r[:], r[:], ALU.mult)
                ku += 1
            for tt in range(4):
                t = j * 4 + tt
                b = t % 2
                P.dma(xt[b][:], x_mid[t * 128:(t + 1) * 128, :], 'ffxt%d' % b, q='act')
                for half in range(2):
                    pp = pd[(t % 2) * 2 + half]
                    for fc in range(32):
                        P.mm(pp[:], uT[:, fc, tt * 128:(tt + 1) * 128], Wd[:, fc, half * 512:(half + 1) * 512], start=(fc == 0), stop=(fc == 31))
                    P.tt('dve', xo[b][:, half * 512:(half + 1) * 512], pp[:], xt[b][:, half * 512:(half + 1) * 512], ALU.add)
                P.dma(x_out[t * 128:(t + 1) * 128, :], xo[b][:], 'ffxo%d' % b, q='pool')

import ml_dtypes
from concourse.bass_utils import run_bass_kernel_spmd

T=4096; D=1024
OFF = {}
_names = ['aq','af','ai','ag','bq','bkc','bvc','bks','bvs','bkw','bvw','bg','cq','ck','cv','cf']
_sizes = [256,256,256,256,512,128,128,128,128,128,128,24,256,256,256,4]
_o = 0
for n_, s_ in zip(_names, _sizes):
    OFF[n_] = (_o, _o + s_); _o += s_
TOK_ORDER = ['bq','bks','bkw','cq','ck','ai','bvs','bvw','cv']
T_ORDER = ['aq','af','ag','bkc','bvc','bg','cf']

def win_layout(w_in):
    L = w_in.shape[0]
    out = np.zeros((L, 1024, 2048 + 1152), np.float32)
    c = 0
    for n_ in TOK_ORDER:
        a, b = OFF[n_]; out[:, :, c:c + b - a] = w_in[:, :, a:b]; c += b - a
    assert c == 2048
    for n_ in T_ORDER:
        a, b = OFF[n_]; out[:, :, c:c + b - a] = w_in[:, :, a:b]; c += b - a
    return out

def rope_tables():
    inv = np.power(np.float32(500000.0), -np.arange(0, 16, 2, dtype=np.float32) / 16).astype(np.float32)
    pos = np.arange(T, dtype=np.float32)
    ang = pos[:, None] * inv[None, :]
    cos = np.cos(ang).astype(np.float32); sin = np.sin(ang).astype(np.float32)
    return (np.ascontiguousarray(cos.reshape(32, 128, 8).transpose(1, 0, 2)),
            np.ascontiguousarray(sin.reshape(32, 128, 8).transpose(1, 0, 2)))

def _skip():
    pass

def const_inputs():
    k = np.arange(128)[:, None]; q = np.arange(128)[None, :]
    mc = np.where(k <= q, 0.0, -30000.0).astype(ml_dtypes.bfloat16)
    mu = np.where(k > q, 0.0, -30000.0).astype(ml_dtypes.bfloat16)
    selneg = np.zeros((24, 24 * 64), np.float32)
    for c in range(24):
        selneg[c, c * 64:(c + 1) * 64] = -1.0
    return dict(ident=np.eye(128, dtype=ml_dtypes.bfloat16), mc=mc, mu=mu, selneg=selneg)

def _unused_ref_proj(inp, layer, x):
    x = x.astype(np.float64)
    h = x / np.sqrt((x * x).mean(-1, keepdims=True) + 1e-6) * inp['norm1_g'][layer]
    return h @ inp['w_in'][layer].astype(np.float64)

def hgrn_consts(inp):
    s = np.arange(128)[:, None]; t = np.arange(128)[None, :]
    mh = ((s // 64 == t // 64) & (s <= t)).astype(ml_dtypes.bfloat16)
    bones = (s // 64 == t // 64).astype(ml_dtypes.bfloat16)
    lbl = np.ascontiguousarray(inp['hgrn_lb_logits'].reshape(2, 2, 128).transpose(1, 2, 0)).astype(np.float32)
    og = np.tile(inp['hgrn_onorm_g'], (1, 2)).reshape(2, 128, 1).astype(np.float32)
    return dict(mh=mh, bones=bones, lbl=lbl, og=og)

def _unused_ref_hgrn(inp, layer, proj):
    def sl(n): a, b = OFF[n]; return proj[:, a:b]
    lbp = np.exp(inp['hgrn_lb_logits'].astype(np.float64)); lbp /= lbp.sum(0, keepdims=True)
    lb_all = np.cumsum(lbp, 0) - lbp[0:1]
    lb = lb_all[layer].reshape(4, 64)
    z = sl('af').reshape(T, 4, 64)
    sig = 1 / (1 + np.exp(-z))
    f = lb + (1 - lb) * sig; logf = np.log(f); k = (1 - lb) * (1 - sig)
    q = sl('aq').reshape(T, 4, 64) * 0.125; v = sl('ai').reshape(T, 4, 64)
    o = np.zeros((T, 4, 64))
    for h in range(4):
        S = np.zeros((64, 64))
        for c in range(64):
            r = slice(c * 64, (c + 1) * 64)
            G = np.cumsum(logf[r, h], 0)
            qc, kc, vc = q[r, h], k[r, h], v[r, h]
            o_inter = (qc * np.exp(G)) @ S
            diff = G[:, None, :] - G[None, :, :]
            mask = np.tril(np.ones((64, 64), bool))
            dec = np.where(mask[:, :, None], np.exp(np.minimum(diff, 0)), 0)
            sc = np.einsum('tk,sk,tsk->ts', qc, kc, dec)
            o[r, h] = o_inter + sc @ vc
            S = S * np.exp(G[-1])[:, None] + (kc * np.exp(G[-1] - G)).T @ vc
    g = sl('ag').reshape(T, 4, 64)
    gate = g / (1 + np.exp(-g))
    on = o / np.sqrt((o * o).mean(-1, keepdims=True) + 1e-6) * inp['hgrn_onorm_g'][layer]
    return (on * gate).reshape(T, 256)

def nsa_consts(inp):
    n_cmp = 255
    ci = np.arange(n_cmp)[:, None]; sj = np.arange(64)[None, :]
    ov = ((ci * 16 <= sj * 64 + 63) & (ci * 16 + 31 >= sj * 64)).astype(np.float32)
    ovaug = np.zeros((256, 72), np.float32); ovaug[:255, :64] = ov; ovaug[:255, 64] = 1.0
    ovaug = np.ascontiguousarray(ovaug.reshape(2, 128, 72).transpose(1, 0, 2)).astype(ml_dtypes.bfloat16)
    nl = np.arange(128)[:, None]; cc = np.arange(3200)[None, :] - 511
    wc = np.where(cc >= 16 * nl, 0.0, -30000.0).astype(ml_dtypes.bfloat16)
    eall = (np.arange(T)[None, :] // 64 == np.arange(64)[:, None]).astype(ml_dtypes.bfloat16)
    q = np.arange(T)[:, None]; j = np.arange(64)[None, :]; cur = q // 64
    am = np.zeros((T, 64), np.float32)
    am[(j == 0) | (j == cur) | (j == cur - 1)] = 1e30
    am[np.broadcast_to(j > cur, am.shape)] = -1e30
    addmask = np.ascontiguousarray(am.reshape(32, 128, 64).transpose(1, 0, 2))
    inv = np.power(np.float32(500000.0), -np.arange(0, 16, 2, dtype=np.float32) / 16).astype(np.float32)
    pos = (np.arange(256, dtype=np.float32) * 16 + 31)
    ang = pos[:, None] * inv[None, :]
    cosc = np.ascontiguousarray(np.cos(ang).astype(np.float32).reshape(2, 128, 8).transpose(1, 0, 2))
    sinc = np.ascontiguousarray(np.sin(ang).astype(np.float32).reshape(2, 128, 8).transpose(1, 0, 2))
    w1r = np.ascontiguousarray(inp['nsa_cmp_w1'].reshape(2, 2, 32, 64, 128).transpose(0, 1, 3, 2, 4)).astype(np.float32)
    posT = np.ascontiguousarray(inp['nsa_cmp_pos'].transpose(0, 1, 3, 2)).astype(np.float32)
    return dict(ovaug=ovaug, wc=wc, eall=eall, addmask=addmask, cosc=cosc, sinc=sinc, w1r=w1r, posT=posT,
                w2=inp['nsa_cmp_w2'].astype(np.float32), kng=inp['nsa_kn_g'].astype(np.float32))

NSA_SHAPES = [('ovaug', [128, 2, 72], BF16), ('wc', [128, 3200], BF16), ('eall', [64, 4096], BF16), ('addmask', [128, 32, 64], F32),
              ('cosc', [128, 2, 8], F32), ('sinc', [128, 2, 8], F32), ('w1r', [2, 2, 64, 32, 128], F32), ('posT', [2, 2, 64, 32], F32),
              ('w2', [2, 2, 128, 64], F32), ('kng', [2, 64], F32)]


import ml_dtypes
from concourse.bass_utils import run_bass_kernel_spmd

_IN_SHAPES = [('x', [T, D], F32), ('win', [2, D, WCOLS], F32), ('g1', [2, 128, 8], F32), ('gq', [2, 1280], F32),
              ('cos', [128, 32, 8], F32), ('sin', [128, 32, 8], F32), ('ident', [128, 128], BF16), ('mc', [128, 128], BF16),
              ('mu', [128, 128], BF16), ('selneg', [24, 1536], F32), ('mh', [128, 128], BF16), ('bones', [128, 128], BF16),
              ('lbl', [2, 128, 2], F32), ('og', [2, 128, 1], F32), ('fb', [2, 4, 1], F32), ('wo', [2, 1024, 1024], F32),
              ('wup', [2, 1024, 4096], F32), ('wdn', [2, 4096, 1024], F32), ('g2', [2, 128, 8], F32)] + NSA_SHAPES

KDEPTH = int(os.environ.get('KDEPTH', '2'))
KPHASES = os.environ.get('KPHASES', '1hnfwf')


def _body(P):
    nc = P.nc
    A = {}
    for k_, shp, dt_ in _IN_SHAPES:
        A[k_] = nc.dram_tensor(k_, shp, dt_, kind="ExternalInput").ap()
    A['y'] = nc.dram_tensor("y", [T, D], F32, kind="ExternalOutput").ap()
    A['qkT'] = nc.dram_tensor("qkT", [1280, T], BF16).ap()
    A['vtok'] = nc.dram_tensor("vtok", [T, 768], BF16).ap()
    A['pT'] = nc.dram_tensor("pT", [TC, T], F32).ap()
    A['mixT'] = nc.dram_tensor("mixT", [1024, T], BF16).ap()
    A['h2T'] = nc.dram_tensor("h2T", [1024, T], BF16).ap()
    A['gsig'] = nc.dram_tensor("gsig", [24, T], F32).ap()
    xm = nc.dram_tensor("xmid", [T, D], F32).ap()
    x1 = nc.dram_tensor("x1", [T, D], F32).ap()
    xin = A['x']
    for layer in range(KDEPTH):
        A['x'] = xin
        with ExitStack() as st:
            phase1(P, st, A, layer)
        with ExitStack() as st:
            consts = load_consts(P, st, A)
            if 'h' in KPHASES:
                phase_hgrn(P, A, layer, consts)
            if 'n' in KPHASES:
                phase_nsa(P, A, layer, consts)
            if 'f' in KPHASES:
                phase_fox(P, A, layer, consts)
            xout = x1 if layer < KDEPTH - 1 else A['y']
            phase_wo_ffn(P, A, layer, consts, xin, xm, xout)
        xin = xout


def _host_inputs(inp):
    cos, sin = rope_tables()
    gq = np.concatenate([np.tile(inp['nsa_qn_g'], (1, 8)), np.tile(inp['nsa_kn_g'], (1, 4)), np.tile(inp['fox_qn_g'], (1, 4)),
                         np.tile(inp['fox_kn_g'], (1, 4))], axis=1).astype(np.float32)
    base = {"win": win_layout(inp['w_in']), "g1": np.ascontiguousarray(inp['norm1_g'].reshape(2, 8, 128).transpose(0, 2, 1)),
            "g2": np.ascontiguousarray(inp['norm2_g'].reshape(2, 8, 128).transpose(0, 2, 1)),
            "gq": gq, "cos": cos, "sin": sin, "fb": inp['fox_fb'].reshape(2, 4, 1).astype(np.float32),
            "wo": inp['w_o'], "wup": inp['w_up'], "wdn": inp['w_down']}
    base.update(const_inputs()); base.update(hgrn_consts(inp)); base.update(nsa_consts(inp))
    return base


def kernel(**inp):
    inp = {k: np.asarray(v) for k, v in inp.items()}
    nc, plan = build_two_pass(lambda: bass.Bass("TRN2", target_bir_lowering=False), _body)
    base = _host_inputs(inp)
    in_maps = []
    for b in range(8):
        m = dict(base); m['x'] = np.ascontiguousarray(inp['x'][b]); in_maps.append(m)
    res = run_bass_kernel_spmd(nc, in_maps, core_ids=list(range(8)))
    return np.stack([r['y'] for r in res.results], axis=0).astype(np.float32)
```
